# Optimizing a Trainium2 kernel written in Bass

```python
import math
import jax
import jax.numpy as jnp
from jax import lax
import numpy as np

D_MODEL = 2048
BATCH = 8
SEQ = 4096
DEPTH = 2

A_HEADS = 8
A_HEAD_DIM = 128
A_WIDTH = A_HEADS * A_HEAD_DIM
A_Q_LORA = 512
A_KV_LORA = 256
IDX_HEADS = 16
IDX_DIM = 64
TOPK_MAX = 256
Q_BLOCK = 128
REL_BUCKETS = 32
REL_MAX_DIST = 128
B_HEADS = 8
B_HEAD_DIM = 128
B_WIDTH = B_HEADS * B_HEAD_DIM
B_CONV = 4
GDN_CHUNK = 64
C_WIDTH = 1024
C_CONV = 31
D_WIDTH = 1024
D_CONV = 3

EPS = 1e-6
AB_SPLITS = (A_Q_LORA, A_KV_LORA, IDX_DIM, IDX_HEADS, A_WIDTH,
             3 * B_WIDTH, B_HEADS, B_HEADS, B_WIDTH)
AB_IN = sum(AB_SPLITS)
AB_MIX = A_WIDTH + B_WIDTH
CD_SPLITS = (C_WIDTH, C_WIDTH, C_WIDTH, D_WIDTH, D_WIDTH, D_WIDTH, D_WIDTH)
CD_IN = sum(CD_SPLITS)
CD_MIX = C_WIDTH + D_WIDTH
N_AB = (DEPTH + 1) // 2
N_CD = DEPTH // 2

kernel_name = 'hybrid_dsa_gdn_conformer_shortconv'


def _split(y, sizes):
    return jnp.split(y, np.cumsum(sizes)[:-1].tolist(), axis=-1)


def rmsnorm(x, w):
    xf = x.astype(jnp.float32)
    y = xf * lax.rsqrt(jnp.mean(xf * xf, axis=-1, keepdims=True) + EPS)
    return (y * w.astype(jnp.float32)).astype(x.dtype)


def layernorm(x, w, b):
    xf = x.astype(jnp.float32)
    xc = xf - jnp.mean(xf, axis=-1, keepdims=True)
    y = xc * lax.rsqrt(jnp.mean(xc * xc, axis=-1, keepdims=True) + EPS)
    return (y * w.astype(jnp.float32) + b.astype(jnp.float32)).astype(x.dtype)


def l2norm(x):
    return x * lax.rsqrt(jnp.sum(x * x, axis=-1, keepdims=True) + EPS)


def causal_dwconv(x, w):
    width, ch = w.shape
    return lax.conv_general_dilated(
        x, w[:, None, :].astype(x.dtype), window_strides=(1,), padding=[(width - 1, 0)],
        dimension_numbers=('NWC', 'WIO', 'NWC'), feature_group_count=ch)


def t5_bucket(dist):
    max_exact = REL_BUCKETS // 2
    large = max_exact + (jnp.log(jnp.maximum(dist, 1).astype(jnp.float32) / max_exact)
                         / math.log(REL_MAX_DIST / max_exact)
                         * (REL_BUCKETS - max_exact)).astype(jnp.int32)
    large = jnp.minimum(large, REL_BUCKETS - 1)
    return jnp.where(dist < max_exact, dist, large)


def dsa_attention(c_q, c_kv, k_idx, w_idx, q_norm, w_uq, w_iq, kv_norm, w_uk, w_uv,
                  q_gain, k_gain, ik_w, ik_b, rel_bias):
    b, t, _ = c_q.shape
    c_q = rmsnorm(c_q, q_norm)
    q = rmsnorm((c_q @ w_uq).reshape(b, t, A_HEADS, A_HEAD_DIM), q_gain)
    q_idx = (c_q @ w_iq).reshape(b, t, IDX_HEADS, IDX_DIM)
    c_kv = rmsnorm(c_kv, kv_norm)
    k = rmsnorm((c_kv @ w_uk).reshape(b, t, A_HEADS, A_HEAD_DIM), k_gain)
    v = (c_kv @ w_uv).reshape(b, t, A_HEADS, A_HEAD_DIM)
    k_idx = layernorm(k_idx, ik_w, ik_b).astype(jnp.float32)
    w_idx = w_idx.astype(jnp.float32) * (IDX_HEADS ** -0.5)
    q_idx = q_idx.astype(jnp.float32)
    n_sel = min(TOPK_MAX, t // 4)
    key_pos = jnp.arange(t, dtype=jnp.int32)

    def query_block(i):
        start = i * Q_BLOCK
        qi = lax.dynamic_slice_in_dim(q_idx, start, Q_BLOCK, axis=1)
        wi = lax.dynamic_slice_in_dim(w_idx, start, Q_BLOCK, axis=1)
        qb = lax.dynamic_slice_in_dim(q, start, Q_BLOCK, axis=1)
        qpos = start + jnp.arange(Q_BLOCK, dtype=jnp.int32)
        rel = jax.nn.relu(jnp.einsum('bqhd,bsd->bqhs', qi, k_idx))
        score = jnp.einsum('bqhs,bqh->bqs', rel, wi) * (IDX_DIM ** -0.5)
        causal = key_pos[None, :] <= qpos[:, None]
        score = jnp.where(causal[None], score, -jnp.inf)
        _, idx = lax.top_k(score, n_sel)
        k_sel = jax.vmap(lambda a, j: a[j])(k, idx)
        v_sel = jax.vmap(lambda a, j: a[j])(v, idx)
        logits = jnp.einsum('bqhd,bqkhd->bqhk', qb, k_sel).astype(jnp.float32) * (A_HEAD_DIM ** -0.5)
        dist = qpos[None, :, None] - idx
        bias = rel_bias[t5_bucket(jnp.maximum(dist, 0))]
        logits = logits + jnp.transpose(bias, (0, 1, 3, 2)).astype(jnp.float32)
        logits = jnp.where((dist >= 0)[:, :, None, :], logits, -jnp.inf)
        p = jax.nn.softmax(logits, axis=-1).astype(v.dtype)
        return jnp.einsum('bqhk,bqkhd->bqhd', p, v_sel)

    out = lax.map(query_block, jnp.arange(t // Q_BLOCK))
    return jnp.moveaxis(out, 0, 1).reshape(b, t, A_WIDTH)


def chunk_gated_delta_rule(q, k, v, g, beta):
    b, t, h, dk = q.shape
    dv = v.shape[-1]
    c = GDN_CHUNK
    n = t // c

    def chunk(a):
        return a.reshape(b, n, c, h, a.shape[-1]).transpose(1, 0, 3, 2, 4)

    q, k, v = chunk(q), chunk(k), chunk(v)
    g = g.reshape(b, n, c, h).transpose(1, 0, 3, 2)
    beta = beta.reshape(b, n, c, h).transpose(1, 0, 3, 2)
    gc = jnp.cumsum(g, axis=-1)
    lower = jnp.tril(jnp.ones((c, c), dtype=bool))
    diff = gc[..., :, None] - gc[..., None, :]
    decay = jnp.where(lower, jnp.exp(jnp.where(lower, diff, 0.0)), 0.0)
    k_beta = k * beta[..., None]
    v_beta = v * beta[..., None]
    strict = jnp.tril(jnp.ones((c, c), dtype=jnp.float32), -1)
    a_mat = jnp.eye(c, dtype=jnp.float32) + jnp.einsum('nbhid,nbhjd->nbhij', k_beta, k) * decay * strict
    rhs = jnp.concatenate([v_beta, k_beta * jnp.exp(gc)[..., None]], axis=-1)
    sol = lax.linalg.triangular_solve(a_mat, rhs, left_side=True, lower=True, unit_diagonal=True)
    u, w = sol[..., :dv], sol[..., dv:]
    attn = jnp.einsum('nbhid,nbhjd->nbhij', q, k) * decay

    def step(state, xs):
        q_i, k_i, u_i, w_i, attn_i, gc_i = xs
        v_new = u_i - jnp.einsum('bhck,bhkv->bhcv', w_i, state)
        o_i = (jnp.einsum('bhck,bhkv->bhcv', q_i * jnp.exp(gc_i)[..., None], state)
               + jnp.einsum('bhij,bhjv->bhiv', attn_i, v_new))
        g_last = gc_i[..., -1]
        state = (state * jnp.exp(g_last)[..., None, None]
                 + jnp.einsum('bhck,bhcv->bhkv', k_i * jnp.exp(g_last[..., None] - gc_i)[..., None], v_new))
        return state, o_i

    s0 = jnp.zeros((b, h, dk, dv), jnp.float32)
    _, o = lax.scan(step, s0, (q, k, u, w, attn, gc))
    return o.transpose(1, 0, 3, 2, 4).reshape(b, t, h, dv)


def gated_deltanet(qkv, beta_raw, alpha_raw, z, conv_w, a_log, dt_bias, o_norm):
    b, t, _ = qkv.shape
    qkv = jax.nn.silu(causal_dwconv(qkv, conv_w)).astype(jnp.float32)
    q, k, v = [a.reshape(b, t, B_HEADS, B_HEAD_DIM) for a in jnp.split(qkv, 3, axis=-1)]
    q = l2norm(q) * (B_HEAD_DIM ** -0.5)
    k = l2norm(k)
    beta = jax.nn.sigmoid(beta_raw.astype(jnp.float32))
    g = -jnp.exp(a_log.astype(jnp.float32)) * jax.nn.softplus(
        alpha_raw.astype(jnp.float32) + dt_bias.astype(jnp.float32))
    o = chunk_gated_delta_rule(q, k, v, g, beta)
    o = rmsnorm(o, o_norm).reshape(b, t, B_WIDTH).astype(z.dtype)
    return o * jax.nn.silu(z)


def ab_mix(h, w_in, q_norm, w_uq, w_iq, kv_norm, w_uk, w_uv, q_gain, k_gain, ik_w, ik_b,
           conv_w, a_log, dt_bias, o_norm, rel_bias):
    c_q, c_kv, k_idx, w_idx, z_a, qkv_b, beta_b, alpha_b, z_b = _split(h @ w_in, AB_SPLITS)
    y_a = dsa_attention(c_q, c_kv, k_idx, w_idx, q_norm, w_uq, w_iq, kv_norm, w_uk, w_uv,
                        q_gain, k_gain, ik_w, ik_b, rel_bias) * jax.nn.silu(z_a)
    y_b = gated_deltanet(qkv_b, beta_b, alpha_b, z_b, conv_w, a_log, dt_bias, o_norm)
    return jnp.concatenate([y_a, y_b], axis=-1)


def cd_mix(h, w_in, dw_w, dw_b, ln_w, ln_b, d_conv_w):
    glu_a, glu_g, z_c, b_gate, c_gate, u_d, z_d = _split(h @ w_in, CD_SPLITS)
    u = glu_a * jax.nn.sigmoid(glu_g)
    u = causal_dwconv(u, dw_w) + dw_b.astype(u.dtype)
    u = jax.nn.silu(layernorm(u, ln_w, ln_b))
    y_c = u * jax.nn.silu(z_c)
    y_d = b_gate * causal_dwconv(c_gate * u_d, d_conv_w) * jax.nn.silu(z_d)
    return jnp.concatenate([y_c, y_d], axis=-1)


def setup_inputs(seed: int = 0) -> dict:
    key = jax.random.key(seed)
    keys = iter(jax.random.split(key, 40))
    f32 = jnp.float32

    def dense(shape, fan_in):
        return jax.random.normal(next(keys), shape, f32) * (fan_in ** -0.5)

    def gain(shape):
        return 1.0 + 0.02 * jax.random.normal(next(keys), shape, f32)

    def small(shape):
        return 0.02 * jax.random.normal(next(keys), shape, f32)

    x = jax.random.normal(next(keys), (BATCH, SEQ, D_MODEL), f32)
    dt = jnp.exp(jax.random.uniform(next(keys), (N_AB, B_HEADS), f32, math.log(1e-3), math.log(1e-1)))
    a_log = jnp.log(jax.random.uniform(next(keys), (N_AB, B_HEADS), f32, 1.0, 16.0))
    return {
        'x': x,
        'norm_w': gain((DEPTH, D_MODEL)),
        'rel_bias': 0.1 * jax.random.normal(next(keys), (REL_BUCKETS, A_HEADS), f32),
        'ab_w_in': dense((N_AB, D_MODEL, AB_IN), D_MODEL),
        'a_q_norm': gain((N_AB, A_Q_LORA)),
        'a_w_uq': dense((N_AB, A_Q_LORA, A_WIDTH), A_Q_LORA),
        'a_w_iq': dense((N_AB, A_Q_LORA, IDX_HEADS * IDX_DIM), A_Q_LORA),
        'a_kv_norm': gain((N_AB, A_KV_LORA)),
        'a_w_uk': dense((N_AB, A_KV_LORA, A_WIDTH), A_KV_LORA),
        'a_w_uv': dense((N_AB, A_KV_LORA, A_WIDTH), A_KV_LORA),
        'a_q_gain': gain((N_AB, A_HEAD_DIM)),
        'a_k_gain': gain((N_AB, A_HEAD_DIM)),
        'a_ik_norm_w': gain((N_AB, IDX_DIM)),
        'a_ik_norm_b': small((N_AB, IDX_DIM)),
        'b_conv_w': dense((N_AB, B_CONV, 3 * B_WIDTH), B_CONV),
        'b_a_log': a_log,
        'b_dt_bias': dt + jnp.log(-jnp.expm1(-dt)),
        'b_o_norm': gain((N_AB, B_HEAD_DIM)),
        'ab_w_out': dense((N_AB, AB_MIX, D_MODEL), AB_MIX),
        'cd_w_in': dense((N_CD, D_MODEL, CD_IN), D_MODEL),
        'c_dw_w': dense((N_CD, C_CONV, C_WIDTH), C_CONV),
        'c_dw_b': small((N_CD, C_WIDTH)),
        'c_ln_w': gain((N_CD, C_WIDTH)),
        'c_ln_b': small((N_CD, C_WIDTH)),
        'd_conv_w': dense((N_CD, D_CONV, D_WIDTH), D_CONV),
        'cd_w_out': dense((N_CD, CD_MIX, D_MODEL), CD_MIX),
    }


def reference(x, norm_w, rel_bias, ab_w_in, a_q_norm, a_w_uq, a_w_iq, a_kv_norm, a_w_uk,
              a_w_uv, a_q_gain, a_k_gain, a_ik_norm_w, a_ik_norm_b, b_conv_w, b_a_log,
              b_dt_bias, b_o_norm, ab_w_out, cd_w_in, c_dw_w, c_dw_b, c_ln_w, c_ln_b,
              d_conv_w, cd_w_out):
    for i in range(DEPTH):
        h = rmsnorm(x, norm_w[i])
        j = i // 2
        if i % 2 == 0:
            y = ab_mix(h, ab_w_in[j], a_q_norm[j], a_w_uq[j], a_w_iq[j], a_kv_norm[j],
                       a_w_uk[j], a_w_uv[j], a_q_gain[j], a_k_gain[j], a_ik_norm_w[j],
                       a_ik_norm_b[j], b_conv_w[j], b_a_log[j], b_dt_bias[j], b_o_norm[j],
                       rel_bias)
            x = x + y @ ab_w_out[j]
        else:
            y = cd_mix(h, cd_w_in[j], c_dw_w[j], c_dw_b[j], c_ln_w[j], c_ln_b[j], d_conv_w[j])
            x = x + y @ cd_w_out[j]
    return x
```

```python
from contextlib import ExitStack
import numpy as np
import concourse.bass as bass
import concourse.mybir as mybir
from concourse.bass_utils import run_bass_kernel_spmd

F32 = mybir.dt.float32
BF16 = mybir.dt.bfloat16
ALU = mybir.AluOpType
AF = mybir.ActivationFunctionType
AX = mybir.AxisListType

T = 4096
D = 2048
NT = T // 128
EPS = 1e-6
ENGS = ("pe", "act", "dve", "pool", "sp")


class Chan:
    def __init__(self, sem, name):
        self.sem = sem
        self.name = name
        self.n = 0


class KB:
    def __init__(self, nc):
        self.nc = nc
        self.es = ExitStack()
        self.q = {e: [] for e in ENGS}
        self.sems = {}
        self.cnt = {}
        self.seen = {e: {} for e in ENGS}
        self.lastw = {}
        self.readers = {}
        self.chans = []
        self.chan_by_sem = {}
        self.nins = 0
        for e in ENGS:
            self.sems[e] = self.es.enter_context(nc.semaphore("s_" + e))
            self.cnt[e] = 0

    def sb(self, name, shape, dt, stack=None):
        return (stack or self.es).enter_context(self.nc.sbuf_tensor(name, list(shape), dt))

    def ps(self, name, shape, dt=F32, stack=None):
        return (stack or self.es).enter_context(self.nc.psum_tensor(name, list(shape), dt))

    def chan(self, name):
        c = Chan(self.es.enter_context(self.nc.semaphore("c_" + name)), name)
        self.chans.append(c)
        self.chan_by_sem[id(c.sem)] = c
        return c

    def _need(self, eng, sem, val):
        ch = self.chan_by_sem.get(id(sem))
        if ch is not None:
            val = max(val, 16 * ch.n)
        cur = self.seen[eng].get(id(sem), 0)
        if val > cur:
            self.seen[eng][id(sem)] = val
            self.q[eng].append(("wait", sem, val))

    def _deps(self, eng, reads, writes, my_sem):
        for r in reads:
            ev = self.lastw.get(r)
            if ev is not None:
                self._need(eng, ev[0], ev[1])
        for w in writes:
            ev = self.lastw.get(w)
            if ev is not None:
                self._need(eng, ev[0], ev[1])
            rd = self.readers.get(w)
            if rd:
                for sem, val in rd.values():
                    if sem is my_sem:
                        continue
                    self._need(eng, sem, val)

    def _commit(self, ev, reads, writes):
        for w in writes:
            self.lastw[w] = ev
            self.readers[w] = {}
        for r in reads:
            d = self.readers.setdefault(r, {})
            d[id(ev[0])] = ev

    def op(self, eng, fn, reads=(), writes=()):
        sem = self.sems[eng]
        self._deps(eng, reads, writes, sem)
        self.cnt[eng] += 1
        ev = (sem, self.cnt[eng])
        self.q[eng].append(("ins", fn, sem, 1))
        self._commit(ev, reads, writes)
        self.nins += 1
        return ev

    def dma(self, eng, ch, out, in_, reads=(), writes=(), **kw):
        self._deps(eng, reads, writes, None)
        ch.n += 1
        ev = (ch.sem, 16 * ch.n)
        self.q[eng].append(("ins", lambda e, o=out, i=in_, k=kw: e.dma_start(out=o, in_=i, **k), ch.sem, 16))
        self._commit(ev, reads, writes)
        self.nins += 1
        return ev

    def _all_events(self):
        evs = [(self.sems[e], self.cnt[e]) for e in ENGS if self.cnt[e] > 0]
        evs += [(c.sem, 16 * c.n) for c in self.chans if c.n > 0]
        return evs

    def barrier(self):
        evs = self._all_events()
        for e in ENGS:
            for sem, val in evs:
                self._need(e, sem, val)

    def finish(self, final_eng="sp"):
        for sem, val in self._all_events():
            self._need(final_eng, sem, val)

    def emit(self):
        nc = self.nc
        q = self.q
        self.q = {e: [] for e in ENGS}

        def replay(eng_obj, items):
            for it in items:
                if it[0] == "wait":
                    eng_obj.wait_ge(it[1], it[2])
                else:
                    it[1](eng_obj).then_inc(it[2], it[3])

        with nc.Block() as block:
            @block.tensor
            def _(e):
                replay(e, q["pe"])

            @block.scalar
            def _(e):
                replay(e, q["act"])

            @block.vector
            def _(e):
                replay(e, q["dve"])

            @block.gpsimd
            def _(e):
                replay(e, q["pool"])

            @block.sync
            def _(e):
                replay(e, q["sp"])

    def end_phase(self):
        self.barrier()
        self.emit()

    def close(self):
        self.es.close()


def MM(kb, out, lhsT, rhs, start, stop, reads, writes):
    return kb.op("pe", lambda e: e.matmul(out, lhsT, rhs, start=start, stop=stop), reads, writes)


def TR(kb, out, in_, ident, reads, writes):
    return kb.op("pe", lambda e: e.transpose(out, in_, ident), reads, writes)


def ACT(kb, out, in_, func, reads, writes, bias=None, scale=None, accum_out=None):
    kw = {}
    if bias is not None:
        kw["bias"] = bias
    if scale is not None:
        kw["scale"] = scale
    if accum_out is not None:
        kw["accum_out"] = accum_out
    return kb.op("act", lambda e: e.activation(out=out, in_=in_, func=func, **kw), reads, writes)


def TS(kb, eng, out, in0, s1, s2, op0, op1, reads, writes, accum_out=None):
    kw = {}
    if op1 is not None:
        kw["op1"] = op1
    if accum_out is not None:
        kw["accum_out"] = accum_out
    return kb.op(eng, lambda e: e.tensor_scalar(out=out, in0=in0, scalar1=s1, scalar2=s2, op0=op0, **kw),
                 reads, writes)


def TT(kb, eng, out, in0, in1, op, reads, writes):
    return kb.op(eng, lambda e: e.tensor_tensor(out=out, in0=in0, in1=in1, op=op), reads, writes)


def STT(kb, out, in0, scalar, in1, op0, op1, reads, writes):
    return kb.op("dve", lambda e: e.scalar_tensor_tensor(out=out, in0=in0, scalar=scalar, in1=in1,
                                                         op0=op0, op1=op1), reads, writes)


def CP(kb, eng, out, in_, reads, writes):
    if eng == "act":
        return kb.op("act", lambda e: e.copy(out=out, in_=in_), reads, writes)
    return kb.op(eng, lambda e: e.tensor_copy(out=out, in_=in_), reads, writes)


def MS(kb, eng, ap, val, writes):
    return kb.op(eng, lambda e: e.memset(ap, val), (), writes)


def RECIP(kb, out, in_, reads, writes):
    return kb.op("dve", lambda e: e.reciprocal(out=out, in_=in_), reads, writes)


def phase_norm_T(kb, C, x_dram, normwT, hT, tag):
    with ExitStack() as st:
        xt = [kb.sb(f"{tag}_xt{i}", [128, D], F32, st) for i in range(2)]
        xn = [kb.sb(f"{tag}_xn{i}", [128, D], F32, st) for i in range(2)]
        sq = kb.sb(f"{tag}_sq", [128, D], BF16, st)
        sm = [kb.sb(f"{tag}_sm{i}", [128, 4], F32, st) for i in range(2)]
        pst = [kb.ps(f"{tag}_pt{i}", [128, 8, 128], F32, st) for i in range(2)]
        ch = [kb.chan(f"{tag}_x{i}") for i in range(2)]
        for tt in range(NT):
            s = tt % 2
            kb.dma("sp", ch[s], xt[s][:], x_dram[tt * 128:(tt + 1) * 128, :], writes=[(tag, "xt", s)])
            ACT(kb, sq[:], xt[s][:], AF.Square, [(tag, "xt", s)], [(tag, "sq"), (tag, "ss", s)],
                accum_out=sm[s][:, 0:1])
            ACT(kb, sm[s][:, 1:2], sm[s][:, 0:1], AF.Sqrt, [(tag, "ss", s)], [(tag, "sd", s)],
                bias=C["eps"][:, 0:1], scale=1.0 / D)
            RECIP(kb, sm[s][:, 2:3], sm[s][:, 1:2], [(tag, "sd", s)], [(tag, "rs", s)])
            TS(kb, "dve", xn[s][:], xt[s][:], sm[s][:, 2:3], None, ALU.mult, None,
               [(tag, "xt", s), (tag, "rs", s)], [(tag, "xn", s)])
            for half in range(2):
                p = half
                for kk in range(8):
                    k = half * 8 + kk
                    TR(kb, pst[p][:, kk, :], xn[s][:, k * 128:(k + 1) * 128], C["ident"][:],
                       [(tag, "xn", s)], [(tag, "pt", p)])
                nw = normwT[:, half * 8:(half + 1) * 8].unsqueeze(2).to_broadcast([128, 8, 128])
                TT(kb, "dve", hT[:, half * 8:(half + 1) * 8, tt * 128:(tt + 1) * 128], pst[p][:], nw,
                   ALU.mult, [(tag, "pt", p)], [("hT", tt)])
        kb.end_phase()


def gemm_fm(kb, tag, actT, act_keys, KC, w_dram, NCH, sink):
    with ExitStack() as st:
        wb = [kb.sb(f"{tag}_wb{i}", [128, KC * 128], BF16, st) for i in range(2)]
        wch = [kb.chan(f"{tag}_w{i}") for i in range(2)]
        pss = [kb.ps(f"{tag}_ps{i}", [128, 1024], F32, st) for i in range(2)]
        it = 0
        for c in range(NCH):
            s = c % 2
            kb.dma("pool", wch[s], wb[s][:], w_dram[c], writes=[(tag, "wb", s)])
            for ts in range(4):
                p = it % 2
                it += 1
                for b in range(2):
                    t0 = ts * 1024 + b * 512
                    for k in range(KC):
                        MM(kb, pss[p][:, b * 512:(b + 1) * 512], wb[s][:, k * 128:(k + 1) * 128],
                           actT[:, k, t0:t0 + 512], k == 0, k == KC - 1,
                           [(tag, "wb", s)] + act_keys, [(tag, "ps", p)])
                sink(c, ts, pss[p], (tag, "ps", p), st)
        kb.end_phase()


class StoreSink:
    def __init__(self, kb, tag, dst, st):
        self.kb = kb
        self.tag = tag
        self.dst = dst
        self.stg = [kb.sb(f"{tag}_stg{i}", [128, 1024], F32, st) for i in range(3)]
        self.ch = [kb.chan(f"{tag}_st{i}") for i in range(3)]
        self.i = 0

    def __call__(self, c, ts, ps, pkey, st):
        kb = self.kb
        s = self.i % 3
        eng = "act" if self.i % 2 == 0 else "dve"
        self.i += 1
        CP(kb, eng, self.stg[s][:], ps[:], [pkey], [(self.tag, "stg", s)])
        kb.dma("sp", self.ch[s], self.dst[c * 128:(c + 1) * 128, ts * 1024:(ts + 1) * 1024], self.stg[s][:],
               reads=[(self.tag, "stg", s)], writes=[(self.tag, "dst", c)])


def phase_inproj(kb, C, tag, x_dram, normwT, w_dram, NCH, pj):
    with ExitStack() as st:
        hT = kb.sb(f"{tag}_hT", [128, 16, T], BF16, st)
        phase_norm_T(kb, C, x_dram, normwT, hT, tag + "n")
        with ExitStack() as st2:
            sink = StoreSink(kb, tag + "s", pj, st2)
            gemm_fm(kb, tag + "g", hT, [], 16, w_dram, NCH, sink)


def phase_outproj(kb, C, tag, yT_dram, wo_dram, xres_dram, out_dram):
    with ExitStack() as st:
        wo = kb.sb(f"{tag}_wo", [128, 16, D], BF16, st)
        wch = kb.chan(f"{tag}_w")
        for k in range(16):
            kb.dma("pool", wch, wo[:, k, :], wo_dram[k], writes=[(tag, "wo")])
        yt = [kb.sb(f"{tag}_yt{i}", [128, 16, 128], BF16, st) for i in range(2)]
        xr = [kb.sb(f"{tag}_xr{i}", [128, D], F32, st) for i in range(2)]
        ot = [kb.sb(f"{tag}_ot{i}", [128, D], F32, st) for i in range(2)]
        ps = [kb.ps(f"{tag}_ps{i}", [128, 512], F32, st) for i in range(4)]
        chy = [kb.chan(f"{tag}_y{i}") for i in range(2)]
        chx = [kb.chan(f"{tag}_x{i}") for i in range(2)]
        cho = [kb.chan(f"{tag}_o{i}") for i in range(2)]
        yv = yT_dram.rearrange("(k p) t -> p k t", p=128)
        for tt in range(NT):
            s = tt % 2
            kb.dma("sp", chy[s], yt[s][:], yv[:, :, tt * 128:(tt + 1) * 128], writes=[(tag, "yt", s)])
            kb.dma("sp", chx[s], xr[s][:], xres_dram[tt * 128:(tt + 1) * 128, :], writes=[(tag, "xr", s)])
            for nb in range(4):
                for k in range(16):
                    MM(kb, ps[nb][:], yt[s][:, k, :], wo[:, k, nb * 512:(nb + 1) * 512], k == 0, k == 15,
                       [(tag, "yt", s), (tag, "wo")], [(tag, "ps", nb)])
                TT(kb, "dve", ot[s][:, nb * 512:(nb + 1) * 512], ps[nb][:], xr[s][:, nb * 512:(nb + 1) * 512],
                   ALU.add, [(tag, "ps", nb), (tag, "xr", s)], [(tag, "ot", s)])
            kb.dma("pool", cho[s], out_dram[tt * 128:(tt + 1) * 128, :], ot[s][:],
                   reads=[(tag, "ot", s)], writes=[(tag, "out", tt)])
        kb.end_phase()


def phase_cd_mix(kb, C, pj, cw, y_dram):
    HB = 2048
    tag = "cd"
    with ExitStack() as st:
        uc = kb.sb("cd_uc", [128, 8, HB], F32, st)
        ab = [kb.sb(f"cd_a{i}", [128, 30 + HB], F32, st) for i in range(2)]
        gb = [kb.sb(f"cd_g{i}", [128, 30 + HB], F32, st) for i in range(2)]
        sq = [kb.sb(f"cd_sq{i}", [128, HB], F32, st) for i in range(2)]
        mean = kb.sb("cd_mean", [128, HB], F32, st)
        rstd = kb.sb("cd_rstd", [128, HB], F32, st)
        m2 = kb.sb("cd_m2", [128, HB], F32, st)
        zb = [kb.sb(f"cd_z{i}", [128, HB], F32, st) for i in range(2)]
        yb = [kb.sb(f"cd_y{i}", [128, HB], BF16, st) for i in range(2)]
        ps_sum = kb.ps("cd_pss", [128, HB], F32, st)
        ps_ssq = kb.ps("cd_psq", [128, HB], F32, st)
        cha = [kb.chan(f"cd_a{i}") for i in range(2)]
        chg = [kb.chan(f"cd_g{i}") for i in range(2)]
        chz = [kb.chan(f"cd_z{i}") for i in range(2)]
        chy = [kb.chan(f"cd_y{i}") for i in range(2)]
        for th in range(2):
            t0 = th * HB
            for cc in range(8):
                s = cc % 2
                r0 = cc * 128
                if th == 0:
                    MS(kb, "pool", ab[s][:, 0:30], 0.0, [(tag, "a", s)])
                    MS(kb, "pool", gb[s][:, 0:30], 0.0, [(tag, "g", s)])
                    kb.dma("sp", cha[s], ab[s][:, 30:30 + HB], pj[r0:r0 + 128, 0:HB], writes=[(tag, "a", s)])
                    kb.dma("sp", chg[s], gb[s][:, 30:30 + HB], pj[1024 + r0:1024 + r0 + 128, 0:HB],
                           writes=[(tag, "g", s)])
                else:
                    kb.dma("sp", cha[s], ab[s][:], pj[r0:r0 + 128, t0 - 30:t0 + HB], writes=[(tag, "a", s)])
                    kb.dma("sp", chg[s], gb[s][:], pj[1024 + r0:1024 + r0 + 128, t0 - 30:t0 + HB],
                           writes=[(tag, "g", s)])
                ACT(kb, gb[s][:], gb[s][:], AF.Sigmoid, [(tag, "g", s)], [(tag, "g", s)])
                TT(kb, "dve", ab[s][:], ab[s][:], gb[s][:], ALU.mult, [(tag, "a", s), (tag, "g", s)], [(tag, "a", s)])
                TS(kb, "dve", uc[:, cc, :], ab[s][:, 30:30 + HB], cw["dww"][:, cc, 30:31], cw["dwb"][:, cc:cc + 1],
                   ALU.mult, ALU.add, [(tag, "a", s)], [(tag, "uc", cc)])
                for j in range(30):
                    STT(kb, uc[:, cc, :], ab[s][:, j:j + HB], cw["dww"][:, cc, j:j + 1], uc[:, cc, :],
                        ALU.mult, ALU.add, [(tag, "a", s), (tag, "uc", cc)], [(tag, "uc", cc)])
                ACT(kb, sq[s][:], uc[:, cc, :], AF.Square, [(tag, "uc", cc)], [(tag, "sq", s)])
                for tb in range(4):
                    MM(kb, ps_sum[:, tb * 512:(tb + 1) * 512], C["ones_f"][:], uc[:, cc, tb * 512:(tb + 1) * 512],
                       cc == 0, cc == 7, [(tag, "uc", cc)], [(tag, "pss")])
                    MM(kb, ps_ssq[:, tb * 512:(tb + 1) * 512], C["ones_f"][:], sq[s][:, tb * 512:(tb + 1) * 512],
                       cc == 0, cc == 7, [(tag, "sq", s)], [(tag, "psq")])
            ACT(kb, mean[:], ps_sum[:], AF.Copy, [(tag, "pss")], [(tag, "mean")], scale=1.0 / 1024)
            TT(kb, "dve", m2[:], mean[:], mean[:], ALU.mult, [(tag, "mean")], [(tag, "m2")])
            STT(kb, m2[:], ps_ssq[:], 1.0 / 1024, m2[:], ALU.mult, ALU.subtract, [(tag, "psq"), (tag, "m2")],
                [(tag, "m2")])
            ACT(kb, m2[:], m2[:], AF.Sqrt, [(tag, "m2")], [(tag, "m2")], bias=C["eps"][:, 0:1])
            RECIP(kb, rstd[:], m2[:], [(tag, "m2")], [(tag, "rstd")])
            for cc in range(8):
                s = cc % 2
                r0 = cc * 128
                kb.dma("sp", chz[s], zb[s][:], pj[2048 + r0:2048 + r0 + 128, t0:t0 + HB], writes=[(tag, "z", s)])
                TT(kb, "dve", uc[:, cc, :], uc[:, cc, :], mean[:], ALU.subtract, [(tag, "uc", cc), (tag, "mean")],
                   [(tag, "uc", cc)])
                TT(kb, "dve", uc[:, cc, :], uc[:, cc, :], rstd[:], ALU.mult, [(tag, "uc", cc), (tag, "rstd")],
                   [(tag, "uc", cc)])
                ACT(kb, uc[:, cc, :], uc[:, cc, :], AF.Silu, [(tag, "uc", cc)], [(tag, "uc", cc)],
                    scale=cw["lnw"][:, cc:cc + 1], bias=cw["lnb"][:, cc:cc + 1])
                ACT(kb, zb[s][:], zb[s][:], AF.Silu, [(tag, "z", s)], [(tag, "z", s)])
                TT(kb, "dve", yb[s][:], uc[:, cc, :], zb[s][:], ALU.mult, [(tag, "uc", cc), (tag, "z", s)],
                   [(tag, "y", s)])
                kb.dma("pool", chy[s], y_dram[r0:r0 + 128, t0:t0 + HB], yb[s][:], reads=[(tag, "y", s)],
                       writes=[(tag, "yd", cc, th)])
        kb.end_phase()
    tag = "sc"
    with ExitStack() as st:
        W = T + 2
        bg = [kb.sb(f"sc_b{i}", [128, T], F32, st) for i in range(2)]
        cg = [kb.sb(f"sc_c{i}", [128, W], F32, st) for i in range(2)]
        ud = [kb.sb(f"sc_u{i}", [128, W], F32, st) for i in range(2)]
        zd = [kb.sb(f"sc_z{i}", [128, T], F32, st) for i in range(2)]
        acc = [kb.sb(f"sc_acc{i}", [128, T], F32, st) for i in range(2)]
        yb = [kb.sb(f"sc_y{i}", [128, T], BF16, st) for i in range(2)]
        chs = {n: [kb.chan(f"sc_{n}{i}") for i in range(2)] for n in ("b", "c", "u", "z", "y")}
        for cc in range(8):
            s = cc % 2
            r0 = cc * 128
            MS(kb, "pool", cg[s][:, 0:2], 0.0, [(tag, "c", s)])
            MS(kb, "pool", ud[s][:, 0:2], 0.0, [(tag, "u", s)])
            kb.dma("sp", chs["b"][s], bg[s][:], pj[3072 + r0:3072 + r0 + 128, :], writes=[(tag, "b", s)])
            kb.dma("sp", chs["c"][s], cg[s][:, 2:W], pj[4096 + r0:4096 + r0 + 128, :], writes=[(tag, "c", s)])
            kb.dma("sp", chs["u"][s], ud[s][:, 2:W], pj[5120 + r0:5120 + r0 + 128, :], writes=[(tag, "u", s)])
            kb.dma("sp", chs["z"][s], zd[s][:], pj[6144 + r0:6144 + r0 + 128, :], writes=[(tag, "z", s)])
            TT(kb, "dve", cg[s][:], cg[s][:], ud[s][:], ALU.mult, [(tag, "c", s), (tag, "u", s)], [(tag, "c", s)])
            TS(kb, "dve", acc[s][:], cg[s][:, 2:W], cw["dcw"][:, cc, 2:3], None, ALU.mult, None,
               [(tag, "c", s)], [(tag, "acc", s)])
            for j in range(2):
                STT(kb, acc[s][:], cg[s][:, j:j + T], cw["dcw"][:, cc, j:j + 1], acc[s][:], ALU.mult, ALU.add,
                    [(tag, "c", s), (tag, "acc", s)], [(tag, "acc", s)])
            ACT(kb, zd[s][:], zd[s][:], AF.Silu, [(tag, "z", s)], [(tag, "z", s)])
            TT(kb, "pool", bg[s][:], bg[s][:], zd[s][:], ALU.mult, [(tag, "b", s), (tag, "z", s)], [(tag, "b", s)])
            TT(kb, "dve", yb[s][:], acc[s][:], bg[s][:], ALU.mult, [(tag, "acc", s), (tag, "b", s)], [(tag, "y", s)])
            kb.dma("pool", chs["y"][s], y_dram[1024 + r0:1024 + r0 + 128, :], yb[s][:], reads=[(tag, "y", s)],
                   writes=[(tag, "yd", cc)])
        kb.end_phase()


def colnorm_phase(kb, C, tag, src, KC, gT, outT):
    with ExitStack() as st:
        raw = [kb.sb(f"{tag}_raw{i}", [128, KC, 1024], F32, st) for i in range(2)]
        sq = [kb.sb(f"{tag}_sq{i}", [128, 1024], F32, st) for i in range(2)]
        rs = kb.sb(f"{tag}_rs", [128, 1024], F32, st)
        ssp = kb.ps(f"{tag}_ssp", [128, 1024], F32, st)
        ch = [kb.chan(f"{tag}_l{i}") for i in range(2)]
        sv = src.rearrange("(k p) t -> p k t", p=128)
        for ts in range(4):
            s = ts % 2
            kb.dma("sp", ch[s], raw[s][:], sv[:, :, ts * 1024:(ts + 1) * 1024], writes=[(tag, "raw", s)])
            for k in range(KC):
                q = k % 2
                ACT(kb, sq[q][:], raw[s][:, k, :], AF.Square, [(tag, "raw", s)], [(tag, "sq", q)])
                for b in range(2):
                    MM(kb, ssp[:, b * 512:(b + 1) * 512], C["ones_f"][:], sq[q][:, b * 512:(b + 1) * 512],
                       k == 0, k == KC - 1, [(tag, "sq", q)], [(tag, "ssp")])
            ACT(kb, rs[:], ssp[:], AF.Sqrt, [(tag, "ssp")], [(tag, "rs")], bias=C["eps"][:, 0:1],
                scale=1.0 / (KC * 128))
            RECIP(kb, rs[:], rs[:], [(tag, "rs")], [(tag, "rs")])
            for k in range(KC):
                STT(kb, outT[:, k, ts * 1024:(ts + 1) * 1024], raw[s][:, k, :], gT[:, k:k + 1], rs[:],
                    ALU.mult, ALU.mult, [(tag, "raw", s), (tag, "rs")], [(tag, "out", k, ts)])
        kb.end_phase()


class HeadNormSink:
    def __init__(self, kb, C, tag, dst, gain, n_norm, raw_dst, st):
        self.kb, self.C, self.tag, self.dst, self.gain = kb, C, tag, dst, gain
        self.n_norm, self.raw_dst = n_norm, raw_dst
        self.sq = [kb.sb(f"{tag}_sq{i}", [128, 1024], F32, st) for i in range(2)]
        self.rs = [kb.sb(f"{tag}_rs{i}", [128, 1024], F32, st) for i in range(2)]
        self.ob = [kb.sb(f"{tag}_ob{i}", [128, 1024], BF16, st) for i in range(2)]
        self.ssp = kb.ps(f"{tag}_ssp", [128, 1024], F32, st)
        self.ch = [kb.chan(f"{tag}_o{i}") for i in range(2)]
        self.i = 0

    def __call__(self, c, ts, ps, pkey, st):
        kb, C, tag = self.kb, self.C, self.tag
        s = self.i % 2
        self.i += 1
        if c < self.n_norm:
            ACT(kb, self.sq[s][:], ps[:], AF.Square, [pkey], [(tag, "sq", s)])
            for b in range(2):
                MM(kb, self.ssp[:, b * 512:(b + 1) * 512], C["ones_f"][:], self.sq[s][:, b * 512:(b + 1) * 512],
                   True, True, [(tag, "sq", s)], [(tag, "ssp")])
            ACT(kb, self.rs[s][:], self.ssp[:], AF.Sqrt, [(tag, "ssp")], [(tag, "rs", s)], bias=C["eps"][:, 0:1],
                scale=1.0 / 128)
            RECIP(kb, self.rs[s][:], self.rs[s][:], [(tag, "rs", s)], [(tag, "rs", s)])
            STT(kb, self.ob[s][:], ps[:], self.gain, self.rs[s][:], ALU.mult, ALU.mult,
                [pkey, (tag, "rs", s)], [(tag, "ob", s)])
            d = self.dst[c]
        else:
            CP(kb, "act", self.ob[s][:], ps[:], [pkey], [(tag, "ob", s)])
            d = self.raw_dst[c - self.n_norm]
        kb.dma("sp", self.ch[s], d[:, ts * 1024:(ts + 1) * 1024], self.ob[s][:], reads=[(tag, "ob", s)],
               writes=[(tag, "dst", c)])


def phase_dsa_prep(kb, C, pj, W, S):
    with ExitStack() as st:
        cqn = kb.sb("cqn", [128, 4, T], BF16, st)
        colnorm_phase(kb, C, "cq", pj[0:512, :], 4, C["qnormT"], cqn)
        with ExitStack() as st2:
            sink = HeadNormSink(kb, C, "qs", S["qT"], C["qgain"][:, 0:1], 8, S["qiT"], st2)
            gemm_fm(kb, "qg", cqn, [], 4, W["w_uqiq"], 16, sink)
    with ExitStack() as st:
        ckvn = kb.sb("ckvn", [128, 2, T], BF16, st)
        colnorm_phase(kb, C, "ckv", pj[512:768, :], 2, C["kvnormT"], ckvn)
        with ExitStack() as st2:
            sink = HeadNormSink(kb, C, "ks", S["kT"], C["kgain"][:, 0:1], 8, None, st2)
            gemm_fm(kb, "kg", ckvn, [], 2, W["w_uk"], 8, sink)
        with ExitStack() as st2:
            wv = kb.sb("wv", [128, 2, 1024], BF16, st2)
            chw = kb.chan("wv")
            for k in range(2):
                kb.dma("pool", chw, wv[:, k, :], W["w_uv"][k], writes=[("wv",)])
            vps = [kb.ps(f"v_ps{i}", [128, 1024], F32, st2) for i in range(2)]
            vb = [kb.sb(f"v_b{i}", [128, 1024], BF16, st2) for i in range(2)]
            chv = [kb.chan(f"v_o{i}") for i in range(2)]
            for tt in range(NT):
                s = tt % 2
                for b in range(2):
                    for k in range(2):
                        MM(kb, vps[s][:, b * 512:(b + 1) * 512], ckvn[:, k, tt * 128:(tt + 1) * 128],
                           wv[:, k, b * 512:(b + 1) * 512], k == 0, k == 1, [("wv",)], [("v", "ps", s)])
                CP(kb, "act" if tt % 2 else "dve", vb[s][:], vps[s][:], [("v", "ps", s)], [("v", "b", s)])
                kb.dma("sp", chv[s], S["V"][tt * 128:(tt + 1) * 128, :], vb[s][:], reads=[("v", "b", s)],
                       writes=[("V", tt)])
            kb.end_phase()


def phase_ki_tmaj(kb, C, pj, S):
    with ExitStack() as st:
        ki = kb.sb("ki_raw", [128, T], F32, st)
        sq = kb.sb("ki_sq", [128, T], F32, st)
        mean = kb.sb("ki_mean", [128, 1024], F32, st)
        var = kb.sb("ki_var", [128, 1024], F32, st)
        kio = kb.sb("ki_o", [128, T], BF16, st)
        sm = kb.sb("ki_sm", [32, T], F32, st)
        tmo = kb.sb("ki_tmo", [128, NT, 32], F32, st)
        ps1 = kb.ps("ki_ps1", [128, 1024], F32, st)
        ps2 = kb.ps("ki_ps2", [128, 1024], F32, st)
        pst = kb.ps("ki_pst", [128, 16, 32], F32, st)
        ch = kb.chan("ki")
        kb.dma("sp", ch, ki[0:64, :], pj[768:832, :], writes=[("ki", "raw")])
        kb.dma("sp", ch, ki[64:128, :], pj[768:832, :], writes=[("ki", "raw")])
        kb.dma("sp", ch, sm[:], pj[896:928, :], writes=[("ki", "sm")])
        ACT(kb, sq[:], ki[:], AF.Square, [("ki", "raw")], [("ki", "sq")])
        for ts in range(4):
            for b in range(2):
                c0 = ts * 1024 + b * 512
                MM(kb, ps1[:, b * 512:(b + 1) * 512], C["ones_f"][0:64, :], ki[0:64, c0:c0 + 512], True, True,
                   [("ki", "raw")], [("ki", "ps1")])
                MM(kb, ps2[:, b * 512:(b + 1) * 512], C["ones_f"][0:64, :], sq[0:64, c0:c0 + 512], True, True,
                   [("ki", "sq")], [("ki", "ps2")])
            ACT(kb, mean[:], ps1[:], AF.Copy, [("ki", "ps1")], [("ki", "mean")], scale=1.0 / 64)
            TT(kb, "dve", var[:], mean[:], mean[:], ALU.mult, [("ki", "mean")], [("ki", "var")])
            STT(kb, var[:], ps2[:], 1.0 / 64, var[:], ALU.mult, ALU.subtract, [("ki", "ps2"), ("ki", "var")],
                [("ki", "var")])
            ACT(kb, var[:], var[:], AF.Sqrt, [("ki", "var")], [("ki", "var")], bias=C["eps"][:, 0:1])
            RECIP(kb, var[:], var[:], [("ki", "var")], [("ki", "var")])
            sl = slice(ts * 1024, (ts + 1) * 1024)
            TT(kb, "dve", ki[:, sl], ki[:, sl], mean[:], ALU.subtract, [("ki", "raw"), ("ki", "mean")], [("ki", "raw")])
            TT(kb, "dve", ki[:, sl], ki[:, sl], var[:], ALU.mult, [("ki", "raw"), ("ki", "var")], [("ki", "raw")])
            TS(kb, "dve", kio[:, sl], ki[:, sl], C["ikw"][:, 0:1], C["ikb"][:, 0:1], ALU.mult, ALU.add,
               [("ki", "raw")], [("ki", "o")])
        kb.dma("sp", ch, S["kiT"], kio[:], reads=[("ki", "o")], writes=[("kiT",)])
        for g in range(2):
            for tt in range(16):
                t = g * 16 + tt
                TR(kb, pst[:, tt, :], sm[:, t * 128:(t + 1) * 128], C["ident"][0:32, 0:32], [("ki", "sm")],
                   [("ki", "pst")])
            CP(kb, "dve", tmo[:, g * 16:(g + 1) * 16, :], pst[:], [("ki", "pst")], [("ki", "tmo")])
        kb.dma("sp", ch, S["tmaj"].rearrange("(n p) c -> p n c", p=128), tmo[:], reads=[("ki", "tmo")],
               writes=[("tmaj",)])
        kb.end_phase()


N_BIS = 24
SCALE_A = 128 ** -0.5


def phase_dsa_attn(kb, C, pj, S, y_dram):
    tag = "at"
    with ExitStack() as st:
        kT = kb.sb("at_kT", [128, 8, T], BF16, st)
        Vt = kb.sb("at_V", [128, NT, 1024], BF16, st)
        kiT = kb.sb("at_kiT", [128, T], BF16, st)
        score1 = kb.sb("at_sc", [128, T], F32, st)
        score = [score1, score1]
        mask = kb.sb("at_mask", [128, T], BF16, st)
        junk = mask
        maskT1 = kb.sb("at_maskT", [128, NT, 128], BF16, st)
        maskT = [maskT1, maskT1]
        rbuf = [kb.sb(f"at_r{i}", [128, 512], F32, st) for i in range(2)]
        qb = [kb.sb(f"at_q{i}", [128, 8, 128], BF16, st) for i in range(2)]
        qib1 = kb.sb("at_qi", [128, 8, 128], BF16, st)
        qib = [qib1, qib1]
        wt = [kb.sb(f"at_w{i}", [128, 32], F32, st) for i in range(2)]
        wab = [kb.sb(f"at_wab{i}", [128, 16], F32, st) for i in range(2)]
        sgn = [kb.sb(f"at_sgn{i}", [128, 16], F32, st) for i in range(2)]
        bs = [kb.sb(f"at_bs{i}", [128, 8], F32, st) for i in range(2)]
        za1 = kb.sb("at_za", [128, 8, 128], F32, st)
        za = [za1, za1]
        pt = [kb.sb(f"at_pt{i}", [128, 4, 128], BF16, st) for i in range(2)]
        ptm = [kb.sb(f"at_ptm{i}", [128, 4, 128], BF16, st) for i in range(2)]
        ost1 = kb.sb("at_ost", [128, 8, 256], F32, st)
        ost = [ost1, ost1]
        yst1 = kb.sb("at_yst", [128, 8, 128], BF16, st)
        yst = [yst1, yst1]
        biasS = C["biasS"]
        ident_b = kb.sb("at_identb", [128, 128], BF16, st)
        ones_b = kb.sb("at_onesb", [128, 128], BF16, st)
        ips = [kb.ps(f"at_ips{i}", [128, 512], F32, st) for i in range(2)]
        lg = [kb.ps(f"at_lg{i}", [128, 4, 128], F32, st) for i in range(2)]
        po = [kb.ps(f"at_po{i}", [128, 128], F32, st) for i in range(2)]
        prs = [kb.ps(f"at_prs{i}", [128, 128], F32, st) for i in range(2)]
        tps = ips[0][:].bitcast(BF16)[:, 0:512].rearrange("p (a b) -> p a b", b=128)
        chl = kb.chan("at_ld")
        chq = [kb.chan(f"at_q{i}") for i in range(2)]
        chy = [kb.chan(f"at_y{i}") for i in range(2)]
        chz = kb.chan("at_z")
        kb.dma("sp", chl, kT[:], S["kT"].rearrange("h p t -> p h t"), writes=[(tag, "kT")])
        kb.dma("sp", chl, Vt[:], S["V"].rearrange("(n p) c -> p n c", p=128), writes=[(tag, "V")])
        kb.dma("sp", chl, kiT[:], S["kiT"], writes=[(tag, "kiT")])
        CP(kb, "dve", ident_b[:], C["ident"][:], [], [(tag, "cst")])
        CP(kb, "dve", ones_b[:], C["ones_f"][:], [], [(tag, "cst")])
        qTv = S["qT"].rearrange("h p t -> p h t")
        qiTv = S["qiT"].rearrange("h p t -> p h t")
        zav = pj[1024:2048, :].rearrange("(h p) t -> p h t", p=128)
        yv = y_dram[0:1024, :].rearrange("(h p) t -> p h t", p=128)
        cnt = {"ips": 0, "r": 0, "lg": 0, "pt": 0, "po": 0}

        def stage_a(i):
            s = i % 2
            c0, c1 = i * 128, (i + 1) * 128
            kb.dma("sp", chq[s], qb[s][:], qTv[:, :, c0:c1], writes=[(tag, "q", s)])
            kb.dma("sp", chq[s], qib[s][:], qiTv[:, :, c0:c1], writes=[(tag, "qi")])
            kb.dma("sp", chq[s], wt[s][:], S["tmaj"][c0:c1, :], writes=[(tag, "w", s)])
            TS(kb, "dve", sgn[s][:], wt[s][:, 0:16], 0.0, 2.0, ALU.is_ge, ALU.mult, [(tag, "w", s)], [(tag, "sgn", s)])
            TS(kb, "dve", sgn[s][:], sgn[s][:], -1.0, None, ALU.add, None, [(tag, "sgn", s)], [(tag, "sgn", s)])
            TT(kb, "dve", wab[s][:], wt[s][:, 0:16], sgn[s][:], ALU.mult, [(tag, "w", s), (tag, "sgn", s)],
               [(tag, "wab", s)])
            nW = (i + 4) // 4
            Wi = nW * 512
            sk = (tag, "score")
            for h in range(16):
                pair, base = h // 2, (h % 2) * 64
                for w in range(nW):
                    p = cnt["ips"] % 2
                    cnt["ips"] += 1
                    r = cnt["r"] % 2
                    cnt["r"] += 1
                    MM(kb, ips[p][:], qib[s][base:base + 64, pair, :], kiT[base:base + 64, w * 512:(w + 1) * 512],
                       True, True, [(tag, "qi"), (tag, "kiT")], [(tag, "ips", p)])
                    ACT(kb, rbuf[r][:], ips[p][:], AF.Relu, [(tag, "ips", p), (tag, "wab", s)], [(tag, "r", r)],
                        scale=wab[s][:, h:h + 1])
                    if h == 0:
                        TS(kb, "dve", score[s][:, w * 512:(w + 1) * 512], rbuf[r][:], sgn[s][:, 0:1], None, ALU.mult,
                           None, [(tag, "r", r), (tag, "sgn", s)], [sk])
                    else:
                        STT(kb, score[s][:, w * 512:(w + 1) * 512], rbuf[r][:], sgn[s][:, h:h + 1],
                            score[s][:, w * 512:(w + 1) * 512], ALU.mult, ALU.add,
                            [(tag, "r", r), (tag, "sgn", s), sk], [sk])
            b = bs[s]
            bk = (tag, "bs", s)
            TS(kb, "dve", junk[:, 0:Wi], score[s][:, 0:Wi], 1.0, None, ALU.mult, ALU.max, [sk], [(tag, "mask"), bk],
               accum_out=b[:, 0:1])
            TS(kb, "dve", junk[:, 0:Wi], score[s][:, 0:Wi], -1.0, None, ALU.mult, ALU.max, [sk], [(tag, "mask"), bk],
               accum_out=b[:, 6:7])
            TT(kb, "dve", b[:, 0:1], b[:, 0:1], b[:, 6:7], ALU.max, [bk], [bk])
            TT(kb, "dve", score[s][:, Wi - 512:Wi], score[s][:, Wi - 512:Wi], C["cbase"][:, 384 - (i % 4) * 128:896 - (i % 4) * 128], ALU.add,
               [sk], [sk])
            TS(kb, "dve", b[:, 1:2], b[:, 0:1], -1.001, -1e-20, ALU.mult, ALU.add, [bk], [bk])
            TS(kb, "dve", b[:, 2:3], b[:, 0:1], 2.002, 2e-20, ALU.mult, ALU.add, [bk], [bk])
            for k in range(1, N_BIS + 1):
                f = 2.0 ** -k
                STT(kb, b[:, 3:4], b[:, 2:3], f, b[:, 1:2], ALU.mult, ALU.add, [bk], [bk])
                TS(kb, "dve", junk[:, 0:Wi], score[s][:, 0:Wi], b[:, 3:4], None, ALU.is_ge, ALU.add,
                   [sk, bk], [(tag, "mask"), bk], accum_out=b[:, 4:5])
                TS(kb, "dve", b[:, 5:6], b[:, 4:5], 255.5, f, ALU.is_ge, ALU.mult, [bk], [bk])
                STT(kb, b[:, 1:2], b[:, 5:6], b[:, 2:3], b[:, 1:2], ALU.mult, ALU.add, [bk], [bk])

        def stage_t(i):
            s = i % 2
            n = i + 1
            TS(kb, "dve", mask[:, 0:n * 128], score[s][:, 0:n * 128], bs[s][:, 1:2], None, ALU.is_ge, None,
               [(tag, "score"), (tag, "bs", s)], [(tag, "mask")])
            for j0 in range(0, n, 4):
                nb = min(4, n - j0)
                for jj in range(nb):
                    j = j0 + jj
                    TR(kb, tps[:, jj, :], mask[:, j * 128:(j + 1) * 128], ident_b[:], [(tag, "mask"), (tag, "cst")],
                       [(tag, "ips", 0)])
                CP(kb, "act", maskT[s][:, j0:j0 + nb, :], tps[:, 0:nb, :], [(tag, "ips", 0)], [(tag, "maskT")])

        def stage_b(i):
            s = i % 2
            n = i + 1
            for h in range(8):
                o = cnt["po"] % 2
                cnt["po"] += 1
                for j0 in range(0, n, 4):
                    nb = min(4, n - j0)
                    p = cnt["lg"] % 2
                    cnt["lg"] += 1
                    x = cnt["pt"] % 2
                    cnt["pt"] += 1
                    for jj in range(nb):
                        j = j0 + jj
                        near = (i - j) <= 1
                        MM(kb, lg[p][:, jj, :], kT[:, h, j * 128:(j + 1) * 128], qb[s][:, h, :], True, not near,
                           [(tag, "kT"), (tag, "q", s)], [(tag, "lg", p)])
                        if near:
                            MM(kb, lg[p][:, jj, :], ident_b[:], biasS[:, h, i - j, :], False, True,
                               [(tag, "cst")], [(tag, "lg", p)])
                    ACT(kb, pt[x][:, 0:nb, :], lg[p][:, 0:nb, :], AF.Exp, [(tag, "lg", p)], [(tag, "pt", x)],
                        scale=SCALE_A, bias=C["cb"][:, h:h + 1])
                    TT(kb, "pool", ptm[x][:, 0:nb, :], pt[x][:, 0:nb, :], maskT[s][:, j0:j0 + nb, :], ALU.mult,
                       [(tag, "pt", x), (tag, "maskT")], [(tag, "ptm", x)])
                    for jj in range(nb):
                        j = j0 + jj
                        MM(kb, po[o][:], Vt[:, j, h * 128:(h + 1) * 128], ptm[x][:, jj, :], j == 0, j == n - 1,
                           [(tag, "V"), (tag, "ptm", x)], [(tag, "po", o)])
                        MM(kb, prs[o][:], ones_b[:], ptm[x][:, jj, :], j == 0, j == n - 1,
                           [(tag, "cst"), (tag, "ptm", x)], [(tag, "prs", o)])
                CP(kb, "act", ost[s][:, h, 0:128], po[o][:], [(tag, "po", o)], [(tag, "ost", h)])
                CP(kb, "act", ost[s][:, h, 128:256], prs[o][:], [(tag, "prs", o)], [(tag, "ost", h)])

        def stage_f(i):
            s = i % 2
            keys = [(tag, "ost", h) for h in range(8)]
            kb.dma("sp", chz, za[s][:], zav[:, :, i * 128:(i + 1) * 128], writes=[(tag, "za")])
            ACT(kb, za[s][:], za[s][:], AF.Silu, [(tag, "za")], [(tag, "za")])
            RECIP(kb, ost[s][:, :, 128:256], ost[s][:, :, 128:256], keys, keys)
            TT(kb, "dve", ost[s][:, :, 0:128], ost[s][:, :, 0:128], ost[s][:, :, 128:256], ALU.mult, keys, keys)
            TT(kb, "dve", yst[s][:], ost[s][:, :, 0:128], za[s][:], ALU.mult, keys + [(tag, "za")], [(tag, "yst")])
            kb.dma("pool", chy[s], yv[:, :, i * 128:(i + 1) * 128], yst[s][:], reads=[(tag, "yst")],
                   writes=[(tag, "y", i)])

        stage_a(0)
        stage_t(0)
        for i in range(NT):
            if i + 1 < NT:
                stage_a(i + 1)
            stage_b(i)
            if i + 1 < NT:
                stage_t(i + 1)
            stage_f(i)
        kb.end_phase()


def phase_gdn_prep(kb, C, pj, S):
    tag = "gp"
    with ExitStack() as st:
        W = T + 3
        raw = [kb.sb(f"gp_raw{i}", [128, W], F32, st) for i in range(2)]
        acc = [kb.sb(f"gp_acc{i}", [128, T], F32, st) for i in range(2)]
        sq = kb.sb("gp_sq", [128, T], F32, st)
        rn = kb.sb("gp_rn", [128, T], F32, st)
        ssp = kb.ps("gp_ssp", [128, T], F32, st)
        chl = [kb.chan(f"gp_l{i}") for i in range(2)]
        chs = [kb.chan(f"gp_s{i}") for i in range(2)]
        for cc in range(24):
            s = cc % 2
            r0 = 2048 + cc * 128
            MS(kb, "pool", raw[s][:, 0:3], 0.0, [(tag, "raw", s)])
            kb.dma("sp", chl[s], raw[s][:, 3:W], pj[r0:r0 + 128, :], writes=[(tag, "raw", s)])
            TS(kb, "dve", acc[s][:], raw[s][:, 3:W], C["cvw"][:, cc, 3:4], None, ALU.mult, None, [(tag, "raw", s)],
               [(tag, "acc", s)])
            for j in range(3):
                STT(kb, acc[s][:], raw[s][:, j:j + T], C["cvw"][:, cc, j:j + 1], acc[s][:], ALU.mult, ALU.add,
                    [(tag, "raw", s), (tag, "acc", s)], [(tag, "acc", s)])
            ACT(kb, acc[s][:], acc[s][:], AF.Silu, [(tag, "acc", s)], [(tag, "acc", s)])
            if cc < 16:
                ACT(kb, sq[:], acc[s][:], AF.Square, [(tag, "acc", s)], [(tag, "sq")])
                for b in range(8):
                    MM(kb, ssp[:, b * 512:(b + 1) * 512], C["ones_f"][:], sq[:, b * 512:(b + 1) * 512], True, True,
                       [(tag, "sq")], [(tag, "ssp")])
                ACT(kb, rn[:], ssp[:], AF.Sqrt, [(tag, "ssp")], [(tag, "rn")], bias=C["eps"][:, 0:1])
                RECIP(kb, rn[:], rn[:], [(tag, "rn")], [(tag, "rn")])
                STT(kb, acc[s][:], acc[s][:], (128 ** -0.5) if cc < 8 else 1.0, rn[:], ALU.mult, ALU.mult,
                    [(tag, "acc", s), (tag, "rn")], [(tag, "acc", s)])
            kb.dma("pool", chs[s], S["gqkv"][cc * 128:(cc + 1) * 128, :], acc[s][:], reads=[(tag, "acc", s)],
                   writes=[("gqkv", cc)])
        kb.end_phase()


import os
GDN_TILES = int(os.environ.get("GDN_TILES", "32"))
GDN_STOP = int(os.environ.get("GDN_STOP", "99"))
GDN_SUB = float(os.environ.get("GDN_SUB", "99"))


def phase_gdn(kb, C, pj, S, y_dram):
    with ExitStack() as stc:
        C = dict(C)
        chc = kb.chan("gd_const")
        for name, shape, dt in GDN_CONST_SPECS:
            d = kb.nc.dram_tensor(name, list(shape), dt, kind="ExternalInput").ap()
            t = kb.sb("c_" + name, shape, dt, stc)
            kb.dma("sp", chc, t[:], d, writes=[("const", name)])
            C[name] = t
        kb.end_phase()
        phase_gdn_prep(kb, C, pj, S)
        _phase_gdn_main(kb, C, pj, S, y_dram)


def _phase_gdn_main(kb, C, pj, S, y_dram):
    tag = "gd"
    H = 8
    with ExitStack() as st:
        def fb(name, shape=(128, H, 128), dt=F32):
            return kb.sb("gd_" + name, list(shape), dt, st)

        tm = fb("tm", (128, NT, 32))
        beta = fb("beta", (128, NT, 8))
        g = fb("g", (128, NT, 8))
        t1 = fb("t1", (128, NT, 8))
        t2 = fb("t2", (128, NT, 8))
        nA = fb("nA", (128, 8))
        qT, kT, vT = fb("qT"), fb("kT"), fb("vT")
        gd, egrow, gcr = fb("gdiag"), fb("egrow"), fb("gcr")
        P1, E1, E2 = fb("P1"), fb("E1"), fb("E2")
        X = [fb("X0"), fb("X1")]
        Y = [fb("Y0"), fb("Y1")]
        P, attnT = fb("P"), fb("attnT")
        vb, kbg, kd, kd1 = fb("vb"), fb("kbg"), fb("kd"), fb("kd1")
        smk = fb("smk", (128, 16))
        u, wT, qgT, vnew = fb("u"), fb("wT"), fb("qgT"), fb("vnew")
        Sst, oacc, zb = fb("S"), fb("oacc"), fb("zb")
        osq, orn = fb("osq"), fb("orn")
        yo = fb("yo", (128, H, 128), BF16)
        sm = fb("sm", (128, 64))
        pA = kb.ps("gd_pA", [128, H, 128], F32, st)
        pB = kb.ps("gd_pB", [128, H, 128], F32, st)
        pC = kb.ps("gd_pC", [128, H, 128], F32, st)
        pO = kb.ps("gd_pO", [128, H, 64], F32, st)
        psm = kb.ps("gd_psm", [128, 32], F32, st)
        ch = kb.chan("gd_l")
        chq = kb.chan("gd_q")
        chz = kb.chan("gd_z")
        chy = kb.chan("gd_y")
        K_ = lambda n: (tag, n)

        def bc_h(ap2d):
            return ap2d.unsqueeze(1).to_broadcast([128, H, 128])

        def bc_f(ap2d):
            return ap2d.unsqueeze(2).to_broadcast([128, H, 128])

        kb.dma("sp", ch, tm[:], S["tmaj"].rearrange("(n p) c -> p n c", p=128), writes=[K_("tm")])
        ACT(kb, beta[:], tm[:, :, 16:24], AF.Sigmoid, [K_("tm")], [K_("beta")])
        dtb = C["dtb_bc"][:].unsqueeze(1).to_broadcast([128, NT, 8])
        TT(kb, "dve", g[:], tm[:, :, 24:32], dtb, ALU.add, [K_("tm")], [K_("g")])
        TS(kb, "dve", t1[:], g[:], -1.0, None, ALU.mult, None, [K_("g")], [K_("t1")])
        TT(kb, "dve", t1[:], t1[:], g[:], ALU.max, [K_("t1"), K_("g")], [K_("t1")])
        ACT(kb, t1[:], t1[:], AF.Exp, [K_("t1")], [K_("t1")], scale=-1.0)
        TS(kb, "dve", t1[:], t1[:], 1.0, None, ALU.add, None, [K_("t1")], [K_("t1")])
        ACT(kb, t1[:], t1[:], AF.Ln, [K_("t1")], [K_("t1")])
        TS(kb, "dve", t2[:], g[:], 0.0, None, ALU.max, None, [K_("g")], [K_("t2")])
        TT(kb, "dve", t2[:], t2[:], t1[:], ALU.add, [K_("t1"), K_("t2")], [K_("t2")])
        ACT(kb, nA[:], C["alog_bc"][:], AF.Exp, [], [K_("nA")])
        TS(kb, "dve", nA[:], nA[:], -1.0, None, ALU.mult, None, [K_("nA")], [K_("nA")])
        TT(kb, "dve", g[:], t2[:], nA[:].unsqueeze(1).to_broadcast([128, NT, 8]), ALU.mult, [K_("t2"), K_("nA")],
           [K_("g")])
        MS(kb, "dve", Sst[:], 0.0, [K_("S")])
        MS(kb, "dve", vnew[:], 0.0, [K_("vnew")])
        gq = S["gqkv"]
        qv = gq[0:1024, :].rearrange("(h p) t -> p h t", p=128)
        kv = gq[1024:2048, :].rearrange("(h p) t -> p h t", p=128)
        vv = gq[2048:3072, :].rearrange("(h p) t -> p h t", p=128)
        zv = pj[5120:6144, :].rearrange("(h p) t -> p h t", p=128)
        yv = y_dram[1024:2048, :].rearrange("(h p) t -> p h t", p=128)

        for n in range(GDN_TILES if GDN_STOP > 0 else 0):
            c0, c1 = n * 128, (n + 1) * 128
            kb.dma("sp", chq, qT[:], qv[:, :, c0:c1], writes=[K_("qT")])
            kb.dma("sp", chq, kT[:], kv[:, :, c0:c1], writes=[K_("kT")])
            kb.dma("sp", chq, vT[:], vv[:, :, c0:c1], writes=[K_("vT")])
            kb.dma("sp", chz, zb[:], zv[:, :, c0:c1], writes=[K_("zb")])
            gn = g[:, n, :]
            bn = beta[:, n, :]
            MM(kb, psm[:, 0:8], C["U2"][:], gn, True, True, [K_("g")], [K_("psm")])
            MM(kb, psm[:, 8:16], C["Bsame"][:], gn, True, True, [K_("g")], [K_("psm")])
            MM(kb, psm[:, 16:24], C["Bsel0"][:], gn, True, True, [K_("g")], [K_("psm")])
            MM(kb, psm[:, 24:32], C["Bsel1"][:], gn, True, True, [K_("g")], [K_("psm")])
            CP(kb, "dve", sm[:, 0:32], psm[:], [K_("psm")], [K_("sm")])
            gc = sm[:, 0:8]
            ACT(kb, sm[:, 32:40], sm[:, 0:8], AF.Exp, [K_("sm")], [K_("sm")])
            TT(kb, "dve", sm[:, 40:48], sm[:, 8:16], sm[:, 0:8], ALU.subtract, [K_("sm")], [K_("sm")])
            ACT(kb, sm[:, 40:48], sm[:, 40:48], AF.Exp, [K_("sm")], [K_("sm")])
            ACT(kb, sm[:, 16:32], sm[:, 16:32], AF.Exp, [K_("sm")], [K_("sm")])
            TT(kb, "dve", sm[:, 48:56], sm[:, 32:40], bn, ALU.mult, [K_("sm"), K_("beta")], [K_("sm")])
            TS(kb, "dve", sm[:, 56:64], bn, -1.0, None, ALU.mult, None, [K_("beta")], [K_("sm")])
            if GDN_SUB <= 0:
                continue
            TT(kb, "dve", gd[:], bc_h(C["U2"][:]), bc_f(gn), ALU.mult, [K_("g")], [K_("gdiag")])
            for b in range(2):
                MM(kb, pA[:, 4 * b:4 * b + 4, :], C["ones_f"][:], gd[:, 4 * b:4 * b + 4, :], True, True,
                   [K_("gdiag")], [K_("pA")])
            if GDN_SUB <= 0.3:
                continue
            CP(kb, "dve", gcr[:], pA[:], [K_("pA")], [K_("gcr")])
            ACT(kb, egrow[:], gcr[:], AF.Exp, [K_("gcr")], [K_("egrow")])
            if GDN_SUB <= 0.4:
                continue
            TT(kb, "dve", P1[:], gcr[:], bc_f(gc), ALU.subtract, [K_("gcr"), K_("sm")], [K_("P1")])
            if GDN_SUB <= 0.5:
                continue
            TS(kb, "dve", E1[:], P1[:], 0.0, None, ALU.max, None, [K_("P1")], [K_("E1")])
            ACT(kb, E1[:], E1[:], AF.Exp, [K_("E1")], [K_("E1")], scale=-1.0)
            if GDN_SUB <= 0.6:
                continue
            TS(kb, "dve", E2[:], P1[:], 0.0, None, ALU.min, None, [K_("P1")], [K_("E2")])
            ACT(kb, E2[:], E2[:], AF.Exp, [K_("E2")], [K_("E2")])
            TT(kb, "dve", E1[:], E1[:], bc_h(C["MLs"][:]), ALU.mult, [K_("E1")], [K_("E1")])
            TT(kb, "dve", E1[:], E1[:], bc_f(sm[:, 56:64]), ALU.mult, [K_("E1"), K_("sm")], [K_("E1")])
            TT(kb, "dve", E2[:], E2[:], bc_h(C["MU"][:]), ALU.mult, [K_("E2")], [K_("E2")])
            if GDN_SUB <= 1:
                continue
            for h in range(H):
                MM(kb, pB[:, h, :], kT[:, h, :], kT[:, h, :], True, True, [K_("kT")], [K_("pB")])
            TT(kb, "dve", X[0][:], pB[:], E1[:], ALU.mult, [K_("pB"), K_("E1")], [K_("X0")])
            for h in range(H):
                MM(kb, pC[:, h, :], kT[:, h, :], qT[:, h, :], True, True, [K_("kT"), K_("qT")], [K_("pC")])
            TT(kb, "dve", attnT[:], pC[:], E2[:], ALU.mult, [K_("pC"), K_("E2")], [K_("attnT")])
            for h in range(H):
                TR(kb, pA[:, h, :], X[0][:, h, :], C["ident"][:], [K_("X0")], [K_("pA")])
            CP(kb, "dve", Y[0][:], pA[:], [K_("pA")], [K_("Y0")])
            TT(kb, "dve", P[:], Y[0][:], bc_h(C["ident"][:]), ALU.add, [K_("Y0")], [K_("P")])
            if GDN_SUB <= 2:
                continue
            cur = 0
            for lvl in range(5):
                nxt = 1 - cur
                xk, yk = K_(f"X{cur}"), K_(f"Y{cur}")
                xn, yn = K_(f"X{nxt}"), K_(f"Y{nxt}")
                for h in range(H):
                    MM(kb, pB[:, h, :], Y[cur][:, h, :], X[cur][:, h, :], True, True, [xk, yk], [K_("pB")])
                CP(kb, "dve", X[nxt][:], pB[:], [K_("pB")], [xn])
                if lvl < 4:
                    for h in range(H):
                        MM(kb, pC[:, h, :], X[cur][:, h, :], Y[cur][:, h, :], True, True, [xk, yk], [K_("pC")])
                    CP(kb, "dve", Y[nxt][:], pC[:], [K_("pC")], [yn])
                for h in range(H):
                    MM(kb, pA[:, h, :], X[nxt][:, h, :], P[:, h, :], True, True, [xn, K_("P")], [K_("pA")])
                TT(kb, "dve", P[:], P[:], pA[:], ALU.add, [K_("pA"), K_("P")], [K_("P")])
                cur = nxt
            if GDN_SUB <= 3:
                continue
            for h in range(H):
                TR(kb, pB[:, h, :], kT[:, h, :], C["ident"][:], [K_("kT")], [K_("pB")])
            TT(kb, "dve", kbg[:], pB[:], bc_f(sm[:, 48:56]), ALU.mult, [K_("pB"), K_("sm")], [K_("kbg")])
            TS(kb, "dve", smk[:, 0:8], sm[:, 40:48], C["Bsel0"][:, 0:1], None, ALU.mult, None, [K_("sm")], [K_("smk")])
            TS(kb, "dve", smk[:, 8:16], sm[:, 40:48], C["Bsel1"][:, 0:1], None, ALU.mult, None, [K_("sm")], [K_("smk")])
            TT(kb, "dve", kd[:], pB[:], bc_f(smk[:, 0:8]), ALU.mult, [K_("pB"), K_("smk")], [K_("kd")])
            TT(kb, "dve", kd1[:], pB[:], bc_f(smk[:, 8:16]), ALU.mult, [K_("pB"), K_("smk")], [K_("kd")])
            for h in range(H):
                TR(kb, pC[:, h, :], vT[:, h, :], C["ident"][:], [K_("vT")], [K_("pC")])
            TT(kb, "dve", vb[:], pC[:], bc_f(bn), ALU.mult, [K_("pC"), K_("beta")], [K_("vb")])
            for h in range(H):
                MM(kb, pA[:, h, :], P[:, h, :], vb[:, h, :], True, True, [K_("P"), K_("vb")], [K_("pA")])
            CP(kb, "dve", u[:], pA[:], [K_("pA")], [K_("u")])
            for h in range(H):
                MM(kb, pB[:, h, :], kbg[:, h, :], P[:, h, :], True, True, [K_("P"), K_("kbg")], [K_("pB")])
            CP(kb, "dve", wT[:], pB[:], [K_("pB")], [K_("wT")])
            TT(kb, "dve", qgT[:], qT[:], egrow[:], ALU.mult, [K_("qT"), K_("egrow")], [K_("qgT")])
            if GDN_STOP <= 1:
                continue
            for c in range(2):
                r0, r1 = c * 64, (c + 1) * 64
                for h in range(H):
                    MM(kb, pA[r0:r1, h, :], wT[:, h, r0:r1], Sst[:, h, :], True, True, [K_("wT"), K_("S")], [K_("pA")])
                if GDN_SUB == 10 or (GDN_SUB == 10.5 and c == 1):
                    continue
                TT(kb, "dve", vnew[r0:r1, :, :], u[r0:r1, :, :], pA[r0:r1, :, :], ALU.subtract, [K_("u"), K_("pA")],
                   [K_("vnew")])
                if GDN_SUB == 11:
                    continue
                for h in range(H):
                    MM(kb, pO[:, h, :], Sst[:, h, :], qgT[:, h, r0:r1], True, False, [K_("S"), K_("qgT")], [K_("pO")])
                    MM(kb, pO[:, h, :], vnew[r0:r1, h, :], attnT[r0:r1, h, r0:r1], False, True,
                       [K_("vnew"), K_("attnT")], [K_("pO")])
                if GDN_SUB == 12:
                    continue
                CP(kb, "dve", oacc[:, :, r0:r1], pO[:], [K_("pO")], [K_("oacc")])
                if GDN_STOP <= 2:
                    continue
                for h in range(H):
                    MM(kb, pB[:, h, :], (kd, kd1)[c][:, h, :], vnew[:, h, :], True, True, [K_("kd"), K_("vnew")],
                       [K_("pB")])
                TT(kb, "dve", Sst[:], Sst[:], bc_f(sm[:, 16 + 8 * c:24 + 8 * c]), ALU.mult, [K_("S"), K_("sm")],
                   [K_("S")])
                TT(kb, "dve", Sst[:], Sst[:], pB[:], ALU.add, [K_("S"), K_("pB")], [K_("S")])
            ACT(kb, osq[:], oacc[:], AF.Square, [K_("oacc")], [K_("osq")])
            for b in range(2):
                MM(kb, pC[:, 4 * b:4 * b + 4, :], C["ones_f"][:], osq[:, 4 * b:4 * b + 4, :], True, True, [K_("osq")],
                   [K_("pC")])
            CP(kb, "dve", orn[:], pC[:], [K_("pC")], [K_("orn")])
            ACT(kb, orn[:], orn[:], AF.Sqrt, [K_("orn")], [K_("orn")], bias=C["eps"][:, 0:1], scale=1.0 / 128)
            RECIP(kb, orn[:], orn[:], [K_("orn")], [K_("orn")])
            STT(kb, oacc[:], oacc[:], C["onorm"][:, 0:1], orn[:], ALU.mult, ALU.mult, [K_("oacc"), K_("orn")],
                [K_("oacc")])
            ACT(kb, zb[:], zb[:], AF.Silu, [K_("zb")], [K_("zb")])
            TT(kb, "dve", yo[:], oacc[:], zb[:], ALU.mult, [K_("oacc"), K_("zb")], [K_("yo")])
            kb.dma("pool", chy, yv[:, :, c0:c1], yo[:], reads=[K_("yo")], writes=[("y0b", n)])
        kb.end_phase()

def load_consts(kb, nc, names_shapes):
    C = {}
    ch = kb.chan("const")
    for name, shape, dt in names_shapes:
        d = nc.dram_tensor(name, list(shape), dt, kind="ExternalInput").ap()
        t = kb.sb("c_" + name, shape, dt)
        kb.dma("sp", ch, t[:], d, writes=[("const", name)])
        C[name] = t
    kb.end_phase()
    with ExitStack() as st:
        d = nc.dram_tensor("biasT", [128, 8, 2, 128], F32, kind="ExternalInput").ap()
        C["biasS"] = kb.sb("c_biasS", [128, 8, 2, 128], BF16)
        bt = kb.sb("biasT_tmp", [128, 8, 2, 128], F32, st)
        kb.dma("sp", ch, bt[:], d, writes=[("const", "biasT")])
        for h in range(8):
            TS(kb, "dve", C["biasS"][:, h, :, :], bt[:, h, :, :], C["cb"][:, h:h + 1], 128 ** 0.5,
               ALU.subtract, ALU.mult, [("const", "biasT")], [("const", "biasS")])
        kb.end_phase()
    return C


CONST_SPECS = [
    ("ident", (128, 128), F32),
    ("ones_f", (128, 128), F32),
    ("eps", (128, 1), F32),
    ("normwT", (128, 2, 16), F32),
    ("dww", (128, 8, 31), F32),
    ("dwb", (128, 8), F32),
    ("lnw", (128, 8), F32),
    ("lnb", (128, 8), F32),
    ("dcw", (128, 8, 3), F32),
    ("cbase", (128, 896), F32),
    ("cb", (128, 8), F32),
    ("qnormT", (128, 4), F32),
    ("kvnormT", (128, 2), F32),
    ("qgain", (128, 1), F32),
    ("kgain", (128, 1), F32),
    ("ikw", (128, 1), F32),
    ("ikb", (128, 1), F32),
]

GDN_CONST_SPECS = [
    ("cvw", (128, 24, 4), F32),
    ("alog_bc", (128, 8), F32),
    ("dtb_bc", (128, 8), F32),
    ("onorm", (128, 1), F32),
    ("U2", (128, 128), F32),
    ("Bsame", (128, 128), F32),
    ("Bsel0", (128, 128), F32),
    ("Bsel1", (128, 128), F32),
    ("MLs", (128, 128), F32),
    ("MU", (128, 128), F32),
]


def build_program(layers=(0, 1), l0_parts=("a", "b"), debug_out=False):
    nc = bass.Bass("TRN2", target_bir_lowering=False)

    def din(name, shape, dt=F32):
        return nc.dram_tensor(name, list(shape), dt, kind="ExternalInput").ap()

    def dscr(name, shape, dt=F32):
        return nc.dram_tensor(name, list(shape), dt, kind="Internal").ap()

    x = din("x", [T, D])
    out = nc.dram_tensor("out", [T, D], F32, kind="ExternalOutput").ap()
    kb = KB(nc)
    C = load_consts(kb, nc, CONST_SPECS)
    src = x
    if 0 in layers:
        ab_w_in = din("ab_w_in", [48, 128, 16 * 128])
        ab_w_out = din("ab_w_out", [16, 128, D])
        W = {"w_uqiq": din("w_uqiq", [16, 128, 4 * 128]), "w_uk": din("w_uk", [8, 128, 2 * 128]),
             "w_uv": din("w_uv", [2, 128, 1024])}
        pj0 = dscr("pj0", [6144, T])
        S = {"qT": dscr("s_qT", [8, 128, T], BF16), "qiT": dscr("s_qiT", [8, 128, T], BF16),
             "kT": dscr("s_kT", [8, 128, T], BF16), "V": dscr("s_V", [T, 1024], BF16),
             "kiT": dscr("s_kiT", [128, T], BF16), "tmaj": dscr("s_tmaj", [T, 32]),
             "gqkv": dscr("s_gqkv", [3072, T])}
        if debug_out:
            y0 = nc.dram_tensor("y0", [2048, T], BF16, kind="ExternalOutput").ap()
        else:
            y0 = dscr("y0", [2048, T], BF16)
        x1 = dscr("x1", [T, D]) if 1 in layers else out
        phase_inproj(kb, C, "l0", src, C["normwT"][:, 0, :], ab_w_in, 48, pj0)
        phase_ki_tmaj(kb, C, pj0, S)
        if "a" in l0_parts:
            phase_dsa_prep(kb, C, pj0, W, S)
            phase_dsa_attn(kb, C, pj0, S, y0)
        if "b" in l0_parts:
            phase_gdn(kb, C, pj0, S, y0)
        if not debug_out:
            phase_outproj(kb, C, "l0o", y0, ab_w_out, src, x1)
        src = x1
    if 1 in layers:
        cd_w_in = din("cd_w_in", [56, 128, 16 * 128])
        cd_w_out = din("cd_w_out", [16, 128, D])
        pj1 = dscr("pj1", [7168, T])
        y1 = dscr("y1", [2048, T], BF16)
        phase_inproj(kb, C, "l1", src, C["normwT"][:, 1, :], cd_w_in, 56, pj1)
        phase_cd_mix(kb, C, pj1, C, y1)
        phase_outproj(kb, C, "l1o", y1, cd_w_out, src, out)
    kb.finish()
    kb.emit()
    kb.close()
    return nc, kb


def tile_w_in(w, nch):
    K, N = w.shape
    assert N == nch * 128
    return np.ascontiguousarray(w.reshape(K // 128, 128, nch, 128).transpose(2, 1, 0, 3)).reshape(nch, 128, -1)


def t5_bucket_np(dist):
    import math
    max_exact = 16
    dd = np.maximum(dist, 1).astype(np.float32)
    large = max_exact + (np.log(dd / max_exact) / math.log(128 / max_exact) * (32 - max_exact)).astype(np.int32)
    large = np.minimum(large, 31)
    return np.where(dist < max_exact, dist, large)


def colT(v, k):
    return np.ascontiguousarray(np.asarray(v, np.float32).reshape(k, 128).T)


def host_consts(inp):
    f = np.float32
    c = {}
    c["ident"] = np.eye(128, dtype=f)
    c["ones_f"] = np.ones((128, 128), f)
    c["eps"] = np.full((128, 1), EPS, f)
    c["normwT"] = np.ascontiguousarray(inp["norm_w"].reshape(2, 16, 128).transpose(2, 0, 1)).astype(f)
    c["dww"] = np.ascontiguousarray(inp["c_dw_w"][0].reshape(31, 8, 128).transpose(2, 1, 0)).astype(f)
    c["dwb"] = colT(inp["c_dw_b"][0], 8)
    c["lnw"] = colT(inp["c_ln_w"][0], 8)
    c["lnb"] = colT(inp["c_ln_b"][0], 8)
    c["dcw"] = np.ascontiguousarray(inp["d_conv_w"][0].reshape(3, 8, 128).transpose(2, 1, 0)).astype(f)
    r = np.arange(128)[:, None]
    cc = np.arange(896)[None, :]
    c["cbase"] = np.where(cc <= r + 384, 0.0, -1e30).astype(f)
    kl = np.arange(128)[:, None, None]
    dd = np.arange(2)[None, :, None]
    ql = np.arange(128)[None, None, :]
    dist = np.maximum(dd * 128 + ql - kl, 0)
    bt = np.asarray(inp["rel_bias"], f)[t5_bucket_np(dist)]
    c["biasT"] = np.ascontiguousarray(bt.transpose(0, 3, 1, 2))
    c["cb"] = np.ascontiguousarray(np.broadcast_to(np.asarray(inp["rel_bias"], f)[31][None, :], (128, 8)))
    c["qnormT"] = colT(inp["a_q_norm"][0], 4)
    c["kvnormT"] = colT(inp["a_kv_norm"][0], 2)
    c["qgain"] = np.asarray(inp["a_q_gain"][0], f).reshape(128, 1).copy()
    c["kgain"] = np.asarray(inp["a_k_gain"][0], f).reshape(128, 1).copy()
    c["ikw"] = np.tile(np.asarray(inp["a_ik_norm_w"][0], f), 2).reshape(128, 1).copy()
    c["ikb"] = np.tile(np.asarray(inp["a_ik_norm_b"][0], f), 2).reshape(128, 1).copy()
    c["cvw"] = np.ascontiguousarray(np.asarray(inp["b_conv_w"][0], f).reshape(4, 24, 128).transpose(2, 1, 0))
    c["alog_bc"] = np.ascontiguousarray(np.broadcast_to(np.asarray(inp["b_a_log"][0], f)[None, :], (128, 8)))
    c["dtb_bc"] = np.ascontiguousarray(np.broadcast_to(np.asarray(inp["b_dt_bias"][0], f)[None, :], (128, 8)))
    c["onorm"] = np.asarray(inp["b_o_norm"][0], f).reshape(128, 1).copy()
    a = np.arange(128)
    same = (a[:, None] // 64) == (a[None, :] // 64)
    c["U2"] = (same & (a[:, None] <= a[None, :])).astype(f)
    c["Bsame"] = same.astype(f)
    c["Bsel0"] = np.ascontiguousarray(np.broadcast_to((a[:, None] < 64), (128, 128))).astype(f)
    c["Bsel1"] = np.ascontiguousarray(np.broadcast_to((a[:, None] >= 64), (128, 128))).astype(f)
    c["MLs"] = (same & (a[:, None] > a[None, :])).astype(f)
    c["MU"] = (same & (a[:, None] <= a[None, :])).astype(f)
    return c


def host_shared(inp, layers=(0, 1)):
    f = np.float32
    sh = host_consts(inp)
    if 0 in layers:
        w = np.asarray(inp["ab_w_in"][0], f)
        wp = np.zeros((D, 6144), f)
        wp[:, 0:832] = w[:, 0:832]
        wp[:, 896:912] = w[:, 832:848]
        wp[:, 912:928] = w[:, 4944:4960]
        wp[:, 1024:2048] = w[:, 848:1872]
        wp[:, 2048:5120] = w[:, 1872:4944]
        wp[:, 5120:6144] = w[:, 4960:5984]
        sh["ab_w_in"] = tile_w_in(wp, 48)
        sh["ab_w_out"] = np.ascontiguousarray(np.asarray(inp["ab_w_out"][0], f).reshape(16, 128, D))
        sh["w_uqiq"] = tile_w_in(np.concatenate([inp["a_w_uq"][0], inp["a_w_iq"][0]], axis=1).astype(f), 16)
        sh["w_uk"] = tile_w_in(np.asarray(inp["a_w_uk"][0], f), 8)
        sh["w_uv"] = np.ascontiguousarray(np.asarray(inp["a_w_uv"][0], f).reshape(2, 128, 1024))
    if 1 in layers:
        sh["cd_w_in"] = tile_w_in(np.asarray(inp["cd_w_in"][0], f), 56)
        sh["cd_w_out"] = np.ascontiguousarray(np.asarray(inp["cd_w_out"][0], f).reshape(16, 128, D))
    return sh


def kernel(**inputs):
    inp = {k: np.asarray(v) for k, v in inputs.items()}
    nc, kb = build_program()
    sh = host_shared(inp)
    x = np.ascontiguousarray(inp["x"], dtype=np.float32)
    in_maps = [dict(sh, x=x[b]) for b in range(8)]
    res = run_bass_kernel_spmd(nc, in_maps, core_ids=list(range(8)))
    return np.stack([np.asarray(r["out"], np.float32) for r in res.results], axis=0)
```

```python
from contextlib import ExitStack
import numpy as np
import concourse.bass as bass
import concourse.mybir as mybir
from concourse.bass_utils import run_bass_kernel_spmd

F32 = mybir.dt.float32
BF16 = mybir.dt.bfloat16
ALU = mybir.AluOpType
AF = mybir.ActivationFunctionType
AX = mybir.AxisListType

T = 4096
D = 2048
NT = T // 128
EPS = 1e-6
ENGS = ("pe", "act", "dve", "pool", "sp")


class Chan:
    def __init__(self, sem, name):
        self.sem = sem
        self.name = name
        self.n = 0


class KB:
    def __init__(self, nc):
        self.nc = nc
        self.es = ExitStack()
        self.q = {e: [] for e in ENGS}
        self.sems = {}
        self.cnt = {}
        self.seen = {e: {} for e in ENGS}
        self.lastw = {}
        self.readers = {}
        self.chans = []
        self.chan_by_sem = {}
        self.nins = 0
        self.pending = {e: False for e in ENGS}
        for e in ENGS:
            self.sems[e] = self.es.enter_context(nc.semaphore("s_" + e))
            self.cnt[e] = 0

    def sb(self, name, shape, dt, stack=None):
        return (stack or self.es).enter_context(self.nc.sbuf_tensor(name, list(shape), dt))

    def ps(self, name, shape, dt=F32, stack=None):
        return (stack or self.es).enter_context(self.nc.psum_tensor(name, list(shape), dt))

    def chan(self, name):
        c = Chan(self.es.enter_context(self.nc.semaphore("c_" + name)), name)
        self.chans.append(c)
        self.chan_by_sem[id(c.sem)] = c
        return c

    def _need0(self, eng, sem, val):
        ch = self.chan_by_sem.get(id(sem))
        if ch is not None:
            val = max(val, 16 * ch.n)
        cur = self.seen[eng].get(id(sem), 0)
        if val > cur:
            self.seen[eng][id(sem)] = val
            self.q[eng].append(("wait", sem, val))

    def _deps(self, eng, reads, writes, my_sem):
        for r in reads:
            ev = self.lastw.get(r)
            if ev is not None:
                self._need(eng, ev[0], ev[1])
        for w in writes:
            ev = self.lastw.get(w)
            if ev is not None:
                self._need(eng, ev[0], ev[1])
            rd = self.readers.get(w)
            if rd:
                for sem, val in rd.values():
                    if sem is my_sem:
                        continue
                    self._need(eng, sem, val)

    def _need(self, eng, sem, val):
        if eng == "pe" and sem is self.sems["pe"]:
            return
        self._need0(eng, sem, val)

    def _commit(self, ev, reads, writes):
        for w in writes:
            self.lastw[w] = ev
            self.readers[w] = {}
        for r in reads:
            d = self.readers.setdefault(r, {})
            d[id(ev[0])] = ev

    def op(self, eng, fn, reads=(), writes=(), signal=True):
        sem = self.sems[eng]
        self._deps(eng, reads, writes, sem)
        if signal:
            self.cnt[eng] += 1
            self.pending[eng] = False
            ev = (sem, self.cnt[eng])
            self.q[eng].append(("ins", fn, sem, 1))
        else:
            self.pending[eng] = True
            ev = (sem, self.cnt[eng] + 1)
            self.q[eng].append(("ins0", fn))
        self._commit(ev, reads, writes)
        self.nins += 1
        return ev

    def dma(self, eng, ch, out, in_, reads=(), writes=(), **kw):
        self._deps(eng, reads, writes, None)
        ch.n += 1
        ev = (ch.sem, 16 * ch.n)
        self.q[eng].append(("ins", lambda e, o=out, i=in_, k=kw: e.dma_start(out=o, in_=i, **k), ch.sem, 16))
        self._commit(ev, reads, writes)
        self.nins += 1
        return ev

    def _flush_pending(self):
        for e in ENGS:
            assert not self.pending[e], "non-signaling op left pending at a barrier on " + e

    def _all_events(self):
        self._flush_pending()
        evs = [(self.sems[e], self.cnt[e]) for e in ENGS if self.cnt[e] > 0]
        evs += [(c.sem, 16 * c.n) for c in self.chans if c.n > 0]
        return evs

    def barrier(self):
        evs = self._all_events()
        for e in ENGS:
            for sem, val in evs:
                self._need(e, sem, val)

    def finish(self, final_eng="sp"):
        for sem, val in self._all_events():
            self._need(final_eng, sem, val)

    def emit(self):
        nc = self.nc
        q = self.q
        self.q = {e: [] for e in ENGS}

        def replay(eng_obj, items):
            for it in items:
                if it[0] == "wait":
                    eng_obj.wait_ge(it[1], it[2])
                elif it[0] == "ins0":
                    it[1](eng_obj)
                else:
                    it[1](eng_obj).then_inc(it[2], it[3])

        with nc.Block() as block:
            @block.tensor
            def _(e):
                replay(e, q["pe"])

            @block.scalar
            def _(e):
                replay(e, q["act"])

            @block.vector
            def _(e):
                replay(e, q["dve"])

            @block.gpsimd
            def _(e):
                replay(e, q["pool"])

            @block.sync
            def _(e):
                replay(e, q["sp"])

    def end_phase(self):
        self.barrier()
        self.emit()

    def close(self):
        self.es.close()


def MM(kb, out, lhsT, rhs, start, stop, reads, writes, signal=None):
    if signal is None:
        signal = stop
    return kb.op("pe", lambda e: e.matmul(out, lhsT, rhs, start=start, stop=stop), reads, writes, signal=signal)


def TR(kb, out, in_, ident, reads, writes, signal=True):
    return kb.op("pe", lambda e: e.transpose(out, in_, ident), reads, writes, signal=signal)


def ACT(kb, out, in_, func, reads, writes, bias=None, scale=None, accum_out=None):
    kw = {}
    if bias is not None:
        kw["bias"] = bias
    if scale is not None:
        kw["scale"] = scale
    if accum_out is not None:
        kw["accum_out"] = accum_out
    return kb.op("act", lambda e: e.activation(out=out, in_=in_, func=func, **kw), reads, writes)


def TS(kb, eng, out, in0, s1, s2, op0, op1, reads, writes, accum_out=None):
    kw = {}
    if op1 is not None:
        kw["op1"] = op1
    if accum_out is not None:
        kw["accum_out"] = accum_out
    return kb.op(eng, lambda e: e.tensor_scalar(out=out, in0=in0, scalar1=s1, scalar2=s2, op0=op0, **kw),
                 reads, writes)


def TT(kb, eng, out, in0, in1, op, reads, writes):
    return kb.op(eng, lambda e: e.tensor_tensor(out=out, in0=in0, in1=in1, op=op), reads, writes)


def STT(kb, out, in0, scalar, in1, op0, op1, reads, writes):
    return kb.op("dve", lambda e: e.scalar_tensor_tensor(out=out, in0=in0, scalar=scalar, in1=in1,
                                                         op0=op0, op1=op1), reads, writes)


def CP(kb, eng, out, in_, reads, writes):
    if eng == "act":
        return kb.op("act", lambda e: e.copy(out=out, in_=in_), reads, writes)
    return kb.op(eng, lambda e: e.tensor_copy(out=out, in_=in_), reads, writes)


def MS(kb, eng, ap, val, writes):
    return kb.op(eng, lambda e: e.memset(ap, val), (), writes)


def RECIP(kb, out, in_, reads, writes):
    return kb.op("dve", lambda e: e.reciprocal(out=out, in_=in_), reads, writes)


def phase_norm_T(kb, C, x_dram, normwT, hT, tag):
    with ExitStack() as st:
        xt = [kb.sb(f"{tag}_xt{i}", [128, D], F32, st) for i in range(2)]
        xn = [kb.sb(f"{tag}_xn{i}", [128, D], F32, st) for i in range(2)]
        sq = kb.sb(f"{tag}_sq", [128, D], BF16, st)
        sm = [kb.sb(f"{tag}_sm{i}", [128, 4], F32, st) for i in range(2)]
        pst = [kb.ps(f"{tag}_pt{i}", [128, 8, 128], F32, st) for i in range(2)]
        ch = [kb.chan(f"{tag}_x{i}") for i in range(2)]
        for tt in range(NT):
            s = tt % 2
            kb.dma("sp", ch[s], xt[s][:], x_dram[tt * 128:(tt + 1) * 128, :], writes=[(tag, "xt", s)])
            ACT(kb, sq[:], xt[s][:], AF.Square, [(tag, "xt", s)], [(tag, "sq"), (tag, "ss", s)],
                accum_out=sm[s][:, 0:1])
            ACT(kb, sm[s][:, 1:2], sm[s][:, 0:1], AF.Sqrt, [(tag, "ss", s)], [(tag, "sd", s)],
                bias=C["eps"][:, 0:1], scale=1.0 / D)
            RECIP(kb, sm[s][:, 2:3], sm[s][:, 1:2], [(tag, "sd", s)], [(tag, "rs", s)])
            TS(kb, "dve", xn[s][:], xt[s][:], sm[s][:, 2:3], None, ALU.mult, None,
               [(tag, "xt", s), (tag, "rs", s)], [(tag, "xn", s)])
            for half in range(2):
                p = half
                for kk in range(8):
                    k = half * 8 + kk
                    TR(kb, pst[p][:, kk, :], xn[s][:, k * 128:(k + 1) * 128], C["ident"][:],
                       [(tag, "xn", s)], [(tag, "pt", p)], signal=(kk == 7))
                nw = normwT[:, half * 8:(half + 1) * 8].unsqueeze(2).to_broadcast([128, 8, 128])
                TT(kb, "dve", hT[:, half * 8:(half + 1) * 8, tt * 128:(tt + 1) * 128], pst[p][:], nw,
                   ALU.mult, [(tag, "pt", p)], [("hT", tt)])
        kb.end_phase()


def gemm_fm(kb, tag, actT, act_keys, KC, w_dram, NCH, sink):
    with ExitStack() as st:
        wb = [kb.sb(f"{tag}_wb{i}", [128, KC * 128], BF16, st) for i in range(2)]
        wch = [kb.chan(f"{tag}_w{i}") for i in range(2)]
        pss = [kb.ps(f"{tag}_ps{i}", [128, 1024], F32, st) for i in range(2)]
        it = 0
        for c in range(NCH):
            s = c % 2
            kb.dma("pool", wch[s], wb[s][:], w_dram[c], writes=[(tag, "wb", s)])
            for ts in range(4):
                p = it % 2
                it += 1
                for b in range(2):
                    t0 = ts * 1024 + b * 512
                    for k in range(KC):
                        MM(kb, pss[p][:, b * 512:(b + 1) * 512], wb[s][:, k * 128:(k + 1) * 128],
                           actT[:, k, t0:t0 + 512], k == 0, k == KC - 1,
                           [(tag, "wb", s)] + act_keys, [(tag, "ps", p)])
                sink(c, ts, pss[p], (tag, "ps", p), st)
        kb.end_phase()


class StoreSink:
    def __init__(self, kb, tag, dst, st):
        self.kb = kb
        self.tag = tag
        self.dst = dst
        self.stg = [kb.sb(f"{tag}_stg{i}", [128, 1024], F32, st) for i in range(3)]
        self.ch = [kb.chan(f"{tag}_st{i}") for i in range(3)]
        self.i = 0

    def __call__(self, c, ts, ps, pkey, st):
        kb = self.kb
        s = self.i % 3
        eng = "act" if self.i % 2 == 0 else "dve"
        self.i += 1
        CP(kb, eng, self.stg[s][:], ps[:], [pkey], [(self.tag, "stg", s)])
        kb.dma("sp", self.ch[s], self.dst[c * 128:(c + 1) * 128, ts * 1024:(ts + 1) * 1024], self.stg[s][:],
               reads=[(self.tag, "stg", s)], writes=[(self.tag, "dst", c)])


def phase_inproj(kb, C, tag, x_dram, normwT, w_dram, NCH, pj):
    with ExitStack() as st:
        hT = kb.sb(f"{tag}_hT", [128, 16, T], BF16, st)
        phase_norm_T(kb, C, x_dram, normwT, hT, tag + "n")
        with ExitStack() as st2:
            sink = StoreSink(kb, tag + "s", pj, st2)
            gemm_fm(kb, tag + "g", hT, [], 16, w_dram, NCH, sink)


def phase_outproj(kb, C, tag, yT_dram, wo_dram, xres_dram, out_dram):
    with ExitStack() as st:
        wo = kb.sb(f"{tag}_wo", [128, 16, D], BF16, st)
        wch = kb.chan(f"{tag}_w")
        for k in range(16):
            kb.dma("pool", wch, wo[:, k, :], wo_dram[k], writes=[(tag, "wo")])
        yt = [kb.sb(f"{tag}_yt{i}", [128, 16, 128], BF16, st) for i in range(2)]
        xr = [kb.sb(f"{tag}_xr{i}", [128, D], F32, st) for i in range(2)]
        ot = [kb.sb(f"{tag}_ot{i}", [128, D], F32, st) for i in range(2)]
        ps = [kb.ps(f"{tag}_ps{i}", [128, 512], F32, st) for i in range(4)]
        chy = [kb.chan(f"{tag}_y{i}") for i in range(2)]
        chx = [kb.chan(f"{tag}_x{i}") for i in range(2)]
        cho = [kb.chan(f"{tag}_o{i}") for i in range(2)]
        yv = yT_dram.rearrange("(k p) t -> p k t", p=128)
        for tt in range(NT):
            s = tt % 2
            kb.dma("sp", chy[s], yt[s][:], yv[:, :, tt * 128:(tt + 1) * 128], writes=[(tag, "yt", s)])
            kb.dma("sp", chx[s], xr[s][:], xres_dram[tt * 128:(tt + 1) * 128, :], writes=[(tag, "xr", s)])
            for nb in range(4):
                for k in range(16):
                    MM(kb, ps[nb][:], yt[s][:, k, :], wo[:, k, nb * 512:(nb + 1) * 512], k == 0, k == 15,
                       [(tag, "yt", s), (tag, "wo")], [(tag, "ps", nb)])
                TT(kb, "dve", ot[s][:, nb * 512:(nb + 1) * 512], ps[nb][:], xr[s][:, nb * 512:(nb + 1) * 512],
                   ALU.add, [(tag, "ps", nb), (tag, "xr", s)], [(tag, "ot", s)])
            kb.dma("pool", cho[s], out_dram[tt * 128:(tt + 1) * 128, :], ot[s][:],
                   reads=[(tag, "ot", s)], writes=[(tag, "out", tt)])
        kb.end_phase()


def phase_cd_mix(kb, C, pj, cw, y_dram):
    HB = 2048
    tag = "cd"
    with ExitStack() as st:
        uc = kb.sb("cd_uc", [128, 8, HB], F32, st)
        ab = [kb.sb(f"cd_a{i}", [128, 30 + HB], F32, st) for i in range(2)]
        gb = [kb.sb(f"cd_g{i}", [128, 30 + HB], F32, st) for i in range(2)]
        sq = [kb.sb(f"cd_sq{i}", [128, HB], F32, st) for i in range(2)]
        mean = kb.sb("cd_mean", [128, HB], F32, st)
        rstd = kb.sb("cd_rstd", [128, HB], F32, st)
        m2 = kb.sb("cd_m2", [128, HB], F32, st)
        zb = [kb.sb(f"cd_z{i}", [128, HB], F32, st) for i in range(2)]
        yb = [kb.sb(f"cd_y{i}", [128, HB], BF16, st) for i in range(2)]
        ps_sum = kb.ps("cd_pss", [128, HB], F32, st)
        ps_ssq = kb.ps("cd_psq", [128, HB], F32, st)
        cha = [kb.chan(f"cd_a{i}") for i in range(2)]
        chg = [kb.chan(f"cd_g{i}") for i in range(2)]
        chz = [kb.chan(f"cd_z{i}") for i in range(2)]
        chy = [kb.chan(f"cd_y{i}") for i in range(2)]
        for th in range(2):
            t0 = th * HB
            for cc in range(8):
                s = cc % 2
                r0 = cc * 128
                if th == 0:
                    MS(kb, "pool", ab[s][:, 0:30], 0.0, [(tag, "a", s)])
                    MS(kb, "pool", gb[s][:, 0:30], 0.0, [(tag, "g", s)])
                    kb.dma("sp", cha[s], ab[s][:, 30:30 + HB], pj[r0:r0 + 128, 0:HB], writes=[(tag, "a", s)])
                    kb.dma("sp", chg[s], gb[s][:, 30:30 + HB], pj[1024 + r0:1024 + r0 + 128, 0:HB],
                           writes=[(tag, "g", s)])
                else:
                    kb.dma("sp", cha[s], ab[s][:], pj[r0:r0 + 128, t0 - 30:t0 + HB], writes=[(tag, "a", s)])
                    kb.dma("sp", chg[s], gb[s][:], pj[1024 + r0:1024 + r0 + 128, t0 - 30:t0 + HB],
                           writes=[(tag, "g", s)])
                ACT(kb, gb[s][:], gb[s][:], AF.Sigmoid, [(tag, "g", s)], [(tag, "g", s)])
                TT(kb, "dve", ab[s][:], ab[s][:], gb[s][:], ALU.mult, [(tag, "a", s), (tag, "g", s)], [(tag, "a", s)])
                TS(kb, "dve", uc[:, cc, :], ab[s][:, 30:30 + HB], cw["dww"][:, cc, 30:31], cw["dwb"][:, cc:cc + 1],
                   ALU.mult, ALU.add, [(tag, "a", s)], [(tag, "uc", cc)])
                for j in range(30):
                    STT(kb, uc[:, cc, :], ab[s][:, j:j + HB], cw["dww"][:, cc, j:j + 1], uc[:, cc, :],
                        ALU.mult, ALU.add, [(tag, "a", s), (tag, "uc", cc)], [(tag, "uc", cc)])
                ACT(kb, sq[s][:], uc[:, cc, :], AF.Square, [(tag, "uc", cc)], [(tag, "sq", s)])
                for tb in range(4):
                    MM(kb, ps_sum[:, tb * 512:(tb + 1) * 512], C["ones_f"][:], uc[:, cc, tb * 512:(tb + 1) * 512],
                       cc == 0, cc == 7, [(tag, "uc", cc)], [(tag, "pss")], signal=(tb == 3))
                    MM(kb, ps_ssq[:, tb * 512:(tb + 1) * 512], C["ones_f"][:], sq[s][:, tb * 512:(tb + 1) * 512],
                       cc == 0, cc == 7, [(tag, "sq", s)], [(tag, "psq")], signal=(tb == 3))
            ACT(kb, mean[:], ps_sum[:], AF.Copy, [(tag, "pss")], [(tag, "mean")], scale=1.0 / 1024)
            TT(kb, "dve", m2[:], mean[:], mean[:], ALU.mult, [(tag, "mean")], [(tag, "m2")])
            STT(kb, m2[:], ps_ssq[:], 1.0 / 1024, m2[:], ALU.mult, ALU.subtract, [(tag, "psq"), (tag, "m2")],
                [(tag, "m2")])
            ACT(kb, m2[:], m2[:], AF.Sqrt, [(tag, "m2")], [(tag, "m2")], bias=C["eps"][:, 0:1])
            RECIP(kb, rstd[:], m2[:], [(tag, "m2")], [(tag, "rstd")])
            for cc in range(8):
                s = cc % 2
                r0 = cc * 128
                kb.dma("sp", chz[s], zb[s][:], pj[2048 + r0:2048 + r0 + 128, t0:t0 + HB], writes=[(tag, "z", s)])
                TT(kb, "dve", uc[:, cc, :], uc[:, cc, :], mean[:], ALU.subtract, [(tag, "uc", cc), (tag, "mean")],
                   [(tag, "uc", cc)])
                TT(kb, "dve", uc[:, cc, :], uc[:, cc, :], rstd[:], ALU.mult, [(tag, "uc", cc), (tag, "rstd")],
                   [(tag, "uc", cc)])
                ACT(kb, uc[:, cc, :], uc[:, cc, :], AF.Silu, [(tag, "uc", cc)], [(tag, "uc", cc)],
                    scale=cw["lnw"][:, cc:cc + 1], bias=cw["lnb"][:, cc:cc + 1])
                ACT(kb, zb[s][:], zb[s][:], AF.Silu, [(tag, "z", s)], [(tag, "z", s)])
                TT(kb, "dve", yb[s][:], uc[:, cc, :], zb[s][:], ALU.mult, [(tag, "uc", cc), (tag, "z", s)],
                   [(tag, "y", s)])
                kb.dma("pool", chy[s], y_dram[r0:r0 + 128, t0:t0 + HB], yb[s][:], reads=[(tag, "y", s)],
                       writes=[(tag, "yd", cc, th)])
        kb.end_phase()
    tag = "sc"
    with ExitStack() as st:
        W = T + 2
        bg = [kb.sb(f"sc_b{i}", [128, T], F32, st) for i in range(2)]
        cg = [kb.sb(f"sc_c{i}", [128, W], F32, st) for i in range(2)]
        ud = [kb.sb(f"sc_u{i}", [128, W], F32, st) for i in range(2)]
        zd = [kb.sb(f"sc_z{i}", [128, T], F32, st) for i in range(2)]
        acc = [kb.sb(f"sc_acc{i}", [128, T], F32, st) for i in range(2)]
        yb = [kb.sb(f"sc_y{i}", [128, T], BF16, st) for i in range(2)]
        chs = {n: [kb.chan(f"sc_{n}{i}") for i in range(2)] for n in ("b", "c", "u", "z", "y")}
        for cc in range(8):
            s = cc % 2
            r0 = cc * 128
            MS(kb, "pool", cg[s][:, 0:2], 0.0, [(tag, "c", s)])
            MS(kb, "pool", ud[s][:, 0:2], 0.0, [(tag, "u", s)])
            kb.dma("sp", chs["b"][s], bg[s][:], pj[3072 + r0:3072 + r0 + 128, :], writes=[(tag, "b", s)])
            kb.dma("sp", chs["c"][s], cg[s][:, 2:W], pj[4096 + r0:4096 + r0 + 128, :], writes=[(tag, "c", s)])
            kb.dma("sp", chs["u"][s], ud[s][:, 2:W], pj[5120 + r0:5120 + r0 + 128, :], writes=[(tag, "u", s)])
            kb.dma("sp", chs["z"][s], zd[s][:], pj[6144 + r0:6144 + r0 + 128, :], writes=[(tag, "z", s)])
            TT(kb, "dve", cg[s][:], cg[s][:], ud[s][:], ALU.mult, [(tag, "c", s), (tag, "u", s)], [(tag, "c", s)])
            TS(kb, "dve", acc[s][:], cg[s][:, 2:W], cw["dcw"][:, cc, 2:3], None, ALU.mult, None,
               [(tag, "c", s)], [(tag, "acc", s)])
            for j in range(2):
                STT(kb, acc[s][:], cg[s][:, j:j + T], cw["dcw"][:, cc, j:j + 1], acc[s][:], ALU.mult, ALU.add,
                    [(tag, "c", s), (tag, "acc", s)], [(tag, "acc", s)])
            ACT(kb, zd[s][:], zd[s][:], AF.Silu, [(tag, "z", s)], [(tag, "z", s)])
            TT(kb, "pool", bg[s][:], bg[s][:], zd[s][:], ALU.mult, [(tag, "b", s), (tag, "z", s)], [(tag, "b", s)])
            TT(kb, "dve", yb[s][:], acc[s][:], bg[s][:], ALU.mult, [(tag, "acc", s), (tag, "b", s)], [(tag, "y", s)])
            kb.dma("pool", chs["y"][s], y_dram[1024 + r0:1024 + r0 + 128, :], yb[s][:], reads=[(tag, "y", s)],
                   writes=[(tag, "yd", cc)])
        kb.end_phase()


def colnorm_phase(kb, C, tag, src, KC, gT, outT):
    with ExitStack() as st:
        raw = [kb.sb(f"{tag}_raw{i}", [128, KC, 1024], F32, st) for i in range(2)]
        sq = [kb.sb(f"{tag}_sq{i}", [128, 1024], F32, st) for i in range(2)]
        rs = kb.sb(f"{tag}_rs", [128, 1024], F32, st)
        ssp = kb.ps(f"{tag}_ssp", [128, 1024], F32, st)
        ch = [kb.chan(f"{tag}_l{i}") for i in range(2)]
        sv = src.rearrange("(k p) t -> p k t", p=128)
        for ts in range(4):
            s = ts % 2
            kb.dma("sp", ch[s], raw[s][:], sv[:, :, ts * 1024:(ts + 1) * 1024], writes=[(tag, "raw", s)])
            for k in range(KC):
                q = k % 2
                ACT(kb, sq[q][:], raw[s][:, k, :], AF.Square, [(tag, "raw", s)], [(tag, "sq", q)])
                for b in range(2):
                    MM(kb, ssp[:, b * 512:(b + 1) * 512], C["ones_f"][:], sq[q][:, b * 512:(b + 1) * 512],
                       k == 0, k == KC - 1, [(tag, "sq", q)], [(tag, "ssp")], signal=True)
            ACT(kb, rs[:], ssp[:], AF.Sqrt, [(tag, "ssp")], [(tag, "rs")], bias=C["eps"][:, 0:1],
                scale=1.0 / (KC * 128))
            RECIP(kb, rs[:], rs[:], [(tag, "rs")], [(tag, "rs")])
            for k in range(KC):
                STT(kb, outT[:, k, ts * 1024:(ts + 1) * 1024], raw[s][:, k, :], gT[:, k:k + 1], rs[:],
                    ALU.mult, ALU.mult, [(tag, "raw", s), (tag, "rs")], [(tag, "out", k, ts)])
        kb.end_phase()


class HeadNormSink:
    def __init__(self, kb, C, tag, dst, gain, n_norm, raw_dst, st):
        self.kb, self.C, self.tag, self.dst, self.gain = kb, C, tag, dst, gain
        self.n_norm, self.raw_dst = n_norm, raw_dst
        self.sq = [kb.sb(f"{tag}_sq{i}", [128, 1024], F32, st) for i in range(2)]
        self.rs = [kb.sb(f"{tag}_rs{i}", [128, 1024], F32, st) for i in range(2)]
        self.ob = [kb.sb(f"{tag}_ob{i}", [128, 1024], BF16, st) for i in range(2)]
        self.ssp = kb.ps(f"{tag}_ssp", [128, 1024], F32, st)
        self.ch = [kb.chan(f"{tag}_o{i}") for i in range(2)]
        self.i = 0

    def __call__(self, c, ts, ps, pkey, st):
        kb, C, tag = self.kb, self.C, self.tag
        s = self.i % 2
        self.i += 1
        if c < self.n_norm:
            ACT(kb, self.sq[s][:], ps[:], AF.Square, [pkey], [(tag, "sq", s)])
            for b in range(2):
                MM(kb, self.ssp[:, b * 512:(b + 1) * 512], C["ones_f"][:], self.sq[s][:, b * 512:(b + 1) * 512],
                   True, True, [(tag, "sq", s)], [(tag, "ssp")])
            ACT(kb, self.rs[s][:], self.ssp[:], AF.Sqrt, [(tag, "ssp")], [(tag, "rs", s)], bias=C["eps"][:, 0:1],
                scale=1.0 / 128)
            RECIP(kb, self.rs[s][:], self.rs[s][:], [(tag, "rs", s)], [(tag, "rs", s)])
            STT(kb, self.ob[s][:], ps[:], self.gain, self.rs[s][:], ALU.mult, ALU.mult,
                [pkey, (tag, "rs", s)], [(tag, "ob", s)])
            d = self.dst[c]
        else:
            CP(kb, "act", self.ob[s][:], ps[:], [pkey], [(tag, "ob", s)])
            d = self.raw_dst[c - self.n_norm]
        kb.dma("sp", self.ch[s], d[:, ts * 1024:(ts + 1) * 1024], self.ob[s][:], reads=[(tag, "ob", s)],
               writes=[(tag, "dst", c)])


def phase_dsa_prep(kb, C, pj, W, S):
    with ExitStack() as st:
        cqn = kb.sb("cqn", [128, 4, T], BF16, st)
        colnorm_phase(kb, C, "cq", pj[0:512, :], 4, C["qnormT"], cqn)
        with ExitStack() as st2:
            sink = HeadNormSink(kb, C, "qs", S["qT"], C["qgain"][:, 0:1], 8, S["qiT"], st2)
            gemm_fm(kb, "qg", cqn, [], 4, W["w_uqiq"], 16, sink)
    with ExitStack() as st:
        ckvn = kb.sb("ckvn", [128, 2, T], BF16, st)
        colnorm_phase(kb, C, "ckv", pj[512:768, :], 2, C["kvnormT"], ckvn)
        with ExitStack() as st2:
            sink = HeadNormSink(kb, C, "ks", S["kT"], C["kgain"][:, 0:1], 8, None, st2)
            gemm_fm(kb, "kg", ckvn, [], 2, W["w_uk"], 8, sink)
        with ExitStack() as st2:
            wv = kb.sb("wv", [128, 2, 1024], BF16, st2)
            chw = kb.chan("wv")
            for k in range(2):
                kb.dma("pool", chw, wv[:, k, :], W["w_uv"][k], writes=[("wv",)])
            vps = [kb.ps(f"v_ps{i}", [128, 1024], F32, st2) for i in range(2)]
            vb = [kb.sb(f"v_b{i}", [128, 1024], BF16, st2) for i in range(2)]
            chv = [kb.chan(f"v_o{i}") for i in range(2)]
            for tt in range(NT):
                s = tt % 2
                for b in range(2):
                    for k in range(2):
                        MM(kb, vps[s][:, b * 512:(b + 1) * 512], ckvn[:, k, tt * 128:(tt + 1) * 128],
                           wv[:, k, b * 512:(b + 1) * 512], k == 0, k == 1, [("wv",)], [("v", "ps", s)])
                CP(kb, "act" if tt % 2 else "dve", vb[s][:], vps[s][:], [("v", "ps", s)], [("v", "b", s)])
                kb.dma("sp", chv[s], S["V"][tt * 128:(tt + 1) * 128, :], vb[s][:], reads=[("v", "b", s)],
                       writes=[("V", tt)])
            kb.end_phase()


def phase_ki_tmaj(kb, C, pj, S):
    with ExitStack() as st:
        ki = kb.sb("ki_raw", [128, T], F32, st)
        sq = kb.sb("ki_sq", [128, T], F32, st)
        mean = kb.sb("ki_mean", [128, 1024], F32, st)
        var = kb.sb("ki_var", [128, 1024], F32, st)
        kio = kb.sb("ki_o", [128, T], BF16, st)
        sm = kb.sb("ki_sm", [32, T], F32, st)
        tmo = kb.sb("ki_tmo", [128, NT, 32], F32, st)
        ps1 = kb.ps("ki_ps1", [128, 1024], F32, st)
        ps2 = kb.ps("ki_ps2", [128, 1024], F32, st)
        pst = kb.ps("ki_pst", [128, 16, 32], F32, st)
        ch = kb.chan("ki")
        kb.dma("sp", ch, ki[0:64, :], pj[768:832, :], writes=[("ki", "raw")])
        kb.dma("sp", ch, ki[64:128, :], pj[768:832, :], writes=[("ki", "raw")])
        kb.dma("sp", ch, sm[:], pj[896:928, :], writes=[("ki", "sm")])
        ACT(kb, sq[:], ki[:], AF.Square, [("ki", "raw")], [("ki", "sq")])
        for ts in range(4):
            for b in range(2):
                c0 = ts * 1024 + b * 512
                MM(kb, ps1[:, b * 512:(b + 1) * 512], C["ones_f"][0:64, :], ki[0:64, c0:c0 + 512], True, True,
                   [("ki", "raw")], [("ki", "ps1")])
                MM(kb, ps2[:, b * 512:(b + 1) * 512], C["ones_f"][0:64, :], sq[0:64, c0:c0 + 512], True, True,
                   [("ki", "sq")], [("ki", "ps2")])
            ACT(kb, mean[:], ps1[:], AF.Copy, [("ki", "ps1")], [("ki", "mean")], scale=1.0 / 64)
            TT(kb, "dve", var[:], mean[:], mean[:], ALU.mult, [("ki", "mean")], [("ki", "var")])
            STT(kb, var[:], ps2[:], 1.0 / 64, var[:], ALU.mult, ALU.subtract, [("ki", "ps2"), ("ki", "var")],
                [("ki", "var")])
            ACT(kb, var[:], var[:], AF.Sqrt, [("ki", "var")], [("ki", "var")], bias=C["eps"][:, 0:1])
            RECIP(kb, var[:], var[:], [("ki", "var")], [("ki", "var")])
            sl = slice(ts * 1024, (ts + 1) * 1024)
            TT(kb, "dve", ki[:, sl], ki[:, sl], mean[:], ALU.subtract, [("ki", "raw"), ("ki", "mean")], [("ki", "raw")])
            TT(kb, "dve", ki[:, sl], ki[:, sl], var[:], ALU.mult, [("ki", "raw"), ("ki", "var")], [("ki", "raw")])
            TS(kb, "dve", kio[:, sl], ki[:, sl], C["ikw"][:, 0:1], C["ikb"][:, 0:1], ALU.mult, ALU.add,
               [("ki", "raw")], [("ki", "o")])
        kb.dma("sp", ch, S["kiT"], kio[:], reads=[("ki", "o")], writes=[("kiT",)])
        for g in range(2):
            for tt in range(16):
                t = g * 16 + tt
                TR(kb, pst[:, tt, :], sm[:, t * 128:(t + 1) * 128], C["ident"][0:32, 0:32], [("ki", "sm")],
                   [("ki", "pst")], signal=(tt == 15))
            CP(kb, "dve", tmo[:, g * 16:(g + 1) * 16, :], pst[:], [("ki", "pst")], [("ki", "tmo")])
        kb.dma("sp", ch, S["tmaj"].rearrange("(n p) c -> p n c", p=128), tmo[:], reads=[("ki", "tmo")],
               writes=[("tmaj",)])
        kb.end_phase()


N_BIS = 16
SCALE_A = 128 ** -0.5


def phase_dsa_attn(kb, C, pj, S, y_dram):
    tag = "at"
    with ExitStack() as st:
        kT = kb.sb("at_kT", [128, 8, T], BF16, st)
        Vt = kb.sb("at_V", [128, NT, 1024], BF16, st)
        kiT = kb.sb("at_kiT", [128, T], BF16, st)
        score1 = kb.sb("at_sc", [128, T], F32, st)
        score = [score1, score1]
        mask = kb.sb("at_mask", [128, T], BF16, st)
        junk = mask
        maskT1 = kb.sb("at_maskT", [128, NT, 128], BF16, st)
        maskT = [maskT1, maskT1]
        rbuf = [kb.sb(f"at_r{i}", [128, 512], BF16, st) for i in range(2)]
        dsg = kb.sb("at_dsg", [128, 16, 128], BF16, st)
        qb = [kb.sb(f"at_q{i}", [128, 8, 128], BF16, st) for i in range(2)]
        qib1 = kb.sb("at_qi", [128, 8, 128], BF16, st)
        qib = [qib1, qib1]
        wt = [kb.sb(f"at_w{i}", [128, 32], F32, st) for i in range(2)]
        wab = [kb.sb(f"at_wab{i}", [128, 16], F32, st) for i in range(2)]
        sgn = [kb.sb(f"at_sgn{i}", [128, 16], F32, st) for i in range(2)]
        bs = [kb.sb(f"at_bs{i}", [128, 8], F32, st) for i in range(2)]
        za1 = kb.sb("at_za", [128, 8, 128], F32, st)
        za = [za1, za1]
        pt = [kb.sb(f"at_pt{i}", [128, 4, 128], BF16, st) for i in range(2)]
        ptm = [kb.sb(f"at_ptm{i}", [128, 4, 128], BF16, st) for i in range(2)]
        ost1 = kb.sb("at_ost", [128, 8, 256], F32, st)
        ost = [ost1, ost1]
        yst1 = kb.sb("at_yst", [128, 8, 128], BF16, st)
        yst = [yst1, yst1]
        biasS = C["biasS"]
        ident_b = kb.sb("at_identb", [128, 128], BF16, st)
        ones_b = kb.sb("at_onesb", [128, 128], BF16, st)
        ips = [kb.ps(f"at_ips{i}", [128, 512], F32, st) for i in range(2)]
        lg = [kb.ps(f"at_lg{i}", [128, 4, 128], F32, st) for i in range(2)]
        po1 = kb.ps("at_po", [128, 128], F32, st)
        prs1 = kb.ps("at_prs", [128, 128], F32, st)
        po, prs = [po1, po1], [prs1, prs1]
        sps = [kb.ps(f"at_sps{i}", [128, 512], F32, st) for i in range(2)]
        tps = ips[0][:].bitcast(BF16)[:, 0:512].rearrange("p (a b) -> p a b", b=128)
        chl = kb.chan("at_ld")
        chq = [kb.chan(f"at_q{i}") for i in range(2)]
        chy = [kb.chan(f"at_y{i}") for i in range(2)]
        chz = kb.chan("at_z")
        kb.dma("sp", chl, kT[:], S["kT"].rearrange("h p t -> p h t"), writes=[(tag, "kT")])
        kb.dma("sp", chl, Vt[:], S["V"].rearrange("(n p) c -> p n c", p=128), writes=[(tag, "V")])
        kb.dma("sp", chl, kiT[:], S["kiT"], writes=[(tag, "kiT")])
        CP(kb, "dve", ident_b[:], C["ident"][:], [], [(tag, "cst")])
        CP(kb, "dve", ones_b[:], C["ones_f"][:], [], [(tag, "cst")])
        qTv = S["qT"].rearrange("h p t -> p h t")
        qiTv = S["qiT"].rearrange("h p t -> p h t")
        zav = pj[1024:2048, :].rearrange("(h p) t -> p h t", p=128)
        yv = y_dram[0:1024, :].rearrange("(h p) t -> p h t", p=128)
        cnt = {"ips": 0, "r": 0, "lg": 0, "pt": 0, "po": 0, "sps": 0}

        def stage_a(i):
            s = i % 2
            c0, c1 = i * 128, (i + 1) * 128
            kb.dma("sp", chq[s], qb[s][:], qTv[:, :, c0:c1], writes=[(tag, "q", s)])
            kb.dma("sp", chq[s], qib[s][:], qiTv[:, :, c0:c1], writes=[(tag, "qi")])
            kb.dma("sp", chq[s], wt[s][:], S["tmaj"][c0:c1, :], writes=[(tag, "w", s)])
            TS(kb, "dve", sgn[s][:], wt[s][:, 0:16], 0.0, 2.0, ALU.is_ge, ALU.mult, [(tag, "w", s)], [(tag, "sgn", s)])
            TS(kb, "dve", sgn[s][:], sgn[s][:], -1.0, None, ALU.add, None, [(tag, "sgn", s)], [(tag, "sgn", s)])
            TT(kb, "dve", wab[s][:], wt[s][:, 0:16], sgn[s][:], ALU.mult, [(tag, "w", s), (tag, "sgn", s)],
               [(tag, "wab", s)])
            nW = (i + 4) // 4
            Wi = nW * 512
            sk = (tag, "score")
            TT(kb, "dve", dsg[:], ident_b[:].unsqueeze(1).to_broadcast([128, 16, 128]),
               sgn[s][:].unsqueeze(2).to_broadcast([128, 16, 128]), ALU.mult, [(tag, "sgn", s), (tag, "cst")],
               [(tag, "dsg")])
            for w in range(nW):
                sp_ = cnt["sps"] % 2
                cnt["sps"] += 1
                units = []

                def acc(u):
                    h_, r_ = u
                    MM(kb, sps[sp_][:], dsg[:, h_, :], rbuf[r_][:], h_ == 0, h_ == 15,
                       [(tag, "dsg"), (tag, "r", r_)], [(tag, "sps", sp_)])

                for h in range(16):
                    pair, base = h // 2, (h % 2) * 64
                    p = cnt["ips"] % 2
                    cnt["ips"] += 1
                    r = cnt["r"] % 2
                    cnt["r"] += 1
                    MM(kb, ips[p][:], qib[s][base:base + 64, pair, :], kiT[base:base + 64, w * 512:(w + 1) * 512],
                       True, True, [(tag, "qi"), (tag, "kiT")], [(tag, "ips", p)])
                    ACT(kb, rbuf[r][:], ips[p][:], AF.Relu, [(tag, "ips", p), (tag, "wab", s)], [(tag, "r", r)],
                        scale=wab[s][:, h:h + 1])
                    if units:
                        acc(units.pop())
                    units.append((h, r))
                acc(units.pop())
                CP(kb, "dve", score[s][:, w * 512:(w + 1) * 512], sps[sp_][:], [(tag, "sps", sp_)], [sk])
            b = bs[s]
            bk = (tag, "bs", s)
            TS(kb, "dve", junk[:, 0:Wi], score[s][:, 0:Wi], 1.0, None, ALU.mult, ALU.max, [sk], [(tag, "mask"), bk],
               accum_out=b[:, 0:1])
            TS(kb, "dve", junk[:, 0:Wi], score[s][:, 0:Wi], -1.0, None, ALU.mult, ALU.max, [sk], [(tag, "mask"), bk],
               accum_out=b[:, 6:7])
            TT(kb, "dve", b[:, 0:1], b[:, 0:1], b[:, 6:7], ALU.max, [bk], [bk])
            TT(kb, "dve", score[s][:, Wi - 512:Wi], score[s][:, Wi - 512:Wi],
               C["cbase"][:, 384 - (i % 4) * 128:896 - (i % 4) * 128], ALU.add, [sk], [sk])
            TS(kb, "dve", b[:, 1:2], b[:, 0:1], -1.001, -1e-20, ALU.mult, ALU.add, [bk], [bk])
            TS(kb, "dve", b[:, 2:3], b[:, 0:1], 2.002, 2e-20, ALU.mult, ALU.add, [bk], [bk])
            for k in range(1, N_BIS + 1):
                f = 2.0 ** -k
                STT(kb, b[:, 3:4], b[:, 2:3], f, b[:, 1:2], ALU.mult, ALU.add, [bk], [bk])
                TS(kb, "dve", junk[:, 0:Wi], score[s][:, 0:Wi], b[:, 3:4], None, ALU.is_ge, ALU.add,
                   [sk, bk], [(tag, "mask"), bk], accum_out=b[:, 4:5])
                TS(kb, "dve", b[:, 5:6], b[:, 4:5], 255.5, f, ALU.is_ge, ALU.mult, [bk], [bk])
                STT(kb, b[:, 1:2], b[:, 5:6], b[:, 2:3], b[:, 1:2], ALU.mult, ALU.add, [bk], [bk])

        def stage_t(i):
            s = i % 2
            n = i + 1
            TS(kb, "dve", mask[:, 0:n * 128], score[s][:, 0:n * 128], bs[s][:, 1:2], None, ALU.is_ge, None,
               [(tag, "score"), (tag, "bs", s)], [(tag, "mask")])
            for j0 in range(0, n, 4):
                nb = min(4, n - j0)
                for jj in range(nb):
                    j = j0 + jj
                    TR(kb, tps[:, jj, :], mask[:, j * 128:(j + 1) * 128], ident_b[:], [(tag, "mask"), (tag, "cst")],
                       [(tag, "ips", 0)], signal=(jj == nb - 1))
                CP(kb, "act", maskT[s][:, j0:j0 + nb, :], tps[:, 0:nb, :], [(tag, "ips", 0)], [(tag, "maskT")])

        def stage_b(i):
            s = i % 2
            n = i + 1
            for h in range(8):
                o = cnt["po"] % 2
                cnt["po"] += 1
                for j0 in range(0, n, 4):
                    nb = min(4, n - j0)
                    p = cnt["lg"] % 2
                    cnt["lg"] += 1
                    x = cnt["pt"] % 2
                    cnt["pt"] += 1
                    for jj in range(nb):
                        j = j0 + jj
                        near = (i - j) <= 1
                        MM(kb, lg[p][:, jj, :], kT[:, h, j * 128:(j + 1) * 128], qb[s][:, h, :], True, not near,
                           [(tag, "kT"), (tag, "q", s)], [(tag, "lg", p)], signal=(jj == nb - 1 and not near))
                        if near:
                            MM(kb, lg[p][:, jj, :], ident_b[:], biasS[:, h, i - j, :], False, True,
                               [(tag, "cst")], [(tag, "lg", p)], signal=(jj == nb - 1))
                    ACT(kb, pt[x][:, 0:nb, :], lg[p][:, 0:nb, :], AF.Exp, [(tag, "lg", p)], [(tag, "pt", x)],
                        scale=SCALE_A, bias=C["cb"][:, h:h + 1])
                    TT(kb, "pool", ptm[x][:, 0:nb, :], pt[x][:, 0:nb, :], maskT[s][:, j0:j0 + nb, :], ALU.mult,
                       [(tag, "pt", x), (tag, "maskT")], [(tag, "ptm", x)])
                    for jj in range(nb):
                        j = j0 + jj
                        MM(kb, po[o][:], Vt[:, j, h * 128:(h + 1) * 128], ptm[x][:, jj, :], j == 0, j == n - 1,
                           [(tag, "V"), (tag, "ptm", x)], [(tag, "po", 0)])
                        MM(kb, prs[o][:], ones_b[:], ptm[x][:, jj, :], j == 0, j == n - 1,
                           [(tag, "cst"), (tag, "ptm", x)], [(tag, "prs", 0)], signal=(jj == nb - 1))
                CP(kb, "act", ost[s][:, h, 0:128], po[o][:], [(tag, "po", 0)], [(tag, "ost", h)])
                CP(kb, "act", ost[s][:, h, 128:256], prs[o][:], [(tag, "prs", 0)], [(tag, "ost", h)])

        def stage_f(i):
            s = i % 2
            keys = [(tag, "ost", h) for h in range(8)]
            kb.dma("sp", chz, za[s][:], zav[:, :, i * 128:(i + 1) * 128], writes=[(tag, "za")])
            ACT(kb, za[s][:], za[s][:], AF.Silu, [(tag, "za")], [(tag, "za")])
            RECIP(kb, ost[s][:, :, 128:256], ost[s][:, :, 128:256], keys, keys)
            TT(kb, "dve", ost[s][:, :, 0:128], ost[s][:, :, 0:128], ost[s][:, :, 128:256], ALU.mult, keys, keys)
            TT(kb, "dve", yst[s][:], ost[s][:, :, 0:128], za[s][:], ALU.mult, keys + [(tag, "za")], [(tag, "yst")])
            kb.dma("pool", chy[s], yv[:, :, i * 128:(i + 1) * 128], yst[s][:], reads=[(tag, "yst")],
                   writes=[(tag, "y", i)])

        stage_a(0)
        stage_t(0)
        for i in range(NT):
            if i + 1 < NT:
                stage_a(i + 1)
            stage_b(i)
            if i + 1 < NT:
                stage_t(i + 1)
            stage_f(i)
        kb.end_phase()


def phase_gdn_prep(kb, C, pj, S):
    tag = "gp"
    with ExitStack() as st:
        W = T + 3
        raw = [kb.sb(f"gp_raw{i}", [128, W], F32, st) for i in range(2)]
        acc = [kb.sb(f"gp_acc{i}", [128, T], F32, st) for i in range(2)]
        sq = kb.sb("gp_sq", [128, T], F32, st)
        rn = kb.sb("gp_rn", [128, T], F32, st)
        ssp = kb.ps("gp_ssp", [128, T], F32, st)
        chl = [kb.chan(f"gp_l{i}") for i in range(2)]
        chs = [kb.chan(f"gp_s{i}") for i in range(2)]
        for cc in range(24):
            s = cc % 2
            r0 = 2048 + cc * 128
            MS(kb, "pool", raw[s][:, 0:3], 0.0, [(tag, "raw", s)])
            kb.dma("sp", chl[s], raw[s][:, 3:W], pj[r0:r0 + 128, :], writes=[(tag, "raw", s)])
            TS(kb, "dve", acc[s][:], raw[s][:, 3:W], C["cvw"][:, cc, 3:4], None, ALU.mult, None, [(tag, "raw", s)],
               [(tag, "acc", s)])
            for j in range(3):
                STT(kb, acc[s][:], raw[s][:, j:j + T], C["cvw"][:, cc, j:j + 1], acc[s][:], ALU.mult, ALU.add,
                    [(tag, "raw", s), (tag, "acc", s)], [(tag, "acc", s)])
            ACT(kb, acc[s][:], acc[s][:], AF.Silu, [(tag, "acc", s)], [(tag, "acc", s)])
            if cc < 16:
                ACT(kb, sq[:], acc[s][:], AF.Square, [(tag, "acc", s)], [(tag, "sq")])
                for b in range(8):
                    MM(kb, ssp[:, b * 512:(b + 1) * 512], C["ones_f"][:], sq[:, b * 512:(b + 1) * 512], True, True,
                       [(tag, "sq")], [(tag, "ssp")], signal=(b == 7))
                ACT(kb, rn[:], ssp[:], AF.Sqrt, [(tag, "ssp")], [(tag, "rn")], bias=C["eps"][:, 0:1])
                RECIP(kb, rn[:], rn[:], [(tag, "rn")], [(tag, "rn")])
                STT(kb, acc[s][:], acc[s][:], (128 ** -0.5) if cc < 8 else 1.0, rn[:], ALU.mult, ALU.mult,
                    [(tag, "acc", s), (tag, "rn")], [(tag, "acc", s)])
            kb.dma("pool", chs[s], S["gqkv"][cc * 128:(cc + 1) * 128, :], acc[s][:], reads=[(tag, "acc", s)],
                   writes=[("gqkv", cc)])
        kb.end_phase()


import os
GDN_TILES = int(os.environ.get("GDN_TILES", "32"))
GDN_STOP = int(os.environ.get("GDN_STOP", "99"))
GDN_SUB = float(os.environ.get("GDN_SUB", "99"))


def phase_gdn(kb, C, pj, S, y_dram):
    with ExitStack() as stc:
        C = dict(C)
        chc = kb.chan("gd_const")
        for name, shape, dt in GDN_CONST_SPECS:
            d = kb.nc.dram_tensor(name, list(shape), dt, kind="ExternalInput").ap()
            t = kb.sb("c_" + name, shape, dt, stc)
            kb.dma("sp", chc, t[:], d, writes=[("const", name)])
            C[name] = t
        kb.end_phase()
        phase_gdn_prep(kb, C, pj, S)
        _phase_gdn_main(kb, C, pj, S, y_dram)


def _phase_gdn_main(kb, C, pj, S, y_dram):
    tag = "gd"
    H = 8
    with ExitStack() as st:
        def fb(name, shape=(128, H, 128), dt=F32):
            return kb.sb("gd_" + name, list(shape), dt, st)

        tm = fb("tm", (128, NT, 32))
        beta = fb("beta", (128, NT, 8))
        g = fb("g", (128, NT, 8))
        t1 = fb("t1", (128, NT, 8))
        t2 = fb("t2", (128, NT, 8))
        nA = fb("nA", (128, 8))
        qT, kT, vT = fb("qT"), fb("kT"), fb("vT")
        gd, egrow, gcr = fb("gdiag"), fb("egrow"), fb("gcr")
        P1, E1, E2 = fb("P1"), fb("E1"), fb("E2")
        X = [fb("X0"), fb("X1")]
        Y = [fb("Y0"), fb("Y1")]
        P, attnT = fb("P"), fb("attnT")
        vb, kbg, kd, kd1 = fb("vb"), fb("kbg"), fb("kd"), fb("kd1")
        smk = fb("smk", (128, 16))
        u, wT, qgT, vnew = fb("u"), fb("wT"), fb("qgT"), fb("vnew")
        Sst, oacc, zb = fb("S"), fb("oacc"), fb("zb")
        osq, orn = fb("osq"), fb("orn")
        yo = fb("yo", (128, H, 128), BF16)
        sm = fb("sm", (128, 64))
        pA = kb.ps("gd_pA", [128, H, 128], F32, st)
        pB = kb.ps("gd_pB", [128, H, 128], F32, st)
        pC = kb.ps("gd_pC", [128, H, 128], F32, st)
        pO = kb.ps("gd_pO", [128, H, 64], F32, st)
        psm = kb.ps("gd_psm", [128, 32], F32, st)
        ch = kb.chan("gd_l")
        chq = kb.chan("gd_q")
        chz = kb.chan("gd_z")
        chy = kb.chan("gd_y")
        K_ = lambda n: (tag, n)

        def bc_h(ap2d):
            return ap2d.unsqueeze(1).to_broadcast([128, H, 128])

        def bc_f(ap2d):
            return ap2d.unsqueeze(2).to_broadcast([128, H, 128])

        kb.dma("sp", ch, tm[:], S["tmaj"].rearrange("(n p) c -> p n c", p=128), writes=[K_("tm")])
        ACT(kb, beta[:], tm[:, :, 16:24], AF.Sigmoid, [K_("tm")], [K_("beta")])
        dtb = C["dtb_bc"][:].unsqueeze(1).to_broadcast([128, NT, 8])
        TT(kb, "dve", g[:], tm[:, :, 24:32], dtb, ALU.add, [K_("tm")], [K_("g")])
        TS(kb, "dve", t1[:], g[:], -1.0, None, ALU.mult, None, [K_("g")], [K_("t1")])
        TT(kb, "dve", t1[:], t1[:], g[:], ALU.max, [K_("t1"), K_("g")], [K_("t1")])
        ACT(kb, t1[:], t1[:], AF.Exp, [K_("t1")], [K_("t1")], scale=-1.0)
        TS(kb, "dve", t1[:], t1[:], 1.0, None, ALU.add, None, [K_("t1")], [K_("t1")])
        ACT(kb, t1[:], t1[:], AF.Ln, [K_("t1")], [K_("t1")])
        TS(kb, "dve", t2[:], g[:], 0.0, None, ALU.max, None, [K_("g")], [K_("t2")])
        TT(kb, "dve", t2[:], t2[:], t1[:], ALU.add, [K_("t1"), K_("t2")], [K_("t2")])
        ACT(kb, nA[:], C["alog_bc"][:], AF.Exp, [], [K_("nA")])
        TS(kb, "dve", nA[:], nA[:], -1.0, None, ALU.mult, None, [K_("nA")], [K_("nA")])
        TT(kb, "dve", g[:], t2[:], nA[:].unsqueeze(1).to_broadcast([128, NT, 8]), ALU.mult, [K_("t2"), K_("nA")],
           [K_("g")])
        MS(kb, "dve", Sst[:], 0.0, [K_("S")])
        MS(kb, "dve", vnew[:], 0.0, [K_("vnew")])
        gq = S["gqkv"]
        qv = gq[0:1024, :].rearrange("(h p) t -> p h t", p=128)
        kv = gq[1024:2048, :].rearrange("(h p) t -> p h t", p=128)
        vv = gq[2048:3072, :].rearrange("(h p) t -> p h t", p=128)
        zv = pj[5120:6144, :].rearrange("(h p) t -> p h t", p=128)
        yv = y_dram[1024:2048, :].rearrange("(h p) t -> p h t", p=128)

        for n in range(GDN_TILES if GDN_STOP > 0 else 0):
            c0, c1 = n * 128, (n + 1) * 128
            kb.dma("sp", chq, qT[:], qv[:, :, c0:c1], writes=[K_("qT")])
            kb.dma("sp", chq, kT[:], kv[:, :, c0:c1], writes=[K_("kT")])
            kb.dma("sp", chq, vT[:], vv[:, :, c0:c1], writes=[K_("vT")])
            kb.dma("sp", chz, zb[:], zv[:, :, c0:c1], writes=[K_("zb")])
            gn = g[:, n, :]
            bn = beta[:, n, :]
            MM(kb, psm[:, 0:8], C["U2"][:], gn, True, True, [K_("g")], [K_("psm")], signal=False)
            MM(kb, psm[:, 8:16], C["Bsame"][:], gn, True, True, [K_("g")], [K_("psm")], signal=False)
            MM(kb, psm[:, 16:24], C["Bsel0"][:], gn, True, True, [K_("g")], [K_("psm")], signal=False)
            MM(kb, psm[:, 24:32], C["Bsel1"][:], gn, True, True, [K_("g")], [K_("psm")])
            CP(kb, "dve", sm[:, 0:32], psm[:], [K_("psm")], [K_("sm")])
            gc = sm[:, 0:8]
            ACT(kb, sm[:, 32:40], sm[:, 0:8], AF.Exp, [K_("sm")], [K_("sm")])
            TT(kb, "dve", sm[:, 40:48], sm[:, 8:16], sm[:, 0:8], ALU.subtract, [K_("sm")], [K_("sm")])
            ACT(kb, sm[:, 40:48], sm[:, 40:48], AF.Exp, [K_("sm")], [K_("sm")])
            ACT(kb, sm[:, 16:32], sm[:, 16:32], AF.Exp, [K_("sm")], [K_("sm")])
            TT(kb, "dve", sm[:, 48:56], sm[:, 32:40], bn, ALU.mult, [K_("sm"), K_("beta")], [K_("sm")])
            TS(kb, "dve", sm[:, 56:64], bn, -1.0, None, ALU.mult, None, [K_("beta")], [K_("sm")])
            if GDN_SUB <= 0:
                continue
            TT(kb, "dve", gd[:], bc_h(C["U2"][:]), bc_f(gn), ALU.mult, [K_("g")], [K_("gdiag")])
            for b in range(2):
                MM(kb, pA[:, 4 * b:4 * b + 4, :], C["ones_f"][:], gd[:, 4 * b:4 * b + 4, :], True, True,
                   [K_("gdiag")], [K_("pA")], signal=(b == 1))
            if GDN_SUB <= 0.3:
                continue
            CP(kb, "dve", gcr[:], pA[:], [K_("pA")], [K_("gcr")])
            ACT(kb, egrow[:], gcr[:], AF.Exp, [K_("gcr")], [K_("egrow")])
            if GDN_SUB <= 0.4:
                continue
            TT(kb, "dve", P1[:], gcr[:], bc_f(gc), ALU.subtract, [K_("gcr"), K_("sm")], [K_("P1")])
            if GDN_SUB <= 0.5:
                continue
            TS(kb, "dve", E1[:], P1[:], 0.0, None, ALU.max, None, [K_("P1")], [K_("E1")])
            ACT(kb, E1[:], E1[:], AF.Exp, [K_("E1")], [K_("E1")], scale=-1.0)
            if GDN_SUB <= 0.6:
                continue
            TS(kb, "dve", E2[:], P1[:], 0.0, None, ALU.min, None, [K_("P1")], [K_("E2")])
            ACT(kb, E2[:], E2[:], AF.Exp, [K_("E2")], [K_("E2")])
            TT(kb, "dve", E1[:], E1[:], bc_h(C["MLs"][:]), ALU.mult, [K_("E1")], [K_("E1")])
            TT(kb, "dve", E1[:], E1[:], bc_f(sm[:, 56:64]), ALU.mult, [K_("E1"), K_("sm")], [K_("E1")])
            TT(kb, "dve", E2[:], E2[:], bc_h(C["MU"][:]), ALU.mult, [K_("E2")], [K_("E2")])
            if GDN_SUB <= 1:
                continue
            for h in range(H):
                MM(kb, pB[:, h, :], kT[:, h, :], kT[:, h, :], True, True, [K_("kT")], [K_("pB")], signal=(h == H - 1))
            TT(kb, "dve", X[0][:], pB[:], E1[:], ALU.mult, [K_("pB"), K_("E1")], [K_("X0")])
            for h in range(H):
                MM(kb, pC[:, h, :], kT[:, h, :], qT[:, h, :], True, True, [K_("kT"), K_("qT")], [K_("pC")], signal=(h == H - 1))
            TT(kb, "dve", attnT[:], pC[:], E2[:], ALU.mult, [K_("pC"), K_("E2")], [K_("attnT")])
            for h in range(H):
                TR(kb, pA[:, h, :], X[0][:, h, :], C["ident"][:], [K_("X0")], [K_("pA")], signal=(h == H - 1))
            CP(kb, "dve", Y[0][:], pA[:], [K_("pA")], [K_("Y0")])
            TT(kb, "dve", P[:], Y[0][:], bc_h(C["ident"][:]), ALU.add, [K_("Y0")], [K_("P")])
            if GDN_SUB <= 2:
                continue
            cur = 0
            for lvl in range(5):
                nxt = 1 - cur
                xk, yk = K_(f"X{cur}"), K_(f"Y{cur}")
                xn, yn = K_(f"X{nxt}"), K_(f"Y{nxt}")
                for h in range(H):
                    MM(kb, pB[:, h, :], Y[cur][:, h, :], X[cur][:, h, :], True, True, [xk, yk], [K_("pB")], signal=(h == H - 1))
                CP(kb, "dve", X[nxt][:], pB[:], [K_("pB")], [xn])
                if lvl < 4:
                    for h in range(H):
                        MM(kb, pC[:, h, :], X[cur][:, h, :], Y[cur][:, h, :], True, True, [xk, yk], [K_("pC")], signal=(h == H - 1))
                    CP(kb, "dve", Y[nxt][:], pC[:], [K_("pC")], [yn])
                for h in range(H):
                    MM(kb, pA[:, h, :], X[nxt][:, h, :], P[:, h, :], True, True, [xn, K_("P")], [K_("pA")], signal=(h == H - 1))
                TT(kb, "dve", P[:], P[:], pA[:], ALU.add, [K_("pA"), K_("P")], [K_("P")])
                cur = nxt
            if GDN_SUB <= 3:
                continue
            for h in range(H):
                TR(kb, pB[:, h, :], kT[:, h, :], C["ident"][:], [K_("kT")], [K_("pB")], signal=(h == H - 1))
            TT(kb, "dve", kbg[:], pB[:], bc_f(sm[:, 48:56]), ALU.mult, [K_("pB"), K_("sm")], [K_("kbg")])
            TS(kb, "dve", smk[:, 0:8], sm[:, 40:48], C["Bsel0"][:, 0:1], None, ALU.mult, None, [K_("sm")], [K_("smk")])
            TS(kb, "dve", smk[:, 8:16], sm[:, 40:48], C["Bsel1"][:, 0:1], None, ALU.mult, None, [K_("sm")], [K_("smk")])
            TT(kb, "dve", kd[:], pB[:], bc_f(smk[:, 0:8]), ALU.mult, [K_("pB"), K_("smk")], [K_("kd")])
            TT(kb, "dve", kd1[:], pB[:], bc_f(smk[:, 8:16]), ALU.mult, [K_("pB"), K_("smk")], [K_("kd")])
            for h in range(H):
                TR(kb, pC[:, h, :], vT[:, h, :], C["ident"][:], [K_("vT")], [K_("pC")], signal=(h == H - 1))
            TT(kb, "dve", vb[:], pC[:], bc_f(bn), ALU.mult, [K_("pC"), K_("beta")], [K_("vb")])
            for h in range(H):
                MM(kb, pA[:, h, :], P[:, h, :], vb[:, h, :], True, True, [K_("P"), K_("vb")], [K_("pA")], signal=(h == H - 1))
            CP(kb, "dve", u[:], pA[:], [K_("pA")], [K_("u")])
            for h in range(H):
                MM(kb, pB[:, h, :], kbg[:, h, :], P[:, h, :], True, True, [K_("P"), K_("kbg")], [K_("pB")], signal=(h == H - 1))
            CP(kb, "dve", wT[:], pB[:], [K_("pB")], [K_("wT")])
            TT(kb, "dve", qgT[:], qT[:], egrow[:], ALU.mult, [K_("qT"), K_("egrow")], [K_("qgT")])
            if GDN_STOP <= 1:
                continue
            for c in range(2):
                r0, r1 = c * 64, (c + 1) * 64
                for h in range(H):
                    MM(kb, pA[r0:r1, h, :], wT[:, h, r0:r1], Sst[:, h, :], True, True, [K_("wT"), K_("S")], [K_("pA")],
                       signal=(h == H - 1))
                if GDN_SUB == 10 or (GDN_SUB == 10.5 and c == 1):
                    continue
                TT(kb, "dve", vnew[r0:r1, :, :], u[r0:r1, :, :], pA[r0:r1, :, :], ALU.subtract, [K_("u"), K_("pA")],
                   [K_("vnew")])
                if GDN_SUB == 11:
                    continue
                for h in range(H):
                    MM(kb, pO[:, h, :], Sst[:, h, :], qgT[:, h, r0:r1], True, False, [K_("S"), K_("qgT")], [K_("pO")])
                    MM(kb, pO[:, h, :], vnew[r0:r1, h, :], attnT[r0:r1, h, r0:r1], False, True,
                       [K_("vnew"), K_("attnT")], [K_("pO")], signal=(h == H - 1))
                if GDN_SUB == 12:
                    continue
                CP(kb, "dve", oacc[:, :, r0:r1], pO[:], [K_("pO")], [K_("oacc")])
                if GDN_STOP <= 2:
                    continue
                for h in range(H):
                    MM(kb, pB[:, h, :], (kd, kd1)[c][:, h, :], vnew[:, h, :], True, True, [K_("kd"), K_("vnew")],
                       [K_("pB")], signal=(h == H - 1))
                TT(kb, "dve", Sst[:], Sst[:], bc_f(sm[:, 16 + 8 * c:24 + 8 * c]), ALU.mult, [K_("S"), K_("sm")],
                   [K_("S")])
                TT(kb, "dve", Sst[:], Sst[:], pB[:], ALU.add, [K_("S"), K_("pB")], [K_("S")])
            ACT(kb, osq[:], oacc[:], AF.Square, [K_("oacc")], [K_("osq")])
            for b in range(2):
                MM(kb, pC[:, 4 * b:4 * b + 4, :], C["ones_f"][:], osq[:, 4 * b:4 * b + 4, :], True, True, [K_("osq")],
                   [K_("pC")], signal=(b == 1))
            CP(kb, "dve", orn[:], pC[:], [K_("pC")], [K_("orn")])
            ACT(kb, orn[:], orn[:], AF.Sqrt, [K_("orn")], [K_("orn")], bias=C["eps"][:, 0:1], scale=1.0 / 128)
            RECIP(kb, orn[:], orn[:], [K_("orn")], [K_("orn")])
            STT(kb, oacc[:], oacc[:], C["onorm"][:, 0:1], orn[:], ALU.mult, ALU.mult, [K_("oacc"), K_("orn")],
                [K_("oacc")])
            ACT(kb, zb[:], zb[:], AF.Silu, [K_("zb")], [K_("zb")])
            TT(kb, "dve", yo[:], oacc[:], zb[:], ALU.mult, [K_("oacc"), K_("zb")], [K_("yo")])
            kb.dma("pool", chy, yv[:, :, c0:c1], yo[:], reads=[K_("yo")], writes=[("y0b", n)])
        kb.end_phase()

def load_consts(kb, nc, names_shapes):
    C = {}
    ch = kb.chan("const")
    for name, shape, dt in names_shapes:
        d = nc.dram_tensor(name, list(shape), dt, kind="ExternalInput").ap()
        if name == "cbase":
            t = kb.sb("c_" + name, shape, BF16)
            kb.dma("pool", ch, t[:], d, writes=[("const", name)])
        else:
            t = kb.sb("c_" + name, shape, dt)
            kb.dma("sp", ch, t[:], d, writes=[("const", name)])
        C[name] = t
    kb.end_phase()
    with ExitStack() as st:
        d = nc.dram_tensor("biasT", [128, 8, 2, 128], F32, kind="ExternalInput").ap()
        C["biasS"] = kb.sb("c_biasS", [128, 8, 2, 128], BF16)
        bt = kb.sb("biasT_tmp", [128, 8, 2, 128], F32, st)
        kb.dma("sp", ch, bt[:], d, writes=[("const", "biasT")])
        for h in range(8):
            TS(kb, "dve", C["biasS"][:, h, :, :], bt[:, h, :, :], C["cb"][:, h:h + 1], 128 ** 0.5,
               ALU.subtract, ALU.mult, [("const", "biasT")], [("const", "biasS")])
        kb.end_phase()
    return C


CONST_SPECS = [
    ("ident", (128, 128), F32),
    ("ones_f", (128, 128), F32),
    ("eps", (128, 1), F32),
    ("normwT", (128, 2, 16), F32),
    ("dww", (128, 8, 31), F32),
    ("dwb", (128, 8), F32),
    ("lnw", (128, 8), F32),
    ("lnb", (128, 8), F32),
    ("dcw", (128, 8, 3), F32),
    ("cbase", (128, 896), F32),
    ("cb", (128, 8), F32),
    ("qnormT", (128, 4), F32),
    ("kvnormT", (128, 2), F32),
    ("qgain", (128, 1), F32),
    ("kgain", (128, 1), F32),
    ("ikw", (128, 1), F32),
    ("ikb", (128, 1), F32),
]

GDN_CONST_SPECS = [
    ("cvw", (128, 24, 4), F32),
    ("alog_bc", (128, 8), F32),
    ("dtb_bc", (128, 8), F32),
    ("onorm", (128, 1), F32),
    ("U2", (128, 128), F32),
    ("Bsame", (128, 128), F32),
    ("Bsel0", (128, 128), F32),
    ("Bsel1", (128, 128), F32),
    ("MLs", (128, 128), F32),
    ("MU", (128, 128), F32),
]


def build_program(layers=(0, 1), l0_parts=("a", "b"), debug_out=False):
    nc = bass.Bass("TRN2", target_bir_lowering=False)

    def din(name, shape, dt=F32):
        return nc.dram_tensor(name, list(shape), dt, kind="ExternalInput").ap()

    def dscr(name, shape, dt=F32):
        return nc.dram_tensor(name, list(shape), dt, kind="Internal").ap()

    x = din("x", [T, D])
    out = nc.dram_tensor("out", [T, D], F32, kind="ExternalOutput").ap()
    kb = KB(nc)
    C = load_consts(kb, nc, CONST_SPECS)
    src = x
    if 0 in layers:
        ab_w_in = din("ab_w_in", [48, 128, 16 * 128])
        ab_w_out = din("ab_w_out", [16, 128, D])
        W = {"w_uqiq": din("w_uqiq", [16, 128, 4 * 128]), "w_uk": din("w_uk", [8, 128, 2 * 128]),
             "w_uv": din("w_uv", [2, 128, 1024])}
        pj0 = dscr("pj0", [6144, T])
        S = {"qT": dscr("s_qT", [8, 128, T], BF16), "qiT": dscr("s_qiT", [8, 128, T], BF16),
             "kT": dscr("s_kT", [8, 128, T], BF16), "V": dscr("s_V", [T, 1024], BF16),
             "kiT": dscr("s_kiT", [128, T], BF16), "tmaj": dscr("s_tmaj", [T, 32]),
             "gqkv": dscr("s_gqkv", [3072, T])}
        if debug_out:
            y0 = nc.dram_tensor("y0", [2048, T], BF16, kind="ExternalOutput").ap()
        else:
            y0 = dscr("y0", [2048, T], BF16)
        x1 = dscr("x1", [T, D]) if 1 in layers else out
        phase_inproj(kb, C, "l0", src, C["normwT"][:, 0, :], ab_w_in, 48, pj0)
        phase_ki_tmaj(kb, C, pj0, S)
        if "a" in l0_parts:
            phase_dsa_prep(kb, C, pj0, W, S)
            phase_dsa_attn(kb, C, pj0, S, y0)
        if "b" in l0_parts:
            phase_gdn(kb, C, pj0, S, y0)
        if not debug_out:
            phase_outproj(kb, C, "l0o", y0, ab_w_out, src, x1)
        src = x1
    if 1 in layers:
        cd_w_in = din("cd_w_in", [56, 128, 16 * 128])
        cd_w_out = din("cd_w_out", [16, 128, D])
        pj1 = dscr("pj1", [7168, T])
        y1 = dscr("y1", [2048, T], BF16)
        phase_inproj(kb, C, "l1", src, C["normwT"][:, 1, :], cd_w_in, 56, pj1)
        phase_cd_mix(kb, C, pj1, C, y1)
        phase_outproj(kb, C, "l1o", y1, cd_w_out, src, out)
    kb.finish()
    kb.emit()
    kb.close()
    return nc, kb


def tile_w_in(w, nch):
    K, N = w.shape
    assert N == nch * 128
    return np.ascontiguousarray(w.reshape(K // 128, 128, nch, 128).transpose(2, 1, 0, 3)).reshape(nch, 128, -1)


def t5_bucket_np(dist):
    import math
    max_exact = 16
    dd = np.maximum(dist, 1).astype(np.float32)
    large = max_exact + (np.log(dd / max_exact) / math.log(128 / max_exact) * (32 - max_exact)).astype(np.int32)
    large = np.minimum(large, 31)
    return np.where(dist < max_exact, dist, large)


def colT(v, k):
    return np.ascontiguousarray(np.asarray(v, np.float32).reshape(k, 128).T)


def host_consts(inp):
    f = np.float32
    c = {}
    c["ident"] = np.eye(128, dtype=f)
    c["ones_f"] = np.ones((128, 128), f)
    c["eps"] = np.full((128, 1), EPS, f)
    c["normwT"] = np.ascontiguousarray(inp["norm_w"].reshape(2, 16, 128).transpose(2, 0, 1)).astype(f)
    c["dww"] = np.ascontiguousarray(inp["c_dw_w"][0].reshape(31, 8, 128).transpose(2, 1, 0)).astype(f)
    c["dwb"] = colT(inp["c_dw_b"][0], 8)
    c["lnw"] = colT(inp["c_ln_w"][0], 8)
    c["lnb"] = colT(inp["c_ln_b"][0], 8)
    c["dcw"] = np.ascontiguousarray(inp["d_conv_w"][0].reshape(3, 8, 128).transpose(2, 1, 0)).astype(f)
    r = np.arange(128)[:, None]
    cc = np.arange(896)[None, :]
    c["cbase"] = np.where(cc <= r + 384, 0.0, -1e30).astype(f)
    kl = np.arange(128)[:, None, None]
    dd = np.arange(2)[None, :, None]
    ql = np.arange(128)[None, None, :]
    dist = np.maximum(dd * 128 + ql - kl, 0)
    bt = np.asarray(inp["rel_bias"], f)[t5_bucket_np(dist)]
    c["biasT"] = np.ascontiguousarray(bt.transpose(0, 3, 1, 2))
    c["cb"] = np.ascontiguousarray(np.broadcast_to(np.asarray(inp["rel_bias"], f)[31][None, :], (128, 8)))
    c["qnormT"] = colT(inp["a_q_norm"][0], 4)
    c["kvnormT"] = colT(inp["a_kv_norm"][0], 2)
    c["qgain"] = np.asarray(inp["a_q_gain"][0], f).reshape(128, 1).copy()
    c["kgain"] = np.asarray(inp["a_k_gain"][0], f).reshape(128, 1).copy()
    c["ikw"] = np.tile(np.asarray(inp["a_ik_norm_w"][0], f), 2).reshape(128, 1).copy()
    c["ikb"] = np.tile(np.asarray(inp["a_ik_norm_b"][0], f), 2).reshape(128, 1).copy()
    c["cvw"] = np.ascontiguousarray(np.asarray(inp["b_conv_w"][0], f).reshape(4, 24, 128).transpose(2, 1, 0))
    c["alog_bc"] = np.ascontiguousarray(np.broadcast_to(np.asarray(inp["b_a_log"][0], f)[None, :], (128, 8)))
    c["dtb_bc"] = np.ascontiguousarray(np.broadcast_to(np.asarray(inp["b_dt_bias"][0], f)[None, :], (128, 8)))
    c["onorm"] = np.asarray(inp["b_o_norm"][0], f).reshape(128, 1).copy()
    a = np.arange(128)
    same = (a[:, None] // 64) == (a[None, :] // 64)
    c["U2"] = (same & (a[:, None] <= a[None, :])).astype(f)
    c["Bsame"] = same.astype(f)
    c["Bsel0"] = np.ascontiguousarray(np.broadcast_to((a[:, None] < 64), (128, 128))).astype(f)
    c["Bsel1"] = np.ascontiguousarray(np.broadcast_to((a[:, None] >= 64), (128, 128))).astype(f)
    c["MLs"] = (same & (a[:, None] > a[None, :])).astype(f)
    c["MU"] = (same & (a[:, None] <= a[None, :])).astype(f)
    return c


def host_shared(inp, layers=(0, 1)):
    f = np.float32
    sh = host_consts(inp)
    if 0 in layers:
        w = np.asarray(inp["ab_w_in"][0], f)
        wp = np.zeros((D, 6144), f)
        wp[:, 0:832] = w[:, 0:832]
        wp[:, 896:912] = w[:, 832:848]
        wp[:, 912:928] = w[:, 4944:4960]
        wp[:, 1024:2048] = w[:, 848:1872]
        wp[:, 2048:5120] = w[:, 1872:4944]
        wp[:, 5120:6144] = w[:, 4960:5984]
        sh["ab_w_in"] = tile_w_in(wp, 48)
        sh["ab_w_out"] = np.ascontiguousarray(np.asarray(inp["ab_w_out"][0], f).reshape(16, 128, D))
        sh["w_uqiq"] = tile_w_in(np.concatenate([inp["a_w_uq"][0], inp["a_w_iq"][0]], axis=1).astype(f), 16)
        sh["w_uk"] = tile_w_in(np.asarray(inp["a_w_uk"][0], f), 8)
        sh["w_uv"] = np.ascontiguousarray(np.asarray(inp["a_w_uv"][0], f).reshape(2, 128, 1024))
    if 1 in layers:
        sh["cd_w_in"] = tile_w_in(np.asarray(inp["cd_w_in"][0], f), 56)
        sh["cd_w_out"] = np.ascontiguousarray(np.asarray(inp["cd_w_out"][0], f).reshape(16, 128, D))
    return sh


def kernel(**inputs):
    inp = {k: np.asarray(v) for k, v in inputs.items()}
    nc, kb = build_program()
    sh = host_shared(inp)
    x = np.ascontiguousarray(inp["x"], dtype=np.float32)
    in_maps = [dict(sh, x=x[b]) for b in range(8)]
    res = run_bass_kernel_spmd(nc, in_maps, core_ids=list(range(8)))
    return np.stack([np.asarray(r["out"], np.float32) for r in res.results], axis=0)
```

```python
from contextlib import ExitStack
import numpy as np
import concourse.bass as bass
import concourse.mybir as mybir
from concourse.bass_utils import run_bass_kernel_spmd

F32 = mybir.dt.float32
BF16 = mybir.dt.bfloat16
ALU = mybir.AluOpType
AF = mybir.ActivationFunctionType
AX = mybir.AxisListType

T = 4096
D = 2048
NT = T // 128
EPS = 1e-6
ENGS = ("pe", "act", "dve", "pool", "sp")


class Chan:
    def __init__(self, sem, name):
        self.sem = sem
        self.name = name
        self.n = 0


class KB:
    def __init__(self, nc):
        self.nc = nc
        self.es = ExitStack()
        self.q = {e: [] for e in ENGS}
        self.sems = {}
        self.cnt = {}
        self.seen = {e: {} for e in ENGS}
        self.lastw = {}
        self.readers = {}
        self.chans = []
        self.chan_by_sem = {}
        self.nins = 0
        self.pending = {e: False for e in ENGS}
        for e in ENGS:
            self.sems[e] = self.es.enter_context(nc.semaphore("s_" + e))
            self.cnt[e] = 0

    def sb(self, name, shape, dt, stack=None):
        return (stack or self.es).enter_context(self.nc.sbuf_tensor(name, list(shape), dt))

    def ps(self, name, shape, dt=F32, stack=None):
        return (stack or self.es).enter_context(self.nc.psum_tensor(name, list(shape), dt))

    def chan(self, name):
        c = Chan(self.es.enter_context(self.nc.semaphore("c_" + name)), name)
        self.chans.append(c)
        self.chan_by_sem[id(c.sem)] = c
        return c

    def _need0(self, eng, sem, val):
        ch = self.chan_by_sem.get(id(sem))
        if ch is not None:
            val = max(val, 16 * ch.n)
        cur = self.seen[eng].get(id(sem), 0)
        if val > cur:
            self.seen[eng][id(sem)] = val
            self.q[eng].append(("wait", sem, val))

    def _deps(self, eng, reads, writes, my_sem):
        for r in reads:
            ev = self.lastw.get(r)
            if ev is not None:
                self._need(eng, ev[0], ev[1])
        for w in writes:
            ev = self.lastw.get(w)
            if ev is not None:
                self._need(eng, ev[0], ev[1])
            rd = self.readers.get(w)
            if rd:
                for sem, val in rd.values():
                    if sem is my_sem:
                        continue
                    self._need(eng, sem, val)

    def _need(self, eng, sem, val):
        if eng == "pe" and sem is self.sems["pe"]:
            return
        self._need0(eng, sem, val)

    def _commit(self, ev, reads, writes):
        for w in writes:
            self.lastw[w] = ev
            self.readers[w] = {}
        for r in reads:
            d = self.readers.setdefault(r, {})
            d[id(ev[0])] = ev

    def op(self, eng, fn, reads=(), writes=(), signal=True):
        sem = self.sems[eng]
        self._deps(eng, reads, writes, sem)
        if signal:
            self.cnt[eng] += 1
            self.pending[eng] = False
            ev = (sem, self.cnt[eng])
            self.q[eng].append(("ins", fn, sem, 1))
        else:
            self.pending[eng] = True
            ev = (sem, self.cnt[eng] + 1)
            self.q[eng].append(("ins0", fn))
        self._commit(ev, reads, writes)
        self.nins += 1
        return ev

    def dma(self, eng, ch, out, in_, reads=(), writes=(), **kw):
        self._deps(eng, reads, writes, None)
        ch.n += 1
        ev = (ch.sem, 16 * ch.n)
        self.q[eng].append(("ins", lambda e, o=out, i=in_, k=kw: e.dma_start(out=o, in_=i, **k), ch.sem, 16))
        self._commit(ev, reads, writes)
        self.nins += 1
        return ev

    def _flush_pending(self):
        for e in ENGS:
            assert not self.pending[e], "non-signaling op left pending at a barrier on " + e

    def _all_events(self):
        self._flush_pending()
        evs = [(self.sems[e], self.cnt[e]) for e in ENGS if self.cnt[e] > 0]
        evs += [(c.sem, 16 * c.n) for c in self.chans if c.n > 0]
        return evs

    def barrier(self):
        evs = self._all_events()
        for e in ENGS:
            for sem, val in evs:
                self._need(e, sem, val)

    def finish(self, final_eng="sp"):
        for sem, val in self._all_events():
            self._need(final_eng, sem, val)

    def emit(self):
        nc = self.nc
        q = self.q
        self.q = {e: [] for e in ENGS}

        def replay(eng_obj, items):
            for it in items:
                if it[0] == "wait":
                    eng_obj.wait_ge(it[1], it[2])
                elif it[0] == "ins0":
                    it[1](eng_obj)
                else:
                    it[1](eng_obj).then_inc(it[2], it[3])

        with nc.Block() as block:
            @block.tensor
            def _(e):
                replay(e, q["pe"])

            @block.scalar
            def _(e):
                replay(e, q["act"])

            @block.vector
            def _(e):
                replay(e, q["dve"])

            @block.gpsimd
            def _(e):
                replay(e, q["pool"])

            @block.sync
            def _(e):
                replay(e, q["sp"])

    def end_phase(self):
        self.barrier()
        self.emit()

    def close(self):
        self.es.close()


def MM(kb, out, lhsT, rhs, start, stop, reads, writes, signal=None):
    if signal is None:
        signal = stop
    return kb.op("pe", lambda e: e.matmul(out, lhsT, rhs, start=start, stop=stop), reads, writes, signal=signal)


def TR(kb, out, in_, ident, reads, writes, signal=True):
    return kb.op("pe", lambda e: e.transpose(out, in_, ident), reads, writes, signal=signal)


def ACT(kb, out, in_, func, reads, writes, bias=None, scale=None, accum_out=None):
    kw = {}
    if bias is not None:
        kw["bias"] = bias
    if scale is not None:
        kw["scale"] = scale
    if accum_out is not None:
        kw["accum_out"] = accum_out
    return kb.op("act", lambda e: e.activation(out=out, in_=in_, func=func, **kw), reads, writes)


def TS(kb, eng, out, in0, s1, s2, op0, op1, reads, writes, accum_out=None):
    kw = {}
    if op1 is not None:
        kw["op1"] = op1
    if accum_out is not None:
        kw["accum_out"] = accum_out
    return kb.op(eng, lambda e: e.tensor_scalar(out=out, in0=in0, scalar1=s1, scalar2=s2, op0=op0, **kw),
                 reads, writes)


def TT(kb, eng, out, in0, in1, op, reads, writes):
    return kb.op(eng, lambda e: e.tensor_tensor(out=out, in0=in0, in1=in1, op=op), reads, writes)


def STT(kb, out, in0, scalar, in1, op0, op1, reads, writes):
    return kb.op("dve", lambda e: e.scalar_tensor_tensor(out=out, in0=in0, scalar=scalar, in1=in1,
                                                         op0=op0, op1=op1), reads, writes)


def CP(kb, eng, out, in_, reads, writes):
    if eng == "act":
        return kb.op("act", lambda e: e.copy(out=out, in_=in_), reads, writes)
    return kb.op(eng, lambda e: e.tensor_copy(out=out, in_=in_), reads, writes)


def MS(kb, eng, ap, val, writes):
    return kb.op(eng, lambda e: e.memset(ap, val), (), writes)


def RECIP(kb, out, in_, reads, writes):
    return kb.op("dve", lambda e: e.reciprocal(out=out, in_=in_), reads, writes)


def phase_norm_T(kb, C, x_dram, normwT, hT, tag):
    with ExitStack() as st:
        xt = [kb.sb(f"{tag}_xt{i}", [128, D], F32, st) for i in range(2)]
        xn = [kb.sb(f"{tag}_xn{i}", [128, D], F32, st) for i in range(2)]
        sq = kb.sb(f"{tag}_sq", [128, D], BF16, st)
        sm = [kb.sb(f"{tag}_sm{i}", [128, 4], F32, st) for i in range(2)]
        pst = [kb.ps(f"{tag}_pt{i}", [128, 8, 128], F32, st) for i in range(2)]
        ch = [kb.chan(f"{tag}_x{i}") for i in range(2)]
        for tt in range(NT):
            s = tt % 2
            kb.dma("sp", ch[s], xt[s][:], x_dram[tt * 128:(tt + 1) * 128, :], writes=[(tag, "xt", s)])
            ACT(kb, sq[:], xt[s][:], AF.Square, [(tag, "xt", s)], [(tag, "sq"), (tag, "ss", s)],
                accum_out=sm[s][:, 0:1])
            ACT(kb, sm[s][:, 1:2], sm[s][:, 0:1], AF.Sqrt, [(tag, "ss", s)], [(tag, "sd", s)],
                bias=C["eps"][:, 0:1], scale=1.0 / D)
            RECIP(kb, sm[s][:, 2:3], sm[s][:, 1:2], [(tag, "sd", s)], [(tag, "rs", s)])
            TS(kb, "dve", xn[s][:], xt[s][:], sm[s][:, 2:3], None, ALU.mult, None,
               [(tag, "xt", s), (tag, "rs", s)], [(tag, "xn", s)])
            for half in range(2):
                p = half
                for kk in range(8):
                    k = half * 8 + kk
                    TR(kb, pst[p][:, kk, :], xn[s][:, k * 128:(k + 1) * 128], C["ident"][:],
                       [(tag, "xn", s)], [(tag, "pt", p)], signal=(kk == 7))
                nw = normwT[:, half * 8:(half + 1) * 8].unsqueeze(2).to_broadcast([128, 8, 128])
                TT(kb, "dve", hT[:, half * 8:(half + 1) * 8, tt * 128:(tt + 1) * 128], pst[p][:], nw,
                   ALU.mult, [(tag, "pt", p)], [("hT", tt)])
        kb.end_phase()


def gemm_fm(kb, tag, actT, act_keys, KC, w_dram, NCH, sink):
    with ExitStack() as st:
        wb = [kb.sb(f"{tag}_wb{i}", [128, KC * 128], BF16, st) for i in range(2)]
        wch = [kb.chan(f"{tag}_w{i}") for i in range(2)]
        pss = [kb.ps(f"{tag}_ps{i}", [128, 1024], F32, st) for i in range(2)]
        it = 0
        for c in range(NCH):
            s = c % 2
            kb.dma("pool", wch[s], wb[s][:], w_dram[c], writes=[(tag, "wb", s)])
            for ts in range(4):
                p = it % 2
                it += 1
                for b in range(2):
                    t0 = ts * 1024 + b * 512
                    for k in range(KC):
                        MM(kb, pss[p][:, b * 512:(b + 1) * 512], wb[s][:, k * 128:(k + 1) * 128],
                           actT[:, k, t0:t0 + 512], k == 0, k == KC - 1,
                           [(tag, "wb", s)] + act_keys, [(tag, "ps", p)])
                sink(c, ts, pss[p], (tag, "ps", p), st)
        kb.end_phase()


class StoreSink:
    def __init__(self, kb, tag, dst, st):
        self.kb = kb
        self.tag = tag
        self.dst = dst
        self.stg = [kb.sb(f"{tag}_stg{i}", [128, 1024], F32, st) for i in range(3)]
        self.ch = [kb.chan(f"{tag}_st{i}") for i in range(3)]
        self.i = 0

    def __call__(self, c, ts, ps, pkey, st):
        kb = self.kb
        s = self.i % 3
        eng = "act" if self.i % 2 == 0 else "dve"
        self.i += 1
        CP(kb, eng, self.stg[s][:], ps[:], [pkey], [(self.tag, "stg", s)])
        kb.dma("sp", self.ch[s], self.dst[c * 128:(c + 1) * 128, ts * 1024:(ts + 1) * 1024], self.stg[s][:],
               reads=[(self.tag, "stg", s)], writes=[(self.tag, "dst", c)])


def phase_inproj(kb, C, tag, x_dram, normwT, w_dram, NCH, pj):
    with ExitStack() as st:
        hT = kb.sb(f"{tag}_hT", [128, 16, T], BF16, st)
        phase_norm_T(kb, C, x_dram, normwT, hT, tag + "n")
        with ExitStack() as st2:
            sink = StoreSink(kb, tag + "s", pj, st2)
            gemm_fm(kb, tag + "g", hT, [], 16, w_dram, NCH, sink)


def phase_outproj(kb, C, tag, yT_dram, wo_dram, xres_dram, out_dram):
    with ExitStack() as st:
        wo = kb.sb(f"{tag}_wo", [128, 16, D], BF16, st)
        wch = kb.chan(f"{tag}_w")
        for k in range(16):
            kb.dma("pool", wch, wo[:, k, :], wo_dram[k], writes=[(tag, "wo")])
        yt = [kb.sb(f"{tag}_yt{i}", [128, 16, 128], BF16, st) for i in range(2)]
        xr = [kb.sb(f"{tag}_xr{i}", [128, D], F32, st) for i in range(2)]
        ot = [kb.sb(f"{tag}_ot{i}", [128, D], F32, st) for i in range(2)]
        ps = [kb.ps(f"{tag}_ps{i}", [128, 512], F32, st) for i in range(4)]
        chy = [kb.chan(f"{tag}_y{i}") for i in range(2)]
        chx = [kb.chan(f"{tag}_x{i}") for i in range(2)]
        cho = [kb.chan(f"{tag}_o{i}") for i in range(2)]
        yv = yT_dram.rearrange("(k p) t -> p k t", p=128)
        for tt in range(NT):
            s = tt % 2
            kb.dma("sp", chy[s], yt[s][:], yv[:, :, tt * 128:(tt + 1) * 128], writes=[(tag, "yt", s)])
            kb.dma("sp", chx[s], xr[s][:], xres_dram[tt * 128:(tt + 1) * 128, :], writes=[(tag, "xr", s)])
            for nb in range(4):
                for k in range(16):
                    MM(kb, ps[nb][:], yt[s][:, k, :], wo[:, k, nb * 512:(nb + 1) * 512], k == 0, k == 15,
                       [(tag, "yt", s), (tag, "wo")], [(tag, "ps", nb)])
                TT(kb, "dve", ot[s][:, nb * 512:(nb + 1) * 512], ps[nb][:], xr[s][:, nb * 512:(nb + 1) * 512],
                   ALU.add, [(tag, "ps", nb), (tag, "xr", s)], [(tag, "ot", s)])
            kb.dma("pool", cho[s], out_dram[tt * 128:(tt + 1) * 128, :], ot[s][:],
                   reads=[(tag, "ot", s)], writes=[(tag, "out", tt)])
        kb.end_phase()


def phase_cd_mix(kb, C, pj, cw_unused, y_dram):
    with ExitStack() as stc:
        cw = {}
        chc = kb.chan("cd_const")
        for name, shape, dt in CD_CONST_SPECS:
            d = kb.nc.dram_tensor(name, list(shape), dt, kind="ExternalInput").ap()
            t = kb.sb("c_" + name, shape, dt, stc)
            kb.dma("sp", chc, t[:], d, writes=[("const", name)])
            cw[name] = t
        kb.end_phase()
        _phase_cd_mix(kb, C, pj, cw, y_dram)


def _phase_cd_mix(kb, C, pj, cw, y_dram):
    HB = 2048
    tag = "cd"
    with ExitStack() as st:
        uc = kb.sb("cd_uc", [128, 8, HB], F32, st)
        ab = [kb.sb(f"cd_a{i}", [128, 30 + HB], F32, st) for i in range(2)]
        gb = [kb.sb(f"cd_g{i}", [128, 30 + HB], F32, st) for i in range(2)]
        sq = [kb.sb(f"cd_sq{i}", [128, HB], F32, st) for i in range(2)]
        mean = kb.sb("cd_mean", [128, HB], F32, st)
        rstd = kb.sb("cd_rstd", [128, HB], F32, st)
        m2 = kb.sb("cd_m2", [128, HB], F32, st)
        zb = [kb.sb(f"cd_z{i}", [128, HB], F32, st) for i in range(2)]
        yb = [kb.sb(f"cd_y{i}", [128, HB], BF16, st) for i in range(2)]
        ps_sum = kb.ps("cd_pss", [128, HB], F32, st)
        ps_ssq = kb.ps("cd_psq", [128, HB], F32, st)
        cha = [kb.chan(f"cd_a{i}") for i in range(2)]
        chg = [kb.chan(f"cd_g{i}") for i in range(2)]
        chz = [kb.chan(f"cd_z{i}") for i in range(2)]
        chy = [kb.chan(f"cd_y{i}") for i in range(2)]
        for th in range(2):
            t0 = th * HB
            for cc in range(8):
                s = cc % 2
                r0 = cc * 128
                if th == 0:
                    MS(kb, "pool", ab[s][:, 0:30], 0.0, [(tag, "a", s)])
                    MS(kb, "pool", gb[s][:, 0:30], 0.0, [(tag, "g", s)])
                    kb.dma("sp", cha[s], ab[s][:, 30:30 + HB], pj[r0:r0 + 128, 0:HB], writes=[(tag, "a", s)])
                    kb.dma("sp", chg[s], gb[s][:, 30:30 + HB], pj[1024 + r0:1024 + r0 + 128, 0:HB],
                           writes=[(tag, "g", s)])
                else:
                    kb.dma("sp", cha[s], ab[s][:], pj[r0:r0 + 128, t0 - 30:t0 + HB], writes=[(tag, "a", s)])
                    kb.dma("sp", chg[s], gb[s][:], pj[1024 + r0:1024 + r0 + 128, t0 - 30:t0 + HB],
                           writes=[(tag, "g", s)])
                ACT(kb, gb[s][:], gb[s][:], AF.Sigmoid, [(tag, "g", s)], [(tag, "g", s)])
                TT(kb, "dve", ab[s][:], ab[s][:], gb[s][:], ALU.mult, [(tag, "a", s), (tag, "g", s)], [(tag, "a", s)])
                TS(kb, "dve", uc[:, cc, :], ab[s][:, 30:30 + HB], cw["dww"][:, cc, 30:31], cw["dwb"][:, cc:cc + 1],
                   ALU.mult, ALU.add, [(tag, "a", s)], [(tag, "uc", cc)])
                for j in range(30):
                    STT(kb, uc[:, cc, :], ab[s][:, j:j + HB], cw["dww"][:, cc, j:j + 1], uc[:, cc, :],
                        ALU.mult, ALU.add, [(tag, "a", s), (tag, "uc", cc)], [(tag, "uc", cc)])
                ACT(kb, sq[s][:], uc[:, cc, :], AF.Square, [(tag, "uc", cc)], [(tag, "sq", s)])
                for tb in range(4):
                    MM(kb, ps_sum[:, tb * 512:(tb + 1) * 512], C["ones_f"][:], uc[:, cc, tb * 512:(tb + 1) * 512],
                       cc == 0, cc == 7, [(tag, "uc", cc)], [(tag, "pss")], signal=(tb == 3))
                    MM(kb, ps_ssq[:, tb * 512:(tb + 1) * 512], C["ones_f"][:], sq[s][:, tb * 512:(tb + 1) * 512],
                       cc == 0, cc == 7, [(tag, "sq", s)], [(tag, "psq")], signal=(tb == 3))
            ACT(kb, mean[:], ps_sum[:], AF.Copy, [(tag, "pss")], [(tag, "mean")], scale=1.0 / 1024)
            TT(kb, "dve", m2[:], mean[:], mean[:], ALU.mult, [(tag, "mean")], [(tag, "m2")])
            STT(kb, m2[:], ps_ssq[:], 1.0 / 1024, m2[:], ALU.mult, ALU.subtract, [(tag, "psq"), (tag, "m2")],
                [(tag, "m2")])
            ACT(kb, m2[:], m2[:], AF.Sqrt, [(tag, "m2")], [(tag, "m2")], bias=C["eps"][:, 0:1])
            RECIP(kb, rstd[:], m2[:], [(tag, "m2")], [(tag, "rstd")])
            for cc in range(8):
                s = cc % 2
                r0 = cc * 128
                kb.dma("sp", chz[s], zb[s][:], pj[2048 + r0:2048 + r0 + 128, t0:t0 + HB], writes=[(tag, "z", s)])
                TT(kb, "dve", uc[:, cc, :], uc[:, cc, :], mean[:], ALU.subtract, [(tag, "uc", cc), (tag, "mean")],
                   [(tag, "uc", cc)])
                TT(kb, "dve", uc[:, cc, :], uc[:, cc, :], rstd[:], ALU.mult, [(tag, "uc", cc), (tag, "rstd")],
                   [(tag, "uc", cc)])
                ACT(kb, uc[:, cc, :], uc[:, cc, :], AF.Silu, [(tag, "uc", cc)], [(tag, "uc", cc)],
                    scale=cw["lnw"][:, cc:cc + 1], bias=cw["lnb"][:, cc:cc + 1])
                ACT(kb, zb[s][:], zb[s][:], AF.Silu, [(tag, "z", s)], [(tag, "z", s)])
                TT(kb, "dve", yb[s][:], uc[:, cc, :], zb[s][:], ALU.mult, [(tag, "uc", cc), (tag, "z", s)],
                   [(tag, "y", s)])
                kb.dma("pool", chy[s], y_dram[r0:r0 + 128, t0:t0 + HB], yb[s][:], reads=[(tag, "y", s)],
                       writes=[(tag, "yd", cc, th)])
        kb.end_phase()
    tag = "sc"
    with ExitStack() as st:
        W = T + 2
        bg = [kb.sb(f"sc_b{i}", [128, T], F32, st) for i in range(2)]
        cg = [kb.sb(f"sc_c{i}", [128, W], F32, st) for i in range(2)]
        ud = [kb.sb(f"sc_u{i}", [128, W], F32, st) for i in range(2)]
        zd = [kb.sb(f"sc_z{i}", [128, T], F32, st) for i in range(2)]
        acc = [kb.sb(f"sc_acc{i}", [128, T], F32, st) for i in range(2)]
        yb = [kb.sb(f"sc_y{i}", [128, T], BF16, st) for i in range(2)]
        chs = {n: [kb.chan(f"sc_{n}{i}") for i in range(2)] for n in ("b", "c", "u", "z", "y")}
        for cc in range(8):
            s = cc % 2
            r0 = cc * 128
            MS(kb, "pool", cg[s][:, 0:2], 0.0, [(tag, "c", s)])
            MS(kb, "pool", ud[s][:, 0:2], 0.0, [(tag, "u", s)])
            kb.dma("sp", chs["b"][s], bg[s][:], pj[3072 + r0:3072 + r0 + 128, :], writes=[(tag, "b", s)])
            kb.dma("sp", chs["c"][s], cg[s][:, 2:W], pj[4096 + r0:4096 + r0 + 128, :], writes=[(tag, "c", s)])
            kb.dma("sp", chs["u"][s], ud[s][:, 2:W], pj[5120 + r0:5120 + r0 + 128, :], writes=[(tag, "u", s)])
            kb.dma("sp", chs["z"][s], zd[s][:], pj[6144 + r0:6144 + r0 + 128, :], writes=[(tag, "z", s)])
            TT(kb, "dve", cg[s][:], cg[s][:], ud[s][:], ALU.mult, [(tag, "c", s), (tag, "u", s)], [(tag, "c", s)])
            TS(kb, "dve", acc[s][:], cg[s][:, 2:W], cw["dcw"][:, cc, 2:3], None, ALU.mult, None,
               [(tag, "c", s)], [(tag, "acc", s)])
            for j in range(2):
                STT(kb, acc[s][:], cg[s][:, j:j + T], cw["dcw"][:, cc, j:j + 1], acc[s][:], ALU.mult, ALU.add,
                    [(tag, "c", s), (tag, "acc", s)], [(tag, "acc", s)])
            ACT(kb, zd[s][:], zd[s][:], AF.Silu, [(tag, "z", s)], [(tag, "z", s)])
            TT(kb, "pool", bg[s][:], bg[s][:], zd[s][:], ALU.mult, [(tag, "b", s), (tag, "z", s)], [(tag, "b", s)])
            TT(kb, "dve", yb[s][:], acc[s][:], bg[s][:], ALU.mult, [(tag, "acc", s), (tag, "b", s)], [(tag, "y", s)])
            kb.dma("pool", chs["y"][s], y_dram[1024 + r0:1024 + r0 + 128, :], yb[s][:], reads=[(tag, "y", s)],
                   writes=[(tag, "yd", cc)])
        kb.end_phase()


def colnorm_phase(kb, C, tag, src, KC, gT, outT):
    with ExitStack() as st:
        raw = [kb.sb(f"{tag}_raw{i}", [128, KC, 1024], F32, st) for i in range(2)]
        sq = [kb.sb(f"{tag}_sq{i}", [128, 1024], F32, st) for i in range(2)]
        rs = kb.sb(f"{tag}_rs", [128, 1024], F32, st)
        ssp = kb.ps(f"{tag}_ssp", [128, 1024], F32, st)
        ch = [kb.chan(f"{tag}_l{i}") for i in range(2)]
        sv = src.rearrange("(k p) t -> p k t", p=128)
        for ts in range(4):
            s = ts % 2
            kb.dma("sp", ch[s], raw[s][:], sv[:, :, ts * 1024:(ts + 1) * 1024], writes=[(tag, "raw", s)])
            for k in range(KC):
                q = k % 2
                ACT(kb, sq[q][:], raw[s][:, k, :], AF.Square, [(tag, "raw", s)], [(tag, "sq", q)])
                for b in range(2):
                    MM(kb, ssp[:, b * 512:(b + 1) * 512], C["ones_f"][:], sq[q][:, b * 512:(b + 1) * 512],
                       k == 0, k == KC - 1, [(tag, "sq", q)], [(tag, "ssp")], signal=True)
            ACT(kb, rs[:], ssp[:], AF.Sqrt, [(tag, "ssp")], [(tag, "rs")], bias=C["eps"][:, 0:1],
                scale=1.0 / (KC * 128))
            RECIP(kb, rs[:], rs[:], [(tag, "rs")], [(tag, "rs")])
            for k in range(KC):
                STT(kb, outT[:, k, ts * 1024:(ts + 1) * 1024], raw[s][:, k, :], gT[:, k:k + 1], rs[:],
                    ALU.mult, ALU.mult, [(tag, "raw", s), (tag, "rs")], [(tag, "out", k, ts)])
        kb.end_phase()


class HeadNormSink:
    def __init__(self, kb, C, tag, dst, gain, n_norm, raw_dst, st):
        self.kb, self.C, self.tag, self.dst, self.gain = kb, C, tag, dst, gain
        self.n_norm, self.raw_dst = n_norm, raw_dst
        self.sq = [kb.sb(f"{tag}_sq{i}", [128, 1024], F32, st) for i in range(2)]
        self.rs = [kb.sb(f"{tag}_rs{i}", [128, 1024], F32, st) for i in range(2)]
        self.ob = [kb.sb(f"{tag}_ob{i}", [128, 1024], BF16, st) for i in range(2)]
        self.ssp = kb.ps(f"{tag}_ssp", [128, 1024], F32, st)
        self.ch = [kb.chan(f"{tag}_o{i}") for i in range(2)]
        self.i = 0

    def __call__(self, c, ts, ps, pkey, st):
        kb, C, tag = self.kb, self.C, self.tag
        s = self.i % 2
        self.i += 1
        if c < self.n_norm:
            ACT(kb, self.sq[s][:], ps[:], AF.Square, [pkey], [(tag, "sq", s)])
            for b in range(2):
                MM(kb, self.ssp[:, b * 512:(b + 1) * 512], C["ones_f"][:], self.sq[s][:, b * 512:(b + 1) * 512],
                   True, True, [(tag, "sq", s)], [(tag, "ssp")])
            ACT(kb, self.rs[s][:], self.ssp[:], AF.Sqrt, [(tag, "ssp")], [(tag, "rs", s)], bias=C["eps"][:, 0:1],
                scale=1.0 / 128)
            RECIP(kb, self.rs[s][:], self.rs[s][:], [(tag, "rs", s)], [(tag, "rs", s)])
            STT(kb, self.ob[s][:], ps[:], self.gain, self.rs[s][:], ALU.mult, ALU.mult,
                [pkey, (tag, "rs", s)], [(tag, "ob", s)])
            d = self.dst[c]
        else:
            CP(kb, "act", self.ob[s][:], ps[:], [pkey], [(tag, "ob", s)])
            d = self.raw_dst[c - self.n_norm]
        kb.dma("sp", self.ch[s], d[:, ts * 1024:(ts + 1) * 1024], self.ob[s][:], reads=[(tag, "ob", s)],
               writes=[(tag, "dst", c)])


def phase_dsa_prep(kb, C, pj, W, S):
    with ExitStack() as st:
        cqn = kb.sb("cqn", [128, 4, T], BF16, st)
        colnorm_phase(kb, C, "cq", pj[0:512, :], 4, C["qnormT"], cqn)
        with ExitStack() as st2:
            sink = HeadNormSink(kb, C, "qs", S["qT"], C["qgain"][:, 0:1], 8, S["qiT"], st2)
            gemm_fm(kb, "qg", cqn, [], 4, W["w_uqiq"], 16, sink)
    with ExitStack() as st:
        ckvn = kb.sb("ckvn", [128, 2, T], BF16, st)
        colnorm_phase(kb, C, "ckv", pj[512:768, :], 2, C["kvnormT"], ckvn)
        with ExitStack() as st2:
            sink = HeadNormSink(kb, C, "ks", S["kT"], C["kgain"][:, 0:1], 8, None, st2)
            gemm_fm(kb, "kg", ckvn, [], 2, W["w_uk"], 8, sink)
        with ExitStack() as st2:
            wv = kb.sb("wv", [128, 2, 1024], BF16, st2)
            chw = kb.chan("wv")
            for k in range(2):
                kb.dma("pool", chw, wv[:, k, :], W["w_uv"][k], writes=[("wv",)])
            vps = [kb.ps(f"v_ps{i}", [128, 1024], F32, st2) for i in range(2)]
            vb = [kb.sb(f"v_b{i}", [128, 1024], BF16, st2) for i in range(2)]
            chv = [kb.chan(f"v_o{i}") for i in range(2)]
            for tt in range(NT):
                s = tt % 2
                for b in range(2):
                    for k in range(2):
                        MM(kb, vps[s][:, b * 512:(b + 1) * 512], ckvn[:, k, tt * 128:(tt + 1) * 128],
                           wv[:, k, b * 512:(b + 1) * 512], k == 0, k == 1, [("wv",)], [("v", "ps", s)])
                CP(kb, "act" if tt % 2 else "dve", vb[s][:], vps[s][:], [("v", "ps", s)], [("v", "b", s)])
                kb.dma("sp", chv[s], S["V"][tt * 128:(tt + 1) * 128, :], vb[s][:], reads=[("v", "b", s)],
                       writes=[("V", tt)])
            kb.end_phase()


def phase_ki_tmaj(kb, C, pj, S):
    with ExitStack() as st:
        ki = kb.sb("ki_raw", [128, T], F32, st)
        sq = kb.sb("ki_sq", [128, T], F32, st)
        mean = kb.sb("ki_mean", [128, 1024], F32, st)
        var = kb.sb("ki_var", [128, 1024], F32, st)
        kio = kb.sb("ki_o", [128, T], BF16, st)
        sm = kb.sb("ki_sm", [32, T], F32, st)
        tmo = kb.sb("ki_tmo", [128, NT, 32], F32, st)
        ps1 = kb.ps("ki_ps1", [128, 1024], F32, st)
        ps2 = kb.ps("ki_ps2", [128, 1024], F32, st)
        pst = kb.ps("ki_pst", [128, 16, 32], F32, st)
        ch = kb.chan("ki")
        kb.dma("sp", ch, ki[0:64, :], pj[768:832, :], writes=[("ki", "raw")])
        kb.dma("sp", ch, ki[64:128, :], pj[768:832, :], writes=[("ki", "raw")])
        kb.dma("sp", ch, sm[:], pj[896:928, :], writes=[("ki", "sm")])
        ACT(kb, sq[:], ki[:], AF.Square, [("ki", "raw")], [("ki", "sq")])
        for ts in range(4):
            for b in range(2):
                c0 = ts * 1024 + b * 512
                MM(kb, ps1[:, b * 512:(b + 1) * 512], C["ones_f"][0:64, :], ki[0:64, c0:c0 + 512], True, True,
                   [("ki", "raw")], [("ki", "ps1")])
                MM(kb, ps2[:, b * 512:(b + 1) * 512], C["ones_f"][0:64, :], sq[0:64, c0:c0 + 512], True, True,
                   [("ki", "sq")], [("ki", "ps2")])
            ACT(kb, mean[:], ps1[:], AF.Copy, [("ki", "ps1")], [("ki", "mean")], scale=1.0 / 64)
            TT(kb, "dve", var[:], mean[:], mean[:], ALU.mult, [("ki", "mean")], [("ki", "var")])
            STT(kb, var[:], ps2[:], 1.0 / 64, var[:], ALU.mult, ALU.subtract, [("ki", "ps2"), ("ki", "var")],
                [("ki", "var")])
            ACT(kb, var[:], var[:], AF.Sqrt, [("ki", "var")], [("ki", "var")], bias=C["eps"][:, 0:1])
            RECIP(kb, var[:], var[:], [("ki", "var")], [("ki", "var")])
            sl = slice(ts * 1024, (ts + 1) * 1024)
            TT(kb, "dve", ki[:, sl], ki[:, sl], mean[:], ALU.subtract, [("ki", "raw"), ("ki", "mean")], [("ki", "raw")])
            TT(kb, "dve", ki[:, sl], ki[:, sl], var[:], ALU.mult, [("ki", "raw"), ("ki", "var")], [("ki", "raw")])
            TS(kb, "dve", kio[:, sl], ki[:, sl], C["ikw"][:, 0:1], C["ikb"][:, 0:1], ALU.mult, ALU.add,
               [("ki", "raw")], [("ki", "o")])
        kb.dma("sp", ch, S["kiT"], kio[:], reads=[("ki", "o")], writes=[("kiT",)])
        for g in range(2):
            for tt in range(16):
                t = g * 16 + tt
                TR(kb, pst[:, tt, :], sm[:, t * 128:(t + 1) * 128], C["ident"][0:32, 0:32], [("ki", "sm")],
                   [("ki", "pst")], signal=(tt == 15))
            CP(kb, "dve", tmo[:, g * 16:(g + 1) * 16, :], pst[:], [("ki", "pst")], [("ki", "tmo")])
        kb.dma("sp", ch, S["tmaj"].rearrange("(n p) c -> p n c", p=128), tmo[:], reads=[("ki", "tmo")],
               writes=[("tmaj",)])
        kb.end_phase()


N_BIS = 16
SCALE_A = 128 ** -0.5


def phase_dsa_attn(kb, C, pj, S, y_dram):
    tag = "at"
    with ExitStack() as st:
        kT = kb.sb("at_kT", [128, 8, T], BF16, st)
        Vt = kb.sb("at_V", [128, NT, 1024], BF16, st)
        kiT = kb.sb("at_kiT", [128, T], BF16, st)
        score1 = kb.sb("at_sc", [128, T], F32, st)
        score = [score1, score1]
        mask = kb.sb("at_mask", [128, T], BF16, st)
        junk = mask
        maskT1 = kb.sb("at_maskT", [128, NT, 128], BF16, st)
        maskT = [maskT1, maskT1]
        rbuf = [kb.sb(f"at_r{i}", [128, 2, 512], BF16, st) for i in range(2)]
        dsg = kb.sb("at_dsg", [128, 16, 128], BF16, st)
        qb = [kb.sb(f"at_q{i}", [128, 8, 128], BF16, st) for i in range(2)]
        qib1 = kb.sb("at_qi", [128, 8, 128], BF16, st)
        qib = [qib1, qib1]
        wt = [kb.sb(f"at_w{i}", [128, 16], F32, st) for i in range(2)]
        bs = [kb.sb(f"at_bs{i}", [128, 8], F32, st) for i in range(2)]
        za1 = kb.sb("at_za", [128, 8, 128], F32, st)
        za = [za1, za1]
        pt = [kb.sb(f"at_pt{i}", [128, 4, 128], BF16, st) for i in range(2)]
        ptm = [kb.sb(f"at_ptm{i}", [128, 4, 128], BF16, st) for i in range(2)]
        ost1 = kb.sb("at_ost", [128, 8, 256], F32, st)
        ost = [ost1, ost1]
        yst1 = kb.sb("at_yst", [128, 8, 128], BF16, st)
        yst = [yst1, yst1]
        biasS = C["biasS"]
        ident_b = kb.sb("at_identb", [128, 128], BF16, st)
        ones_b = kb.sb("at_onesb", [128, 128], BF16, st)
        ips = [kb.ps(f"at_ips{i}", [128, 2, 512], F32, st) for i in range(2)]
        lg = [ips[i][:, 0, :].rearrange("p (a b) -> p a b", b=128) for i in range(2)]
        po1 = kb.ps("at_po", [128, 128], F32, st)
        prs1 = kb.ps("at_prs", [128, 128], F32, st)
        po, prs = [po1, po1], [prs1, prs1]
        sps1 = kb.ps("at_sps", [128, 512], F32, st)
        sps = [sps1, sps1]
        tps = ips[0][:, 0, :].bitcast(BF16)[:, 0:512].rearrange("p (a b) -> p a b", b=128)
        chl = kb.chan("at_ld")
        chq = [kb.chan(f"at_q{i}") for i in range(2)]
        chy = [kb.chan(f"at_y{i}") for i in range(2)]
        chz = kb.chan("at_z")
        kb.dma("sp", chl, kT[:], S["kT"].rearrange("h p t -> p h t"), writes=[(tag, "kT")])
        kb.dma("sp", chl, Vt[:], S["V"].rearrange("(n p) c -> p n c", p=128), writes=[(tag, "V")])
        kb.dma("sp", chl, kiT[:], S["kiT"], writes=[(tag, "kiT")])
        CP(kb, "dve", ident_b[:], C["ident"][:], [], [(tag, "cst")])
        CP(kb, "dve", ones_b[:], C["ones_f"][:], [], [(tag, "cst")])
        qTv = S["qT"].rearrange("h p t -> p h t")
        qiTv = S["qiT"].rearrange("h p t -> p h t")
        zav = pj[1024:2048, :].rearrange("(h p) t -> p h t", p=128)
        yv = y_dram[0:1024, :].rearrange("(h p) t -> p h t", p=128)
        cnt = {"ips": 0, "r": 0, "lg": 0, "pt": 0, "po": 0, "sps": 0}

        def stage_a(i):
            s = i % 2
            c0, c1 = i * 128, (i + 1) * 128
            kb.dma("sp", chq[s], qb[s][:], qTv[:, :, c0:c1], writes=[(tag, "q", s)])
            kb.dma("sp", chq[s], qib[s][:], qiTv[:, :, c0:c1], writes=[(tag, "qi")])
            kb.dma("sp", chq[s], wt[s][:], S["tmaj"][c0:c1, 0:16], writes=[(tag, "w", s)])
            nW = (i + 4) // 4
            Wi = nW * 512
            sk = (tag, "score")
            TT(kb, "dve", dsg[:], ident_b[:].unsqueeze(1).to_broadcast([128, 16, 128]),
               wt[s][:, 0:16].unsqueeze(2).to_broadcast([128, 16, 128]), ALU.mult, [(tag, "w", s), (tag, "cst")],
               [(tag, "dsg")])
            for w in range(nW):
                sp_ = cnt["sps"] % 2
                cnt["sps"] += 1
                units = []

                def acc(u):
                    h0, r_ = u
                    for e_ in range(2):
                        MM(kb, sps[sp_][:], dsg[:, h0 + e_, :], rbuf[r_][:, e_, :], h0 + e_ == 0, h0 + e_ == 15,
                           [(tag, "dsg"), (tag, "r", r_)], [(tag, "sps", 0)], signal=(e_ == 1))

                for pair in range(8):
                    p = cnt["ips"] % 2
                    cnt["ips"] += 1
                    r = cnt["r"] % 2
                    cnt["r"] += 1
                    for e_ in range(2):
                        base = e_ * 64
                        MM(kb, ips[p][:, e_, :], qib[s][base:base + 64, pair, :],
                           kiT[base:base + 64, w * 512:(w + 1) * 512], True, True, [(tag, "qi"), (tag, "kiT")],
                           [(tag, "ips", p)], signal=(e_ == 1))
                    ACT(kb, rbuf[r][:], ips[p][:], AF.Relu, [(tag, "ips", p)], [(tag, "r", r)])
                    if units:
                        acc(units.pop())
                    units.append((2 * pair, r))
                acc(units.pop())
                CP(kb, "dve", score[s][:, w * 512:(w + 1) * 512], sps[sp_][:], [(tag, "sps", 0)], [sk])
            b = bs[s]
            bk = (tag, "bs", s)
            TS(kb, "dve", junk[:, 0:Wi], score[s][:, 0:Wi], 1.0, None, ALU.mult, ALU.max, [sk], [(tag, "mask"), bk],
               accum_out=b[:, 0:1])
            TS(kb, "dve", junk[:, 0:Wi], score[s][:, 0:Wi], -1.0, None, ALU.mult, ALU.max, [sk], [(tag, "mask"), bk],
               accum_out=b[:, 6:7])
            TT(kb, "dve", b[:, 0:1], b[:, 0:1], b[:, 6:7], ALU.max, [bk], [bk])
            TT(kb, "dve", score[s][:, Wi - 512:Wi], score[s][:, Wi - 512:Wi],
               C["cbase"][:, 384 - (i % 4) * 128:896 - (i % 4) * 128], ALU.add, [sk], [sk])
            TS(kb, "dve", b[:, 1:2], b[:, 0:1], -1.001, -1e-20, ALU.mult, ALU.add, [bk], [bk])
            TS(kb, "dve", b[:, 2:3], b[:, 0:1], 2.002, 2e-20, ALU.mult, ALU.add, [bk], [bk])
            for k in range(1, N_BIS + 1):
                f = 2.0 ** -k
                STT(kb, b[:, 3:4], b[:, 2:3], f, b[:, 1:2], ALU.mult, ALU.add, [bk], [bk])
                TS(kb, "dve", junk[:, 0:Wi], score[s][:, 0:Wi], b[:, 3:4], None, ALU.is_ge, ALU.add,
                   [sk, bk], [(tag, "mask"), bk], accum_out=b[:, 4:5])
                TS(kb, "dve", b[:, 5:6], b[:, 4:5], 255.5, f, ALU.is_ge, ALU.mult, [bk], [bk])
                STT(kb, b[:, 1:2], b[:, 5:6], b[:, 2:3], b[:, 1:2], ALU.mult, ALU.add, [bk], [bk])

        def stage_t(i):
            s = i % 2
            n = i + 1
            TS(kb, "dve", mask[:, 0:n * 128], score[s][:, 0:n * 128], bs[s][:, 1:2], None, ALU.is_ge, None,
               [(tag, "score"), (tag, "bs", s)], [(tag, "mask")])
            for j0 in range(0, n, 4):
                nb = min(4, n - j0)
                for jj in range(nb):
                    j = j0 + jj
                    TR(kb, tps[:, jj, :], mask[:, j * 128:(j + 1) * 128], ident_b[:], [(tag, "mask"), (tag, "cst")],
                       [(tag, "ips", 0)], signal=(jj == nb - 1))
                CP(kb, "act", maskT[s][:, j0:j0 + nb, :], tps[:, 0:nb, :], [(tag, "ips", 0)], [(tag, "maskT")])

        def stage_b(i):
            s = i % 2
            n = i + 1
            for h in range(8):
                o = cnt["po"] % 2
                cnt["po"] += 1
                for j0 in range(0, n, 4):
                    nb = min(4, n - j0)
                    p = cnt["lg"] % 2
                    cnt["lg"] += 1
                    x = cnt["pt"] % 2
                    cnt["pt"] += 1
                    for jj in range(nb):
                        j = j0 + jj
                        near = (i - j) <= 1
                        MM(kb, lg[p][:, jj, :], kT[:, h, j * 128:(j + 1) * 128], qb[s][:, h, :], True, not near,
                           [(tag, "kT"), (tag, "q", s)], [(tag, "ips", p)], signal=(jj == nb - 1 and not near))
                        if near:
                            MM(kb, lg[p][:, jj, :], ident_b[:], biasS[:, h, i - j, :], False, True,
                               [(tag, "cst")], [(tag, "ips", p)], signal=(jj == nb - 1))
                    ACT(kb, pt[x][:, 0:nb, :], lg[p][:, 0:nb, :], AF.Exp, [(tag, "ips", p)], [(tag, "pt", x)],
                        scale=SCALE_A, bias=C["cb"][:, h:h + 1])
                    TT(kb, "pool", ptm[x][:, 0:nb, :], pt[x][:, 0:nb, :], maskT[s][:, j0:j0 + nb, :], ALU.mult,
                       [(tag, "pt", x), (tag, "maskT")], [(tag, "ptm", x)])
                    for jj in range(nb):
                        j = j0 + jj
                        MM(kb, po[o][:], Vt[:, j, h * 128:(h + 1) * 128], ptm[x][:, jj, :], j == 0, j == n - 1,
                           [(tag, "V"), (tag, "ptm", x)], [(tag, "po", 0)])
                        MM(kb, prs[o][:], ones_b[:], ptm[x][:, jj, :], j == 0, j == n - 1,
                           [(tag, "cst"), (tag, "ptm", x)], [(tag, "prs", 0)], signal=(jj == nb - 1))
                CP(kb, "act", ost[s][:, h, 0:128], po[o][:], [(tag, "po", 0)], [(tag, "ost", h)])
                CP(kb, "act", ost[s][:, h, 128:256], prs[o][:], [(tag, "prs", 0)], [(tag, "ost", h)])

        def stage_f(i):
            s = i % 2
            keys = [(tag, "ost", h) for h in range(8)]
            kb.dma("sp", chz, za[s][:], zav[:, :, i * 128:(i + 1) * 128], writes=[(tag, "za")])
            ACT(kb, za[s][:], za[s][:], AF.Silu, [(tag, "za")], [(tag, "za")])
            RECIP(kb, ost[s][:, :, 128:256], ost[s][:, :, 128:256], keys, keys)
            TT(kb, "dve", ost[s][:, :, 0:128], ost[s][:, :, 0:128], ost[s][:, :, 128:256], ALU.mult, keys, keys)
            TT(kb, "dve", yst[s][:], ost[s][:, :, 0:128], za[s][:], ALU.mult, keys + [(tag, "za")], [(tag, "yst")])
            kb.dma("pool", chy[s], yv[:, :, i * 128:(i + 1) * 128], yst[s][:], reads=[(tag, "yst")],
                   writes=[(tag, "y", i)])

        stage_a(0)
        stage_t(0)
        for i in range(NT):
            if i + 1 < NT:
                stage_a(i + 1)
            stage_b(i)
            if i + 1 < NT:
                stage_t(i + 1)
            stage_f(i)
        kb.end_phase()


def phase_gdn_prep(kb, C, pj, S):
    tag = "gp"
    with ExitStack() as st:
        W = T + 3
        raw = [kb.sb(f"gp_raw{i}", [128, W], F32, st) for i in range(2)]
        acc = [kb.sb(f"gp_acc{i}", [128, T], F32, st) for i in range(2)]
        sq = kb.sb("gp_sq", [128, T], F32, st)
        rn = kb.sb("gp_rn", [128, T], F32, st)
        ssp = kb.ps("gp_ssp", [128, T], F32, st)
        chl = [kb.chan(f"gp_l{i}") for i in range(2)]
        chs = [kb.chan(f"gp_s{i}") for i in range(2)]
        for cc in range(24):
            s = cc % 2
            r0 = 2048 + cc * 128
            MS(kb, "pool", raw[s][:, 0:3], 0.0, [(tag, "raw", s)])
            kb.dma("sp", chl[s], raw[s][:, 3:W], pj[r0:r0 + 128, :], writes=[(tag, "raw", s)])
            TS(kb, "dve", acc[s][:], raw[s][:, 3:W], C["cvw"][:, cc, 3:4], None, ALU.mult, None, [(tag, "raw", s)],
               [(tag, "acc", s)])
            for j in range(3):
                STT(kb, acc[s][:], raw[s][:, j:j + T], C["cvw"][:, cc, j:j + 1], acc[s][:], ALU.mult, ALU.add,
                    [(tag, "raw", s), (tag, "acc", s)], [(tag, "acc", s)])
            ACT(kb, acc[s][:], acc[s][:], AF.Silu, [(tag, "acc", s)], [(tag, "acc", s)])
            if cc < 16:
                ACT(kb, sq[:], acc[s][:], AF.Square, [(tag, "acc", s)], [(tag, "sq")])
                for b in range(8):
                    MM(kb, ssp[:, b * 512:(b + 1) * 512], C["ones_f"][:], sq[:, b * 512:(b + 1) * 512], True, True,
                       [(tag, "sq")], [(tag, "ssp")], signal=(b == 7))
                ACT(kb, rn[:], ssp[:], AF.Sqrt, [(tag, "ssp")], [(tag, "rn")], bias=C["eps"][:, 0:1])
                RECIP(kb, rn[:], rn[:], [(tag, "rn")], [(tag, "rn")])
                STT(kb, acc[s][:], acc[s][:], (128 ** -0.5) if cc < 8 else 1.0, rn[:], ALU.mult, ALU.mult,
                    [(tag, "acc", s), (tag, "rn")], [(tag, "acc", s)])
            kb.dma("pool", chs[s], S["gqkv"][cc * 128:(cc + 1) * 128, :], acc[s][:], reads=[(tag, "acc", s)],
                   writes=[("gqkv", cc)])
        kb.end_phase()


import os
GDN_TILES = int(os.environ.get("GDN_TILES", "32"))
GDN_STOP = int(os.environ.get("GDN_STOP", "99"))
GDN_SUB = float(os.environ.get("GDN_SUB", "99"))


def phase_gdn(kb, C, pj, S, y_dram):
    with ExitStack() as stc:
        C = dict(C)
        chc = kb.chan("gd_const")
        for name, shape, dt in GDN_CONST_SPECS:
            d = kb.nc.dram_tensor(name, list(shape), dt, kind="ExternalInput").ap()
            t = kb.sb("c_" + name, shape, dt, stc)
            kb.dma("sp", chc, t[:], d, writes=[("const", name)])
            C[name] = t
        kb.end_phase()
        phase_gdn_prep(kb, C, pj, S)
        _phase_gdn_main(kb, C, pj, S, y_dram)


def _phase_gdn_main(kb, C, pj, S, y_dram):
    tag = "gd"
    H = 8
    with ExitStack() as st:
        def fb(name, shape=(128, H, 128), dt=F32):
            return kb.sb("gd_" + name, list(shape), dt, st)

        tm = fb("tm", (128, NT, 32))
        beta = fb("beta", (128, NT, 8))
        g = fb("g", (128, NT, 8))
        t1 = fb("t1", (128, NT, 8))
        t2 = fb("t2", (128, NT, 8))
        nA = fb("nA", (128, 8))
        qT, kT, vT = fb("qT"), fb("kT"), fb("vT")
        gd, egrow, gcr = fb("gdiag"), fb("egrow"), fb("gcr")
        P1, E1, E2 = fb("P1"), fb("E1"), fb("E2")
        X = [fb("X0"), fb("X1")]
        Y = [fb("Y0"), fb("Y1")]
        P, attnT = fb("P"), fb("attnT")
        vb, kbg, kd, kd1 = fb("vb"), fb("kbg"), fb("kd"), fb("kd1")
        smk = fb("smk", (128, 16))
        u, wT, qgT, vnew = fb("u"), fb("wT"), fb("qgT"), fb("vnew")
        Sst, oacc, zb = fb("S"), fb("oacc"), fb("zb")
        osq, orn = fb("osq"), fb("orn")
        yo = fb("yo", (128, H, 128), BF16)
        sm = fb("sm", (128, 64))
        pA = kb.ps("gd_pA", [128, H, 128], F32, st)
        pB = kb.ps("gd_pB", [128, H, 128], F32, st)
        pC = kb.ps("gd_pC", [128, H, 128], F32, st)
        pO = kb.ps("gd_pO", [128, H, 64], F32, st)
        psm = kb.ps("gd_psm", [128, 32], F32, st)
        ch = kb.chan("gd_l")
        chq = kb.chan("gd_q")
        chz = kb.chan("gd_z")
        chy = kb.chan("gd_y")
        K_ = lambda n: (tag, n)

        def bc_h(ap2d):
            return ap2d.unsqueeze(1).to_broadcast([128, H, 128])

        def bc_f(ap2d):
            return ap2d.unsqueeze(2).to_broadcast([128, H, 128])

        kb.dma("sp", ch, tm[:], S["tmaj"].rearrange("(n p) c -> p n c", p=128), writes=[K_("tm")])
        ACT(kb, beta[:], tm[:, :, 16:24], AF.Sigmoid, [K_("tm")], [K_("beta")])
        dtb = C["dtb_bc"][:].unsqueeze(1).to_broadcast([128, NT, 8])
        TT(kb, "dve", g[:], tm[:, :, 24:32], dtb, ALU.add, [K_("tm")], [K_("g")])
        TS(kb, "dve", t1[:], g[:], -1.0, None, ALU.mult, None, [K_("g")], [K_("t1")])
        TT(kb, "dve", t1[:], t1[:], g[:], ALU.max, [K_("t1"), K_("g")], [K_("t1")])
        ACT(kb, t1[:], t1[:], AF.Exp, [K_("t1")], [K_("t1")], scale=-1.0)
        TS(kb, "dve", t1[:], t1[:], 1.0, None, ALU.add, None, [K_("t1")], [K_("t1")])
        ACT(kb, t1[:], t1[:], AF.Ln, [K_("t1")], [K_("t1")])
        TS(kb, "dve", t2[:], g[:], 0.0, None, ALU.max, None, [K_("g")], [K_("t2")])
        TT(kb, "dve", t2[:], t2[:], t1[:], ALU.add, [K_("t1"), K_("t2")], [K_("t2")])
        ACT(kb, nA[:], C["alog_bc"][:], AF.Exp, [], [K_("nA")])
        TS(kb, "dve", nA[:], nA[:], -1.0, None, ALU.mult, None, [K_("nA")], [K_("nA")])
        TT(kb, "dve", g[:], t2[:], nA[:].unsqueeze(1).to_broadcast([128, NT, 8]), ALU.mult, [K_("t2"), K_("nA")],
           [K_("g")])
        MS(kb, "dve", Sst[:], 0.0, [K_("S")])
        MS(kb, "dve", vnew[:], 0.0, [K_("vnew")])
        gq = S["gqkv"]
        qv = gq[0:1024, :].rearrange("(h p) t -> p h t", p=128)
        kv = gq[1024:2048, :].rearrange("(h p) t -> p h t", p=128)
        vv = gq[2048:3072, :].rearrange("(h p) t -> p h t", p=128)
        zv = pj[5120:6144, :].rearrange("(h p) t -> p h t", p=128)
        yv = y_dram[1024:2048, :].rearrange("(h p) t -> p h t", p=128)

        for n in range(GDN_TILES if GDN_STOP > 0 else 0):
            c0, c1 = n * 128, (n + 1) * 128
            kb.dma("sp", chq, qT[:], qv[:, :, c0:c1], writes=[K_("qT")])
            kb.dma("sp", chq, kT[:], kv[:, :, c0:c1], writes=[K_("kT")])
            kb.dma("sp", chq, vT[:], vv[:, :, c0:c1], writes=[K_("vT")])
            kb.dma("sp", chz, zb[:], zv[:, :, c0:c1], writes=[K_("zb")])
            gn = g[:, n, :]
            bn = beta[:, n, :]
            MM(kb, psm[:, 0:8], C["U2"][:], gn, True, True, [K_("g")], [K_("psm")], signal=False)
            MM(kb, psm[:, 8:16], C["Bsame"][:], gn, True, True, [K_("g")], [K_("psm")], signal=False)
            MM(kb, psm[:, 16:24], C["Bsel0"][:], gn, True, True, [K_("g")], [K_("psm")], signal=False)
            MM(kb, psm[:, 24:32], C["Bsel1"][:], gn, True, True, [K_("g")], [K_("psm")])
            CP(kb, "dve", sm[:, 0:32], psm[:], [K_("psm")], [K_("sm")])
            gc = sm[:, 0:8]
            ACT(kb, sm[:, 32:40], sm[:, 0:8], AF.Exp, [K_("sm")], [K_("sm")])
            TT(kb, "dve", sm[:, 40:48], sm[:, 8:16], sm[:, 0:8], ALU.subtract, [K_("sm")], [K_("sm")])
            ACT(kb, sm[:, 40:48], sm[:, 40:48], AF.Exp, [K_("sm")], [K_("sm")])
            ACT(kb, sm[:, 16:32], sm[:, 16:32], AF.Exp, [K_("sm")], [K_("sm")])
            TT(kb, "dve", sm[:, 48:56], sm[:, 32:40], bn, ALU.mult, [K_("sm"), K_("beta")], [K_("sm")])
            TS(kb, "dve", sm[:, 56:64], bn, -1.0, None, ALU.mult, None, [K_("beta")], [K_("sm")])
            if GDN_SUB <= 0:
                continue
            TT(kb, "dve", gd[:], bc_h(C["U2"][:]), bc_f(gn), ALU.mult, [K_("g")], [K_("gdiag")])
            for b in range(2):
                MM(kb, pA[:, 4 * b:4 * b + 4, :], C["ones_f"][:], gd[:, 4 * b:4 * b + 4, :], True, True,
                   [K_("gdiag")], [K_("pA")], signal=(b == 1))
            if GDN_SUB <= 0.3:
                continue
            CP(kb, "dve", gcr[:], pA[:], [K_("pA")], [K_("gcr")])
            ACT(kb, egrow[:], gcr[:], AF.Exp, [K_("gcr")], [K_("egrow")])
            if GDN_SUB <= 0.4:
                continue
            TT(kb, "dve", P1[:], gcr[:], bc_f(gc), ALU.subtract, [K_("gcr"), K_("sm")], [K_("P1")])
            if GDN_SUB <= 0.5:
                continue
            TS(kb, "dve", E1[:], P1[:], 0.0, None, ALU.max, None, [K_("P1")], [K_("E1")])
            ACT(kb, E1[:], E1[:], AF.Exp, [K_("E1")], [K_("E1")], scale=-1.0)
            if GDN_SUB <= 0.6:
                continue
            TS(kb, "dve", E2[:], P1[:], 0.0, None, ALU.min, None, [K_("P1")], [K_("E2")])
            ACT(kb, E2[:], E2[:], AF.Exp, [K_("E2")], [K_("E2")])
            TT(kb, "dve", E1[:], E1[:], bc_h(C["MLs"][:]), ALU.mult, [K_("E1")], [K_("E1")])
            TT(kb, "dve", E1[:], E1[:], bc_f(sm[:, 56:64]), ALU.mult, [K_("E1"), K_("sm")], [K_("E1")])
            TT(kb, "dve", E2[:], E2[:], bc_h(C["MU"][:]), ALU.mult, [K_("E2")], [K_("E2")])
            if GDN_SUB <= 1:
                continue
            for h in range(H):
                MM(kb, pB[:, h, :], kT[:, h, :], kT[:, h, :], True, True, [K_("kT")], [K_("pB")], signal=(h == H - 1))
            TT(kb, "dve", X[0][:], pB[:], E1[:], ALU.mult, [K_("pB"), K_("E1")], [K_("X0")])
            for h in range(H):
                MM(kb, pC[:, h, :], kT[:, h, :], qT[:, h, :], True, True, [K_("kT"), K_("qT")], [K_("pC")], signal=(h == H - 1))
            TT(kb, "dve", attnT[:], pC[:], E2[:], ALU.mult, [K_("pC"), K_("E2")], [K_("attnT")])
            for h in range(H):
                TR(kb, pA[:, h, :], X[0][:, h, :], C["ident"][:], [K_("X0")], [K_("pA")], signal=(h == H - 1))
            CP(kb, "dve", Y[0][:], pA[:], [K_("pA")], [K_("Y0")])
            TT(kb, "dve", P[:], Y[0][:], bc_h(C["ident"][:]), ALU.add, [K_("Y0")], [K_("P")])
            if GDN_SUB <= 2:
                continue
            cur = 0
            for lvl in range(5):
                nxt = 1 - cur
                xk, yk = K_(f"X{cur}"), K_(f"Y{cur}")
                xn, yn = K_(f"X{nxt}"), K_(f"Y{nxt}")
                for h in range(H):
                    MM(kb, pB[:, h, :], Y[cur][:, h, :], X[cur][:, h, :], True, True, [xk, yk], [K_("pB")], signal=(h == H - 1))
                CP(kb, "dve", X[nxt][:], pB[:], [K_("pB")], [xn])
                if lvl < 4:
                    for h in range(H):
                        MM(kb, pC[:, h, :], X[cur][:, h, :], Y[cur][:, h, :], True, True, [xk, yk], [K_("pC")], signal=(h == H - 1))
                    CP(kb, "dve", Y[nxt][:], pC[:], [K_("pC")], [yn])
                for h in range(H):
                    MM(kb, pA[:, h, :], X[nxt][:, h, :], P[:, h, :], True, True, [xn, K_("P")], [K_("pA")], signal=(h == H - 1))
                TT(kb, "dve", P[:], P[:], pA[:], ALU.add, [K_("pA"), K_("P")], [K_("P")])
                cur = nxt
            if GDN_SUB <= 3:
                continue
            for h in range(H):
                TR(kb, pB[:, h, :], kT[:, h, :], C["ident"][:], [K_("kT")], [K_("pB")], signal=(h == H - 1))
            TT(kb, "dve", kbg[:], pB[:], bc_f(sm[:, 48:56]), ALU.mult, [K_("pB"), K_("sm")], [K_("kbg")])
            TS(kb, "dve", smk[:, 0:8], sm[:, 40:48], C["Bsel0"][:, 0:1], None, ALU.mult, None, [K_("sm")], [K_("smk")])
            TS(kb, "dve", smk[:, 8:16], sm[:, 40:48], C["Bsel1"][:, 0:1], None, ALU.mult, None, [K_("sm")], [K_("smk")])
            TT(kb, "dve", kd[:], pB[:], bc_f(smk[:, 0:8]), ALU.mult, [K_("pB"), K_("smk")], [K_("kd")])
            TT(kb, "dve", kd1[:], pB[:], bc_f(smk[:, 8:16]), ALU.mult, [K_("pB"), K_("smk")], [K_("kd")])
            for h in range(H):
                TR(kb, pC[:, h, :], vT[:, h, :], C["ident"][:], [K_("vT")], [K_("pC")], signal=(h == H - 1))
            TT(kb, "dve", vb[:], pC[:], bc_f(bn), ALU.mult, [K_("pC"), K_("beta")], [K_("vb")])
            for h in range(H):
                MM(kb, pA[:, h, :], P[:, h, :], vb[:, h, :], True, True, [K_("P"), K_("vb")], [K_("pA")], signal=(h == H - 1))
            CP(kb, "dve", u[:], pA[:], [K_("pA")], [K_("u")])
            for h in range(H):
                MM(kb, pB[:, h, :], kbg[:, h, :], P[:, h, :], True, True, [K_("P"), K_("kbg")], [K_("pB")], signal=(h == H - 1))
            CP(kb, "dve", wT[:], pB[:], [K_("pB")], [K_("wT")])
            TT(kb, "dve", qgT[:], qT[:], egrow[:], ALU.mult, [K_("qT"), K_("egrow")], [K_("qgT")])
            if GDN_STOP <= 1:
                continue
            for c in range(2):
                r0, r1 = c * 64, (c + 1) * 64
                for h in range(H):
                    MM(kb, pA[r0:r1, h, :], wT[:, h, r0:r1], Sst[:, h, :], True, True, [K_("wT"), K_("S")], [K_("pA")],
                       signal=(h == H - 1))
                if GDN_SUB == 10 or (GDN_SUB == 10.5 and c == 1):
                    continue
                TT(kb, "dve", vnew[r0:r1, :, :], u[r0:r1, :, :], pA[r0:r1, :, :], ALU.subtract, [K_("u"), K_("pA")],
                   [K_("vnew")])
                if GDN_SUB == 11:
                    continue
                for h in range(H):
                    MM(kb, pO[:, h, :], Sst[:, h, :], qgT[:, h, r0:r1], True, False, [K_("S"), K_("qgT")], [K_("pO")])
                    MM(kb, pO[:, h, :], vnew[r0:r1, h, :], attnT[r0:r1, h, r0:r1], False, True,
                       [K_("vnew"), K_("attnT")], [K_("pO")], signal=(h == H - 1))
                if GDN_SUB == 12:
                    continue
                CP(kb, "dve", oacc[:, :, r0:r1], pO[:], [K_("pO")], [K_("oacc")])
                if GDN_STOP <= 2:
                    continue
                for h in range(H):
                    MM(kb, pB[:, h, :], (kd, kd1)[c][:, h, :], vnew[:, h, :], True, True, [K_("kd"), K_("vnew")],
                       [K_("pB")], signal=(h == H - 1))
                TT(kb, "dve", Sst[:], Sst[:], bc_f(sm[:, 16 + 8 * c:24 + 8 * c]), ALU.mult, [K_("S"), K_("sm")],
                   [K_("S")])
                TT(kb, "dve", Sst[:], Sst[:], pB[:], ALU.add, [K_("S"), K_("pB")], [K_("S")])
            ACT(kb, osq[:], oacc[:], AF.Square, [K_("oacc")], [K_("osq")])
            for b in range(2):
                MM(kb, pC[:, 4 * b:4 * b + 4, :], C["ones_f"][:], osq[:, 4 * b:4 * b + 4, :], True, True, [K_("osq")],
                   [K_("pC")], signal=(b == 1))
            CP(kb, "dve", orn[:], pC[:], [K_("pC")], [K_("orn")])
            ACT(kb, orn[:], orn[:], AF.Sqrt, [K_("orn")], [K_("orn")], bias=C["eps"][:, 0:1], scale=1.0 / 128)
            RECIP(kb, orn[:], orn[:], [K_("orn")], [K_("orn")])
            STT(kb, oacc[:], oacc[:], C["onorm"][:, 0:1], orn[:], ALU.mult, ALU.mult, [K_("oacc"), K_("orn")],
                [K_("oacc")])
            ACT(kb, zb[:], zb[:], AF.Silu, [K_("zb")], [K_("zb")])
            TT(kb, "dve", yo[:], oacc[:], zb[:], ALU.mult, [K_("oacc"), K_("zb")], [K_("yo")])
            kb.dma("pool", chy, yv[:, :, c0:c1], yo[:], reads=[K_("yo")], writes=[("y0b", n)])
        kb.end_phase()

def load_consts(kb, nc, names_shapes):
    C = {}
    ch = kb.chan("const")
    for name, shape, dt in names_shapes:
        d = nc.dram_tensor(name, list(shape), dt, kind="ExternalInput").ap()
        if name == "cbase":
            t = kb.sb("c_" + name, shape, BF16)
            kb.dma("pool", ch, t[:], d, writes=[("const", name)])
        else:
            t = kb.sb("c_" + name, shape, dt)
            kb.dma("sp", ch, t[:], d, writes=[("const", name)])
        C[name] = t
    kb.end_phase()
    with ExitStack() as st:
        d = nc.dram_tensor("biasT", [128, 8, 2, 128], F32, kind="ExternalInput").ap()
        C["biasS"] = kb.sb("c_biasS", [128, 8, 2, 128], BF16)
        bt = kb.sb("biasT_tmp", [128, 8, 2, 128], F32, st)
        kb.dma("sp", ch, bt[:], d, writes=[("const", "biasT")])
        for h in range(8):
            TS(kb, "dve", C["biasS"][:, h, :, :], bt[:, h, :, :], C["cb"][:, h:h + 1], 128 ** 0.5,
               ALU.subtract, ALU.mult, [("const", "biasT")], [("const", "biasS")])
        kb.end_phase()
    return C


CONST_SPECS = [
    ("ident", (128, 128), F32),
    ("ones_f", (128, 128), F32),
    ("eps", (128, 1), F32),
    ("normwT", (128, 2, 16), F32),
    ("cbase", (128, 896), F32),
    ("cb", (128, 8), F32),
    ("qnormT", (128, 4), F32),
    ("kvnormT", (128, 2), F32),
    ("qgain", (128, 1), F32),
    ("kgain", (128, 1), F32),
    ("ikw", (128, 1), F32),
    ("ikb", (128, 1), F32),
]

CD_CONST_SPECS = [
    ("dww", (128, 8, 31), F32),
    ("dwb", (128, 8), F32),
    ("lnw", (128, 8), F32),
    ("lnb", (128, 8), F32),
    ("dcw", (128, 8, 3), F32),
]

GDN_CONST_SPECS = [
    ("cvw", (128, 24, 4), F32),
    ("alog_bc", (128, 8), F32),
    ("dtb_bc", (128, 8), F32),
    ("onorm", (128, 1), F32),
    ("U2", (128, 128), F32),
    ("Bsame", (128, 128), F32),
    ("Bsel0", (128, 128), F32),
    ("Bsel1", (128, 128), F32),
    ("MLs", (128, 128), F32),
    ("MU", (128, 128), F32),
]


def build_program(layers=(0, 1), l0_parts=("a", "b"), debug_out=False):
    nc = bass.Bass("TRN2", target_bir_lowering=False)

    def din(name, shape, dt=F32):
        return nc.dram_tensor(name, list(shape), dt, kind="ExternalInput").ap()

    def dscr(name, shape, dt=F32):
        return nc.dram_tensor(name, list(shape), dt, kind="Internal").ap()

    x = din("x", [T, D])
    out = nc.dram_tensor("out", [T, D], F32, kind="ExternalOutput").ap()
    kb = KB(nc)
    C = load_consts(kb, nc, CONST_SPECS)
    src = x
    if 0 in layers:
        ab_w_in = din("ab_w_in", [48, 128, 16 * 128])
        ab_w_out = din("ab_w_out", [16, 128, D])
        W = {"w_uqiq": din("w_uqiq", [16, 128, 4 * 128]), "w_uk": din("w_uk", [8, 128, 2 * 128]),
             "w_uv": din("w_uv", [2, 128, 1024])}
        pj0 = dscr("pj0", [6144, T])
        S = {"qT": dscr("s_qT", [8, 128, T], BF16), "qiT": dscr("s_qiT", [8, 128, T], BF16),
             "kT": dscr("s_kT", [8, 128, T], BF16), "V": dscr("s_V", [T, 1024], BF16),
             "kiT": dscr("s_kiT", [128, T], BF16), "tmaj": dscr("s_tmaj", [T, 32]),
             "gqkv": dscr("s_gqkv", [3072, T])}
        if debug_out:
            y0 = nc.dram_tensor("y0", [2048, T], BF16, kind="ExternalOutput").ap()
        else:
            y0 = dscr("y0", [2048, T], BF16)
        x1 = dscr("x1", [T, D]) if 1 in layers else out
        phase_inproj(kb, C, "l0", src, C["normwT"][:, 0, :], ab_w_in, 48, pj0)
        phase_ki_tmaj(kb, C, pj0, S)
        if "a" in l0_parts:
            phase_dsa_prep(kb, C, pj0, W, S)
            phase_dsa_attn(kb, C, pj0, S, y0)
        if "b" in l0_parts:
            phase_gdn(kb, C, pj0, S, y0)
        if not debug_out:
            phase_outproj(kb, C, "l0o", y0, ab_w_out, src, x1)
        src = x1
    if 1 in layers:
        cd_w_in = din("cd_w_in", [56, 128, 16 * 128])
        cd_w_out = din("cd_w_out", [16, 128, D])
        pj1 = dscr("pj1", [7168, T])
        y1 = dscr("y1", [2048, T], BF16)
        phase_inproj(kb, C, "l1", src, C["normwT"][:, 1, :], cd_w_in, 56, pj1)
        phase_cd_mix(kb, C, pj1, C, y1)
        phase_outproj(kb, C, "l1o", y1, cd_w_out, src, out)
    kb.finish()
    kb.emit()
    kb.close()
    return nc, kb


def tile_w_in(w, nch):
    K, N = w.shape
    assert N == nch * 128
    return np.ascontiguousarray(w.reshape(K // 128, 128, nch, 128).transpose(2, 1, 0, 3)).reshape(nch, 128, -1)


def t5_bucket_np(dist):
    import math
    max_exact = 16
    dd = np.maximum(dist, 1).astype(np.float32)
    large = max_exact + (np.log(dd / max_exact) / math.log(128 / max_exact) * (32 - max_exact)).astype(np.int32)
    large = np.minimum(large, 31)
    return np.where(dist < max_exact, dist, large)


def colT(v, k):
    return np.ascontiguousarray(np.asarray(v, np.float32).reshape(k, 128).T)


def host_consts(inp):
    f = np.float32
    c = {}
    c["ident"] = np.eye(128, dtype=f)
    c["ones_f"] = np.ones((128, 128), f)
    c["eps"] = np.full((128, 1), EPS, f)
    c["normwT"] = np.ascontiguousarray(inp["norm_w"].reshape(2, 16, 128).transpose(2, 0, 1)).astype(f)
    c["dww"] = np.ascontiguousarray(inp["c_dw_w"][0].reshape(31, 8, 128).transpose(2, 1, 0)).astype(f)
    c["dwb"] = colT(inp["c_dw_b"][0], 8)
    c["lnw"] = colT(inp["c_ln_w"][0], 8)
    c["lnb"] = colT(inp["c_ln_b"][0], 8)
    c["dcw"] = np.ascontiguousarray(inp["d_conv_w"][0].reshape(3, 8, 128).transpose(2, 1, 0)).astype(f)
    r = np.arange(128)[:, None]
    cc = np.arange(896)[None, :]
    c["cbase"] = np.where(cc <= r + 384, 0.0, -1e30).astype(f)
    kl = np.arange(128)[:, None, None]
    dd = np.arange(2)[None, :, None]
    ql = np.arange(128)[None, None, :]
    dist = np.maximum(dd * 128 + ql - kl, 0)
    bt = np.asarray(inp["rel_bias"], f)[t5_bucket_np(dist)]
    c["biasT"] = np.ascontiguousarray(bt.transpose(0, 3, 1, 2))
    c["cb"] = np.ascontiguousarray(np.broadcast_to(np.asarray(inp["rel_bias"], f)[31][None, :], (128, 8)))
    c["qnormT"] = colT(inp["a_q_norm"][0], 4)
    c["kvnormT"] = colT(inp["a_kv_norm"][0], 2)
    c["qgain"] = np.asarray(inp["a_q_gain"][0], f).reshape(128, 1).copy()
    c["kgain"] = np.asarray(inp["a_k_gain"][0], f).reshape(128, 1).copy()
    c["ikw"] = np.tile(np.asarray(inp["a_ik_norm_w"][0], f), 2).reshape(128, 1).copy()
    c["ikb"] = np.tile(np.asarray(inp["a_ik_norm_b"][0], f), 2).reshape(128, 1).copy()
    c["cvw"] = np.ascontiguousarray(np.asarray(inp["b_conv_w"][0], f).reshape(4, 24, 128).transpose(2, 1, 0))
    c["alog_bc"] = np.ascontiguousarray(np.broadcast_to(np.asarray(inp["b_a_log"][0], f)[None, :], (128, 8)))
    c["dtb_bc"] = np.ascontiguousarray(np.broadcast_to(np.asarray(inp["b_dt_bias"][0], f)[None, :], (128, 8)))
    c["onorm"] = np.asarray(inp["b_o_norm"][0], f).reshape(128, 1).copy()
    a = np.arange(128)
    same = (a[:, None] // 64) == (a[None, :] // 64)
    c["U2"] = (same & (a[:, None] <= a[None, :])).astype(f)
    c["Bsame"] = same.astype(f)
    c["Bsel0"] = np.ascontiguousarray(np.broadcast_to((a[:, None] < 64), (128, 128))).astype(f)
    c["Bsel1"] = np.ascontiguousarray(np.broadcast_to((a[:, None] >= 64), (128, 128))).astype(f)
    c["MLs"] = (same & (a[:, None] > a[None, :])).astype(f)
    c["MU"] = (same & (a[:, None] <= a[None, :])).astype(f)
    return c


def host_shared(inp, layers=(0, 1)):
    f = np.float32
    sh = host_consts(inp)
    if 0 in layers:
        w = np.asarray(inp["ab_w_in"][0], f)
        wp = np.zeros((D, 6144), f)
        wp[:, 0:832] = w[:, 0:832]
        wp[:, 896:912] = w[:, 832:848]
        wp[:, 912:928] = w[:, 4944:4960]
        wp[:, 1024:2048] = w[:, 848:1872]
        wp[:, 2048:5120] = w[:, 1872:4944]
        wp[:, 5120:6144] = w[:, 4960:5984]
        sh["ab_w_in"] = tile_w_in(wp, 48)
        sh["ab_w_out"] = np.ascontiguousarray(np.asarray(inp["ab_w_out"][0], f).reshape(16, 128, D))
        sh["w_uqiq"] = tile_w_in(np.concatenate([inp["a_w_uq"][0], inp["a_w_iq"][0]], axis=1).astype(f), 16)
        sh["w_uk"] = tile_w_in(np.asarray(inp["a_w_uk"][0], f), 8)
        sh["w_uv"] = np.ascontiguousarray(np.asarray(inp["a_w_uv"][0], f).reshape(2, 128, 1024))
    if 1 in layers:
        sh["cd_w_in"] = tile_w_in(np.asarray(inp["cd_w_in"][0], f), 56)
        sh["cd_w_out"] = np.ascontiguousarray(np.asarray(inp["cd_w_out"][0], f).reshape(16, 128, D))
    return sh


def kernel(**inputs):
    inp = {k: np.asarray(v) for k, v in inputs.items()}
    nc, kb = build_program()
    sh = host_shared(inp)
    x = np.ascontiguousarray(inp["x"], dtype=np.float32)
    in_maps = [dict(sh, x=x[b]) for b in range(8)]
    res = run_bass_kernel_spmd(nc, in_maps, core_ids=list(range(8)))
    return np.stack([np.asarray(r["out"], np.float32) for r in res.results], axis=0)
```

```python
from contextlib import ExitStack
import numpy as np
import concourse.bass as bass
import concourse.mybir as mybir
from concourse.bass_utils import run_bass_kernel_spmd

F32 = mybir.dt.float32
BF16 = mybir.dt.bfloat16
ALU = mybir.AluOpType
AF = mybir.ActivationFunctionType
AX = mybir.AxisListType

T = 4096
D = 2048
NT = T // 128
EPS = 1e-6
ENGS = ("pe", "act", "dve", "pool", "sp")


class Chan:
    def __init__(self, sem, name):
        self.sem = sem
        self.name = name
        self.n = 0


class KB:
    def __init__(self, nc):
        self.nc = nc
        self.es = ExitStack()
        self.q = {e: [] for e in ENGS}
        self.sems = {}
        self.cnt = {}
        self.seen = {e: {} for e in ENGS}
        self.lastw = {}
        self.readers = {}
        self.chans = []
        self.chan_by_sem = {}
        self.nins = 0
        self.pending = {e: False for e in ENGS}
        for e in ENGS:
            self.sems[e] = self.es.enter_context(nc.semaphore("s_" + e))
            self.cnt[e] = 0

    def sb(self, name, shape, dt, stack=None):
        return (stack or self.es).enter_context(self.nc.sbuf_tensor(name, list(shape), dt))

    def ps(self, name, shape, dt=F32, stack=None):
        return (stack or self.es).enter_context(self.nc.psum_tensor(name, list(shape), dt))

    def chan(self, name):
        c = Chan(self.es.enter_context(self.nc.semaphore("c_" + name)), name)
        self.chans.append(c)
        self.chan_by_sem[id(c.sem)] = c
        return c

    def _need0(self, eng, sem, val):
        ch = self.chan_by_sem.get(id(sem))
        if ch is not None:
            val = max(val, 16 * ch.n)
        cur = self.seen[eng].get(id(sem), 0)
        if val > cur:
            self.seen[eng][id(sem)] = val
            self.q[eng].append(("wait", sem, val))

    def _deps(self, eng, reads, writes, my_sem):
        for r in reads:
            ev = self.lastw.get(r)
            if ev is not None:
                self._need(eng, ev[0], ev[1])
        for w in writes:
            ev = self.lastw.get(w)
            if ev is not None:
                self._need(eng, ev[0], ev[1])
            rd = self.readers.get(w)
            if rd:
                for sem, val in rd.values():
                    if sem is my_sem:
                        continue
                    self._need(eng, sem, val)

    def _need(self, eng, sem, val):
        if eng == "pe" and sem is self.sems["pe"]:
            return
        self._need0(eng, sem, val)

    def _commit(self, ev, reads, writes):
        for w in writes:
            self.lastw[w] = ev
            self.readers[w] = {}
        for r in reads:
            d = self.readers.setdefault(r, {})
            d[id(ev[0])] = ev

    def op(self, eng, fn, reads=(), writes=(), signal=True):
        sem = self.sems[eng]
        self._deps(eng, reads, writes, sem)
        if signal:
            self.cnt[eng] += 1
            self.pending[eng] = False
            ev = (sem, self.cnt[eng])
            self.q[eng].append(("ins", fn, sem, 1))
        else:
            self.pending[eng] = True
            ev = (sem, self.cnt[eng] + 1)
            self.q[eng].append(("ins0", fn))
        self._commit(ev, reads, writes)
        self.nins += 1
        return ev

    def dma(self, eng, ch, out, in_, reads=(), writes=(), **kw):
        self._deps(eng, reads, writes, None)
        ch.n += 1
        ev = (ch.sem, 16 * ch.n)
        self.q[eng].append(("ins", lambda e, o=out, i=in_, k=kw: e.dma_start(out=o, in_=i, **k), ch.sem, 16))
        self._commit(ev, reads, writes)
        self.nins += 1
        return ev

    def _flush_pending(self):
        for e in ENGS:
            assert not self.pending[e], "non-signaling op left pending at a barrier on " + e

    def _all_events(self):
        self._flush_pending()
        evs = [(self.sems[e], self.cnt[e]) for e in ENGS if self.cnt[e] > 0]
        evs += [(c.sem, 16 * c.n) for c in self.chans if c.n > 0]
        return evs

    def barrier(self):
        evs = self._all_events()
        for e in ENGS:
            for sem, val in evs:
                self._need(e, sem, val)

    def finish(self, final_eng="sp"):
        for sem, val in self._all_events():
            self._need(final_eng, sem, val)

    def emit(self):
        nc = self.nc
        q = self.q
        self.q = {e: [] for e in ENGS}

        def replay(eng_obj, items):
            for it in items:
                if it[0] == "wait":
                    eng_obj.wait_ge(it[1], it[2])
                elif it[0] == "ins0":
                    it[1](eng_obj)
                else:
                    it[1](eng_obj).then_inc(it[2], it[3])

        with nc.Block() as block:
            @block.tensor
            def _(e):
                replay(e, q["pe"])

            @block.scalar
            def _(e):
                replay(e, q["act"])

            @block.vector
            def _(e):
                replay(e, q["dve"])

            @block.gpsimd
            def _(e):
                replay(e, q["pool"])

            @block.sync
            def _(e):
                replay(e, q["sp"])

    def end_phase(self):
        self.barrier()
        self.emit()

    def close(self):
        self.es.close()


def MM(kb, out, lhsT, rhs, start, stop, reads, writes, signal=None):
    if signal is None:
        signal = stop
    return kb.op("pe", lambda e: e.matmul(out, lhsT, rhs, start=start, stop=stop), reads, writes, signal=signal)


def TR(kb, out, in_, ident, reads, writes, signal=True):
    return kb.op("pe", lambda e: e.transpose(out, in_, ident), reads, writes, signal=signal)


def ACT(kb, out, in_, func, reads, writes, bias=None, scale=None, accum_out=None):
    kw = {}
    if bias is not None:
        kw["bias"] = bias
    if scale is not None:
        kw["scale"] = scale
    if accum_out is not None:
        kw["accum_out"] = accum_out
    return kb.op("act", lambda e: e.activation(out=out, in_=in_, func=func, **kw), reads, writes)


def TS(kb, eng, out, in0, s1, s2, op0, op1, reads, writes, accum_out=None):
    kw = {}
    if op1 is not None:
        kw["op1"] = op1
    if accum_out is not None:
        kw["accum_out"] = accum_out
    return kb.op(eng, lambda e: e.tensor_scalar(out=out, in0=in0, scalar1=s1, scalar2=s2, op0=op0, **kw),
                 reads, writes)


def TT(kb, eng, out, in0, in1, op, reads, writes):
    return kb.op(eng, lambda e: e.tensor_tensor(out=out, in0=in0, in1=in1, op=op), reads, writes)


def STT(kb, out, in0, scalar, in1, op0, op1, reads, writes):
    return kb.op("dve", lambda e: e.scalar_tensor_tensor(out=out, in0=in0, scalar=scalar, in1=in1,
                                                         op0=op0, op1=op1), reads, writes)


def CP(kb, eng, out, in_, reads, writes):
    if eng == "act":
        return kb.op("act", lambda e: e.copy(out=out, in_=in_), reads, writes)
    return kb.op(eng, lambda e: e.tensor_copy(out=out, in_=in_), reads, writes)


def MS(kb, eng, ap, val, writes):
    return kb.op(eng, lambda e: e.memset(ap, val), (), writes)


def RECIP(kb, out, in_, reads, writes):
    return kb.op("dve", lambda e: e.reciprocal(out=out, in_=in_), reads, writes)


def phase_norm_T(kb, C, x_dram, normwT, hT, tag):
    with ExitStack() as st:
        xt = [kb.sb(f"{tag}_xt{i}", [128, D], F32, st) for i in range(2)]
        xn = [kb.sb(f"{tag}_xn{i}", [128, D], F32, st) for i in range(2)]
        sq = kb.sb(f"{tag}_sq", [128, D], BF16, st)
        sm = [kb.sb(f"{tag}_sm{i}", [128, 4], F32, st) for i in range(2)]
        pst = [kb.ps(f"{tag}_pt{i}", [128, 8, 128], F32, st) for i in range(2)]
        ch = [kb.chan(f"{tag}_x{i}") for i in range(2)]
        for tt in range(NT):
            s = tt % 2
            kb.dma("sp", ch[s], xt[s][:], x_dram[tt * 128:(tt + 1) * 128, :], writes=[(tag, "xt", s)])
            ACT(kb, sq[:], xt[s][:], AF.Square, [(tag, "xt", s)], [(tag, "sq"), (tag, "ss", s)],
                accum_out=sm[s][:, 0:1])
            ACT(kb, sm[s][:, 1:2], sm[s][:, 0:1], AF.Sqrt, [(tag, "ss", s)], [(tag, "sd", s)],
                bias=C["eps"][:, 0:1], scale=1.0 / D)
            RECIP(kb, sm[s][:, 2:3], sm[s][:, 1:2], [(tag, "sd", s)], [(tag, "rs", s)])
            TS(kb, "dve", xn[s][:], xt[s][:], sm[s][:, 2:3], None, ALU.mult, None,
               [(tag, "xt", s), (tag, "rs", s)], [(tag, "xn", s)])
            for half in range(2):
                p = half
                for kk in range(8):
                    k = half * 8 + kk
                    TR(kb, pst[p][:, kk, :], xn[s][:, k * 128:(k + 1) * 128], C["ident"][:],
                       [(tag, "xn", s)], [(tag, "pt", p)], signal=(kk == 7))
                nw = normwT[:, half * 8:(half + 1) * 8].unsqueeze(2).to_broadcast([128, 8, 128])
                TT(kb, "dve", hT[:, half * 8:(half + 1) * 8, tt * 128:(tt + 1) * 128], pst[p][:], nw,
                   ALU.mult, [(tag, "pt", p)], [("hT", tt)])
        kb.end_phase()


def gemm_fm(kb, tag, actT, act_keys, KC, w_dram, NCH, sink):
    with ExitStack() as st:
        wb = [kb.sb(f"{tag}_wb{i}", [128, KC * 128], BF16, st) for i in range(2)]
        wch = [kb.chan(f"{tag}_w{i}") for i in range(2)]
        pss = [kb.ps(f"{tag}_ps{i}", [128, 1024], F32, st) for i in range(2)]
        it = 0
        for c in range(NCH):
            s = c % 2
            kb.dma("pool", wch[s], wb[s][:], w_dram[c], writes=[(tag, "wb", s)])
            for ts in range(4):
                p = it % 2
                it += 1
                for b in range(2):
                    t0 = ts * 1024 + b * 512
                    for k in range(KC):
                        MM(kb, pss[p][:, b * 512:(b + 1) * 512], wb[s][:, k * 128:(k + 1) * 128],
                           actT[:, k, t0:t0 + 512], k == 0, k == KC - 1,
                           [(tag, "wb", s)] + act_keys, [(tag, "ps", p)])
                sink(c, ts, pss[p], (tag, "ps", p), st)
        kb.end_phase()


class StoreSink:
    def __init__(self, kb, tag, dst, st):
        self.kb = kb
        self.tag = tag
        self.dst = dst
        self.stg = [kb.sb(f"{tag}_stg{i}", [128, 1024], F32, st) for i in range(3)]
        self.ch = [kb.chan(f"{tag}_st{i}") for i in range(3)]
        self.i = 0

    def __call__(self, c, ts, ps, pkey, st):
        kb = self.kb
        s = self.i % 3
        eng = "act" if self.i % 2 == 0 else "dve"
        self.i += 1
        CP(kb, eng, self.stg[s][:], ps[:], [pkey], [(self.tag, "stg", s)])
        kb.dma("sp", self.ch[s], self.dst[c * 128:(c + 1) * 128, ts * 1024:(ts + 1) * 1024], self.stg[s][:],
               reads=[(self.tag, "stg", s)], writes=[(self.tag, "dst", c)])


def phase_inproj(kb, C, tag, x_dram, normwT, w_dram, NCH, pj):
    with ExitStack() as st:
        hT = kb.sb(f"{tag}_hT", [128, 16, T], BF16, st)
        phase_norm_T(kb, C, x_dram, normwT, hT, tag + "n")
        with ExitStack() as st2:
            sink = StoreSink(kb, tag + "s", pj, st2)
            gemm_fm(kb, tag + "g", hT, [], 16, w_dram, NCH, sink)


def phase_outproj(kb, C, tag, yT_dram, wo_dram, xres_dram, out_dram):
    with ExitStack() as st:
        wo = kb.sb(f"{tag}_wo", [128, 16, D], BF16, st)
        wch = kb.chan(f"{tag}_w")
        for k in range(16):
            kb.dma("pool", wch, wo[:, k, :], wo_dram[k], writes=[(tag, "wo")])
        yt = [kb.sb(f"{tag}_yt{i}", [128, 16, 128], BF16, st) for i in range(2)]
        xr = [kb.sb(f"{tag}_xr{i}", [128, D], F32, st) for i in range(2)]
        ot = [kb.sb(f"{tag}_ot{i}", [128, D], F32, st) for i in range(2)]
        ps = [kb.ps(f"{tag}_ps{i}", [128, 512], F32, st) for i in range(4)]
        chy = [kb.chan(f"{tag}_y{i}") for i in range(2)]
        chx = [kb.chan(f"{tag}_x{i}") for i in range(2)]
        cho = [kb.chan(f"{tag}_o{i}") for i in range(2)]
        yv = yT_dram.rearrange("(k p) t -> p k t", p=128)
        for tt in range(NT):
            s = tt % 2
            kb.dma("sp", chy[s], yt[s][:], yv[:, :, tt * 128:(tt + 1) * 128], writes=[(tag, "yt", s)])
            kb.dma("sp", chx[s], xr[s][:], xres_dram[tt * 128:(tt + 1) * 128, :], writes=[(tag, "xr", s)])
            for nb in range(4):
                for k in range(16):
                    MM(kb, ps[nb][:], yt[s][:, k, :], wo[:, k, nb * 512:(nb + 1) * 512], k == 0, k == 15,
                       [(tag, "yt", s), (tag, "wo")], [(tag, "ps", nb)])
                TT(kb, "dve", ot[s][:, nb * 512:(nb + 1) * 512], ps[nb][:], xr[s][:, nb * 512:(nb + 1) * 512],
                   ALU.add, [(tag, "ps", nb), (tag, "xr", s)], [(tag, "ot", s)])
            kb.dma("pool", cho[s], out_dram[tt * 128:(tt + 1) * 128, :], ot[s][:],
                   reads=[(tag, "ot", s)], writes=[(tag, "out", tt)])
        kb.end_phase()


def phase_cd_mix(kb, C, pj, cw_unused, y_dram):
    with ExitStack() as stc:
        cw = {}
        chc = kb.chan("cd_const")
        for name, shape, dt in CD_CONST_SPECS:
            d = kb.nc.dram_tensor(name, list(shape), dt, kind="ExternalInput").ap()
            t = kb.sb("c_" + name, shape, dt, stc)
            kb.dma("sp", chc, t[:], d, writes=[("const", name)])
            cw[name] = t
        kb.end_phase()
        _phase_cd_mix(kb, C, pj, cw, y_dram)


def _phase_cd_mix(kb, C, pj, cw, y_dram):
    HB = 2048
    tag = "cd"
    with ExitStack() as st:
        uc = kb.sb("cd_uc", [128, 8, HB], F32, st)
        ab = [kb.sb(f"cd_a{i}", [128, 30 + HB], F32, st) for i in range(2)]
        gb = [kb.sb(f"cd_g{i}", [128, 30 + HB], F32, st) for i in range(2)]
        sq = [kb.sb(f"cd_sq{i}", [128, HB], F32, st) for i in range(2)]
        mean = kb.sb("cd_mean", [128, HB], F32, st)
        rstd = kb.sb("cd_rstd", [128, HB], F32, st)
        m2 = kb.sb("cd_m2", [128, HB], F32, st)
        zb = [kb.sb(f"cd_z{i}", [128, HB], F32, st) for i in range(2)]
        yb = [kb.sb(f"cd_y{i}", [128, HB], BF16, st) for i in range(2)]
        ps_sum = kb.ps("cd_pss", [128, HB], F32, st)
        ps_ssq = kb.ps("cd_psq", [128, HB], F32, st)
        cha = [kb.chan(f"cd_a{i}") for i in range(2)]
        chg = [kb.chan(f"cd_g{i}") for i in range(2)]
        chz = [kb.chan(f"cd_z{i}") for i in range(2)]
        chy = [kb.chan(f"cd_y{i}") for i in range(2)]
        for th in range(2):
            t0 = th * HB
            for cc in range(8):
                s = cc % 2
                r0 = cc * 128
                if th == 0:
                    MS(kb, "pool", ab[s][:, 0:30], 0.0, [(tag, "a", s)])
                    MS(kb, "pool", gb[s][:, 0:30], 0.0, [(tag, "g", s)])
                    kb.dma("sp", cha[s], ab[s][:, 30:30 + HB], pj[r0:r0 + 128, 0:HB], writes=[(tag, "a", s)])
                    kb.dma("sp", chg[s], gb[s][:, 30:30 + HB], pj[1024 + r0:1024 + r0 + 128, 0:HB],
                           writes=[(tag, "g", s)])
                else:
                    kb.dma("sp", cha[s], ab[s][:], pj[r0:r0 + 128, t0 - 30:t0 + HB], writes=[(tag, "a", s)])
                    kb.dma("sp", chg[s], gb[s][:], pj[1024 + r0:1024 + r0 + 128, t0 - 30:t0 + HB],
                           writes=[(tag, "g", s)])
                ACT(kb, gb[s][:], gb[s][:], AF.Sigmoid, [(tag, "g", s)], [(tag, "g", s)])
                TT(kb, "dve", ab[s][:], ab[s][:], gb[s][:], ALU.mult, [(tag, "a", s), (tag, "g", s)], [(tag, "a", s)])
                TS(kb, "dve", uc[:, cc, :], ab[s][:, 30:30 + HB], cw["dww"][:, cc, 30:31], cw["dwb"][:, cc:cc + 1],
                   ALU.mult, ALU.add, [(tag, "a", s)], [(tag, "uc", cc)])
                for j in range(30):
                    STT(kb, uc[:, cc, :], ab[s][:, j:j + HB], cw["dww"][:, cc, j:j + 1], uc[:, cc, :],
                        ALU.mult, ALU.add, [(tag, "a", s), (tag, "uc", cc)], [(tag, "uc", cc)])
                ACT(kb, sq[s][:], uc[:, cc, :], AF.Square, [(tag, "uc", cc)], [(tag, "sq", s)])
                for tb in range(4):
                    MM(kb, ps_sum[:, tb * 512:(tb + 1) * 512], C["ones_f"][:], uc[:, cc, tb * 512:(tb + 1) * 512],
                       cc == 0, cc == 7, [(tag, "uc", cc)], [(tag, "pss")], signal=(tb == 3))
                    MM(kb, ps_ssq[:, tb * 512:(tb + 1) * 512], C["ones_f"][:], sq[s][:, tb * 512:(tb + 1) * 512],
                       cc == 0, cc == 7, [(tag, "sq", s)], [(tag, "psq")], signal=(tb == 3))
            ACT(kb, mean[:], ps_sum[:], AF.Copy, [(tag, "pss")], [(tag, "mean")], scale=1.0 / 1024)
            TT(kb, "dve", m2[:], mean[:], mean[:], ALU.mult, [(tag, "mean")], [(tag, "m2")])
            STT(kb, m2[:], ps_ssq[:], 1.0 / 1024, m2[:], ALU.mult, ALU.subtract, [(tag, "psq"), (tag, "m2")],
                [(tag, "m2")])
            ACT(kb, m2[:], m2[:], AF.Sqrt, [(tag, "m2")], [(tag, "m2")], bias=C["eps"][:, 0:1])
            RECIP(kb, rstd[:], m2[:], [(tag, "m2")], [(tag, "rstd")])
            for cc in range(8):
                s = cc % 2
                r0 = cc * 128
                kb.dma("sp", chz[s], zb[s][:], pj[2048 + r0:2048 + r0 + 128, t0:t0 + HB], writes=[(tag, "z", s)])
                TT(kb, "dve", uc[:, cc, :], uc[:, cc, :], mean[:], ALU.subtract, [(tag, "uc", cc), (tag, "mean")],
                   [(tag, "uc", cc)])
                TT(kb, "dve", uc[:, cc, :], uc[:, cc, :], rstd[:], ALU.mult, [(tag, "uc", cc), (tag, "rstd")],
                   [(tag, "uc", cc)])
                ACT(kb, uc[:, cc, :], uc[:, cc, :], AF.Silu, [(tag, "uc", cc)], [(tag, "uc", cc)],
                    scale=cw["lnw"][:, cc:cc + 1], bias=cw["lnb"][:, cc:cc + 1])
                ACT(kb, zb[s][:], zb[s][:], AF.Silu, [(tag, "z", s)], [(tag, "z", s)])
                TT(kb, "dve", yb[s][:], uc[:, cc, :], zb[s][:], ALU.mult, [(tag, "uc", cc), (tag, "z", s)],
                   [(tag, "y", s)])
                kb.dma("pool", chy[s], y_dram[r0:r0 + 128, t0:t0 + HB], yb[s][:], reads=[(tag, "y", s)],
                       writes=[(tag, "yd", cc, th)])
        kb.end_phase()
    tag = "sc"
    with ExitStack() as st:
        W = T + 2
        bg = [kb.sb(f"sc_b{i}", [128, T], F32, st) for i in range(2)]
        cg = [kb.sb(f"sc_c{i}", [128, W], F32, st) for i in range(2)]
        ud = [kb.sb(f"sc_u{i}", [128, W], F32, st) for i in range(2)]
        zd = [kb.sb(f"sc_z{i}", [128, T], F32, st) for i in range(2)]
        acc = [kb.sb(f"sc_acc{i}", [128, T], F32, st) for i in range(2)]
        yb = [kb.sb(f"sc_y{i}", [128, T], BF16, st) for i in range(2)]
        chs = {n: [kb.chan(f"sc_{n}{i}") for i in range(2)] for n in ("b", "c", "u", "z", "y")}
        for cc in range(8):
            s = cc % 2
            r0 = cc * 128
            MS(kb, "pool", cg[s][:, 0:2], 0.0, [(tag, "c", s)])
            MS(kb, "pool", ud[s][:, 0:2], 0.0, [(tag, "u", s)])
            kb.dma("sp", chs["b"][s], bg[s][:], pj[3072 + r0:3072 + r0 + 128, :], writes=[(tag, "b", s)])
            kb.dma("sp", chs["c"][s], cg[s][:, 2:W], pj[4096 + r0:4096 + r0 + 128, :], writes=[(tag, "c", s)])
            kb.dma("sp", chs["u"][s], ud[s][:, 2:W], pj[5120 + r0:5120 + r0 + 128, :], writes=[(tag, "u", s)])
            kb.dma("sp", chs["z"][s], zd[s][:], pj[6144 + r0:6144 + r0 + 128, :], writes=[(tag, "z", s)])
            TT(kb, "dve", cg[s][:], cg[s][:], ud[s][:], ALU.mult, [(tag, "c", s), (tag, "u", s)], [(tag, "c", s)])
            TS(kb, "dve", acc[s][:], cg[s][:, 2:W], cw["dcw"][:, cc, 2:3], None, ALU.mult, None,
               [(tag, "c", s)], [(tag, "acc", s)])
            for j in range(2):
                STT(kb, acc[s][:], cg[s][:, j:j + T], cw["dcw"][:, cc, j:j + 1], acc[s][:], ALU.mult, ALU.add,
                    [(tag, "c", s), (tag, "acc", s)], [(tag, "acc", s)])
            ACT(kb, zd[s][:], zd[s][:], AF.Silu, [(tag, "z", s)], [(tag, "z", s)])
            TT(kb, "pool", bg[s][:], bg[s][:], zd[s][:], ALU.mult, [(tag, "b", s), (tag, "z", s)], [(tag, "b", s)])
            TT(kb, "dve", yb[s][:], acc[s][:], bg[s][:], ALU.mult, [(tag, "acc", s), (tag, "b", s)], [(tag, "y", s)])
            kb.dma("pool", chs["y"][s], y_dram[1024 + r0:1024 + r0 + 128, :], yb[s][:], reads=[(tag, "y", s)],
                   writes=[(tag, "yd", cc)])
        kb.end_phase()


def colnorm_phase(kb, C, tag, src, KC, gT, outT):
    with ExitStack() as st:
        raw = [kb.sb(f"{tag}_raw{i}", [128, KC, 1024], F32, st) for i in range(2)]
        sq = [kb.sb(f"{tag}_sq{i}", [128, 1024], F32, st) for i in range(2)]
        rs = kb.sb(f"{tag}_rs", [128, 1024], F32, st)
        ssp = kb.ps(f"{tag}_ssp", [128, 1024], F32, st)
        ch = [kb.chan(f"{tag}_l{i}") for i in range(2)]
        sv = src.rearrange("(k p) t -> p k t", p=128)
        for ts in range(4):
            s = ts % 2
            kb.dma("sp", ch[s], raw[s][:], sv[:, :, ts * 1024:(ts + 1) * 1024], writes=[(tag, "raw", s)])
            for k in range(KC):
                q = k % 2
                ACT(kb, sq[q][:], raw[s][:, k, :], AF.Square, [(tag, "raw", s)], [(tag, "sq", q)])
                for b in range(2):
                    MM(kb, ssp[:, b * 512:(b + 1) * 512], C["ones_f"][:], sq[q][:, b * 512:(b + 1) * 512],
                       k == 0, k == KC - 1, [(tag, "sq", q)], [(tag, "ssp")], signal=True)
            ACT(kb, rs[:], ssp[:], AF.Sqrt, [(tag, "ssp")], [(tag, "rs")], bias=C["eps"][:, 0:1],
                scale=1.0 / (KC * 128))
            RECIP(kb, rs[:], rs[:], [(tag, "rs")], [(tag, "rs")])
            for k in range(KC):
                STT(kb, outT[:, k, ts * 1024:(ts + 1) * 1024], raw[s][:, k, :], gT[:, k:k + 1], rs[:],
                    ALU.mult, ALU.mult, [(tag, "raw", s), (tag, "rs")], [(tag, "out", k, ts)])
        kb.end_phase()


class HeadNormSink:
    def __init__(self, kb, C, tag, dst, gain, n_norm, raw_dst, st):
        self.kb, self.C, self.tag, self.dst, self.gain = kb, C, tag, dst, gain
        self.n_norm, self.raw_dst = n_norm, raw_dst
        self.sq = [kb.sb(f"{tag}_sq{i}", [128, 1024], F32, st) for i in range(2)]
        self.rs = [kb.sb(f"{tag}_rs{i}", [128, 1024], F32, st) for i in range(2)]
        self.ob = [kb.sb(f"{tag}_ob{i}", [128, 1024], BF16, st) for i in range(2)]
        self.ssp = kb.ps(f"{tag}_ssp", [128, 1024], F32, st)
        self.ch = [kb.chan(f"{tag}_o{i}") for i in range(2)]
        self.i = 0

    def __call__(self, c, ts, ps, pkey, st):
        kb, C, tag = self.kb, self.C, self.tag
        s = self.i % 2
        self.i += 1
        if c < self.n_norm:
            ACT(kb, self.sq[s][:], ps[:], AF.Square, [pkey], [(tag, "sq", s)])
            for b in range(2):
                MM(kb, self.ssp[:, b * 512:(b + 1) * 512], C["ones_f"][:], self.sq[s][:, b * 512:(b + 1) * 512],
                   True, True, [(tag, "sq", s)], [(tag, "ssp")])
            ACT(kb, self.rs[s][:], self.ssp[:], AF.Sqrt, [(tag, "ssp")], [(tag, "rs", s)], bias=C["eps"][:, 0:1],
                scale=1.0 / 128)
            RECIP(kb, self.rs[s][:], self.rs[s][:], [(tag, "rs", s)], [(tag, "rs", s)])
            STT(kb, self.ob[s][:], ps[:], self.gain, self.rs[s][:], ALU.mult, ALU.mult,
                [pkey, (tag, "rs", s)], [(tag, "ob", s)])
            d = self.dst[c]
        else:
            CP(kb, "act", self.ob[s][:], ps[:], [pkey], [(tag, "ob", s)])
            d = self.raw_dst[c - self.n_norm]
        kb.dma("sp", self.ch[s], d[:, ts * 1024:(ts + 1) * 1024], self.ob[s][:], reads=[(tag, "ob", s)],
               writes=[(tag, "dst", c)])


def phase_dsa_prep(kb, C, pj, W, S):
    with ExitStack() as st:
        cqn = kb.sb("cqn", [128, 4, T], BF16, st)
        colnorm_phase(kb, C, "cq", pj[0:512, :], 4, C["qnormT"], cqn)
        with ExitStack() as st2:
            sink = HeadNormSink(kb, C, "qs", S["qT"], C["qgain"][:, 0:1], 8, S["qiT"], st2)
            gemm_fm(kb, "qg", cqn, [], 4, W["w_uqiq"], 16, sink)
    with ExitStack() as st:
        ckvn = kb.sb("ckvn", [128, 2, T], BF16, st)
        colnorm_phase(kb, C, "ckv", pj[512:768, :], 2, C["kvnormT"], ckvn)
        with ExitStack() as st2:
            sink = HeadNormSink(kb, C, "ks", S["kT"], C["kgain"][:, 0:1], 8, None, st2)
            gemm_fm(kb, "kg", ckvn, [], 2, W["w_uk"], 8, sink)
        with ExitStack() as st2:
            wv = kb.sb("wv", [128, 2, 1024], BF16, st2)
            chw = kb.chan("wv")
            for k in range(2):
                kb.dma("pool", chw, wv[:, k, :], W["w_uv"][k], writes=[("wv",)])
            vps = [kb.ps(f"v_ps{i}", [128, 1024], F32, st2) for i in range(2)]
            vb = [kb.sb(f"v_b{i}", [128, 1024], BF16, st2) for i in range(2)]
            chv = [kb.chan(f"v_o{i}") for i in range(2)]
            for tt in range(NT):
                s = tt % 2
                for b in range(2):
                    for k in range(2):
                        MM(kb, vps[s][:, b * 512:(b + 1) * 512], ckvn[:, k, tt * 128:(tt + 1) * 128],
                           wv[:, k, b * 512:(b + 1) * 512], k == 0, k == 1, [("wv",)], [("v", "ps", s)])
                CP(kb, "act" if tt % 2 else "dve", vb[s][:], vps[s][:], [("v", "ps", s)], [("v", "b", s)])
                kb.dma("sp", chv[s], S["V"][tt * 128:(tt + 1) * 128, :], vb[s][:], reads=[("v", "b", s)],
                       writes=[("V", tt)])
            kb.end_phase()


def phase_ki_tmaj(kb, C, pj, S):
    with ExitStack() as st:
        ki = kb.sb("ki_raw", [128, T], F32, st)
        sq = kb.sb("ki_sq", [128, T], F32, st)
        mean = kb.sb("ki_mean", [128, 1024], F32, st)
        var = kb.sb("ki_var", [128, 1024], F32, st)
        kio = kb.sb("ki_o", [128, T], BF16, st)
        sm = kb.sb("ki_sm", [32, T], F32, st)
        tmo = kb.sb("ki_tmo", [128, NT, 32], F32, st)
        ps1 = kb.ps("ki_ps1", [128, 1024], F32, st)
        ps2 = kb.ps("ki_ps2", [128, 1024], F32, st)
        pst = kb.ps("ki_pst", [128, 16, 32], F32, st)
        ch = kb.chan("ki")
        kb.dma("sp", ch, ki[0:64, :], pj[768:832, :], writes=[("ki", "raw")])
        kb.dma("sp", ch, ki[64:128, :], pj[768:832, :], writes=[("ki", "raw")])
        kb.dma("sp", ch, sm[:], pj[896:928, :], writes=[("ki", "sm")])
        ACT(kb, sq[:], ki[:], AF.Square, [("ki", "raw")], [("ki", "sq")])
        for ts in range(4):
            for b in range(2):
                c0 = ts * 1024 + b * 512
                MM(kb, ps1[:, b * 512:(b + 1) * 512], C["ones_f"][0:64, :], ki[0:64, c0:c0 + 512], True, True,
                   [("ki", "raw")], [("ki", "ps1")])
                MM(kb, ps2[:, b * 512:(b + 1) * 512], C["ones_f"][0:64, :], sq[0:64, c0:c0 + 512], True, True,
                   [("ki", "sq")], [("ki", "ps2")])
            ACT(kb, mean[:], ps1[:], AF.Copy, [("ki", "ps1")], [("ki", "mean")], scale=1.0 / 64)
            TT(kb, "dve", var[:], mean[:], mean[:], ALU.mult, [("ki", "mean")], [("ki", "var")])
            STT(kb, var[:], ps2[:], 1.0 / 64, var[:], ALU.mult, ALU.subtract, [("ki", "ps2"), ("ki", "var")],
                [("ki", "var")])
            ACT(kb, var[:], var[:], AF.Sqrt, [("ki", "var")], [("ki", "var")], bias=C["eps"][:, 0:1])
            RECIP(kb, var[:], var[:], [("ki", "var")], [("ki", "var")])
            sl = slice(ts * 1024, (ts + 1) * 1024)
            TT(kb, "dve", ki[:, sl], ki[:, sl], mean[:], ALU.subtract, [("ki", "raw"), ("ki", "mean")], [("ki", "raw")])
            TT(kb, "dve", ki[:, sl], ki[:, sl], var[:], ALU.mult, [("ki", "raw"), ("ki", "var")], [("ki", "raw")])
            TS(kb, "dve", kio[:, sl], ki[:, sl], C["ikw"][:, 0:1], C["ikb"][:, 0:1], ALU.mult, ALU.add,
               [("ki", "raw")], [("ki", "o")])
        kb.dma("sp", ch, S["kiT"], kio[:], reads=[("ki", "o")], writes=[("kiT",)])
        for g in range(2):
            for tt in range(16):
                t = g * 16 + tt
                TR(kb, pst[:, tt, :], sm[:, t * 128:(t + 1) * 128], C["ident"][0:32, 0:32], [("ki", "sm")],
                   [("ki", "pst")], signal=(tt == 15))
            CP(kb, "dve", tmo[:, g * 16:(g + 1) * 16, :], pst[:], [("ki", "pst")], [("ki", "tmo")])
        kb.dma("sp", ch, S["tmaj"].rearrange("(n p) c -> p n c", p=128), tmo[:], reads=[("ki", "tmo")],
               writes=[("tmaj",)])
        kb.end_phase()


N_BIS = 16
SCALE_A = 128 ** -0.5


def phase_dsa_attn(kb, C, pj, S, y_dram):
    tag = "at"
    with ExitStack() as st:
        kT = kb.sb("at_kT", [128, 8, T], BF16, st)
        Vt = kb.sb("at_V", [128, NT, 1024], BF16, st)
        kiT = kb.sb("at_kiT", [128, T], BF16, st)
        score1 = kb.sb("at_sc", [128, T], F32, st)
        score = [score1, score1]
        mask = kb.sb("at_mask", [128, T], BF16, st)
        junk = mask
        maskT1 = kb.sb("at_maskT", [128, NT, 128], BF16, st)
        maskT = [maskT1, maskT1]
        rbuf = [kb.sb(f"at_r{i}", [128, 2, 512], BF16, st) for i in range(2)]
        dsg = kb.sb("at_dsg", [128, 16, 128], BF16, st)
        qb = [kb.sb(f"at_q{i}", [128, 8, 128], BF16, st) for i in range(2)]
        qib1 = kb.sb("at_qi", [128, 8, 128], BF16, st)
        qib = [qib1, qib1]
        wt = [kb.sb(f"at_w{i}", [128, 16], F32, st) for i in range(2)]
        bs = [kb.sb(f"at_bs{i}", [128, 8], F32, st) for i in range(2)]
        za1 = kb.sb("at_za", [128, 8, 128], F32, st)
        za = [za1, za1]
        pt = [kb.sb(f"at_pt{i}", [128, 4, 128], BF16, st) for i in range(3)]
        ost1 = kb.sb("at_ost", [128, 8, 256], F32, st)
        ost = [ost1, ost1]
        yst1 = kb.sb("at_yst", [128, 8, 128], BF16, st)
        yst = [yst1, yst1]
        biasS = C["biasS"]
        ident_b = kb.sb("at_identb", [128, 128], BF16, st)
        ones_b = kb.sb("at_onesb", [128, 128], BF16, st)
        ips = [kb.ps(f"at_ips{i}", [128, 2, 512], F32, st) for i in range(2)]
        lg = [ips[i][:, 0, :].rearrange("p (a b) -> p a b", b=128) for i in range(2)]
        po1 = kb.ps("at_po", [128, 128], F32, st)
        prs1 = kb.ps("at_prs", [128, 128], F32, st)
        po, prs = [po1, po1], [prs1, prs1]
        sps1 = kb.ps("at_sps", [128, 512], F32, st)
        sps = [sps1, sps1]
        tps = ips[0][:, 0, :].bitcast(BF16)[:, 0:512].rearrange("p (a b) -> p a b", b=128)
        chl = kb.chan("at_ld")
        chq = [kb.chan(f"at_q{i}") for i in range(2)]
        chy = [kb.chan(f"at_y{i}") for i in range(2)]
        chz = kb.chan("at_z")
        kb.dma("sp", chl, kT[:], S["kT"].rearrange("h p t -> p h t"), writes=[(tag, "kT")])
        kb.dma("sp", chl, Vt[:], S["V"].rearrange("(n p) c -> p n c", p=128), writes=[(tag, "V")])
        kb.dma("sp", chl, kiT[:], S["kiT"], writes=[(tag, "kiT")])
        CP(kb, "dve", ident_b[:], C["ident"][:], [], [(tag, "cst")])
        CP(kb, "dve", ones_b[:], C["ones_f"][:], [], [(tag, "cst")])
        qTv = S["qT"].rearrange("h p t -> p h t")
        qiTv = S["qiT"].rearrange("h p t -> p h t")
        zav = pj[1024:2048, :].rearrange("(h p) t -> p h t", p=128)
        yv = y_dram[0:1024, :].rearrange("(h p) t -> p h t", p=128)
        cnt = {"ips": 0, "r": 0, "lg": 0, "pt": 0, "po": 0, "sps": 0}

        def stage_a(i):
            s = i % 2
            c0, c1 = i * 128, (i + 1) * 128
            kb.dma("sp", chq[s], qb[s][:], qTv[:, :, c0:c1], writes=[(tag, "q", s)])
            kb.dma("sp", chq[s], qib[s][:], qiTv[:, :, c0:c1], writes=[(tag, "qi")])
            kb.dma("sp", chq[s], wt[s][:], S["tmaj"][c0:c1, 0:16], writes=[(tag, "w", s)])
            nW = (i + 4) // 4
            Wi = nW * 512
            sk = (tag, "score")
            TT(kb, "dve", dsg[:], ident_b[:].unsqueeze(1).to_broadcast([128, 16, 128]),
               wt[s][:, 0:16].unsqueeze(2).to_broadcast([128, 16, 128]), ALU.mult, [(tag, "w", s), (tag, "cst")],
               [(tag, "dsg")])
            for w in range(nW):
                sp_ = cnt["sps"] % 2
                cnt["sps"] += 1
                units = []

                def acc(u):
                    h0, r_ = u
                    for e_ in range(2):
                        MM(kb, sps[sp_][:], dsg[:, h0 + e_, :], rbuf[r_][:, e_, :], h0 + e_ == 0, h0 + e_ == 15,
                           [(tag, "dsg"), (tag, "r", r_)], [(tag, "sps", 0)], signal=(e_ == 1))

                for pair in range(8):
                    p = cnt["ips"] % 2
                    cnt["ips"] += 1
                    r = cnt["r"] % 2
                    cnt["r"] += 1
                    for e_ in range(2):
                        base = e_ * 64
                        MM(kb, ips[p][:, e_, :], qib[s][base:base + 64, pair, :],
                           kiT[base:base + 64, w * 512:(w + 1) * 512], True, True, [(tag, "qi"), (tag, "kiT")],
                           [(tag, "ips", p)], signal=(e_ == 1))
                    if pair % 2 == 0:
                        ACT(kb, rbuf[r][:], ips[p][:], AF.Relu, [(tag, "ips", p)], [(tag, "r", r)])
                    else:
                        TS(kb, "dve", rbuf[r][:], ips[p][:], 0.0, None, ALU.max, None, [(tag, "ips", p)], [(tag, "r", r)])
                    if units:
                        acc(units.pop())
                    units.append((2 * pair, r))
                acc(units.pop())
                CP(kb, "dve", score[s][:, w * 512:(w + 1) * 512], sps[sp_][:], [(tag, "sps", 0)], [sk])
            b = bs[s]
            bk = (tag, "bs", s)
            TS(kb, "dve", junk[:, 0:Wi], score[s][:, 0:Wi], 1.0, None, ALU.mult, ALU.max, [sk], [(tag, "mask"), bk],
               accum_out=b[:, 0:1])
            TS(kb, "dve", junk[:, 0:Wi], score[s][:, 0:Wi], -1.0, None, ALU.mult, ALU.max, [sk], [(tag, "mask"), bk],
               accum_out=b[:, 6:7])
            TT(kb, "dve", b[:, 0:1], b[:, 0:1], b[:, 6:7], ALU.max, [bk], [bk])
            TT(kb, "dve", score[s][:, Wi - 512:Wi], score[s][:, Wi - 512:Wi],
               C["cbase"][:, 384 - (i % 4) * 128:896 - (i % 4) * 128], ALU.add, [sk], [sk])
            TS(kb, "dve", b[:, 1:2], b[:, 0:1], -1.001, -1e-20, ALU.mult, ALU.add, [bk], [bk])
            TS(kb, "dve", b[:, 2:3], b[:, 0:1], 2.002, 2e-20, ALU.mult, ALU.add, [bk], [bk])
            for k in range(1, N_BIS + 1):
                f = 2.0 ** -k
                STT(kb, b[:, 3:4], b[:, 2:3], f, b[:, 1:2], ALU.mult, ALU.add, [bk], [bk])
                TS(kb, "dve", junk[:, 0:Wi], score[s][:, 0:Wi], b[:, 3:4], None, ALU.is_ge, ALU.add,
                   [sk, bk], [(tag, "mask"), bk], accum_out=b[:, 4:5])
                TS(kb, "dve", b[:, 5:6], b[:, 4:5], 255.5, f, ALU.is_ge, ALU.mult, [bk], [bk])
                STT(kb, b[:, 1:2], b[:, 5:6], b[:, 2:3], b[:, 1:2], ALU.mult, ALU.add, [bk], [bk])

        def stage_t(i):
            s = i % 2
            n = i + 1
            TS(kb, "dve", mask[:, 0:n * 128], score[s][:, 0:n * 128], bs[s][:, 1:2], None, ALU.is_ge, None,
               [(tag, "score"), (tag, "bs", s)], [(tag, "mask")])
            for j0 in range(0, n, 4):
                nb = min(4, n - j0)
                for jj in range(nb):
                    j = j0 + jj
                    TR(kb, tps[:, jj, :], mask[:, j * 128:(j + 1) * 128], ident_b[:], [(tag, "mask"), (tag, "cst")],
                       [(tag, "ips", 0)], signal=(jj == nb - 1))
                ACT(kb, maskT[s][:, j0:j0 + nb, :], tps[:, 0:nb, :], AF.Copy, [(tag, "ips", 0)], [(tag, "maskT")],
                    scale=30000.0, bias=-30000.0)

        def stage_b(i):
            s = i % 2
            n = i + 1
            groups = [(h, j0, min(4, n - j0)) for h in range(8) for j0 in range(0, n, 4)]
            slots = {}

            def qk(gi):
                h, j0, nb = groups[gi]
                p = cnt["lg"] % 2
                cnt["lg"] += 1
                x = cnt["pt"] % 3
                cnt["pt"] += 1
                slots[gi] = x
                for jj in range(nb):
                    j = j0 + jj
                    near = (i - j) <= 1
                    MM(kb, lg[p][:, jj, :], kT[:, h, j * 128:(j + 1) * 128], qb[s][:, h, :], True, False,
                       [(tag, "kT"), (tag, "q", s)], [(tag, "ips", p)], signal=False)
                    if near:
                        MM(kb, lg[p][:, jj, :], ident_b[:], biasS[:, h, i - j, :], False, False,
                           [(tag, "cst")], [(tag, "ips", p)], signal=False)
                    MM(kb, lg[p][:, jj, :], ident_b[:], maskT[s][:, j, :], False, True,
                       [(tag, "cst"), (tag, "maskT")], [(tag, "ips", p)], signal=(jj == nb - 1))
                ACT(kb, pt[x][:, 0:nb, :], lg[p][:, 0:nb, :], AF.Exp, [(tag, "ips", p)], [(tag, "pt", x)],
                    scale=SCALE_A, bias=C["cb"][:, h:h + 1])

            def pv(gi):
                h, j0, nb = groups[gi]
                x = slots.pop(gi)
                for jj in range(nb):
                    j = j0 + jj
                    MM(kb, po[0][:], Vt[:, j, h * 128:(h + 1) * 128], pt[x][:, jj, :], j == 0, j == n - 1,
                       [(tag, "V"), (tag, "pt", x)], [(tag, "po", 0)])
                    MM(kb, prs[0][:], ones_b[:], pt[x][:, jj, :], j == 0, j == n - 1,
                       [(tag, "cst"), (tag, "pt", x)], [(tag, "prs", 0)], signal=(jj == nb - 1))
                if j0 + nb == n:
                    CP(kb, "act", ost[s][:, h, 0:128], po[0][:], [(tag, "po", 0)], [(tag, "ost", h)])
                    CP(kb, "act", ost[s][:, h, 128:256], prs[0][:], [(tag, "prs", 0)], [(tag, "ost", h)])

            qk(0)
            for gi in range(len(groups)):
                if gi + 1 < len(groups):
                    qk(gi + 1)
                pv(gi)

        def stage_f(i):
            s = i % 2
            keys = [(tag, "ost", h) for h in range(8)]
            kb.dma("sp", chz, za[s][:], zav[:, :, i * 128:(i + 1) * 128], writes=[(tag, "za")])
            ACT(kb, za[s][:], za[s][:], AF.Silu, [(tag, "za")], [(tag, "za")])
            RECIP(kb, ost[s][:, :, 128:256], ost[s][:, :, 128:256], keys, keys)
            TT(kb, "dve", ost[s][:, :, 0:128], ost[s][:, :, 0:128], ost[s][:, :, 128:256], ALU.mult, keys, keys)
            TT(kb, "dve", yst[s][:], ost[s][:, :, 0:128], za[s][:], ALU.mult, keys + [(tag, "za")], [(tag, "yst")])
            kb.dma("pool", chy[s], yv[:, :, i * 128:(i + 1) * 128], yst[s][:], reads=[(tag, "yst")],
                   writes=[(tag, "y", i)])

        stage_a(0)
        stage_t(0)
        for i in range(NT):
            if i + 1 < NT:
                stage_a(i + 1)
            stage_b(i)
            if i + 1 < NT:
                stage_t(i + 1)
            stage_f(i)
        kb.end_phase()


def phase_gdn_prep(kb, C, pj, S):
    tag = "gp"
    with ExitStack() as st:
        W = T + 3
        raw = [kb.sb(f"gp_raw{i}", [128, W], F32, st) for i in range(2)]
        acc = [kb.sb(f"gp_acc{i}", [128, T], F32, st) for i in range(2)]
        sq = kb.sb("gp_sq", [128, T], F32, st)
        rn = kb.sb("gp_rn", [128, T], F32, st)
        ssp = kb.ps("gp_ssp", [128, T], F32, st)
        chl = [kb.chan(f"gp_l{i}") for i in range(2)]
        chs = [kb.chan(f"gp_s{i}") for i in range(2)]
        for cc in range(24):
            s = cc % 2
            r0 = 2048 + cc * 128
            MS(kb, "pool", raw[s][:, 0:3], 0.0, [(tag, "raw", s)])
            kb.dma("sp", chl[s], raw[s][:, 3:W], pj[r0:r0 + 128, :], writes=[(tag, "raw", s)])
            TS(kb, "dve", acc[s][:], raw[s][:, 3:W], C["cvw"][:, cc, 3:4], None, ALU.mult, None, [(tag, "raw", s)],
               [(tag, "acc", s)])
            for j in range(3):
                STT(kb, acc[s][:], raw[s][:, j:j + T], C["cvw"][:, cc, j:j + 1], acc[s][:], ALU.mult, ALU.add,
                    [(tag, "raw", s), (tag, "acc", s)], [(tag, "acc", s)])
            ACT(kb, acc[s][:], acc[s][:], AF.Silu, [(tag, "acc", s)], [(tag, "acc", s)])
            if cc < 16:
                ACT(kb, sq[:], acc[s][:], AF.Square, [(tag, "acc", s)], [(tag, "sq")])
                for b in range(8):
                    MM(kb, ssp[:, b * 512:(b + 1) * 512], C["ones_f"][:], sq[:, b * 512:(b + 1) * 512], True, True,
                       [(tag, "sq")], [(tag, "ssp")], signal=(b == 7))
                ACT(kb, rn[:], ssp[:], AF.Sqrt, [(tag, "ssp")], [(tag, "rn")], bias=C["eps"][:, 0:1])
                RECIP(kb, rn[:], rn[:], [(tag, "rn")], [(tag, "rn")])
                STT(kb, acc[s][:], acc[s][:], (128 ** -0.5) if cc < 8 else 1.0, rn[:], ALU.mult, ALU.mult,
                    [(tag, "acc", s), (tag, "rn")], [(tag, "acc", s)])
            kb.dma("pool", chs[s], S["gqkv"][cc * 128:(cc + 1) * 128, :], acc[s][:], reads=[(tag, "acc", s)],
                   writes=[("gqkv", cc)])
        kb.end_phase()


import os
GDN_TILES = int(os.environ.get("GDN_TILES", "32"))
GDN_STOP = int(os.environ.get("GDN_STOP", "99"))
GDN_SUB = float(os.environ.get("GDN_SUB", "99"))


def phase_gdn(kb, C, pj, S, y_dram):
    with ExitStack() as stc:
        C = dict(C)
        chc = kb.chan("gd_const")
        for name, shape, dt in GDN_CONST_SPECS:
            d = kb.nc.dram_tensor(name, list(shape), dt, kind="ExternalInput").ap()
            t = kb.sb("c_" + name, shape, dt, stc)
            kb.dma("sp", chc, t[:], d, writes=[("const", name)])
            C[name] = t
        kb.end_phase()
        phase_gdn_prep(kb, C, pj, S)
        _phase_gdn_main(kb, C, pj, S, y_dram)


def _phase_gdn_main(kb, C, pj, S, y_dram):
    tag = "gd"
    H = 8
    with ExitStack() as st:
        def fb(name, shape=(128, H, 128), dt=F32):
            return kb.sb("gd_" + name, list(shape), dt, st)

        tm = fb("tm", (128, NT, 32))
        beta = fb("beta", (128, NT, 8))
        g = fb("g", (128, NT, 8))
        t1 = fb("t1", (128, NT, 8))
        t2 = fb("t2", (128, NT, 8))
        nA = fb("nA", (128, 8))
        qT, kT, vT = fb("qT"), fb("kT"), fb("vT")
        gd, egrow, gcr = fb("gdiag"), fb("egrow"), fb("gcr")
        P1, E1, E2 = fb("P1"), fb("E1"), fb("E2")
        X = [fb("X0"), fb("X1")]
        Y = [fb("Y0"), fb("Y1")]
        P, attnT = fb("P"), fb("attnT")
        vb, kbg, kd, kd1 = fb("vb"), fb("kbg"), fb("kd"), fb("kd1")
        smk = fb("smk", (128, 16))
        u, wT, qgT, vnew = fb("u"), fb("wT"), fb("qgT"), fb("vnew")
        Sst, oacc, zb = fb("S"), fb("oacc"), fb("zb")
        osq, orn = fb("osq"), fb("orn")
        yo = fb("yo", (128, H, 128), BF16)
        sm = fb("sm", (128, 64))
        pA = kb.ps("gd_pA", [128, H, 128], F32, st)
        pB = kb.ps("gd_pB", [128, H, 128], F32, st)
        pC = kb.ps("gd_pC", [128, H, 128], F32, st)
        pO = kb.ps("gd_pO", [128, H, 64], F32, st)
        psm = kb.ps("gd_psm", [128, 32], F32, st)
        ch = kb.chan("gd_l")
        chq = kb.chan("gd_q")
        chz = kb.chan("gd_z")
        chy = kb.chan("gd_y")
        K_ = lambda n: (tag, n)

        def bc_h(ap2d):
            return ap2d.unsqueeze(1).to_broadcast([128, H, 128])

        def bc_f(ap2d):
            return ap2d.unsqueeze(2).to_broadcast([128, H, 128])

        kb.dma("sp", ch, tm[:], S["tmaj"].rearrange("(n p) c -> p n c", p=128), writes=[K_("tm")])
        ACT(kb, beta[:], tm[:, :, 16:24], AF.Sigmoid, [K_("tm")], [K_("beta")])
        dtb = C["dtb_bc"][:].unsqueeze(1).to_broadcast([128, NT, 8])
        TT(kb, "dve", g[:], tm[:, :, 24:32], dtb, ALU.add, [K_("tm")], [K_("g")])
        TS(kb, "dve", t1[:], g[:], -1.0, None, ALU.mult, None, [K_("g")], [K_("t1")])
        TT(kb, "dve", t1[:], t1[:], g[:], ALU.max, [K_("t1"), K_("g")], [K_("t1")])
        ACT(kb, t1[:], t1[:], AF.Exp, [K_("t1")], [K_("t1")], scale=-1.0)
        TS(kb, "dve", t1[:], t1[:], 1.0, None, ALU.add, None, [K_("t1")], [K_("t1")])
        ACT(kb, t1[:], t1[:], AF.Ln, [K_("t1")], [K_("t1")])
        TS(kb, "dve", t2[:], g[:], 0.0, None, ALU.max, None, [K_("g")], [K_("t2")])
        TT(kb, "dve", t2[:], t2[:], t1[:], ALU.add, [K_("t1"), K_("t2")], [K_("t2")])
        ACT(kb, nA[:], C["alog_bc"][:], AF.Exp, [], [K_("nA")])
        TS(kb, "dve", nA[:], nA[:], -1.0, None, ALU.mult, None, [K_("nA")], [K_("nA")])
        TT(kb, "dve", g[:], t2[:], nA[:].unsqueeze(1).to_broadcast([128, NT, 8]), ALU.mult, [K_("t2"), K_("nA")],
           [K_("g")])
        MS(kb, "dve", Sst[:], 0.0, [K_("S")])
        MS(kb, "dve", vnew[:], 0.0, [K_("vnew")])
        gq = S["gqkv"]
        qv = gq[0:1024, :].rearrange("(h p) t -> p h t", p=128)
        kv = gq[1024:2048, :].rearrange("(h p) t -> p h t", p=128)
        vv = gq[2048:3072, :].rearrange("(h p) t -> p h t", p=128)
        zv = pj[5120:6144, :].rearrange("(h p) t -> p h t", p=128)
        yv = y_dram[1024:2048, :].rearrange("(h p) t -> p h t", p=128)

        for n in range(GDN_TILES if GDN_STOP > 0 else 0):
            c0, c1 = n * 128, (n + 1) * 128
            kb.dma("sp", chq, qT[:], qv[:, :, c0:c1], writes=[K_("qT")])
            kb.dma("sp", chq, kT[:], kv[:, :, c0:c1], writes=[K_("kT")])
            kb.dma("sp", chq, vT[:], vv[:, :, c0:c1], writes=[K_("vT")])
            kb.dma("sp", chz, zb[:], zv[:, :, c0:c1], writes=[K_("zb")])
            gn = g[:, n, :]
            bn = beta[:, n, :]
            MM(kb, psm[:, 0:8], C["U2"][:], gn, True, True, [K_("g")], [K_("psm")], signal=False)
            MM(kb, psm[:, 8:16], C["Bsame"][:], gn, True, True, [K_("g")], [K_("psm")], signal=False)
            MM(kb, psm[:, 16:24], C["Bsel0"][:], gn, True, True, [K_("g")], [K_("psm")], signal=False)
            MM(kb, psm[:, 24:32], C["Bsel1"][:], gn, True, True, [K_("g")], [K_("psm")])
            CP(kb, "dve", sm[:, 0:32], psm[:], [K_("psm")], [K_("sm")])
            gc = sm[:, 0:8]
            ACT(kb, sm[:, 32:40], sm[:, 0:8], AF.Exp, [K_("sm")], [K_("sm")])
            TT(kb, "dve", sm[:, 40:48], sm[:, 8:16], sm[:, 0:8], ALU.subtract, [K_("sm")], [K_("sm")])
            ACT(kb, sm[:, 40:48], sm[:, 40:48], AF.Exp, [K_("sm")], [K_("sm")])
            ACT(kb, sm[:, 16:32], sm[:, 16:32], AF.Exp, [K_("sm")], [K_("sm")])
            TT(kb, "dve", sm[:, 48:56], sm[:, 32:40], bn, ALU.mult, [K_("sm"), K_("beta")], [K_("sm")])
            TS(kb, "dve", sm[:, 56:64], bn, -1.0, None, ALU.mult, None, [K_("beta")], [K_("sm")])
            if GDN_SUB <= 0:
                continue
            TT(kb, "dve", gd[:], bc_h(C["U2"][:]), bc_f(gn), ALU.mult, [K_("g")], [K_("gdiag")])
            for b in range(2):
                MM(kb, pA[:, 4 * b:4 * b + 4, :], C["ones_f"][:], gd[:, 4 * b:4 * b + 4, :], True, True,
                   [K_("gdiag")], [K_("pA")], signal=(b == 1))
            if GDN_SUB <= 0.3:
                continue
            CP(kb, "dve", gcr[:], pA[:], [K_("pA")], [K_("gcr")])
            ACT(kb, egrow[:], gcr[:], AF.Exp, [K_("gcr")], [K_("egrow")])
            if GDN_SUB <= 0.4:
                continue
            TT(kb, "dve", P1[:], gcr[:], bc_f(gc), ALU.subtract, [K_("gcr"), K_("sm")], [K_("P1")])
            if GDN_SUB <= 0.5:
                continue
            TS(kb, "dve", E1[:], P1[:], 0.0, None, ALU.max, None, [K_("P1")], [K_("E1")])
            ACT(kb, E1[:], E1[:], AF.Exp, [K_("E1")], [K_("E1")], scale=-1.0)
            if GDN_SUB <= 0.6:
                continue
            TS(kb, "dve", E2[:], P1[:], 0.0, None, ALU.min, None, [K_("P1")], [K_("E2")])
            ACT(kb, E2[:], E2[:], AF.Exp, [K_("E2")], [K_("E2")])
            TT(kb, "dve", E1[:], E1[:], bc_h(C["MLs"][:]), ALU.mult, [K_("E1")], [K_("E1")])
            TT(kb, "dve", E1[:], E1[:], bc_f(sm[:, 56:64]), ALU.mult, [K_("E1"), K_("sm")], [K_("E1")])
            TT(kb, "dve", E2[:], E2[:], bc_h(C["MU"][:]), ALU.mult, [K_("E2")], [K_("E2")])
            if GDN_SUB <= 1:
                continue
            for h in range(H):
                MM(kb, pB[:, h, :], kT[:, h, :], kT[:, h, :], True, True, [K_("kT")], [K_("pB")], signal=(h == H - 1))
            TT(kb, "dve", X[0][:], pB[:], E1[:], ALU.mult, [K_("pB"), K_("E1")], [K_("X0")])
            for h in range(H):
                MM(kb, pC[:, h, :], kT[:, h, :], qT[:, h, :], True, True, [K_("kT"), K_("qT")], [K_("pC")], signal=(h == H - 1))
            TT(kb, "dve", attnT[:], pC[:], E2[:], ALU.mult, [K_("pC"), K_("E2")], [K_("attnT")])
            for h in range(H):
                TR(kb, pA[:, h, :], X[0][:, h, :], C["ident"][:], [K_("X0")], [K_("pA")], signal=(h == H - 1))
            CP(kb, "dve", Y[0][:], pA[:], [K_("pA")], [K_("Y0")])
            TT(kb, "dve", P[:], Y[0][:], bc_h(C["ident"][:]), ALU.add, [K_("Y0")], [K_("P")])
            if GDN_SUB <= 2:
                continue
            cur = 0
            for lvl in range(5):
                nxt = 1 - cur
                xk, yk = K_(f"X{cur}"), K_(f"Y{cur}")
                xn, yn = K_(f"X{nxt}"), K_(f"Y{nxt}")
                for h in range(H):
                    MM(kb, pB[:, h, :], Y[cur][:, h, :], X[cur][:, h, :], True, True, [xk, yk], [K_("pB")], signal=(h == H - 1))
                CP(kb, "dve", X[nxt][:], pB[:], [K_("pB")], [xn])
                if lvl < 4:
                    for h in range(H):
                        MM(kb, pC[:, h, :], X[cur][:, h, :], Y[cur][:, h, :], True, True, [xk, yk], [K_("pC")], signal=(h == H - 1))
                    CP(kb, "dve", Y[nxt][:], pC[:], [K_("pC")], [yn])
                for h in range(H):
                    MM(kb, pA[:, h, :], X[nxt][:, h, :], P[:, h, :], True, True, [xn, K_("P")], [K_("pA")], signal=(h == H - 1))
                TT(kb, "dve", P[:], P[:], pA[:], ALU.add, [K_("pA"), K_("P")], [K_("P")])
                cur = nxt
            if GDN_SUB <= 3:
                continue
            for h in range(H):
                TR(kb, pB[:, h, :], kT[:, h, :], C["ident"][:], [K_("kT")], [K_("pB")], signal=(h == H - 1))
            TT(kb, "dve", kbg[:], pB[:], bc_f(sm[:, 48:56]), ALU.mult, [K_("pB"), K_("sm")], [K_("kbg")])
            TS(kb, "dve", smk[:, 0:8], sm[:, 40:48], C["Bsel0"][:, 0:1], None, ALU.mult, None, [K_("sm")], [K_("smk")])
            TS(kb, "dve", smk[:, 8:16], sm[:, 40:48], C["Bsel1"][:, 0:1], None, ALU.mult, None, [K_("sm")], [K_("smk")])
            TT(kb, "dve", kd[:], pB[:], bc_f(smk[:, 0:8]), ALU.mult, [K_("pB"), K_("smk")], [K_("kd")])
            TT(kb, "dve", kd1[:], pB[:], bc_f(smk[:, 8:16]), ALU.mult, [K_("pB"), K_("smk")], [K_("kd")])
            for h in range(H):
                TR(kb, pC[:, h, :], vT[:, h, :], C["ident"][:], [K_("vT")], [K_("pC")], signal=(h == H - 1))
            TT(kb, "dve", vb[:], pC[:], bc_f(bn), ALU.mult, [K_("pC"), K_("beta")], [K_("vb")])
            for h in range(H):
                MM(kb, pA[:, h, :], P[:, h, :], vb[:, h, :], True, True, [K_("P"), K_("vb")], [K_("pA")], signal=(h == H - 1))
            CP(kb, "dve", u[:], pA[:], [K_("pA")], [K_("u")])
            for h in range(H):
                MM(kb, pB[:, h, :], kbg[:, h, :], P[:, h, :], True, True, [K_("P"), K_("kbg")], [K_("pB")], signal=(h == H - 1))
            CP(kb, "dve", wT[:], pB[:], [K_("pB")], [K_("wT")])
            TT(kb, "dve", qgT[:], qT[:], egrow[:], ALU.mult, [K_("qT"), K_("egrow")], [K_("qgT")])
            if GDN_STOP <= 1:
                continue
            for c in range(2):
                r0, r1 = c * 64, (c + 1) * 64
                for h in range(H):
                    MM(kb, pA[r0:r1, h, :], wT[:, h, r0:r1], Sst[:, h, :], True, True, [K_("wT"), K_("S")], [K_("pA")],
                       signal=(h == H - 1))
                if GDN_SUB == 10 or (GDN_SUB == 10.5 and c == 1):
                    continue
                TT(kb, "dve", vnew[r0:r1, :, :], u[r0:r1, :, :], pA[r0:r1, :, :], ALU.subtract, [K_("u"), K_("pA")],
                   [K_("vnew")])
                if GDN_SUB == 11:
                    continue
                for h in range(H):
                    MM(kb, pO[:, h, :], Sst[:, h, :], qgT[:, h, r0:r1], True, False, [K_("S"), K_("qgT")], [K_("pO")])
                    MM(kb, pO[:, h, :], vnew[r0:r1, h, :], attnT[r0:r1, h, r0:r1], False, True,
                       [K_("vnew"), K_("attnT")], [K_("pO")], signal=(h == H - 1))
                if GDN_SUB == 12:
                    continue
                CP(kb, "dve", oacc[:, :, r0:r1], pO[:], [K_("pO")], [K_("oacc")])
                if GDN_STOP <= 2:
                    continue
                for h in range(H):
                    MM(kb, pB[:, h, :], (kd, kd1)[c][:, h, :], vnew[:, h, :], True, True, [K_("kd"), K_("vnew")],
                       [K_("pB")], signal=(h == H - 1))
                TT(kb, "dve", Sst[:], Sst[:], bc_f(sm[:, 16 + 8 * c:24 + 8 * c]), ALU.mult, [K_("S"), K_("sm")],
                   [K_("S")])
                TT(kb, "dve", Sst[:], Sst[:], pB[:], ALU.add, [K_("S"), K_("pB")], [K_("S")])
            ACT(kb, osq[:], oacc[:], AF.Square, [K_("oacc")], [K_("osq")])
            for b in range(2):
                MM(kb, pC[:, 4 * b:4 * b + 4, :], C["ones_f"][:], osq[:, 4 * b:4 * b + 4, :], True, True, [K_("osq")],
                   [K_("pC")], signal=(b == 1))
            CP(kb, "dve", orn[:], pC[:], [K_("pC")], [K_("orn")])
            ACT(kb, orn[:], orn[:], AF.Sqrt, [K_("orn")], [K_("orn")], bias=C["eps"][:, 0:1], scale=1.0 / 128)
            RECIP(kb, orn[:], orn[:], [K_("orn")], [K_("orn")])
            STT(kb, oacc[:], oacc[:], C["onorm"][:, 0:1], orn[:], ALU.mult, ALU.mult, [K_("oacc"), K_("orn")],
                [K_("oacc")])
            ACT(kb, zb[:], zb[:], AF.Silu, [K_("zb")], [K_("zb")])
            TT(kb, "dve", yo[:], oacc[:], zb[:], ALU.mult, [K_("oacc"), K_("zb")], [K_("yo")])
            kb.dma("pool", chy, yv[:, :, c0:c1], yo[:], reads=[K_("yo")], writes=[("y0b", n)])
        kb.end_phase()

def load_consts(kb, nc, names_shapes):
    C = {}
    ch = kb.chan("const")
    for name, shape, dt in names_shapes:
        d = nc.dram_tensor(name, list(shape), dt, kind="ExternalInput").ap()
        if name == "cbase":
            t = kb.sb("c_" + name, shape, BF16)
            kb.dma("pool", ch, t[:], d, writes=[("const", name)])
        else:
            t = kb.sb("c_" + name, shape, dt)
            kb.dma("sp", ch, t[:], d, writes=[("const", name)])
        C[name] = t
    kb.end_phase()
    with ExitStack() as st:
        d = nc.dram_tensor("biasT", [128, 8, 2, 128], F32, kind="ExternalInput").ap()
        C["biasS"] = kb.sb("c_biasS", [128, 8, 2, 128], BF16)
        bt = kb.sb("biasT_tmp", [128, 8, 2, 128], F32, st)
        kb.dma("sp", ch, bt[:], d, writes=[("const", "biasT")])
        for h in range(8):
            TS(kb, "dve", C["biasS"][:, h, :, :], bt[:, h, :, :], C["cb"][:, h:h + 1], 128 ** 0.5,
               ALU.subtract, ALU.mult, [("const", "biasT")], [("const", "biasS")])
        kb.end_phase()
    return C


CONST_SPECS = [
    ("ident", (128, 128), F32),
    ("ones_f", (128, 128), F32),
    ("eps", (128, 1), F32),
    ("normwT", (128, 2, 16), F32),
    ("cbase", (128, 896), F32),
    ("cb", (128, 8), F32),
    ("qnormT", (128, 4), F32),
    ("kvnormT", (128, 2), F32),
    ("qgain", (128, 1), F32),
    ("kgain", (128, 1), F32),
    ("ikw", (128, 1), F32),
    ("ikb", (128, 1), F32),
]

CD_CONST_SPECS = [
    ("dww", (128, 8, 31), F32),
    ("dwb", (128, 8), F32),
    ("lnw", (128, 8), F32),
    ("lnb", (128, 8), F32),
    ("dcw", (128, 8, 3), F32),
]

GDN_CONST_SPECS = [
    ("cvw", (128, 24, 4), F32),
    ("alog_bc", (128, 8), F32),
    ("dtb_bc", (128, 8), F32),
    ("onorm", (128, 1), F32),
    ("U2", (128, 128), F32),
    ("Bsame", (128, 128), F32),
    ("Bsel0", (128, 128), F32),
    ("Bsel1", (128, 128), F32),
    ("MLs", (128, 128), F32),
    ("MU", (128, 128), F32),
]


def build_program(layers=(0, 1), l0_parts=("a", "b"), debug_out=False):
    nc = bass.Bass("TRN2", target_bir_lowering=False)

    def din(name, shape, dt=F32):
        return nc.dram_tensor(name, list(shape), dt, kind="ExternalInput").ap()

    def dscr(name, shape, dt=F32):
        return nc.dram_tensor(name, list(shape), dt, kind="Internal").ap()

    x = din("x", [T, D])
    out = nc.dram_tensor("out", [T, D], F32, kind="ExternalOutput").ap()
    kb = KB(nc)
    C = load_consts(kb, nc, CONST_SPECS)
    src = x
    if 0 in layers:
        ab_w_in = din("ab_w_in", [48, 128, 16 * 128])
        ab_w_out = din("ab_w_out", [16, 128, D])
        W = {"w_uqiq": din("w_uqiq", [16, 128, 4 * 128]), "w_uk": din("w_uk", [8, 128, 2 * 128]),
             "w_uv": din("w_uv", [2, 128, 1024])}
        pj0 = dscr("pj0", [6144, T])
        S = {"qT": dscr("s_qT", [8, 128, T], BF16), "qiT": dscr("s_qiT", [8, 128, T], BF16),
             "kT": dscr("s_kT", [8, 128, T], BF16), "V": dscr("s_V", [T, 1024], BF16),
             "kiT": dscr("s_kiT", [128, T], BF16), "tmaj": dscr("s_tmaj", [T, 32]),
             "gqkv": dscr("s_gqkv", [3072, T])}
        if debug_out:
            y0 = nc.dram_tensor("y0", [2048, T], BF16, kind="ExternalOutput").ap()
        else:
            y0 = dscr("y0", [2048, T], BF16)
        x1 = dscr("x1", [T, D]) if 1 in layers else out
        phase_inproj(kb, C, "l0", src, C["normwT"][:, 0, :], ab_w_in, 48, pj0)
        phase_ki_tmaj(kb, C, pj0, S)
        if "a" in l0_parts:
            phase_dsa_prep(kb, C, pj0, W, S)
            phase_dsa_attn(kb, C, pj0, S, y0)
        if "b" in l0_parts:
            phase_gdn(kb, C, pj0, S, y0)
        if not debug_out:
            phase_outproj(kb, C, "l0o", y0, ab_w_out, src, x1)
        src = x1
    if 1 in layers:
        cd_w_in = din("cd_w_in", [56, 128, 16 * 128])
        cd_w_out = din("cd_w_out", [16, 128, D])
        pj1 = dscr("pj1", [7168, T])
        y1 = dscr("y1", [2048, T], BF16)
        phase_inproj(kb, C, "l1", src, C["normwT"][:, 1, :], cd_w_in, 56, pj1)
        phase_cd_mix(kb, C, pj1, C, y1)
        phase_outproj(kb, C, "l1o", y1, cd_w_out, src, out)
    kb.finish()
    kb.emit()
    kb.close()
    return nc, kb


def tile_w_in(w, nch):
    K, N = w.shape
    assert N == nch * 128
    return np.ascontiguousarray(w.reshape(K // 128, 128, nch, 128).transpose(2, 1, 0, 3)).reshape(nch, 128, -1)


def t5_bucket_np(dist):
    import math
    max_exact = 16
    dd = np.maximum(dist, 1).astype(np.float32)
    large = max_exact + (np.log(dd / max_exact) / math.log(128 / max_exact) * (32 - max_exact)).astype(np.int32)
    large = np.minimum(large, 31)
    return np.where(dist < max_exact, dist, large)


def colT(v, k):
    return np.ascontiguousarray(np.asarray(v, np.float32).reshape(k, 128).T)


def host_consts(inp):
    f = np.float32
    c = {}
    c["ident"] = np.eye(128, dtype=f)
    c["ones_f"] = np.ones((128, 128), f)
    c["eps"] = np.full((128, 1), EPS, f)
    c["normwT"] = np.ascontiguousarray(inp["norm_w"].reshape(2, 16, 128).transpose(2, 0, 1)).astype(f)
    c["dww"] = np.ascontiguousarray(inp["c_dw_w"][0].reshape(31, 8, 128).transpose(2, 1, 0)).astype(f)
    c["dwb"] = colT(inp["c_dw_b"][0], 8)
    c["lnw"] = colT(inp["c_ln_w"][0], 8)
    c["lnb"] = colT(inp["c_ln_b"][0], 8)
    c["dcw"] = np.ascontiguousarray(inp["d_conv_w"][0].reshape(3, 8, 128).transpose(2, 1, 0)).astype(f)
    r = np.arange(128)[:, None]
    cc = np.arange(896)[None, :]
    c["cbase"] = np.where(cc <= r + 384, 0.0, -1e30).astype(f)
    kl = np.arange(128)[:, None, None]
    dd = np.arange(2)[None, :, None]
    ql = np.arange(128)[None, None, :]
    dist = np.maximum(dd * 128 + ql - kl, 0)
    bt = np.asarray(inp["rel_bias"], f)[t5_bucket_np(dist)]
    c["biasT"] = np.ascontiguousarray(bt.transpose(0, 3, 1, 2))
    c["cb"] = np.ascontiguousarray(np.broadcast_to(np.asarray(inp["rel_bias"], f)[31][None, :], (128, 8)))
    c["qnormT"] = colT(inp["a_q_norm"][0], 4)
    c["kvnormT"] = colT(inp["a_kv_norm"][0], 2)
    c["qgain"] = np.asarray(inp["a_q_gain"][0], f).reshape(128, 1).copy()
    c["kgain"] = np.asarray(inp["a_k_gain"][0], f).reshape(128, 1).copy()
    c["ikw"] = np.tile(np.asarray(inp["a_ik_norm_w"][0], f), 2).reshape(128, 1).copy()
    c["ikb"] = np.tile(np.asarray(inp["a_ik_norm_b"][0], f), 2).reshape(128, 1).copy()
    c["cvw"] = np.ascontiguousarray(np.asarray(inp["b_conv_w"][0], f).reshape(4, 24, 128).transpose(2, 1, 0))
    c["alog_bc"] = np.ascontiguousarray(np.broadcast_to(np.asarray(inp["b_a_log"][0], f)[None, :], (128, 8)))
    c["dtb_bc"] = np.ascontiguousarray(np.broadcast_to(np.asarray(inp["b_dt_bias"][0], f)[None, :], (128, 8)))
    c["onorm"] = np.asarray(inp["b_o_norm"][0], f).reshape(128, 1).copy()
    a = np.arange(128)
    same = (a[:, None] // 64) == (a[None, :] // 64)
    c["U2"] = (same & (a[:, None] <= a[None, :])).astype(f)
    c["Bsame"] = same.astype(f)
    c["Bsel0"] = np.ascontiguousarray(np.broadcast_to((a[:, None] < 64), (128, 128))).astype(f)
    c["Bsel1"] = np.ascontiguousarray(np.broadcast_to((a[:, None] >= 64), (128, 128))).astype(f)
    c["MLs"] = (same & (a[:, None] > a[None, :])).astype(f)
    c["MU"] = (same & (a[:, None] <= a[None, :])).astype(f)
    return c


def host_shared(inp, layers=(0, 1)):
    f = np.float32
    sh = host_consts(inp)
    if 0 in layers:
        w = np.asarray(inp["ab_w_in"][0], f)
        wp = np.zeros((D, 6144), f)
        wp[:, 0:832] = w[:, 0:832]
        wp[:, 896:912] = w[:, 832:848]
        wp[:, 912:928] = w[:, 4944:4960]
        wp[:, 1024:2048] = w[:, 848:1872]
        wp[:, 2048:5120] = w[:, 1872:4944]
        wp[:, 5120:6144] = w[:, 4960:5984]
        sh["ab_w_in"] = tile_w_in(wp, 48)
        sh["ab_w_out"] = np.ascontiguousarray(np.asarray(inp["ab_w_out"][0], f).reshape(16, 128, D))
        sh["w_uqiq"] = tile_w_in(np.concatenate([inp["a_w_uq"][0], inp["a_w_iq"][0]], axis=1).astype(f), 16)
        sh["w_uk"] = tile_w_in(np.asarray(inp["a_w_uk"][0], f), 8)
        sh["w_uv"] = np.ascontiguousarray(np.asarray(inp["a_w_uv"][0], f).reshape(2, 128, 1024))
    if 1 in layers:
        sh["cd_w_in"] = tile_w_in(np.asarray(inp["cd_w_in"][0], f), 56)
        sh["cd_w_out"] = np.ascontiguousarray(np.asarray(inp["cd_w_out"][0], f).reshape(16, 128, D))
    return sh


def kernel(**inputs):
    inp = {k: np.asarray(v) for k, v in inputs.items()}
    nc, kb = build_program()
    sh = host_shared(inp)
    x = np.ascontiguousarray(inp["x"], dtype=np.float32)
    in_maps = [dict(sh, x=x[b]) for b in range(8)]
    res = run_bass_kernel_spmd(nc, in_maps, core_ids=list(range(8)))
    return np.stack([np.asarray(r["out"], np.float32) for r in res.results], axis=0)
```

```python
from contextlib import ExitStack
import numpy as np
import concourse.bass as bass
import concourse.mybir as mybir
from concourse.bass_utils import run_bass_kernel_spmd

F32 = mybir.dt.float32
BF16 = mybir.dt.bfloat16
ALU = mybir.AluOpType
AF = mybir.ActivationFunctionType
AX = mybir.AxisListType

T = 4096
D = 2048
NT = T // 128
EPS = 1e-6
ENGS = ("pe", "act", "dve", "pool", "sp")


class Chan:
    def __init__(self, sem, name):
        self.sem = sem
        self.name = name
        self.n = 0


class KB:
    def __init__(self, nc):
        self.nc = nc
        self.es = ExitStack()
        self.q = {e: [] for e in ENGS}
        self.sems = {}
        self.cnt = {}
        self.seen = {e: {} for e in ENGS}
        self.lastw = {}
        self.readers = {}
        self.chans = []
        self.chan_by_sem = {}
        self.nins = 0
        self.pending = {e: False for e in ENGS}
        for e in ENGS:
            self.sems[e] = self.es.enter_context(nc.semaphore("s_" + e))
            self.cnt[e] = 0

    def sb(self, name, shape, dt, stack=None):
        return (stack or self.es).enter_context(self.nc.sbuf_tensor(name, list(shape), dt))

    def ps(self, name, shape, dt=F32, stack=None):
        return (stack or self.es).enter_context(self.nc.psum_tensor(name, list(shape), dt))

    def chan(self, name):
        c = Chan(self.es.enter_context(self.nc.semaphore("c_" + name)), name)
        self.chans.append(c)
        self.chan_by_sem[id(c.sem)] = c
        return c

    def _need0(self, eng, sem, val):
        ch = self.chan_by_sem.get(id(sem))
        if ch is not None:
            val = max(val, 16 * ch.n)
        cur = self.seen[eng].get(id(sem), 0)
        if val > cur:
            self.seen[eng][id(sem)] = val
            self.q[eng].append(("wait", sem, val))

    def _deps(self, eng, reads, writes, my_sem):
        for r in reads:
            ev = self.lastw.get(r)
            if ev is not None:
                self._need(eng, ev[0], ev[1])
        for w in writes:
            ev = self.lastw.get(w)
            if ev is not None:
                self._need(eng, ev[0], ev[1])
            rd = self.readers.get(w)
            if rd:
                for sem, val in rd.values():
                    if sem is my_sem:
                        continue
                    self._need(eng, sem, val)

    def _need(self, eng, sem, val):
        if eng == "pe" and sem is self.sems["pe"]:
            return
        self._need0(eng, sem, val)

    def _commit(self, ev, reads, writes):
        for w in writes:
            self.lastw[w] = ev
            self.readers[w] = {}
        for r in reads:
            d = self.readers.setdefault(r, {})
            d[id(ev[0])] = ev

    def op(self, eng, fn, reads=(), writes=(), signal=True):
        sem = self.sems[eng]
        self._deps(eng, reads, writes, sem)
        if signal:
            self.cnt[eng] += 1
            self.pending[eng] = False
            ev = (sem, self.cnt[eng])
            self.q[eng].append(("ins", fn, sem, 1))
        else:
            self.pending[eng] = True
            ev = (sem, self.cnt[eng] + 1)
            self.q[eng].append(("ins0", fn))
        self._commit(ev, reads, writes)
        self.nins += 1
        return ev

    def dma(self, eng, ch, out, in_, reads=(), writes=(), **kw):
        self._deps(eng, reads, writes, None)
        ch.n += 1
        ev = (ch.sem, 16 * ch.n)
        self.q[eng].append(("ins", lambda e, o=out, i=in_, k=kw: e.dma_start(out=o, in_=i, **k), ch.sem, 16))
        self._commit(ev, reads, writes)
        self.nins += 1
        return ev

    def _flush_pending(self):
        for e in ENGS:
            assert not self.pending[e], "non-signaling op left pending at a barrier on " + e

    def _all_events(self):
        self._flush_pending()
        evs = [(self.sems[e], self.cnt[e]) for e in ENGS if self.cnt[e] > 0]
        evs += [(c.sem, 16 * c.n) for c in self.chans if c.n > 0]
        return evs

    def barrier(self):
        evs = self._all_events()
        for e in ENGS:
            for sem, val in evs:
                self._need(e, sem, val)

    def finish(self, final_eng="sp"):
        for sem, val in self._all_events():
            self._need(final_eng, sem, val)

    def emit(self):
        nc = self.nc
        q = self.q
        self.q = {e: [] for e in ENGS}

        def replay(eng_obj, items):
            for it in items:
                if it[0] == "wait":
                    eng_obj.wait_ge(it[1], it[2])
                elif it[0] == "ins0":
                    it[1](eng_obj)
                else:
                    it[1](eng_obj).then_inc(it[2], it[3])

        with nc.Block() as block:
            @block.tensor
            def _(e):
                replay(e, q["pe"])

            @block.scalar
            def _(e):
                replay(e, q["act"])

            @block.vector
            def _(e):
                replay(e, q["dve"])

            @block.gpsimd
            def _(e):
                replay(e, q["pool"])

            @block.sync
            def _(e):
                replay(e, q["sp"])

    def end_phase(self):
        self.barrier()
        self.emit()

    def close(self):
        self.es.close()


def MM(kb, out, lhsT, rhs, start, stop, reads, writes, signal=None):
    if signal is None:
        signal = stop
    return kb.op("pe", lambda e: e.matmul(out, lhsT, rhs, start=start, stop=stop), reads, writes, signal=signal)


def TR(kb, out, in_, ident, reads, writes, signal=True):
    return kb.op("pe", lambda e: e.transpose(out, in_, ident), reads, writes, signal=signal)


def ACT(kb, out, in_, func, reads, writes, bias=None, scale=None, accum_out=None):
    kw = {}
    if bias is not None:
        kw["bias"] = bias
    if scale is not None:
        kw["scale"] = scale
    if accum_out is not None:
        kw["accum_out"] = accum_out
    return kb.op("act", lambda e: e.activation(out=out, in_=in_, func=func, **kw), reads, writes)


def TS(kb, eng, out, in0, s1, s2, op0, op1, reads, writes, accum_out=None):
    kw = {}
    if op1 is not None:
        kw["op1"] = op1
    if accum_out is not None:
        kw["accum_out"] = accum_out
    return kb.op(eng, lambda e: e.tensor_scalar(out=out, in0=in0, scalar1=s1, scalar2=s2, op0=op0, **kw),
                 reads, writes)


def TT(kb, eng, out, in0, in1, op, reads, writes):
    return kb.op(eng, lambda e: e.tensor_tensor(out=out, in0=in0, in1=in1, op=op), reads, writes)


def STT(kb, out, in0, scalar, in1, op0, op1, reads, writes):
    return kb.op("dve", lambda e: e.scalar_tensor_tensor(out=out, in0=in0, scalar=scalar, in1=in1,
                                                         op0=op0, op1=op1), reads, writes)


def CP(kb, eng, out, in_, reads, writes):
    if eng == "act":
        return kb.op("act", lambda e: e.copy(out=out, in_=in_), reads, writes)
    return kb.op(eng, lambda e: e.tensor_copy(out=out, in_=in_), reads, writes)


def MS(kb, eng, ap, val, writes):
    return kb.op(eng, lambda e: e.memset(ap, val), (), writes)


def RECIP(kb, out, in_, reads, writes):
    return kb.op("dve", lambda e: e.reciprocal(out=out, in_=in_), reads, writes)


def phase_norm_T(kb, C, x_dram, normwT, hT, tag):
    with ExitStack() as st:
        xt = [kb.sb(f"{tag}_xt{i}", [128, D], F32, st) for i in range(2)]
        xn = [kb.sb(f"{tag}_xn{i}", [128, D], F32, st) for i in range(2)]
        sq = kb.sb(f"{tag}_sq", [128, D], BF16, st)
        sm = [kb.sb(f"{tag}_sm{i}", [128, 4], F32, st) for i in range(2)]
        pst = [kb.ps(f"{tag}_pt{i}", [128, 8, 128], F32, st) for i in range(2)]
        ch = [kb.chan(f"{tag}_x{i}") for i in range(2)]
        for tt in range(NT):
            s = tt % 2
            kb.dma("sp", ch[s], xt[s][:], x_dram[tt * 128:(tt + 1) * 128, :], writes=[(tag, "xt", s)])
            ACT(kb, sq[:], xt[s][:], AF.Square, [(tag, "xt", s)], [(tag, "sq"), (tag, "ss", s)],
                accum_out=sm[s][:, 0:1])
            ACT(kb, sm[s][:, 1:2], sm[s][:, 0:1], AF.Sqrt, [(tag, "ss", s)], [(tag, "sd", s)],
                bias=C["eps"][:, 0:1], scale=1.0 / D)
            RECIP(kb, sm[s][:, 2:3], sm[s][:, 1:2], [(tag, "sd", s)], [(tag, "rs", s)])
            TS(kb, "dve", xn[s][:], xt[s][:], sm[s][:, 2:3], None, ALU.mult, None,
               [(tag, "xt", s), (tag, "rs", s)], [(tag, "xn", s)])
            for half in range(2):
                p = half
                for kk in range(8):
                    k = half * 8 + kk
                    TR(kb, pst[p][:, kk, :], xn[s][:, k * 128:(k + 1) * 128], C["ident"][:],
                       [(tag, "xn", s)], [(tag, "pt", p)], signal=(kk == 7))
                nw = normwT[:, half * 8:(half + 1) * 8].unsqueeze(2).to_broadcast([128, 8, 128])
                TT(kb, "dve", hT[:, half * 8:(half + 1) * 8, tt * 128:(tt + 1) * 128], pst[p][:], nw,
                   ALU.mult, [(tag, "pt", p)], [("hT", tt)])
        kb.end_phase()


def gemm_fm(kb, tag, actT, act_keys, KC, w_dram, NCH, sink):
    with ExitStack() as st:
        wb = [kb.sb(f"{tag}_wb{i}", [128, KC * 128], BF16, st) for i in range(2)]
        wch = [kb.chan(f"{tag}_w{i}") for i in range(2)]
        pss = [kb.ps(f"{tag}_ps{i}", [128, 1024], F32, st) for i in range(2)]
        it = 0
        for c in range(NCH):
            s = c % 2
            kb.dma("pool", wch[s], wb[s][:], w_dram[c], writes=[(tag, "wb", s)])
            for ts in range(4):
                p = it % 2
                it += 1
                for b in range(2):
                    t0 = ts * 1024 + b * 512
                    for k in range(KC):
                        MM(kb, pss[p][:, b * 512:(b + 1) * 512], wb[s][:, k * 128:(k + 1) * 128],
                           actT[:, k, t0:t0 + 512], k == 0, k == KC - 1,
                           [(tag, "wb", s)] + act_keys, [(tag, "ps", p)])
                sink(c, ts, pss[p], (tag, "ps", p), st)
        kb.end_phase()


class StoreSink:
    def __init__(self, kb, tag, dst, st):
        self.kb = kb
        self.tag = tag
        self.dst = dst
        self.stg = [kb.sb(f"{tag}_stg{i}", [128, 1024], F32, st) for i in range(3)]
        self.ch = [kb.chan(f"{tag}_st{i}") for i in range(3)]
        self.i = 0

    def __call__(self, c, ts, ps, pkey, st):
        kb = self.kb
        s = self.i % 3
        eng = "act" if self.i % 2 == 0 else "dve"
        self.i += 1
        CP(kb, eng, self.stg[s][:], ps[:], [pkey], [(self.tag, "stg", s)])
        kb.dma("sp", self.ch[s], self.dst[c * 128:(c + 1) * 128, ts * 1024:(ts + 1) * 1024], self.stg[s][:],
               reads=[(self.tag, "stg", s)], writes=[(self.tag, "dst", c)])


def phase_inproj(kb, C, tag, x_dram, normwT, w_dram, NCH, pj):
    with ExitStack() as st:
        hT = kb.sb(f"{tag}_hT", [128, 16, T], BF16, st)
        phase_norm_T(kb, C, x_dram, normwT, hT, tag + "n")
        with ExitStack() as st2:
            sink = StoreSink(kb, tag + "s", pj, st2)
            gemm_fm(kb, tag + "g", hT, [], 16, w_dram, NCH, sink)


def phase_outproj(kb, C, tag, yT_dram, wo_dram, xres_dram, out_dram):
    with ExitStack() as st:
        wo = kb.sb(f"{tag}_wo", [128, 16, D], BF16, st)
        wch = kb.chan(f"{tag}_w")
        for k in range(16):
            kb.dma("pool", wch, wo[:, k, :], wo_dram[k], writes=[(tag, "wo")])
        yt = [kb.sb(f"{tag}_yt{i}", [128, 16, 128], BF16, st) for i in range(2)]
        xr = [kb.sb(f"{tag}_xr{i}", [128, D], F32, st) for i in range(2)]
        ot = [kb.sb(f"{tag}_ot{i}", [128, D], F32, st) for i in range(2)]
        ps = [kb.ps(f"{tag}_ps{i}", [128, 512], F32, st) for i in range(4)]
        chy = [kb.chan(f"{tag}_y{i}") for i in range(2)]
        chx = [kb.chan(f"{tag}_x{i}") for i in range(2)]
        cho = [kb.chan(f"{tag}_o{i}") for i in range(2)]
        yv = yT_dram.rearrange("(k p) t -> p k t", p=128)
        for tt in range(NT):
            s = tt % 2
            kb.dma("sp", chy[s], yt[s][:], yv[:, :, tt * 128:(tt + 1) * 128], writes=[(tag, "yt", s)])
            kb.dma("sp", chx[s], xr[s][:], xres_dram[tt * 128:(tt + 1) * 128, :], writes=[(tag, "xr", s)])
            for nb in range(4):
                for k in range(16):
                    MM(kb, ps[nb][:], yt[s][:, k, :], wo[:, k, nb * 512:(nb + 1) * 512], k == 0, k == 15,
                       [(tag, "yt", s), (tag, "wo")], [(tag, "ps", nb)])
                TT(kb, "dve", ot[s][:, nb * 512:(nb + 1) * 512], ps[nb][:], xr[s][:, nb * 512:(nb + 1) * 512],
                   ALU.add, [(tag, "ps", nb), (tag, "xr", s)], [(tag, "ot", s)])
            kb.dma("pool", cho[s], out_dram[tt * 128:(tt + 1) * 128, :], ot[s][:],
                   reads=[(tag, "ot", s)], writes=[(tag, "out", tt)])
        kb.end_phase()


def phase_cd_mix(kb, C, pj, cw_unused, y_dram):
    with ExitStack() as stc:
        cw = {}
        chc = kb.chan("cd_const")
        for name, shape, dt in CD_CONST_SPECS:
            d = kb.nc.dram_tensor(name, list(shape), dt, kind="ExternalInput").ap()
            t = kb.sb("c_" + name, shape, dt, stc)
            kb.dma("sp", chc, t[:], d, writes=[("const", name)])
            cw[name] = t
        kb.end_phase()
        _phase_cd_mix(kb, C, pj, cw, y_dram)


def _phase_cd_mix(kb, C, pj, cw, y_dram):
    HB = 2048
    tag = "cd"
    with ExitStack() as st:
        QB = 1024
        ident_b = kb.sb("cd_identb", [128, 128], BF16, st)
        dg = kb.sb("cd_dg", [128, 8, 31, 128], BF16, st)
        uc = kb.sb("cd_uc", [128, 8, QB], F32, st)
        ab = [kb.sb(f"cd_a{i}", [128, 30 + QB], F32, st) for i in range(2)]
        gb = [kb.sb(f"cd_g{i}", [128, 30 + QB], F32, st) for i in range(2)]
        ub = [kb.sb(f"cd_u{i}", [128, 30 + QB], BF16, st) for i in range(2)]
        sq = [kb.sb(f"cd_sq{i}", [128, QB], F32, st) for i in range(2)]
        mean = kb.sb("cd_mean", [128, QB], F32, st)
        rstd = kb.sb("cd_rstd", [128, QB], F32, st)
        m2 = kb.sb("cd_m2", [128, QB], F32, st)
        zb = [kb.sb(f"cd_z{i}", [128, QB], F32, st) for i in range(2)]
        yb = [kb.sb(f"cd_y{i}", [128, QB], BF16, st) for i in range(2)]
        pc = [kb.ps(f"cd_pc{i}", [128, QB], F32, st) for i in range(2)]
        ps_sum = kb.ps("cd_pss", [128, QB], F32, st)
        ps_ssq = kb.ps("cd_psq", [128, QB], F32, st)
        cha = [kb.chan(f"cd_a{i}") for i in range(2)]
        chg = [kb.chan(f"cd_g{i}") for i in range(2)]
        chz = [kb.chan(f"cd_z{i}") for i in range(2)]
        chy = [kb.chan(f"cd_y{i}") for i in range(2)]
        CP(kb, "dve", ident_b[:], C["ident"][:], [], [(tag, "cst")])
        for cc in range(8):
            TT(kb, "dve", dg[:, cc, :, :], ident_b[:].unsqueeze(1).to_broadcast([128, 31, 128]),
               cw["dww"][:, cc, :].unsqueeze(2).to_broadcast([128, 31, 128]), ALU.mult, [(tag, "cst")], [(tag, "dg")])
        it = 0
        for tq in range(T // QB):
            t0 = tq * QB
            for cc in range(8):
                s = it % 2
                it += 1
                r0 = cc * 128
                if tq == 0:
                    MS(kb, "pool", ab[s][:, 0:30], 0.0, [(tag, "a", s)])
                    MS(kb, "pool", gb[s][:, 0:30], 0.0, [(tag, "g", s)])
                    kb.dma("sp", cha[s], ab[s][:, 30:30 + QB], pj[r0:r0 + 128, 0:QB], writes=[(tag, "a", s)])
                    kb.dma("sp", chg[s], gb[s][:, 30:30 + QB], pj[1024 + r0:1024 + r0 + 128, 0:QB],
                           writes=[(tag, "g", s)])
                else:
                    kb.dma("sp", cha[s], ab[s][:], pj[r0:r0 + 128, t0 - 30:t0 + QB], writes=[(tag, "a", s)])
                    kb.dma("sp", chg[s], gb[s][:], pj[1024 + r0:1024 + r0 + 128, t0 - 30:t0 + QB],
                           writes=[(tag, "g", s)])
                ACT(kb, gb[s][:], gb[s][:], AF.Sigmoid, [(tag, "g", s)], [(tag, "g", s)])
                TT(kb, "dve", ub[s][:], ab[s][:], gb[s][:], ALU.mult, [(tag, "a", s), (tag, "g", s)], [(tag, "u", s)])
                for b in range(QB // 512):
                    for j in range(31):
                        MM(kb, pc[s][:, b * 512:(b + 1) * 512], dg[:, cc, j, :], ub[s][:, j + b * 512:j + b * 512 + 512],
                           j == 0, j == 30, [(tag, "dg"), (tag, "u", s)], [(tag, "pc", s)])
                ACT(kb, uc[:, cc, :], pc[s][:], AF.Identity, [(tag, "pc", s)], [(tag, "uc", cc)],
                    bias=cw["dwb"][:, cc:cc + 1])
                ACT(kb, sq[s][:], uc[:, cc, :], AF.Square, [(tag, "uc", cc)], [(tag, "sq", s)])
                for tb in range(QB // 512):
                    last = tb == QB // 512 - 1
                    MM(kb, ps_sum[:, tb * 512:(tb + 1) * 512], C["ones_f"][:], uc[:, cc, tb * 512:(tb + 1) * 512],
                       cc == 0, cc == 7, [(tag, "uc", cc)], [(tag, "pss")], signal=last)
                    MM(kb, ps_ssq[:, tb * 512:(tb + 1) * 512], C["ones_f"][:], sq[s][:, tb * 512:(tb + 1) * 512],
                       cc == 0, cc == 7, [(tag, "sq", s)], [(tag, "psq")], signal=last)
            ACT(kb, mean[:], ps_sum[:], AF.Copy, [(tag, "pss")], [(tag, "mean")], scale=1.0 / 1024)
            TT(kb, "dve", m2[:], mean[:], mean[:], ALU.mult, [(tag, "mean")], [(tag, "m2")])
            STT(kb, m2[:], ps_ssq[:], 1.0 / 1024, m2[:], ALU.mult, ALU.subtract, [(tag, "psq"), (tag, "m2")],
                [(tag, "m2")])
            ACT(kb, m2[:], m2[:], AF.Sqrt, [(tag, "m2")], [(tag, "m2")], bias=C["eps"][:, 0:1])
            RECIP(kb, rstd[:], m2[:], [(tag, "m2")], [(tag, "rstd")])
            for cc in range(8):
                s = cc % 2
                r0 = cc * 128
                kb.dma("sp", chz[s], zb[s][:], pj[2048 + r0:2048 + r0 + 128, t0:t0 + QB], writes=[(tag, "z", s)])
                TT(kb, "dve", uc[:, cc, :], uc[:, cc, :], mean[:], ALU.subtract, [(tag, "uc", cc), (tag, "mean")],
                   [(tag, "uc", cc)])
                TT(kb, "dve", uc[:, cc, :], uc[:, cc, :], rstd[:], ALU.mult, [(tag, "uc", cc), (tag, "rstd")],
                   [(tag, "uc", cc)])
                ACT(kb, uc[:, cc, :], uc[:, cc, :], AF.Silu, [(tag, "uc", cc)], [(tag, "uc", cc)],
                    scale=cw["lnw"][:, cc:cc + 1], bias=cw["lnb"][:, cc:cc + 1])
                ACT(kb, zb[s][:], zb[s][:], AF.Silu, [(tag, "z", s)], [(tag, "z", s)])
                TT(kb, "dve", yb[s][:], uc[:, cc, :], zb[s][:], ALU.mult, [(tag, "uc", cc), (tag, "z", s)],
                   [(tag, "y", s)])
                kb.dma("pool", chy[s], y_dram[r0:r0 + 128, t0:t0 + QB], yb[s][:], reads=[(tag, "y", s)],
                       writes=[(tag, "yd", cc, tq)])
        kb.end_phase()
    tag = "sc"
    with ExitStack() as st:
        W = T + 2
        bg = [kb.sb(f"sc_b{i}", [128, T], F32, st) for i in range(2)]
        cg = [kb.sb(f"sc_c{i}", [128, W], F32, st) for i in range(2)]
        ud = [kb.sb(f"sc_u{i}", [128, W], F32, st) for i in range(2)]
        zd = [kb.sb(f"sc_z{i}", [128, T], F32, st) for i in range(2)]
        acc = [kb.sb(f"sc_acc{i}", [128, T], F32, st) for i in range(2)]
        yb = [kb.sb(f"sc_y{i}", [128, T], BF16, st) for i in range(2)]
        chs = {n: [kb.chan(f"sc_{n}{i}") for i in range(2)] for n in ("b", "c", "u", "z", "y")}
        for cc in range(8):
            s = cc % 2
            r0 = cc * 128
            MS(kb, "pool", cg[s][:, 0:2], 0.0, [(tag, "c", s)])
            MS(kb, "pool", ud[s][:, 0:2], 0.0, [(tag, "u", s)])
            kb.dma("sp", chs["b"][s], bg[s][:], pj[3072 + r0:3072 + r0 + 128, :], writes=[(tag, "b", s)])
            kb.dma("sp", chs["c"][s], cg[s][:, 2:W], pj[4096 + r0:4096 + r0 + 128, :], writes=[(tag, "c", s)])
            kb.dma("sp", chs["u"][s], ud[s][:, 2:W], pj[5120 + r0:5120 + r0 + 128, :], writes=[(tag, "u", s)])
            kb.dma("sp", chs["z"][s], zd[s][:], pj[6144 + r0:6144 + r0 + 128, :], writes=[(tag, "z", s)])
            TT(kb, "dve", cg[s][:], cg[s][:], ud[s][:], ALU.mult, [(tag, "c", s), (tag, "u", s)], [(tag, "c", s)])
            TS(kb, "dve", acc[s][:], cg[s][:, 2:W], cw["dcw"][:, cc, 2:3], None, ALU.mult, None,
               [(tag, "c", s)], [(tag, "acc", s)])
            for j in range(2):
                STT(kb, acc[s][:], cg[s][:, j:j + T], cw["dcw"][:, cc, j:j + 1], acc[s][:], ALU.mult, ALU.add,
                    [(tag, "c", s), (tag, "acc", s)], [(tag, "acc", s)])
            ACT(kb, zd[s][:], zd[s][:], AF.Silu, [(tag, "z", s)], [(tag, "z", s)])
            TT(kb, "pool", bg[s][:], bg[s][:], zd[s][:], ALU.mult, [(tag, "b", s), (tag, "z", s)], [(tag, "b", s)])
            TT(kb, "dve", yb[s][:], acc[s][:], bg[s][:], ALU.mult, [(tag, "acc", s), (tag, "b", s)], [(tag, "y", s)])
            kb.dma("pool", chs["y"][s], y_dram[1024 + r0:1024 + r0 + 128, :], yb[s][:], reads=[(tag, "y", s)],
                   writes=[(tag, "yd", cc)])
        kb.end_phase()


def colnorm_phase(kb, C, tag, src, KC, gT, outT):
    with ExitStack() as st:
        raw = [kb.sb(f"{tag}_raw{i}", [128, KC, 1024], F32, st) for i in range(2)]
        sq = [kb.sb(f"{tag}_sq{i}", [128, 1024], F32, st) for i in range(2)]
        rs = kb.sb(f"{tag}_rs", [128, 1024], F32, st)
        ssp = kb.ps(f"{tag}_ssp", [128, 1024], F32, st)
        ch = [kb.chan(f"{tag}_l{i}") for i in range(2)]
        sv = src.rearrange("(k p) t -> p k t", p=128)
        for ts in range(4):
            s = ts % 2
            kb.dma("sp", ch[s], raw[s][:], sv[:, :, ts * 1024:(ts + 1) * 1024], writes=[(tag, "raw", s)])
            for k in range(KC):
                q = k % 2
                ACT(kb, sq[q][:], raw[s][:, k, :], AF.Square, [(tag, "raw", s)], [(tag, "sq", q)])
                for b in range(2):
                    MM(kb, ssp[:, b * 512:(b + 1) * 512], C["ones_f"][:], sq[q][:, b * 512:(b + 1) * 512],
                       k == 0, k == KC - 1, [(tag, "sq", q)], [(tag, "ssp")], signal=True)
            ACT(kb, rs[:], ssp[:], AF.Sqrt, [(tag, "ssp")], [(tag, "rs")], bias=C["eps"][:, 0:1],
                scale=1.0 / (KC * 128))
            RECIP(kb, rs[:], rs[:], [(tag, "rs")], [(tag, "rs")])
            for k in range(KC):
                STT(kb, outT[:, k, ts * 1024:(ts + 1) * 1024], raw[s][:, k, :], gT[:, k:k + 1], rs[:],
                    ALU.mult, ALU.mult, [(tag, "raw", s), (tag, "rs")], [(tag, "out", k, ts)])
        kb.end_phase()


class HeadNormSink:
    def __init__(self, kb, C, tag, dst, gain, n_norm, raw_dst, st):
        self.kb, self.C, self.tag, self.dst, self.gain = kb, C, tag, dst, gain
        self.n_norm, self.raw_dst = n_norm, raw_dst
        self.sq = [kb.sb(f"{tag}_sq{i}", [128, 1024], F32, st) for i in range(2)]
        self.rs = [kb.sb(f"{tag}_rs{i}", [128, 1024], F32, st) for i in range(2)]
        self.ob = [kb.sb(f"{tag}_ob{i}", [128, 1024], BF16, st) for i in range(2)]
        self.ssp = kb.ps(f"{tag}_ssp", [128, 1024], F32, st)
        self.ch = [kb.chan(f"{tag}_o{i}") for i in range(2)]
        self.i = 0

    def __call__(self, c, ts, ps, pkey, st):
        kb, C, tag = self.kb, self.C, self.tag
        s = self.i % 2
        self.i += 1
        if c < self.n_norm:
            ACT(kb, self.sq[s][:], ps[:], AF.Square, [pkey], [(tag, "sq", s)])
            for b in range(2):
                MM(kb, self.ssp[:, b * 512:(b + 1) * 512], C["ones_f"][:], self.sq[s][:, b * 512:(b + 1) * 512],
                   True, True, [(tag, "sq", s)], [(tag, "ssp")])
            ACT(kb, self.rs[s][:], self.ssp[:], AF.Sqrt, [(tag, "ssp")], [(tag, "rs", s)], bias=C["eps"][:, 0:1],
                scale=1.0 / 128)
            RECIP(kb, self.rs[s][:], self.rs[s][:], [(tag, "rs", s)], [(tag, "rs", s)])
            STT(kb, self.ob[s][:], ps[:], self.gain, self.rs[s][:], ALU.mult, ALU.mult,
                [pkey, (tag, "rs", s)], [(tag, "ob", s)])
            d = self.dst[c]
        else:
            CP(kb, "act", self.ob[s][:], ps[:], [pkey], [(tag, "ob", s)])
            d = self.raw_dst[c - self.n_norm]
        kb.dma("sp", self.ch[s], d[:, ts * 1024:(ts + 1) * 1024], self.ob[s][:], reads=[(tag, "ob", s)],
               writes=[(tag, "dst", c)])


def phase_dsa_prep(kb, C, pj, W, S):
    with ExitStack() as st:
        cqn = kb.sb("cqn", [128, 4, T], BF16, st)
        colnorm_phase(kb, C, "cq", pj[0:512, :], 4, C["qnormT"], cqn)
        with ExitStack() as st2:
            sink = HeadNormSink(kb, C, "qs", S["qT"], C["qgain"][:, 0:1], 8, S["qiT"], st2)
            gemm_fm(kb, "qg", cqn, [], 4, W["w_uqiq"], 16, sink)
    with ExitStack() as st:
        ckvn = kb.sb("ckvn", [128, 2, T], BF16, st)
        colnorm_phase(kb, C, "ckv", pj[512:768, :], 2, C["kvnormT"], ckvn)
        with ExitStack() as st2:
            sink = HeadNormSink(kb, C, "ks", S["kT"], C["kgain"][:, 0:1], 8, None, st2)
            gemm_fm(kb, "kg", ckvn, [], 2, W["w_uk"], 8, sink)
        with ExitStack() as st2:
            wv = kb.sb("wv", [128, 2, 1024], BF16, st2)
            chw = kb.chan("wv")
            for k in range(2):
                kb.dma("pool", chw, wv[:, k, :], W["w_uv"][k], writes=[("wv",)])
            vps = [kb.ps(f"v_ps{i}", [128, 1024], F32, st2) for i in range(2)]
            vb = [kb.sb(f"v_b{i}", [128, 1024], BF16, st2) for i in range(2)]
            chv = [kb.chan(f"v_o{i}") for i in range(2)]
            for tt in range(NT):
                s = tt % 2
                for b in range(2):
                    for k in range(2):
                        MM(kb, vps[s][:, b * 512:(b + 1) * 512], ckvn[:, k, tt * 128:(tt + 1) * 128],
                           wv[:, k, b * 512:(b + 1) * 512], k == 0, k == 1, [("wv",)], [("v", "ps", s)])
                CP(kb, "act" if tt % 2 else "dve", vb[s][:], vps[s][:], [("v", "ps", s)], [("v", "b", s)])
                kb.dma("sp", chv[s], S["V"][tt * 128:(tt + 1) * 128, :], vb[s][:], reads=[("v", "b", s)],
                       writes=[("V", tt)])
            kb.end_phase()


def phase_ki_tmaj(kb, C, pj, S):
    with ExitStack() as st:
        ki = kb.sb("ki_raw", [128, T], F32, st)
        sq = kb.sb("ki_sq", [128, T], F32, st)
        mean = kb.sb("ki_mean", [128, 1024], F32, st)
        var = kb.sb("ki_var", [128, 1024], F32, st)
        kio = kb.sb("ki_o", [128, T], BF16, st)
        sm = kb.sb("ki_sm", [32, T], F32, st)
        tmo = kb.sb("ki_tmo", [128, NT, 32], F32, st)
        ps1 = kb.ps("ki_ps1", [128, 1024], F32, st)
        ps2 = kb.ps("ki_ps2", [128, 1024], F32, st)
        pst = kb.ps("ki_pst", [128, 16, 32], F32, st)
        ch = kb.chan("ki")
        kb.dma("sp", ch, ki[0:64, :], pj[768:832, :], writes=[("ki", "raw")])
        kb.dma("sp", ch, ki[64:128, :], pj[768:832, :], writes=[("ki", "raw")])
        kb.dma("sp", ch, sm[:], pj[896:928, :], writes=[("ki", "sm")])
        ACT(kb, sq[:], ki[:], AF.Square, [("ki", "raw")], [("ki", "sq")])
        for ts in range(4):
            for b in range(2):
                c0 = ts * 1024 + b * 512
                MM(kb, ps1[:, b * 512:(b + 1) * 512], C["ones_f"][0:64, :], ki[0:64, c0:c0 + 512], True, True,
                   [("ki", "raw")], [("ki", "ps1")])
                MM(kb, ps2[:, b * 512:(b + 1) * 512], C["ones_f"][0:64, :], sq[0:64, c0:c0 + 512], True, True,
                   [("ki", "sq")], [("ki", "ps2")])
            ACT(kb, mean[:], ps1[:], AF.Copy, [("ki", "ps1")], [("ki", "mean")], scale=1.0 / 64)
            TT(kb, "dve", var[:], mean[:], mean[:], ALU.mult, [("ki", "mean")], [("ki", "var")])
            STT(kb, var[:], ps2[:], 1.0 / 64, var[:], ALU.mult, ALU.subtract, [("ki", "ps2"), ("ki", "var")],
                [("ki", "var")])
            ACT(kb, var[:], var[:], AF.Sqrt, [("ki", "var")], [("ki", "var")], bias=C["eps"][:, 0:1])
            RECIP(kb, var[:], var[:], [("ki", "var")], [("ki", "var")])
            sl = slice(ts * 1024, (ts + 1) * 1024)
            TT(kb, "dve", ki[:, sl], ki[:, sl], mean[:], ALU.subtract, [("ki", "raw"), ("ki", "mean")], [("ki", "raw")])
            TT(kb, "dve", ki[:, sl], ki[:, sl], var[:], ALU.mult, [("ki", "raw"), ("ki", "var")], [("ki", "raw")])
            TS(kb, "dve", kio[:, sl], ki[:, sl], C["ikw"][:, 0:1], C["ikb"][:, 0:1], ALU.mult, ALU.add,
               [("ki", "raw")], [("ki", "o")])
        kb.dma("sp", ch, S["kiT"], kio[:], reads=[("ki", "o")], writes=[("kiT",)])
        for g in range(2):
            for tt in range(16):
                t = g * 16 + tt
                TR(kb, pst[:, tt, :], sm[:, t * 128:(t + 1) * 128], C["ident"][0:32, 0:32], [("ki", "sm")],
                   [("ki", "pst")], signal=(tt == 15))
            CP(kb, "dve", tmo[:, g * 16:(g + 1) * 16, :], pst[:], [("ki", "pst")], [("ki", "tmo")])
        kb.dma("sp", ch, S["tmaj"].rearrange("(n p) c -> p n c", p=128), tmo[:], reads=[("ki", "tmo")],
               writes=[("tmaj",)])
        kb.end_phase()


N_BIS = 16
SCALE_A = 128 ** -0.5


def phase_dsa_attn(kb, C, pj, S, y_dram):
    tag = "at"
    with ExitStack() as st:
        kT = kb.sb("at_kT", [128, 8, T], BF16, st)
        Vt = kb.sb("at_V", [128, NT, 1024], BF16, st)
        kiT = kb.sb("at_kiT", [128, T], BF16, st)
        score1 = kb.sb("at_sc", [128, T], F32, st)
        score = [score1, score1]
        mask = kb.sb("at_mask", [128, T], BF16, st)
        junk = mask
        maskT1 = kb.sb("at_maskT", [128, NT, 128], BF16, st)
        maskT = [maskT1, maskT1]
        rbuf = [kb.sb(f"at_r{i}", [128, 2, 512], BF16, st) for i in range(2)]
        dsg = kb.sb("at_dsg", [128, 16, 128], BF16, st)
        qb = [kb.sb(f"at_q{i}", [128, 8, 128], BF16, st) for i in range(2)]
        qib1 = kb.sb("at_qi", [128, 8, 128], BF16, st)
        qib = [qib1, qib1]
        wt = [kb.sb(f"at_w{i}", [128, 16], F32, st) for i in range(2)]
        bs = [kb.sb(f"at_bs{i}", [128, 8], F32, st) for i in range(2)]
        za1 = kb.sb("at_za", [128, 8, 128], F32, st)
        za = [za1, za1]
        pt = [kb.sb(f"at_pt{i}", [128, 4, 128], BF16, st) for i in range(3)]
        ost1 = kb.sb("at_ost", [128, 8, 256], F32, st)
        ost = [ost1, ost1]
        yst1 = kb.sb("at_yst", [128, 8, 128], BF16, st)
        yst = [yst1, yst1]
        biasS = C["biasS"]
        ident_b = kb.sb("at_identb", [128, 128], BF16, st)
        ones_b = kb.sb("at_onesb", [128, 128], BF16, st)
        ips = [kb.ps(f"at_ips{i}", [128, 2, 512], F32, st) for i in range(2)]
        lg = [ips[i][:, 0, :].rearrange("p (a b) -> p a b", b=128) for i in range(2)]
        po1 = kb.ps("at_po", [128, 128], F32, st)
        prs1 = kb.ps("at_prs", [128, 128], F32, st)
        po, prs = [po1, po1], [prs1, prs1]
        sps1 = kb.ps("at_sps", [128, 512], F32, st)
        sps = [sps1, sps1]
        tps = ips[0][:, 0, :].bitcast(BF16)[:, 0:512].rearrange("p (a b) -> p a b", b=128)
        chl = kb.chan("at_ld")
        chq = [kb.chan(f"at_q{i}") for i in range(2)]
        chy = [kb.chan(f"at_y{i}") for i in range(2)]
        chz = kb.chan("at_z")
        kb.dma("sp", chl, kT[:], S["kT"].rearrange("h p t -> p h t"), writes=[(tag, "kT")])
        kb.dma("sp", chl, Vt[:], S["V"].rearrange("(n p) c -> p n c", p=128), writes=[(tag, "V")])
        kb.dma("sp", chl, kiT[:], S["kiT"], writes=[(tag, "kiT")])
        CP(kb, "dve", ident_b[:], C["ident"][:], [], [(tag, "cst")])
        CP(kb, "dve", ones_b[:], C["ones_f"][:], [], [(tag, "cst")])
        qTv = S["qT"].rearrange("h p t -> p h t")
        qiTv = S["qiT"].rearrange("h p t -> p h t")
        zav = pj[1024:2048, :].rearrange("(h p) t -> p h t", p=128)
        yv = y_dram[0:1024, :].rearrange("(h p) t -> p h t", p=128)
        cnt = {"ips": 0, "r": 0, "lg": 0, "pt": 0, "po": 0, "sps": 0}

        def stage_a(i):
            s = i % 2
            c0, c1 = i * 128, (i + 1) * 128
            kb.dma("sp", chq[s], qb[s][:], qTv[:, :, c0:c1], writes=[(tag, "q", s)])
            kb.dma("sp", chq[s], qib[s][:], qiTv[:, :, c0:c1], writes=[(tag, "qi")])
            kb.dma("sp", chq[s], wt[s][:], S["tmaj"][c0:c1, 0:16], writes=[(tag, "w", s)])
            nW = (i + 4) // 4
            Wi = nW * 512
            sk = (tag, "score")
            TT(kb, "dve", dsg[:], ident_b[:].unsqueeze(1).to_broadcast([128, 16, 128]),
               wt[s][:, 0:16].unsqueeze(2).to_broadcast([128, 16, 128]), ALU.mult, [(tag, "w", s), (tag, "cst")],
               [(tag, "dsg")])
            for w in range(nW):
                sp_ = cnt["sps"] % 2
                cnt["sps"] += 1
                units = []

                def acc(u):
                    h0, r_ = u
                    for e_ in range(2):
                        MM(kb, sps[sp_][:], dsg[:, h0 + e_, :], rbuf[r_][:, e_, :], h0 + e_ == 0, h0 + e_ == 15,
                           [(tag, "dsg"), (tag, "r", r_)], [(tag, "sps", 0)], signal=(e_ == 1))

                for pair in range(8):
                    p = cnt["ips"] % 2
                    cnt["ips"] += 1
                    r = cnt["r"] % 2
                    cnt["r"] += 1
                    for e_ in range(2):
                        base = e_ * 64
                        MM(kb, ips[p][:, e_, :], qib[s][base:base + 64, pair, :],
                           kiT[base:base + 64, w * 512:(w + 1) * 512], True, True, [(tag, "qi"), (tag, "kiT")],
                           [(tag, "ips", p)], signal=(e_ == 1))
                    if pair % 2 == 0:
                        ACT(kb, rbuf[r][:], ips[p][:], AF.Relu, [(tag, "ips", p)], [(tag, "r", r)])
                    else:
                        TS(kb, "dve", rbuf[r][:], ips[p][:], 0.0, None, ALU.max, None, [(tag, "ips", p)], [(tag, "r", r)])
                    if units:
                        acc(units.pop())
                    units.append((2 * pair, r))
                acc(units.pop())
                CP(kb, "dve", score[s][:, w * 512:(w + 1) * 512], sps[sp_][:], [(tag, "sps", 0)], [sk])
            b = bs[s]
            bk = (tag, "bs", s)
            TS(kb, "dve", junk[:, 0:Wi], score[s][:, 0:Wi], 1.0, None, ALU.mult, ALU.max, [sk], [(tag, "mask"), bk],
               accum_out=b[:, 0:1])
            TS(kb, "dve", junk[:, 0:Wi], score[s][:, 0:Wi], -1.0, None, ALU.mult, ALU.max, [sk], [(tag, "mask"), bk],
               accum_out=b[:, 6:7])
            TT(kb, "dve", b[:, 0:1], b[:, 0:1], b[:, 6:7], ALU.max, [bk], [bk])
            TT(kb, "dve", score[s][:, Wi - 512:Wi], score[s][:, Wi - 512:Wi],
               C["cbase"][:, 384 - (i % 4) * 128:896 - (i % 4) * 128], ALU.add, [sk], [sk])
            TS(kb, "dve", b[:, 1:2], b[:, 0:1], -1.001, -1e-20, ALU.mult, ALU.add, [bk], [bk])
            TS(kb, "dve", b[:, 2:3], b[:, 0:1], 2.002, 2e-20, ALU.mult, ALU.add, [bk], [bk])
            for k in range(1, N_BIS + 1):
                f = 2.0 ** -k
                STT(kb, b[:, 3:4], b[:, 2:3], f, b[:, 1:2], ALU.mult, ALU.add, [bk], [bk])
                TS(kb, "dve", junk[:, 0:Wi], score[s][:, 0:Wi], b[:, 3:4], None, ALU.is_ge, ALU.add,
                   [sk, bk], [(tag, "mask"), bk], accum_out=b[:, 4:5])
                TS(kb, "dve", b[:, 5:6], b[:, 4:5], 255.5, f, ALU.is_ge, ALU.mult, [bk], [bk])
                STT(kb, b[:, 1:2], b[:, 5:6], b[:, 2:3], b[:, 1:2], ALU.mult, ALU.add, [bk], [bk])

        def stage_t(i):
            s = i % 2
            n = i + 1
            TS(kb, "dve", mask[:, 0:n * 128], score[s][:, 0:n * 128], bs[s][:, 1:2], None, ALU.is_ge, None,
               [(tag, "score"), (tag, "bs", s)], [(tag, "mask")])
            for j0 in range(0, n, 4):
                nb = min(4, n - j0)
                for jj in range(nb):
                    j = j0 + jj
                    TR(kb, tps[:, jj, :], mask[:, j * 128:(j + 1) * 128], ident_b[:], [(tag, "mask"), (tag, "cst")],
                       [(tag, "ips", 0)], signal=(jj == nb - 1))
                ACT(kb, maskT[s][:, j0:j0 + nb, :], tps[:, 0:nb, :], AF.Copy, [(tag, "ips", 0)], [(tag, "maskT")],
                    scale=30000.0, bias=-30000.0)

        def stage_b(i):
            s = i % 2
            n = i + 1
            groups = [(h, j0, min(4, n - j0)) for h in range(8) for j0 in range(0, n, 4)]
            slots = {}

            def qk(gi):
                h, j0, nb = groups[gi]
                p = cnt["lg"] % 2
                cnt["lg"] += 1
                x = cnt["pt"] % 3
                cnt["pt"] += 1
                slots[gi] = x
                for jj in range(nb):
                    j = j0 + jj
                    near = (i - j) <= 1
                    MM(kb, lg[p][:, jj, :], kT[:, h, j * 128:(j + 1) * 128], qb[s][:, h, :], True, False,
                       [(tag, "kT"), (tag, "q", s)], [(tag, "ips", p)], signal=False)
                    if near:
                        MM(kb, lg[p][:, jj, :], ident_b[:], biasS[:, h, i - j, :], False, False,
                           [(tag, "cst")], [(tag, "ips", p)], signal=False)
                    MM(kb, lg[p][:, jj, :], ident_b[:], maskT[s][:, j, :], False, True,
                       [(tag, "cst"), (tag, "maskT")], [(tag, "ips", p)], signal=(jj == nb - 1))
                ACT(kb, pt[x][:, 0:nb, :], lg[p][:, 0:nb, :], AF.Exp, [(tag, "ips", p)], [(tag, "pt", x)],
                    scale=SCALE_A, bias=C["cb"][:, h:h + 1])

            def pv(gi):
                h, j0, nb = groups[gi]
                x = slots.pop(gi)
                for jj in range(nb):
                    j = j0 + jj
                    MM(kb, po[0][:], Vt[:, j, h * 128:(h + 1) * 128], pt[x][:, jj, :], j == 0, j == n - 1,
                       [(tag, "V"), (tag, "pt", x)], [(tag, "po", 0)])
                    MM(kb, prs[0][:], ones_b[:], pt[x][:, jj, :], j == 0, j == n - 1,
                       [(tag, "cst"), (tag, "pt", x)], [(tag, "prs", 0)], signal=(jj == nb - 1))
                if j0 + nb == n:
                    CP(kb, "act", ost[s][:, h, 0:128], po[0][:], [(tag, "po", 0)], [(tag, "ost", h)])
                    CP(kb, "act", ost[s][:, h, 128:256], prs[0][:], [(tag, "prs", 0)], [(tag, "ost", h)])

            qk(0)
            for gi in range(len(groups)):
                if gi + 1 < len(groups):
                    qk(gi + 1)
                pv(gi)

        def stage_f(i):
            s = i % 2
            keys = [(tag, "ost", h) for h in range(8)]
            kb.dma("sp", chz, za[s][:], zav[:, :, i * 128:(i + 1) * 128], writes=[(tag, "za")])
            ACT(kb, za[s][:], za[s][:], AF.Silu, [(tag, "za")], [(tag, "za")])
            RECIP(kb, ost[s][:, :, 128:256], ost[s][:, :, 128:256], keys, keys)
            TT(kb, "dve", ost[s][:, :, 0:128], ost[s][:, :, 0:128], ost[s][:, :, 128:256], ALU.mult, keys, keys)
            TT(kb, "dve", yst[s][:], ost[s][:, :, 0:128], za[s][:], ALU.mult, keys + [(tag, "za")], [(tag, "yst")])
            kb.dma("pool", chy[s], yv[:, :, i * 128:(i + 1) * 128], yst[s][:], reads=[(tag, "yst")],
                   writes=[(tag, "y", i)])

        stage_a(0)
        stage_t(0)
        for i in range(NT):
            if i + 1 < NT:
                stage_a(i + 1)
            stage_b(i)
            if i + 1 < NT:
                stage_t(i + 1)
            stage_f(i)
        kb.end_phase()


def phase_gdn_prep(kb, C, pj, S):
    tag = "gp"
    with ExitStack() as st:
        W = T + 3
        HB = T // 2
        raw = [kb.sb(f"gp_raw{i}", [128, W], F32, st) for i in range(2)]
        acc = [kb.sb(f"gp_acc{i}", [128, T], F32, st) for i in range(2)]
        sq = [kb.sb(f"gp_sq{i}", [128, T], F32, st) for i in range(2)]
        rn = [kb.sb(f"gp_rn{i}", [128, T], F32, st) for i in range(2)]
        ssp = [kb.ps(f"gp_ssp{i}", [128, HB], F32, st) for i in range(2)]
        chl = [kb.chan(f"gp_l{i}") for i in range(2)]
        chs = [kb.chan(f"gp_s{i}") for i in range(2)]

        def head(cc):
            s = cc % 2
            r0 = 2048 + cc * 128
            MS(kb, "pool", raw[s][:, 0:3], 0.0, [(tag, "raw", s)])
            kb.dma("sp", chl[s], raw[s][:, 3:W], pj[r0:r0 + 128, :], writes=[(tag, "raw", s)])
            TS(kb, "dve", acc[s][:], raw[s][:, 3:W], C["cvw"][:, cc, 3:4], None, ALU.mult, None, [(tag, "raw", s)],
               [(tag, "acc", s)])
            for j in range(3):
                STT(kb, acc[s][:], raw[s][:, j:j + T], C["cvw"][:, cc, j:j + 1], acc[s][:], ALU.mult, ALU.add,
                    [(tag, "raw", s), (tag, "acc", s)], [(tag, "acc", s)])
            ACT(kb, acc[s][:], acc[s][:], AF.Silu, [(tag, "acc", s)], [(tag, "acc", s)])
            if cc < 16:
                ACT(kb, sq[s][:], acc[s][:], AF.Square, [(tag, "acc", s)], [(tag, "sq", s)])
                for hb in range(2):
                    for b in range(4):
                        c0 = hb * HB + b * 512
                        MM(kb, ssp[hb][:, b * 512:(b + 1) * 512], C["ones_f"][:], sq[s][:, c0:c0 + 512], True, True,
                           [(tag, "sq", s)], [(tag, "ssp", hb)], signal=(b == 3))
                    ACT(kb, rn[s][:, hb * HB:(hb + 1) * HB], ssp[hb][:], AF.Sqrt, [(tag, "ssp", hb)],
                        [(tag, "rn", s, hb)], bias=C["eps"][:, 0:1])

        def tail(cc):
            s = cc % 2
            if cc < 16:
                keys = [(tag, "rn", s, 0), (tag, "rn", s, 1)]
                RECIP(kb, rn[s][:], rn[s][:], keys, keys)
                STT(kb, acc[s][:], acc[s][:], (128 ** -0.5) if cc < 8 else 1.0, rn[s][:], ALU.mult, ALU.mult,
                    [(tag, "acc", s)] + keys, [(tag, "acc", s)])
            kb.dma("pool", chs[s], S["gqkv"][cc * 128:(cc + 1) * 128, :], acc[s][:], reads=[(tag, "acc", s)],
                   writes=[("gqkv", cc)])

        head(0)
        for cc in range(24):
            if cc + 1 < 24:
                head(cc + 1)
            tail(cc)
        kb.end_phase()


import os
GDN_TILES = int(os.environ.get("GDN_TILES", "32"))
GDN_STOP = int(os.environ.get("GDN_STOP", "99"))
GDN_SUB = float(os.environ.get("GDN_SUB", "99"))


def phase_gdn(kb, C, pj, S, y_dram):
    with ExitStack() as stc:
        C = dict(C)
        chc = kb.chan("gd_const")
        for name, shape, dt in GDN_CONST_SPECS:
            d = kb.nc.dram_tensor(name, list(shape), dt, kind="ExternalInput").ap()
            t = kb.sb("c_" + name, shape, dt, stc)
            kb.dma("sp", chc, t[:], d, writes=[("const", name)])
            C[name] = t
        kb.end_phase()
        phase_gdn_prep(kb, C, pj, S)
        _phase_gdn_main(kb, C, pj, S, y_dram)


def _phase_gdn_main(kb, C, pj, S, y_dram):
    tag = "gd"
    H = 8
    with ExitStack() as st:
        def fb(name, shape=(128, H, 128), dt=F32):
            return kb.sb("gd_" + name, list(shape), dt, st)

        tm = fb("tm", (128, NT, 32))
        beta = fb("beta", (128, NT, 8))
        g = fb("g", (128, NT, 8))
        t1 = fb("t1", (128, NT, 8))
        t2 = fb("t2", (128, NT, 8))
        nA = fb("nA", (128, 8))
        qT, kT, vT = fb("qT"), fb("kT"), fb("vT")
        gd, egrow, gcr = fb("gdiag"), fb("egrow"), fb("gcr")
        P1, E1, E2 = fb("P1"), fb("E1"), fb("E2")
        X = [fb("X0"), fb("X1")]
        Y = [fb("Y0"), fb("Y1")]
        P, attnT = fb("P"), fb("attnT")
        vb, kbg, kd, kd1 = fb("vb"), fb("kbg"), fb("kd"), fb("kd1")
        smk = fb("smk", (128, 16))
        u, wT, qgT, vnew = fb("u"), fb("wT"), fb("qgT"), fb("vnew")
        Sst, oacc, zb = fb("S"), fb("oacc"), fb("zb")
        osq, orn = fb("osq"), fb("orn")
        yo = fb("yo", (128, H, 128), BF16)
        sm = fb("sm", (128, 64))
        pA = kb.ps("gd_pA", [128, H, 128], F32, st)
        pB = kb.ps("gd_pB", [128, H, 128], F32, st)
        pC = kb.ps("gd_pC", [128, H, 128], F32, st)
        pO = kb.ps("gd_pO", [128, H, 64], F32, st)
        psm = kb.ps("gd_psm", [128, 32], F32, st)
        ch = kb.chan("gd_l")
        chq = kb.chan("gd_q")
        chz = kb.chan("gd_z")
        chy = kb.chan("gd_y")
        K_ = lambda n: (tag, n)

        def bc_h(ap2d):
            return ap2d.unsqueeze(1).to_broadcast([128, H, 128])

        def bc_f(ap2d):
            return ap2d.unsqueeze(2).to_broadcast([128, H, 128])

        kb.dma("sp", ch, tm[:], S["tmaj"].rearrange("(n p) c -> p n c", p=128), writes=[K_("tm")])
        ACT(kb, beta[:], tm[:, :, 16:24], AF.Sigmoid, [K_("tm")], [K_("beta")])
        dtb = C["dtb_bc"][:].unsqueeze(1).to_broadcast([128, NT, 8])
        TT(kb, "dve", g[:], tm[:, :, 24:32], dtb, ALU.add, [K_("tm")], [K_("g")])
        TS(kb, "dve", t1[:], g[:], -1.0, None, ALU.mult, None, [K_("g")], [K_("t1")])
        TT(kb, "dve", t1[:], t1[:], g[:], ALU.max, [K_("t1"), K_("g")], [K_("t1")])
        ACT(kb, t1[:], t1[:], AF.Exp, [K_("t1")], [K_("t1")], scale=-1.0)
        TS(kb, "dve", t1[:], t1[:], 1.0, None, ALU.add, None, [K_("t1")], [K_("t1")])
        ACT(kb, t1[:], t1[:], AF.Ln, [K_("t1")], [K_("t1")])
        TS(kb, "dve", t2[:], g[:], 0.0, None, ALU.max, None, [K_("g")], [K_("t2")])
        TT(kb, "dve", t2[:], t2[:], t1[:], ALU.add, [K_("t1"), K_("t2")], [K_("t2")])
        ACT(kb, nA[:], C["alog_bc"][:], AF.Exp, [], [K_("nA")])
        TS(kb, "dve", nA[:], nA[:], -1.0, None, ALU.mult, None, [K_("nA")], [K_("nA")])
        TT(kb, "dve", g[:], t2[:], nA[:].unsqueeze(1).to_broadcast([128, NT, 8]), ALU.mult, [K_("t2"), K_("nA")],
           [K_("g")])
        MS(kb, "dve", Sst[:], 0.0, [K_("S")])
        MS(kb, "dve", vnew[:], 0.0, [K_("vnew")])
        gq = S["gqkv"]
        qv = gq[0:1024, :].rearrange("(h p) t -> p h t", p=128)
        kv = gq[1024:2048, :].rearrange("(h p) t -> p h t", p=128)
        vv = gq[2048:3072, :].rearrange("(h p) t -> p h t", p=128)
        zv = pj[5120:6144, :].rearrange("(h p) t -> p h t", p=128)
        yv = y_dram[1024:2048, :].rearrange("(h p) t -> p h t", p=128)

        for n in range(GDN_TILES if GDN_STOP > 0 else 0):
            c0, c1 = n * 128, (n + 1) * 128
            kb.dma("sp", chq, qT[:], qv[:, :, c0:c1], writes=[K_("qT")])
            kb.dma("sp", chq, kT[:], kv[:, :, c0:c1], writes=[K_("kT")])
            kb.dma("sp", chq, vT[:], vv[:, :, c0:c1], writes=[K_("vT")])
            kb.dma("sp", chz, zb[:], zv[:, :, c0:c1], writes=[K_("zb")])
            gn = g[:, n, :]
            bn = beta[:, n, :]
            MM(kb, psm[:, 0:8], C["U2"][:], gn, True, True, [K_("g")], [K_("psm")], signal=False)
            MM(kb, psm[:, 8:16], C["Bsame"][:], gn, True, True, [K_("g")], [K_("psm")], signal=False)
            MM(kb, psm[:, 16:24], C["Bsel0"][:], gn, True, True, [K_("g")], [K_("psm")], signal=False)
            MM(kb, psm[:, 24:32], C["Bsel1"][:], gn, True, True, [K_("g")], [K_("psm")])
            CP(kb, "dve", sm[:, 0:32], psm[:], [K_("psm")], [K_("sm")])
            gc = sm[:, 0:8]
            ACT(kb, sm[:, 32:40], sm[:, 0:8], AF.Exp, [K_("sm")], [K_("sm")])
            TT(kb, "dve", sm[:, 40:48], sm[:, 8:16], sm[:, 0:8], ALU.subtract, [K_("sm")], [K_("sm")])
            ACT(kb, sm[:, 40:48], sm[:, 40:48], AF.Exp, [K_("sm")], [K_("sm")])
            ACT(kb, sm[:, 16:32], sm[:, 16:32], AF.Exp, [K_("sm")], [K_("sm")])
            TT(kb, "dve", sm[:, 48:56], sm[:, 32:40], bn, ALU.mult, [K_("sm"), K_("beta")], [K_("sm")])
            TS(kb, "dve", sm[:, 56:64], bn, -1.0, None, ALU.mult, None, [K_("beta")], [K_("sm")])
            if GDN_SUB <= 0:
                continue
            TT(kb, "dve", gd[:], bc_h(C["U2"][:]), bc_f(gn), ALU.mult, [K_("g")], [K_("gdiag")])
            for b in range(2):
                MM(kb, pA[:, 4 * b:4 * b + 4, :], C["ones_f"][:], gd[:, 4 * b:4 * b + 4, :], True, True,
                   [K_("gdiag")], [K_("pA")], signal=(b == 1))
            if GDN_SUB <= 0.3:
                continue
            CP(kb, "dve", gcr[:], pA[:], [K_("pA")], [K_("gcr")])
            ACT(kb, egrow[:], gcr[:], AF.Exp, [K_("gcr")], [K_("egrow")])
            if GDN_SUB <= 0.4:
                continue
            TT(kb, "dve", P1[:], gcr[:], bc_f(gc), ALU.subtract, [K_("gcr"), K_("sm")], [K_("P1")])
            if GDN_SUB <= 0.5:
                continue
            TS(kb, "dve", E1[:], P1[:], 0.0, None, ALU.max, None, [K_("P1")], [K_("E1")])
            ACT(kb, E1[:], E1[:], AF.Exp, [K_("E1")], [K_("E1")], scale=-1.0)
            if GDN_SUB <= 0.6:
                continue
            TS(kb, "dve", E2[:], P1[:], 0.0, None, ALU.min, None, [K_("P1")], [K_("E2")])
            ACT(kb, E2[:], E2[:], AF.Exp, [K_("E2")], [K_("E2")])
            TT(kb, "dve", E1[:], E1[:], bc_h(C["MLs"][:]), ALU.mult, [K_("E1")], [K_("E1")])
            TT(kb, "dve", E1[:], E1[:], bc_f(sm[:, 56:64]), ALU.mult, [K_("E1"), K_("sm")], [K_("E1")])
            TT(kb, "dve", E2[:], E2[:], bc_h(C["MU"][:]), ALU.mult, [K_("E2")], [K_("E2")])
            if GDN_SUB <= 1:
                continue
            for h in range(H):
                MM(kb, pB[:, h, :], kT[:, h, :], kT[:, h, :], True, True, [K_("kT")], [K_("pB")], signal=(h == H - 1))
            TT(kb, "dve", X[0][:], pB[:], E1[:], ALU.mult, [K_("pB"), K_("E1")], [K_("X0")])
            for h in range(H):
                MM(kb, pC[:, h, :], kT[:, h, :], qT[:, h, :], True, True, [K_("kT"), K_("qT")], [K_("pC")], signal=(h == H - 1))
            TT(kb, "dve", attnT[:], pC[:], E2[:], ALU.mult, [K_("pC"), K_("E2")], [K_("attnT")])
            for h in range(H):
                TR(kb, pA[:, h, :], X[0][:, h, :], C["ident"][:], [K_("X0")], [K_("pA")], signal=(h == H - 1))
            CP(kb, "dve", Y[0][:], pA[:], [K_("pA")], [K_("Y0")])
            TT(kb, "dve", P[:], Y[0][:], bc_h(C["ident"][:]), ALU.add, [K_("Y0")], [K_("P")])
            if GDN_SUB <= 2:
                continue
            cur = 0
            for lvl in range(5):
                nxt = 1 - cur
                xk, yk = K_(f"X{cur}"), K_(f"Y{cur}")
                xn, yn = K_(f"X{nxt}"), K_(f"Y{nxt}")
                for h in range(H):
                    MM(kb, pB[:, h, :], Y[cur][:, h, :], X[cur][:, h, :], True, True, [xk, yk], [K_("pB")], signal=(h == H - 1))
                CP(kb, "dve", X[nxt][:], pB[:], [K_("pB")], [xn])
                if lvl < 4:
                    for h in range(H):
                        MM(kb, pC[:, h, :], X[cur][:, h, :], Y[cur][:, h, :], True, True, [xk, yk], [K_("pC")], signal=(h == H - 1))
                    CP(kb, "dve", Y[nxt][:], pC[:], [K_("pC")], [yn])
                for h in range(H):
                    MM(kb, pA[:, h, :], X[nxt][:, h, :], P[:, h, :], True, True, [xn, K_("P")], [K_("pA")], signal=(h == H - 1))
                TT(kb, "dve", P[:], P[:], pA[:], ALU.add, [K_("pA"), K_("P")], [K_("P")])
                cur = nxt
            if GDN_SUB <= 3:
                continue
            for h in range(H):
                TR(kb, pB[:, h, :], kT[:, h, :], C["ident"][:], [K_("kT")], [K_("pB")], signal=(h == H - 1))
            TT(kb, "dve", kbg[:], pB[:], bc_f(sm[:, 48:56]), ALU.mult, [K_("pB"), K_("sm")], [K_("kbg")])
            TS(kb, "dve", smk[:, 0:8], sm[:, 40:48], C["Bsel0"][:, 0:1], None, ALU.mult, None, [K_("sm")], [K_("smk")])
            TS(kb, "dve", smk[:, 8:16], sm[:, 40:48], C["Bsel1"][:, 0:1], None, ALU.mult, None, [K_("sm")], [K_("smk")])
            TT(kb, "dve", kd[:], pB[:], bc_f(smk[:, 0:8]), ALU.mult, [K_("pB"), K_("smk")], [K_("kd")])
            TT(kb, "dve", kd1[:], pB[:], bc_f(smk[:, 8:16]), ALU.mult, [K_("pB"), K_("smk")], [K_("kd")])
            for h in range(H):
                TR(kb, pC[:, h, :], vT[:, h, :], C["ident"][:], [K_("vT")], [K_("pC")], signal=(h == H - 1))
            TT(kb, "dve", vb[:], pC[:], bc_f(bn), ALU.mult, [K_("pC"), K_("beta")], [K_("vb")])
            for h in range(H):
                MM(kb, pA[:, h, :], P[:, h, :], vb[:, h, :], True, True, [K_("P"), K_("vb")], [K_("pA")], signal=(h == H - 1))
            CP(kb, "dve", u[:], pA[:], [K_("pA")], [K_("u")])
            for h in range(H):
                MM(kb, pB[:, h, :], kbg[:, h, :], P[:, h, :], True, True, [K_("P"), K_("kbg")], [K_("pB")], signal=(h == H - 1))
            CP(kb, "dve", wT[:], pB[:], [K_("pB")], [K_("wT")])
            TT(kb, "dve", qgT[:], qT[:], egrow[:], ALU.mult, [K_("qT"), K_("egrow")], [K_("qgT")])
            if GDN_STOP <= 1:
                continue
            for c in range(2):
                r0, r1 = c * 64, (c + 1) * 64
                for h in range(H):
                    MM(kb, pA[r0:r1, h, :], wT[:, h, r0:r1], Sst[:, h, :], True, True, [K_("wT"), K_("S")], [K_("pA")],
                       signal=(h == H - 1))
                if GDN_SUB == 10 or (GDN_SUB == 10.5 and c == 1):
                    continue
                TT(kb, "dve", vnew[r0:r1, :, :], u[r0:r1, :, :], pA[r0:r1, :, :], ALU.subtract, [K_("u"), K_("pA")],
                   [K_("vnew")])
                if GDN_SUB == 11:
                    continue
                for h in range(H):
                    MM(kb, pO[:, h, :], Sst[:, h, :], qgT[:, h, r0:r1], True, False, [K_("S"), K_("qgT")], [K_("pO")])
                    MM(kb, pO[:, h, :], vnew[r0:r1, h, :], attnT[r0:r1, h, r0:r1], False, True,
                       [K_("vnew"), K_("attnT")], [K_("pO")], signal=(h == H - 1))
                if GDN_SUB == 12:
                    continue
                CP(kb, "dve", oacc[:, :, r0:r1], pO[:], [K_("pO")], [K_("oacc")])
                if GDN_STOP <= 2:
                    continue
                for h in range(H):
                    MM(kb, pB[:, h, :], (kd, kd1)[c][:, h, :], vnew[:, h, :], True, True, [K_("kd"), K_("vnew")],
                       [K_("pB")], signal=(h == H - 1))
                TT(kb, "dve", Sst[:], Sst[:], bc_f(sm[:, 16 + 8 * c:24 + 8 * c]), ALU.mult, [K_("S"), K_("sm")],
                   [K_("S")])
                TT(kb, "dve", Sst[:], Sst[:], pB[:], ALU.add, [K_("S"), K_("pB")], [K_("S")])
            ACT(kb, osq[:], oacc[:], AF.Square, [K_("oacc")], [K_("osq")])
            for b in range(2):
                MM(kb, pC[:, 4 * b:4 * b + 4, :], C["ones_f"][:], osq[:, 4 * b:4 * b + 4, :], True, True, [K_("osq")],
                   [K_("pC")], signal=(b == 1))
            CP(kb, "dve", orn[:], pC[:], [K_("pC")], [K_("orn")])
            ACT(kb, orn[:], orn[:], AF.Sqrt, [K_("orn")], [K_("orn")], bias=C["eps"][:, 0:1], scale=1.0 / 128)
            RECIP(kb, orn[:], orn[:], [K_("orn")], [K_("orn")])
            STT(kb, oacc[:], oacc[:], C["onorm"][:, 0:1], orn[:], ALU.mult, ALU.mult, [K_("oacc"), K_("orn")],
                [K_("oacc")])
            ACT(kb, zb[:], zb[:], AF.Silu, [K_("zb")], [K_("zb")])
            TT(kb, "dve", yo[:], oacc[:], zb[:], ALU.mult, [K_("oacc"), K_("zb")], [K_("yo")])
            kb.dma("pool", chy, yv[:, :, c0:c1], yo[:], reads=[K_("yo")], writes=[("y0b", n)])
        kb.end_phase()

def load_consts(kb, nc, names_shapes):
    C = {}
    ch = kb.chan("const")
    for name, shape, dt in names_shapes:
        d = nc.dram_tensor(name, list(shape), dt, kind="ExternalInput").ap()
        if name == "cbase":
            t = kb.sb("c_" + name, shape, BF16)
            kb.dma("pool", ch, t[:], d, writes=[("const", name)])
        else:
            t = kb.sb("c_" + name, shape, dt)
            kb.dma("sp", ch, t[:], d, writes=[("const", name)])
        C[name] = t
    kb.end_phase()
    with ExitStack() as st:
        d = nc.dram_tensor("biasT", [128, 8, 2, 128], F32, kind="ExternalInput").ap()
        C["biasS"] = kb.sb("c_biasS", [128, 8, 2, 128], BF16)
        bt = kb.sb("biasT_tmp", [128, 8, 2, 128], F32, st)
        kb.dma("sp", ch, bt[:], d, writes=[("const", "biasT")])
        for h in range(8):
            TS(kb, "dve", C["biasS"][:, h, :, :], bt[:, h, :, :], C["cb"][:, h:h + 1], 128 ** 0.5,
               ALU.subtract, ALU.mult, [("const", "biasT")], [("const", "biasS")])
        kb.end_phase()
    return C


CONST_SPECS = [
    ("ident", (128, 128), F32),
    ("ones_f", (128, 128), F32),
    ("eps", (128, 1), F32),
    ("normwT", (128, 2, 16), F32),
    ("cbase", (128, 896), F32),
    ("cb", (128, 8), F32),
    ("qnormT", (128, 4), F32),
    ("kvnormT", (128, 2), F32),
    ("qgain", (128, 1), F32),
    ("kgain", (128, 1), F32),
    ("ikw", (128, 1), F32),
    ("ikb", (128, 1), F32),
]

CD_CONST_SPECS = [
    ("dww", (128, 8, 31), F32),
    ("dwb", (128, 8), F32),
    ("lnw", (128, 8), F32),
    ("lnb", (128, 8), F32),
    ("dcw", (128, 8, 3), F32),
]

GDN_CONST_SPECS = [
    ("cvw", (128, 24, 4), F32),
    ("alog_bc", (128, 8), F32),
    ("dtb_bc", (128, 8), F32),
    ("onorm", (128, 1), F32),
    ("U2", (128, 128), F32),
    ("Bsame", (128, 128), F32),
    ("Bsel0", (128, 128), F32),
    ("Bsel1", (128, 128), F32),
    ("MLs", (128, 128), F32),
    ("MU", (128, 128), F32),
]


def build_program(layers=(0, 1), l0_parts=("a", "b"), debug_out=False):
    nc = bass.Bass("TRN2", target_bir_lowering=False)

    def din(name, shape, dt=F32):
        return nc.dram_tensor(name, list(shape), dt, kind="ExternalInput").ap()

    def dscr(name, shape, dt=F32):
        return nc.dram_tensor(name, list(shape), dt, kind="Internal").ap()

    x = din("x", [T, D])
    out = nc.dram_tensor("out", [T, D], F32, kind="ExternalOutput").ap()
    kb = KB(nc)
    C = load_consts(kb, nc, CONST_SPECS)
    src = x
    if 0 in layers:
        ab_w_in = din("ab_w_in", [48, 128, 16 * 128])
        ab_w_out = din("ab_w_out", [16, 128, D])
        W = {"w_uqiq": din("w_uqiq", [16, 128, 4 * 128]), "w_uk": din("w_uk", [8, 128, 2 * 128]),
             "w_uv": din("w_uv", [2, 128, 1024])}
        pj0 = dscr("pj0", [6144, T])
        S = {"qT": dscr("s_qT", [8, 128, T], BF16), "qiT": dscr("s_qiT", [8, 128, T], BF16),
             "kT": dscr("s_kT", [8, 128, T], BF16), "V": dscr("s_V", [T, 1024], BF16),
             "kiT": dscr("s_kiT", [128, T], BF16), "tmaj": dscr("s_tmaj", [T, 32]),
             "gqkv": dscr("s_gqkv", [3072, T])}
        if debug_out:
            y0 = nc.dram_tensor("y0", [2048, T], BF16, kind="ExternalOutput").ap()
        else:
            y0 = dscr("y0", [2048, T], BF16)
        x1 = dscr("x1", [T, D]) if 1 in layers else out
        phase_inproj(kb, C, "l0", src, C["normwT"][:, 0, :], ab_w_in, 48, pj0)
        phase_ki_tmaj(kb, C, pj0, S)
        if "a" in l0_parts:
            phase_dsa_prep(kb, C, pj0, W, S)
            phase_dsa_attn(kb, C, pj0, S, y0)
        if "b" in l0_parts:
            phase_gdn(kb, C, pj0, S, y0)
        if not debug_out:
            phase_outproj(kb, C, "l0o", y0, ab_w_out, src, x1)
        src = x1
    if 1 in layers:
        cd_w_in = din("cd_w_in", [56, 128, 16 * 128])
        cd_w_out = din("cd_w_out", [16, 128, D])
        pj1 = dscr("pj1", [7168, T])
        y1 = dscr("y1", [2048, T], BF16)
        phase_inproj(kb, C, "l1", src, C["normwT"][:, 1, :], cd_w_in, 56, pj1)
        phase_cd_mix(kb, C, pj1, C, y1)
        phase_outproj(kb, C, "l1o", y1, cd_w_out, src, out)
    kb.finish()
    kb.emit()
    kb.close()
    return nc, kb


def tile_w_in(w, nch):
    K, N = w.shape
    assert N == nch * 128
    return np.ascontiguousarray(w.reshape(K // 128, 128, nch, 128).transpose(2, 1, 0, 3)).reshape(nch, 128, -1)


def t5_bucket_np(dist):
    import math
    max_exact = 16
    dd = np.maximum(dist, 1).astype(np.float32)
    large = max_exact + (np.log(dd / max_exact) / math.log(128 / max_exact) * (32 - max_exact)).astype(np.int32)
    large = np.minimum(large, 31)
    return np.where(dist < max_exact, dist, large)


def colT(v, k):
    return np.ascontiguousarray(np.asarray(v, np.float32).reshape(k, 128).T)


def host_consts(inp):
    f = np.float32
    c = {}
    c["ident"] = np.eye(128, dtype=f)
    c["ones_f"] = np.ones((128, 128), f)
    c["eps"] = np.full((128, 1), EPS, f)
    c["normwT"] = np.ascontiguousarray(inp["norm_w"].reshape(2, 16, 128).transpose(2, 0, 1)).astype(f)
    c["dww"] = np.ascontiguousarray(inp["c_dw_w"][0].reshape(31, 8, 128).transpose(2, 1, 0)).astype(f)
    c["dwb"] = colT(inp["c_dw_b"][0], 8)
    c["lnw"] = colT(inp["c_ln_w"][0], 8)
    c["lnb"] = colT(inp["c_ln_b"][0], 8)
    c["dcw"] = np.ascontiguousarray(inp["d_conv_w"][0].reshape(3, 8, 128).transpose(2, 1, 0)).astype(f)
    r = np.arange(128)[:, None]
    cc = np.arange(896)[None, :]
    c["cbase"] = np.where(cc <= r + 384, 0.0, -1e30).astype(f)
    kl = np.arange(128)[:, None, None]
    dd = np.arange(2)[None, :, None]
    ql = np.arange(128)[None, None, :]
    dist = np.maximum(dd * 128 + ql - kl, 0)
    bt = np.asarray(inp["rel_bias"], f)[t5_bucket_np(dist)]
    c["biasT"] = np.ascontiguousarray(bt.transpose(0, 3, 1, 2))
    c["cb"] = np.ascontiguousarray(np.broadcast_to(np.asarray(inp["rel_bias"], f)[31][None, :], (128, 8)))
    c["qnormT"] = colT(inp["a_q_norm"][0], 4)
    c["kvnormT"] = colT(inp["a_kv_norm"][0], 2)
    c["qgain"] = np.asarray(inp["a_q_gain"][0], f).reshape(128, 1).copy()
    c["kgain"] = np.asarray(inp["a_k_gain"][0], f).reshape(128, 1).copy()
    c["ikw"] = np.tile(np.asarray(inp["a_ik_norm_w"][0], f), 2).reshape(128, 1).copy()
    c["ikb"] = np.tile(np.asarray(inp["a_ik_norm_b"][0], f), 2).reshape(128, 1).copy()
    c["cvw"] = np.ascontiguousarray(np.asarray(inp["b_conv_w"][0], f).reshape(4, 24, 128).transpose(2, 1, 0))
    c["alog_bc"] = np.ascontiguousarray(np.broadcast_to(np.asarray(inp["b_a_log"][0], f)[None, :], (128, 8)))
    c["dtb_bc"] = np.ascontiguousarray(np.broadcast_to(np.asarray(inp["b_dt_bias"][0], f)[None, :], (128, 8)))
    c["onorm"] = np.asarray(inp["b_o_norm"][0], f).reshape(128, 1).copy()
    a = np.arange(128)
    same = (a[:, None] // 64) == (a[None, :] // 64)
    c["U2"] = (same & (a[:, None] <= a[None, :])).astype(f)
    c["Bsame"] = same.astype(f)
    c["Bsel0"] = np.ascontiguousarray(np.broadcast_to((a[:, None] < 64), (128, 128))).astype(f)
    c["Bsel1"] = np.ascontiguousarray(np.broadcast_to((a[:, None] >= 64), (128, 128))).astype(f)
    c["MLs"] = (same & (a[:, None] > a[None, :])).astype(f)
    c["MU"] = (same & (a[:, None] <= a[None, :])).astype(f)
    return c


def host_shared(inp, layers=(0, 1)):
    f = np.float32
    sh = host_consts(inp)
    if 0 in layers:
        w = np.asarray(inp["ab_w_in"][0], f)
        wp = np.zeros((D, 6144), f)
        wp[:, 0:832] = w[:, 0:832]
        wp[:, 896:912] = w[:, 832:848]
        wp[:, 912:928] = w[:, 4944:4960]
        wp[:, 1024:2048] = w[:, 848:1872]
        wp[:, 2048:5120] = w[:, 1872:4944]
        wp[:, 5120:6144] = w[:, 4960:5984]
        sh["ab_w_in"] = tile_w_in(wp, 48)
        sh["ab_w_out"] = np.ascontiguousarray(np.asarray(inp["ab_w_out"][0], f).reshape(16, 128, D))
        sh["w_uqiq"] = tile_w_in(np.concatenate([inp["a_w_uq"][0], inp["a_w_iq"][0]], axis=1).astype(f), 16)
        sh["w_uk"] = tile_w_in(np.asarray(inp["a_w_uk"][0], f), 8)
        sh["w_uv"] = np.ascontiguousarray(np.asarray(inp["a_w_uv"][0], f).reshape(2, 128, 1024))
    if 1 in layers:
        sh["cd_w_in"] = tile_w_in(np.asarray(inp["cd_w_in"][0], f), 56)
        sh["cd_w_out"] = np.ascontiguousarray(np.asarray(inp["cd_w_out"][0], f).reshape(16, 128, D))
    return sh


def kernel(**inputs):
    inp = {k: np.asarray(v) for k, v in inputs.items()}
    nc, kb = build_program()
    sh = host_shared(inp)
    x = np.ascontiguousarray(inp["x"], dtype=np.float32)
    in_maps = [dict(sh, x=x[b]) for b in range(8)]
    res = run_bass_kernel_spmd(nc, in_maps, core_ids=list(range(8)))
    return np.stack([np.asarray(r["out"], np.float32) for r in res.results], axis=0)
```

```python
from contextlib import ExitStack
import numpy as np
import concourse.bass as bass
import concourse.mybir as mybir
from concourse.bass_utils import run_bass_kernel_spmd

F32 = mybir.dt.float32
BF16 = mybir.dt.bfloat16
ALU = mybir.AluOpType
AF = mybir.ActivationFunctionType
AX = mybir.AxisListType

T = 4096
D = 2048
NT = T // 128
EPS = 1e-6
ENGS = ("pe", "act", "dve", "pool", "sp")


class Chan:
    def __init__(self, sem, name):
        self.sem = sem
        self.name = name
        self.n = 0


class KB:
    def __init__(self, nc):
        self.nc = nc
        self.es = ExitStack()
        self.q = {e: [] for e in ENGS}
        self.sems = {}
        self.cnt = {}
        self.seen = {e: {} for e in ENGS}
        self.lastw = {}
        self.readers = {}
        self.chans = []
        self.chan_by_sem = {}
        self.nins = 0
        self.pending = {e: False for e in ENGS}
        for e in ENGS:
            self.sems[e] = self.es.enter_context(nc.semaphore("s_" + e))
            self.cnt[e] = 0

    def sb(self, name, shape, dt, stack=None):
        return (stack or self.es).enter_context(self.nc.sbuf_tensor(name, list(shape), dt))

    def ps(self, name, shape, dt=F32, stack=None):
        return (stack or self.es).enter_context(self.nc.psum_tensor(name, list(shape), dt))

    def chan(self, name):
        c = Chan(self.es.enter_context(self.nc.semaphore("c_" + name)), name)
        self.chans.append(c)
        self.chan_by_sem[id(c.sem)] = c
        return c

    def _need0(self, eng, sem, val):
        ch = self.chan_by_sem.get(id(sem))
        if ch is not None:
            val = max(val, 16 * ch.n)
        cur = self.seen[eng].get(id(sem), 0)
        if val > cur:
            self.seen[eng][id(sem)] = val
            self.q[eng].append(("wait", sem, val))

    def _deps(self, eng, reads, writes, my_sem):
        for r in reads:
            ev = self.lastw.get(r)
            if ev is not None:
                self._need(eng, ev[0], ev[1])
        for w in writes:
            ev = self.lastw.get(w)
            if ev is not None:
                self._need(eng, ev[0], ev[1])
            rd = self.readers.get(w)
            if rd:
                for sem, val in rd.values():
                    if sem is my_sem:
                        continue
                    self._need(eng, sem, val)

    def _need(self, eng, sem, val):
        if eng == "pe" and sem is self.sems["pe"]:
            return
        self._need0(eng, sem, val)

    def _commit(self, ev, reads, writes):
        for w in writes:
            self.lastw[w] = ev
            self.readers[w] = {}
        for r in reads:
            d = self.readers.setdefault(r, {})
            d[id(ev[0])] = ev

    def op(self, eng, fn, reads=(), writes=(), signal=True):
        sem = self.sems[eng]
        self._deps(eng, reads, writes, sem)
        if signal:
            self.cnt[eng] += 1
            self.pending[eng] = False
            ev = (sem, self.cnt[eng])
            self.q[eng].append(("ins", fn, sem, 1))
        else:
            self.pending[eng] = True
            ev = (sem, self.cnt[eng] + 1)
            self.q[eng].append(("ins0", fn))
        self._commit(ev, reads, writes)
        self.nins += 1
        return ev

    def dma(self, eng, ch, out, in_, reads=(), writes=(), **kw):
        self._deps(eng, reads, writes, None)
        ch.n += 1
        ev = (ch.sem, 16 * ch.n)
        self.q[eng].append(("ins", lambda e, o=out, i=in_, k=kw: e.dma_start(out=o, in_=i, **k), ch.sem, 16))
        self._commit(ev, reads, writes)
        self.nins += 1
        return ev

    def _flush_pending(self):
        for e in ENGS:
            assert not self.pending[e], "non-signaling op left pending at a barrier on " + e

    def _all_events(self):
        self._flush_pending()
        evs = [(self.sems[e], self.cnt[e]) for e in ENGS if self.cnt[e] > 0]
        evs += [(c.sem, 16 * c.n) for c in self.chans if c.n > 0]
        return evs

    def barrier(self):
        evs = self._all_events()
        for e in ENGS:
            for sem, val in evs:
                self._need(e, sem, val)

    def finish(self, final_eng="sp"):
        for sem, val in self._all_events():
            self._need(final_eng, sem, val)

    def emit(self):
        nc = self.nc
        q = self.q
        self.q = {e: [] for e in ENGS}

        def replay(eng_obj, items):
            for it in items:
                if it[0] == "wait":
                    eng_obj.wait_ge(it[1], it[2])
                elif it[0] == "ins0":
                    it[1](eng_obj)
                else:
                    it[1](eng_obj).then_inc(it[2], it[3])

        with nc.Block() as block:
            @block.tensor
            def _(e):
                replay(e, q["pe"])

            @block.scalar
            def _(e):
                replay(e, q["act"])

            @block.vector
            def _(e):
                replay(e, q["dve"])

            @block.gpsimd
            def _(e):
                replay(e, q["pool"])

            @block.sync
            def _(e):
                replay(e, q["sp"])

    def end_phase(self):
        self.barrier()
        self.emit()

    def close(self):
        self.es.close()


def MM(kb, out, lhsT, rhs, start, stop, reads, writes, signal=None):
    if signal is None:
        signal = stop
    return kb.op("pe", lambda e: e.matmul(out, lhsT, rhs, start=start, stop=stop), reads, writes, signal=signal)


def TR(kb, out, in_, ident, reads, writes, signal=True):
    return kb.op("pe", lambda e: e.transpose(out, in_, ident), reads, writes, signal=signal)


def ACT(kb, out, in_, func, reads, writes, bias=None, scale=None, accum_out=None):
    kw = {}
    if bias is not None:
        kw["bias"] = bias
    if scale is not None:
        kw["scale"] = scale
    if accum_out is not None:
        kw["accum_out"] = accum_out
    return kb.op("act", lambda e: e.activation(out=out, in_=in_, func=func, **kw), reads, writes)


def TS(kb, eng, out, in0, s1, s2, op0, op1, reads, writes, accum_out=None):
    kw = {}
    if op1 is not None:
        kw["op1"] = op1
    if accum_out is not None:
        kw["accum_out"] = accum_out
    return kb.op(eng, lambda e: e.tensor_scalar(out=out, in0=in0, scalar1=s1, scalar2=s2, op0=op0, **kw),
                 reads, writes)


def TT(kb, eng, out, in0, in1, op, reads, writes):
    return kb.op(eng, lambda e: e.tensor_tensor(out=out, in0=in0, in1=in1, op=op), reads, writes)


def STT(kb, out, in0, scalar, in1, op0, op1, reads, writes):
    return kb.op("dve", lambda e: e.scalar_tensor_tensor(out=out, in0=in0, scalar=scalar, in1=in1,
                                                         op0=op0, op1=op1), reads, writes)


def CP(kb, eng, out, in_, reads, writes):
    if eng == "act":
        return kb.op("act", lambda e: e.copy(out=out, in_=in_), reads, writes)
    return kb.op(eng, lambda e: e.tensor_copy(out=out, in_=in_), reads, writes)


def MS(kb, eng, ap, val, writes):
    return kb.op(eng, lambda e: e.memset(ap, val), (), writes)


def RECIP(kb, out, in_, reads, writes):
    return kb.op("dve", lambda e: e.reciprocal(out=out, in_=in_), reads, writes)


def phase_norm_T(kb, C, x_dram, normwT, hT, tag):
    with ExitStack() as st:
        xt = [kb.sb(f"{tag}_xt{i}", [128, D], F32, st) for i in range(2)]
        xn = [kb.sb(f"{tag}_xn{i}", [128, D], F32, st) for i in range(2)]
        sq = kb.sb(f"{tag}_sq", [128, D], BF16, st)
        sm = [kb.sb(f"{tag}_sm{i}", [128, 4], F32, st) for i in range(2)]
        pst = [kb.ps(f"{tag}_pt{i}", [128, 8, 128], F32, st) for i in range(2)]
        ch = [kb.chan(f"{tag}_x{i}") for i in range(2)]
        for tt in range(NT):
            s = tt % 2
            kb.dma("sp", ch[s], xt[s][:], x_dram[tt * 128:(tt + 1) * 128, :], writes=[(tag, "xt", s)])
            ACT(kb, sq[:], xt[s][:], AF.Square, [(tag, "xt", s)], [(tag, "sq"), (tag, "ss", s)],
                accum_out=sm[s][:, 0:1])
            ACT(kb, sm[s][:, 1:2], sm[s][:, 0:1], AF.Sqrt, [(tag, "ss", s)], [(tag, "sd", s)],
                bias=C["eps"][:, 0:1], scale=1.0 / D)
            RECIP(kb, sm[s][:, 2:3], sm[s][:, 1:2], [(tag, "sd", s)], [(tag, "rs", s)])
            TS(kb, "dve", xn[s][:], xt[s][:], sm[s][:, 2:3], None, ALU.mult, None,
               [(tag, "xt", s), (tag, "rs", s)], [(tag, "xn", s)])
            for half in range(2):
                p = half
                for kk in range(8):
                    k = half * 8 + kk
                    TR(kb, pst[p][:, kk, :], xn[s][:, k * 128:(k + 1) * 128], C["ident"][:],
                       [(tag, "xn", s)], [(tag, "pt", p)], signal=(kk == 7))
                nw = normwT[:, half * 8:(half + 1) * 8].unsqueeze(2).to_broadcast([128, 8, 128])
                TT(kb, "dve", hT[:, half * 8:(half + 1) * 8, tt * 128:(tt + 1) * 128], pst[p][:], nw,
                   ALU.mult, [(tag, "pt", p)], [("hT", tt)])
        kb.end_phase()


def gemm_fm(kb, tag, actT, act_keys, KC, w_dram, NCH, sink):
    with ExitStack() as st:
        wb = [kb.sb(f"{tag}_wb{i}", [128, KC * 128], BF16, st) for i in range(2)]
        wch = [kb.chan(f"{tag}_w{i}") for i in range(2)]
        pss = [kb.ps(f"{tag}_ps{i}", [128, 1024], F32, st) for i in range(2)]
        it = 0
        for c in range(NCH):
            s = c % 2
            kb.dma("pool", wch[s], wb[s][:], w_dram[c], writes=[(tag, "wb", s)])
            for ts in range(4):
                p = it % 2
                it += 1
                for b in range(2):
                    t0 = ts * 1024 + b * 512
                    for k in range(KC):
                        MM(kb, pss[p][:, b * 512:(b + 1) * 512], wb[s][:, k * 128:(k + 1) * 128],
                           actT[:, k, t0:t0 + 512], k == 0, k == KC - 1,
                           [(tag, "wb", s)] + act_keys, [(tag, "ps", p)])
                sink(c, ts, pss[p], (tag, "ps", p), st)
        kb.end_phase()


class StoreSink:
    def __init__(self, kb, tag, dst, st):
        self.kb = kb
        self.tag = tag
        self.dst = dst
        self.stg = [kb.sb(f"{tag}_stg{i}", [128, 1024], F32, st) for i in range(3)]
        self.ch = [kb.chan(f"{tag}_st{i}") for i in range(3)]
        self.i = 0

    def __call__(self, c, ts, ps, pkey, st):
        kb = self.kb
        s = self.i % 3
        eng = "act" if self.i % 2 == 0 else "dve"
        self.i += 1
        CP(kb, eng, self.stg[s][:], ps[:], [pkey], [(self.tag, "stg", s)])
        kb.dma("sp", self.ch[s], self.dst[c * 128:(c + 1) * 128, ts * 1024:(ts + 1) * 1024], self.stg[s][:],
               reads=[(self.tag, "stg", s)], writes=[(self.tag, "dst", c)])


def phase_inproj(kb, C, tag, x_dram, normwT, w_dram, NCH, pj):
    with ExitStack() as st:
        hT = kb.sb(f"{tag}_hT", [128, 16, T], BF16, st)
        phase_norm_T(kb, C, x_dram, normwT, hT, tag + "n")
        with ExitStack() as st2:
            sink = StoreSink(kb, tag + "s", pj, st2)
            gemm_fm(kb, tag + "g", hT, [], 16, w_dram, NCH, sink)


def phase_outproj(kb, C, tag, yT_dram, wo_dram, xres_dram, out_dram):
    with ExitStack() as st:
        wo = kb.sb(f"{tag}_wo", [128, 16, D], BF16, st)
        wch = kb.chan(f"{tag}_w")
        for k in range(16):
            kb.dma("pool", wch, wo[:, k, :], wo_dram[k], writes=[(tag, "wo")])
        yt = [kb.sb(f"{tag}_yt{i}", [128, 16, 128], BF16, st) for i in range(2)]
        xr = [kb.sb(f"{tag}_xr{i}", [128, D], F32, st) for i in range(2)]
        ot = [kb.sb(f"{tag}_ot{i}", [128, D], F32, st) for i in range(2)]
        ps = [kb.ps(f"{tag}_ps{i}", [128, 512], F32, st) for i in range(4)]
        chy = [kb.chan(f"{tag}_y{i}") for i in range(2)]
        chx = [kb.chan(f"{tag}_x{i}") for i in range(2)]
        cho = [kb.chan(f"{tag}_o{i}") for i in range(2)]
        yv = yT_dram.rearrange("(k p) t -> p k t", p=128)
        for tt in range(NT):
            s = tt % 2
            kb.dma("sp", chy[s], yt[s][:], yv[:, :, tt * 128:(tt + 1) * 128], writes=[(tag, "yt", s)])
            kb.dma("sp", chx[s], xr[s][:], xres_dram[tt * 128:(tt + 1) * 128, :], writes=[(tag, "xr", s)])
            for nb in range(4):
                for k in range(16):
                    MM(kb, ps[nb][:], yt[s][:, k, :], wo[:, k, nb * 512:(nb + 1) * 512], k == 0, k == 15,
                       [(tag, "yt", s), (tag, "wo")], [(tag, "ps", nb)])
                TT(kb, "dve", ot[s][:, nb * 512:(nb + 1) * 512], ps[nb][:], xr[s][:, nb * 512:(nb + 1) * 512],
                   ALU.add, [(tag, "ps", nb), (tag, "xr", s)], [(tag, "ot", s)])
            kb.dma("pool", cho[s], out_dram[tt * 128:(tt + 1) * 128, :], ot[s][:],
                   reads=[(tag, "ot", s)], writes=[(tag, "out", tt)])
        kb.end_phase()


def phase_cd_mix(kb, C, pj, cw_unused, y_dram):
    with ExitStack() as stc:
        cw = {}
        chc = kb.chan("cd_const")
        for name, shape, dt in CD_CONST_SPECS:
            d = kb.nc.dram_tensor(name, list(shape), dt, kind="ExternalInput").ap()
            t = kb.sb("c_" + name, shape, dt, stc)
            kb.dma("sp", chc, t[:], d, writes=[("const", name)])
            cw[name] = t
        kb.end_phase()
        _phase_cd_mix(kb, C, pj, cw, y_dram)


def _phase_cd_mix(kb, C, pj, cw, y_dram):
    HB = 2048
    tag = "cd"
    with ExitStack() as st:
        QB = 1024
        ident_b = kb.sb("cd_identb", [128, 128], BF16, st)
        dg = kb.sb("cd_dg", [128, 8, 31, 128], BF16, st)
        uc = kb.sb("cd_uc", [128, 8, QB], F32, st)
        ab = [kb.sb(f"cd_a{i}", [128, 30 + QB], F32, st) for i in range(2)]
        gb = [kb.sb(f"cd_g{i}", [128, 30 + QB], F32, st) for i in range(2)]
        ub = [kb.sb(f"cd_u{i}", [128, 30 + QB], BF16, st) for i in range(2)]
        sq = [kb.sb(f"cd_sq{i}", [128, QB], F32, st) for i in range(2)]
        mean = kb.sb("cd_mean", [128, QB], F32, st)
        rstd = kb.sb("cd_rstd", [128, QB], F32, st)
        m2 = kb.sb("cd_m2", [128, QB], F32, st)
        zb = [kb.sb(f"cd_z{i}", [128, QB], F32, st) for i in range(2)]
        yb = [kb.sb(f"cd_y{i}", [128, QB], BF16, st) for i in range(2)]
        pc = [kb.ps(f"cd_pc{i}", [128, QB], F32, st) for i in range(2)]
        ps_sum = kb.ps("cd_pss", [128, QB], F32, st)
        ps_ssq = kb.ps("cd_psq", [128, QB], F32, st)
        cha = [kb.chan(f"cd_a{i}") for i in range(2)]
        chg = [kb.chan(f"cd_g{i}") for i in range(2)]
        chz = [kb.chan(f"cd_z{i}") for i in range(2)]
        chy = [kb.chan(f"cd_y{i}") for i in range(2)]
        CP(kb, "dve", ident_b[:], C["ident"][:], [], [(tag, "cst")])
        for cc in range(8):
            TT(kb, "dve", dg[:, cc, :, :], ident_b[:].unsqueeze(1).to_broadcast([128, 31, 128]),
               cw["dww"][:, cc, :].unsqueeze(2).to_broadcast([128, 31, 128]), ALU.mult, [(tag, "cst")], [(tag, "dg")])
        it = 0
        for tq in range(T // QB):
            t0 = tq * QB
            for cc in range(8):
                s = it % 2
                it += 1
                r0 = cc * 128
                if tq == 0:
                    MS(kb, "pool", ab[s][:, 0:30], 0.0, [(tag, "a", s)])
                    MS(kb, "pool", gb[s][:, 0:30], 0.0, [(tag, "g", s)])
                    kb.dma("sp", cha[s], ab[s][:, 30:30 + QB], pj[r0:r0 + 128, 0:QB], writes=[(tag, "a", s)])
                    kb.dma("sp", chg[s], gb[s][:, 30:30 + QB], pj[1024 + r0:1024 + r0 + 128, 0:QB],
                           writes=[(tag, "g", s)])
                else:
                    kb.dma("sp", cha[s], ab[s][:], pj[r0:r0 + 128, t0 - 30:t0 + QB], writes=[(tag, "a", s)])
                    kb.dma("sp", chg[s], gb[s][:], pj[1024 + r0:1024 + r0 + 128, t0 - 30:t0 + QB],
                           writes=[(tag, "g", s)])
                ACT(kb, gb[s][:], gb[s][:], AF.Sigmoid, [(tag, "g", s)], [(tag, "g", s)])
                TT(kb, "dve", ub[s][:], ab[s][:], gb[s][:], ALU.mult, [(tag, "a", s), (tag, "g", s)], [(tag, "u", s)])
                for b in range(QB // 512):
                    for j in range(31):
                        MM(kb, pc[s][:, b * 512:(b + 1) * 512], dg[:, cc, j, :], ub[s][:, j + b * 512:j + b * 512 + 512],
                           j == 0, j == 30, [(tag, "dg"), (tag, "u", s)], [(tag, "pc", s)])
                ACT(kb, uc[:, cc, :], pc[s][:], AF.Identity, [(tag, "pc", s)], [(tag, "uc", cc)],
                    bias=cw["dwb"][:, cc:cc + 1])
                ACT(kb, sq[s][:], uc[:, cc, :], AF.Square, [(tag, "uc", cc)], [(tag, "sq", s)])
                for tb in range(QB // 512):
                    last = tb == QB // 512 - 1
                    MM(kb, ps_sum[:, tb * 512:(tb + 1) * 512], C["ones_f"][:], uc[:, cc, tb * 512:(tb + 1) * 512],
                       cc == 0, cc == 7, [(tag, "uc", cc)], [(tag, "pss")], signal=last)
                    MM(kb, ps_ssq[:, tb * 512:(tb + 1) * 512], C["ones_f"][:], sq[s][:, tb * 512:(tb + 1) * 512],
                       cc == 0, cc == 7, [(tag, "sq", s)], [(tag, "psq")], signal=last)
            ACT(kb, mean[:], ps_sum[:], AF.Copy, [(tag, "pss")], [(tag, "mean")], scale=1.0 / 1024)
            TT(kb, "dve", m2[:], mean[:], mean[:], ALU.mult, [(tag, "mean")], [(tag, "m2")])
            STT(kb, m2[:], ps_ssq[:], 1.0 / 1024, m2[:], ALU.mult, ALU.subtract, [(tag, "psq"), (tag, "m2")],
                [(tag, "m2")])
            ACT(kb, m2[:], m2[:], AF.Sqrt, [(tag, "m2")], [(tag, "m2")], bias=C["eps"][:, 0:1])
            RECIP(kb, rstd[:], m2[:], [(tag, "m2")], [(tag, "rstd")])
            for cc in range(8):
                s = cc % 2
                r0 = cc * 128
                kb.dma("sp", chz[s], zb[s][:], pj[2048 + r0:2048 + r0 + 128, t0:t0 + QB], writes=[(tag, "z", s)])
                TT(kb, "dve", uc[:, cc, :], uc[:, cc, :], mean[:], ALU.subtract, [(tag, "uc", cc), (tag, "mean")],
                   [(tag, "uc", cc)])
                TT(kb, "dve", uc[:, cc, :], uc[:, cc, :], rstd[:], ALU.mult, [(tag, "uc", cc), (tag, "rstd")],
                   [(tag, "uc", cc)])
                ACT(kb, uc[:, cc, :], uc[:, cc, :], AF.Silu, [(tag, "uc", cc)], [(tag, "uc", cc)],
                    scale=cw["lnw"][:, cc:cc + 1], bias=cw["lnb"][:, cc:cc + 1])
                ACT(kb, zb[s][:], zb[s][:], AF.Silu, [(tag, "z", s)], [(tag, "z", s)])
                TT(kb, "dve", yb[s][:], uc[:, cc, :], zb[s][:], ALU.mult, [(tag, "uc", cc), (tag, "z", s)],
                   [(tag, "y", s)])
                kb.dma("pool", chy[s], y_dram[r0:r0 + 128, t0:t0 + QB], yb[s][:], reads=[(tag, "y", s)],
                       writes=[(tag, "yd", cc, tq)])
        kb.end_phase()
    tag = "sc"
    with ExitStack() as st:
        W = T + 2
        bg = [kb.sb(f"sc_b{i}", [128, T], F32, st) for i in range(2)]
        cg = [kb.sb(f"sc_c{i}", [128, W], F32, st) for i in range(2)]
        ud = [kb.sb(f"sc_u{i}", [128, W], F32, st) for i in range(2)]
        zd = [kb.sb(f"sc_z{i}", [128, T], F32, st) for i in range(2)]
        acc = [kb.sb(f"sc_acc{i}", [128, T], F32, st) for i in range(2)]
        yb = [kb.sb(f"sc_y{i}", [128, T], BF16, st) for i in range(2)]
        chs = {n: [kb.chan(f"sc_{n}{i}") for i in range(2)] for n in ("b", "c", "u", "z", "y")}
        for cc in range(8):
            s = cc % 2
            r0 = cc * 128
            MS(kb, "pool", cg[s][:, 0:2], 0.0, [(tag, "c", s)])
            MS(kb, "pool", ud[s][:, 0:2], 0.0, [(tag, "u", s)])
            kb.dma("sp", chs["b"][s], bg[s][:], pj[3072 + r0:3072 + r0 + 128, :], writes=[(tag, "b", s)])
            kb.dma("sp", chs["c"][s], cg[s][:, 2:W], pj[4096 + r0:4096 + r0 + 128, :], writes=[(tag, "c", s)])
            kb.dma("sp", chs["u"][s], ud[s][:, 2:W], pj[5120 + r0:5120 + r0 + 128, :], writes=[(tag, "u", s)])
            kb.dma("sp", chs["z"][s], zd[s][:], pj[6144 + r0:6144 + r0 + 128, :], writes=[(tag, "z", s)])
            TT(kb, "dve", cg[s][:], cg[s][:], ud[s][:], ALU.mult, [(tag, "c", s), (tag, "u", s)], [(tag, "c", s)])
            TS(kb, "dve", acc[s][:], cg[s][:, 2:W], cw["dcw"][:, cc, 2:3], None, ALU.mult, None,
               [(tag, "c", s)], [(tag, "acc", s)])
            for j in range(2):
                STT(kb, acc[s][:], cg[s][:, j:j + T], cw["dcw"][:, cc, j:j + 1], acc[s][:], ALU.mult, ALU.add,
                    [(tag, "c", s), (tag, "acc", s)], [(tag, "acc", s)])
            ACT(kb, zd[s][:], zd[s][:], AF.Silu, [(tag, "z", s)], [(tag, "z", s)])
            TT(kb, "pool", bg[s][:], bg[s][:], zd[s][:], ALU.mult, [(tag, "b", s), (tag, "z", s)], [(tag, "b", s)])
            TT(kb, "dve", yb[s][:], acc[s][:], bg[s][:], ALU.mult, [(tag, "acc", s), (tag, "b", s)], [(tag, "y", s)])
            kb.dma("pool", chs["y"][s], y_dram[1024 + r0:1024 + r0 + 128, :], yb[s][:], reads=[(tag, "y", s)],
                   writes=[(tag, "yd", cc)])
        kb.end_phase()


def colnorm_phase(kb, C, tag, src, KC, gT, outT):
    with ExitStack() as st:
        raw = [kb.sb(f"{tag}_raw{i}", [128, KC, 1024], F32, st) for i in range(2)]
        sq = [kb.sb(f"{tag}_sq{i}", [128, 1024], F32, st) for i in range(2)]
        rs = kb.sb(f"{tag}_rs", [128, 1024], F32, st)
        ssp = kb.ps(f"{tag}_ssp", [128, 1024], F32, st)
        ch = [kb.chan(f"{tag}_l{i}") for i in range(2)]
        sv = src.rearrange("(k p) t -> p k t", p=128)
        for ts in range(4):
            s = ts % 2
            kb.dma("sp", ch[s], raw[s][:], sv[:, :, ts * 1024:(ts + 1) * 1024], writes=[(tag, "raw", s)])
            for k in range(KC):
                q = k % 2
                ACT(kb, sq[q][:], raw[s][:, k, :], AF.Square, [(tag, "raw", s)], [(tag, "sq", q)])
                for b in range(2):
                    MM(kb, ssp[:, b * 512:(b + 1) * 512], C["ones_f"][:], sq[q][:, b * 512:(b + 1) * 512],
                       k == 0, k == KC - 1, [(tag, "sq", q)], [(tag, "ssp")], signal=True)
            ACT(kb, rs[:], ssp[:], AF.Sqrt, [(tag, "ssp")], [(tag, "rs")], bias=C["eps"][:, 0:1],
                scale=1.0 / (KC * 128))
            RECIP(kb, rs[:], rs[:], [(tag, "rs")], [(tag, "rs")])
            for k in range(KC):
                STT(kb, outT[:, k, ts * 1024:(ts + 1) * 1024], raw[s][:, k, :], gT[:, k:k + 1], rs[:],
                    ALU.mult, ALU.mult, [(tag, "raw", s), (tag, "rs")], [(tag, "out", k, ts)])
        kb.end_phase()


class HeadNormSink:
    def __init__(self, kb, C, tag, dst, gain, n_norm, raw_dst, st):
        self.kb, self.C, self.tag, self.dst, self.gain = kb, C, tag, dst, gain
        self.n_norm, self.raw_dst = n_norm, raw_dst
        self.sq = [kb.sb(f"{tag}_sq{i}", [128, 1024], F32, st) for i in range(2)]
        self.rs = [kb.sb(f"{tag}_rs{i}", [128, 1024], F32, st) for i in range(2)]
        self.ob = [kb.sb(f"{tag}_ob{i}", [128, 1024], BF16, st) for i in range(2)]
        self.ssp = kb.ps(f"{tag}_ssp", [128, 1024], F32, st)
        self.ch = [kb.chan(f"{tag}_o{i}") for i in range(2)]
        self.i = 0

    def __call__(self, c, ts, ps, pkey, st):
        kb, C, tag = self.kb, self.C, self.tag
        s = self.i % 2
        self.i += 1
        if c < self.n_norm:
            ACT(kb, self.sq[s][:], ps[:], AF.Square, [pkey], [(tag, "sq", s)])
            for b in range(2):
                MM(kb, self.ssp[:, b * 512:(b + 1) * 512], C["ones_f"][:], self.sq[s][:, b * 512:(b + 1) * 512],
                   True, True, [(tag, "sq", s)], [(tag, "ssp")])
            ACT(kb, self.rs[s][:], self.ssp[:], AF.Sqrt, [(tag, "ssp")], [(tag, "rs", s)], bias=C["eps"][:, 0:1],
                scale=1.0 / 128)
            RECIP(kb, self.rs[s][:], self.rs[s][:], [(tag, "rs", s)], [(tag, "rs", s)])
            STT(kb, self.ob[s][:], ps[:], self.gain, self.rs[s][:], ALU.mult, ALU.mult,
                [pkey, (tag, "rs", s)], [(tag, "ob", s)])
            d = self.dst[c]
        else:
            CP(kb, "act", self.ob[s][:], ps[:], [pkey], [(tag, "ob", s)])
            d = self.raw_dst[c - self.n_norm]
        kb.dma("sp", self.ch[s], d[:, ts * 1024:(ts + 1) * 1024], self.ob[s][:], reads=[(tag, "ob", s)],
               writes=[(tag, "dst", c)])


def phase_dsa_prep(kb, C, pj, W, S):
    with ExitStack() as st:
        cqn = kb.sb("cqn", [128, 4, T], BF16, st)
        colnorm_phase(kb, C, "cq", pj[0:512, :], 4, C["qnormT"], cqn)
        with ExitStack() as st2:
            sink = HeadNormSink(kb, C, "qs", S["qT"], C["qgain"][:, 0:1], 8, S["qiT"], st2)
            gemm_fm(kb, "qg", cqn, [], 4, W["w_uqiq"], 16, sink)
    with ExitStack() as st:
        ckvn = kb.sb("ckvn", [128, 2, T], BF16, st)
        colnorm_phase(kb, C, "ckv", pj[512:768, :], 2, C["kvnormT"], ckvn)
        with ExitStack() as st2:
            sink = HeadNormSink(kb, C, "ks", S["kT"], C["kgain"][:, 0:1], 8, None, st2)
            gemm_fm(kb, "kg", ckvn, [], 2, W["w_uk"], 8, sink)
        with ExitStack() as st2:
            wv = kb.sb("wv", [128, 2, 1024], BF16, st2)
            chw = kb.chan("wv")
            for k in range(2):
                kb.dma("pool", chw, wv[:, k, :], W["w_uv"][k], writes=[("wv",)])
            vps = [kb.ps(f"v_ps{i}", [128, 1024], F32, st2) for i in range(2)]
            vb = [kb.sb(f"v_b{i}", [128, 1024], BF16, st2) for i in range(2)]
            chv = [kb.chan(f"v_o{i}") for i in range(2)]
            for tt in range(NT):
                s = tt % 2
                for b in range(2):
                    for k in range(2):
                        MM(kb, vps[s][:, b * 512:(b + 1) * 512], ckvn[:, k, tt * 128:(tt + 1) * 128],
                           wv[:, k, b * 512:(b + 1) * 512], k == 0, k == 1, [("wv",)], [("v", "ps", s)])
                CP(kb, "act" if tt % 2 else "dve", vb[s][:], vps[s][:], [("v", "ps", s)], [("v", "b", s)])
                kb.dma("sp", chv[s], S["V"][tt * 128:(tt + 1) * 128, :], vb[s][:], reads=[("v", "b", s)],
                       writes=[("V", tt)])
            kb.end_phase()


def phase_ki_tmaj(kb, C, pj, S):
    with ExitStack() as st:
        ki = kb.sb("ki_raw", [128, T], F32, st)
        sq = kb.sb("ki_sq", [128, T], F32, st)
        mean = kb.sb("ki_mean", [128, 1024], F32, st)
        var = kb.sb("ki_var", [128, 1024], F32, st)
        kio = kb.sb("ki_o", [128, T], BF16, st)
        sm = kb.sb("ki_sm", [32, T], F32, st)
        tmo = kb.sb("ki_tmo", [128, NT, 32], F32, st)
        ps1 = kb.ps("ki_ps1", [128, 1024], F32, st)
        ps2 = kb.ps("ki_ps2", [128, 1024], F32, st)
        pst = kb.ps("ki_pst", [128, 16, 32], F32, st)
        ch = kb.chan("ki")
        kb.dma("sp", ch, ki[0:64, :], pj[768:832, :], writes=[("ki", "raw")])
        kb.dma("sp", ch, ki[64:128, :], pj[768:832, :], writes=[("ki", "raw")])
        kb.dma("sp", ch, sm[:], pj[896:928, :], writes=[("ki", "sm")])
        ACT(kb, sq[:], ki[:], AF.Square, [("ki", "raw")], [("ki", "sq")])
        for ts in range(4):
            for b in range(2):
                c0 = ts * 1024 + b * 512
                MM(kb, ps1[:, b * 512:(b + 1) * 512], C["ones_f"][0:64, :], ki[0:64, c0:c0 + 512], True, True,
                   [("ki", "raw")], [("ki", "ps1")])
                MM(kb, ps2[:, b * 512:(b + 1) * 512], C["ones_f"][0:64, :], sq[0:64, c0:c0 + 512], True, True,
                   [("ki", "sq")], [("ki", "ps2")])
            ACT(kb, mean[:], ps1[:], AF.Copy, [("ki", "ps1")], [("ki", "mean")], scale=1.0 / 64)
            TT(kb, "dve", var[:], mean[:], mean[:], ALU.mult, [("ki", "mean")], [("ki", "var")])
            STT(kb, var[:], ps2[:], 1.0 / 64, var[:], ALU.mult, ALU.subtract, [("ki", "ps2"), ("ki", "var")],
                [("ki", "var")])
            ACT(kb, var[:], var[:], AF.Sqrt, [("ki", "var")], [("ki", "var")], bias=C["eps"][:, 0:1])
            RECIP(kb, var[:], var[:], [("ki", "var")], [("ki", "var")])
            sl = slice(ts * 1024, (ts + 1) * 1024)
            TT(kb, "dve", ki[:, sl], ki[:, sl], mean[:], ALU.subtract, [("ki", "raw"), ("ki", "mean")], [("ki", "raw")])
            TT(kb, "dve", ki[:, sl], ki[:, sl], var[:], ALU.mult, [("ki", "raw"), ("ki", "var")], [("ki", "raw")])
            TS(kb, "dve", kio[:, sl], ki[:, sl], C["ikw"][:, 0:1], C["ikb"][:, 0:1], ALU.mult, ALU.add,
               [("ki", "raw")], [("ki", "o")])
        kb.dma("sp", ch, S["kiT"], kio[:], reads=[("ki", "o")], writes=[("kiT",)])
        for g in range(2):
            for tt in range(16):
                t = g * 16 + tt
                TR(kb, pst[:, tt, :], sm[:, t * 128:(t + 1) * 128], C["ident"][0:32, 0:32], [("ki", "sm")],
                   [("ki", "pst")], signal=(tt == 15))
            CP(kb, "dve", tmo[:, g * 16:(g + 1) * 16, :], pst[:], [("ki", "pst")], [("ki", "tmo")])
        kb.dma("sp", ch, S["tmaj"].rearrange("(n p) c -> p n c", p=128), tmo[:], reads=[("ki", "tmo")],
               writes=[("tmaj",)])
        kb.end_phase()


N_BIS = 16
SCALE_A = 128 ** -0.5


def phase_dsa_attn(kb, C, pj, S, y_dram):
    tag = "at"
    with ExitStack() as st:
        kT = kb.sb("at_kT", [128, 8, T], BF16, st)
        Vt = kb.sb("at_V", [128, NT, 1024], BF16, st)
        kiT = kb.sb("at_kiT", [128, T], BF16, st)
        score1 = kb.sb("at_sc", [128, T], F32, st)
        score = [score1, score1]
        mask = kb.sb("at_mask", [128, T], BF16, st)
        junk = mask
        maskT1 = kb.sb("at_maskT", [128, NT, 128], BF16, st)
        maskT = [maskT1, maskT1]
        rbuf = [kb.sb(f"at_r{i}", [128, 2, 512], BF16, st) for i in range(2)]
        dsg = kb.sb("at_dsg", [128, 16, 128], BF16, st)
        qb = [kb.sb(f"at_q{i}", [128, 8, 128], BF16, st) for i in range(2)]
        qib1 = kb.sb("at_qi", [128, 8, 128], BF16, st)
        qib = [qib1, qib1]
        wt = [kb.sb(f"at_w{i}", [128, 16], F32, st) for i in range(2)]
        bs = [kb.sb(f"at_bs{i}", [128, 8], F32, st) for i in range(2)]
        za1 = kb.sb("at_za", [128, 8, 128], F32, st)
        za = [za1, za1]
        pt = [kb.sb(f"at_pt{i}", [128, 4, 128], BF16, st) for i in range(3)]
        ost1 = kb.sb("at_ost", [128, 8, 256], F32, st)
        ost = [ost1, ost1]
        yst1 = kb.sb("at_yst", [128, 8, 128], BF16, st)
        yst = [yst1, yst1]
        biasS = C["biasS"]
        ident_b = kb.sb("at_identb", [128, 128], BF16, st)
        ones_b = kb.sb("at_onesb", [128, 128], BF16, st)
        ips = [kb.ps(f"at_ips{i}", [128, 2, 512], F32, st) for i in range(2)]
        lg = [ips[i][:, 0, :].rearrange("p (a b) -> p a b", b=128) for i in range(2)]
        po1 = kb.ps("at_po", [128, 128], F32, st)
        prs1 = kb.ps("at_prs", [128, 128], F32, st)
        po, prs = [po1, po1], [prs1, prs1]
        sps1 = kb.ps("at_sps", [128, 512], F32, st)
        sps = [sps1, sps1]
        tps = ips[0][:, 0, :].bitcast(BF16)[:, 0:512].rearrange("p (a b) -> p a b", b=128)
        chl = kb.chan("at_ld")
        chq = [kb.chan(f"at_q{i}") for i in range(2)]
        chy = [kb.chan(f"at_y{i}") for i in range(2)]
        chz = kb.chan("at_z")
        kb.dma("sp", chl, kT[:], S["kT"].rearrange("h p t -> p h t"), writes=[(tag, "kT")])
        kb.dma("sp", chl, Vt[:], S["V"].rearrange("(n p) c -> p n c", p=128), writes=[(tag, "V")])
        kb.dma("sp", chl, kiT[:], S["kiT"], writes=[(tag, "kiT")])
        CP(kb, "dve", ident_b[:], C["ident"][:], [], [(tag, "cst")])
        CP(kb, "dve", ones_b[:], C["ones_f"][:], [], [(tag, "cst")])
        qTv = S["qT"].rearrange("h p t -> p h t")
        qiTv = S["qiT"].rearrange("h p t -> p h t")
        zav = pj[1024:2048, :].rearrange("(h p) t -> p h t", p=128)
        yv = y_dram[0:1024, :].rearrange("(h p) t -> p h t", p=128)
        cnt = {"ips": 0, "r": 0, "lg": 0, "pt": 0, "po": 0, "sps": 0}

        def stage_a(i):
            s = i % 2
            c0, c1 = i * 128, (i + 1) * 128
            kb.dma("sp", chq[s], qb[s][:], qTv[:, :, c0:c1], writes=[(tag, "q", s)])
            kb.dma("sp", chq[s], qib[s][:], qiTv[:, :, c0:c1], writes=[(tag, "qi")])
            kb.dma("sp", chq[s], wt[s][:], S["tmaj"][c0:c1, 0:16], writes=[(tag, "w", s)])
            nW = (i + 4) // 4
            Wi = nW * 512
            sk = (tag, "score")
            TT(kb, "dve", dsg[:], ident_b[:].unsqueeze(1).to_broadcast([128, 16, 128]),
               wt[s][:, 0:16].unsqueeze(2).to_broadcast([128, 16, 128]), ALU.mult, [(tag, "w", s), (tag, "cst")],
               [(tag, "dsg")])
            for w in range(nW):
                sp_ = cnt["sps"] % 2
                cnt["sps"] += 1
                units = []

                def acc(u):
                    h0, r_ = u
                    for e_ in range(2):
                        MM(kb, sps[sp_][:], dsg[:, h0 + e_, :], rbuf[r_][:, e_, :], h0 + e_ == 0, h0 + e_ == 15,
                           [(tag, "dsg"), (tag, "r", r_)], [(tag, "sps", 0)], signal=(e_ == 1))

                for pair in range(8):
                    p = cnt["ips"] % 2
                    cnt["ips"] += 1
                    r = cnt["r"] % 2
                    cnt["r"] += 1
                    for e_ in range(2):
                        base = e_ * 64
                        MM(kb, ips[p][:, e_, :], qib[s][base:base + 64, pair, :],
                           kiT[base:base + 64, w * 512:(w + 1) * 512], True, True, [(tag, "qi"), (tag, "kiT")],
                           [(tag, "ips", p)], signal=(e_ == 1))
                    if pair % 2 == 0:
                        ACT(kb, rbuf[r][:], ips[p][:], AF.Relu, [(tag, "ips", p)], [(tag, "r", r)])
                    else:
                        TS(kb, "dve", rbuf[r][:], ips[p][:], 0.0, None, ALU.max, None, [(tag, "ips", p)], [(tag, "r", r)])
                    if units:
                        acc(units.pop())
                    units.append((2 * pair, r))
                acc(units.pop())
                CP(kb, "dve", score[s][:, w * 512:(w + 1) * 512], sps[sp_][:], [(tag, "sps", 0)], [sk])
            b = bs[s]
            bk = (tag, "bs", s)
            TS(kb, "dve", junk[:, 0:Wi], score[s][:, 0:Wi], 1.0, None, ALU.mult, ALU.max, [sk], [(tag, "mask"), bk],
               accum_out=b[:, 0:1])
            TS(kb, "dve", junk[:, 0:Wi], score[s][:, 0:Wi], -1.0, None, ALU.mult, ALU.max, [sk], [(tag, "mask"), bk],
               accum_out=b[:, 6:7])
            TT(kb, "dve", b[:, 0:1], b[:, 0:1], b[:, 6:7], ALU.max, [bk], [bk])
            TT(kb, "dve", score[s][:, Wi - 512:Wi], score[s][:, Wi - 512:Wi],
               C["cbase"][:, 384 - (i % 4) * 128:896 - (i % 4) * 128], ALU.add, [sk], [sk])
            TS(kb, "dve", b[:, 1:2], b[:, 0:1], -1.001, -1e-20, ALU.mult, ALU.add, [bk], [bk])
            TS(kb, "dve", b[:, 2:3], b[:, 0:1], 2.002, 2e-20, ALU.mult, ALU.add, [bk], [bk])
            for k in range(1, N_BIS + 1):
                f = 2.0 ** -k
                STT(kb, b[:, 3:4], b[:, 2:3], f, b[:, 1:2], ALU.mult, ALU.add, [bk], [bk])
                TS(kb, "dve", junk[:, 0:Wi], score[s][:, 0:Wi], b[:, 3:4], None, ALU.is_ge, ALU.add,
                   [sk, bk], [(tag, "mask"), bk], accum_out=b[:, 4:5])
                TS(kb, "dve", b[:, 5:6], b[:, 4:5], 255.5, f, ALU.is_ge, ALU.mult, [bk], [bk])
                STT(kb, b[:, 1:2], b[:, 5:6], b[:, 2:3], b[:, 1:2], ALU.mult, ALU.add, [bk], [bk])

        def stage_t(i):
            s = i % 2
            n = i + 1
            TS(kb, "dve", mask[:, 0:n * 128], score[s][:, 0:n * 128], bs[s][:, 1:2], None, ALU.is_ge, None,
               [(tag, "score"), (tag, "bs", s)], [(tag, "mask")])
            for j0 in range(0, n, 4):
                nb = min(4, n - j0)
                for jj in range(nb):
                    j = j0 + jj
                    TR(kb, tps[:, jj, :], mask[:, j * 128:(j + 1) * 128], ident_b[:], [(tag, "mask"), (tag, "cst")],
                       [(tag, "ips", 0)], signal=(jj == nb - 1))
                ACT(kb, maskT[s][:, j0:j0 + nb, :], tps[:, 0:nb, :], AF.Copy, [(tag, "ips", 0)], [(tag, "maskT")],
                    scale=30000.0, bias=-30000.0)

        def stage_b(i):
            s = i % 2
            n = i + 1
            groups = [(h, j0, min(4, n - j0)) for h in range(8) for j0 in range(0, n, 4)]
            slots = {}

            def qk(gi):
                h, j0, nb = groups[gi]
                p = cnt["lg"] % 2
                cnt["lg"] += 1
                x = cnt["pt"] % 3
                cnt["pt"] += 1
                slots[gi] = x
                for jj in range(nb):
                    j = j0 + jj
                    near = (i - j) <= 1
                    MM(kb, lg[p][:, jj, :], kT[:, h, j * 128:(j + 1) * 128], qb[s][:, h, :], True, False,
                       [(tag, "kT"), (tag, "q", s)], [(tag, "ips", p)], signal=False)
                    if near:
                        MM(kb, lg[p][:, jj, :], ident_b[:], biasS[:, h, i - j, :], False, False,
                           [(tag, "cst")], [(tag, "ips", p)], signal=False)
                    MM(kb, lg[p][:, jj, :], ident_b[:], maskT[s][:, j, :], False, True,
                       [(tag, "cst"), (tag, "maskT")], [(tag, "ips", p)], signal=(jj == nb - 1))
                ACT(kb, pt[x][:, 0:nb, :], lg[p][:, 0:nb, :], AF.Exp, [(tag, "ips", p)], [(tag, "pt", x)],
                    scale=SCALE_A, bias=C["cb"][:, h:h + 1])

            def pv(gi):
                h, j0, nb = groups[gi]
                x = slots.pop(gi)
                for jj in range(nb):
                    j = j0 + jj
                    MM(kb, po[0][:], Vt[:, j, h * 128:(h + 1) * 128], pt[x][:, jj, :], j == 0, j == n - 1,
                       [(tag, "V"), (tag, "pt", x)], [(tag, "po", 0)])
                    MM(kb, prs[0][:], ones_b[:], pt[x][:, jj, :], j == 0, j == n - 1,
                       [(tag, "cst"), (tag, "pt", x)], [(tag, "prs", 0)], signal=(jj == nb - 1))
                if j0 + nb == n:
                    CP(kb, "act", ost[s][:, h, 0:128], po[0][:], [(tag, "po", 0)], [(tag, "ost", h)])
                    CP(kb, "act", ost[s][:, h, 128:256], prs[0][:], [(tag, "prs", 0)], [(tag, "ost", h)])

            qk(0)
            for gi in range(len(groups)):
                if gi + 1 < len(groups):
                    qk(gi + 1)
                pv(gi)

        def stage_f(i):
            s = i % 2
            keys = [(tag, "ost", h) for h in range(8)]
            kb.dma("sp", chz, za[s][:], zav[:, :, i * 128:(i + 1) * 128], writes=[(tag, "za")])
            ACT(kb, za[s][:], za[s][:], AF.Silu, [(tag, "za")], [(tag, "za")])
            RECIP(kb, ost[s][:, :, 128:256], ost[s][:, :, 128:256], keys, keys)
            TT(kb, "dve", ost[s][:, :, 0:128], ost[s][:, :, 0:128], ost[s][:, :, 128:256], ALU.mult, keys, keys)
            TT(kb, "dve", yst[s][:], ost[s][:, :, 0:128], za[s][:], ALU.mult, keys + [(tag, "za")], [(tag, "yst")])
            kb.dma("pool", chy[s], yv[:, :, i * 128:(i + 1) * 128], yst[s][:], reads=[(tag, "yst")],
                   writes=[(tag, "y", i)])

        stage_a(0)
        stage_t(0)
        for i in range(NT):
            if i + 1 < NT:
                stage_a(i + 1)
            stage_b(i)
            if i + 1 < NT:
                stage_t(i + 1)
            stage_f(i)
        kb.end_phase()


def phase_gdn_prep(kb, C, pj, S):
    tag = "gp"
    with ExitStack() as st:
        W = T + 3
        HB = T // 2
        raw = [kb.sb(f"gp_raw{i}", [128, W], F32, st) for i in range(2)]
        acc = [kb.sb(f"gp_acc{i}", [128, T], F32, st) for i in range(2)]
        sq = [kb.sb(f"gp_sq{i}", [128, T], F32, st) for i in range(2)]
        rn = [kb.sb(f"gp_rn{i}", [128, T], F32, st) for i in range(2)]
        ssp = [kb.ps(f"gp_ssp{i}", [128, HB], F32, st) for i in range(2)]
        chl = [kb.chan(f"gp_l{i}") for i in range(2)]
        chs = [kb.chan(f"gp_s{i}") for i in range(2)]

        def head(cc):
            s = cc % 2
            r0 = 2048 + cc * 128
            MS(kb, "pool", raw[s][:, 0:3], 0.0, [(tag, "raw", s)])
            kb.dma("sp", chl[s], raw[s][:, 3:W], pj[r0:r0 + 128, :], writes=[(tag, "raw", s)])
            TS(kb, "dve", acc[s][:], raw[s][:, 3:W], C["cvw"][:, cc, 3:4], None, ALU.mult, None, [(tag, "raw", s)],
               [(tag, "acc", s)])
            for j in range(3):
                STT(kb, acc[s][:], raw[s][:, j:j + T], C["cvw"][:, cc, j:j + 1], acc[s][:], ALU.mult, ALU.add,
                    [(tag, "raw", s), (tag, "acc", s)], [(tag, "acc", s)])
            ACT(kb, acc[s][:], acc[s][:], AF.Silu, [(tag, "acc", s)], [(tag, "acc", s)])
            if cc < 16:
                ACT(kb, sq[s][:], acc[s][:], AF.Square, [(tag, "acc", s)], [(tag, "sq", s)])
                for hb in range(2):
                    for b in range(4):
                        c0 = hb * HB + b * 512
                        MM(kb, ssp[hb][:, b * 512:(b + 1) * 512], C["ones_f"][:], sq[s][:, c0:c0 + 512], True, True,
                           [(tag, "sq", s)], [(tag, "ssp", hb)], signal=(b == 3))
                    ACT(kb, rn[s][:, hb * HB:(hb + 1) * HB], ssp[hb][:], AF.Sqrt, [(tag, "ssp", hb)],
                        [(tag, "rn", s, hb)], bias=C["eps"][:, 0:1])

        def tail(cc):
            s = cc % 2
            if cc < 16:
                keys = [(tag, "rn", s, 0), (tag, "rn", s, 1)]
                RECIP(kb, rn[s][:], rn[s][:], keys, keys)
                STT(kb, acc[s][:], acc[s][:], (128 ** -0.5) if cc < 8 else 1.0, rn[s][:], ALU.mult, ALU.mult,
                    [(tag, "acc", s)] + keys, [(tag, "acc", s)])
            kb.dma("pool", chs[s], S["gqkv"][cc * 128:(cc + 1) * 128, :], acc[s][:], reads=[(tag, "acc", s)],
                   writes=[("gqkv", cc)])

        head(0)
        for cc in range(24):
            if cc + 1 < 24:
                head(cc + 1)
            tail(cc)
        kb.end_phase()


import os
GDN_TILES = int(os.environ.get("GDN_TILES", "32"))
GDN_STOP = int(os.environ.get("GDN_STOP", "99"))
GDN_SUB = float(os.environ.get("GDN_SUB", "99"))


def phase_gdn(kb, C, pj, S, y_dram):
    with ExitStack() as stc:
        C = dict(C)
        chc = kb.chan("gd_const")
        for name, shape, dt in GDN_CONST_SPECS:
            d = kb.nc.dram_tensor(name, list(shape), dt, kind="ExternalInput").ap()
            t = kb.sb("c_" + name, shape, dt, stc)
            kb.dma("sp", chc, t[:], d, writes=[("const", name)])
            C[name] = t
        kb.end_phase()
        phase_gdn_prep(kb, C, pj, S)
        _phase_gdn_main(kb, C, pj, S, y_dram)


def _phase_gdn_main(kb, C, pj, S, y_dram):
    tag = "gd"
    H = 8
    with ExitStack() as st:
        def fb(name, shape=(128, H, 128), dt=F32):
            return kb.sb("gd_" + name, list(shape), dt, st)

        tm = fb("tm", (128, NT, 32))
        beta = fb("beta", (128, NT, 8))
        g = fb("g", (128, NT, 8))
        t1 = fb("t1", (128, NT, 8))
        t2 = fb("t2", (128, NT, 8))
        nA = fb("nA", (128, 8))
        qT, kT, vT = fb("qT"), fb("kT"), fb("vT")
        gd, egrow, gcr = fb("gdiag"), fb("egrow"), fb("gcr")
        P1, E1, E2 = fb("P1"), fb("E1"), fb("E2")
        HS = (128, H, 128)
        X = [fb("X0", HS, BF16), fb("X1", HS, BF16)]
        Y = [fb("Y0", HS, BF16), fb("Y1", HS, BF16)]
        P, attnT = fb("P", HS, BF16), fb("attnT", HS, BF16)
        vb, kbg, kd, kd1 = fb("vb", HS, BF16), fb("kbg", HS, BF16), fb("kd", HS, BF16), fb("kd1", HS, BF16)
        smk = fb("smk", (128, 16))
        u, wT, qgT, vnew = fb("u"), fb("wT", HS, BF16), fb("qgT", HS, BF16), fb("vnew", HS, BF16)
        Sst, oacc, zb = fb("S"), fb("oacc"), fb("zb")
        Sb = fb("Sb", HS, BF16)
        ident_b = fb("identb", (128, 128), BF16)
        osq, orn = fb("osq"), fb("orn")
        yo = fb("yo", (128, H, 128), BF16)
        sm = fb("sm", (128, 64))
        pA = kb.ps("gd_pA", [128, H, 128], F32, st)
        pB = kb.ps("gd_pB", [128, H, 128], F32, st)
        pC = kb.ps("gd_pC", [128, H, 128], F32, st)
        pO = kb.ps("gd_pO", [128, H, 64], F32, st)
        psm = kb.ps("gd_psm", [128, 32], F32, st)
        ch = kb.chan("gd_l")
        chq = kb.chan("gd_q")
        chz = kb.chan("gd_z")
        chy = kb.chan("gd_y")
        K_ = lambda n: (tag, n)

        def bc_h(ap2d):
            return ap2d.unsqueeze(1).to_broadcast([128, H, 128])

        def bc_f(ap2d):
            return ap2d.unsqueeze(2).to_broadcast([128, H, 128])

        kb.dma("sp", ch, tm[:], S["tmaj"].rearrange("(n p) c -> p n c", p=128), writes=[K_("tm")])
        ACT(kb, beta[:], tm[:, :, 16:24], AF.Sigmoid, [K_("tm")], [K_("beta")])
        dtb = C["dtb_bc"][:].unsqueeze(1).to_broadcast([128, NT, 8])
        TT(kb, "dve", g[:], tm[:, :, 24:32], dtb, ALU.add, [K_("tm")], [K_("g")])
        TS(kb, "dve", t1[:], g[:], -1.0, None, ALU.mult, None, [K_("g")], [K_("t1")])
        TT(kb, "dve", t1[:], t1[:], g[:], ALU.max, [K_("t1"), K_("g")], [K_("t1")])
        ACT(kb, t1[:], t1[:], AF.Exp, [K_("t1")], [K_("t1")], scale=-1.0)
        TS(kb, "dve", t1[:], t1[:], 1.0, None, ALU.add, None, [K_("t1")], [K_("t1")])
        ACT(kb, t1[:], t1[:], AF.Ln, [K_("t1")], [K_("t1")])
        TS(kb, "dve", t2[:], g[:], 0.0, None, ALU.max, None, [K_("g")], [K_("t2")])
        TT(kb, "dve", t2[:], t2[:], t1[:], ALU.add, [K_("t1"), K_("t2")], [K_("t2")])
        ACT(kb, nA[:], C["alog_bc"][:], AF.Exp, [], [K_("nA")])
        TS(kb, "dve", nA[:], nA[:], -1.0, None, ALU.mult, None, [K_("nA")], [K_("nA")])
        TT(kb, "dve", g[:], t2[:], nA[:].unsqueeze(1).to_broadcast([128, NT, 8]), ALU.mult, [K_("t2"), K_("nA")],
           [K_("g")])
        MS(kb, "dve", Sst[:], 0.0, [K_("S")])
        MS(kb, "dve", Sb[:], 0.0, [K_("Sb")])
        MS(kb, "dve", vnew[:], 0.0, [K_("vnew")])
        CP(kb, "dve", ident_b[:], C["ident"][:], [], [K_("identb")])
        pAb = pA[:, 0:4, :].bitcast(BF16).rearrange("p a (c b) -> p (a c) b", b=128)
        gq = S["gqkv"]
        qv = gq[0:1024, :].rearrange("(h p) t -> p h t", p=128)
        kv = gq[1024:2048, :].rearrange("(h p) t -> p h t", p=128)
        vv = gq[2048:3072, :].rearrange("(h p) t -> p h t", p=128)
        zv = pj[5120:6144, :].rearrange("(h p) t -> p h t", p=128)
        yv = y_dram[1024:2048, :].rearrange("(h p) t -> p h t", p=128)

        for n in range(GDN_TILES if GDN_STOP > 0 else 0):
            c0, c1 = n * 128, (n + 1) * 128
            kb.dma("sp", chq, qT[:], qv[:, :, c0:c1], writes=[K_("qT")])
            kb.dma("sp", chq, kT[:], kv[:, :, c0:c1], writes=[K_("kT")])
            kb.dma("sp", chq, vT[:], vv[:, :, c0:c1], writes=[K_("vT")])
            kb.dma("sp", chz, zb[:], zv[:, :, c0:c1], writes=[K_("zb")])
            gn = g[:, n, :]
            bn = beta[:, n, :]
            MM(kb, psm[:, 0:8], C["U2"][:], gn, True, True, [K_("g")], [K_("psm")], signal=False)
            MM(kb, psm[:, 8:16], C["Bsame"][:], gn, True, True, [K_("g")], [K_("psm")], signal=False)
            MM(kb, psm[:, 16:24], C["Bsel0"][:], gn, True, True, [K_("g")], [K_("psm")], signal=False)
            MM(kb, psm[:, 24:32], C["Bsel1"][:], gn, True, True, [K_("g")], [K_("psm")])
            CP(kb, "dve", sm[:, 0:32], psm[:], [K_("psm")], [K_("sm")])
            gc = sm[:, 0:8]
            ACT(kb, sm[:, 32:40], sm[:, 0:8], AF.Exp, [K_("sm")], [K_("sm")])
            TT(kb, "dve", sm[:, 40:48], sm[:, 8:16], sm[:, 0:8], ALU.subtract, [K_("sm")], [K_("sm")])
            ACT(kb, sm[:, 40:48], sm[:, 40:48], AF.Exp, [K_("sm")], [K_("sm")])
            ACT(kb, sm[:, 16:32], sm[:, 16:32], AF.Exp, [K_("sm")], [K_("sm")])
            TT(kb, "dve", sm[:, 48:56], sm[:, 32:40], bn, ALU.mult, [K_("sm"), K_("beta")], [K_("sm")])
            TS(kb, "dve", sm[:, 56:64], bn, -1.0, None, ALU.mult, None, [K_("beta")], [K_("sm")])
            if GDN_SUB <= 0:
                continue
            TT(kb, "dve", gd[:], bc_h(C["U2"][:]), bc_f(gn), ALU.mult, [K_("g")], [K_("gdiag")])
            for b in range(2):
                MM(kb, pA[:, 4 * b:4 * b + 4, :], C["ones_f"][:], gd[:, 4 * b:4 * b + 4, :], True, True,
                   [K_("gdiag")], [K_("pA")], signal=(b == 1))
            if GDN_SUB <= 0.3:
                continue
            CP(kb, "dve", gcr[:], pA[:], [K_("pA")], [K_("gcr")])
            ACT(kb, egrow[:], gcr[:], AF.Exp, [K_("gcr")], [K_("egrow")])
            if GDN_SUB <= 0.4:
                continue
            TT(kb, "dve", P1[:], gcr[:], bc_f(gc), ALU.subtract, [K_("gcr"), K_("sm")], [K_("P1")])
            if GDN_SUB <= 0.5:
                continue
            TS(kb, "dve", E1[:], P1[:], 0.0, None, ALU.max, None, [K_("P1")], [K_("E1")])
            ACT(kb, E1[:], E1[:], AF.Exp, [K_("E1")], [K_("E1")], scale=-1.0)
            if GDN_SUB <= 0.6:
                continue
            TS(kb, "dve", E2[:], P1[:], 0.0, None, ALU.min, None, [K_("P1")], [K_("E2")])
            ACT(kb, E2[:], E2[:], AF.Exp, [K_("E2")], [K_("E2")])
            TT(kb, "dve", E1[:], E1[:], bc_h(C["MLs"][:]), ALU.mult, [K_("E1")], [K_("E1")])
            TT(kb, "dve", E1[:], E1[:], bc_f(sm[:, 56:64]), ALU.mult, [K_("E1"), K_("sm")], [K_("E1")])
            TT(kb, "dve", E2[:], E2[:], bc_h(C["MU"][:]), ALU.mult, [K_("E2")], [K_("E2")])
            if GDN_SUB <= 1:
                continue
            for h in range(H):
                MM(kb, pB[:, h, :], kT[:, h, :], kT[:, h, :], True, True, [K_("kT")], [K_("pB")], signal=(h == H - 1))
            TT(kb, "dve", X[0][:], pB[:], E1[:], ALU.mult, [K_("pB"), K_("E1")], [K_("X0")])
            for h in range(H):
                MM(kb, pC[:, h, :], kT[:, h, :], qT[:, h, :], True, True, [K_("kT"), K_("qT")], [K_("pC")], signal=(h == H - 1))
            TT(kb, "dve", attnT[:], pC[:], E2[:], ALU.mult, [K_("pC"), K_("E2")], [K_("attnT")])
            for h in range(H):
                TR(kb, pAb[:, h, :], X[0][:, h, :], ident_b[:], [K_("X0"), K_("identb")], [K_("pA")],
                   signal=(h == H - 1))
            CP(kb, "dve", Y[0][:], pAb, [K_("pA")], [K_("Y0")])
            TT(kb, "dve", P[:], Y[0][:], bc_h(C["ident"][:]), ALU.add, [K_("Y0")], [K_("P")])
            if GDN_SUB <= 2:
                continue
            cur = 0
            for lvl in range(5):
                nxt = 1 - cur
                xk, yk = K_(f"X{cur}"), K_(f"Y{cur}")
                xn, yn = K_(f"X{nxt}"), K_(f"Y{nxt}")
                for h in range(H):
                    MM(kb, pB[:, h, :], Y[cur][:, h, :], X[cur][:, h, :], True, True, [xk, yk], [K_("pB")], signal=(h == H - 1))
                CP(kb, "dve", X[nxt][:], pB[:], [K_("pB")], [xn])
                if lvl < 4:
                    for h in range(H):
                        MM(kb, pC[:, h, :], X[cur][:, h, :], Y[cur][:, h, :], True, True, [xk, yk], [K_("pC")], signal=(h == H - 1))
                    CP(kb, "dve", Y[nxt][:], pC[:], [K_("pC")], [yn])
                for h in range(H):
                    MM(kb, pA[:, h, :], X[nxt][:, h, :], P[:, h, :], True, True, [xn, K_("P")], [K_("pA")], signal=(h == H - 1))
                TT(kb, "dve", P[:], P[:], pA[:], ALU.add, [K_("pA"), K_("P")], [K_("P")])
                cur = nxt
            if GDN_SUB <= 3:
                continue
            for h in range(H):
                TR(kb, pB[:, h, :], kT[:, h, :], C["ident"][:], [K_("kT")], [K_("pB")], signal=(h == H - 1))
            TT(kb, "dve", kbg[:], pB[:], bc_f(sm[:, 48:56]), ALU.mult, [K_("pB"), K_("sm")], [K_("kbg")])
            TS(kb, "dve", smk[:, 0:8], sm[:, 40:48], C["Bsel0"][:, 0:1], None, ALU.mult, None, [K_("sm")], [K_("smk")])
            TS(kb, "dve", smk[:, 8:16], sm[:, 40:48], C["Bsel1"][:, 0:1], None, ALU.mult, None, [K_("sm")], [K_("smk")])
            TT(kb, "dve", kd[:], pB[:], bc_f(smk[:, 0:8]), ALU.mult, [K_("pB"), K_("smk")], [K_("kd")])
            TT(kb, "dve", kd1[:], pB[:], bc_f(smk[:, 8:16]), ALU.mult, [K_("pB"), K_("smk")], [K_("kd")])
            for h in range(H):
                TR(kb, pC[:, h, :], vT[:, h, :], C["ident"][:], [K_("vT")], [K_("pC")], signal=(h == H - 1))
            TT(kb, "dve", vb[:], pC[:], bc_f(bn), ALU.mult, [K_("pC"), K_("beta")], [K_("vb")])
            for h in range(H):
                MM(kb, pA[:, h, :], P[:, h, :], vb[:, h, :], True, True, [K_("P"), K_("vb")], [K_("pA")], signal=(h == H - 1))
            CP(kb, "dve", u[:], pA[:], [K_("pA")], [K_("u")])
            for h in range(H):
                MM(kb, pB[:, h, :], kbg[:, h, :], P[:, h, :], True, True, [K_("P"), K_("kbg")], [K_("pB")], signal=(h == H - 1))
            CP(kb, "dve", wT[:], pB[:], [K_("pB")], [K_("wT")])
            TT(kb, "dve", qgT[:], qT[:], egrow[:], ALU.mult, [K_("qT"), K_("egrow")], [K_("qgT")])
            if GDN_STOP <= 1:
                continue
            for c in range(2):
                r0, r1 = c * 64, (c + 1) * 64
                for h in range(H):
                    MM(kb, pA[r0:r1, h, :], wT[:, h, r0:r1], Sb[:, h, :], True, True, [K_("wT"), K_("Sb")], [K_("pA")],
                       signal=(h == H - 1))
                if GDN_SUB == 10 or (GDN_SUB == 10.5 and c == 1):
                    continue
                TT(kb, "dve", vnew[r0:r1, :, :], u[r0:r1, :, :], pA[r0:r1, :, :], ALU.subtract, [K_("u"), K_("pA")],
                   [K_("vnew")])
                if GDN_SUB == 11:
                    continue
                for h in range(H):
                    MM(kb, pO[:, h, :], Sb[:, h, :], qgT[:, h, r0:r1], True, False, [K_("Sb"), K_("qgT")], [K_("pO")])
                    MM(kb, pO[:, h, :], vnew[r0:r1, h, :], attnT[r0:r1, h, r0:r1], False, True,
                       [K_("vnew"), K_("attnT")], [K_("pO")], signal=(h == H - 1))
                if GDN_SUB == 12:
                    continue
                CP(kb, "dve", oacc[:, :, r0:r1], pO[:], [K_("pO")], [K_("oacc")])
                if GDN_STOP <= 2:
                    continue
                for h in range(H):
                    MM(kb, pB[:, h, :], (kd, kd1)[c][:, h, :], vnew[:, h, :], True, True, [K_("kd"), K_("vnew")],
                       [K_("pB")], signal=(h == H - 1))
                TT(kb, "dve", Sst[:], Sst[:], bc_f(sm[:, 16 + 8 * c:24 + 8 * c]), ALU.mult, [K_("S"), K_("sm")],
                   [K_("S")])
                TT(kb, "dve", Sst[:], Sst[:], pB[:], ALU.add, [K_("S"), K_("pB")], [K_("S")])
                CP(kb, "pool", Sb[:], Sst[:], [K_("S")], [K_("Sb")])
            ACT(kb, osq[:], oacc[:], AF.Square, [K_("oacc")], [K_("osq")])
            for b in range(2):
                MM(kb, pC[:, 4 * b:4 * b + 4, :], C["ones_f"][:], osq[:, 4 * b:4 * b + 4, :], True, True, [K_("osq")],
                   [K_("pC")], signal=(b == 1))
            CP(kb, "dve", orn[:], pC[:], [K_("pC")], [K_("orn")])
            ACT(kb, orn[:], orn[:], AF.Sqrt, [K_("orn")], [K_("orn")], bias=C["eps"][:, 0:1], scale=1.0 / 128)
            RECIP(kb, orn[:], orn[:], [K_("orn")], [K_("orn")])
            STT(kb, oacc[:], oacc[:], C["onorm"][:, 0:1], orn[:], ALU.mult, ALU.mult, [K_("oacc"), K_("orn")],
                [K_("oacc")])
            ACT(kb, zb[:], zb[:], AF.Silu, [K_("zb")], [K_("zb")])
            TT(kb, "dve", yo[:], oacc[:], zb[:], ALU.mult, [K_("oacc"), K_("zb")], [K_("yo")])
            kb.dma("pool", chy, yv[:, :, c0:c1], yo[:], reads=[K_("yo")], writes=[("y0b", n)])
        kb.end_phase()

def load_consts(kb, nc, names_shapes):
    C = {}
    ch = kb.chan("const")
    for name, shape, dt in names_shapes:
        d = nc.dram_tensor(name, list(shape), dt, kind="ExternalInput").ap()
        if name == "cbase":
            t = kb.sb("c_" + name, shape, BF16)
            kb.dma("pool", ch, t[:], d, writes=[("const", name)])
        else:
            t = kb.sb("c_" + name, shape, dt)
            kb.dma("sp", ch, t[:], d, writes=[("const", name)])
        C[name] = t
    kb.end_phase()
    with ExitStack() as st:
        d = nc.dram_tensor("biasT", [128, 8, 2, 128], F32, kind="ExternalInput").ap()
        C["biasS"] = kb.sb("c_biasS", [128, 8, 2, 128], BF16)
        bt = kb.sb("biasT_tmp", [128, 8, 2, 128], F32, st)
        kb.dma("sp", ch, bt[:], d, writes=[("const", "biasT")])
        for h in range(8):
            TS(kb, "dve", C["biasS"][:, h, :, :], bt[:, h, :, :], C["cb"][:, h:h + 1], 128 ** 0.5,
               ALU.subtract, ALU.mult, [("const", "biasT")], [("const", "biasS")])
        kb.end_phase()
    return C


CONST_SPECS = [
    ("ident", (128, 128), F32),
    ("ones_f", (128, 128), F32),
    ("eps", (128, 1), F32),
    ("normwT", (128, 2, 16), F32),
    ("cbase", (128, 896), F32),
    ("cb", (128, 8), F32),
    ("qnormT", (128, 4), F32),
    ("kvnormT", (128, 2), F32),
    ("qgain", (128, 1), F32),
    ("kgain", (128, 1), F32),
    ("ikw", (128, 1), F32),
    ("ikb", (128, 1), F32),
]

CD_CONST_SPECS = [
    ("dww", (128, 8, 31), F32),
    ("dwb", (128, 8), F32),
    ("lnw", (128, 8), F32),
    ("lnb", (128, 8), F32),
    ("dcw", (128, 8, 3), F32),
]

GDN_CONST_SPECS = [
    ("cvw", (128, 24, 4), F32),
    ("alog_bc", (128, 8), F32),
    ("dtb_bc", (128, 8), F32),
    ("onorm", (128, 1), F32),
    ("U2", (128, 128), F32),
    ("Bsame", (128, 128), F32),
    ("Bsel0", (128, 128), F32),
    ("Bsel1", (128, 128), F32),
    ("MLs", (128, 128), F32),
    ("MU", (128, 128), F32),
]


def build_program(layers=(0, 1), l0_parts=("a", "b"), debug_out=False):
    nc = bass.Bass("TRN2", target_bir_lowering=False)

    def din(name, shape, dt=F32):
        return nc.dram_tensor(name, list(shape), dt, kind="ExternalInput").ap()

    def dscr(name, shape, dt=F32):
        return nc.dram_tensor(name, list(shape), dt, kind="Internal").ap()

    x = din("x", [T, D])
    out = nc.dram_tensor("out", [T, D], F32, kind="ExternalOutput").ap()
    kb = KB(nc)
    C = load_consts(kb, nc, CONST_SPECS)
    src = x
    if 0 in layers:
        ab_w_in = din("ab_w_in", [48, 128, 16 * 128])
        ab_w_out = din("ab_w_out", [16, 128, D])
        W = {"w_uqiq": din("w_uqiq", [16, 128, 4 * 128]), "w_uk": din("w_uk", [8, 128, 2 * 128]),
             "w_uv": din("w_uv", [2, 128, 1024])}
        pj0 = dscr("pj0", [6144, T])
        S = {"qT": dscr("s_qT", [8, 128, T], BF16), "qiT": dscr("s_qiT", [8, 128, T], BF16),
             "kT": dscr("s_kT", [8, 128, T], BF16), "V": dscr("s_V", [T, 1024], BF16),
             "kiT": dscr("s_kiT", [128, T], BF16), "tmaj": dscr("s_tmaj", [T, 32]),
             "gqkv": dscr("s_gqkv", [3072, T])}
        if debug_out:
            y0 = nc.dram_tensor("y0", [2048, T], BF16, kind="ExternalOutput").ap()
        else:
            y0 = dscr("y0", [2048, T], BF16)
        x1 = dscr("x1", [T, D]) if 1 in layers else out
        phase_inproj(kb, C, "l0", src, C["normwT"][:, 0, :], ab_w_in, 48, pj0)
        phase_ki_tmaj(kb, C, pj0, S)
        if "a" in l0_parts:
            phase_dsa_prep(kb, C, pj0, W, S)
            phase_dsa_attn(kb, C, pj0, S, y0)
        if "b" in l0_parts:
            phase_gdn(kb, C, pj0, S, y0)
        if not debug_out:
            phase_outproj(kb, C, "l0o", y0, ab_w_out, src, x1)
        src = x1
    if 1 in layers:
        cd_w_in = din("cd_w_in", [56, 128, 16 * 128])
        cd_w_out = din("cd_w_out", [16, 128, D])
        pj1 = dscr("pj1", [7168, T])
        y1 = dscr("y1", [2048, T], BF16)
        phase_inproj(kb, C, "l1", src, C["normwT"][:, 1, :], cd_w_in, 56, pj1)
        phase_cd_mix(kb, C, pj1, C, y1)
        phase_outproj(kb, C, "l1o", y1, cd_w_out, src, out)
    kb.finish()
    kb.emit()
    kb.close()
    return nc, kb


def tile_w_in(w, nch):
    K, N = w.shape
    assert N == nch * 128
    return np.ascontiguousarray(w.reshape(K // 128, 128, nch, 128).transpose(2, 1, 0, 3)).reshape(nch, 128, -1)


def t5_bucket_np(dist):
    import math
    max_exact = 16
    dd = np.maximum(dist, 1).astype(np.float32)
    large = max_exact + (np.log(dd / max_exact) / math.log(128 / max_exact) * (32 - max_exact)).astype(np.int32)
    large = np.minimum(large, 31)
    return np.where(dist < max_exact, dist, large)


def colT(v, k):
    return np.ascontiguousarray(np.asarray(v, np.float32).reshape(k, 128).T)


def host_consts(inp):
    f = np.float32
    c = {}
    c["ident"] = np.eye(128, dtype=f)
    c["ones_f"] = np.ones((128, 128), f)
    c["eps"] = np.full((128, 1), EPS, f)
    c["normwT"] = np.ascontiguousarray(inp["norm_w"].reshape(2, 16, 128).transpose(2, 0, 1)).astype(f)
    c["dww"] = np.ascontiguousarray(inp["c_dw_w"][0].reshape(31, 8, 128).transpose(2, 1, 0)).astype(f)
    c["dwb"] = colT(inp["c_dw_b"][0], 8)
    c["lnw"] = colT(inp["c_ln_w"][0], 8)
    c["lnb"] = colT(inp["c_ln_b"][0], 8)
    c["dcw"] = np.ascontiguousarray(inp["d_conv_w"][0].reshape(3, 8, 128).transpose(2, 1, 0)).astype(f)
    r = np.arange(128)[:, None]
    cc = np.arange(896)[None, :]
    c["cbase"] = np.where(cc <= r + 384, 0.0, -1e30).astype(f)
    kl = np.arange(128)[:, None, None]
    dd = np.arange(2)[None, :, None]
    ql = np.arange(128)[None, None, :]
    dist = np.maximum(dd * 128 + ql - kl, 0)
    bt = np.asarray(inp["rel_bias"], f)[t5_bucket_np(dist)]
    c["biasT"] = np.ascontiguousarray(bt.transpose(0, 3, 1, 2))
    c["cb"] = np.ascontiguousarray(np.broadcast_to(np.asarray(inp["rel_bias"], f)[31][None, :], (128, 8)))
    c["qnormT"] = colT(inp["a_q_norm"][0], 4)
    c["kvnormT"] = colT(inp["a_kv_norm"][0], 2)
    c["qgain"] = np.asarray(inp["a_q_gain"][0], f).reshape(128, 1).copy()
    c["kgain"] = np.asarray(inp["a_k_gain"][0], f).reshape(128, 1).copy()
    c["ikw"] = np.tile(np.asarray(inp["a_ik_norm_w"][0], f), 2).reshape(128, 1).copy()
    c["ikb"] = np.tile(np.asarray(inp["a_ik_norm_b"][0], f), 2).reshape(128, 1).copy()
    c["cvw"] = np.ascontiguousarray(np.asarray(inp["b_conv_w"][0], f).reshape(4, 24, 128).transpose(2, 1, 0))
    c["alog_bc"] = np.ascontiguousarray(np.broadcast_to(np.asarray(inp["b_a_log"][0], f)[None, :], (128, 8)))
    c["dtb_bc"] = np.ascontiguousarray(np.broadcast_to(np.asarray(inp["b_dt_bias"][0], f)[None, :], (128, 8)))
    c["onorm"] = np.asarray(inp["b_o_norm"][0], f).reshape(128, 1).copy()
    a = np.arange(128)
    same = (a[:, None] // 64) == (a[None, :] // 64)
    c["U2"] = (same & (a[:, None] <= a[None, :])).astype(f)
    c["Bsame"] = same.astype(f)
    c["Bsel0"] = np.ascontiguousarray(np.broadcast_to((a[:, None] < 64), (128, 128))).astype(f)
    c["Bsel1"] = np.ascontiguousarray(np.broadcast_to((a[:, None] >= 64), (128, 128))).astype(f)
    c["MLs"] = (same & (a[:, None] > a[None, :])).astype(f)
    c["MU"] = (same & (a[:, None] <= a[None, :])).astype(f)
    return c


def host_shared(inp, layers=(0, 1)):
    f = np.float32
    sh = host_consts(inp)
    if 0 in layers:
        w = np.asarray(inp["ab_w_in"][0], f)
        wp = np.zeros((D, 6144), f)
        wp[:, 0:832] = w[:, 0:832]
        wp[:, 896:912] = w[:, 832:848]
        wp[:, 912:928] = w[:, 4944:4960]
        wp[:, 1024:2048] = w[:, 848:1872]
        wp[:, 2048:5120] = w[:, 1872:4944]
        wp[:, 5120:6144] = w[:, 4960:5984]
        sh["ab_w_in"] = tile_w_in(wp, 48)
        sh["ab_w_out"] = np.ascontiguousarray(np.asarray(inp["ab_w_out"][0], f).reshape(16, 128, D))
        sh["w_uqiq"] = tile_w_in(np.concatenate([inp["a_w_uq"][0], inp["a_w_iq"][0]], axis=1).astype(f), 16)
        sh["w_uk"] = tile_w_in(np.asarray(inp["a_w_uk"][0], f), 8)
        sh["w_uv"] = np.ascontiguousarray(np.asarray(inp["a_w_uv"][0], f).reshape(2, 128, 1024))
    if 1 in layers:
        sh["cd_w_in"] = tile_w_in(np.asarray(inp["cd_w_in"][0], f), 56)
        sh["cd_w_out"] = np.ascontiguousarray(np.asarray(inp["cd_w_out"][0], f).reshape(16, 128, D))
    return sh


def kernel(**inputs):
    inp = {k: np.asarray(v) for k, v in inputs.items()}
    nc, kb = build_program()
    sh = host_shared(inp)
    x = np.ascontiguousarray(inp["x"], dtype=np.float32)
    in_maps = [dict(sh, x=x[b]) for b in range(8)]
    res = run_bass_kernel_spmd(nc, in_maps, core_ids=list(range(8)))
    return np.stack([np.asarray(r["out"], np.float32) for r in res.results], axis=0)
```

```python
from contextlib import ExitStack
import numpy as np
import concourse.bass as bass
import concourse.mybir as mybir
from concourse.bass_utils import run_bass_kernel_spmd

F32 = mybir.dt.float32
BF16 = mybir.dt.bfloat16
ALU = mybir.AluOpType
AF = mybir.ActivationFunctionType
AX = mybir.AxisListType

T = 4096
D = 2048
NT = T // 128
EPS = 1e-6
ENGS = ("pe", "act", "dve", "pool", "sp")


class Chan:
    def __init__(self, sem, name):
        self.sem = sem
        self.name = name
        self.n = 0


class KB:
    def __init__(self, nc):
        self.nc = nc
        self.es = ExitStack()
        self.q = {e: [] for e in ENGS}
        self.sems = {}
        self.cnt = {}
        self.seen = {e: {} for e in ENGS}
        self.lastw = {}
        self.readers = {}
        self.chans = []
        self.chan_by_sem = {}
        self.nins = 0
        self.pending = {e: False for e in ENGS}
        for e in ENGS:
            self.sems[e] = self.es.enter_context(nc.semaphore("s_" + e))
            self.cnt[e] = 0

    def sb(self, name, shape, dt, stack=None):
        return (stack or self.es).enter_context(self.nc.sbuf_tensor(name, list(shape), dt))

    def ps(self, name, shape, dt=F32, stack=None):
        return (stack or self.es).enter_context(self.nc.psum_tensor(name, list(shape), dt))

    def chan(self, name):
        c = Chan(self.es.enter_context(self.nc.semaphore("c_" + name)), name)
        self.chans.append(c)
        self.chan_by_sem[id(c.sem)] = c
        return c

    def _need0(self, eng, sem, val):
        ch = self.chan_by_sem.get(id(sem))
        if ch is not None:
            val = max(val, 16 * ch.n)
        cur = self.seen[eng].get(id(sem), 0)
        if val > cur:
            self.seen[eng][id(sem)] = val
            self.q[eng].append(("wait", sem, val))

    def _deps(self, eng, reads, writes, my_sem):
        for r in reads:
            ev = self.lastw.get(r)
            if ev is not None:
                self._need(eng, ev[0], ev[1])
        for w in writes:
            ev = self.lastw.get(w)
            if ev is not None:
                self._need(eng, ev[0], ev[1])
            rd = self.readers.get(w)
            if rd:
                for sem, val in rd.values():
                    if sem is my_sem:
                        continue
                    self._need(eng, sem, val)

    def _need(self, eng, sem, val):
        if eng == "pe" and sem is self.sems["pe"]:
            return
        self._need0(eng, sem, val)

    def _commit(self, ev, reads, writes):
        for w in writes:
            self.lastw[w] = ev
            self.readers[w] = {}
        for r in reads:
            d = self.readers.setdefault(r, {})
            d[id(ev[0])] = ev

    def op(self, eng, fn, reads=(), writes=(), signal=True):
        sem = self.sems[eng]
        self._deps(eng, reads, writes, sem)
        if signal:
            self.cnt[eng] += 1
            self.pending[eng] = False
            ev = (sem, self.cnt[eng])
            self.q[eng].append(("ins", fn, sem, 1))
        else:
            self.pending[eng] = True
            ev = (sem, self.cnt[eng] + 1)
            self.q[eng].append(("ins0", fn))
        self._commit(ev, reads, writes)
        self.nins += 1
        return ev

    def dma(self, eng, ch, out, in_, reads=(), writes=(), **kw):
        self._deps(eng, reads, writes, None)
        ch.n += 1
        ev = (ch.sem, 16 * ch.n)
        self.q[eng].append(("ins", lambda e, o=out, i=in_, k=kw: e.dma_start(out=o, in_=i, **k), ch.sem, 16))
        self._commit(ev, reads, writes)
        self.nins += 1
        return ev

    def _flush_pending(self):
        for e in ENGS:
            assert not self.pending[e], "non-signaling op left pending at a barrier on " + e

    def _all_events(self):
        self._flush_pending()
        evs = [(self.sems[e], self.cnt[e]) for e in ENGS if self.cnt[e] > 0]
        evs += [(c.sem, 16 * c.n) for c in self.chans if c.n > 0]
        return evs

    def barrier(self):
        evs = self._all_events()
        for e in ENGS:
            for sem, val in evs:
                self._need(e, sem, val)

    def finish(self, final_eng="sp"):
        for sem, val in self._all_events():
            self._need(final_eng, sem, val)

    def emit(self):
        nc = self.nc
        q = self.q
        self.q = {e: [] for e in ENGS}

        def replay(eng_obj, items):
            for it in items:
                if it[0] == "wait":
                    eng_obj.wait_ge(it[1], it[2])
                elif it[0] == "ins0":
                    it[1](eng_obj)
                else:
                    it[1](eng_obj).then_inc(it[2], it[3])

        with nc.Block() as block:
            @block.tensor
            def _(e):
                replay(e, q["pe"])

            @block.scalar
            def _(e):
                replay(e, q["act"])

            @block.vector
            def _(e):
                replay(e, q["dve"])

            @block.gpsimd
            def _(e):
                replay(e, q["pool"])

            @block.sync
            def _(e):
                replay(e, q["sp"])

    def end_phase(self):
        self.barrier()
        self.emit()

    def close(self):
        self.es.close()


def MM(kb, out, lhsT, rhs, start, stop, reads, writes, signal=None):
    if signal is None:
        signal = stop
    return kb.op("pe", lambda e: e.matmul(out, lhsT, rhs, start=start, stop=stop), reads, writes, signal=signal)


def TR(kb, out, in_, ident, reads, writes, signal=True):
    return kb.op("pe", lambda e: e.transpose(out, in_, ident), reads, writes, signal=signal)


def ACT(kb, out, in_, func, reads, writes, bias=None, scale=None, accum_out=None):
    kw = {}
    if bias is not None:
        kw["bias"] = bias
    if scale is not None:
        kw["scale"] = scale
    if accum_out is not None:
        kw["accum_out"] = accum_out
    return kb.op("act", lambda e: e.activation(out=out, in_=in_, func=func, **kw), reads, writes)


def TS(kb, eng, out, in0, s1, s2, op0, op1, reads, writes, accum_out=None):
    kw = {}
    if op1 is not None:
        kw["op1"] = op1
    if accum_out is not None:
        kw["accum_out"] = accum_out
    return kb.op(eng, lambda e: e.tensor_scalar(out=out, in0=in0, scalar1=s1, scalar2=s2, op0=op0, **kw),
                 reads, writes)


def TT(kb, eng, out, in0, in1, op, reads, writes):
    return kb.op(eng, lambda e: e.tensor_tensor(out=out, in0=in0, in1=in1, op=op), reads, writes)


def STT(kb, out, in0, scalar, in1, op0, op1, reads, writes):
    return kb.op("dve", lambda e: e.scalar_tensor_tensor(out=out, in0=in0, scalar=scalar, in1=in1,
                                                         op0=op0, op1=op1), reads, writes)


def CP(kb, eng, out, in_, reads, writes):
    if eng == "act":
        return kb.op("act", lambda e: e.copy(out=out, in_=in_), reads, writes)
    return kb.op(eng, lambda e: e.tensor_copy(out=out, in_=in_), reads, writes)


def MS(kb, eng, ap, val, writes):
    return kb.op(eng, lambda e: e.memset(ap, val), (), writes)


def RECIP(kb, out, in_, reads, writes):
    return kb.op("dve", lambda e: e.reciprocal(out=out, in_=in_), reads, writes)


def phase_norm_T(kb, C, x_dram, normwT, hT, tag):
    with ExitStack() as st:
        xt = [kb.sb(f"{tag}_xt{i}", [128, D], F32, st) for i in range(2)]
        xn = [kb.sb(f"{tag}_xn{i}", [128, D], F32, st) for i in range(2)]
        sq = kb.sb(f"{tag}_sq", [128, D], BF16, st)
        sm = [kb.sb(f"{tag}_sm{i}", [128, 4], F32, st) for i in range(2)]
        pst = [kb.ps(f"{tag}_pt{i}", [128, 8, 128], F32, st) for i in range(2)]
        ch = [kb.chan(f"{tag}_x{i}") for i in range(2)]
        for tt in range(NT):
            s = tt % 2
            kb.dma("sp", ch[s], xt[s][:], x_dram[tt * 128:(tt + 1) * 128, :], writes=[(tag, "xt", s)])
            ACT(kb, sq[:], xt[s][:], AF.Square, [(tag, "xt", s)], [(tag, "sq"), (tag, "ss", s)],
                accum_out=sm[s][:, 0:1])
            ACT(kb, sm[s][:, 1:2], sm[s][:, 0:1], AF.Sqrt, [(tag, "ss", s)], [(tag, "sd", s)],
                bias=C["eps"][:, 0:1], scale=1.0 / D)
            RECIP(kb, sm[s][:, 2:3], sm[s][:, 1:2], [(tag, "sd", s)], [(tag, "rs", s)])
            TS(kb, "dve", xn[s][:], xt[s][:], sm[s][:, 2:3], None, ALU.mult, None,
               [(tag, "xt", s), (tag, "rs", s)], [(tag, "xn", s)])
            for half in range(2):
                p = half
                for kk in range(8):
                    k = half * 8 + kk
                    TR(kb, pst[p][:, kk, :], xn[s][:, k * 128:(k + 1) * 128], C["ident"][:],
                       [(tag, "xn", s)], [(tag, "pt", p)], signal=(kk == 7))
                nw = normwT[:, half * 8:(half + 1) * 8].unsqueeze(2).to_broadcast([128, 8, 128])
                TT(kb, "dve", hT[:, half * 8:(half + 1) * 8, tt * 128:(tt + 1) * 128], pst[p][:], nw,
                   ALU.mult, [(tag, "pt", p)], [("hT", tt)])
        kb.end_phase()


def gemm_fm(kb, tag, actT, act_keys, KC, w_dram, NCH, sink):
    with ExitStack() as st:
        wb = [kb.sb(f"{tag}_wb{i}", [128, KC * 128], BF16, st) for i in range(2)]
        wch = [kb.chan(f"{tag}_w{i}") for i in range(2)]
        pss = [kb.ps(f"{tag}_ps{i}", [128, 1024], F32, st) for i in range(2)]
        it = 0
        for c in range(NCH):
            s = c % 2
            kb.dma("pool", wch[s], wb[s][:], w_dram[c], writes=[(tag, "wb", s)])
            for ts in range(4):
                p = it % 2
                it += 1
                for b in range(2):
                    t0 = ts * 1024 + b * 512
                    for k in range(KC):
                        MM(kb, pss[p][:, b * 512:(b + 1) * 512], wb[s][:, k * 128:(k + 1) * 128],
                           actT[:, k, t0:t0 + 512], k == 0, k == KC - 1,
                           [(tag, "wb", s)] + act_keys, [(tag, "ps", p)])
                sink(c, ts, pss[p], (tag, "ps", p), st)
        kb.end_phase()


class StoreSink:
    def __init__(self, kb, tag, dst, st):
        self.kb = kb
        self.tag = tag
        self.dst = dst
        self.stg = [kb.sb(f"{tag}_stg{i}", [128, 1024], F32, st) for i in range(3)]
        self.ch = [kb.chan(f"{tag}_st{i}") for i in range(3)]
        self.i = 0

    def __call__(self, c, ts, ps, pkey, st):
        kb = self.kb
        s = self.i % 3
        eng = "act" if self.i % 2 == 0 else "dve"
        self.i += 1
        CP(kb, eng, self.stg[s][:], ps[:], [pkey], [(self.tag, "stg", s)])
        kb.dma("sp", self.ch[s], self.dst[c * 128:(c + 1) * 128, ts * 1024:(ts + 1) * 1024], self.stg[s][:],
               reads=[(self.tag, "stg", s)], writes=[(self.tag, "dst", c)])


def phase_inproj(kb, C, tag, x_dram, normwT, w_dram, NCH, pj):
    with ExitStack() as st:
        hT = kb.sb(f"{tag}_hT", [128, 16, T], BF16, st)
        phase_norm_T(kb, C, x_dram, normwT, hT, tag + "n")
        with ExitStack() as st2:
            sink = StoreSink(kb, tag + "s", pj, st2)
            gemm_fm(kb, tag + "g", hT, [], 16, w_dram, NCH, sink)


def phase_outproj(kb, C, tag, yT_dram, wo_dram, xres_dram, out_dram):
    with ExitStack() as st:
        wo = kb.sb(f"{tag}_wo", [128, 16, D], BF16, st)
        wch = kb.chan(f"{tag}_w")
        for k in range(16):
            kb.dma("pool", wch, wo[:, k, :], wo_dram[k], writes=[(tag, "wo")])
        yt = [kb.sb(f"{tag}_yt{i}", [128, 16, 128], BF16, st) for i in range(2)]
        xr = [kb.sb(f"{tag}_xr{i}", [128, D], F32, st) for i in range(2)]
        ot = [kb.sb(f"{tag}_ot{i}", [128, D], F32, st) for i in range(2)]
        ps = [kb.ps(f"{tag}_ps{i}", [128, 512], F32, st) for i in range(4)]
        chy = [kb.chan(f"{tag}_y{i}") for i in range(2)]
        chx = [kb.chan(f"{tag}_x{i}") for i in range(2)]
        cho = [kb.chan(f"{tag}_o{i}") for i in range(2)]
        yv = yT_dram.rearrange("(k p) t -> p k t", p=128)
        for tt in range(NT):
            s = tt % 2
            kb.dma("sp", chy[s], yt[s][:], yv[:, :, tt * 128:(tt + 1) * 128], writes=[(tag, "yt", s)])
            kb.dma("sp", chx[s], xr[s][:], xres_dram[tt * 128:(tt + 1) * 128, :], writes=[(tag, "xr", s)])
            for nb in range(4):
                for k in range(16):
                    MM(kb, ps[nb][:], yt[s][:, k, :], wo[:, k, nb * 512:(nb + 1) * 512], k == 0, k == 15,
                       [(tag, "yt", s), (tag, "wo")], [(tag, "ps", nb)])
                TT(kb, "dve", ot[s][:, nb * 512:(nb + 1) * 512], ps[nb][:], xr[s][:, nb * 512:(nb + 1) * 512],
                   ALU.add, [(tag, "ps", nb), (tag, "xr", s)], [(tag, "ot", s)])
            kb.dma("pool", cho[s], out_dram[tt * 128:(tt + 1) * 128, :], ot[s][:],
                   reads=[(tag, "ot", s)], writes=[(tag, "out", tt)])
        kb.end_phase()


def phase_cd_mix(kb, C, pj, cw_unused, y_dram):
    with ExitStack() as stc:
        cw = {}
        chc = kb.chan("cd_const")
        for name, shape, dt in CD_CONST_SPECS:
            d = kb.nc.dram_tensor(name, list(shape), dt, kind="ExternalInput").ap()
            t = kb.sb("c_" + name, shape, dt, stc)
            kb.dma("sp", chc, t[:], d, writes=[("const", name)])
            cw[name] = t
        kb.end_phase()
        _phase_cd_mix(kb, C, pj, cw, y_dram)


def _phase_cd_mix(kb, C, pj, cw, y_dram):
    HB = 2048
    tag = "cd"
    with ExitStack() as st:
        QB = 1024
        ident_b = kb.sb("cd_identb", [128, 128], BF16, st)
        dg = kb.sb("cd_dg", [128, 8, 31, 128], BF16, st)
        uc = kb.sb("cd_uc", [128, 8, QB], F32, st)
        ab = [kb.sb(f"cd_a{i}", [128, 30 + QB], F32, st) for i in range(2)]
        gb = [kb.sb(f"cd_g{i}", [128, 30 + QB], F32, st) for i in range(2)]
        ub = [kb.sb(f"cd_u{i}", [128, 30 + QB], BF16, st) for i in range(2)]
        sq = [kb.sb(f"cd_sq{i}", [128, QB], F32, st) for i in range(2)]
        mean = kb.sb("cd_mean", [128, QB], F32, st)
        rstd = kb.sb("cd_rstd", [128, QB], F32, st)
        m2 = kb.sb("cd_m2", [128, QB], F32, st)
        zb = [kb.sb(f"cd_z{i}", [128, QB], F32, st) for i in range(2)]
        yb = [kb.sb(f"cd_y{i}", [128, QB], BF16, st) for i in range(2)]
        pc = [kb.ps(f"cd_pc{i}", [128, QB], F32, st) for i in range(2)]
        ps_sum = kb.ps("cd_pss", [128, QB], F32, st)
        ps_ssq = kb.ps("cd_psq", [128, QB], F32, st)
        cha = [kb.chan(f"cd_a{i}") for i in range(2)]
        chg = [kb.chan(f"cd_g{i}") for i in range(2)]
        chz = [kb.chan(f"cd_z{i}") for i in range(2)]
        chy = [kb.chan(f"cd_y{i}") for i in range(2)]
        CP(kb, "dve", ident_b[:], C["ident"][:], [], [(tag, "cst")])
        for cc in range(8):
            TT(kb, "dve", dg[:, cc, :, :], ident_b[:].unsqueeze(1).to_broadcast([128, 31, 128]),
               cw["dww"][:, cc, :].unsqueeze(2).to_broadcast([128, 31, 128]), ALU.mult, [(tag, "cst")], [(tag, "dg")])
        it = 0
        for tq in range(T // QB):
            t0 = tq * QB
            for cc in range(8):
                s = it % 2
                it += 1
                r0 = cc * 128
                if tq == 0:
                    MS(kb, "pool", ab[s][:, 0:30], 0.0, [(tag, "a", s)])
                    MS(kb, "pool", gb[s][:, 0:30], 0.0, [(tag, "g", s)])
                    kb.dma("sp", cha[s], ab[s][:, 30:30 + QB], pj[r0:r0 + 128, 0:QB], writes=[(tag, "a", s)])
                    kb.dma("sp", chg[s], gb[s][:, 30:30 + QB], pj[1024 + r0:1024 + r0 + 128, 0:QB],
                           writes=[(tag, "g", s)])
                else:
                    kb.dma("sp", cha[s], ab[s][:], pj[r0:r0 + 128, t0 - 30:t0 + QB], writes=[(tag, "a", s)])
                    kb.dma("sp", chg[s], gb[s][:], pj[1024 + r0:1024 + r0 + 128, t0 - 30:t0 + QB],
                           writes=[(tag, "g", s)])
                ACT(kb, gb[s][:], gb[s][:], AF.Sigmoid, [(tag, "g", s)], [(tag, "g", s)])
                TT(kb, "dve", ub[s][:], ab[s][:], gb[s][:], ALU.mult, [(tag, "a", s), (tag, "g", s)], [(tag, "u", s)])
                for b in range(QB // 512):
                    for j in range(31):
                        MM(kb, pc[s][:, b * 512:(b + 1) * 512], dg[:, cc, j, :], ub[s][:, j + b * 512:j + b * 512 + 512],
                           j == 0, j == 30, [(tag, "dg"), (tag, "u", s)], [(tag, "pc", s)])
                ACT(kb, uc[:, cc, :], pc[s][:], AF.Identity, [(tag, "pc", s)], [(tag, "uc", cc)],
                    bias=cw["dwb"][:, cc:cc + 1])
                ACT(kb, sq[s][:], uc[:, cc, :], AF.Square, [(tag, "uc", cc)], [(tag, "sq", s)])
                for tb in range(QB // 512):
                    last = tb == QB // 512 - 1
                    MM(kb, ps_sum[:, tb * 512:(tb + 1) * 512], C["ones_f"][:], uc[:, cc, tb * 512:(tb + 1) * 512],
                       cc == 0, cc == 7, [(tag, "uc", cc)], [(tag, "pss")], signal=last)
                    MM(kb, ps_ssq[:, tb * 512:(tb + 1) * 512], C["ones_f"][:], sq[s][:, tb * 512:(tb + 1) * 512],
                       cc == 0, cc == 7, [(tag, "sq", s)], [(tag, "psq")], signal=last)
            ACT(kb, mean[:], ps_sum[:], AF.Copy, [(tag, "pss")], [(tag, "mean")], scale=1.0 / 1024)
            TT(kb, "dve", m2[:], mean[:], mean[:], ALU.mult, [(tag, "mean")], [(tag, "m2")])
            STT(kb, m2[:], ps_ssq[:], 1.0 / 1024, m2[:], ALU.mult, ALU.subtract, [(tag, "psq"), (tag, "m2")],
                [(tag, "m2")])
            ACT(kb, m2[:], m2[:], AF.Sqrt, [(tag, "m2")], [(tag, "m2")], bias=C["eps"][:, 0:1])
            RECIP(kb, rstd[:], m2[:], [(tag, "m2")], [(tag, "rstd")])
            for cc in range(8):
                s = cc % 2
                r0 = cc * 128
                kb.dma("sp", chz[s], zb[s][:], pj[2048 + r0:2048 + r0 + 128, t0:t0 + QB], writes=[(tag, "z", s)])
                TT(kb, "dve", uc[:, cc, :], uc[:, cc, :], mean[:], ALU.subtract, [(tag, "uc", cc), (tag, "mean")],
                   [(tag, "uc", cc)])
                TT(kb, "dve", uc[:, cc, :], uc[:, cc, :], rstd[:], ALU.mult, [(tag, "uc", cc), (tag, "rstd")],
                   [(tag, "uc", cc)])
                ACT(kb, uc[:, cc, :], uc[:, cc, :], AF.Silu, [(tag, "uc", cc)], [(tag, "uc", cc)],
                    scale=cw["lnw"][:, cc:cc + 1], bias=cw["lnb"][:, cc:cc + 1])
                ACT(kb, zb[s][:], zb[s][:], AF.Silu, [(tag, "z", s)], [(tag, "z", s)])
                TT(kb, "dve", yb[s][:], uc[:, cc, :], zb[s][:], ALU.mult, [(tag, "uc", cc), (tag, "z", s)],
                   [(tag, "y", s)])
                kb.dma("pool", chy[s], y_dram[r0:r0 + 128, t0:t0 + QB], yb[s][:], reads=[(tag, "y", s)],
                       writes=[(tag, "yd", cc, tq)])
        kb.end_phase()
    tag = "sc"
    with ExitStack() as st:
        W = T + 2
        bg = [kb.sb(f"sc_b{i}", [128, T], F32, st) for i in range(2)]
        cg = [kb.sb(f"sc_c{i}", [128, W], F32, st) for i in range(2)]
        ud = [kb.sb(f"sc_u{i}", [128, W], F32, st) for i in range(2)]
        zd = [kb.sb(f"sc_z{i}", [128, T], F32, st) for i in range(2)]
        acc = [kb.sb(f"sc_acc{i}", [128, T], F32, st) for i in range(2)]
        yb = [kb.sb(f"sc_y{i}", [128, T], BF16, st) for i in range(2)]
        chs = {n: [kb.chan(f"sc_{n}{i}") for i in range(2)] for n in ("b", "c", "u", "z", "y")}
        for cc in range(8):
            s = cc % 2
            r0 = cc * 128
            MS(kb, "pool", cg[s][:, 0:2], 0.0, [(tag, "c", s)])
            MS(kb, "pool", ud[s][:, 0:2], 0.0, [(tag, "u", s)])
            kb.dma("sp", chs["b"][s], bg[s][:], pj[3072 + r0:3072 + r0 + 128, :], writes=[(tag, "b", s)])
            kb.dma("sp", chs["c"][s], cg[s][:, 2:W], pj[4096 + r0:4096 + r0 + 128, :], writes=[(tag, "c", s)])
            kb.dma("sp", chs["u"][s], ud[s][:, 2:W], pj[5120 + r0:5120 + r0 + 128, :], writes=[(tag, "u", s)])
            kb.dma("sp", chs["z"][s], zd[s][:], pj[6144 + r0:6144 + r0 + 128, :], writes=[(tag, "z", s)])
            TT(kb, "dve", cg[s][:], cg[s][:], ud[s][:], ALU.mult, [(tag, "c", s), (tag, "u", s)], [(tag, "c", s)])
            TS(kb, "dve", acc[s][:], cg[s][:, 2:W], cw["dcw"][:, cc, 2:3], None, ALU.mult, None,
               [(tag, "c", s)], [(tag, "acc", s)])
            for j in range(2):
                STT(kb, acc[s][:], cg[s][:, j:j + T], cw["dcw"][:, cc, j:j + 1], acc[s][:], ALU.mult, ALU.add,
                    [(tag, "c", s), (tag, "acc", s)], [(tag, "acc", s)])
            ACT(kb, zd[s][:], zd[s][:], AF.Silu, [(tag, "z", s)], [(tag, "z", s)])
            TT(kb, "pool", bg[s][:], bg[s][:], zd[s][:], ALU.mult, [(tag, "b", s), (tag, "z", s)], [(tag, "b", s)])
            TT(kb, "dve", yb[s][:], acc[s][:], bg[s][:], ALU.mult, [(tag, "acc", s), (tag, "b", s)], [(tag, "y", s)])
            kb.dma("pool", chs["y"][s], y_dram[1024 + r0:1024 + r0 + 128, :], yb[s][:], reads=[(tag, "y", s)],
                   writes=[(tag, "yd", cc)])
        kb.end_phase()


def colnorm_phase(kb, C, tag, src, KC, gT, outT):
    with ExitStack() as st:
        raw = [kb.sb(f"{tag}_raw{i}", [128, KC, 1024], F32, st) for i in range(2)]
        sq = [kb.sb(f"{tag}_sq{i}", [128, 1024], F32, st) for i in range(2)]
        rs = kb.sb(f"{tag}_rs", [128, 1024], F32, st)
        ssp = kb.ps(f"{tag}_ssp", [128, 1024], F32, st)
        ch = [kb.chan(f"{tag}_l{i}") for i in range(2)]
        sv = src.rearrange("(k p) t -> p k t", p=128)
        for ts in range(4):
            s = ts % 2
            kb.dma("sp", ch[s], raw[s][:], sv[:, :, ts * 1024:(ts + 1) * 1024], writes=[(tag, "raw", s)])
            for k in range(KC):
                q = k % 2
                ACT(kb, sq[q][:], raw[s][:, k, :], AF.Square, [(tag, "raw", s)], [(tag, "sq", q)])
                for b in range(2):
                    MM(kb, ssp[:, b * 512:(b + 1) * 512], C["ones_f"][:], sq[q][:, b * 512:(b + 1) * 512],
                       k == 0, k == KC - 1, [(tag, "sq", q)], [(tag, "ssp")], signal=True)
            ACT(kb, rs[:], ssp[:], AF.Sqrt, [(tag, "ssp")], [(tag, "rs")], bias=C["eps"][:, 0:1],
                scale=1.0 / (KC * 128))
            RECIP(kb, rs[:], rs[:], [(tag, "rs")], [(tag, "rs")])
            for k in range(KC):
                STT(kb, outT[:, k, ts * 1024:(ts + 1) * 1024], raw[s][:, k, :], gT[:, k:k + 1], rs[:],
                    ALU.mult, ALU.mult, [(tag, "raw", s), (tag, "rs")], [(tag, "out", k, ts)])
        kb.end_phase()


class HeadNormSink:
    def __init__(self, kb, C, tag, dst, gain, n_norm, raw_dst, st):
        self.kb, self.C, self.tag, self.dst, self.gain = kb, C, tag, dst, gain
        self.n_norm, self.raw_dst = n_norm, raw_dst
        self.sq = [kb.sb(f"{tag}_sq{i}", [128, 1024], F32, st) for i in range(2)]
        self.rs = [kb.sb(f"{tag}_rs{i}", [128, 1024], F32, st) for i in range(2)]
        self.ob = [kb.sb(f"{tag}_ob{i}", [128, 1024], BF16, st) for i in range(2)]
        self.ssp = kb.ps(f"{tag}_ssp", [128, 1024], F32, st)
        self.ch = [kb.chan(f"{tag}_o{i}") for i in range(2)]
        self.i = 0

    def __call__(self, c, ts, ps, pkey, st):
        kb, C, tag = self.kb, self.C, self.tag
        s = self.i % 2
        self.i += 1
        if c < self.n_norm:
            ACT(kb, self.sq[s][:], ps[:], AF.Square, [pkey], [(tag, "sq", s)])
            for b in range(2):
                MM(kb, self.ssp[:, b * 512:(b + 1) * 512], C["ones_f"][:], self.sq[s][:, b * 512:(b + 1) * 512],
                   True, True, [(tag, "sq", s)], [(tag, "ssp")])
            ACT(kb, self.rs[s][:], self.ssp[:], AF.Sqrt, [(tag, "ssp")], [(tag, "rs", s)], bias=C["eps"][:, 0:1],
                scale=1.0 / 128)
            RECIP(kb, self.rs[s][:], self.rs[s][:], [(tag, "rs", s)], [(tag, "rs", s)])
            STT(kb, self.ob[s][:], ps[:], self.gain, self.rs[s][:], ALU.mult, ALU.mult,
                [pkey, (tag, "rs", s)], [(tag, "ob", s)])
            d = self.dst[c]
        else:
            CP(kb, "act", self.ob[s][:], ps[:], [pkey], [(tag, "ob", s)])
            d = self.raw_dst[c - self.n_norm]
        kb.dma("sp", self.ch[s], d[:, ts * 1024:(ts + 1) * 1024], self.ob[s][:], reads=[(tag, "ob", s)],
               writes=[(tag, "dst", c)])


def phase_dsa_prep(kb, C, pj, W, S):
    with ExitStack() as st:
        cqn = kb.sb("cqn", [128, 4, T], BF16, st)
        colnorm_phase(kb, C, "cq", pj[0:512, :], 4, C["qnormT"], cqn)
        with ExitStack() as st2:
            sink = HeadNormSink(kb, C, "qs", S["qT"], C["qgain"][:, 0:1], 8, S["qiT"], st2)
            gemm_fm(kb, "qg", cqn, [], 4, W["w_uqiq"], 16, sink)
    with ExitStack() as st:
        ckvn = kb.sb("ckvn", [128, 2, T], BF16, st)
        colnorm_phase(kb, C, "ckv", pj[512:768, :], 2, C["kvnormT"], ckvn)
        with ExitStack() as st2:
            sink = HeadNormSink(kb, C, "ks", S["kT"], C["kgain"][:, 0:1], 8, None, st2)
            gemm_fm(kb, "kg", ckvn, [], 2, W["w_uk"], 8, sink)
        with ExitStack() as st2:
            wv = kb.sb("wv", [128, 2, 1024], BF16, st2)
            chw = kb.chan("wv")
            for k in range(2):
                kb.dma("pool", chw, wv[:, k, :], W["w_uv"][k], writes=[("wv",)])
            vps = [kb.ps(f"v_ps{i}", [128, 1024], F32, st2) for i in range(2)]
            vb = [kb.sb(f"v_b{i}", [128, 1024], BF16, st2) for i in range(2)]
            chv = [kb.chan(f"v_o{i}") for i in range(2)]
            for tt in range(NT):
                s = tt % 2
                for b in range(2):
                    for k in range(2):
                        MM(kb, vps[s][:, b * 512:(b + 1) * 512], ckvn[:, k, tt * 128:(tt + 1) * 128],
                           wv[:, k, b * 512:(b + 1) * 512], k == 0, k == 1, [("wv",)], [("v", "ps", s)])
                CP(kb, "act" if tt % 2 else "dve", vb[s][:], vps[s][:], [("v", "ps", s)], [("v", "b", s)])
                kb.dma("sp", chv[s], S["V"][tt * 128:(tt + 1) * 128, :], vb[s][:], reads=[("v", "b", s)],
                       writes=[("V", tt)])
            kb.end_phase()


def phase_ki_tmaj(kb, C, pj, S):
    with ExitStack() as st:
        ki = kb.sb("ki_raw", [128, T], F32, st)
        sq = kb.sb("ki_sq", [128, T], F32, st)
        mean = kb.sb("ki_mean", [128, 1024], F32, st)
        var = kb.sb("ki_var", [128, 1024], F32, st)
        kio = kb.sb("ki_o", [128, T], BF16, st)
        sm = kb.sb("ki_sm", [32, T], F32, st)
        tmo = kb.sb("ki_tmo", [128, NT, 32], F32, st)
        ps1 = kb.ps("ki_ps1", [128, 1024], F32, st)
        ps2 = kb.ps("ki_ps2", [128, 1024], F32, st)
        pst = kb.ps("ki_pst", [128, 16, 32], F32, st)
        ch = kb.chan("ki")
        kb.dma("sp", ch, ki[0:64, :], pj[768:832, :], writes=[("ki", "raw")])
        kb.dma("sp", ch, ki[64:128, :], pj[768:832, :], writes=[("ki", "raw")])
        kb.dma("sp", ch, sm[:], pj[896:928, :], writes=[("ki", "sm")])
        ACT(kb, sq[:], ki[:], AF.Square, [("ki", "raw")], [("ki", "sq")])
        for ts in range(4):
            for b in range(2):
                c0 = ts * 1024 + b * 512
                MM(kb, ps1[:, b * 512:(b + 1) * 512], C["ones_f"][0:64, :], ki[0:64, c0:c0 + 512], True, True,
                   [("ki", "raw")], [("ki", "ps1")])
                MM(kb, ps2[:, b * 512:(b + 1) * 512], C["ones_f"][0:64, :], sq[0:64, c0:c0 + 512], True, True,
                   [("ki", "sq")], [("ki", "ps2")])
            ACT(kb, mean[:], ps1[:], AF.Copy, [("ki", "ps1")], [("ki", "mean")], scale=1.0 / 64)
            TT(kb, "dve", var[:], mean[:], mean[:], ALU.mult, [("ki", "mean")], [("ki", "var")])
            STT(kb, var[:], ps2[:], 1.0 / 64, var[:], ALU.mult, ALU.subtract, [("ki", "ps2"), ("ki", "var")],
                [("ki", "var")])
            ACT(kb, var[:], var[:], AF.Sqrt, [("ki", "var")], [("ki", "var")], bias=C["eps"][:, 0:1])
            RECIP(kb, var[:], var[:], [("ki", "var")], [("ki", "var")])
            sl = slice(ts * 1024, (ts + 1) * 1024)
            TT(kb, "dve", ki[:, sl], ki[:, sl], mean[:], ALU.subtract, [("ki", "raw"), ("ki", "mean")], [("ki", "raw")])
            TT(kb, "dve", ki[:, sl], ki[:, sl], var[:], ALU.mult, [("ki", "raw"), ("ki", "var")], [("ki", "raw")])
            TS(kb, "dve", kio[:, sl], ki[:, sl], C["ikw"][:, 0:1], C["ikb"][:, 0:1], ALU.mult, ALU.add,
               [("ki", "raw")], [("ki", "o")])
        kb.dma("sp", ch, S["kiT"], kio[:], reads=[("ki", "o")], writes=[("kiT",)])
        for g in range(2):
            for tt in range(16):
                t = g * 16 + tt
                TR(kb, pst[:, tt, :], sm[:, t * 128:(t + 1) * 128], C["ident"][0:32, 0:32], [("ki", "sm")],
                   [("ki", "pst")], signal=(tt == 15))
            CP(kb, "dve", tmo[:, g * 16:(g + 1) * 16, :], pst[:], [("ki", "pst")], [("ki", "tmo")])
        kb.dma("sp", ch, S["tmaj"].rearrange("(n p) c -> p n c", p=128), tmo[:], reads=[("ki", "tmo")],
               writes=[("tmaj",)])
        kb.end_phase()


N_BIS = 16
SCALE_A = 128 ** -0.5


def phase_dsa_attn(kb, C, pj, S, y_dram):
    tag = "at"
    with ExitStack() as st:
        kT = kb.sb("at_kT", [128, 8, T], BF16, st)
        Vt = kb.sb("at_V", [128, NT, 1024], BF16, st)
        kiT = kb.sb("at_kiT", [128, T], BF16, st)
        score1 = kb.sb("at_sc", [128, T], F32, st)
        score = [score1, score1]
        mask = kb.sb("at_mask", [128, T], BF16, st)
        junk = mask
        maskT1 = kb.sb("at_maskT", [128, NT, 128], BF16, st)
        maskT = [maskT1, maskT1]
        rbuf = [kb.sb(f"at_r{i}", [128, 2, 512], BF16, st) for i in range(2)]
        dsg = kb.sb("at_dsg", [128, 16, 128], BF16, st)
        qb = [kb.sb(f"at_q{i}", [128, 8, 128], BF16, st) for i in range(2)]
        qib1 = kb.sb("at_qi", [128, 8, 128], BF16, st)
        qib = [qib1, qib1]
        wt = [kb.sb(f"at_w{i}", [128, 16], F32, st) for i in range(2)]
        bs = [kb.sb(f"at_bs{i}", [128, 8], F32, st) for i in range(2)]
        wks1 = kb.sb("at_wk", [128, N_BIS], F32, st)
        wks = [wks1, wks1]
        fpow = kb.sb("at_fpow", [128, N_BIS], F32, st)
        za1 = kb.sb("at_za", [128, 8, 128], F32, st)
        za = [za1, za1]
        pt = [kb.sb(f"at_pt{i}", [128, 4, 128], BF16, st) for i in range(2)]
        ost1 = kb.sb("at_ost", [128, 8, 256], F32, st)
        ost = [ost1, ost1]
        yst1 = kb.sb("at_yst", [128, 8, 128], BF16, st)
        yst = [yst1, yst1]
        biasS = C["biasS"]
        ident_b = kb.sb("at_identb", [128, 128], BF16, st)
        ones_b = kb.sb("at_onesb", [128, 128], BF16, st)
        ips = [kb.ps(f"at_ips{i}", [128, 2, 512], F32, st) for i in range(2)]
        lg = [ips[i][:, 0, :].rearrange("p (a b) -> p a b", b=128) for i in range(2)]
        po1 = kb.ps("at_po", [128, 128], F32, st)
        prs1 = kb.ps("at_prs", [128, 128], F32, st)
        po, prs = [po1, po1], [prs1, prs1]
        sps1 = kb.ps("at_sps", [128, 512], F32, st)
        sps = [sps1, sps1]
        tps = ips[0][:, 0, :].bitcast(BF16)[:, 0:512].rearrange("p (a b) -> p a b", b=128)
        chl = kb.chan("at_ld")
        chq = [kb.chan(f"at_q{i}") for i in range(2)]
        chy = [kb.chan(f"at_y{i}") for i in range(2)]
        chz = kb.chan("at_z")
        kb.dma("sp", chl, kT[:], S["kT"].rearrange("h p t -> p h t"), writes=[(tag, "kT")])
        kb.dma("sp", chl, Vt[:], S["V"].rearrange("(n p) c -> p n c", p=128), writes=[(tag, "V")])
        kb.dma("sp", chl, kiT[:], S["kiT"], writes=[(tag, "kiT")])
        CP(kb, "dve", ident_b[:], C["ident"][:], [], [(tag, "cst")])
        CP(kb, "dve", ones_b[:], C["ones_f"][:], [], [(tag, "cst")])
        for k in range(1, N_BIS + 1):
            MS(kb, "dve", fpow[:, k - 1:k], 2.0 ** -k, [(tag, "cst")])
        qTv = S["qT"].rearrange("h p t -> p h t")
        qiTv = S["qiT"].rearrange("h p t -> p h t")
        zav = pj[1024:2048, :].rearrange("(h p) t -> p h t", p=128)
        yv = y_dram[0:1024, :].rearrange("(h p) t -> p h t", p=128)
        cnt = {"ips": 0, "r": 0, "lg": 0, "pt": 0, "po": 0, "sps": 0}

        def stage_a(i):
            s = i % 2
            c0, c1 = i * 128, (i + 1) * 128
            kb.dma("sp", chq[s], qb[s][:], qTv[:, :, c0:c1], writes=[(tag, "q", s)])
            kb.dma("sp", chq[s], qib[s][:], qiTv[:, :, c0:c1], writes=[(tag, "qi")])
            kb.dma("sp", chq[s], wt[s][:], S["tmaj"][c0:c1, 0:16], writes=[(tag, "w", s)])
            nW = (i + 4) // 4
            Wi = nW * 512
            sk = (tag, "score")
            TT(kb, "dve", dsg[:], ident_b[:].unsqueeze(1).to_broadcast([128, 16, 128]),
               wt[s][:, 0:16].unsqueeze(2).to_broadcast([128, 16, 128]), ALU.mult, [(tag, "w", s), (tag, "cst")],
               [(tag, "dsg")])
            for w in range(nW):
                sp_ = cnt["sps"] % 2
                cnt["sps"] += 1
                units = []

                def acc(u):
                    h0, r_ = u
                    for e_ in range(2):
                        MM(kb, sps[sp_][:], dsg[:, h0 + e_, :], rbuf[r_][:, e_, :], h0 + e_ == 0, h0 + e_ == 15,
                           [(tag, "dsg"), (tag, "r", r_)], [(tag, "sps", 0)], signal=(e_ == 1))

                for pair in range(8):
                    p = cnt["ips"] % 2
                    cnt["ips"] += 1
                    r = cnt["r"] % 2
                    cnt["r"] += 1
                    for e_ in range(2):
                        base = e_ * 64
                        MM(kb, ips[p][:, e_, :], qib[s][base:base + 64, pair, :],
                           kiT[base:base + 64, w * 512:(w + 1) * 512], True, True, [(tag, "qi"), (tag, "kiT")],
                           [(tag, "ips", p)], signal=(e_ == 1))
                    if pair % 4 != 3:
                        ACT(kb, rbuf[r][:], ips[p][:], AF.Relu, [(tag, "ips", p)], [(tag, "r", r)])
                    else:
                        TS(kb, "dve", rbuf[r][:], ips[p][:], 0.0, None, ALU.max, None, [(tag, "ips", p)], [(tag, "r", r)])
                    if units:
                        acc(units.pop())
                    units.append((2 * pair, r))
                acc(units.pop())
                CP(kb, "dve", score[s][:, w * 512:(w + 1) * 512], sps[sp_][:], [(tag, "sps", 0)], [sk])
            b = bs[s]
            bk = (tag, "bs", s)
            TS(kb, "dve", junk[:, 0:Wi], score[s][:, 0:Wi], 1.0, None, ALU.mult, ALU.max, [sk], [(tag, "mask"), bk],
               accum_out=b[:, 0:1])
            TS(kb, "dve", junk[:, 0:Wi], score[s][:, 0:Wi], -1.0, None, ALU.mult, ALU.max, [sk], [(tag, "mask"), bk],
               accum_out=b[:, 6:7])
            TT(kb, "dve", b[:, 0:1], b[:, 0:1], b[:, 6:7], ALU.max, [bk], [bk])
            TT(kb, "dve", score[s][:, Wi - 512:Wi], score[s][:, Wi - 512:Wi],
               C["cbase"][:, 384 - (i % 4) * 128:896 - (i % 4) * 128], ALU.add, [sk], [sk])
            TS(kb, "dve", b[:, 1:2], b[:, 0:1], -1.001, -1e-20, ALU.mult, ALU.add, [bk], [bk])
            TS(kb, "dve", b[:, 2:3], b[:, 0:1], 2.002, 2e-20, ALU.mult, ALU.add, [bk], [bk])
            wk = wks[s]
            TS(kb, "dve", wk[:], fpow[:], b[:, 2:3], None, ALU.mult, None, [bk, (tag, "cst")], [bk])
            TT(kb, "dve", b[:, 3:4], b[:, 1:2], wk[:, 0:1], ALU.add, [bk], [bk])
            for k in range(1, N_BIS + 1):
                TS(kb, "dve", junk[:, 0:Wi], score[s][:, 0:Wi], b[:, 3:4], None, ALU.is_ge, ALU.add,
                   [sk, bk], [(tag, "mask"), bk], accum_out=b[:, 4:5])
                TS(kb, "dve", b[:, 5:6], b[:, 4:5], 255.5, -0.5, ALU.is_ge, ALU.add, [bk], [bk])
                STT(kb, b[:, 3:4], b[:, 5:6], wk[:, k - 1:k], b[:, 3:4], ALU.mult, ALU.add, [bk], [bk])
            STT(kb, b[:, 1:2], wk[:, N_BIS - 1:N_BIS], -0.5, b[:, 3:4], ALU.mult, ALU.add, [bk], [bk])

        def stage_t(i):
            s = i % 2
            n = i + 1
            TS(kb, "dve", mask[:, 0:n * 128], score[s][:, 0:n * 128], bs[s][:, 1:2], None, ALU.is_ge, None,
               [(tag, "score"), (tag, "bs", s)], [(tag, "mask")])
            for j0 in range(0, n, 4):
                nb = min(4, n - j0)
                for jj in range(nb):
                    j = j0 + jj
                    TR(kb, tps[:, jj, :], mask[:, j * 128:(j + 1) * 128], ident_b[:], [(tag, "mask"), (tag, "cst")],
                       [(tag, "ips", 0)], signal=(jj == nb - 1))
                ACT(kb, maskT[s][:, j0:j0 + nb, :], tps[:, 0:nb, :], AF.Copy, [(tag, "ips", 0)], [(tag, "maskT")],
                    scale=30000.0, bias=-30000.0)

        def stage_b(i):
            s = i % 2
            n = i + 1
            groups = [(h, j0, min(4, n - j0)) for h in range(8) for j0 in range(0, n, 4)]
            slots = {}

            def qk(gi):
                h, j0, nb = groups[gi]
                p = cnt["lg"] % 2
                cnt["lg"] += 1
                x = cnt["pt"] % 2
                cnt["pt"] += 1
                slots[gi] = x
                for jj in range(nb):
                    j = j0 + jj
                    near = (i - j) <= 1
                    MM(kb, lg[p][:, jj, :], kT[:, h, j * 128:(j + 1) * 128], qb[s][:, h, :], True, False,
                       [(tag, "kT"), (tag, "q", s)], [(tag, "ips", p)], signal=False)
                    if near:
                        MM(kb, lg[p][:, jj, :], ident_b[:], biasS[:, h, i - j, :], False, False,
                           [(tag, "cst")], [(tag, "ips", p)], signal=False)
                    MM(kb, lg[p][:, jj, :], ident_b[:], maskT[s][:, j, :], False, True,
                       [(tag, "cst"), (tag, "maskT")], [(tag, "ips", p)], signal=(jj == nb - 1))
                ACT(kb, pt[x][:, 0:nb, :], lg[p][:, 0:nb, :], AF.Exp, [(tag, "ips", p)], [(tag, "pt", x)],
                    scale=SCALE_A, bias=C["cb"][:, h:h + 1])

            def pv(gi):
                h, j0, nb = groups[gi]
                x = slots.pop(gi)
                for jj in range(nb):
                    j = j0 + jj
                    MM(kb, po[0][:], Vt[:, j, h * 128:(h + 1) * 128], pt[x][:, jj, :], j == 0, j == n - 1,
                       [(tag, "V"), (tag, "pt", x)], [(tag, "po", 0)])
                    MM(kb, prs[0][:], ones_b[:], pt[x][:, jj, :], j == 0, j == n - 1,
                       [(tag, "cst"), (tag, "pt", x)], [(tag, "prs", 0)], signal=(jj == nb - 1))
                if j0 + nb == n:
                    CP(kb, "act", ost[s][:, h, 0:128], po[0][:], [(tag, "po", 0)], [(tag, "ost", h)])
                    CP(kb, "act", ost[s][:, h, 128:256], prs[0][:], [(tag, "prs", 0)], [(tag, "ost", h)])

            qk(0)
            for gi in range(len(groups)):
                if gi + 1 < len(groups):
                    qk(gi + 1)
                pv(gi)

        def stage_f(i):
            s = i % 2
            keys = [(tag, "ost", h) for h in range(8)]
            kb.dma("sp", chz, za[s][:], zav[:, :, i * 128:(i + 1) * 128], writes=[(tag, "za")])
            ACT(kb, za[s][:], za[s][:], AF.Silu, [(tag, "za")], [(tag, "za")])
            RECIP(kb, ost[s][:, :, 128:256], ost[s][:, :, 128:256], keys, keys)
            TT(kb, "dve", ost[s][:, :, 0:128], ost[s][:, :, 0:128], ost[s][:, :, 128:256], ALU.mult, keys, keys)
            TT(kb, "dve", yst[s][:], ost[s][:, :, 0:128], za[s][:], ALU.mult, keys + [(tag, "za")], [(tag, "yst")])
            kb.dma("pool", chy[s], yv[:, :, i * 128:(i + 1) * 128], yst[s][:], reads=[(tag, "yst")],
                   writes=[(tag, "y", i)])

        stage_a(0)
        stage_t(0)
        for i in range(NT):
            if i + 1 < NT:
                stage_a(i + 1)
            stage_b(i)
            if i + 1 < NT:
                stage_t(i + 1)
            stage_f(i)
        kb.end_phase()


def phase_gdn_prep(kb, C, pj, S):
    tag = "gp"
    with ExitStack() as st:
        W = T + 3
        HB = T // 2
        ident_b = kb.sb("gp_identb", [128, 128], BF16, st)
        dg = kb.sb("gp_dg", [128, 24, 4, 128], BF16, st)
        rawb = [kb.sb(f"gp_rawb{i}", [128, W], BF16, st) for i in range(2)]
        acc = [kb.sb(f"gp_acc{i}", [128, T], F32, st) for i in range(2)]
        sq = [kb.sb(f"gp_sq{i}", [128, T], F32, st) for i in range(2)]
        rn = [kb.sb(f"gp_rn{i}", [128, T], F32, st) for i in range(2)]
        pcv = kb.ps("gp_pcv", [128, HB], F32, st)
        ssp = kb.ps("gp_ssp", [128, HB], F32, st)
        chl = [kb.chan(f"gp_l{i}") for i in range(2)]
        chs = [kb.chan(f"gp_s{i}") for i in range(2)]
        CP(kb, "dve", ident_b[:], C["ident"][:], [], [(tag, "cst")])
        for cc in range(24):
            TT(kb, "dve", dg[:, cc, :, :], ident_b[:].unsqueeze(1).to_broadcast([128, 4, 128]),
               C["cvw"][:, cc, :].unsqueeze(2).to_broadcast([128, 4, 128]), ALU.mult, [(tag, "cst")], [(tag, "dg")])

        def head(cc):
            s = cc % 2
            r0 = 2048 + cc * 128
            MS(kb, "dve", rawb[s][:, 0:3], 0.0, [(tag, "raw", s)])
            kb.dma("pool", chl[s], rawb[s][:, 3:W], pj[r0:r0 + 128, :], writes=[(tag, "raw", s)])
            for hb in range(2):
                for b in range(4):
                    c0 = hb * HB + b * 512
                    for j in range(4):
                        MM(kb, pcv[:, b * 512:(b + 1) * 512], dg[:, cc, j, :], rawb[s][:, c0 + j:c0 + j + 512],
                           j == 0, j == 3, [(tag, "dg"), (tag, "raw", s)], [(tag, "pcv")], signal=(j == 3 and b == 3))
                ACT(kb, acc[s][:, hb * HB:(hb + 1) * HB], pcv[:], AF.Silu, [(tag, "pcv")], [(tag, "acc", s, hb)])
                if cc < 16:
                    ACT(kb, sq[s][:, hb * HB:(hb + 1) * HB], acc[s][:, hb * HB:(hb + 1) * HB], AF.Square,
                        [(tag, "acc", s, hb)], [(tag, "sq", s, hb)])
                    for b in range(4):
                        c0 = hb * HB + b * 512
                        MM(kb, ssp[:, b * 512:(b + 1) * 512], C["ones_f"][:], sq[s][:, c0:c0 + 512], True, True,
                           [(tag, "sq", s, hb)], [(tag, "ssp")], signal=(b == 3))
                    ACT(kb, rn[s][:, hb * HB:(hb + 1) * HB], ssp[:], AF.Sqrt, [(tag, "ssp")],
                        [(tag, "rn", s, hb)], bias=C["eps"][:, 0:1])

        def tail(cc):
            s = cc % 2
            akeys = [(tag, "acc", s, 0), (tag, "acc", s, 1)]
            if cc < 16:
                keys = [(tag, "rn", s, 0), (tag, "rn", s, 1)]
                RECIP(kb, rn[s][:], rn[s][:], keys, keys)
                STT(kb, acc[s][:], acc[s][:], (128 ** -0.5) if cc < 8 else 1.0, rn[s][:], ALU.mult, ALU.mult,
                    akeys + keys, akeys)
            kb.dma("sp", chs[s], S["gqkv"][cc * 128:(cc + 1) * 128, :], acc[s][:], reads=akeys,
                   writes=[("gqkv", cc)])

        head(0)
        for cc in range(24):
            if cc + 1 < 24:
                head(cc + 1)
            tail(cc)
        kb.end_phase()


import os
GDN_TILES = int(os.environ.get("GDN_TILES", "32"))
GDN_STOP = int(os.environ.get("GDN_STOP", "99"))
GDN_SUB = float(os.environ.get("GDN_SUB", "99"))


def phase_gdn(kb, C, pj, S, y_dram):
    with ExitStack() as stc:
        C = dict(C)
        chc = kb.chan("gd_const")
        for name, shape, dt in GDN_CONST_SPECS:
            d = kb.nc.dram_tensor(name, list(shape), dt, kind="ExternalInput").ap()
            t = kb.sb("c_" + name, shape, dt, stc)
            kb.dma("sp", chc, t[:], d, writes=[("const", name)])
            C[name] = t
        kb.end_phase()
        phase_gdn_prep(kb, C, pj, S)
        _phase_gdn_main(kb, C, pj, S, y_dram)


def _phase_gdn_main(kb, C, pj, S, y_dram):
    tag = "gd"
    H = 8
    with ExitStack() as st:
        def fb(name, shape=(128, H, 128), dt=F32):
            return kb.sb("gd_" + name, list(shape), dt, st)

        tm = fb("tm", (128, NT, 32))
        beta = fb("beta", (128, NT, 8))
        g = fb("g", (128, NT, 8))
        t1 = fb("t1", (128, NT, 8))
        t2 = fb("t2", (128, NT, 8))
        nA = fb("nA", (128, 8))
        qT, kT, vT = fb("qT"), fb("kT"), fb("vT")
        gd, egrow, gcr = fb("gdiag"), fb("egrow"), fb("gcr")
        P1, E1, E2 = fb("P1"), fb("E1"), fb("E2")
        HS = (128, H, 128)
        X = [fb("X0", HS, BF16), fb("X1", HS, BF16)]
        Y = [fb("Y0", HS, BF16), fb("Y1", HS, BF16)]
        P, attnT = fb("P", HS, BF16), fb("attnT", HS, BF16)
        vb, kbg, kd, kd1 = fb("vb", HS, BF16), fb("kbg", HS, BF16), fb("kd", HS, BF16), fb("kd1", HS, BF16)
        smk = fb("smk", (128, 16))
        u, wT, qgT, vnew = fb("u"), fb("wT", HS, BF16), fb("qgT", HS, BF16), fb("vnew", HS, BF16)
        Sst, oacc, zb = fb("S"), fb("oacc"), fb("zb")
        Sb = fb("Sb", HS, BF16)
        ident_b = fb("identb", (128, 128), BF16)
        osq, orn = fb("osq"), fb("orn")
        yo = fb("yo", (128, H, 128), BF16)
        sm = fb("sm", (128, 64))
        pA = kb.ps("gd_pA", [128, H, 128], F32, st)
        pB = kb.ps("gd_pB", [128, H, 128], F32, st)
        pC = kb.ps("gd_pC", [128, H, 128], F32, st)
        pO = kb.ps("gd_pO", [128, H, 64], F32, st)
        psm = kb.ps("gd_psm", [128, 32], F32, st)
        ch = kb.chan("gd_l")
        chq = kb.chan("gd_q")
        chz = kb.chan("gd_z")
        chy = kb.chan("gd_y")
        K_ = lambda n: (tag, n)

        def bc_h(ap2d):
            return ap2d.unsqueeze(1).to_broadcast([128, H, 128])

        def bc_f(ap2d):
            return ap2d.unsqueeze(2).to_broadcast([128, H, 128])

        kb.dma("sp", ch, tm[:], S["tmaj"].rearrange("(n p) c -> p n c", p=128), writes=[K_("tm")])
        ACT(kb, beta[:], tm[:, :, 16:24], AF.Sigmoid, [K_("tm")], [K_("beta")])
        dtb = C["dtb_bc"][:].unsqueeze(1).to_broadcast([128, NT, 8])
        TT(kb, "dve", g[:], tm[:, :, 24:32], dtb, ALU.add, [K_("tm")], [K_("g")])
        TS(kb, "dve", t1[:], g[:], -1.0, None, ALU.mult, None, [K_("g")], [K_("t1")])
        TT(kb, "dve", t1[:], t1[:], g[:], ALU.max, [K_("t1"), K_("g")], [K_("t1")])
        ACT(kb, t1[:], t1[:], AF.Exp, [K_("t1")], [K_("t1")], scale=-1.0)
        TS(kb, "dve", t1[:], t1[:], 1.0, None, ALU.add, None, [K_("t1")], [K_("t1")])
        ACT(kb, t1[:], t1[:], AF.Ln, [K_("t1")], [K_("t1")])
        TS(kb, "dve", t2[:], g[:], 0.0, None, ALU.max, None, [K_("g")], [K_("t2")])
        TT(kb, "dve", t2[:], t2[:], t1[:], ALU.add, [K_("t1"), K_("t2")], [K_("t2")])
        ACT(kb, nA[:], C["alog_bc"][:], AF.Exp, [], [K_("nA")])
        TS(kb, "dve", nA[:], nA[:], -1.0, None, ALU.mult, None, [K_("nA")], [K_("nA")])
        TT(kb, "dve", g[:], t2[:], nA[:].unsqueeze(1).to_broadcast([128, NT, 8]), ALU.mult, [K_("t2"), K_("nA")],
           [K_("g")])
        MS(kb, "dve", Sst[:], 0.0, [K_("S")])
        MS(kb, "dve", Sb[:], 0.0, [K_("Sb")])
        MS(kb, "dve", vnew[:], 0.0, [K_("vnew")])
        CP(kb, "dve", ident_b[:], C["ident"][:], [], [K_("identb")])
        pAb = pA[:, 0:4, :].bitcast(BF16).rearrange("p a (c b) -> p (a c) b", b=128)
        gq = S["gqkv"]
        qv = gq[0:1024, :].rearrange("(h p) t -> p h t", p=128)
        kv = gq[1024:2048, :].rearrange("(h p) t -> p h t", p=128)
        vv = gq[2048:3072, :].rearrange("(h p) t -> p h t", p=128)
        zv = pj[5120:6144, :].rearrange("(h p) t -> p h t", p=128)
        yv = y_dram[1024:2048, :].rearrange("(h p) t -> p h t", p=128)

        for n in range(GDN_TILES if GDN_STOP > 0 else 0):
            c0, c1 = n * 128, (n + 1) * 128
            kb.dma("sp", chq, qT[:], qv[:, :, c0:c1], writes=[K_("qT")])
            kb.dma("sp", chq, kT[:], kv[:, :, c0:c1], writes=[K_("kT")])
            kb.dma("sp", chq, vT[:], vv[:, :, c0:c1], writes=[K_("vT")])
            kb.dma("sp", chz, zb[:], zv[:, :, c0:c1], writes=[K_("zb")])
            gn = g[:, n, :]
            bn = beta[:, n, :]
            MM(kb, psm[:, 0:8], C["U2"][:], gn, True, True, [K_("g")], [K_("psm")], signal=False)
            MM(kb, psm[:, 8:16], C["Bsame"][:], gn, True, True, [K_("g")], [K_("psm")], signal=False)
            MM(kb, psm[:, 16:24], C["Bsel0"][:], gn, True, True, [K_("g")], [K_("psm")], signal=False)
            MM(kb, psm[:, 24:32], C["Bsel1"][:], gn, True, True, [K_("g")], [K_("psm")])
            CP(kb, "dve", sm[:, 0:32], psm[:], [K_("psm")], [K_("sm")])
            gc = sm[:, 0:8]
            ACT(kb, sm[:, 32:40], sm[:, 0:8], AF.Exp, [K_("sm")], [K_("sm")])
            TT(kb, "dve", sm[:, 40:48], sm[:, 8:16], sm[:, 0:8], ALU.subtract, [K_("sm")], [K_("sm")])
            ACT(kb, sm[:, 40:48], sm[:, 40:48], AF.Exp, [K_("sm")], [K_("sm")])
            ACT(kb, sm[:, 16:32], sm[:, 16:32], AF.Exp, [K_("sm")], [K_("sm")])
            TT(kb, "dve", sm[:, 48:56], sm[:, 32:40], bn, ALU.mult, [K_("sm"), K_("beta")], [K_("sm")])
            TS(kb, "dve", sm[:, 56:64], bn, -1.0, None, ALU.mult, None, [K_("beta")], [K_("sm")])
            if GDN_SUB <= 0:
                continue
            TT(kb, "dve", gd[:], bc_h(C["U2"][:]), bc_f(gn), ALU.mult, [K_("g")], [K_("gdiag")])
            for b in range(2):
                MM(kb, pA[:, 4 * b:4 * b + 4, :], C["ones_f"][:], gd[:, 4 * b:4 * b + 4, :], True, True,
                   [K_("gdiag")], [K_("pA")], signal=(b == 1))
            if GDN_SUB <= 0.3:
                continue
            CP(kb, "dve", gcr[:], pA[:], [K_("pA")], [K_("gcr")])
            ACT(kb, egrow[:], gcr[:], AF.Exp, [K_("gcr")], [K_("egrow")])
            if GDN_SUB <= 0.4:
                continue
            TT(kb, "dve", P1[:], gcr[:], bc_f(gc), ALU.subtract, [K_("gcr"), K_("sm")], [K_("P1")])
            if GDN_SUB <= 0.5:
                continue
            TS(kb, "dve", E1[:], P1[:], 0.0, None, ALU.max, None, [K_("P1")], [K_("E1")])
            ACT(kb, E1[:], E1[:], AF.Exp, [K_("E1")], [K_("E1")], scale=-1.0)
            if GDN_SUB <= 0.6:
                continue
            TS(kb, "dve", E2[:], P1[:], 0.0, None, ALU.min, None, [K_("P1")], [K_("E2")])
            ACT(kb, E2[:], E2[:], AF.Exp, [K_("E2")], [K_("E2")])
            TT(kb, "dve", E1[:], E1[:], bc_h(C["MLs"][:]), ALU.mult, [K_("E1")], [K_("E1")])
            TT(kb, "dve", E1[:], E1[:], bc_f(sm[:, 56:64]), ALU.mult, [K_("E1"), K_("sm")], [K_("E1")])
            TT(kb, "dve", E2[:], E2[:], bc_h(C["MU"][:]), ALU.mult, [K_("E2")], [K_("E2")])
            if GDN_SUB <= 1:
                continue
            for h in range(H):
                MM(kb, pB[:, h, :], kT[:, h, :], kT[:, h, :], True, True, [K_("kT")], [K_("pB")], signal=(h == H - 1))
            TT(kb, "dve", X[0][:], pB[:], E1[:], ALU.mult, [K_("pB"), K_("E1")], [K_("X0")])
            for h in range(H):
                MM(kb, pC[:, h, :], kT[:, h, :], qT[:, h, :], True, True, [K_("kT"), K_("qT")], [K_("pC")], signal=(h == H - 1))
            TT(kb, "dve", attnT[:], pC[:], E2[:], ALU.mult, [K_("pC"), K_("E2")], [K_("attnT")])
            for h in range(H):
                TR(kb, pAb[:, h, :], X[0][:, h, :], ident_b[:], [K_("X0"), K_("identb")], [K_("pA")],
                   signal=(h == H - 1))
            CP(kb, "dve", Y[0][:], pAb, [K_("pA")], [K_("Y0")])
            TT(kb, "dve", P[:], Y[0][:], bc_h(C["ident"][:]), ALU.add, [K_("Y0")], [K_("P")])
            if GDN_SUB <= 2:
                continue
            cur = 0
            for lvl in range(5):
                nxt = 1 - cur
                xk, yk = K_(f"X{cur}"), K_(f"Y{cur}")
                xn, yn = K_(f"X{nxt}"), K_(f"Y{nxt}")
                for h in range(H):
                    MM(kb, pB[:, h, :], Y[cur][:, h, :], X[cur][:, h, :], True, True, [xk, yk], [K_("pB")], signal=(h == H - 1))
                CP(kb, "dve", X[nxt][:], pB[:], [K_("pB")], [xn])
                if lvl < 4:
                    for h in range(H):
                        MM(kb, pC[:, h, :], X[cur][:, h, :], Y[cur][:, h, :], True, True, [xk, yk], [K_("pC")], signal=(h == H - 1))
                    CP(kb, "dve", Y[nxt][:], pC[:], [K_("pC")], [yn])
                for h in range(H):
                    MM(kb, pA[:, h, :], X[nxt][:, h, :], P[:, h, :], True, True, [xn, K_("P")], [K_("pA")], signal=(h == H - 1))
                TT(kb, "dve", P[:], P[:], pA[:], ALU.add, [K_("pA"), K_("P")], [K_("P")])
                cur = nxt
            if GDN_SUB <= 3:
                continue
            for h in range(H):
                TR(kb, pB[:, h, :], kT[:, h, :], C["ident"][:], [K_("kT")], [K_("pB")], signal=(h == H - 1))
            TT(kb, "dve", kbg[:], pB[:], bc_f(sm[:, 48:56]), ALU.mult, [K_("pB"), K_("sm")], [K_("kbg")])
            TS(kb, "dve", smk[:, 0:8], sm[:, 40:48], C["Bsel0"][:, 0:1], None, ALU.mult, None, [K_("sm")], [K_("smk")])
            TS(kb, "dve", smk[:, 8:16], sm[:, 40:48], C["Bsel1"][:, 0:1], None, ALU.mult, None, [K_("sm")], [K_("smk")])
            TT(kb, "dve", kd[:], pB[:], bc_f(smk[:, 0:8]), ALU.mult, [K_("pB"), K_("smk")], [K_("kd")])
            TT(kb, "dve", kd1[:], pB[:], bc_f(smk[:, 8:16]), ALU.mult, [K_("pB"), K_("smk")], [K_("kd")])
            for h in range(H):
                TR(kb, pC[:, h, :], vT[:, h, :], C["ident"][:], [K_("vT")], [K_("pC")], signal=(h == H - 1))
            TT(kb, "dve", vb[:], pC[:], bc_f(bn), ALU.mult, [K_("pC"), K_("beta")], [K_("vb")])
            for h in range(H):
                MM(kb, pA[:, h, :], P[:, h, :], vb[:, h, :], True, True, [K_("P"), K_("vb")], [K_("pA")], signal=(h == H - 1))
            CP(kb, "dve", u[:], pA[:], [K_("pA")], [K_("u")])
            for h in range(H):
                MM(kb, pB[:, h, :], kbg[:, h, :], P[:, h, :], True, True, [K_("P"), K_("kbg")], [K_("pB")], signal=(h == H - 1))
            CP(kb, "dve", wT[:], pB[:], [K_("pB")], [K_("wT")])
            TT(kb, "dve", qgT[:], qT[:], egrow[:], ALU.mult, [K_("qT"), K_("egrow")], [K_("qgT")])
            if GDN_STOP <= 1:
                continue
            for c in range(2):
                r0, r1 = c * 64, (c + 1) * 64
                for h in range(H):
                    MM(kb, pA[r0:r1, h, :], wT[:, h, r0:r1], Sb[:, h, :], True, True, [K_("wT"), K_("Sb")], [K_("pA")],
                       signal=(h == H - 1))
                if GDN_SUB == 10 or (GDN_SUB == 10.5 and c == 1):
                    continue
                TT(kb, "dve", vnew[r0:r1, :, :], u[r0:r1, :, :], pA[r0:r1, :, :], ALU.subtract, [K_("u"), K_("pA")],
                   [K_("vnew")])
                if GDN_SUB == 11:
                    continue
                for h in range(H):
                    MM(kb, pO[:, h, :], Sb[:, h, :], qgT[:, h, r0:r1], True, False, [K_("Sb"), K_("qgT")], [K_("pO")])
                    MM(kb, pO[:, h, :], vnew[r0:r1, h, :], attnT[r0:r1, h, r0:r1], False, True,
                       [K_("vnew"), K_("attnT")], [K_("pO")], signal=(h == H - 1))
                if GDN_SUB == 12:
                    continue
                CP(kb, "dve", oacc[:, :, r0:r1], pO[:], [K_("pO")], [K_("oacc")])
                if GDN_STOP <= 2:
                    continue
                for h in range(H):
                    MM(kb, pB[:, h, :], (kd, kd1)[c][:, h, :], vnew[:, h, :], True, True, [K_("kd"), K_("vnew")],
                       [K_("pB")], signal=(h == H - 1))
                TT(kb, "dve", Sst[:], Sst[:], bc_f(sm[:, 16 + 8 * c:24 + 8 * c]), ALU.mult, [K_("S"), K_("sm")],
                   [K_("S")])
                TT(kb, "dve", Sst[:], Sst[:], pB[:], ALU.add, [K_("S"), K_("pB")], [K_("S")])
                CP(kb, "pool", Sb[:], Sst[:], [K_("S")], [K_("Sb")])
            ACT(kb, osq[:], oacc[:], AF.Square, [K_("oacc")], [K_("osq")])
            for b in range(2):
                MM(kb, pC[:, 4 * b:4 * b + 4, :], C["ones_f"][:], osq[:, 4 * b:4 * b + 4, :], True, True, [K_("osq")],
                   [K_("pC")], signal=(b == 1))
            CP(kb, "dve", orn[:], pC[:], [K_("pC")], [K_("orn")])
            ACT(kb, orn[:], orn[:], AF.Sqrt, [K_("orn")], [K_("orn")], bias=C["eps"][:, 0:1], scale=1.0 / 128)
            RECIP(kb, orn[:], orn[:], [K_("orn")], [K_("orn")])
            STT(kb, oacc[:], oacc[:], C["onorm"][:, 0:1], orn[:], ALU.mult, ALU.mult, [K_("oacc"), K_("orn")],
                [K_("oacc")])
            ACT(kb, zb[:], zb[:], AF.Silu, [K_("zb")], [K_("zb")])
            TT(kb, "dve", yo[:], oacc[:], zb[:], ALU.mult, [K_("oacc"), K_("zb")], [K_("yo")])
            kb.dma("pool", chy, yv[:, :, c0:c1], yo[:], reads=[K_("yo")], writes=[("y0b", n)])
        kb.end_phase()

def load_consts(kb, nc, names_shapes):
    C = {}
    ch = kb.chan("const")
    for name, shape, dt in names_shapes:
        d = nc.dram_tensor(name, list(shape), dt, kind="ExternalInput").ap()
        if name == "cbase":
            t = kb.sb("c_" + name, shape, BF16)
            kb.dma("pool", ch, t[:], d, writes=[("const", name)])
        else:
            t = kb.sb("c_" + name, shape, dt)
            kb.dma("sp", ch, t[:], d, writes=[("const", name)])
        C[name] = t
    kb.end_phase()
    with ExitStack() as st:
        d = nc.dram_tensor("biasT", [128, 8, 2, 128], F32, kind="ExternalInput").ap()
        C["biasS"] = kb.sb("c_biasS", [128, 8, 2, 128], BF16)
        bt = kb.sb("biasT_tmp", [128, 8, 2, 128], F32, st)
        kb.dma("sp", ch, bt[:], d, writes=[("const", "biasT")])
        for h in range(8):
            TS(kb, "dve", C["biasS"][:, h, :, :], bt[:, h, :, :], C["cb"][:, h:h + 1], 128 ** 0.5,
               ALU.subtract, ALU.mult, [("const", "biasT")], [("const", "biasS")])
        kb.end_phase()
    return C


CONST_SPECS = [
    ("ident", (128, 128), F32),
    ("ones_f", (128, 128), F32),
    ("eps", (128, 1), F32),
    ("normwT", (128, 2, 16), F32),
    ("cbase", (128, 896), F32),
    ("cb", (128, 8), F32),
    ("qnormT", (128, 4), F32),
    ("kvnormT", (128, 2), F32),
    ("qgain", (128, 1), F32),
    ("kgain", (128, 1), F32),
    ("ikw", (128, 1), F32),
    ("ikb", (128, 1), F32),
]

CD_CONST_SPECS = [
    ("dww", (128, 8, 31), F32),
    ("dwb", (128, 8), F32),
    ("lnw", (128, 8), F32),
    ("lnb", (128, 8), F32),
    ("dcw", (128, 8, 3), F32),
]

GDN_CONST_SPECS = [
    ("cvw", (128, 24, 4), F32),
    ("alog_bc", (128, 8), F32),
    ("dtb_bc", (128, 8), F32),
    ("onorm", (128, 1), F32),
    ("U2", (128, 128), F32),
    ("Bsame", (128, 128), F32),
    ("Bsel0", (128, 128), F32),
    ("Bsel1", (128, 128), F32),
    ("MLs", (128, 128), F32),
    ("MU", (128, 128), F32),
]


def build_program(layers=(0, 1), l0_parts=("a", "b"), debug_out=False):
    nc = bass.Bass("TRN2", target_bir_lowering=False)

    def din(name, shape, dt=F32):
        return nc.dram_tensor(name, list(shape), dt, kind="ExternalInput").ap()

    def dscr(name, shape, dt=F32):
        return nc.dram_tensor(name, list(shape), dt, kind="Internal").ap()

    x = din("x", [T, D])
    out = nc.dram_tensor("out", [T, D], F32, kind="ExternalOutput").ap()
    kb = KB(nc)
    C = load_consts(kb, nc, CONST_SPECS)
    src = x
    if 0 in layers:
        ab_w_in = din("ab_w_in", [48, 128, 16 * 128])
        ab_w_out = din("ab_w_out", [16, 128, D])
        W = {"w_uqiq": din("w_uqiq", [16, 128, 4 * 128]), "w_uk": din("w_uk", [8, 128, 2 * 128]),
             "w_uv": din("w_uv", [2, 128, 1024])}
        pj0 = dscr("pj0", [6144, T])
        S = {"qT": dscr("s_qT", [8, 128, T], BF16), "qiT": dscr("s_qiT", [8, 128, T], BF16),
             "kT": dscr("s_kT", [8, 128, T], BF16), "V": dscr("s_V", [T, 1024], BF16),
             "kiT": dscr("s_kiT", [128, T], BF16), "tmaj": dscr("s_tmaj", [T, 32]),
             "gqkv": dscr("s_gqkv", [3072, T])}
        if debug_out:
            y0 = nc.dram_tensor("y0", [2048, T], BF16, kind="ExternalOutput").ap()
        else:
            y0 = dscr("y0", [2048, T], BF16)
        x1 = dscr("x1", [T, D]) if 1 in layers else out
        phase_inproj(kb, C, "l0", src, C["normwT"][:, 0, :], ab_w_in, 48, pj0)
        phase_ki_tmaj(kb, C, pj0, S)
        if "a" in l0_parts:
            phase_dsa_prep(kb, C, pj0, W, S)
            phase_dsa_attn(kb, C, pj0, S, y0)
        if "b" in l0_parts:
            phase_gdn(kb, C, pj0, S, y0)
        if not debug_out:
            phase_outproj(kb, C, "l0o", y0, ab_w_out, src, x1)
        src = x1
    if 1 in layers:
        cd_w_in = din("cd_w_in", [56, 128, 16 * 128])
        cd_w_out = din("cd_w_out", [16, 128, D])
        pj1 = dscr("pj1", [7168, T])
        y1 = dscr("y1", [2048, T], BF16)
        phase_inproj(kb, C, "l1", src, C["normwT"][:, 1, :], cd_w_in, 56, pj1)
        phase_cd_mix(kb, C, pj1, C, y1)
        phase_outproj(kb, C, "l1o", y1, cd_w_out, src, out)
    kb.finish()
    kb.emit()
    kb.close()
    return nc, kb


def tile_w_in(w, nch):
    K, N = w.shape
    assert N == nch * 128
    return np.ascontiguousarray(w.reshape(K // 128, 128, nch, 128).transpose(2, 1, 0, 3)).reshape(nch, 128, -1)


def t5_bucket_np(dist):
    import math
    max_exact = 16
    dd = np.maximum(dist, 1).astype(np.float32)
    large = max_exact + (np.log(dd / max_exact) / math.log(128 / max_exact) * (32 - max_exact)).astype(np.int32)
    large = np.minimum(large, 31)
    return np.where(dist < max_exact, dist, large)


def colT(v, k):
    return np.ascontiguousarray(np.asarray(v, np.float32).reshape(k, 128).T)


def host_consts(inp):
    f = np.float32
    c = {}
    c["ident"] = np.eye(128, dtype=f)
    c["ones_f"] = np.ones((128, 128), f)
    c["eps"] = np.full((128, 1), EPS, f)
    c["normwT"] = np.ascontiguousarray(inp["norm_w"].reshape(2, 16, 128).transpose(2, 0, 1)).astype(f)
    c["dww"] = np.ascontiguousarray(inp["c_dw_w"][0].reshape(31, 8, 128).transpose(2, 1, 0)).astype(f)
    c["dwb"] = colT(inp["c_dw_b"][0], 8)
    c["lnw"] = colT(inp["c_ln_w"][0], 8)
    c["lnb"] = colT(inp["c_ln_b"][0], 8)
    c["dcw"] = np.ascontiguousarray(inp["d_conv_w"][0].reshape(3, 8, 128).transpose(2, 1, 0)).astype(f)
    r = np.arange(128)[:, None]
    cc = np.arange(896)[None, :]
    c["cbase"] = np.where(cc <= r + 384, 0.0, -1e30).astype(f)
    kl = np.arange(128)[:, None, None]
    dd = np.arange(2)[None, :, None]
    ql = np.arange(128)[None, None, :]
    dist = np.maximum(dd * 128 + ql - kl, 0)
    bt = np.asarray(inp["rel_bias"], f)[t5_bucket_np(dist)]
    c["biasT"] = np.ascontiguousarray(bt.transpose(0, 3, 1, 2))
    c["cb"] = np.ascontiguousarray(np.broadcast_to(np.asarray(inp["rel_bias"], f)[31][None, :], (128, 8)))
    c["qnormT"] = colT(inp["a_q_norm"][0], 4)
    c["kvnormT"] = colT(inp["a_kv_norm"][0], 2)
    c["qgain"] = np.asarray(inp["a_q_gain"][0], f).reshape(128, 1).copy()
    c["kgain"] = np.asarray(inp["a_k_gain"][0], f).reshape(128, 1).copy()
    c["ikw"] = np.tile(np.asarray(inp["a_ik_norm_w"][0], f), 2).reshape(128, 1).copy()
    c["ikb"] = np.tile(np.asarray(inp["a_ik_norm_b"][0], f), 2).reshape(128, 1).copy()
    c["cvw"] = np.ascontiguousarray(np.asarray(inp["b_conv_w"][0], f).reshape(4, 24, 128).transpose(2, 1, 0))
    c["alog_bc"] = np.ascontiguousarray(np.broadcast_to(np.asarray(inp["b_a_log"][0], f)[None, :], (128, 8)))
    c["dtb_bc"] = np.ascontiguousarray(np.broadcast_to(np.asarray(inp["b_dt_bias"][0], f)[None, :], (128, 8)))
    c["onorm"] = np.asarray(inp["b_o_norm"][0], f).reshape(128, 1).copy()
    a = np.arange(128)
    same = (a[:, None] // 64) == (a[None, :] // 64)
    c["U2"] = (same & (a[:, None] <= a[None, :])).astype(f)
    c["Bsame"] = same.astype(f)
    c["Bsel0"] = np.ascontiguousarray(np.broadcast_to((a[:, None] < 64), (128, 128))).astype(f)
    c["Bsel1"] = np.ascontiguousarray(np.broadcast_to((a[:, None] >= 64), (128, 128))).astype(f)
    c["MLs"] = (same & (a[:, None] > a[None, :])).astype(f)
    c["MU"] = (same & (a[:, None] <= a[None, :])).astype(f)
    return c


def host_shared(inp, layers=(0, 1)):
    f = np.float32
    sh = host_consts(inp)
    if 0 in layers:
        w = np.asarray(inp["ab_w_in"][0], f)
        wp = np.zeros((D, 6144), f)
        wp[:, 0:832] = w[:, 0:832]
        wp[:, 896:912] = w[:, 832:848]
        wp[:, 912:928] = w[:, 4944:4960]
        wp[:, 1024:2048] = w[:, 848:1872]
        wp[:, 2048:5120] = w[:, 1872:4944]
        wp[:, 5120:6144] = w[:, 4960:5984]
        sh["ab_w_in"] = tile_w_in(wp, 48)
        sh["ab_w_out"] = np.ascontiguousarray(np.asarray(inp["ab_w_out"][0], f).reshape(16, 128, D))
        sh["w_uqiq"] = tile_w_in(np.concatenate([inp["a_w_uq"][0], inp["a_w_iq"][0]], axis=1).astype(f), 16)
        sh["w_uk"] = tile_w_in(np.asarray(inp["a_w_uk"][0], f), 8)
        sh["w_uv"] = np.ascontiguousarray(np.asarray(inp["a_w_uv"][0], f).reshape(2, 128, 1024))
    if 1 in layers:
        sh["cd_w_in"] = tile_w_in(np.asarray(inp["cd_w_in"][0], f), 56)
        sh["cd_w_out"] = np.ascontiguousarray(np.asarray(inp["cd_w_out"][0], f).reshape(16, 128, D))
    return sh


def kernel(**inputs):
    inp = {k: np.asarray(v) for k, v in inputs.items()}
    nc, kb = build_program()
    sh = host_shared(inp)
    x = np.ascontiguousarray(inp["x"], dtype=np.float32)
    in_maps = [dict(sh, x=x[b]) for b in range(8)]
    res = run_bass_kernel_spmd(nc, in_maps, core_ids=list(range(8)))
    return np.stack([np.asarray(r["out"], np.float32) for r in res.results], axis=0)
```

```python
from contextlib import ExitStack
import numpy as np
import concourse.bass as bass
import concourse.mybir as mybir
from concourse.bass_utils import run_bass_kernel_spmd

F32 = mybir.dt.float32
BF16 = mybir.dt.bfloat16
ALU = mybir.AluOpType
AF = mybir.ActivationFunctionType
AX = mybir.AxisListType

T = 4096
D = 2048
NT = T // 128
EPS = 1e-6
ENGS = ("pe", "act", "dve", "pool", "sp")


class Chan:
    def __init__(self, sem, name):
        self.sem = sem
        self.name = name
        self.n = 0


class KB:
    def __init__(self, nc):
        self.nc = nc
        self.es = ExitStack()
        self.q = {e: [] for e in ENGS}
        self.sems = {}
        self.cnt = {}
        self.seen = {e: {} for e in ENGS}
        self.lastw = {}
        self.readers = {}
        self.chans = []
        self.chan_by_sem = {}
        self.nins = 0
        self.pending = {e: False for e in ENGS}
        for e in ENGS:
            self.sems[e] = self.es.enter_context(nc.semaphore("s_" + e))
            self.cnt[e] = 0

    def sb(self, name, shape, dt, stack=None):
        return (stack or self.es).enter_context(self.nc.sbuf_tensor(name, list(shape), dt))

    def ps(self, name, shape, dt=F32, stack=None):
        return (stack or self.es).enter_context(self.nc.psum_tensor(name, list(shape), dt))

    def chan(self, name):
        c = Chan(self.es.enter_context(self.nc.semaphore("c_" + name)), name)
        self.chans.append(c)
        self.chan_by_sem[id(c.sem)] = c
        return c

    def _need0(self, eng, sem, val):
        ch = self.chan_by_sem.get(id(sem))
        if ch is not None:
            val = max(val, 16 * ch.n)
        cur = self.seen[eng].get(id(sem), 0)
        if val > cur:
            self.seen[eng][id(sem)] = val
            self.q[eng].append(("wait", sem, val))

    def _deps(self, eng, reads, writes, my_sem):
        for r in reads:
            ev = self.lastw.get(r)
            if ev is not None:
                self._need(eng, ev[0], ev[1])
        for w in writes:
            ev = self.lastw.get(w)
            if ev is not None:
                self._need(eng, ev[0], ev[1])
            rd = self.readers.get(w)
            if rd:
                for sem, val in rd.values():
                    if sem is my_sem:
                        continue
                    self._need(eng, sem, val)

    def _need(self, eng, sem, val):
        if eng == "pe" and sem is self.sems["pe"]:
            return
        self._need0(eng, sem, val)

    def _commit(self, ev, reads, writes):
        for w in writes:
            self.lastw[w] = ev
            self.readers[w] = {}
        for r in reads:
            d = self.readers.setdefault(r, {})
            d[id(ev[0])] = ev

    def op(self, eng, fn, reads=(), writes=(), signal=True):
        sem = self.sems[eng]
        self._deps(eng, reads, writes, sem)
        if signal:
            self.cnt[eng] += 1
            self.pending[eng] = False
            ev = (sem, self.cnt[eng])
            self.q[eng].append(("ins", fn, sem, 1))
        else:
            self.pending[eng] = True
            ev = (sem, self.cnt[eng] + 1)
            self.q[eng].append(("ins0", fn))
        self._commit(ev, reads, writes)
        self.nins += 1
        return ev

    def dma(self, eng, ch, out, in_, reads=(), writes=(), **kw):
        self._deps(eng, reads, writes, None)
        ch.n += 1
        ev = (ch.sem, 16 * ch.n)
        self.q[eng].append(("ins", lambda e, o=out, i=in_, k=kw: e.dma_start(out=o, in_=i, **k), ch.sem, 16))
        self._commit(ev, reads, writes)
        self.nins += 1
        return ev

    def _flush_pending(self):
        for e in ENGS:
            assert not self.pending[e], "non-signaling op left pending at a barrier on " + e

    def _all_events(self):
        self._flush_pending()
        evs = [(self.sems[e], self.cnt[e]) for e in ENGS if self.cnt[e] > 0]
        evs += [(c.sem, 16 * c.n) for c in self.chans if c.n > 0]
        return evs

    def barrier(self):
        evs = self._all_events()
        for e in ENGS:
            for sem, val in evs:
                self._need(e, sem, val)

    def finish(self, final_eng="sp"):
        for sem, val in self._all_events():
            self._need(final_eng, sem, val)

    def emit(self):
        nc = self.nc
        q = self.q
        self.q = {e: [] for e in ENGS}

        def replay(eng_obj, items):
            for it in items:
                if it[0] == "wait":
                    eng_obj.wait_ge(it[1], it[2])
                elif it[0] == "ins0":
                    it[1](eng_obj)
                else:
                    it[1](eng_obj).then_inc(it[2], it[3])

        with nc.Block() as block:
            @block.tensor
            def _(e):
                replay(e, q["pe"])

            @block.scalar
            def _(e):
                replay(e, q["act"])

            @block.vector
            def _(e):
                replay(e, q["dve"])

            @block.gpsimd
            def _(e):
                replay(e, q["pool"])

            @block.sync
            def _(e):
                replay(e, q["sp"])

    def end_phase(self):
        self.barrier()
        self.emit()

    def close(self):
        self.es.close()


def MM(kb, out, lhsT, rhs, start, stop, reads, writes, signal=None):
    if signal is None:
        signal = stop
    return kb.op("pe", lambda e: e.matmul(out, lhsT, rhs, start=start, stop=stop), reads, writes, signal=signal)


def TR(kb, out, in_, ident, reads, writes, signal=True):
    return kb.op("pe", lambda e: e.transpose(out, in_, ident), reads, writes, signal=signal)


def ACT(kb, out, in_, func, reads, writes, bias=None, scale=None, accum_out=None):
    kw = {}
    if bias is not None:
        kw["bias"] = bias
    if scale is not None:
        kw["scale"] = scale
    if accum_out is not None:
        kw["accum_out"] = accum_out
    return kb.op("act", lambda e: e.activation(out=out, in_=in_, func=func, **kw), reads, writes)


def TS(kb, eng, out, in0, s1, s2, op0, op1, reads, writes, accum_out=None):
    kw = {}
    if op1 is not None:
        kw["op1"] = op1
    if accum_out is not None:
        kw["accum_out"] = accum_out
    return kb.op(eng, lambda e: e.tensor_scalar(out=out, in0=in0, scalar1=s1, scalar2=s2, op0=op0, **kw),
                 reads, writes)


def TT(kb, eng, out, in0, in1, op, reads, writes):
    return kb.op(eng, lambda e: e.tensor_tensor(out=out, in0=in0, in1=in1, op=op), reads, writes)


def STT(kb, out, in0, scalar, in1, op0, op1, reads, writes):
    return kb.op("dve", lambda e: e.scalar_tensor_tensor(out=out, in0=in0, scalar=scalar, in1=in1,
                                                         op0=op0, op1=op1), reads, writes)


def CP(kb, eng, out, in_, reads, writes):
    if eng == "act":
        return kb.op("act", lambda e: e.copy(out=out, in_=in_), reads, writes)
    return kb.op(eng, lambda e: e.tensor_copy(out=out, in_=in_), reads, writes)


def MS(kb, eng, ap, val, writes):
    return kb.op(eng, lambda e: e.memset(ap, val), (), writes)


def RECIP(kb, out, in_, reads, writes):
    return kb.op("dve", lambda e: e.reciprocal(out=out, in_=in_), reads, writes)


def phase_norm_T(kb, C, x_dram, normwT, hT, tag):
    with ExitStack() as st:
        xt = [kb.sb(f"{tag}_xt{i}", [128, D], F32, st) for i in range(2)]
        xn = [kb.sb(f"{tag}_xn{i}", [128, D], F32, st) for i in range(2)]
        sq = kb.sb(f"{tag}_sq", [128, D], BF16, st)
        sm = [kb.sb(f"{tag}_sm{i}", [128, 4], F32, st) for i in range(2)]
        pst = [kb.ps(f"{tag}_pt{i}", [128, 8, 128], F32, st) for i in range(2)]
        ch = [kb.chan(f"{tag}_x{i}") for i in range(2)]
        for tt in range(NT):
            s = tt % 2
            kb.dma("sp", ch[s], xt[s][:], x_dram[tt * 128:(tt + 1) * 128, :], writes=[(tag, "xt", s)])
            ACT(kb, sq[:], xt[s][:], AF.Square, [(tag, "xt", s)], [(tag, "sq"), (tag, "ss", s)],
                accum_out=sm[s][:, 0:1])
            ACT(kb, sm[s][:, 1:2], sm[s][:, 0:1], AF.Sqrt, [(tag, "ss", s)], [(tag, "sd", s)],
                bias=C["eps"][:, 0:1], scale=1.0 / D)
            RECIP(kb, sm[s][:, 2:3], sm[s][:, 1:2], [(tag, "sd", s)], [(tag, "rs", s)])
            TS(kb, "dve", xn[s][:], xt[s][:], sm[s][:, 2:3], None, ALU.mult, None,
               [(tag, "xt", s), (tag, "rs", s)], [(tag, "xn", s)])
            for half in range(2):
                p = half
                for kk in range(8):
                    k = half * 8 + kk
                    TR(kb, pst[p][:, kk, :], xn[s][:, k * 128:(k + 1) * 128], C["ident"][:],
                       [(tag, "xn", s)], [(tag, "pt", p)], signal=(kk == 7))
                nw = normwT[:, half * 8:(half + 1) * 8].unsqueeze(2).to_broadcast([128, 8, 128])
                TT(kb, "dve", hT[:, half * 8:(half + 1) * 8, tt * 128:(tt + 1) * 128], pst[p][:], nw,
                   ALU.mult, [(tag, "pt", p)], [("hT", tt)])
        kb.end_phase()


def gemm_fm(kb, tag, actT, act_keys, KC, w_dram, NCH, sink):
    with ExitStack() as st:
        wb = [kb.sb(f"{tag}_wb{i}", [128, KC * 128], BF16, st) for i in range(2)]
        wch = [kb.chan(f"{tag}_w{i}") for i in range(2)]
        pss = [kb.ps(f"{tag}_ps{i}", [128, 1024], F32, st) for i in range(2)]
        it = 0
        for c in range(NCH):
            s = c % 2
            kb.dma("pool", wch[s], wb[s][:], w_dram[c], writes=[(tag, "wb", s)])
            for ts in range(4):
                p = it % 2
                it += 1
                for b in range(2):
                    t0 = ts * 1024 + b * 512
                    for k in range(KC):
                        MM(kb, pss[p][:, b * 512:(b + 1) * 512], wb[s][:, k * 128:(k + 1) * 128],
                           actT[:, k, t0:t0 + 512], k == 0, k == KC - 1,
                           [(tag, "wb", s)] + act_keys, [(tag, "ps", p)])
                sink(c, ts, pss[p], (tag, "ps", p), st)
        kb.end_phase()


class StoreSink:
    def __init__(self, kb, tag, dst, st):
        self.kb = kb
        self.tag = tag
        self.dst = dst
        self.stg = [kb.sb(f"{tag}_stg{i}", [128, 1024], F32, st) for i in range(3)]
        self.ch = [kb.chan(f"{tag}_st{i}") for i in range(3)]
        self.i = 0

    def __call__(self, c, ts, ps, pkey, st):
        kb = self.kb
        s = self.i % 3
        eng = "act" if self.i % 2 == 0 else "dve"
        self.i += 1
        CP(kb, eng, self.stg[s][:], ps[:], [pkey], [(self.tag, "stg", s)])
        kb.dma("sp", self.ch[s], self.dst[c * 128:(c + 1) * 128, ts * 1024:(ts + 1) * 1024], self.stg[s][:],
               reads=[(self.tag, "stg", s)], writes=[(self.tag, "dst", c)])


def phase_inproj(kb, C, tag, x_dram, normwT, w_dram, NCH, pj):
    with ExitStack() as st:
        hT = kb.sb(f"{tag}_hT", [128, 16, T], BF16, st)
        phase_norm_T(kb, C, x_dram, normwT, hT, tag + "n")
        with ExitStack() as st2:
            sink = StoreSink(kb, tag + "s", pj, st2)
            gemm_fm(kb, tag + "g", hT, [], 16, w_dram, NCH, sink)


def phase_outproj(kb, C, tag, yT_dram, wo_dram, xres_dram, out_dram):
    with ExitStack() as st:
        wo = kb.sb(f"{tag}_wo", [128, 16, D], BF16, st)
        wch = kb.chan(f"{tag}_w")
        for k in range(16):
            kb.dma("pool", wch, wo[:, k, :], wo_dram[k], writes=[(tag, "wo")])
        yt = [kb.sb(f"{tag}_yt{i}", [128, 16, 128], BF16, st) for i in range(2)]
        xr = [kb.sb(f"{tag}_xr{i}", [128, D], F32, st) for i in range(2)]
        ot = [kb.sb(f"{tag}_ot{i}", [128, D], F32, st) for i in range(2)]
        ps = [kb.ps(f"{tag}_ps{i}", [128, 512], F32, st) for i in range(4)]
        chy = [kb.chan(f"{tag}_y{i}") for i in range(2)]
        chx = [kb.chan(f"{tag}_x{i}") for i in range(2)]
        cho = [kb.chan(f"{tag}_o{i}") for i in range(2)]
        yv = yT_dram.rearrange("(k p) t -> p k t", p=128)
        for tt in range(NT):
            s = tt % 2
            kb.dma("sp", chy[s], yt[s][:], yv[:, :, tt * 128:(tt + 1) * 128], writes=[(tag, "yt", s)])
            kb.dma("sp", chx[s], xr[s][:], xres_dram[tt * 128:(tt + 1) * 128, :], writes=[(tag, "xr", s)])
            for nb in range(4):
                for k in range(16):
                    MM(kb, ps[nb][:], yt[s][:, k, :], wo[:, k, nb * 512:(nb + 1) * 512], k == 0, k == 15,
                       [(tag, "yt", s), (tag, "wo")], [(tag, "ps", nb)])
                TT(kb, "dve", ot[s][:, nb * 512:(nb + 1) * 512], ps[nb][:], xr[s][:, nb * 512:(nb + 1) * 512],
                   ALU.add, [(tag, "ps", nb), (tag, "xr", s)], [(tag, "ot", s)])
            kb.dma("pool", cho[s], out_dram[tt * 128:(tt + 1) * 128, :], ot[s][:],
                   reads=[(tag, "ot", s)], writes=[(tag, "out", tt)])
        kb.end_phase()


def phase_cd_mix(kb, C, pj, cw_unused, y_dram):
    with ExitStack() as stc:
        cw = {}
        chc = kb.chan("cd_const")
        for name, shape, dt in CD_CONST_SPECS:
            d = kb.nc.dram_tensor(name, list(shape), dt, kind="ExternalInput").ap()
            t = kb.sb("c_" + name, shape, dt, stc)
            kb.dma("sp", chc, t[:], d, writes=[("const", name)])
            cw[name] = t
        kb.end_phase()
        _phase_cd_mix(kb, C, pj, cw, y_dram)


def _phase_cd_mix(kb, C, pj, cw, y_dram):
    HB = 2048
    tag = "cd"
    with ExitStack() as st:
        QB = 1024
        ident_b = kb.sb("cd_identb", [128, 128], BF16, st)
        dg = kb.sb("cd_dg", [128, 8, 31, 128], BF16, st)
        uc = kb.sb("cd_uc", [128, 8, QB], F32, st)
        ab = [kb.sb(f"cd_a{i}", [128, 30 + QB], F32, st) for i in range(2)]
        gb = [kb.sb(f"cd_g{i}", [128, 30 + QB], F32, st) for i in range(2)]
        ub = [kb.sb(f"cd_u{i}", [128, 30 + QB], BF16, st) for i in range(2)]
        sq = [kb.sb(f"cd_sq{i}", [128, QB], F32, st) for i in range(2)]
        mean = kb.sb("cd_mean", [128, QB], F32, st)
        rstd = kb.sb("cd_rstd", [128, QB], F32, st)
        m2 = kb.sb("cd_m2", [128, QB], F32, st)
        zb = [kb.sb(f"cd_z{i}", [128, QB], F32, st) for i in range(2)]
        yb = [kb.sb(f"cd_y{i}", [128, QB], BF16, st) for i in range(2)]
        pc = [kb.ps(f"cd_pc{i}", [128, QB], F32, st) for i in range(2)]
        ps_sum = kb.ps("cd_pss", [128, QB], F32, st)
        ps_ssq = kb.ps("cd_psq", [128, QB], F32, st)
        cha = [kb.chan(f"cd_a{i}") for i in range(2)]
        chg = [kb.chan(f"cd_g{i}") for i in range(2)]
        chz = [kb.chan(f"cd_z{i}") for i in range(2)]
        chy = [kb.chan(f"cd_y{i}") for i in range(2)]
        CP(kb, "dve", ident_b[:], C["ident"][:], [], [(tag, "cst")])
        for cc in range(8):
            TT(kb, "dve", dg[:, cc, :, :], ident_b[:].unsqueeze(1).to_broadcast([128, 31, 128]),
               cw["dww"][:, cc, :].unsqueeze(2).to_broadcast([128, 31, 128]), ALU.mult, [(tag, "cst")], [(tag, "dg")])
        it = 0
        for tq in range(T // QB):
            t0 = tq * QB
            for cc in range(8):
                s = it % 2
                it += 1
                r0 = cc * 128
                if tq == 0:
                    MS(kb, "pool", ab[s][:, 0:30], 0.0, [(tag, "a", s)])
                    MS(kb, "pool", gb[s][:, 0:30], 0.0, [(tag, "g", s)])
                    kb.dma("sp", cha[s], ab[s][:, 30:30 + QB], pj[r0:r0 + 128, 0:QB], writes=[(tag, "a", s)])
                    kb.dma("sp", chg[s], gb[s][:, 30:30 + QB], pj[1024 + r0:1024 + r0 + 128, 0:QB],
                           writes=[(tag, "g", s)])
                else:
                    kb.dma("sp", cha[s], ab[s][:], pj[r0:r0 + 128, t0 - 30:t0 + QB], writes=[(tag, "a", s)])
                    kb.dma("sp", chg[s], gb[s][:], pj[1024 + r0:1024 + r0 + 128, t0 - 30:t0 + QB],
                           writes=[(tag, "g", s)])
                ACT(kb, gb[s][:], gb[s][:], AF.Sigmoid, [(tag, "g", s)], [(tag, "g", s)])
                TT(kb, "dve", ub[s][:], ab[s][:], gb[s][:], ALU.mult, [(tag, "a", s), (tag, "g", s)], [(tag, "u", s)])
                for b in range(QB // 512):
                    for j in range(31):
                        MM(kb, pc[s][:, b * 512:(b + 1) * 512], dg[:, cc, j, :], ub[s][:, j + b * 512:j + b * 512 + 512],
                           j == 0, j == 30, [(tag, "dg"), (tag, "u", s)], [(tag, "pc", s)])
                ACT(kb, uc[:, cc, :], pc[s][:], AF.Identity, [(tag, "pc", s)], [(tag, "uc", cc)],
                    bias=cw["dwb"][:, cc:cc + 1])
                ACT(kb, sq[s][:], uc[:, cc, :], AF.Square, [(tag, "uc", cc)], [(tag, "sq", s)])
                for tb in range(QB // 512):
                    last = tb == QB // 512 - 1
                    MM(kb, ps_sum[:, tb * 512:(tb + 1) * 512], C["ones_f"][:], uc[:, cc, tb * 512:(tb + 1) * 512],
                       cc == 0, cc == 7, [(tag, "uc", cc)], [(tag, "pss")], signal=last)
                    MM(kb, ps_ssq[:, tb * 512:(tb + 1) * 512], C["ones_f"][:], sq[s][:, tb * 512:(tb + 1) * 512],
                       cc == 0, cc == 7, [(tag, "sq", s)], [(tag, "psq")], signal=last)
            ACT(kb, mean[:], ps_sum[:], AF.Copy, [(tag, "pss")], [(tag, "mean")], scale=1.0 / 1024)
            TT(kb, "dve", m2[:], mean[:], mean[:], ALU.mult, [(tag, "mean")], [(tag, "m2")])
            STT(kb, m2[:], ps_ssq[:], 1.0 / 1024, m2[:], ALU.mult, ALU.subtract, [(tag, "psq"), (tag, "m2")],
                [(tag, "m2")])
            ACT(kb, m2[:], m2[:], AF.Sqrt, [(tag, "m2")], [(tag, "m2")], bias=C["eps"][:, 0:1])
            RECIP(kb, rstd[:], m2[:], [(tag, "m2")], [(tag, "rstd")])
            for cc in range(8):
                s = cc % 2
                r0 = cc * 128
                kb.dma("sp", chz[s], zb[s][:], pj[2048 + r0:2048 + r0 + 128, t0:t0 + QB], writes=[(tag, "z", s)])
                TT(kb, "dve", uc[:, cc, :], uc[:, cc, :], mean[:], ALU.subtract, [(tag, "uc", cc), (tag, "mean")],
                   [(tag, "uc", cc)])
                TT(kb, "dve", uc[:, cc, :], uc[:, cc, :], rstd[:], ALU.mult, [(tag, "uc", cc), (tag, "rstd")],
                   [(tag, "uc", cc)])
                ACT(kb, uc[:, cc, :], uc[:, cc, :], AF.Silu, [(tag, "uc", cc)], [(tag, "uc", cc)],
                    scale=cw["lnw"][:, cc:cc + 1], bias=cw["lnb"][:, cc:cc + 1])
                ACT(kb, zb[s][:], zb[s][:], AF.Silu, [(tag, "z", s)], [(tag, "z", s)])
                TT(kb, "dve", yb[s][:], uc[:, cc, :], zb[s][:], ALU.mult, [(tag, "uc", cc), (tag, "z", s)],
                   [(tag, "y", s)])
                kb.dma("pool", chy[s], y_dram[r0:r0 + 128, t0:t0 + QB], yb[s][:], reads=[(tag, "y", s)],
                       writes=[(tag, "yd", cc, tq)])
        kb.end_phase()
    tag = "sc"
    with ExitStack() as st:
        W = T + 2
        bg = [kb.sb(f"sc_b{i}", [128, T], F32, st) for i in range(2)]
        cg = [kb.sb(f"sc_c{i}", [128, W], F32, st) for i in range(2)]
        ud = [kb.sb(f"sc_u{i}", [128, W], F32, st) for i in range(2)]
        zd = [kb.sb(f"sc_z{i}", [128, T], F32, st) for i in range(2)]
        acc = [kb.sb(f"sc_acc{i}", [128, T], F32, st) for i in range(2)]
        yb = [kb.sb(f"sc_y{i}", [128, T], BF16, st) for i in range(2)]
        chs = {n: [kb.chan(f"sc_{n}{i}") for i in range(2)] for n in ("b", "c", "u", "z", "y")}
        for cc in range(8):
            s = cc % 2
            r0 = cc * 128
            MS(kb, "pool", cg[s][:, 0:2], 0.0, [(tag, "c", s)])
            MS(kb, "pool", ud[s][:, 0:2], 0.0, [(tag, "u", s)])
            kb.dma("sp", chs["b"][s], bg[s][:], pj[3072 + r0:3072 + r0 + 128, :], writes=[(tag, "b", s)])
            kb.dma("sp", chs["c"][s], cg[s][:, 2:W], pj[4096 + r0:4096 + r0 + 128, :], writes=[(tag, "c", s)])
            kb.dma("sp", chs["u"][s], ud[s][:, 2:W], pj[5120 + r0:5120 + r0 + 128, :], writes=[(tag, "u", s)])
            kb.dma("sp", chs["z"][s], zd[s][:], pj[6144 + r0:6144 + r0 + 128, :], writes=[(tag, "z", s)])
            TT(kb, "dve", cg[s][:], cg[s][:], ud[s][:], ALU.mult, [(tag, "c", s), (tag, "u", s)], [(tag, "c", s)])
            TS(kb, "dve", acc[s][:], cg[s][:, 2:W], cw["dcw"][:, cc, 2:3], None, ALU.mult, None,
               [(tag, "c", s)], [(tag, "acc", s)])
            for j in range(2):
                STT(kb, acc[s][:], cg[s][:, j:j + T], cw["dcw"][:, cc, j:j + 1], acc[s][:], ALU.mult, ALU.add,
                    [(tag, "c", s), (tag, "acc", s)], [(tag, "acc", s)])
            ACT(kb, zd[s][:], zd[s][:], AF.Silu, [(tag, "z", s)], [(tag, "z", s)])
            TT(kb, "pool", bg[s][:], bg[s][:], zd[s][:], ALU.mult, [(tag, "b", s), (tag, "z", s)], [(tag, "b", s)])
            TT(kb, "dve", yb[s][:], acc[s][:], bg[s][:], ALU.mult, [(tag, "acc", s), (tag, "b", s)], [(tag, "y", s)])
            kb.dma("pool", chs["y"][s], y_dram[1024 + r0:1024 + r0 + 128, :], yb[s][:], reads=[(tag, "y", s)],
                   writes=[(tag, "yd", cc)])
        kb.end_phase()


def colnorm_phase(kb, C, tag, src, KC, gT, outT):
    with ExitStack() as st:
        raw = [kb.sb(f"{tag}_raw{i}", [128, KC, 1024], F32, st) for i in range(2)]
        sq = [kb.sb(f"{tag}_sq{i}", [128, 1024], F32, st) for i in range(2)]
        rs = kb.sb(f"{tag}_rs", [128, 1024], F32, st)
        ssp = kb.ps(f"{tag}_ssp", [128, 1024], F32, st)
        ch = [kb.chan(f"{tag}_l{i}") for i in range(2)]
        sv = src.rearrange("(k p) t -> p k t", p=128)
        for ts in range(4):
            s = ts % 2
            kb.dma("sp", ch[s], raw[s][:], sv[:, :, ts * 1024:(ts + 1) * 1024], writes=[(tag, "raw", s)])
            for k in range(KC):
                q = k % 2
                ACT(kb, sq[q][:], raw[s][:, k, :], AF.Square, [(tag, "raw", s)], [(tag, "sq", q)])
                for b in range(2):
                    MM(kb, ssp[:, b * 512:(b + 1) * 512], C["ones_f"][:], sq[q][:, b * 512:(b + 1) * 512],
                       k == 0, k == KC - 1, [(tag, "sq", q)], [(tag, "ssp")], signal=True)
            ACT(kb, rs[:], ssp[:], AF.Sqrt, [(tag, "ssp")], [(tag, "rs")], bias=C["eps"][:, 0:1],
                scale=1.0 / (KC * 128))
            RECIP(kb, rs[:], rs[:], [(tag, "rs")], [(tag, "rs")])
            for k in range(KC):
                STT(kb, outT[:, k, ts * 1024:(ts + 1) * 1024], raw[s][:, k, :], gT[:, k:k + 1], rs[:],
                    ALU.mult, ALU.mult, [(tag, "raw", s), (tag, "rs")], [(tag, "out", k, ts)])
        kb.end_phase()


class HeadNormSink:
    def __init__(self, kb, C, tag, dst, gain, n_norm, raw_dst, st):
        self.kb, self.C, self.tag, self.dst, self.gain = kb, C, tag, dst, gain
        self.n_norm, self.raw_dst = n_norm, raw_dst
        self.sq = [kb.sb(f"{tag}_sq{i}", [128, 1024], F32, st) for i in range(2)]
        self.rs = [kb.sb(f"{tag}_rs{i}", [128, 1024], F32, st) for i in range(2)]
        self.ob = [kb.sb(f"{tag}_ob{i}", [128, 1024], BF16, st) for i in range(2)]
        self.ssp = kb.ps(f"{tag}_ssp", [128, 1024], F32, st)
        self.ch = [kb.chan(f"{tag}_o{i}") for i in range(2)]
        self.i = 0

    def __call__(self, c, ts, ps, pkey, st):
        kb, C, tag = self.kb, self.C, self.tag
        s = self.i % 2
        self.i += 1
        if c < self.n_norm:
            ACT(kb, self.sq[s][:], ps[:], AF.Square, [pkey], [(tag, "sq", s)])
            for b in range(2):
                MM(kb, self.ssp[:, b * 512:(b + 1) * 512], C["ones_f"][:], self.sq[s][:, b * 512:(b + 1) * 512],
                   True, True, [(tag, "sq", s)], [(tag, "ssp")])
            ACT(kb, self.rs[s][:], self.ssp[:], AF.Sqrt, [(tag, "ssp")], [(tag, "rs", s)], bias=C["eps"][:, 0:1],
                scale=1.0 / 128)
            RECIP(kb, self.rs[s][:], self.rs[s][:], [(tag, "rs", s)], [(tag, "rs", s)])
            STT(kb, self.ob[s][:], ps[:], self.gain, self.rs[s][:], ALU.mult, ALU.mult,
                [pkey, (tag, "rs", s)], [(tag, "ob", s)])
            d = self.dst[c]
        else:
            CP(kb, "act", self.ob[s][:], ps[:], [pkey], [(tag, "ob", s)])
            d = self.raw_dst[c - self.n_norm]
        kb.dma("sp", self.ch[s], d[:, ts * 1024:(ts + 1) * 1024], self.ob[s][:], reads=[(tag, "ob", s)],
               writes=[(tag, "dst", c)])


def phase_dsa_prep(kb, C, pj, W, S):
    with ExitStack() as st:
        cqn = kb.sb("cqn", [128, 4, T], BF16, st)
        colnorm_phase(kb, C, "cq", pj[0:512, :], 4, C["qnormT"], cqn)
        with ExitStack() as st2:
            sink = HeadNormSink(kb, C, "qs", S["qT"], C["qgain"][:, 0:1], 8, S["qiT"], st2)
            gemm_fm(kb, "qg", cqn, [], 4, W["w_uqiq"], 16, sink)
    with ExitStack() as st:
        ckvn = kb.sb("ckvn", [128, 2, T], BF16, st)
        colnorm_phase(kb, C, "ckv", pj[512:768, :], 2, C["kvnormT"], ckvn)
        with ExitStack() as st2:
            sink = HeadNormSink(kb, C, "ks", S["kT"], C["kgain"][:, 0:1], 8, None, st2)
            gemm_fm(kb, "kg", ckvn, [], 2, W["w_uk"], 8, sink)
        with ExitStack() as st2:
            wv = kb.sb("wv", [128, 2, 1024], BF16, st2)
            chw = kb.chan("wv")
            for k in range(2):
                kb.dma("pool", chw, wv[:, k, :], W["w_uv"][k], writes=[("wv",)])
            vps = [kb.ps(f"v_ps{i}", [128, 1024], F32, st2) for i in range(2)]
            vb = [kb.sb(f"v_b{i}", [128, 1024], BF16, st2) for i in range(2)]
            chv = [kb.chan(f"v_o{i}") for i in range(2)]
            for tt in range(NT):
                s = tt % 2
                for b in range(2):
                    for k in range(2):
                        MM(kb, vps[s][:, b * 512:(b + 1) * 512], ckvn[:, k, tt * 128:(tt + 1) * 128],
                           wv[:, k, b * 512:(b + 1) * 512], k == 0, k == 1, [("wv",)], [("v", "ps", s)])
                CP(kb, "act" if tt % 2 else "dve", vb[s][:], vps[s][:], [("v", "ps", s)], [("v", "b", s)])
                kb.dma("sp", chv[s], S["V"][tt * 128:(tt + 1) * 128, :], vb[s][:], reads=[("v", "b", s)],
                       writes=[("V", tt)])
            kb.end_phase()


def phase_ki_tmaj(kb, C, pj, S):
    with ExitStack() as st:
        ki = kb.sb("ki_raw", [128, T], F32, st)
        sq = kb.sb("ki_sq", [128, T], F32, st)
        mean = kb.sb("ki_mean", [128, 1024], F32, st)
        var = kb.sb("ki_var", [128, 1024], F32, st)
        kio = kb.sb("ki_o", [128, T], BF16, st)
        sm = kb.sb("ki_sm", [32, T], F32, st)
        tmo = kb.sb("ki_tmo", [128, NT, 32], F32, st)
        ps1 = kb.ps("ki_ps1", [128, 1024], F32, st)
        ps2 = kb.ps("ki_ps2", [128, 1024], F32, st)
        pst = kb.ps("ki_pst", [128, 16, 32], F32, st)
        ch = kb.chan("ki")
        kb.dma("sp", ch, ki[0:64, :], pj[768:832, :], writes=[("ki", "raw")])
        kb.dma("sp", ch, ki[64:128, :], pj[768:832, :], writes=[("ki", "raw")])
        kb.dma("sp", ch, sm[:], pj[896:928, :], writes=[("ki", "sm")])
        ACT(kb, sq[:], ki[:], AF.Square, [("ki", "raw")], [("ki", "sq")])
        for ts in range(4):
            for b in range(2):
                c0 = ts * 1024 + b * 512
                MM(kb, ps1[:, b * 512:(b + 1) * 512], C["ones_f"][0:64, :], ki[0:64, c0:c0 + 512], True, True,
                   [("ki", "raw")], [("ki", "ps1")])
                MM(kb, ps2[:, b * 512:(b + 1) * 512], C["ones_f"][0:64, :], sq[0:64, c0:c0 + 512], True, True,
                   [("ki", "sq")], [("ki", "ps2")])
            ACT(kb, mean[:], ps1[:], AF.Copy, [("ki", "ps1")], [("ki", "mean")], scale=1.0 / 64)
            TT(kb, "dve", var[:], mean[:], mean[:], ALU.mult, [("ki", "mean")], [("ki", "var")])
            STT(kb, var[:], ps2[:], 1.0 / 64, var[:], ALU.mult, ALU.subtract, [("ki", "ps2"), ("ki", "var")],
                [("ki", "var")])
            ACT(kb, var[:], var[:], AF.Sqrt, [("ki", "var")], [("ki", "var")], bias=C["eps"][:, 0:1])
            RECIP(kb, var[:], var[:], [("ki", "var")], [("ki", "var")])
            sl = slice(ts * 1024, (ts + 1) * 1024)
            TT(kb, "dve", ki[:, sl], ki[:, sl], mean[:], ALU.subtract, [("ki", "raw"), ("ki", "mean")], [("ki", "raw")])
            TT(kb, "dve", ki[:, sl], ki[:, sl], var[:], ALU.mult, [("ki", "raw"), ("ki", "var")], [("ki", "raw")])
            TS(kb, "dve", kio[:, sl], ki[:, sl], C["ikw"][:, 0:1], C["ikb"][:, 0:1], ALU.mult, ALU.add,
               [("ki", "raw")], [("ki", "o")])
        kb.dma("sp", ch, S["kiT"], kio[:], reads=[("ki", "o")], writes=[("kiT",)])
        for g in range(2):
            for tt in range(16):
                t = g * 16 + tt
                TR(kb, pst[:, tt, :], sm[:, t * 128:(t + 1) * 128], C["ident"][0:32, 0:32], [("ki", "sm")],
                   [("ki", "pst")], signal=(tt == 15))
            CP(kb, "dve", tmo[:, g * 16:(g + 1) * 16, :], pst[:], [("ki", "pst")], [("ki", "tmo")])
        kb.dma("sp", ch, S["tmaj"].rearrange("(n p) c -> p n c", p=128), tmo[:], reads=[("ki", "tmo")],
               writes=[("tmaj",)])
        kb.end_phase()


N_BIS = 13
SCALE_A = 128 ** -0.5


def phase_dsa_attn(kb, C, pj, S, y_dram):
    tag = "at"
    with ExitStack() as st:
        kT = kb.sb("at_kT", [128, 8, T], BF16, st)
        Vt = kb.sb("at_V", [128, NT, 1024], BF16, st)
        kiT = kb.sb("at_kiT", [128, T], BF16, st)
        score1 = kb.sb("at_sc", [128, T], F32, st)
        score = [score1, score1]
        mask = kb.sb("at_mask", [128, T], BF16, st)
        junk = mask
        maskT1 = kb.sb("at_maskT", [128, NT, 128], BF16, st)
        maskT = [maskT1, maskT1]
        rbuf = [kb.sb(f"at_r{i}", [128, 2, 512], BF16, st) for i in range(2)]
        dsg = kb.sb("at_dsg", [128, 16, 128], BF16, st)
        qb = [kb.sb(f"at_q{i}", [128, 8, 128], BF16, st) for i in range(2)]
        qib1 = kb.sb("at_qi", [128, 8, 128], BF16, st)
        qib = [qib1, qib1]
        wt = [kb.sb(f"at_w{i}", [128, 16], F32, st) for i in range(2)]
        bs = [kb.sb(f"at_bs{i}", [128, 8], F32, st) for i in range(2)]
        wks1 = kb.sb("at_wk", [128, N_BIS], F32, st)
        wks = [wks1, wks1]
        fpow = kb.sb("at_fpow", [128, N_BIS], F32, st)
        za1 = kb.sb("at_za", [128, 8, 128], F32, st)
        za = [za1, za1]
        pt = [kb.sb(f"at_pt{i}", [128, 4, 128], BF16, st) for i in range(2)]
        ost1 = kb.sb("at_ost", [128, 8, 256], F32, st)
        ost = [ost1, ost1]
        yst1 = kb.sb("at_yst", [128, 8, 128], BF16, st)
        yst = [yst1, yst1]
        biasS = C["biasS"]
        ident_b = kb.sb("at_identb", [128, 128], BF16, st)
        ones_b = kb.sb("at_onesb", [128, 128], BF16, st)
        ips = [kb.ps(f"at_ips{i}", [128, 2, 512], F32, st) for i in range(2)]
        lg = [ips[i][:, 0, :].rearrange("p (a b) -> p a b", b=128) for i in range(2)]
        po1 = kb.ps("at_po", [128, 128], F32, st)
        prs1 = kb.ps("at_prs", [128, 128], F32, st)
        po, prs = [po1, po1], [prs1, prs1]
        sps1 = kb.ps("at_sps", [128, 512], F32, st)
        sps = [sps1, sps1]
        tps = ips[0][:, 0, :].bitcast(BF16)[:, 0:512].rearrange("p (a b) -> p a b", b=128)
        chl = kb.chan("at_ld")
        chq = [kb.chan(f"at_q{i}") for i in range(2)]
        chy = [kb.chan(f"at_y{i}") for i in range(2)]
        chz = kb.chan("at_z")
        kb.dma("sp", chl, kT[:], S["kT"].rearrange("h p t -> p h t"), writes=[(tag, "kT")])
        kb.dma("sp", chl, Vt[:], S["V"].rearrange("(n p) c -> p n c", p=128), writes=[(tag, "V")])
        kb.dma("sp", chl, kiT[:], S["kiT"], writes=[(tag, "kiT")])
        CP(kb, "dve", ident_b[:], C["ident"][:], [], [(tag, "cst")])
        CP(kb, "dve", ones_b[:], C["ones_f"][:], [], [(tag, "cst")])
        for k in range(1, N_BIS + 1):
            MS(kb, "dve", fpow[:, k - 1:k], 2.0 ** -k, [(tag, "cst")])
        qTv = S["qT"].rearrange("h p t -> p h t")
        qiTv = S["qiT"].rearrange("h p t -> p h t")
        zav = pj[1024:2048, :].rearrange("(h p) t -> p h t", p=128)
        yv = y_dram[0:1024, :].rearrange("(h p) t -> p h t", p=128)
        cnt = {"ips": 0, "r": 0, "lg": 0, "pt": 0, "po": 0, "sps": 0}

        def stage_a(i):
            s = i % 2
            c0, c1 = i * 128, (i + 1) * 128
            kb.dma("sp", chq[s], qb[s][:], qTv[:, :, c0:c1], writes=[(tag, "q", s)])
            kb.dma("sp", chq[s], qib[s][:], qiTv[:, :, c0:c1], writes=[(tag, "qi")])
            kb.dma("sp", chq[s], wt[s][:], S["tmaj"][c0:c1, 0:16], writes=[(tag, "w", s)])
            nW = (i + 4) // 4
            Wi = nW * 512
            sk = (tag, "score")
            TT(kb, "dve", dsg[:], ident_b[:].unsqueeze(1).to_broadcast([128, 16, 128]),
               wt[s][:, 0:16].unsqueeze(2).to_broadcast([128, 16, 128]), ALU.mult, [(tag, "w", s), (tag, "cst")],
               [(tag, "dsg")])
            for w in range(nW):
                sp_ = cnt["sps"] % 2
                cnt["sps"] += 1
                units = []

                def acc(u):
                    h0, r_ = u
                    for e_ in range(2):
                        MM(kb, sps[sp_][:], dsg[:, h0 + e_, :], rbuf[r_][:, e_, :], h0 + e_ == 0, h0 + e_ == 15,
                           [(tag, "dsg"), (tag, "r", r_)], [(tag, "sps", 0)], signal=(e_ == 1))

                for pair in range(8):
                    p = cnt["ips"] % 2
                    cnt["ips"] += 1
                    r = cnt["r"] % 2
                    cnt["r"] += 1
                    for e_ in range(2):
                        base = e_ * 64
                        MM(kb, ips[p][:, e_, :], qib[s][base:base + 64, pair, :],
                           kiT[base:base + 64, w * 512:(w + 1) * 512], True, True, [(tag, "qi"), (tag, "kiT")],
                           [(tag, "ips", p)], signal=(e_ == 1))
                    if pair % 4 != 3:
                        ACT(kb, rbuf[r][:], ips[p][:], AF.Relu, [(tag, "ips", p)], [(tag, "r", r)])
                    else:
                        TS(kb, "dve", rbuf[r][:], ips[p][:], 0.0, None, ALU.max, None, [(tag, "ips", p)], [(tag, "r", r)])
                    if units:
                        acc(units.pop())
                    units.append((2 * pair, r))
                acc(units.pop())
                CP(kb, "dve", score[s][:, w * 512:(w + 1) * 512], sps[sp_][:], [(tag, "sps", 0)], [sk])
            b = bs[s]
            bk = (tag, "bs", s)
            TS(kb, "dve", junk[:, 0:Wi], score[s][:, 0:Wi], 1.0, None, ALU.mult, ALU.max, [sk], [(tag, "mask"), bk],
               accum_out=b[:, 0:1])
            TS(kb, "dve", junk[:, 0:Wi], score[s][:, 0:Wi], -1.0, None, ALU.mult, ALU.max, [sk], [(tag, "mask"), bk],
               accum_out=b[:, 6:7])
            TT(kb, "dve", b[:, 0:1], b[:, 0:1], b[:, 6:7], ALU.max, [bk], [bk])
            TT(kb, "dve", score[s][:, Wi - 512:Wi], score[s][:, Wi - 512:Wi],
               C["cbase"][:, 384 - (i % 4) * 128:896 - (i % 4) * 128], ALU.add, [sk], [sk])
            TS(kb, "dve", b[:, 1:2], b[:, 0:1], -1.001, -1e-20, ALU.mult, ALU.add, [bk], [bk])
            TS(kb, "dve", b[:, 2:3], b[:, 0:1], 2.002, 2e-20, ALU.mult, ALU.add, [bk], [bk])
            wk = wks[s]
            TS(kb, "dve", wk[:], fpow[:], b[:, 2:3], None, ALU.mult, None, [bk, (tag, "cst")], [bk])
            TT(kb, "dve", b[:, 3:4], b[:, 1:2], wk[:, 0:1], ALU.add, [bk], [bk])
            for k in range(1, N_BIS + 1):
                TS(kb, "dve", junk[:, 0:Wi], score[s][:, 0:Wi], b[:, 3:4], None, ALU.is_ge, ALU.add,
                   [sk, bk], [(tag, "mask"), bk], accum_out=b[:, 4:5])
                TS(kb, "dve", b[:, 5:6], b[:, 4:5], 255.5, -0.5, ALU.is_ge, ALU.add, [bk], [bk])
                STT(kb, b[:, 3:4], b[:, 5:6], wk[:, k - 1:k], b[:, 3:4], ALU.mult, ALU.add, [bk], [bk])
            STT(kb, b[:, 1:2], wk[:, N_BIS - 1:N_BIS], -0.5, b[:, 3:4], ALU.mult, ALU.add, [bk], [bk])

        def stage_t(i):
            s = i % 2
            n = i + 1
            TS(kb, "dve", mask[:, 0:n * 128], score[s][:, 0:n * 128], bs[s][:, 1:2], None, ALU.is_ge, None,
               [(tag, "score"), (tag, "bs", s)], [(tag, "mask")])
            for j0 in range(0, n, 4):
                nb = min(4, n - j0)
                for jj in range(nb):
                    j = j0 + jj
                    TR(kb, tps[:, jj, :], mask[:, j * 128:(j + 1) * 128], ident_b[:], [(tag, "mask"), (tag, "cst")],
                       [(tag, "ips", 0)], signal=(jj == nb - 1))
                ACT(kb, maskT[s][:, j0:j0 + nb, :], tps[:, 0:nb, :], AF.Copy, [(tag, "ips", 0)], [(tag, "maskT")],
                    scale=30000.0, bias=-30000.0)

        def stage_b(i):
            s = i % 2
            n = i + 1
            groups = [(h, j0, min(4, n - j0)) for h in range(8) for j0 in range(0, n, 4)]
            slots = {}

            def qk(gi):
                h, j0, nb = groups[gi]
                p = cnt["lg"] % 2
                cnt["lg"] += 1
                x = cnt["pt"] % 2
                cnt["pt"] += 1
                slots[gi] = x
                for jj in range(nb):
                    j = j0 + jj
                    near = (i - j) <= 1
                    MM(kb, lg[p][:, jj, :], kT[:, h, j * 128:(j + 1) * 128], qb[s][:, h, :], True, False,
                       [(tag, "kT"), (tag, "q", s)], [(tag, "ips", p)], signal=False)
                    if near:
                        MM(kb, lg[p][:, jj, :], ident_b[:], biasS[:, h, i - j, :], False, False,
                           [(tag, "cst")], [(tag, "ips", p)], signal=False)
                    MM(kb, lg[p][:, jj, :], ident_b[:], maskT[s][:, j, :], False, True,
                       [(tag, "cst"), (tag, "maskT")], [(tag, "ips", p)], signal=(jj == nb - 1))
                ACT(kb, pt[x][:, 0:nb, :], lg[p][:, 0:nb, :], AF.Exp, [(tag, "ips", p)], [(tag, "pt", x)],
                    scale=SCALE_A, bias=C["cb"][:, h:h + 1])

            def pv(gi):
                h, j0, nb = groups[gi]
                x = slots.pop(gi)
                for jj in range(nb):
                    j = j0 + jj
                    MM(kb, po[0][:], Vt[:, j, h * 128:(h + 1) * 128], pt[x][:, jj, :], j == 0, j == n - 1,
                       [(tag, "V"), (tag, "pt", x)], [(tag, "po", 0)])
                    MM(kb, prs[0][:], ones_b[:], pt[x][:, jj, :], j == 0, j == n - 1,
                       [(tag, "cst"), (tag, "pt", x)], [(tag, "prs", 0)], signal=(jj == nb - 1))
                if j0 + nb == n:
                    CP(kb, "act", ost[s][:, h, 0:128], po[0][:], [(tag, "po", 0)], [(tag, "ost", h)])
                    CP(kb, "act", ost[s][:, h, 128:256], prs[0][:], [(tag, "prs", 0)], [(tag, "ost", h)])

            qk(0)
            for gi in range(len(groups)):
                if gi + 1 < len(groups):
                    qk(gi + 1)
                pv(gi)

        def stage_f(i):
            s = i % 2
            keys = [(tag, "ost", h) for h in range(8)]
            kb.dma("sp", chz, za[s][:], zav[:, :, i * 128:(i + 1) * 128], writes=[(tag, "za")])
            ACT(kb, za[s][:], za[s][:], AF.Silu, [(tag, "za")], [(tag, "za")])
            RECIP(kb, ost[s][:, :, 128:256], ost[s][:, :, 128:256], keys, keys)
            TT(kb, "dve", ost[s][:, :, 0:128], ost[s][:, :, 0:128], ost[s][:, :, 128:256], ALU.mult, keys, keys)
            TT(kb, "dve", yst[s][:], ost[s][:, :, 0:128], za[s][:], ALU.mult, keys + [(tag, "za")], [(tag, "yst")])
            kb.dma("pool", chy[s], yv[:, :, i * 128:(i + 1) * 128], yst[s][:], reads=[(tag, "yst")],
                   writes=[(tag, "y", i)])

        stage_a(0)
        stage_t(0)
        for i in range(NT):
            if i + 1 < NT:
                stage_a(i + 1)
            stage_b(i)
            if i + 1 < NT:
                stage_t(i + 1)
            stage_f(i)
        kb.end_phase()


def phase_gdn_prep(kb, C, pj, S):
    tag = "gp"
    with ExitStack() as st:
        W = T + 3
        HB = T // 2
        ident_b = kb.sb("gp_identb", [128, 128], BF16, st)
        dg = kb.sb("gp_dg", [128, 24, 4, 128], BF16, st)
        rawb = [kb.sb(f"gp_rawb{i}", [128, W], BF16, st) for i in range(2)]
        acc = [kb.sb(f"gp_acc{i}", [128, T], F32, st) for i in range(2)]
        sq = [kb.sb(f"gp_sq{i}", [128, T], F32, st) for i in range(2)]
        rn = [kb.sb(f"gp_rn{i}", [128, T], F32, st) for i in range(2)]
        pcv = kb.ps("gp_pcv", [128, HB], F32, st)
        ssp = kb.ps("gp_ssp", [128, HB], F32, st)
        chl = [kb.chan(f"gp_l{i}") for i in range(2)]
        chs = [kb.chan(f"gp_s{i}") for i in range(2)]
        CP(kb, "dve", ident_b[:], C["ident"][:], [], [(tag, "cst")])
        for cc in range(24):
            TT(kb, "dve", dg[:, cc, :, :], ident_b[:].unsqueeze(1).to_broadcast([128, 4, 128]),
               C["cvw"][:, cc, :].unsqueeze(2).to_broadcast([128, 4, 128]), ALU.mult, [(tag, "cst")], [(tag, "dg")])

        def head(cc):
            s = cc % 2
            r0 = 2048 + cc * 128
            MS(kb, "dve", rawb[s][:, 0:3], 0.0, [(tag, "raw", s)])
            kb.dma("pool", chl[s], rawb[s][:, 3:W], pj[r0:r0 + 128, :], writes=[(tag, "raw", s)])
            for hb in range(2):
                for b in range(4):
                    c0 = hb * HB + b * 512
                    for j in range(4):
                        MM(kb, pcv[:, b * 512:(b + 1) * 512], dg[:, cc, j, :], rawb[s][:, c0 + j:c0 + j + 512],
                           j == 0, j == 3, [(tag, "dg"), (tag, "raw", s)], [(tag, "pcv")], signal=(j == 3 and b == 3))
                ACT(kb, acc[s][:, hb * HB:(hb + 1) * HB], pcv[:], AF.Silu, [(tag, "pcv")], [(tag, "acc", s, hb)])
                if cc < 16:
                    ACT(kb, sq[s][:, hb * HB:(hb + 1) * HB], acc[s][:, hb * HB:(hb + 1) * HB], AF.Square,
                        [(tag, "acc", s, hb)], [(tag, "sq", s, hb)])
                    for b in range(4):
                        c0 = hb * HB + b * 512
                        MM(kb, ssp[:, b * 512:(b + 1) * 512], C["ones_f"][:], sq[s][:, c0:c0 + 512], True, True,
                           [(tag, "sq", s, hb)], [(tag, "ssp")], signal=(b == 3))
                    ACT(kb, rn[s][:, hb * HB:(hb + 1) * HB], ssp[:], AF.Sqrt, [(tag, "ssp")],
                        [(tag, "rn", s, hb)], bias=C["eps"][:, 0:1])

        def tail(cc):
            s = cc % 2
            akeys = [(tag, "acc", s, 0), (tag, "acc", s, 1)]
            if cc < 16:
                keys = [(tag, "rn", s, 0), (tag, "rn", s, 1)]
                RECIP(kb, rn[s][:], rn[s][:], keys, keys)
                STT(kb, acc[s][:], acc[s][:], (128 ** -0.5) if cc < 8 else 1.0, rn[s][:], ALU.mult, ALU.mult,
                    akeys + keys, akeys)
            kb.dma("sp", chs[s], S["gqkv"][cc * 128:(cc + 1) * 128, :], acc[s][:], reads=akeys,
                   writes=[("gqkv", cc)])

        head(0)
        for cc in range(24):
            if cc + 1 < 24:
                head(cc + 1)
            tail(cc)
        kb.end_phase()


import os
GDN_TILES = int(os.environ.get("GDN_TILES", "32"))
GDN_STOP = int(os.environ.get("GDN_STOP", "99"))
GDN_SUB = float(os.environ.get("GDN_SUB", "99"))
GDN_EVAC = os.environ.get("GDN_EVAC", "act")


def phase_gdn(kb, C, pj, S, y_dram):
    with ExitStack() as stc:
        C = dict(C)
        chc = kb.chan("gd_const")
        for name, shape, dt in GDN_CONST_SPECS:
            d = kb.nc.dram_tensor(name, list(shape), dt, kind="ExternalInput").ap()
            t = kb.sb("c_" + name, shape, dt, stc)
            kb.dma("sp", chc, t[:], d, writes=[("const", name)])
            C[name] = t
        kb.end_phase()
        phase_gdn_prep(kb, C, pj, S)
        _phase_gdn_main(kb, C, pj, S, y_dram)


def _phase_gdn_main(kb, C, pj, S, y_dram):
    tag = "gd"
    H = 8
    with ExitStack() as st:
        def fb(name, shape=(128, H, 128), dt=F32):
            return kb.sb("gd_" + name, list(shape), dt, st)

        tm = fb("tm", (128, NT, 32))
        beta = fb("beta", (128, NT, 8))
        g = fb("g", (128, NT, 8))
        t1 = fb("t1", (128, NT, 8))
        t2 = fb("t2", (128, NT, 8))
        nA = fb("nA", (128, 8))
        qT, kT, vT = fb("qT"), fb("kT"), fb("vT")
        gd, egrow, gcr = fb("gdiag"), fb("egrow"), fb("gcr")
        P1, E1, E2 = fb("P1"), fb("E1"), fb("E2")
        HS = (128, H, 128)
        X = [fb("X0", HS, BF16), fb("X1", HS, BF16)]
        Y = [fb("Y0", HS, BF16), fb("Y1", HS, BF16)]
        P, attnT = fb("P", HS, BF16), fb("attnT", HS, BF16)
        vb, kbg, kd, kd1 = fb("vb", HS, BF16), fb("kbg", HS, BF16), fb("kd", HS, BF16), fb("kd1", HS, BF16)
        smk = fb("smk", (128, 16))
        u, wT, qgT, vnew = fb("u"), fb("wT", HS, BF16), fb("qgT", HS, BF16), fb("vnew", HS, BF16)
        Sst, oacc, zb = fb("S"), fb("oacc"), fb("zb")
        Sb = fb("Sb", HS, BF16)
        ident_b = fb("identb", (128, 128), BF16)
        osq, orn = fb("osq"), fb("orn")
        yo = fb("yo", (128, H, 128), BF16)
        sm = fb("sm", (128, 64))
        pA = kb.ps("gd_pA", [128, H, 128], F32, st)
        pB = kb.ps("gd_pB", [128, H, 128], F32, st)
        pC = kb.ps("gd_pC", [128, H, 128], F32, st)
        pO = kb.ps("gd_pO", [128, H, 64], F32, st)
        psm = kb.ps("gd_psm", [128, 32], F32, st)
        ch = kb.chan("gd_l")
        chq = kb.chan("gd_q")
        chz = kb.chan("gd_z")
        chy = kb.chan("gd_y")
        K_ = lambda n: (tag, n)

        def bc_h(ap2d):
            return ap2d.unsqueeze(1).to_broadcast([128, H, 128])

        def bc_f(ap2d):
            return ap2d.unsqueeze(2).to_broadcast([128, H, 128])

        kb.dma("sp", ch, tm[:], S["tmaj"].rearrange("(n p) c -> p n c", p=128), writes=[K_("tm")])
        ACT(kb, beta[:], tm[:, :, 16:24], AF.Sigmoid, [K_("tm")], [K_("beta")])
        dtb = C["dtb_bc"][:].unsqueeze(1).to_broadcast([128, NT, 8])
        TT(kb, "dve", g[:], tm[:, :, 24:32], dtb, ALU.add, [K_("tm")], [K_("g")])
        TS(kb, "dve", t1[:], g[:], -1.0, None, ALU.mult, None, [K_("g")], [K_("t1")])
        TT(kb, "dve", t1[:], t1[:], g[:], ALU.max, [K_("t1"), K_("g")], [K_("t1")])
        ACT(kb, t1[:], t1[:], AF.Exp, [K_("t1")], [K_("t1")], scale=-1.0)
        TS(kb, "dve", t1[:], t1[:], 1.0, None, ALU.add, None, [K_("t1")], [K_("t1")])
        ACT(kb, t1[:], t1[:], AF.Ln, [K_("t1")], [K_("t1")])
        TS(kb, "dve", t2[:], g[:], 0.0, None, ALU.max, None, [K_("g")], [K_("t2")])
        TT(kb, "dve", t2[:], t2[:], t1[:], ALU.add, [K_("t1"), K_("t2")], [K_("t2")])
        ACT(kb, nA[:], C["alog_bc"][:], AF.Exp, [], [K_("nA")])
        TS(kb, "dve", nA[:], nA[:], -1.0, None, ALU.mult, None, [K_("nA")], [K_("nA")])
        TT(kb, "dve", g[:], t2[:], nA[:].unsqueeze(1).to_broadcast([128, NT, 8]), ALU.mult, [K_("t2"), K_("nA")],
           [K_("g")])
        MS(kb, "dve", Sst[:], 0.0, [K_("S")])
        MS(kb, "dve", Sb[:], 0.0, [K_("Sb")])
        MS(kb, "dve", vnew[:], 0.0, [K_("vnew")])
        CP(kb, "dve", ident_b[:], C["ident"][:], [], [K_("identb")])
        pAb = pA[:, 0:4, :].bitcast(BF16).rearrange("p a (c b) -> p (a c) b", b=128)
        gq = S["gqkv"]
        qv = gq[0:1024, :].rearrange("(h p) t -> p h t", p=128)
        kv = gq[1024:2048, :].rearrange("(h p) t -> p h t", p=128)
        vv = gq[2048:3072, :].rearrange("(h p) t -> p h t", p=128)
        zv = pj[5120:6144, :].rearrange("(h p) t -> p h t", p=128)
        yv = y_dram[1024:2048, :].rearrange("(h p) t -> p h t", p=128)

        for n in range(GDN_TILES if GDN_STOP > 0 else 0):
            c0, c1 = n * 128, (n + 1) * 128
            kb.dma("sp", chq, qT[:], qv[:, :, c0:c1], writes=[K_("qT")])
            kb.dma("sp", chq, kT[:], kv[:, :, c0:c1], writes=[K_("kT")])
            kb.dma("sp", chq, vT[:], vv[:, :, c0:c1], writes=[K_("vT")])
            kb.dma("sp", chz, zb[:], zv[:, :, c0:c1], writes=[K_("zb")])
            gn = g[:, n, :]
            bn = beta[:, n, :]
            MM(kb, psm[:, 0:8], C["U2"][:], gn, True, True, [K_("g")], [K_("psm")], signal=False)
            MM(kb, psm[:, 8:16], C["Bsame"][:], gn, True, True, [K_("g")], [K_("psm")], signal=False)
            MM(kb, psm[:, 16:24], C["Bsel0"][:], gn, True, True, [K_("g")], [K_("psm")], signal=False)
            MM(kb, psm[:, 24:32], C["Bsel1"][:], gn, True, True, [K_("g")], [K_("psm")])
            CP(kb, "dve", sm[:, 0:32], psm[:], [K_("psm")], [K_("sm")])
            gc = sm[:, 0:8]
            ACT(kb, sm[:, 32:40], sm[:, 0:8], AF.Exp, [K_("sm")], [K_("sm")])
            TT(kb, "dve", sm[:, 40:48], sm[:, 8:16], sm[:, 0:8], ALU.subtract, [K_("sm")], [K_("sm")])
            ACT(kb, sm[:, 40:48], sm[:, 40:48], AF.Exp, [K_("sm")], [K_("sm")])
            ACT(kb, sm[:, 16:32], sm[:, 16:32], AF.Exp, [K_("sm")], [K_("sm")])
            TT(kb, "dve", sm[:, 48:56], sm[:, 32:40], bn, ALU.mult, [K_("sm"), K_("beta")], [K_("sm")])
            TS(kb, "dve", sm[:, 56:64], bn, -1.0, None, ALU.mult, None, [K_("beta")], [K_("sm")])
            if GDN_SUB <= 0:
                continue
            TT(kb, "dve", gd[:], bc_h(C["U2"][:]), bc_f(gn), ALU.mult, [K_("g")], [K_("gdiag")])
            for b in range(2):
                MM(kb, pA[:, 4 * b:4 * b + 4, :], C["ones_f"][:], gd[:, 4 * b:4 * b + 4, :], True, True,
                   [K_("gdiag")], [K_("pA")], signal=(b == 1))
            if GDN_SUB <= 0.3:
                continue
            CP(kb, "dve", gcr[:], pA[:], [K_("pA")], [K_("gcr")])
            ACT(kb, egrow[:], gcr[:], AF.Exp, [K_("gcr")], [K_("egrow")])
            if GDN_SUB <= 0.4:
                continue
            TT(kb, "dve", P1[:], gcr[:], bc_f(gc), ALU.subtract, [K_("gcr"), K_("sm")], [K_("P1")])
            if GDN_SUB <= 0.5:
                continue
            ACT(kb, E1[:], P1[:], AF.Relu, [K_("P1")], [K_("E1")])
            ACT(kb, E1[:], E1[:], AF.Exp, [K_("E1")], [K_("E1")], scale=-1.0)
            if GDN_SUB <= 0.6:
                continue
            ACT(kb, E2[:], P1[:], AF.Relu, [K_("P1")], [K_("E2")], scale=-1.0)
            ACT(kb, E2[:], E2[:], AF.Exp, [K_("E2")], [K_("E2")], scale=-1.0)
            TT(kb, "dve", E1[:], E1[:], bc_h(C["MLs"][:]), ALU.mult, [K_("E1")], [K_("E1")])
            TT(kb, "dve", E1[:], E1[:], bc_f(sm[:, 56:64]), ALU.mult, [K_("E1"), K_("sm")], [K_("E1")])
            TT(kb, "dve", E2[:], E2[:], bc_h(C["MU"][:]), ALU.mult, [K_("E2")], [K_("E2")])
            if GDN_SUB <= 1:
                continue
            for h in range(H):
                MM(kb, pB[:, h, :], kT[:, h, :], kT[:, h, :], True, True, [K_("kT")], [K_("pB")], signal=(h == H - 1))
            TT(kb, "dve", X[0][:], pB[:], E1[:], ALU.mult, [K_("pB"), K_("E1")], [K_("X0")])
            for h in range(H):
                MM(kb, pC[:, h, :], kT[:, h, :], qT[:, h, :], True, True, [K_("kT"), K_("qT")], [K_("pC")], signal=(h == H - 1))
            TT(kb, "dve", attnT[:], pC[:], E2[:], ALU.mult, [K_("pC"), K_("E2")], [K_("attnT")])
            for h in range(H):
                TR(kb, pAb[:, h, :], X[0][:, h, :], ident_b[:], [K_("X0"), K_("identb")], [K_("pA")],
                   signal=(h == H - 1))
            CP(kb, "dve", Y[0][:], pAb, [K_("pA")], [K_("Y0")])
            TT(kb, "dve", P[:], Y[0][:], bc_h(C["ident"][:]), ALU.add, [K_("Y0")], [K_("P")])
            if GDN_SUB <= 2:
                continue
            cur = 0
            for lvl in range(5):
                nxt = 1 - cur
                xk, yk = K_(f"X{cur}"), K_(f"Y{cur}")
                xn, yn = K_(f"X{nxt}"), K_(f"Y{nxt}")
                for h in range(H):
                    MM(kb, pB[:, h, :], Y[cur][:, h, :], X[cur][:, h, :], True, True, [xk, yk], [K_("pB")], signal=(h == H - 1))
                CP(kb, GDN_EVAC, X[nxt][:], pB[:], [K_("pB")], [xn])
                if lvl < 4:
                    for h in range(H):
                        MM(kb, pC[:, h, :], X[cur][:, h, :], Y[cur][:, h, :], True, True, [xk, yk], [K_("pC")], signal=(h == H - 1))
                    CP(kb, GDN_EVAC, Y[nxt][:], pC[:], [K_("pC")], [yn])
                for h in range(H):
                    MM(kb, pA[:, h, :], X[nxt][:, h, :], P[:, h, :], True, True, [xn, K_("P")], [K_("pA")], signal=(h == H - 1))
                TT(kb, "dve", P[:], P[:], pA[:], ALU.add, [K_("pA"), K_("P")], [K_("P")])
                cur = nxt
            if GDN_SUB <= 3:
                continue
            for h in range(H):
                TR(kb, pB[:, h, :], kT[:, h, :], C["ident"][:], [K_("kT")], [K_("pB")], signal=(h == H - 1))
            TT(kb, "dve", kbg[:], pB[:], bc_f(sm[:, 48:56]), ALU.mult, [K_("pB"), K_("sm")], [K_("kbg")])
            TS(kb, "dve", smk[:, 0:8], sm[:, 40:48], C["Bsel0"][:, 0:1], None, ALU.mult, None, [K_("sm")], [K_("smk")])
            TS(kb, "dve", smk[:, 8:16], sm[:, 40:48], C["Bsel1"][:, 0:1], None, ALU.mult, None, [K_("sm")], [K_("smk")])
            TT(kb, "dve", kd[:], pB[:], bc_f(smk[:, 0:8]), ALU.mult, [K_("pB"), K_("smk")], [K_("kd")])
            TT(kb, "dve", kd1[:], pB[:], bc_f(smk[:, 8:16]), ALU.mult, [K_("pB"), K_("smk")], [K_("kd")])
            for h in range(H):
                TR(kb, pC[:, h, :], vT[:, h, :], C["ident"][:], [K_("vT")], [K_("pC")], signal=(h == H - 1))
            TT(kb, "dve", vb[:], pC[:], bc_f(bn), ALU.mult, [K_("pC"), K_("beta")], [K_("vb")])
            for h in range(H):
                MM(kb, pA[:, h, :], P[:, h, :], vb[:, h, :], True, True, [K_("P"), K_("vb")], [K_("pA")], signal=(h == H - 1))
            CP(kb, "dve", u[:], pA[:], [K_("pA")], [K_("u")])
            for h in range(H):
                MM(kb, pB[:, h, :], kbg[:, h, :], P[:, h, :], True, True, [K_("P"), K_("kbg")], [K_("pB")], signal=(h == H - 1))
            CP(kb, GDN_EVAC, wT[:], pB[:], [K_("pB")], [K_("wT")])
            TT(kb, "dve", qgT[:], qT[:], egrow[:], ALU.mult, [K_("qT"), K_("egrow")], [K_("qgT")])
            if GDN_STOP <= 1:
                continue
            for c in range(2):
                r0, r1 = c * 64, (c + 1) * 64
                for h in range(H):
                    MM(kb, pA[r0:r1, h, :], wT[:, h, r0:r1], Sb[:, h, :], True, True, [K_("wT"), K_("Sb")], [K_("pA")],
                       signal=(h == H - 1))
                if GDN_SUB == 10 or (GDN_SUB == 10.5 and c == 1):
                    continue
                TT(kb, "dve", vnew[r0:r1, :, :], u[r0:r1, :, :], pA[r0:r1, :, :], ALU.subtract, [K_("u"), K_("pA")],
                   [K_("vnew")])
                if GDN_SUB == 11:
                    continue
                for h in range(H):
                    MM(kb, pO[:, h, :], Sb[:, h, :], qgT[:, h, r0:r1], True, False, [K_("Sb"), K_("qgT")], [K_("pO")])
                    MM(kb, pO[:, h, :], vnew[r0:r1, h, :], attnT[r0:r1, h, r0:r1], False, True,
                       [K_("vnew"), K_("attnT")], [K_("pO")], signal=(h == H - 1))
                if GDN_SUB == 12:
                    continue
                CP(kb, GDN_EVAC, oacc[:, :, r0:r1], pO[:], [K_("pO")], [K_("oacc")])
                if GDN_STOP <= 2:
                    continue
                for h in range(H):
                    MM(kb, pB[:, h, :], (kd, kd1)[c][:, h, :], vnew[:, h, :], True, True, [K_("kd"), K_("vnew")],
                       [K_("pB")], signal=(h == H - 1))
                TT(kb, "dve", Sst[:], Sst[:], bc_f(sm[:, 16 + 8 * c:24 + 8 * c]), ALU.mult, [K_("S"), K_("sm")],
                   [K_("S")])
                TT(kb, "dve", Sst[:], Sst[:], pB[:], ALU.add, [K_("S"), K_("pB")], [K_("S")])
                CP(kb, "pool", Sb[:], Sst[:], [K_("S")], [K_("Sb")])
            ACT(kb, osq[:], oacc[:], AF.Square, [K_("oacc")], [K_("osq")])
            for b in range(2):
                MM(kb, pC[:, 4 * b:4 * b + 4, :], C["ones_f"][:], osq[:, 4 * b:4 * b + 4, :], True, True, [K_("osq")],
                   [K_("pC")], signal=(b == 1))
            CP(kb, "dve", orn[:], pC[:], [K_("pC")], [K_("orn")])
            ACT(kb, orn[:], orn[:], AF.Sqrt, [K_("orn")], [K_("orn")], bias=C["eps"][:, 0:1], scale=1.0 / 128)
            RECIP(kb, orn[:], orn[:], [K_("orn")], [K_("orn")])
            STT(kb, oacc[:], oacc[:], C["onorm"][:, 0:1], orn[:], ALU.mult, ALU.mult, [K_("oacc"), K_("orn")],
                [K_("oacc")])
            ACT(kb, zb[:], zb[:], AF.Silu, [K_("zb")], [K_("zb")])
            TT(kb, "dve", yo[:], oacc[:], zb[:], ALU.mult, [K_("oacc"), K_("zb")], [K_("yo")])
            kb.dma("pool", chy, yv[:, :, c0:c1], yo[:], reads=[K_("yo")], writes=[("y0b", n)])
        kb.end_phase()

def load_consts(kb, nc, names_shapes):
    C = {}
    ch = kb.chan("const")
    for name, shape, dt in names_shapes:
        d = nc.dram_tensor(name, list(shape), dt, kind="ExternalInput").ap()
        if name == "cbase":
            t = kb.sb("c_" + name, shape, BF16)
            kb.dma("pool", ch, t[:], d, writes=[("const", name)])
        else:
            t = kb.sb("c_" + name, shape, dt)
            kb.dma("sp", ch, t[:], d, writes=[("const", name)])
        C[name] = t
    kb.end_phase()
    with ExitStack() as st:
        d = nc.dram_tensor("biasT", [128, 8, 2, 128], F32, kind="ExternalInput").ap()
        C["biasS"] = kb.sb("c_biasS", [128, 8, 2, 128], BF16)
        bt = kb.sb("biasT_tmp", [128, 8, 2, 128], F32, st)
        kb.dma("sp", ch, bt[:], d, writes=[("const", "biasT")])
        for h in range(8):
            TS(kb, "dve", C["biasS"][:, h, :, :], bt[:, h, :, :], C["cb"][:, h:h + 1], 128 ** 0.5,
               ALU.subtract, ALU.mult, [("const", "biasT")], [("const", "biasS")])
        kb.end_phase()
    return C


CONST_SPECS = [
    ("ident", (128, 128), F32),
    ("ones_f", (128, 128), F32),
    ("eps", (128, 1), F32),
    ("normwT", (128, 2, 16), F32),
    ("cbase", (128, 896), F32),
    ("cb", (128, 8), F32),
    ("qnormT", (128, 4), F32),
    ("kvnormT", (128, 2), F32),
    ("qgain", (128, 1), F32),
    ("kgain", (128, 1), F32),
    ("ikw", (128, 1), F32),
    ("ikb", (128, 1), F32),
]

CD_CONST_SPECS = [
    ("dww", (128, 8, 31), F32),
    ("dwb", (128, 8), F32),
    ("lnw", (128, 8), F32),
    ("lnb", (128, 8), F32),
    ("dcw", (128, 8, 3), F32),
]

GDN_CONST_SPECS = [
    ("cvw", (128, 24, 4), F32),
    ("alog_bc", (128, 8), F32),
    ("dtb_bc", (128, 8), F32),
    ("onorm", (128, 1), F32),
    ("U2", (128, 128), F32),
    ("Bsame", (128, 128), F32),
    ("Bsel0", (128, 128), F32),
    ("Bsel1", (128, 128), F32),
    ("MLs", (128, 128), F32),
    ("MU", (128, 128), F32),
]


def build_program(layers=(0, 1), l0_parts=("a", "b"), debug_out=False):
    nc = bass.Bass("TRN2", target_bir_lowering=False)

    def din(name, shape, dt=F32):
        return nc.dram_tensor(name, list(shape), dt, kind="ExternalInput").ap()

    def dscr(name, shape, dt=F32):
        return nc.dram_tensor(name, list(shape), dt, kind="Internal").ap()

    x = din("x", [T, D])
    out = nc.dram_tensor("out", [T, D], F32, kind="ExternalOutput").ap()
    kb = KB(nc)
    C = load_consts(kb, nc, CONST_SPECS)
    src = x
    if 0 in layers:
        ab_w_in = din("ab_w_in", [48, 128, 16 * 128])
        ab_w_out = din("ab_w_out", [16, 128, D])
        W = {"w_uqiq": din("w_uqiq", [16, 128, 4 * 128]), "w_uk": din("w_uk", [8, 128, 2 * 128]),
             "w_uv": din("w_uv", [2, 128, 1024])}
        pj0 = dscr("pj0", [6144, T])
        S = {"qT": dscr("s_qT", [8, 128, T], BF16), "qiT": dscr("s_qiT", [8, 128, T], BF16),
             "kT": dscr("s_kT", [8, 128, T], BF16), "V": dscr("s_V", [T, 1024], BF16),
             "kiT": dscr("s_kiT", [128, T], BF16), "tmaj": dscr("s_tmaj", [T, 32]),
             "gqkv": dscr("s_gqkv", [3072, T])}
        if debug_out:
            y0 = nc.dram_tensor("y0", [2048, T], BF16, kind="ExternalOutput").ap()
        else:
            y0 = dscr("y0", [2048, T], BF16)
        x1 = dscr("x1", [T, D]) if 1 in layers else out
        phase_inproj(kb, C, "l0", src, C["normwT"][:, 0, :], ab_w_in, 48, pj0)
        phase_ki_tmaj(kb, C, pj0, S)
        if "a" in l0_parts:
            phase_dsa_prep(kb, C, pj0, W, S)
            phase_dsa_attn(kb, C, pj0, S, y0)
        if "b" in l0_parts:
            phase_gdn(kb, C, pj0, S, y0)
        if not debug_out:
            phase_outproj(kb, C, "l0o", y0, ab_w_out, src, x1)
        src = x1
    if 1 in layers:
        cd_w_in = din("cd_w_in", [56, 128, 16 * 128])
        cd_w_out = din("cd_w_out", [16, 128, D])
        pj1 = dscr("pj1", [7168, T])
        y1 = dscr("y1", [2048, T], BF16)
        phase_inproj(kb, C, "l1", src, C["normwT"][:, 1, :], cd_w_in, 56, pj1)
        phase_cd_mix(kb, C, pj1, C, y1)
        phase_outproj(kb, C, "l1o", y1, cd_w_out, src, out)
    kb.finish()
    kb.emit()
    kb.close()
    return nc, kb


def tile_w_in(w, nch):
    K, N = w.shape
    assert N == nch * 128
    return np.ascontiguousarray(w.reshape(K // 128, 128, nch, 128).transpose(2, 1, 0, 3)).reshape(nch, 128, -1)


def t5_bucket_np(dist):
    import math
    max_exact = 16
    dd = np.maximum(dist, 1).astype(np.float32)
    large = max_exact + (np.log(dd / max_exact) / math.log(128 / max_exact) * (32 - max_exact)).astype(np.int32)
    large = np.minimum(large, 31)
    return np.where(dist < max_exact, dist, large)


def colT(v, k):
    return np.ascontiguousarray(np.asarray(v, np.float32).reshape(k, 128).T)


def host_consts(inp):
    f = np.float32
    c = {}
    c["ident"] = np.eye(128, dtype=f)
    c["ones_f"] = np.ones((128, 128), f)
    c["eps"] = np.full((128, 1), EPS, f)
    c["normwT"] = np.ascontiguousarray(inp["norm_w"].reshape(2, 16, 128).transpose(2, 0, 1)).astype(f)
    c["dww"] = np.ascontiguousarray(inp["c_dw_w"][0].reshape(31, 8, 128).transpose(2, 1, 0)).astype(f)
    c["dwb"] = colT(inp["c_dw_b"][0], 8)
    c["lnw"] = colT(inp["c_ln_w"][0], 8)
    c["lnb"] = colT(inp["c_ln_b"][0], 8)
    c["dcw"] = np.ascontiguousarray(inp["d_conv_w"][0].reshape(3, 8, 128).transpose(2, 1, 0)).astype(f)
    r = np.arange(128)[:, None]
    cc = np.arange(896)[None, :]
    c["cbase"] = np.where(cc <= r + 384, 0.0, -1e30).astype(f)
    kl = np.arange(128)[:, None, None]
    dd = np.arange(2)[None, :, None]
    ql = np.arange(128)[None, None, :]
    dist = np.maximum(dd * 128 + ql - kl, 0)
    bt = np.asarray(inp["rel_bias"], f)[t5_bucket_np(dist)]
    c["biasT"] = np.ascontiguousarray(bt.transpose(0, 3, 1, 2))
    c["cb"] = np.ascontiguousarray(np.broadcast_to(np.asarray(inp["rel_bias"], f)[31][None, :], (128, 8)))
    c["qnormT"] = colT(inp["a_q_norm"][0], 4)
    c["kvnormT"] = colT(inp["a_kv_norm"][0], 2)
    c["qgain"] = np.asarray(inp["a_q_gain"][0], f).reshape(128, 1).copy()
    c["kgain"] = np.asarray(inp["a_k_gain"][0], f).reshape(128, 1).copy()
    c["ikw"] = np.tile(np.asarray(inp["a_ik_norm_w"][0], f), 2).reshape(128, 1).copy()
    c["ikb"] = np.tile(np.asarray(inp["a_ik_norm_b"][0], f), 2).reshape(128, 1).copy()
    c["cvw"] = np.ascontiguousarray(np.asarray(inp["b_conv_w"][0], f).reshape(4, 24, 128).transpose(2, 1, 0))
    c["alog_bc"] = np.ascontiguousarray(np.broadcast_to(np.asarray(inp["b_a_log"][0], f)[None, :], (128, 8)))
    c["dtb_bc"] = np.ascontiguousarray(np.broadcast_to(np.asarray(inp["b_dt_bias"][0], f)[None, :], (128, 8)))
    c["onorm"] = np.asarray(inp["b_o_norm"][0], f).reshape(128, 1).copy()
    a = np.arange(128)
    same = (a[:, None] // 64) == (a[None, :] // 64)
    c["U2"] = (same & (a[:, None] <= a[None, :])).astype(f)
    c["Bsame"] = same.astype(f)
    c["Bsel0"] = np.ascontiguousarray(np.broadcast_to((a[:, None] < 64), (128, 128))).astype(f)
    c["Bsel1"] = np.ascontiguousarray(np.broadcast_to((a[:, None] >= 64), (128, 128))).astype(f)
    c["MLs"] = (same & (a[:, None] > a[None, :])).astype(f)
    c["MU"] = (same & (a[:, None] <= a[None, :])).astype(f)
    return c


def host_shared(inp, layers=(0, 1)):
    f = np.float32
    sh = host_consts(inp)
    if 0 in layers:
        w = np.asarray(inp["ab_w_in"][0], f)
        wp = np.zeros((D, 6144), f)
        wp[:, 0:832] = w[:, 0:832]
        wp[:, 896:912] = w[:, 832:848]
        wp[:, 912:928] = w[:, 4944:4960]
        wp[:, 1024:2048] = w[:, 848:1872]
        wp[:, 2048:5120] = w[:, 1872:4944]
        wp[:, 5120:6144] = w[:, 4960:5984]
        sh["ab_w_in"] = tile_w_in(wp, 48)
        sh["ab_w_out"] = np.ascontiguousarray(np.asarray(inp["ab_w_out"][0], f).reshape(16, 128, D))
        sh["w_uqiq"] = tile_w_in(np.concatenate([inp["a_w_uq"][0], inp["a_w_iq"][0]], axis=1).astype(f), 16)
        sh["w_uk"] = tile_w_in(np.asarray(inp["a_w_uk"][0], f), 8)
        sh["w_uv"] = np.ascontiguousarray(np.asarray(inp["a_w_uv"][0], f).reshape(2, 128, 1024))
    if 1 in layers:
        sh["cd_w_in"] = tile_w_in(np.asarray(inp["cd_w_in"][0], f), 56)
        sh["cd_w_out"] = np.ascontiguousarray(np.asarray(inp["cd_w_out"][0], f).reshape(16, 128, D))
    return sh


def kernel(**inputs):
    inp = {k: np.asarray(v) for k, v in inputs.items()}
    nc, kb = build_program()
    sh = host_shared(inp)
    x = np.ascontiguousarray(inp["x"], dtype=np.float32)
    in_maps = [dict(sh, x=x[b]) for b in range(8)]
    res = run_bass_kernel_spmd(nc, in_maps, core_ids=list(range(8)))
    return np.stack([np.asarray(r["out"], np.float32) for r in res.results], axis=0)
```

```python
from contextlib import ExitStack
import numpy as np
import concourse.bass as bass
import concourse.mybir as mybir
from concourse.bass_utils import run_bass_kernel_spmd

F32 = mybir.dt.float32
BF16 = mybir.dt.bfloat16
ALU = mybir.AluOpType
AF = mybir.ActivationFunctionType
AX = mybir.AxisListType

T = 4096
D = 2048
NT = T // 128
EPS = 1e-6
ENGS = ("pe", "act", "dve", "pool", "sp")


class Chan:
    def __init__(self, sem, name):
        self.sem = sem
        self.name = name
        self.n = 0


class KB:
    def __init__(self, nc):
        self.nc = nc
        self.es = ExitStack()
        self.q = {e: [] for e in ENGS}
        self.sems = {}
        self.cnt = {}
        self.seen = {e: {} for e in ENGS}
        self.lastw = {}
        self.readers = {}
        self.chans = []
        self.chan_by_sem = {}
        self.nins = 0
        self.pending = {e: False for e in ENGS}
        for e in ENGS:
            self.sems[e] = self.es.enter_context(nc.semaphore("s_" + e))
            self.cnt[e] = 0

    def sb(self, name, shape, dt, stack=None):
        return (stack or self.es).enter_context(self.nc.sbuf_tensor(name, list(shape), dt))

    def ps(self, name, shape, dt=F32, stack=None):
        return (stack or self.es).enter_context(self.nc.psum_tensor(name, list(shape), dt))

    def chan(self, name):
        c = Chan(self.es.enter_context(self.nc.semaphore("c_" + name)), name)
        self.chans.append(c)
        self.chan_by_sem[id(c.sem)] = c
        return c

    def _need0(self, eng, sem, val):
        ch = self.chan_by_sem.get(id(sem))
        if ch is not None:
            val = max(val, 16 * ch.n)
        cur = self.seen[eng].get(id(sem), 0)
        if val > cur:
            self.seen[eng][id(sem)] = val
            self.q[eng].append(("wait", sem, val))

    def _deps(self, eng, reads, writes, my_sem):
        for r in reads:
            ev = self.lastw.get(r)
            if ev is not None:
                self._need(eng, ev[0], ev[1])
        for w in writes:
            ev = self.lastw.get(w)
            if ev is not None:
                self._need(eng, ev[0], ev[1])
            rd = self.readers.get(w)
            if rd:
                for sem, val in rd.values():
                    if sem is my_sem:
                        continue
                    self._need(eng, sem, val)

    def _need(self, eng, sem, val):
        if eng == "pe" and sem is self.sems["pe"]:
            return
        self._need0(eng, sem, val)

    def _commit(self, ev, reads, writes):
        for w in writes:
            self.lastw[w] = ev
            self.readers[w] = {}
        for r in reads:
            d = self.readers.setdefault(r, {})
            d[id(ev[0])] = ev

    def op(self, eng, fn, reads=(), writes=(), signal=True):
        sem = self.sems[eng]
        self._deps(eng, reads, writes, sem)
        if signal:
            self.cnt[eng] += 1
            self.pending[eng] = False
            ev = (sem, self.cnt[eng])
            self.q[eng].append(("ins", fn, sem, 1))
        else:
            self.pending[eng] = True
            ev = (sem, self.cnt[eng] + 1)
            self.q[eng].append(("ins0", fn))
        self._commit(ev, reads, writes)
        self.nins += 1
        return ev

    def dma(self, eng, ch, out, in_, reads=(), writes=(), **kw):
        self._deps(eng, reads, writes, None)
        ch.n += 1
        ev = (ch.sem, 16 * ch.n)
        self.q[eng].append(("ins", lambda e, o=out, i=in_, k=kw: e.dma_start(out=o, in_=i, **k), ch.sem, 16))
        self._commit(ev, reads, writes)
        self.nins += 1
        return ev

    def _flush_pending(self):
        for e in ENGS:
            assert not self.pending[e], "non-signaling op left pending at a barrier on " + e

    def _all_events(self):
        self._flush_pending()
        evs = [(self.sems[e], self.cnt[e]) for e in ENGS if self.cnt[e] > 0]
        evs += [(c.sem, 16 * c.n) for c in self.chans if c.n > 0]
        return evs

    def barrier(self):
        evs = self._all_events()
        for e in ENGS:
            for sem, val in evs:
                self._need(e, sem, val)

    def finish(self, final_eng="sp"):
        for sem, val in self._all_events():
            self._need(final_eng, sem, val)

    def emit(self):
        nc = self.nc
        q = self.q
        self.q = {e: [] for e in ENGS}

        def replay(eng_obj, items):
            for it in items:
                if it[0] == "wait":
                    eng_obj.wait_ge(it[1], it[2])
                elif it[0] == "ins0":
                    it[1](eng_obj)
                else:
                    it[1](eng_obj).then_inc(it[2], it[3])

        with nc.Block() as block:
            @block.tensor
            def _(e):
                replay(e, q["pe"])

            @block.scalar
            def _(e):
                replay(e, q["act"])

            @block.vector
            def _(e):
                replay(e, q["dve"])

            @block.gpsimd
            def _(e):
                replay(e, q["pool"])

            @block.sync
            def _(e):
                replay(e, q["sp"])

    def end_phase(self):
        self.barrier()
        self.emit()

    def close(self):
        self.es.close()


def MM(kb, out, lhsT, rhs, start, stop, reads, writes, signal=None):
    if signal is None:
        signal = stop
    return kb.op("pe", lambda e: e.matmul(out, lhsT, rhs, start=start, stop=stop), reads, writes, signal=signal)


def TR(kb, out, in_, ident, reads, writes, signal=True):
    return kb.op("pe", lambda e: e.transpose(out, in_, ident), reads, writes, signal=signal)


def ACT(kb, out, in_, func, reads, writes, bias=None, scale=None, accum_out=None):
    kw = {}
    if bias is not None:
        kw["bias"] = bias
    if scale is not None:
        kw["scale"] = scale
    if accum_out is not None:
        kw["accum_out"] = accum_out
    return kb.op("act", lambda e: e.activation(out=out, in_=in_, func=func, **kw), reads, writes)


def TS(kb, eng, out, in0, s1, s2, op0, op1, reads, writes, accum_out=None):
    kw = {}
    if op1 is not None:
        kw["op1"] = op1
    if accum_out is not None:
        kw["accum_out"] = accum_out
    return kb.op(eng, lambda e: e.tensor_scalar(out=out, in0=in0, scalar1=s1, scalar2=s2, op0=op0, **kw),
                 reads, writes)


def TT(kb, eng, out, in0, in1, op, reads, writes):
    return kb.op(eng, lambda e: e.tensor_tensor(out=out, in0=in0, in1=in1, op=op), reads, writes)


def STT(kb, out, in0, scalar, in1, op0, op1, reads, writes):
    return kb.op("dve", lambda e: e.scalar_tensor_tensor(out=out, in0=in0, scalar=scalar, in1=in1,
                                                         op0=op0, op1=op1), reads, writes)


def CP(kb, eng, out, in_, reads, writes):
    if eng == "act":
        return kb.op("act", lambda e: e.copy(out=out, in_=in_), reads, writes)
    return kb.op(eng, lambda e: e.tensor_copy(out=out, in_=in_), reads, writes)


def MS(kb, eng, ap, val, writes):
    return kb.op(eng, lambda e: e.memset(ap, val), (), writes)


def RECIP(kb, out, in_, reads, writes):
    return kb.op("dve", lambda e: e.reciprocal(out=out, in_=in_), reads, writes)


def phase_norm_T(kb, C, x_dram, normwT, hT, tag):
    with ExitStack() as st:
        xt = [kb.sb(f"{tag}_xt{i}", [128, D], F32, st) for i in range(2)]
        xn = [kb.sb(f"{tag}_xn{i}", [128, D], F32, st) for i in range(2)]
        sq = kb.sb(f"{tag}_sq", [128, D], BF16, st)
        sm = [kb.sb(f"{tag}_sm{i}", [128, 4], F32, st) for i in range(2)]
        pst = [kb.ps(f"{tag}_pt{i}", [128, 8, 128], F32, st) for i in range(2)]
        ch = [kb.chan(f"{tag}_x{i}") for i in range(2)]
        for tt in range(NT):
            s = tt % 2
            kb.dma("sp", ch[s], xt[s][:], x_dram[tt * 128:(tt + 1) * 128, :], writes=[(tag, "xt", s)])
            ACT(kb, sq[:], xt[s][:], AF.Square, [(tag, "xt", s)], [(tag, "sq"), (tag, "ss", s)],
                accum_out=sm[s][:, 0:1])
            ACT(kb, sm[s][:, 1:2], sm[s][:, 0:1], AF.Sqrt, [(tag, "ss", s)], [(tag, "sd", s)],
                bias=C["eps"][:, 0:1], scale=1.0 / D)
            RECIP(kb, sm[s][:, 2:3], sm[s][:, 1:2], [(tag, "sd", s)], [(tag, "rs", s)])
            TS(kb, "dve", xn[s][:], xt[s][:], sm[s][:, 2:3], None, ALU.mult, None,
               [(tag, "xt", s), (tag, "rs", s)], [(tag, "xn", s)])
            for half in range(2):
                p = half
                for kk in range(8):
                    k = half * 8 + kk
                    TR(kb, pst[p][:, kk, :], xn[s][:, k * 128:(k + 1) * 128], C["ident"][:],
                       [(tag, "xn", s)], [(tag, "pt", p)], signal=(kk == 7))
                nw = normwT[:, half * 8:(half + 1) * 8].unsqueeze(2).to_broadcast([128, 8, 128])
                TT(kb, "dve", hT[:, half * 8:(half + 1) * 8, tt * 128:(tt + 1) * 128], pst[p][:], nw,
                   ALU.mult, [(tag, "pt", p)], [("hT", tt)])
        kb.end_phase()


def gemm_fm(kb, tag, actT, act_keys, KC, w_dram, NCH, sink):
    with ExitStack() as st:
        wb = [kb.sb(f"{tag}_wb{i}", [128, KC * 128], BF16, st) for i in range(2)]
        wch = [kb.chan(f"{tag}_w{i}") for i in range(2)]
        pss = [kb.ps(f"{tag}_ps{i}", [128, 1024], F32, st) for i in range(2)]
        it = 0
        for c in range(NCH):
            s = c % 2
            kb.dma("pool", wch[s], wb[s][:], w_dram[c], writes=[(tag, "wb", s)])
            for ts in range(4):
                p = it % 2
                it += 1
                for b in range(2):
                    t0 = ts * 1024 + b * 512
                    for k in range(KC):
                        MM(kb, pss[p][:, b * 512:(b + 1) * 512], wb[s][:, k * 128:(k + 1) * 128],
                           actT[:, k, t0:t0 + 512], k == 0, k == KC - 1,
                           [(tag, "wb", s)] + act_keys, [(tag, "ps", p)])
                sink(c, ts, pss[p], (tag, "ps", p), st)
        kb.end_phase()


class StoreSink:
    def __init__(self, kb, tag, dst, st):
        self.kb = kb
        self.tag = tag
        self.dst = dst
        self.stg = [kb.sb(f"{tag}_stg{i}", [128, 1024], F32, st) for i in range(3)]
        self.ch = [kb.chan(f"{tag}_st{i}") for i in range(3)]
        self.i = 0

    def __call__(self, c, ts, ps, pkey, st):
        kb = self.kb
        s = self.i % 3
        eng = "act" if self.i % 2 == 0 else "dve"
        self.i += 1
        CP(kb, eng, self.stg[s][:], ps[:], [pkey], [(self.tag, "stg", s)])
        kb.dma("sp", self.ch[s], self.dst[c * 128:(c + 1) * 128, ts * 1024:(ts + 1) * 1024], self.stg[s][:],
               reads=[(self.tag, "stg", s)], writes=[(self.tag, "dst", c)])


def phase_inproj(kb, C, tag, x_dram, normwT, w_dram, NCH, pj):
    with ExitStack() as st:
        hT = kb.sb(f"{tag}_hT", [128, 16, T], BF16, st)
        phase_norm_T(kb, C, x_dram, normwT, hT, tag + "n")
        with ExitStack() as st2:
            sink = StoreSink(kb, tag + "s", pj, st2)
            gemm_fm(kb, tag + "g", hT, [], 16, w_dram, NCH, sink)


def phase_outproj(kb, C, tag, yT_dram, wo_dram, xres_dram, out_dram):
    with ExitStack() as st:
        wo = kb.sb(f"{tag}_wo", [128, 16, D], BF16, st)
        wch = kb.chan(f"{tag}_w")
        for k in range(16):
            kb.dma("pool", wch, wo[:, k, :], wo_dram[k], writes=[(tag, "wo")])
        yt = [kb.sb(f"{tag}_yt{i}", [128, 16, 128], BF16, st) for i in range(2)]
        xr = [kb.sb(f"{tag}_xr{i}", [128, D], F32, st) for i in range(2)]
        ot = [kb.sb(f"{tag}_ot{i}", [128, D], F32, st) for i in range(2)]
        ps = [kb.ps(f"{tag}_ps{i}", [128, 512], F32, st) for i in range(4)]
        chy = [kb.chan(f"{tag}_y{i}") for i in range(2)]
        chx = [kb.chan(f"{tag}_x{i}") for i in range(2)]
        cho = [kb.chan(f"{tag}_o{i}") for i in range(2)]
        yv = yT_dram.rearrange("(k p) t -> p k t", p=128)
        for tt in range(NT):
            s = tt % 2
            kb.dma("sp", chy[s], yt[s][:], yv[:, :, tt * 128:(tt + 1) * 128], writes=[(tag, "yt", s)])
            kb.dma("sp", chx[s], xr[s][:], xres_dram[tt * 128:(tt + 1) * 128, :], writes=[(tag, "xr", s)])
            for nb in range(4):
                for k in range(16):
                    MM(kb, ps[nb][:], yt[s][:, k, :], wo[:, k, nb * 512:(nb + 1) * 512], k == 0, k == 15,
                       [(tag, "yt", s), (tag, "wo")], [(tag, "ps", nb)])
                TT(kb, "dve", ot[s][:, nb * 512:(nb + 1) * 512], ps[nb][:], xr[s][:, nb * 512:(nb + 1) * 512],
                   ALU.add, [(tag, "ps", nb), (tag, "xr", s)], [(tag, "ot", s)])
            kb.dma("pool", cho[s], out_dram[tt * 128:(tt + 1) * 128, :], ot[s][:],
                   reads=[(tag, "ot", s)], writes=[(tag, "out", tt)])
        kb.end_phase()


def phase_cd_mix(kb, C, pj, cw_unused, y_dram):
    with ExitStack() as stc:
        cw = {}
        chc = kb.chan("cd_const")
        for name, shape, dt in CD_CONST_SPECS:
            d = kb.nc.dram_tensor(name, list(shape), dt, kind="ExternalInput").ap()
            t = kb.sb("c_" + name, shape, dt, stc)
            kb.dma("sp", chc, t[:], d, writes=[("const", name)])
            cw[name] = t
        kb.end_phase()
        _phase_cd_mix(kb, C, pj, cw, y_dram)


def _phase_cd_mix(kb, C, pj, cw, y_dram):
    HB = 2048
    tag = "cd"
    with ExitStack() as st:
        QB = 1024
        ident_b = kb.sb("cd_identb", [128, 128], BF16, st)
        dg = kb.sb("cd_dg", [128, 8, 31, 128], BF16, st)
        uc = kb.sb("cd_uc", [128, 8, QB], F32, st)
        ab = [kb.sb(f"cd_a{i}", [128, 30 + QB], F32, st) for i in range(2)]
        gb = [kb.sb(f"cd_g{i}", [128, 30 + QB], F32, st) for i in range(2)]
        ub = [kb.sb(f"cd_u{i}", [128, 30 + QB], BF16, st) for i in range(2)]
        sq = [kb.sb(f"cd_sq{i}", [128, QB], F32, st) for i in range(2)]
        mean = kb.sb("cd_mean", [128, QB], F32, st)
        rstd = kb.sb("cd_rstd", [128, QB], F32, st)
        m2 = kb.sb("cd_m2", [128, QB], F32, st)
        zb = [kb.sb(f"cd_z{i}", [128, QB], F32, st) for i in range(2)]
        yb = [kb.sb(f"cd_y{i}", [128, QB], BF16, st) for i in range(2)]
        pc = [kb.ps(f"cd_pc{i}", [128, QB], F32, st) for i in range(2)]
        ps_sum = kb.ps("cd_pss", [128, QB], F32, st)
        ps_ssq = kb.ps("cd_psq", [128, QB], F32, st)
        cha = [kb.chan(f"cd_a{i}") for i in range(2)]
        chg = [kb.chan(f"cd_g{i}") for i in range(2)]
        chz = [kb.chan(f"cd_z{i}") for i in range(2)]
        chy = [kb.chan(f"cd_y{i}") for i in range(2)]
        CP(kb, "dve", ident_b[:], C["ident"][:], [], [(tag, "cst")])
        for cc in range(8):
            TT(kb, "dve", dg[:, cc, :, :], ident_b[:].unsqueeze(1).to_broadcast([128, 31, 128]),
               cw["dww"][:, cc, :].unsqueeze(2).to_broadcast([128, 31, 128]), ALU.mult, [(tag, "cst")], [(tag, "dg")])
        it = 0
        for tq in range(T // QB):
            t0 = tq * QB
            for cc in range(8):
                s = it % 2
                it += 1
                r0 = cc * 128
                if tq == 0:
                    MS(kb, "pool", ab[s][:, 0:30], 0.0, [(tag, "a", s)])
                    MS(kb, "pool", gb[s][:, 0:30], 0.0, [(tag, "g", s)])
                    kb.dma("sp", cha[s], ab[s][:, 30:30 + QB], pj[r0:r0 + 128, 0:QB], writes=[(tag, "a", s)])
                    kb.dma("sp", chg[s], gb[s][:, 30:30 + QB], pj[1024 + r0:1024 + r0 + 128, 0:QB],
                           writes=[(tag, "g", s)])
                else:
                    kb.dma("sp", cha[s], ab[s][:], pj[r0:r0 + 128, t0 - 30:t0 + QB], writes=[(tag, "a", s)])
                    kb.dma("sp", chg[s], gb[s][:], pj[1024 + r0:1024 + r0 + 128, t0 - 30:t0 + QB],
                           writes=[(tag, "g", s)])
                ACT(kb, gb[s][:], gb[s][:], AF.Sigmoid, [(tag, "g", s)], [(tag, "g", s)])
                TT(kb, "dve", ub[s][:], ab[s][:], gb[s][:], ALU.mult, [(tag, "a", s), (tag, "g", s)], [(tag, "u", s)])
                for b in range(QB // 512):
                    for j in range(31):
                        MM(kb, pc[s][:, b * 512:(b + 1) * 512], dg[:, cc, j, :], ub[s][:, j + b * 512:j + b * 512 + 512],
                           j == 0, j == 30, [(tag, "dg"), (tag, "u", s)], [(tag, "pc", s)])
                ACT(kb, uc[:, cc, :], pc[s][:], AF.Identity, [(tag, "pc", s)], [(tag, "uc", cc)],
                    bias=cw["dwb"][:, cc:cc + 1])
                ACT(kb, sq[s][:], uc[:, cc, :], AF.Square, [(tag, "uc", cc)], [(tag, "sq", s)])
                for tb in range(QB // 512):
                    last = tb == QB // 512 - 1
                    MM(kb, ps_sum[:, tb * 512:(tb + 1) * 512], C["ones_f"][:], uc[:, cc, tb * 512:(tb + 1) * 512],
                       cc == 0, cc == 7, [(tag, "uc", cc)], [(tag, "pss")], signal=last)
                    MM(kb, ps_ssq[:, tb * 512:(tb + 1) * 512], C["ones_f"][:], sq[s][:, tb * 512:(tb + 1) * 512],
                       cc == 0, cc == 7, [(tag, "sq", s)], [(tag, "psq")], signal=last)
            ACT(kb, mean[:], ps_sum[:], AF.Copy, [(tag, "pss")], [(tag, "mean")], scale=1.0 / 1024)
            TT(kb, "dve", m2[:], mean[:], mean[:], ALU.mult, [(tag, "mean")], [(tag, "m2")])
            STT(kb, m2[:], ps_ssq[:], 1.0 / 1024, m2[:], ALU.mult, ALU.subtract, [(tag, "psq"), (tag, "m2")],
                [(tag, "m2")])
            ACT(kb, m2[:], m2[:], AF.Sqrt, [(tag, "m2")], [(tag, "m2")], bias=C["eps"][:, 0:1])
            RECIP(kb, rstd[:], m2[:], [(tag, "m2")], [(tag, "rstd")])
            for cc in range(8):
                s = cc % 2
                r0 = cc * 128
                kb.dma("sp", chz[s], zb[s][:], pj[2048 + r0:2048 + r0 + 128, t0:t0 + QB], writes=[(tag, "z", s)])
                TT(kb, "dve", uc[:, cc, :], uc[:, cc, :], mean[:], ALU.subtract, [(tag, "uc", cc), (tag, "mean")],
                   [(tag, "uc", cc)])
                TT(kb, "dve", uc[:, cc, :], uc[:, cc, :], rstd[:], ALU.mult, [(tag, "uc", cc), (tag, "rstd")],
                   [(tag, "uc", cc)])
                ACT(kb, uc[:, cc, :], uc[:, cc, :], AF.Silu, [(tag, "uc", cc)], [(tag, "uc", cc)],
                    scale=cw["lnw"][:, cc:cc + 1], bias=cw["lnb"][:, cc:cc + 1])
                ACT(kb, zb[s][:], zb[s][:], AF.Silu, [(tag, "z", s)], [(tag, "z", s)])
                TT(kb, "dve", yb[s][:], uc[:, cc, :], zb[s][:], ALU.mult, [(tag, "uc", cc), (tag, "z", s)],
                   [(tag, "y", s)])
                kb.dma("pool", chy[s], y_dram[r0:r0 + 128, t0:t0 + QB], yb[s][:], reads=[(tag, "y", s)],
                       writes=[(tag, "yd", cc, tq)])
        kb.end_phase()
    tag = "sc"
    with ExitStack() as st:
        W = T + 2
        bg = [kb.sb(f"sc_b{i}", [128, T], F32, st) for i in range(2)]
        cg = [kb.sb(f"sc_c{i}", [128, W], F32, st) for i in range(2)]
        ud = [kb.sb(f"sc_u{i}", [128, W], F32, st) for i in range(2)]
        zd = [kb.sb(f"sc_z{i}", [128, T], F32, st) for i in range(2)]
        acc = [kb.sb(f"sc_acc{i}", [128, T], F32, st) for i in range(2)]
        yb = [kb.sb(f"sc_y{i}", [128, T], BF16, st) for i in range(2)]
        chs = {n: [kb.chan(f"sc_{n}{i}") for i in range(2)] for n in ("b", "c", "u", "z", "y")}
        for cc in range(8):
            s = cc % 2
            r0 = cc * 128
            MS(kb, "pool", cg[s][:, 0:2], 0.0, [(tag, "c", s)])
            MS(kb, "pool", ud[s][:, 0:2], 0.0, [(tag, "u", s)])
            kb.dma("sp", chs["b"][s], bg[s][:], pj[3072 + r0:3072 + r0 + 128, :], writes=[(tag, "b", s)])
            kb.dma("sp", chs["c"][s], cg[s][:, 2:W], pj[4096 + r0:4096 + r0 + 128, :], writes=[(tag, "c", s)])
            kb.dma("sp", chs["u"][s], ud[s][:, 2:W], pj[5120 + r0:5120 + r0 + 128, :], writes=[(tag, "u", s)])
            kb.dma("sp", chs["z"][s], zd[s][:], pj[6144 + r0:6144 + r0 + 128, :], writes=[(tag, "z", s)])
            TT(kb, "dve", cg[s][:], cg[s][:], ud[s][:], ALU.mult, [(tag, "c", s), (tag, "u", s)], [(tag, "c", s)])
            TS(kb, "dve", acc[s][:], cg[s][:, 2:W], cw["dcw"][:, cc, 2:3], None, ALU.mult, None,
               [(tag, "c", s)], [(tag, "acc", s)])
            for j in range(2):
                STT(kb, acc[s][:], cg[s][:, j:j + T], cw["dcw"][:, cc, j:j + 1], acc[s][:], ALU.mult, ALU.add,
                    [(tag, "c", s), (tag, "acc", s)], [(tag, "acc", s)])
            ACT(kb, zd[s][:], zd[s][:], AF.Silu, [(tag, "z", s)], [(tag, "z", s)])
            TT(kb, "pool", bg[s][:], bg[s][:], zd[s][:], ALU.mult, [(tag, "b", s), (tag, "z", s)], [(tag, "b", s)])
            TT(kb, "dve", yb[s][:], acc[s][:], bg[s][:], ALU.mult, [(tag, "acc", s), (tag, "b", s)], [(tag, "y", s)])
            kb.dma("pool", chs["y"][s], y_dram[1024 + r0:1024 + r0 + 128, :], yb[s][:], reads=[(tag, "y", s)],
                   writes=[(tag, "yd", cc)])
        kb.end_phase()


def colnorm_phase(kb, C, tag, src, KC, gT, outT):
    with ExitStack() as st:
        raw = [kb.sb(f"{tag}_raw{i}", [128, KC, 1024], F32, st) for i in range(2)]
        sq = [kb.sb(f"{tag}_sq{i}", [128, 1024], F32, st) for i in range(2)]
        rs = kb.sb(f"{tag}_rs", [128, 1024], F32, st)
        ssp = kb.ps(f"{tag}_ssp", [128, 1024], F32, st)
        ch = [kb.chan(f"{tag}_l{i}") for i in range(2)]
        sv = src.rearrange("(k p) t -> p k t", p=128)
        for ts in range(4):
            s = ts % 2
            kb.dma("sp", ch[s], raw[s][:], sv[:, :, ts * 1024:(ts + 1) * 1024], writes=[(tag, "raw", s)])
            for k in range(KC):
                q = k % 2
                ACT(kb, sq[q][:], raw[s][:, k, :], AF.Square, [(tag, "raw", s)], [(tag, "sq", q)])
                for b in range(2):
                    MM(kb, ssp[:, b * 512:(b + 1) * 512], C["ones_f"][:], sq[q][:, b * 512:(b + 1) * 512],
                       k == 0, k == KC - 1, [(tag, "sq", q)], [(tag, "ssp")], signal=True)
            ACT(kb, rs[:], ssp[:], AF.Sqrt, [(tag, "ssp")], [(tag, "rs")], bias=C["eps"][:, 0:1],
                scale=1.0 / (KC * 128))
            RECIP(kb, rs[:], rs[:], [(tag, "rs")], [(tag, "rs")])
            for k in range(KC):
                STT(kb, outT[:, k, ts * 1024:(ts + 1) * 1024], raw[s][:, k, :], gT[:, k:k + 1], rs[:],
                    ALU.mult, ALU.mult, [(tag, "raw", s), (tag, "rs")], [(tag, "out", k, ts)])
        kb.end_phase()


class HeadNormSink:
    def __init__(self, kb, C, tag, dst, gain, n_norm, raw_dst, st):
        self.kb, self.C, self.tag, self.dst, self.gain = kb, C, tag, dst, gain
        self.n_norm, self.raw_dst = n_norm, raw_dst
        self.sq = [kb.sb(f"{tag}_sq{i}", [128, 1024], F32, st) for i in range(2)]
        self.rs = [kb.sb(f"{tag}_rs{i}", [128, 1024], F32, st) for i in range(2)]
        self.ob = [kb.sb(f"{tag}_ob{i}", [128, 1024], BF16, st) for i in range(2)]
        self.ssp = kb.ps(f"{tag}_ssp", [128, 1024], F32, st)
        self.ch = [kb.chan(f"{tag}_o{i}") for i in range(2)]
        self.i = 0

    def __call__(self, c, ts, ps, pkey, st):
        kb, C, tag = self.kb, self.C, self.tag
        s = self.i % 2
        self.i += 1
        if c < self.n_norm:
            ACT(kb, self.sq[s][:], ps[:], AF.Square, [pkey], [(tag, "sq", s)])
            for b in range(2):
                MM(kb, self.ssp[:, b * 512:(b + 1) * 512], C["ones_f"][:], self.sq[s][:, b * 512:(b + 1) * 512],
                   True, True, [(tag, "sq", s)], [(tag, "ssp")])
            ACT(kb, self.rs[s][:], self.ssp[:], AF.Sqrt, [(tag, "ssp")], [(tag, "rs", s)], bias=C["eps"][:, 0:1],
                scale=1.0 / 128)
            RECIP(kb, self.rs[s][:], self.rs[s][:], [(tag, "rs", s)], [(tag, "rs", s)])
            STT(kb, self.ob[s][:], ps[:], self.gain, self.rs[s][:], ALU.mult, ALU.mult,
                [pkey, (tag, "rs", s)], [(tag, "ob", s)])
            d = self.dst[c]
        else:
            CP(kb, "act", self.ob[s][:], ps[:], [pkey], [(tag, "ob", s)])
            d = self.raw_dst[c - self.n_norm]
        kb.dma("sp", self.ch[s], d[:, ts * 1024:(ts + 1) * 1024], self.ob[s][:], reads=[(tag, "ob", s)],
               writes=[(tag, "dst", c)])


def phase_dsa_prep(kb, C, pj, W, S):
    with ExitStack() as st:
        cqn = kb.sb("cqn", [128, 4, T], BF16, st)
        colnorm_phase(kb, C, "cq", pj[0:512, :], 4, C["qnormT"], cqn)
        with ExitStack() as st2:
            sink = HeadNormSink(kb, C, "qs", S["qT"], C["qgain"][:, 0:1], 8, S["qiT"], st2)
            gemm_fm(kb, "qg", cqn, [], 4, W["w_uqiq"], 16, sink)
    with ExitStack() as st:
        ckvn = kb.sb("ckvn", [128, 2, T], BF16, st)
        colnorm_phase(kb, C, "ckv", pj[512:768, :], 2, C["kvnormT"], ckvn)
        with ExitStack() as st2:
            sink = HeadNormSink(kb, C, "ks", S["kT"], C["kgain"][:, 0:1], 8, None, st2)
            gemm_fm(kb, "kg", ckvn, [], 2, W["w_uk"], 8, sink)
        with ExitStack() as st2:
            wv = kb.sb("wv", [128, 2, 1024], BF16, st2)
            chw = kb.chan("wv")
            for k in range(2):
                kb.dma("pool", chw, wv[:, k, :], W["w_uv"][k], writes=[("wv",)])
            vps = [kb.ps(f"v_ps{i}", [128, 1024], F32, st2) for i in range(2)]
            vb = [kb.sb(f"v_b{i}", [128, 1024], BF16, st2) for i in range(2)]
            chv = [kb.chan(f"v_o{i}") for i in range(2)]
            for tt in range(NT):
                s = tt % 2
                for b in range(2):
                    for k in range(2):
                        MM(kb, vps[s][:, b * 512:(b + 1) * 512], ckvn[:, k, tt * 128:(tt + 1) * 128],
                           wv[:, k, b * 512:(b + 1) * 512], k == 0, k == 1, [("wv",)], [("v", "ps", s)])
                CP(kb, "act" if tt % 2 else "dve", vb[s][:], vps[s][:], [("v", "ps", s)], [("v", "b", s)])
                kb.dma("sp", chv[s], S["V"][tt * 128:(tt + 1) * 128, :], vb[s][:], reads=[("v", "b", s)],
                       writes=[("V", tt)])
            kb.end_phase()


def phase_ki_tmaj(kb, C, pj, S):
    with ExitStack() as st:
        ki = kb.sb("ki_raw", [128, T], F32, st)
        sq = kb.sb("ki_sq", [128, T], F32, st)
        mean = kb.sb("ki_mean", [128, 1024], F32, st)
        var = kb.sb("ki_var", [128, 1024], F32, st)
        kio = kb.sb("ki_o", [128, T], BF16, st)
        sm = kb.sb("ki_sm", [32, T], F32, st)
        tmo = kb.sb("ki_tmo", [128, NT, 32], F32, st)
        ps1 = kb.ps("ki_ps1", [128, 1024], F32, st)
        ps2 = kb.ps("ki_ps2", [128, 1024], F32, st)
        pst = kb.ps("ki_pst", [128, 16, 32], F32, st)
        ch = kb.chan("ki")
        kb.dma("sp", ch, ki[0:64, :], pj[768:832, :], writes=[("ki", "raw")])
        kb.dma("sp", ch, ki[64:128, :], pj[768:832, :], writes=[("ki", "raw")])
        kb.dma("sp", ch, sm[:], pj[896:928, :], writes=[("ki", "sm")])
        ACT(kb, sq[:], ki[:], AF.Square, [("ki", "raw")], [("ki", "sq")])
        for ts in range(4):
            for b in range(2):
                c0 = ts * 1024 + b * 512
                MM(kb, ps1[:, b * 512:(b + 1) * 512], C["ones_f"][0:64, :], ki[0:64, c0:c0 + 512], True, True,
                   [("ki", "raw")], [("ki", "ps1")])
                MM(kb, ps2[:, b * 512:(b + 1) * 512], C["ones_f"][0:64, :], sq[0:64, c0:c0 + 512], True, True,
                   [("ki", "sq")], [("ki", "ps2")])
            ACT(kb, mean[:], ps1[:], AF.Copy, [("ki", "ps1")], [("ki", "mean")], scale=1.0 / 64)
            TT(kb, "dve", var[:], mean[:], mean[:], ALU.mult, [("ki", "mean")], [("ki", "var")])
            STT(kb, var[:], ps2[:], 1.0 / 64, var[:], ALU.mult, ALU.subtract, [("ki", "ps2"), ("ki", "var")],
                [("ki", "var")])
            ACT(kb, var[:], var[:], AF.Sqrt, [("ki", "var")], [("ki", "var")], bias=C["eps"][:, 0:1])
            RECIP(kb, var[:], var[:], [("ki", "var")], [("ki", "var")])
            sl = slice(ts * 1024, (ts + 1) * 1024)
            TT(kb, "dve", ki[:, sl], ki[:, sl], mean[:], ALU.subtract, [("ki", "raw"), ("ki", "mean")], [("ki", "raw")])
            TT(kb, "dve", ki[:, sl], ki[:, sl], var[:], ALU.mult, [("ki", "raw"), ("ki", "var")], [("ki", "raw")])
            TS(kb, "dve", kio[:, sl], ki[:, sl], C["ikw"][:, 0:1], C["ikb"][:, 0:1], ALU.mult, ALU.add,
               [("ki", "raw")], [("ki", "o")])
        kb.dma("sp", ch, S["kiT"], kio[:], reads=[("ki", "o")], writes=[("kiT",)])
        for g in range(2):
            for tt in range(16):
                t = g * 16 + tt
                TR(kb, pst[:, tt, :], sm[:, t * 128:(t + 1) * 128], C["ident"][0:32, 0:32], [("ki", "sm")],
                   [("ki", "pst")], signal=(tt == 15))
            CP(kb, "dve", tmo[:, g * 16:(g + 1) * 16, :], pst[:], [("ki", "pst")], [("ki", "tmo")])
        kb.dma("sp", ch, S["tmaj"].rearrange("(n p) c -> p n c", p=128), tmo[:], reads=[("ki", "tmo")],
               writes=[("tmaj",)])
        kb.end_phase()


N_BIS = 13
SCALE_A = 128 ** -0.5


def phase_dsa_attn(kb, C, pj, S, y_dram):
    tag = "at"
    with ExitStack() as st:
        kT = kb.sb("at_kT", [128, 8, T], BF16, st)
        Vt = kb.sb("at_V", [128, NT, 1024], BF16, st)
        kiT = kb.sb("at_kiT", [128, T], BF16, st)
        score1 = kb.sb("at_sc", [128, T], F32, st)
        score = [score1, score1]
        mask = kb.sb("at_mask", [128, T], BF16, st)
        junk = mask
        maskT1 = kb.sb("at_maskT", [128, NT, 128], BF16, st)
        maskT = [maskT1, maskT1]
        rbuf = [kb.sb(f"at_r{i}", [128, 2, 512], BF16, st) for i in range(2)]
        dsg = kb.sb("at_dsg", [128, 16, 128], BF16, st)
        qb = [kb.sb(f"at_q{i}", [128, 8, 128], BF16, st) for i in range(2)]
        qib1 = kb.sb("at_qi", [128, 8, 128], BF16, st)
        qib = [qib1, qib1]
        wt = [kb.sb(f"at_w{i}", [128, 16], F32, st) for i in range(2)]
        bs = [kb.sb(f"at_bs{i}", [128, 8], F32, st) for i in range(2)]
        wks1 = kb.sb("at_wk", [128, N_BIS], F32, st)
        wks = [wks1, wks1]
        fpow = kb.sb("at_fpow", [128, N_BIS], F32, st)
        za1 = kb.sb("at_za", [128, 8, 128], F32, st)
        za = [za1, za1]
        pt = [kb.sb(f"at_pt{i}", [128, 4, 128], BF16, st) for i in range(2)]
        ost1 = kb.sb("at_ost", [128, 8, 256], F32, st)
        ost = [ost1, ost1]
        yst1 = kb.sb("at_yst", [128, 8, 128], BF16, st)
        yst = [yst1, yst1]
        biasS = C["biasS"]
        ident_b = kb.sb("at_identb", [128, 128], BF16, st)
        ones_b = kb.sb("at_onesb", [128, 128], BF16, st)
        ips = [kb.ps(f"at_ips{i}", [128, 2, 512], F32, st) for i in range(2)]
        lg = [ips[i][:, 0, :].rearrange("p (a b) -> p a b", b=128) for i in range(2)]
        po1 = kb.ps("at_po", [128, 128], F32, st)
        prs1 = kb.ps("at_prs", [128, 128], F32, st)
        po, prs = [po1, po1], [prs1, prs1]
        sps1 = kb.ps("at_sps", [128, 512], F32, st)
        sps = [sps1, sps1]
        tps = ips[0][:, 0, :].bitcast(BF16)[:, 0:512].rearrange("p (a b) -> p a b", b=128)
        chl = kb.chan("at_ld")
        chq = [kb.chan(f"at_q{i}") for i in range(2)]
        chy = [kb.chan(f"at_y{i}") for i in range(2)]
        chz = kb.chan("at_z")
        kb.dma("sp", chl, kT[:], S["kT"].rearrange("h p t -> p h t"), writes=[(tag, "kT")])
        kb.dma("sp", chl, Vt[:], S["V"].rearrange("(n p) c -> p n c", p=128), writes=[(tag, "V")])
        kb.dma("sp", chl, kiT[:], S["kiT"], writes=[(tag, "kiT")])
        CP(kb, "dve", ident_b[:], C["ident"][:], [], [(tag, "cst")])
        CP(kb, "dve", ones_b[:], C["ones_f"][:], [], [(tag, "cst")])
        for k in range(1, N_BIS + 1):
            MS(kb, "dve", fpow[:, k - 1:k], 2.0 ** -k, [(tag, "cst")])
        qTv = S["qT"].rearrange("h p t -> p h t")
        qiTv = S["qiT"].rearrange("h p t -> p h t")
        zav = pj[1024:2048, :].rearrange("(h p) t -> p h t", p=128)
        yv = y_dram[0:1024, :].rearrange("(h p) t -> p h t", p=128)
        cnt = {"ips": 0, "r": 0, "lg": 0, "pt": 0, "po": 0, "sps": 0}

        def stage_a(i):
            s = i % 2
            c0, c1 = i * 128, (i + 1) * 128
            kb.dma("sp", chq[s], qb[s][:], qTv[:, :, c0:c1], writes=[(tag, "q", s)])
            kb.dma("sp", chq[s], qib[s][:], qiTv[:, :, c0:c1], writes=[(tag, "qi")])
            kb.dma("sp", chq[s], wt[s][:], S["tmaj"][c0:c1, 0:16], writes=[(tag, "w", s)])
            nW = (i + 4) // 4
            Wi = nW * 512
            sk = (tag, "score")
            TT(kb, "dve", dsg[:], ident_b[:].unsqueeze(1).to_broadcast([128, 16, 128]),
               wt[s][:, 0:16].unsqueeze(2).to_broadcast([128, 16, 128]), ALU.mult, [(tag, "w", s), (tag, "cst")],
               [(tag, "dsg")])
            for w in range(nW):
                sp_ = cnt["sps"] % 2
                cnt["sps"] += 1
                units = []

                def acc(u):
                    h0, r_ = u
                    for e_ in range(2):
                        MM(kb, sps[sp_][:], dsg[:, h0 + e_, :], rbuf[r_][:, e_, :], h0 + e_ == 0, h0 + e_ == 15,
                           [(tag, "dsg"), (tag, "r", r_)], [(tag, "sps", 0)], signal=(e_ == 1))

                for pair in range(8):
                    p = cnt["ips"] % 2
                    cnt["ips"] += 1
                    r = cnt["r"] % 2
                    cnt["r"] += 1
                    for e_ in range(2):
                        base = e_ * 64
                        MM(kb, ips[p][:, e_, :], qib[s][base:base + 64, pair, :],
                           kiT[base:base + 64, w * 512:(w + 1) * 512], True, True, [(tag, "qi"), (tag, "kiT")],
                           [(tag, "ips", p)], signal=(e_ == 1))
                    if pair % 4 != 3:
                        ACT(kb, rbuf[r][:], ips[p][:], AF.Relu, [(tag, "ips", p)], [(tag, "r", r)])
                    else:
                        TS(kb, "dve", rbuf[r][:], ips[p][:], 0.0, None, ALU.max, None, [(tag, "ips", p)], [(tag, "r", r)])
                    if units:
                        acc(units.pop())
                    units.append((2 * pair, r))
                acc(units.pop())
                CP(kb, "dve", score[s][:, w * 512:(w + 1) * 512], sps[sp_][:], [(tag, "sps", 0)], [sk])
            b = bs[s]
            bk = (tag, "bs", s)
            TS(kb, "dve", junk[:, 0:Wi], score[s][:, 0:Wi], 1.0, None, ALU.mult, ALU.max, [sk], [(tag, "mask"), bk],
               accum_out=b[:, 0:1])
            TS(kb, "dve", junk[:, 0:Wi], score[s][:, 0:Wi], -1.0, None, ALU.mult, ALU.max, [sk], [(tag, "mask"), bk],
               accum_out=b[:, 6:7])
            TT(kb, "dve", b[:, 0:1], b[:, 0:1], b[:, 6:7], ALU.max, [bk], [bk])
            TT(kb, "dve", score[s][:, Wi - 512:Wi], score[s][:, Wi - 512:Wi],
               C["cbase"][:, 384 - (i % 4) * 128:896 - (i % 4) * 128], ALU.add, [sk], [sk])
            TS(kb, "dve", b[:, 1:2], b[:, 0:1], -1.001, -1e-20, ALU.mult, ALU.add, [bk], [bk])
            TS(kb, "dve", b[:, 2:3], b[:, 0:1], 2.002, 2e-20, ALU.mult, ALU.add, [bk], [bk])
            wk = wks[s]
            TS(kb, "dve", wk[:], fpow[:], b[:, 2:3], None, ALU.mult, None, [bk, (tag, "cst")], [bk])
            TT(kb, "dve", b[:, 3:4], b[:, 1:2], wk[:, 0:1], ALU.add, [bk], [bk])
            for k in range(1, N_BIS + 1):
                TS(kb, "dve", junk[:, 0:Wi], score[s][:, 0:Wi], b[:, 3:4], None, ALU.is_ge, ALU.add,
                   [sk, bk], [(tag, "mask"), bk], accum_out=b[:, 4:5])
                TS(kb, "dve", b[:, 5:6], b[:, 4:5], 255.5, -0.5, ALU.is_ge, ALU.add, [bk], [bk])
                STT(kb, b[:, 3:4], b[:, 5:6], wk[:, k - 1:k], b[:, 3:4], ALU.mult, ALU.add, [bk], [bk])
            STT(kb, b[:, 1:2], wk[:, N_BIS - 1:N_BIS], -0.5, b[:, 3:4], ALU.mult, ALU.add, [bk], [bk])

        def stage_t(i):
            s = i % 2
            n = i + 1
            TS(kb, "dve", mask[:, 0:n * 128], score[s][:, 0:n * 128], bs[s][:, 1:2], None, ALU.is_ge, None,
               [(tag, "score"), (tag, "bs", s)], [(tag, "mask")])
            for j0 in range(0, n, 4):
                nb = min(4, n - j0)
                for jj in range(nb):
                    j = j0 + jj
                    TR(kb, tps[:, jj, :], mask[:, j * 128:(j + 1) * 128], ident_b[:], [(tag, "mask"), (tag, "cst")],
                       [(tag, "ips", 0)], signal=(jj == nb - 1))
                ACT(kb, maskT[s][:, j0:j0 + nb, :], tps[:, 0:nb, :], AF.Copy, [(tag, "ips", 0)], [(tag, "maskT")],
                    scale=30000.0, bias=-30000.0)

        def stage_b(i):
            s = i % 2
            n = i + 1
            groups = [(h, j0, min(4, n - j0)) for h in range(8) for j0 in range(0, n, 4)]
            slots = {}

            def qk(gi):
                h, j0, nb = groups[gi]
                p = cnt["lg"] % 2
                cnt["lg"] += 1
                x = cnt["pt"] % 2
                cnt["pt"] += 1
                slots[gi] = x
                for jj in range(nb):
                    j = j0 + jj
                    near = (i - j) <= 1
                    MM(kb, lg[p][:, jj, :], kT[:, h, j * 128:(j + 1) * 128], qb[s][:, h, :], True, False,
                       [(tag, "kT"), (tag, "q", s)], [(tag, "ips", p)], signal=False)
                    if near:
                        MM(kb, lg[p][:, jj, :], ident_b[:], biasS[:, h, i - j, :], False, False,
                           [(tag, "cst")], [(tag, "ips", p)], signal=False)
                    MM(kb, lg[p][:, jj, :], ident_b[:], maskT[s][:, j, :], False, True,
                       [(tag, "cst"), (tag, "maskT")], [(tag, "ips", p)], signal=(jj == nb - 1))
                ACT(kb, pt[x][:, 0:nb, :], lg[p][:, 0:nb, :], AF.Exp, [(tag, "ips", p)], [(tag, "pt", x)],
                    scale=SCALE_A, bias=C["cb"][:, h:h + 1])

            def pv(gi):
                h, j0, nb = groups[gi]
                x = slots.pop(gi)
                for jj in range(nb):
                    j = j0 + jj
                    MM(kb, po[0][:], Vt[:, j, h * 128:(h + 1) * 128], pt[x][:, jj, :], j == 0, j == n - 1,
                       [(tag, "V"), (tag, "pt", x)], [(tag, "po", 0)])
                    MM(kb, prs[0][:], ones_b[:], pt[x][:, jj, :], j == 0, j == n - 1,
                       [(tag, "cst"), (tag, "pt", x)], [(tag, "prs", 0)], signal=(jj == nb - 1))
                if j0 + nb == n:
                    CP(kb, "act", ost[s][:, h, 0:128], po[0][:], [(tag, "po", 0)], [(tag, "ost", h)])
                    CP(kb, "act", ost[s][:, h, 128:256], prs[0][:], [(tag, "prs", 0)], [(tag, "ost", h)])

            qk(0)
            for gi in range(len(groups)):
                if gi + 1 < len(groups):
                    qk(gi + 1)
                pv(gi)

        def stage_f(i):
            s = i % 2
            keys = [(tag, "ost", h) for h in range(8)]
            kb.dma("sp", chz, za[s][:], zav[:, :, i * 128:(i + 1) * 128], writes=[(tag, "za")])
            ACT(kb, za[s][:], za[s][:], AF.Silu, [(tag, "za")], [(tag, "za")])
            RECIP(kb, ost[s][:, :, 128:256], ost[s][:, :, 128:256], keys, keys)
            TT(kb, "dve", ost[s][:, :, 0:128], ost[s][:, :, 0:128], ost[s][:, :, 128:256], ALU.mult, keys, keys)
            TT(kb, "dve", yst[s][:], ost[s][:, :, 0:128], za[s][:], ALU.mult, keys + [(tag, "za")], [(tag, "yst")])
            kb.dma("pool", chy[s], yv[:, :, i * 128:(i + 1) * 128], yst[s][:], reads=[(tag, "yst")],
                   writes=[(tag, "y", i)])

        stage_a(0)
        stage_t(0)
        for i in range(NT):
            if i + 1 < NT:
                stage_a(i + 1)
            stage_b(i)
            if i + 1 < NT:
                stage_t(i + 1)
            stage_f(i)
        kb.end_phase()


def phase_gdn_prep(kb, C, pj, S):
    tag = "gp"
    with ExitStack() as st:
        W = T + 3
        HB = T // 2
        ident_b = kb.sb("gp_identb", [128, 128], BF16, st)
        dg = kb.sb("gp_dg", [128, 24, 4, 128], BF16, st)
        rawb = [kb.sb(f"gp_rawb{i}", [128, W], BF16, st) for i in range(2)]
        acc = [kb.sb(f"gp_acc{i}", [128, T], F32, st) for i in range(2)]
        sq = [kb.sb(f"gp_sq{i}", [128, T], F32, st) for i in range(2)]
        rn = [kb.sb(f"gp_rn{i}", [128, T], F32, st) for i in range(2)]
        pcv = kb.ps("gp_pcv", [128, HB], F32, st)
        ssp = kb.ps("gp_ssp", [128, HB], F32, st)
        chl = [kb.chan(f"gp_l{i}") for i in range(2)]
        chs = [kb.chan(f"gp_s{i}") for i in range(2)]
        CP(kb, "dve", ident_b[:], C["ident"][:], [], [(tag, "cst")])
        for cc in range(24):
            TT(kb, "dve", dg[:, cc, :, :], ident_b[:].unsqueeze(1).to_broadcast([128, 4, 128]),
               C["cvw"][:, cc, :].unsqueeze(2).to_broadcast([128, 4, 128]), ALU.mult, [(tag, "cst")], [(tag, "dg")])

        def head(cc):
            s = cc % 2
            r0 = 2048 + cc * 128
            MS(kb, "dve", rawb[s][:, 0:3], 0.0, [(tag, "raw", s)])
            kb.dma("pool", chl[s], rawb[s][:, 3:W], pj[r0:r0 + 128, :], writes=[(tag, "raw", s)])
            for hb in range(2):
                for b in range(4):
                    c0 = hb * HB + b * 512
                    for j in range(4):
                        MM(kb, pcv[:, b * 512:(b + 1) * 512], dg[:, cc, j, :], rawb[s][:, c0 + j:c0 + j + 512],
                           j == 0, j == 3, [(tag, "dg"), (tag, "raw", s)], [(tag, "pcv")], signal=(j == 3 and b == 3))
                ACT(kb, acc[s][:, hb * HB:(hb + 1) * HB], pcv[:], AF.Silu, [(tag, "pcv")], [(tag, "acc", s, hb)])
                if cc < 16:
                    ACT(kb, sq[s][:, hb * HB:(hb + 1) * HB], acc[s][:, hb * HB:(hb + 1) * HB], AF.Square,
                        [(tag, "acc", s, hb)], [(tag, "sq", s, hb)])
                    for b in range(4):
                        c0 = hb * HB + b * 512
                        MM(kb, ssp[:, b * 512:(b + 1) * 512], C["ones_f"][:], sq[s][:, c0:c0 + 512], True, True,
                           [(tag, "sq", s, hb)], [(tag, "ssp")], signal=(b == 3))
                    ACT(kb, rn[s][:, hb * HB:(hb + 1) * HB], ssp[:], AF.Sqrt, [(tag, "ssp")],
                        [(tag, "rn", s, hb)], bias=C["eps"][:, 0:1])

        def tail(cc):
            s = cc % 2
            akeys = [(tag, "acc", s, 0), (tag, "acc", s, 1)]
            if cc < 16:
                keys = [(tag, "rn", s, 0), (tag, "rn", s, 1)]
                RECIP(kb, rn[s][:], rn[s][:], keys, keys)
                STT(kb, acc[s][:], acc[s][:], (128 ** -0.5) if cc < 8 else 1.0, rn[s][:], ALU.mult, ALU.mult,
                    akeys + keys, akeys)
            kb.dma("sp", chs[s], S["gqkv"][cc * 128:(cc + 1) * 128, :], acc[s][:], reads=akeys,
                   writes=[("gqkv", cc)])

        head(0)
        for cc in range(24):
            if cc + 1 < 24:
                head(cc + 1)
            tail(cc)
        kb.end_phase()


import os
GDN_TILES = int(os.environ.get("GDN_TILES", "32"))
GDN_STOP = int(os.environ.get("GDN_STOP", "99"))
GDN_SUB = float(os.environ.get("GDN_SUB", "99"))
GDN_EVAC = os.environ.get("GDN_EVAC", "act")


def phase_gdn(kb, C, pj, S, y_dram):
    with ExitStack() as stc:
        C = dict(C)
        chc = kb.chan("gd_const")
        for name, shape, dt in GDN_CONST_SPECS:
            d = kb.nc.dram_tensor(name, list(shape), dt, kind="ExternalInput").ap()
            t = kb.sb("c_" + name, shape, dt, stc)
            kb.dma("sp", chc, t[:], d, writes=[("const", name)])
            C[name] = t
        kb.end_phase()
        phase_gdn_prep(kb, C, pj, S)
        _phase_gdn_main(kb, C, pj, S, y_dram)


def _phase_gdn_main(kb, C, pj, S, y_dram):
    tag = "gd"
    H = 8
    with ExitStack() as st:
        def fb(name, shape=(128, H, 128), dt=F32):
            return kb.sb("gd_" + name, list(shape), dt, st)

        tm = fb("tm", (128, NT, 32))
        beta = fb("beta", (128, NT, 8))
        g = fb("g", (128, NT, 8))
        t1 = fb("t1", (128, NT, 8))
        t2 = fb("t2", (128, NT, 8))
        nA = fb("nA", (128, 8))
        qT, kT, vT = fb("qT"), fb("kT"), fb("vT")
        gd, egrow, gcr = fb("gdiag"), fb("egrow"), fb("gcr")
        P1, E1, E2 = fb("P1"), fb("E1"), fb("E2")
        HS = (128, H, 128)
        X = [fb("X0", HS, BF16), fb("X1", HS, BF16)]
        Y = [fb("Y0", HS, BF16), fb("Y1", HS, BF16)]
        P, attnT = fb("P", HS, BF16), fb("attnT", HS, BF16)
        vb, kbg, kd, kd1 = fb("vb", HS, BF16), fb("kbg", HS, BF16), fb("kd", HS, BF16), fb("kd1", HS, BF16)
        smk = fb("smk", (128, 16))
        u, wT, qgT, vnew = fb("u"), fb("wT", HS, BF16), fb("qgT", HS, BF16), fb("vnew", HS, BF16)
        Sst, oacc, zb = fb("S"), fb("oacc"), fb("zb")
        Sb = fb("Sb", HS, BF16)
        ident_b = fb("identb", (128, 128), BF16)
        osq, orn = fb("osq"), fb("orn")
        yo = fb("yo", (128, H, 128), BF16)
        sm = fb("sm", (128, 64))
        pA = kb.ps("gd_pA", [128, H, 128], F32, st)
        pB = kb.ps("gd_pB", [128, H, 128], F32, st)
        pC = kb.ps("gd_pC", [128, H, 128], F32, st)
        pO = kb.ps("gd_pO", [128, H, 64], F32, st)
        psm = kb.ps("gd_psm", [128, 32], F32, st)
        ch = kb.chan("gd_l")
        chq = kb.chan("gd_q")
        chz = kb.chan("gd_z")
        chy = kb.chan("gd_y")
        K_ = lambda n: (tag, n)

        def bc_h(ap2d):
            return ap2d.unsqueeze(1).to_broadcast([128, H, 128])

        def bc_f(ap2d):
            return ap2d.unsqueeze(2).to_broadcast([128, H, 128])

        kb.dma("sp", ch, tm[:], S["tmaj"].rearrange("(n p) c -> p n c", p=128), writes=[K_("tm")])
        ACT(kb, beta[:], tm[:, :, 16:24], AF.Sigmoid, [K_("tm")], [K_("beta")])
        dtb = C["dtb_bc"][:].unsqueeze(1).to_broadcast([128, NT, 8])
        TT(kb, "dve", g[:], tm[:, :, 24:32], dtb, ALU.add, [K_("tm")], [K_("g")])
        TS(kb, "dve", t1[:], g[:], -1.0, None, ALU.mult, None, [K_("g")], [K_("t1")])
        TT(kb, "dve", t1[:], t1[:], g[:], ALU.max, [K_("t1"), K_("g")], [K_("t1")])
        ACT(kb, t1[:], t1[:], AF.Exp, [K_("t1")], [K_("t1")], scale=-1.0)
        TS(kb, "dve", t1[:], t1[:], 1.0, None, ALU.add, None, [K_("t1")], [K_("t1")])
        ACT(kb, t1[:], t1[:], AF.Ln, [K_("t1")], [K_("t1")])
        TS(kb, "dve", t2[:], g[:], 0.0, None, ALU.max, None, [K_("g")], [K_("t2")])
        TT(kb, "dve", t2[:], t2[:], t1[:], ALU.add, [K_("t1"), K_("t2")], [K_("t2")])
        ACT(kb, nA[:], C["alog_bc"][:], AF.Exp, [], [K_("nA")])
        TS(kb, "dve", nA[:], nA[:], -1.0, None, ALU.mult, None, [K_("nA")], [K_("nA")])
        TT(kb, "dve", g[:], t2[:], nA[:].unsqueeze(1).to_broadcast([128, NT, 8]), ALU.mult, [K_("t2"), K_("nA")],
           [K_("g")])
        MS(kb, "dve", Sst[:], 0.0, [K_("S")])
        MS(kb, "dve", Sb[:], 0.0, [K_("Sb")])
        MS(kb, "dve", vnew[:], 0.0, [K_("vnew")])
        CP(kb, "dve", ident_b[:], C["ident"][:], [], [K_("identb")])
        pAb = pA[:, 0:4, :].bitcast(BF16).rearrange("p a (c b) -> p (a c) b", b=128)
        gq = S["gqkv"]
        qv = gq[0:1024, :].rearrange("(h p) t -> p h t", p=128)
        kv = gq[1024:2048, :].rearrange("(h p) t -> p h t", p=128)
        vv = gq[2048:3072, :].rearrange("(h p) t -> p h t", p=128)
        zv = pj[5120:6144, :].rearrange("(h p) t -> p h t", p=128)
        yv = y_dram[1024:2048, :].rearrange("(h p) t -> p h t", p=128)

        HH = 4

        def tile_shared(n):
            c0, c1 = n * 128, (n + 1) * 128
            kb.dma("sp", chq, qT[:], qv[:, :, c0:c1], writes=[K_("qT")])
            kb.dma("sp", chq, kT[:], kv[:, :, c0:c1], writes=[K_("kT")])
            kb.dma("sp", chq, vT[:], vv[:, :, c0:c1], writes=[K_("vT")])
            kb.dma("sp", chz, zb[:], zv[:, :, c0:c1], writes=[K_("zb")])
            gn = g[:, n, :]
            bn = beta[:, n, :]
            MM(kb, psm[:, 0:8], C["U2"][:], gn, True, True, [K_("g")], [K_("psm")], signal=False)
            MM(kb, psm[:, 8:16], C["Bsame"][:], gn, True, True, [K_("g")], [K_("psm")], signal=False)
            MM(kb, psm[:, 16:24], C["Bsel0"][:], gn, True, True, [K_("g")], [K_("psm")], signal=False)
            MM(kb, psm[:, 24:32], C["Bsel1"][:], gn, True, True, [K_("g")], [K_("psm")])
            CP(kb, "dve", sm[:, 0:32], psm[:], [K_("psm")], [K_("sm")])
            ACT(kb, sm[:, 32:40], sm[:, 0:8], AF.Exp, [K_("sm")], [K_("sm")])
            TT(kb, "dve", sm[:, 40:48], sm[:, 8:16], sm[:, 0:8], ALU.subtract, [K_("sm")], [K_("sm")])
            ACT(kb, sm[:, 40:48], sm[:, 40:48], AF.Exp, [K_("sm")], [K_("sm")])
            ACT(kb, sm[:, 16:32], sm[:, 16:32], AF.Exp, [K_("sm")], [K_("sm")])
            TT(kb, "dve", sm[:, 48:56], sm[:, 32:40], bn, ALU.mult, [K_("sm"), K_("beta")], [K_("sm")])
            TS(kb, "dve", sm[:, 56:64], bn, -1.0, None, ALU.mult, None, [K_("beta")], [K_("sm")])
            TS(kb, "dve", smk[:, 0:8], sm[:, 40:48], C["Bsel0"][:, 0:1], None, ALU.mult, None, [K_("sm")], [K_("smk")])
            TS(kb, "dve", smk[:, 8:16], sm[:, 40:48], C["Bsel1"][:, 0:1], None, ALU.mult, None, [K_("sm")], [K_("smk")])
            ACT(kb, zb[:], zb[:], AF.Silu, [K_("zb")], [K_("zb")])

        def tile_half(n, hh):
            c0, c1 = n * 128, (n + 1) * 128
            h0, h1 = hh * HH, (hh + 1) * HH
            hs = slice(h0, h1)
            heads = range(h0, h1)
            last = h1 - 1
            k_ = lambda nm: (tag, nm, hh)
            gn = g[:, n, hs]
            bn = beta[:, n, hs]

            def bh(ap2d):
                return ap2d.unsqueeze(1).to_broadcast([128, HH, 128])

            def bf(ap2d):
                return ap2d.unsqueeze(2).to_broadcast([128, HH, 128])

            pAb = pA[:, h0:h0 + 2, :].bitcast(BF16).rearrange("p a (c b) -> p (a c) b", b=128)
            gc = sm[:, h0:h1]
            TT(kb, "dve", gd[:, hs, :], bh(C["U2"][:]), bf(gn), ALU.mult, [K_("g")], [k_("gdiag")])
            MM(kb, pA[:, hs, :], C["ones_f"][:], gd[:, hs, :], True, True, [k_("gdiag")], [k_("pA")])
            yield
            CP(kb, "dve", gcr[:, hs, :], pA[:, hs, :], [k_("pA")], [k_("gcr")])
            ACT(kb, egrow[:, hs, :], gcr[:, hs, :], AF.Exp, [k_("gcr")], [k_("egrow")])
            TT(kb, "dve", P1[:, hs, :], gcr[:, hs, :], bf(gc), ALU.subtract, [k_("gcr"), K_("sm")], [k_("P1")])
            yield
            ACT(kb, E1[:, hs, :], P1[:, hs, :], AF.Relu, [k_("P1")], [k_("E1")])
            ACT(kb, E1[:, hs, :], E1[:, hs, :], AF.Exp, [k_("E1")], [k_("E1")], scale=-1.0)
            ACT(kb, E2[:, hs, :], P1[:, hs, :], AF.Relu, [k_("P1")], [k_("E2")], scale=-1.0)
            ACT(kb, E2[:, hs, :], E2[:, hs, :], AF.Exp, [k_("E2")], [k_("E2")], scale=-1.0)
            yield
            TT(kb, "dve", E1[:, hs, :], E1[:, hs, :], bh(C["MLs"][:]), ALU.mult, [k_("E1")], [k_("E1")])
            TT(kb, "dve", E1[:, hs, :], E1[:, hs, :], bf(sm[:, 56 + h0:56 + h1]), ALU.mult, [k_("E1"), K_("sm")],
               [k_("E1")])
            TT(kb, "dve", E2[:, hs, :], E2[:, hs, :], bh(C["MU"][:]), ALU.mult, [k_("E2")], [k_("E2")])
            yield
            for h in heads:
                MM(kb, pB[:, h, :], kT[:, h, :], kT[:, h, :], True, True, [K_("kT")], [k_("pB")], signal=(h == last))
            for h in heads:
                MM(kb, pC[:, h, :], kT[:, h, :], qT[:, h, :], True, True, [K_("kT"), K_("qT")], [k_("pC")],
                   signal=(h == last))
            yield
            TT(kb, "dve", X[0][:, hs, :], pB[:, hs, :], E1[:, hs, :], ALU.mult, [k_("pB"), k_("E1")], [k_("X0")])
            TT(kb, "dve", attnT[:, hs, :], pC[:, hs, :], E2[:, hs, :], ALU.mult, [k_("pC"), k_("E2")], [k_("attnT")])
            yield
            for h in heads:
                TR(kb, pAb[:, h - h0, :], X[0][:, h, :], ident_b[:], [k_("X0"), K_("identb")], [k_("pA")],
                   signal=(h == last))
            yield
            CP(kb, "dve", Y[0][:, hs, :], pAb, [k_("pA")], [k_("Y0")])
            TT(kb, "dve", P[:, hs, :], Y[0][:, hs, :], bh(C["ident"][:]), ALU.add, [k_("Y0")], [k_("P")])
            yield
            cur = 0
            for lvl in range(5):
                nxt = 1 - cur
                xk, yk = k_(f"X{cur}"), k_(f"Y{cur}")
                xn, yn = k_(f"X{nxt}"), k_(f"Y{nxt}")
                for h in heads:
                    MM(kb, pB[:, h, :], Y[cur][:, h, :], X[cur][:, h, :], True, True, [xk, yk], [k_("pB")],
                       signal=(h == last))
                if lvl < 4:
                    for h in heads:
                        MM(kb, pC[:, h, :], X[cur][:, h, :], Y[cur][:, h, :], True, True, [xk, yk], [k_("pC")],
                           signal=(h == last))
                yield
                CP(kb, GDN_EVAC, X[nxt][:, hs, :], pB[:, hs, :], [k_("pB")], [xn])
                if lvl < 4:
                    CP(kb, GDN_EVAC, Y[nxt][:, hs, :], pC[:, hs, :], [k_("pC")], [yn])
                yield
                for h in heads:
                    MM(kb, pA[:, h, :], X[nxt][:, h, :], P[:, h, :], True, True, [xn, k_("P")], [k_("pA")],
                       signal=(h == last))
                yield
                TT(kb, "dve", P[:, hs, :], P[:, hs, :], pA[:, hs, :], ALU.add, [k_("pA"), k_("P")], [k_("P")])
                yield
                cur = nxt
            for h in heads:
                TR(kb, pB[:, h, :], kT[:, h, :], C["ident"][:], [K_("kT")], [k_("pB")], signal=(h == last))
            for h in heads:
                TR(kb, pC[:, h, :], vT[:, h, :], C["ident"][:], [K_("vT")], [k_("pC")], signal=(h == last))
            yield
            TT(kb, "dve", kbg[:, hs, :], pB[:, hs, :], bf(sm[:, 48 + h0:48 + h1]), ALU.mult, [k_("pB"), K_("sm")],
               [k_("kbg")])
            TT(kb, "dve", kd[:, hs, :], pB[:, hs, :], bf(smk[:, h0:h1]), ALU.mult, [k_("pB"), K_("smk")], [k_("kd")])
            TT(kb, "dve", kd1[:, hs, :], pB[:, hs, :], bf(smk[:, 8 + h0:8 + h1]), ALU.mult, [k_("pB"), K_("smk")],
               [k_("kd")])
            TT(kb, "dve", vb[:, hs, :], pC[:, hs, :], bf(bn), ALU.mult, [k_("pC"), K_("beta")], [k_("vb")])
            yield
            for h in heads:
                MM(kb, pA[:, h, :], P[:, h, :], vb[:, h, :], True, True, [k_("P"), k_("vb")], [k_("pA")],
                   signal=(h == last))
            for h in heads:
                MM(kb, pB[:, h, :], kbg[:, h, :], P[:, h, :], True, True, [k_("P"), k_("kbg")], [k_("pB")],
                   signal=(h == last))
            yield
            CP(kb, "dve", u[:, hs, :], pA[:, hs, :], [k_("pA")], [k_("u")])
            CP(kb, GDN_EVAC, wT[:, hs, :], pB[:, hs, :], [k_("pB")], [k_("wT")])
            TT(kb, "dve", qgT[:, hs, :], qT[:, hs, :], egrow[:, hs, :], ALU.mult, [K_("qT"), k_("egrow")], [k_("qgT")])
            yield
            for c in range(2):
                r0, r1 = c * 64, (c + 1) * 64
                for h in heads:
                    MM(kb, pA[r0:r1, h, :], wT[:, h, r0:r1], Sb[:, h, :], True, True, [k_("wT"), k_("Sb")], [k_("pA")],
                       signal=(h == last))
                yield
                TT(kb, "dve", vnew[r0:r1, hs, :], u[r0:r1, hs, :], pA[r0:r1, hs, :], ALU.subtract,
                   [k_("u"), k_("pA")], [k_("vnew")])
                yield
                for h in heads:
                    MM(kb, pO[:, h, :], Sb[:, h, :], qgT[:, h, r0:r1], True, False, [k_("Sb"), k_("qgT")], [k_("pO")])
                    MM(kb, pO[:, h, :], vnew[r0:r1, h, :], attnT[r0:r1, h, r0:r1], False, True,
                       [k_("vnew"), k_("attnT")], [k_("pO")], signal=(h == last))
                for h in heads:
                    MM(kb, pB[:, h, :], (kd, kd1)[c][:, h, :], vnew[:, h, :], True, True, [k_("kd"), k_("vnew")],
                       [k_("pB")], signal=(h == last))
                yield
                CP(kb, GDN_EVAC, oacc[:, hs, r0:r1], pO[:, hs, :], [k_("pO")], [k_("oacc")])
                TT(kb, "dve", Sst[:, hs, :], Sst[:, hs, :], bf(sm[:, 16 + 8 * c + h0:16 + 8 * c + h1]), ALU.mult,
                   [k_("S"), K_("sm")], [k_("S")])
                TT(kb, "dve", Sst[:, hs, :], Sst[:, hs, :], pB[:, hs, :], ALU.add, [k_("S"), k_("pB")], [k_("S")])
                CP(kb, "act", Sb[:, hs, :], Sst[:, hs, :], [k_("S")], [k_("Sb")])
                yield
            ACT(kb, osq[:, hs, :], oacc[:, hs, :], AF.Square, [k_("oacc")], [k_("osq")])
            MM(kb, pC[:, hs, :], C["ones_f"][:], osq[:, hs, :], True, True, [k_("osq")], [k_("pC")])
            yield
            CP(kb, "dve", orn[:, hs, :], pC[:, hs, :], [k_("pC")], [k_("orn")])
            ACT(kb, orn[:, hs, :], orn[:, hs, :], AF.Sqrt, [k_("orn")], [k_("orn")], bias=C["eps"][:, 0:1],
                scale=1.0 / 128)
            yield
            RECIP(kb, orn[:, hs, :], orn[:, hs, :], [k_("orn")], [k_("orn")])
            STT(kb, oacc[:, hs, :], oacc[:, hs, :], C["onorm"][:, 0:1], orn[:, hs, :], ALU.mult, ALU.mult,
                [k_("oacc"), k_("orn")], [k_("oacc")])
            TT(kb, "dve", yo[:, hs, :], oacc[:, hs, :], zb[:, hs, :], ALU.mult, [k_("oacc"), K_("zb")], [k_("yo")])
            kb.dma("pool", chy, yv[:, hs, c0:c1], yo[:, hs, :], reads=[k_("yo")], writes=[("y0b", n, hh)])

        for n in range(GDN_TILES):
            tile_shared(n)
            gens = [tile_half(n, 0), tile_half(n, 1)]
            while gens:
                for gnr in list(gens):
                    try:
                        next(gnr)
                    except StopIteration:
                        gens.remove(gnr)
        kb.end_phase()


def load_consts(kb, nc, names_shapes):
    C = {}
    ch = kb.chan("const")
    for name, shape, dt in names_shapes:
        d = nc.dram_tensor(name, list(shape), dt, kind="ExternalInput").ap()
        if name == "cbase":
            t = kb.sb("c_" + name, shape, BF16)
            kb.dma("pool", ch, t[:], d, writes=[("const", name)])
        else:
            t = kb.sb("c_" + name, shape, dt)
            kb.dma("sp", ch, t[:], d, writes=[("const", name)])
        C[name] = t
    kb.end_phase()
    with ExitStack() as st:
        d = nc.dram_tensor("biasT", [128, 8, 2, 128], F32, kind="ExternalInput").ap()
        C["biasS"] = kb.sb("c_biasS", [128, 8, 2, 128], BF16)
        bt = kb.sb("biasT_tmp", [128, 8, 2, 128], F32, st)
        kb.dma("sp", ch, bt[:], d, writes=[("const", "biasT")])
        for h in range(8):
            TS(kb, "dve", C["biasS"][:, h, :, :], bt[:, h, :, :], C["cb"][:, h:h + 1], 128 ** 0.5,
               ALU.subtract, ALU.mult, [("const", "biasT")], [("const", "biasS")])
        kb.end_phase()
    return C


CONST_SPECS = [
    ("ident", (128, 128), F32),
    ("ones_f", (128, 128), F32),
    ("eps", (128, 1), F32),
    ("normwT", (128, 2, 16), F32),
    ("cbase", (128, 896), F32),
    ("cb", (128, 8), F32),
    ("qnormT", (128, 4), F32),
    ("kvnormT", (128, 2), F32),
    ("qgain", (128, 1), F32),
    ("kgain", (128, 1), F32),
    ("ikw", (128, 1), F32),
    ("ikb", (128, 1), F32),
]

CD_CONST_SPECS = [
    ("dww", (128, 8, 31), F32),
    ("dwb", (128, 8), F32),
    ("lnw", (128, 8), F32),
    ("lnb", (128, 8), F32),
    ("dcw", (128, 8, 3), F32),
]

GDN_CONST_SPECS = [
    ("cvw", (128, 24, 4), F32),
    ("alog_bc", (128, 8), F32),
    ("dtb_bc", (128, 8), F32),
    ("onorm", (128, 1), F32),
    ("U2", (128, 128), F32),
    ("Bsame", (128, 128), F32),
    ("Bsel0", (128, 128), F32),
    ("Bsel1", (128, 128), F32),
    ("MLs", (128, 128), F32),
    ("MU", (128, 128), F32),
]


def build_program(layers=(0, 1), l0_parts=("a", "b"), debug_out=False):
    nc = bass.Bass("TRN2", target_bir_lowering=False)

    def din(name, shape, dt=F32):
        return nc.dram_tensor(name, list(shape), dt, kind="ExternalInput").ap()

    def dscr(name, shape, dt=F32):
        return nc.dram_tensor(name, list(shape), dt, kind="Internal").ap()

    x = din("x", [T, D])
    out = nc.dram_tensor("out", [T, D], F32, kind="ExternalOutput").ap()
    kb = KB(nc)
    C = load_consts(kb, nc, CONST_SPECS)
    src = x
    if 0 in layers:
        ab_w_in = din("ab_w_in", [48, 128, 16 * 128])
        ab_w_out = din("ab_w_out", [16, 128, D])
        W = {"w_uqiq": din("w_uqiq", [16, 128, 4 * 128]), "w_uk": din("w_uk", [8, 128, 2 * 128]),
             "w_uv": din("w_uv", [2, 128, 1024])}
        pj0 = dscr("pj0", [6144, T])
        S = {"qT": dscr("s_qT", [8, 128, T], BF16), "qiT": dscr("s_qiT", [8, 128, T], BF16),
             "kT": dscr("s_kT", [8, 128, T], BF16), "V": dscr("s_V", [T, 1024], BF16),
             "kiT": dscr("s_kiT", [128, T], BF16), "tmaj": dscr("s_tmaj", [T, 32]),
             "gqkv": dscr("s_gqkv", [3072, T])}
        if debug_out:
            y0 = nc.dram_tensor("y0", [2048, T], BF16, kind="ExternalOutput").ap()
        else:
            y0 = dscr("y0", [2048, T], BF16)
        x1 = dscr("x1", [T, D]) if 1 in layers else out
        phase_inproj(kb, C, "l0", src, C["normwT"][:, 0, :], ab_w_in, 48, pj0)
        phase_ki_tmaj(kb, C, pj0, S)
        if "a" in l0_parts:
            phase_dsa_prep(kb, C, pj0, W, S)
            phase_dsa_attn(kb, C, pj0, S, y0)
        if "b" in l0_parts:
            phase_gdn(kb, C, pj0, S, y0)
        if not debug_out:
            phase_outproj(kb, C, "l0o", y0, ab_w_out, src, x1)
        src = x1
    if 1 in layers:
        cd_w_in = din("cd_w_in", [56, 128, 16 * 128])
        cd_w_out = din("cd_w_out", [16, 128, D])
        pj1 = dscr("pj1", [7168, T])
        y1 = dscr("y1", [2048, T], BF16)
        phase_inproj(kb, C, "l1", src, C["normwT"][:, 1, :], cd_w_in, 56, pj1)
        phase_cd_mix(kb, C, pj1, C, y1)
        phase_outproj(kb, C, "l1o", y1, cd_w_out, src, out)
    kb.finish()
    kb.emit()
    kb.close()
    return nc, kb


def tile_w_in(w, nch):
    K, N = w.shape
    assert N == nch * 128
    return np.ascontiguousarray(w.reshape(K // 128, 128, nch, 128).transpose(2, 1, 0, 3)).reshape(nch, 128, -1)


def t5_bucket_np(dist):
    import math
    max_exact = 16
    dd = np.maximum(dist, 1).astype(np.float32)
    large = max_exact + (np.log(dd / max_exact) / math.log(128 / max_exact) * (32 - max_exact)).astype(np.int32)
    large = np.minimum(large, 31)
    return np.where(dist < max_exact, dist, large)


def colT(v, k):
    return np.ascontiguousarray(np.asarray(v, np.float32).reshape(k, 128).T)


def host_consts(inp):
    f = np.float32
    c = {}
    c["ident"] = np.eye(128, dtype=f)
    c["ones_f"] = np.ones((128, 128), f)
    c["eps"] = np.full((128, 1), EPS, f)
    c["normwT"] = np.ascontiguousarray(inp["norm_w"].reshape(2, 16, 128).transpose(2, 0, 1)).astype(f)
    c["dww"] = np.ascontiguousarray(inp["c_dw_w"][0].reshape(31, 8, 128).transpose(2, 1, 0)).astype(f)
    c["dwb"] = colT(inp["c_dw_b"][0], 8)
    c["lnw"] = colT(inp["c_ln_w"][0], 8)
    c["lnb"] = colT(inp["c_ln_b"][0], 8)
    c["dcw"] = np.ascontiguousarray(inp["d_conv_w"][0].reshape(3, 8, 128).transpose(2, 1, 0)).astype(f)
    r = np.arange(128)[:, None]
    cc = np.arange(896)[None, :]
    c["cbase"] = np.where(cc <= r + 384, 0.0, -1e30).astype(f)
    kl = np.arange(128)[:, None, None]
    dd = np.arange(2)[None, :, None]
    ql = np.arange(128)[None, None, :]
    dist = np.maximum(dd * 128 + ql - kl, 0)
    bt = np.asarray(inp["rel_bias"], f)[t5_bucket_np(dist)]
    c["biasT"] = np.ascontiguousarray(bt.transpose(0, 3, 1, 2))
    c["cb"] = np.ascontiguousarray(np.broadcast_to(np.asarray(inp["rel_bias"], f)[31][None, :], (128, 8)))
    c["qnormT"] = colT(inp["a_q_norm"][0], 4)
    c["kvnormT"] = colT(inp["a_kv_norm"][0], 2)
    c["qgain"] = np.asarray(inp["a_q_gain"][0], f).reshape(128, 1).copy()
    c["kgain"] = np.asarray(inp["a_k_gain"][0], f).reshape(128, 1).copy()
    c["ikw"] = np.tile(np.asarray(inp["a_ik_norm_w"][0], f), 2).reshape(128, 1).copy()
    c["ikb"] = np.tile(np.asarray(inp["a_ik_norm_b"][0], f), 2).reshape(128, 1).copy()
    c["cvw"] = np.ascontiguousarray(np.asarray(inp["b_conv_w"][0], f).reshape(4, 24, 128).transpose(2, 1, 0))
    c["alog_bc"] = np.ascontiguousarray(np.broadcast_to(np.asarray(inp["b_a_log"][0], f)[None, :], (128, 8)))
    c["dtb_bc"] = np.ascontiguousarray(np.broadcast_to(np.asarray(inp["b_dt_bias"][0], f)[None, :], (128, 8)))
    c["onorm"] = np.asarray(inp["b_o_norm"][0], f).reshape(128, 1).copy()
    a = np.arange(128)
    same = (a[:, None] // 64) == (a[None, :] // 64)
    c["U2"] = (same & (a[:, None] <= a[None, :])).astype(f)
    c["Bsame"] = same.astype(f)
    c["Bsel0"] = np.ascontiguousarray(np.broadcast_to((a[:, None] < 64), (128, 128))).astype(f)
    c["Bsel1"] = np.ascontiguousarray(np.broadcast_to((a[:, None] >= 64), (128, 128))).astype(f)
    c["MLs"] = (same & (a[:, None] > a[None, :])).astype(f)
    c["MU"] = (same & (a[:, None] <= a[None, :])).astype(f)
    return c


def host_shared(inp, layers=(0, 1)):
    f = np.float32
    sh = host_consts(inp)
    if 0 in layers:
        w = np.asarray(inp["ab_w_in"][0], f)
        wp = np.zeros((D, 6144), f)
        wp[:, 0:832] = w[:, 0:832]
        wp[:, 896:912] = w[:, 832:848]
        wp[:, 912:928] = w[:, 4944:4960]
        wp[:, 1024:2048] = w[:, 848:1872]
        wp[:, 2048:5120] = w[:, 1872:4944]
        wp[:, 5120:6144] = w[:, 4960:5984]
        sh["ab_w_in"] = tile_w_in(wp, 48)
        sh["ab_w_out"] = np.ascontiguousarray(np.asarray(inp["ab_w_out"][0], f).reshape(16, 128, D))
        sh["w_uqiq"] = tile_w_in(np.concatenate([inp["a_w_uq"][0], inp["a_w_iq"][0]], axis=1).astype(f), 16)
        sh["w_uk"] = tile_w_in(np.asarray(inp["a_w_uk"][0], f), 8)
        sh["w_uv"] = np.ascontiguousarray(np.asarray(inp["a_w_uv"][0], f).reshape(2, 128, 1024))
    if 1 in layers:
        sh["cd_w_in"] = tile_w_in(np.asarray(inp["cd_w_in"][0], f), 56)
        sh["cd_w_out"] = np.ascontiguousarray(np.asarray(inp["cd_w_out"][0], f).reshape(16, 128, D))
    return sh


def kernel(**inputs):
    inp = {k: np.asarray(v) for k, v in inputs.items()}
    nc, kb = build_program()
    sh = host_shared(inp)
    x = np.ascontiguousarray(inp["x"], dtype=np.float32)
    in_maps = [dict(sh, x=x[b]) for b in range(8)]
    res = run_bass_kernel_spmd(nc, in_maps, core_ids=list(range(8)))
    return np.stack([np.asarray(r["out"], np.float32) for r in res.results], axis=0)
```

```python
from contextlib import ExitStack
import numpy as np
import concourse.bass as bass
import concourse.mybir as mybir
from concourse.bass_utils import run_bass_kernel_spmd

F32 = mybir.dt.float32
BF16 = mybir.dt.bfloat16
ALU = mybir.AluOpType
AF = mybir.ActivationFunctionType
AX = mybir.AxisListType

T = 4096
D = 2048
NT = T // 128
EPS = 1e-6
ENGS = ("pe", "act", "dve", "pool", "sp")


class Chan:
    def __init__(self, sem, name):
        self.sem = sem
        self.name = name
        self.n = 0


class KB:
    def __init__(self, nc):
        self.nc = nc
        self.es = ExitStack()
        self.q = {e: [] for e in ENGS}
        self.sems = {}
        self.cnt = {}
        self.seen = {e: {} for e in ENGS}
        self.lastw = {}
        self.readers = {}
        self.chans = []
        self.chan_by_sem = {}
        self.nins = 0
        self.pending = {e: False for e in ENGS}
        for e in ENGS:
            self.sems[e] = self.es.enter_context(nc.semaphore("s_" + e))
            self.cnt[e] = 0

    def sb(self, name, shape, dt, stack=None):
        return (stack or self.es).enter_context(self.nc.sbuf_tensor(name, list(shape), dt))

    def ps(self, name, shape, dt=F32, stack=None):
        return (stack or self.es).enter_context(self.nc.psum_tensor(name, list(shape), dt))

    def chan(self, name):
        c = Chan(self.es.enter_context(self.nc.semaphore("c_" + name)), name)
        self.chans.append(c)
        self.chan_by_sem[id(c.sem)] = c
        return c

    def _need0(self, eng, sem, val):
        ch = self.chan_by_sem.get(id(sem))
        if ch is not None:
            val = max(val, 16 * ch.n)
        cur = self.seen[eng].get(id(sem), 0)
        if val > cur:
            self.seen[eng][id(sem)] = val
            self.q[eng].append(("wait", sem, val))

    def _deps(self, eng, reads, writes, my_sem):
        for r in reads:
            ev = self.lastw.get(r)
            if ev is not None:
                self._need(eng, ev[0], ev[1])
        for w in writes:
            ev = self.lastw.get(w)
            if ev is not None:
                self._need(eng, ev[0], ev[1])
            rd = self.readers.get(w)
            if rd:
                for sem, val in rd.values():
                    if sem is my_sem:
                        continue
                    self._need(eng, sem, val)

    def _need(self, eng, sem, val):
        if eng == "pe" and sem is self.sems["pe"]:
            return
        self._need0(eng, sem, val)

    def _commit(self, ev, reads, writes):
        for w in writes:
            self.lastw[w] = ev
            self.readers[w] = {}
        for r in reads:
            d = self.readers.setdefault(r, {})
            d[id(ev[0])] = ev

    def op(self, eng, fn, reads=(), writes=(), signal=True):
        sem = self.sems[eng]
        self._deps(eng, reads, writes, sem)
        if signal:
            self.cnt[eng] += 1
            self.pending[eng] = False
            ev = (sem, self.cnt[eng])
            self.q[eng].append(("ins", fn, sem, 1))
        else:
            self.pending[eng] = True
            ev = (sem, self.cnt[eng] + 1)
            self.q[eng].append(("ins0", fn))
        self._commit(ev, reads, writes)
        self.nins += 1
        return ev

    def dma(self, eng, ch, out, in_, reads=(), writes=(), **kw):
        self._deps(eng, reads, writes, None)
        ch.n += 1
        ev = (ch.sem, 16 * ch.n)
        self.q[eng].append(("ins", lambda e, o=out, i=in_, k=kw: e.dma_start(out=o, in_=i, **k), ch.sem, 16))
        self._commit(ev, reads, writes)
        self.nins += 1
        return ev

    def _flush_pending(self):
        for e in ENGS:
            assert not self.pending[e], "non-signaling op left pending at a barrier on " + e

    def _all_events(self):
        self._flush_pending()
        evs = [(self.sems[e], self.cnt[e]) for e in ENGS if self.cnt[e] > 0]
        evs += [(c.sem, 16 * c.n) for c in self.chans if c.n > 0]
        return evs

    def barrier(self):
        evs = self._all_events()
        for e in ENGS:
            for sem, val in evs:
                self._need(e, sem, val)

    def finish(self, final_eng="sp"):
        for sem, val in self._all_events():
            self._need(final_eng, sem, val)

    def emit(self):
        nc = self.nc
        q = self.q
        self.q = {e: [] for e in ENGS}

        def replay(eng_obj, items):
            for it in items:
                if it[0] == "wait":
                    eng_obj.wait_ge(it[1], it[2])
                elif it[0] == "ins0":
                    it[1](eng_obj)
                else:
                    it[1](eng_obj).then_inc(it[2], it[3])

        with nc.Block() as block:
            @block.tensor
            def _(e):
                replay(e, q["pe"])

            @block.scalar
            def _(e):
                replay(e, q["act"])

            @block.vector
            def _(e):
                replay(e, q["dve"])

            @block.gpsimd
            def _(e):
                replay(e, q["pool"])

            @block.sync
            def _(e):
                replay(e, q["sp"])

    def end_phase(self):
        self.barrier()
        self.emit()

    def close(self):
        self.es.close()


def MM(kb, out, lhsT, rhs, start, stop, reads, writes, signal=None):
    if signal is None:
        signal = stop
    return kb.op("pe", lambda e: e.matmul(out, lhsT, rhs, start=start, stop=stop), reads, writes, signal=signal)


def TR(kb, out, in_, ident, reads, writes, signal=True):
    return kb.op("pe", lambda e: e.transpose(out, in_, ident), reads, writes, signal=signal)


def ACT(kb, out, in_, func, reads, writes, bias=None, scale=None, accum_out=None):
    kw = {}
    if bias is not None:
        kw["bias"] = bias
    if scale is not None:
        kw["scale"] = scale
    if accum_out is not None:
        kw["accum_out"] = accum_out
    return kb.op("act", lambda e: e.activation(out=out, in_=in_, func=func, **kw), reads, writes)


def TS(kb, eng, out, in0, s1, s2, op0, op1, reads, writes, accum_out=None):
    kw = {}
    if op1 is not None:
        kw["op1"] = op1
    if accum_out is not None:
        kw["accum_out"] = accum_out
    return kb.op(eng, lambda e: e.tensor_scalar(out=out, in0=in0, scalar1=s1, scalar2=s2, op0=op0, **kw),
                 reads, writes)


def TT(kb, eng, out, in0, in1, op, reads, writes):
    return kb.op(eng, lambda e: e.tensor_tensor(out=out, in0=in0, in1=in1, op=op), reads, writes)


def STT(kb, out, in0, scalar, in1, op0, op1, reads, writes):
    return kb.op("dve", lambda e: e.scalar_tensor_tensor(out=out, in0=in0, scalar=scalar, in1=in1,
                                                         op0=op0, op1=op1), reads, writes)


def CP(kb, eng, out, in_, reads, writes):
    if eng == "act":
        return kb.op("act", lambda e: e.copy(out=out, in_=in_), reads, writes)
    return kb.op(eng, lambda e: e.tensor_copy(out=out, in_=in_), reads, writes)


def MS(kb, eng, ap, val, writes):
    return kb.op(eng, lambda e: e.memset(ap, val), (), writes)


def RECIP(kb, out, in_, reads, writes):
    return kb.op("dve", lambda e: e.reciprocal(out=out, in_=in_), reads, writes)


def phase_norm_T(kb, C, x_dram, normwT, hT, tag):
    with ExitStack() as st:
        xt = [kb.sb(f"{tag}_xt{i}", [128, D], F32, st) for i in range(2)]
        xn = [kb.sb(f"{tag}_xn{i}", [128, D], F32, st) for i in range(2)]
        sq = kb.sb(f"{tag}_sq", [128, D], BF16, st)
        sm = [kb.sb(f"{tag}_sm{i}", [128, 4], F32, st) for i in range(2)]
        pst = [kb.ps(f"{tag}_pt{i}", [128, 8, 128], F32, st) for i in range(2)]
        ch = [kb.chan(f"{tag}_x{i}") for i in range(2)]
        for tt in range(NT):
            s = tt % 2
            kb.dma("sp", ch[s], xt[s][:], x_dram[tt * 128:(tt + 1) * 128, :], writes=[(tag, "xt", s)])
            ACT(kb, sq[:], xt[s][:], AF.Square, [(tag, "xt", s)], [(tag, "sq"), (tag, "ss", s)],
                accum_out=sm[s][:, 0:1])
            ACT(kb, sm[s][:, 1:2], sm[s][:, 0:1], AF.Sqrt, [(tag, "ss", s)], [(tag, "sd", s)],
                bias=C["eps"][:, 0:1], scale=1.0 / D)
            RECIP(kb, sm[s][:, 2:3], sm[s][:, 1:2], [(tag, "sd", s)], [(tag, "rs", s)])
            TS(kb, "dve", xn[s][:], xt[s][:], sm[s][:, 2:3], None, ALU.mult, None,
               [(tag, "xt", s), (tag, "rs", s)], [(tag, "xn", s)])
            for half in range(2):
                p = half
                for kk in range(8):
                    k = half * 8 + kk
                    TR(kb, pst[p][:, kk, :], xn[s][:, k * 128:(k + 1) * 128], C["ident"][:],
                       [(tag, "xn", s)], [(tag, "pt", p)], signal=(kk == 7))
                nw = normwT[:, half * 8:(half + 1) * 8].unsqueeze(2).to_broadcast([128, 8, 128])
                TT(kb, "dve", hT[:, half * 8:(half + 1) * 8, tt * 128:(tt + 1) * 128], pst[p][:], nw,
                   ALU.mult, [(tag, "pt", p)], [("hT", tt)])
        kb.end_phase()


def gemm_fm(kb, tag, actT, act_keys, KC, w_dram, NCH, sink):
    with ExitStack() as st:
        wb = [kb.sb(f"{tag}_wb{i}", [128, KC * 128], BF16, st) for i in range(2)]
        wch = [kb.chan(f"{tag}_w{i}") for i in range(2)]
        pss = [kb.ps(f"{tag}_ps{i}", [128, 1024], F32, st) for i in range(2)]
        it = 0
        for c in range(NCH):
            s = c % 2
            kb.dma("pool", wch[s], wb[s][:], w_dram[c], writes=[(tag, "wb", s)])
            for ts in range(4):
                p = it % 2
                it += 1
                for b in range(2):
                    t0 = ts * 1024 + b * 512
                    for k in range(KC):
                        MM(kb, pss[p][:, b * 512:(b + 1) * 512], wb[s][:, k * 128:(k + 1) * 128],
                           actT[:, k, t0:t0 + 512], k == 0, k == KC - 1,
                           [(tag, "wb", s)] + act_keys, [(tag, "ps", p)])
                sink(c, ts, pss[p], (tag, "ps", p), st)
        kb.end_phase()


class StoreSink:
    def __init__(self, kb, tag, dst, st):
        self.kb = kb
        self.tag = tag
        self.dst = dst
        self.stg = [kb.sb(f"{tag}_stg{i}", [128, 1024], F32, st) for i in range(3)]
        self.ch = [kb.chan(f"{tag}_st{i}") for i in range(3)]
        self.i = 0

    def __call__(self, c, ts, ps, pkey, st):
        kb = self.kb
        s = self.i % 3
        eng = "act" if self.i % 2 == 0 else "dve"
        self.i += 1
        CP(kb, eng, self.stg[s][:], ps[:], [pkey], [(self.tag, "stg", s)])
        kb.dma("sp", self.ch[s], self.dst[c * 128:(c + 1) * 128, ts * 1024:(ts + 1) * 1024], self.stg[s][:],
               reads=[(self.tag, "stg", s)], writes=[(self.tag, "dst", c)])


def phase_inproj(kb, C, tag, x_dram, normwT, w_dram, NCH, pj):
    with ExitStack() as st:
        hT = kb.sb(f"{tag}_hT", [128, 16, T], BF16, st)
        phase_norm_T(kb, C, x_dram, normwT, hT, tag + "n")
        with ExitStack() as st2:
            sink = StoreSink(kb, tag + "s", pj, st2)
            gemm_fm(kb, tag + "g", hT, [], 16, w_dram, NCH, sink)


def phase_outproj(kb, C, tag, yT_dram, wo_dram, xres_dram, out_dram):
    with ExitStack() as st:
        wo = kb.sb(f"{tag}_wo", [128, 16, D], BF16, st)
        wch = kb.chan(f"{tag}_w")
        for k in range(16):
            kb.dma("pool", wch, wo[:, k, :], wo_dram[k], writes=[(tag, "wo")])
        yt = [kb.sb(f"{tag}_yt{i}", [128, 16, 128], BF16, st) for i in range(2)]
        xr = [kb.sb(f"{tag}_xr{i}", [128, D], F32, st) for i in range(2)]
        ot = [kb.sb(f"{tag}_ot{i}", [128, D], F32, st) for i in range(2)]
        ps = [kb.ps(f"{tag}_ps{i}", [128, 512], F32, st) for i in range(4)]
        chy = [kb.chan(f"{tag}_y{i}") for i in range(2)]
        chx = [kb.chan(f"{tag}_x{i}") for i in range(2)]
        cho = [kb.chan(f"{tag}_o{i}") for i in range(2)]
        yv = yT_dram.rearrange("(k p) t -> p k t", p=128)
        for tt in range(NT):
            s = tt % 2
            kb.dma("sp", chy[s], yt[s][:], yv[:, :, tt * 128:(tt + 1) * 128], writes=[(tag, "yt", s)])
            kb.dma("sp", chx[s], xr[s][:], xres_dram[tt * 128:(tt + 1) * 128, :], writes=[(tag, "xr", s)])
            for nb in range(4):
                for k in range(16):
                    MM(kb, ps[nb][:], yt[s][:, k, :], wo[:, k, nb * 512:(nb + 1) * 512], k == 0, k == 15,
                       [(tag, "yt", s), (tag, "wo")], [(tag, "ps", nb)])
                TT(kb, "dve", ot[s][:, nb * 512:(nb + 1) * 512], ps[nb][:], xr[s][:, nb * 512:(nb + 1) * 512],
                   ALU.add, [(tag, "ps", nb), (tag, "xr", s)], [(tag, "ot", s)])
            kb.dma("pool", cho[s], out_dram[tt * 128:(tt + 1) * 128, :], ot[s][:],
                   reads=[(tag, "ot", s)], writes=[(tag, "out", tt)])
        kb.end_phase()


def phase_cd_mix(kb, C, pj, cw_unused, y_dram):
    with ExitStack() as stc:
        cw = {}
        chc = kb.chan("cd_const")
        for name, shape, dt in CD_CONST_SPECS:
            d = kb.nc.dram_tensor(name, list(shape), dt, kind="ExternalInput").ap()
            t = kb.sb("c_" + name, shape, dt, stc)
            kb.dma("sp", chc, t[:], d, writes=[("const", name)])
            cw[name] = t
        kb.end_phase()
        _phase_cd_mix(kb, C, pj, cw, y_dram)


def _phase_cd_mix(kb, C, pj, cw, y_dram):
    HB = 2048
    tag = "cd"
    with ExitStack() as st:
        QB = 1024
        ident_b = kb.sb("cd_identb", [128, 128], BF16, st)
        dg = kb.sb("cd_dg", [128, 8, 31, 128], BF16, st)
        uc = kb.sb("cd_uc", [128, 8, QB], F32, st)
        ab = [kb.sb(f"cd_a{i}", [128, 30 + QB], F32, st) for i in range(2)]
        gb = [kb.sb(f"cd_g{i}", [128, 30 + QB], F32, st) for i in range(2)]
        ub = [kb.sb(f"cd_u{i}", [128, 30 + QB], BF16, st) for i in range(2)]
        sq = [kb.sb(f"cd_sq{i}", [128, QB], F32, st) for i in range(2)]
        mean = kb.sb("cd_mean", [128, QB], F32, st)
        rstd = kb.sb("cd_rstd", [128, QB], F32, st)
        m2 = kb.sb("cd_m2", [128, QB], F32, st)
        zb = [kb.sb(f"cd_z{i}", [128, QB], F32, st) for i in range(2)]
        yb = [kb.sb(f"cd_y{i}", [128, QB], BF16, st) for i in range(2)]
        pc = [kb.ps(f"cd_pc{i}", [128, QB], F32, st) for i in range(2)]
        ps_sum = kb.ps("cd_pss", [128, QB], F32, st)
        ps_ssq = kb.ps("cd_psq", [128, QB], F32, st)
        cha = [kb.chan(f"cd_a{i}") for i in range(2)]
        chg = [kb.chan(f"cd_g{i}") for i in range(2)]
        chz = [kb.chan(f"cd_z{i}") for i in range(2)]
        chy = [kb.chan(f"cd_y{i}") for i in range(2)]
        CP(kb, "dve", ident_b[:], C["ident"][:], [], [(tag, "cst")])
        for cc in range(8):
            TT(kb, "dve", dg[:, cc, :, :], ident_b[:].unsqueeze(1).to_broadcast([128, 31, 128]),
               cw["dww"][:, cc, :].unsqueeze(2).to_broadcast([128, 31, 128]), ALU.mult, [(tag, "cst")], [(tag, "dg")])
        it = 0
        for tq in range(T // QB):
            t0 = tq * QB
            for cc in range(8):
                s = it % 2
                it += 1
                r0 = cc * 128
                if tq == 0:
                    MS(kb, "pool", ab[s][:, 0:30], 0.0, [(tag, "a", s)])
                    MS(kb, "pool", gb[s][:, 0:30], 0.0, [(tag, "g", s)])
                    kb.dma("sp", cha[s], ab[s][:, 30:30 + QB], pj[r0:r0 + 128, 0:QB], writes=[(tag, "a", s)])
                    kb.dma("sp", chg[s], gb[s][:, 30:30 + QB], pj[1024 + r0:1024 + r0 + 128, 0:QB],
                           writes=[(tag, "g", s)])
                else:
                    kb.dma("sp", cha[s], ab[s][:], pj[r0:r0 + 128, t0 - 30:t0 + QB], writes=[(tag, "a", s)])
                    kb.dma("sp", chg[s], gb[s][:], pj[1024 + r0:1024 + r0 + 128, t0 - 30:t0 + QB],
                           writes=[(tag, "g", s)])
                ACT(kb, gb[s][:], gb[s][:], AF.Sigmoid, [(tag, "g", s)], [(tag, "g", s)])
                TT(kb, "dve", ub[s][:], ab[s][:], gb[s][:], ALU.mult, [(tag, "a", s), (tag, "g", s)], [(tag, "u", s)])
                for b in range(QB // 512):
                    for j in range(31):
                        MM(kb, pc[s][:, b * 512:(b + 1) * 512], dg[:, cc, j, :], ub[s][:, j + b * 512:j + b * 512 + 512],
                           j == 0, j == 30, [(tag, "dg"), (tag, "u", s)], [(tag, "pc", s)])
                ACT(kb, uc[:, cc, :], pc[s][:], AF.Identity, [(tag, "pc", s)], [(tag, "uc", cc)],
                    bias=cw["dwb"][:, cc:cc + 1])
                ACT(kb, sq[s][:], uc[:, cc, :], AF.Square, [(tag, "uc", cc)], [(tag, "sq", s)])
                for tb in range(QB // 512):
                    last = tb == QB // 512 - 1
                    MM(kb, ps_sum[:, tb * 512:(tb + 1) * 512], C["ones_f"][:], uc[:, cc, tb * 512:(tb + 1) * 512],
                       cc == 0, cc == 7, [(tag, "uc", cc)], [(tag, "pss")], signal=last)
                    MM(kb, ps_ssq[:, tb * 512:(tb + 1) * 512], C["ones_f"][:], sq[s][:, tb * 512:(tb + 1) * 512],
                       cc == 0, cc == 7, [(tag, "sq", s)], [(tag, "psq")], signal=last)
            ACT(kb, mean[:], ps_sum[:], AF.Copy, [(tag, "pss")], [(tag, "mean")], scale=1.0 / 1024)
            TT(kb, "dve", m2[:], mean[:], mean[:], ALU.mult, [(tag, "mean")], [(tag, "m2")])
            STT(kb, m2[:], ps_ssq[:], 1.0 / 1024, m2[:], ALU.mult, ALU.subtract, [(tag, "psq"), (tag, "m2")],
                [(tag, "m2")])
            ACT(kb, m2[:], m2[:], AF.Sqrt, [(tag, "m2")], [(tag, "m2")], bias=C["eps"][:, 0:1])
            RECIP(kb, rstd[:], m2[:], [(tag, "m2")], [(tag, "rstd")])
            for cc in range(8):
                s = cc % 2
                r0 = cc * 128
                kb.dma("sp", chz[s], zb[s][:], pj[2048 + r0:2048 + r0 + 128, t0:t0 + QB], writes=[(tag, "z", s)])
                TT(kb, "dve", uc[:, cc, :], uc[:, cc, :], mean[:], ALU.subtract, [(tag, "uc", cc), (tag, "mean")],
                   [(tag, "uc", cc)])
                TT(kb, "dve", uc[:, cc, :], uc[:, cc, :], rstd[:], ALU.mult, [(tag, "uc", cc), (tag, "rstd")],
                   [(tag, "uc", cc)])
                ACT(kb, uc[:, cc, :], uc[:, cc, :], AF.Silu, [(tag, "uc", cc)], [(tag, "uc", cc)],
                    scale=cw["lnw"][:, cc:cc + 1], bias=cw["lnb"][:, cc:cc + 1])
                ACT(kb, zb[s][:], zb[s][:], AF.Silu, [(tag, "z", s)], [(tag, "z", s)])
                TT(kb, "dve", yb[s][:], uc[:, cc, :], zb[s][:], ALU.mult, [(tag, "uc", cc), (tag, "z", s)],
                   [(tag, "y", s)])
                kb.dma("pool", chy[s], y_dram[r0:r0 + 128, t0:t0 + QB], yb[s][:], reads=[(tag, "y", s)],
                       writes=[(tag, "yd", cc, tq)])
        kb.end_phase()
    tag = "sc"
    with ExitStack() as st:
        W = T + 2
        bg = [kb.sb(f"sc_b{i}", [128, T], F32, st) for i in range(2)]
        cg = [kb.sb(f"sc_c{i}", [128, W], F32, st) for i in range(2)]
        ud = [kb.sb(f"sc_u{i}", [128, W], F32, st) for i in range(2)]
        zd = [kb.sb(f"sc_z{i}", [128, T], F32, st) for i in range(2)]
        acc = [kb.sb(f"sc_acc{i}", [128, T], F32, st) for i in range(2)]
        yb = [kb.sb(f"sc_y{i}", [128, T], BF16, st) for i in range(2)]
        chs = {n: [kb.chan(f"sc_{n}{i}") for i in range(2)] for n in ("b", "c", "u", "z", "y")}
        for cc in range(8):
            s = cc % 2
            r0 = cc * 128
            MS(kb, "pool", cg[s][:, 0:2], 0.0, [(tag, "c", s)])
            MS(kb, "pool", ud[s][:, 0:2], 0.0, [(tag, "u", s)])
            kb.dma("sp", chs["b"][s], bg[s][:], pj[3072 + r0:3072 + r0 + 128, :], writes=[(tag, "b", s)])
            kb.dma("sp", chs["c"][s], cg[s][:, 2:W], pj[4096 + r0:4096 + r0 + 128, :], writes=[(tag, "c", s)])
            kb.dma("sp", chs["u"][s], ud[s][:, 2:W], pj[5120 + r0:5120 + r0 + 128, :], writes=[(tag, "u", s)])
            kb.dma("sp", chs["z"][s], zd[s][:], pj[6144 + r0:6144 + r0 + 128, :], writes=[(tag, "z", s)])
            TT(kb, "dve", cg[s][:], cg[s][:], ud[s][:], ALU.mult, [(tag, "c", s), (tag, "u", s)], [(tag, "c", s)])
            TS(kb, "dve", acc[s][:], cg[s][:, 2:W], cw["dcw"][:, cc, 2:3], None, ALU.mult, None,
               [(tag, "c", s)], [(tag, "acc", s)])
            for j in range(2):
                STT(kb, acc[s][:], cg[s][:, j:j + T], cw["dcw"][:, cc, j:j + 1], acc[s][:], ALU.mult, ALU.add,
                    [(tag, "c", s), (tag, "acc", s)], [(tag, "acc", s)])
            ACT(kb, zd[s][:], zd[s][:], AF.Silu, [(tag, "z", s)], [(tag, "z", s)])
            TT(kb, "pool", bg[s][:], bg[s][:], zd[s][:], ALU.mult, [(tag, "b", s), (tag, "z", s)], [(tag, "b", s)])
            TT(kb, "dve", yb[s][:], acc[s][:], bg[s][:], ALU.mult, [(tag, "acc", s), (tag, "b", s)], [(tag, "y", s)])
            kb.dma("pool", chs["y"][s], y_dram[1024 + r0:1024 + r0 + 128, :], yb[s][:], reads=[(tag, "y", s)],
                   writes=[(tag, "yd", cc)])
        kb.end_phase()


def colnorm_phase(kb, C, tag, src, KC, gT, outT):
    with ExitStack() as st:
        raw = [kb.sb(f"{tag}_raw{i}", [128, KC, 1024], F32, st) for i in range(2)]
        sq = [kb.sb(f"{tag}_sq{i}", [128, 1024], F32, st) for i in range(2)]
        rs = kb.sb(f"{tag}_rs", [128, 1024], F32, st)
        ssp = kb.ps(f"{tag}_ssp", [128, 1024], F32, st)
        ch = [kb.chan(f"{tag}_l{i}") for i in range(2)]
        sv = src.rearrange("(k p) t -> p k t", p=128)
        for ts in range(4):
            s = ts % 2
            kb.dma("sp", ch[s], raw[s][:], sv[:, :, ts * 1024:(ts + 1) * 1024], writes=[(tag, "raw", s)])
            for k in range(KC):
                q = k % 2
                ACT(kb, sq[q][:], raw[s][:, k, :], AF.Square, [(tag, "raw", s)], [(tag, "sq", q)])
                for b in range(2):
                    MM(kb, ssp[:, b * 512:(b + 1) * 512], C["ones_f"][:], sq[q][:, b * 512:(b + 1) * 512],
                       k == 0, k == KC - 1, [(tag, "sq", q)], [(tag, "ssp")], signal=True)
            ACT(kb, rs[:], ssp[:], AF.Sqrt, [(tag, "ssp")], [(tag, "rs")], bias=C["eps"][:, 0:1],
                scale=1.0 / (KC * 128))
            RECIP(kb, rs[:], rs[:], [(tag, "rs")], [(tag, "rs")])
            for k in range(KC):
                STT(kb, outT[:, k, ts * 1024:(ts + 1) * 1024], raw[s][:, k, :], gT[:, k:k + 1], rs[:],
                    ALU.mult, ALU.mult, [(tag, "raw", s), (tag, "rs")], [(tag, "out", k, ts)])
        kb.end_phase()


class HeadNormSink:
    def __init__(self, kb, C, tag, dst, gain, n_norm, raw_dst, st):
        self.kb, self.C, self.tag, self.dst, self.gain = kb, C, tag, dst, gain
        self.n_norm, self.raw_dst = n_norm, raw_dst
        self.sq = [kb.sb(f"{tag}_sq{i}", [128, 1024], F32, st) for i in range(2)]
        self.rs = [kb.sb(f"{tag}_rs{i}", [128, 1024], F32, st) for i in range(2)]
        self.ob = [kb.sb(f"{tag}_ob{i}", [128, 1024], BF16, st) for i in range(2)]
        self.ssp = kb.ps(f"{tag}_ssp", [128, 1024], F32, st)
        self.ch = [kb.chan(f"{tag}_o{i}") for i in range(2)]
        self.i = 0

    def __call__(self, c, ts, ps, pkey, st):
        kb, C, tag = self.kb, self.C, self.tag
        s = self.i % 2
        self.i += 1
        if c < self.n_norm:
            ACT(kb, self.sq[s][:], ps[:], AF.Square, [pkey], [(tag, "sq", s)])
            for b in range(2):
                MM(kb, self.ssp[:, b * 512:(b + 1) * 512], C["ones_f"][:], self.sq[s][:, b * 512:(b + 1) * 512],
                   True, True, [(tag, "sq", s)], [(tag, "ssp")])
            ACT(kb, self.rs[s][:], self.ssp[:], AF.Sqrt, [(tag, "ssp")], [(tag, "rs", s)], bias=C["eps"][:, 0:1],
                scale=1.0 / 128)
            RECIP(kb, self.rs[s][:], self.rs[s][:], [(tag, "rs", s)], [(tag, "rs", s)])
            STT(kb, self.ob[s][:], ps[:], self.gain, self.rs[s][:], ALU.mult, ALU.mult,
                [pkey, (tag, "rs", s)], [(tag, "ob", s)])
            d = self.dst[c]
        else:
            CP(kb, "act", self.ob[s][:], ps[:], [pkey], [(tag, "ob", s)])
            d = self.raw_dst[c - self.n_norm]
        kb.dma("sp", self.ch[s], d[:, ts * 1024:(ts + 1) * 1024], self.ob[s][:], reads=[(tag, "ob", s)],
               writes=[(tag, "dst", c)])


def phase_dsa_prep(kb, C, pj, W, S):
    with ExitStack() as st:
        cqn = kb.sb("cqn", [128, 4, T], BF16, st)
        colnorm_phase(kb, C, "cq", pj[0:512, :], 4, C["qnormT"], cqn)
        with ExitStack() as st2:
            sink = HeadNormSink(kb, C, "qs", S["qT"], C["qgain"][:, 0:1], 8, S["qiT"], st2)
            gemm_fm(kb, "qg", cqn, [], 4, W["w_uqiq"], 16, sink)
    with ExitStack() as st:
        ckvn = kb.sb("ckvn", [128, 2, T], BF16, st)
        colnorm_phase(kb, C, "ckv", pj[512:768, :], 2, C["kvnormT"], ckvn)
        with ExitStack() as st2:
            sink = HeadNormSink(kb, C, "ks", S["kT"], C["kgain"][:, 0:1], 8, None, st2)
            gemm_fm(kb, "kg", ckvn, [], 2, W["w_uk"], 8, sink)
        with ExitStack() as st2:
            wv = kb.sb("wv", [128, 2, 1024], BF16, st2)
            chw = kb.chan("wv")
            for k in range(2):
                kb.dma("pool", chw, wv[:, k, :], W["w_uv"][k], writes=[("wv",)])
            vps = [kb.ps(f"v_ps{i}", [128, 1024], F32, st2) for i in range(2)]
            vb = [kb.sb(f"v_b{i}", [128, 1024], BF16, st2) for i in range(2)]
            chv = [kb.chan(f"v_o{i}") for i in range(2)]
            for tt in range(NT):
                s = tt % 2
                for b in range(2):
                    for k in range(2):
                        MM(kb, vps[s][:, b * 512:(b + 1) * 512], ckvn[:, k, tt * 128:(tt + 1) * 128],
                           wv[:, k, b * 512:(b + 1) * 512], k == 0, k == 1, [("wv",)], [("v", "ps", s)])
                CP(kb, "act" if tt % 2 else "dve", vb[s][:], vps[s][:], [("v", "ps", s)], [("v", "b", s)])
                kb.dma("sp", chv[s], S["V"][tt * 128:(tt + 1) * 128, :], vb[s][:], reads=[("v", "b", s)],
                       writes=[("V", tt)])
            kb.end_phase()


def phase_ki_tmaj(kb, C, pj, S):
    with ExitStack() as st:
        ki = kb.sb("ki_raw", [128, T], F32, st)
        sq = kb.sb("ki_sq", [128, T], F32, st)
        mean = kb.sb("ki_mean", [128, 1024], F32, st)
        var = kb.sb("ki_var", [128, 1024], F32, st)
        kio = kb.sb("ki_o", [128, T], BF16, st)
        sm = kb.sb("ki_sm", [32, T], F32, st)
        tmo = kb.sb("ki_tmo", [128, NT, 32], F32, st)
        ps1 = kb.ps("ki_ps1", [128, 1024], F32, st)
        ps2 = kb.ps("ki_ps2", [128, 1024], F32, st)
        pst = kb.ps("ki_pst", [128, 16, 32], F32, st)
        ch = kb.chan("ki")
        kb.dma("sp", ch, ki[0:64, :], pj[768:832, :], writes=[("ki", "raw")])
        kb.dma("sp", ch, ki[64:128, :], pj[768:832, :], writes=[("ki", "raw")])
        kb.dma("sp", ch, sm[:], pj[896:928, :], writes=[("ki", "sm")])
        ACT(kb, sq[:], ki[:], AF.Square, [("ki", "raw")], [("ki", "sq")])
        for ts in range(4):
            for b in range(2):
                c0 = ts * 1024 + b * 512
                MM(kb, ps1[:, b * 512:(b + 1) * 512], C["ones_f"][0:64, :], ki[0:64, c0:c0 + 512], True, True,
                   [("ki", "raw")], [("ki", "ps1")])
                MM(kb, ps2[:, b * 512:(b + 1) * 512], C["ones_f"][0:64, :], sq[0:64, c0:c0 + 512], True, True,
                   [("ki", "sq")], [("ki", "ps2")])
            ACT(kb, mean[:], ps1[:], AF.Copy, [("ki", "ps1")], [("ki", "mean")], scale=1.0 / 64)
            TT(kb, "dve", var[:], mean[:], mean[:], ALU.mult, [("ki", "mean")], [("ki", "var")])
            STT(kb, var[:], ps2[:], 1.0 / 64, var[:], ALU.mult, ALU.subtract, [("ki", "ps2"), ("ki", "var")],
                [("ki", "var")])
            ACT(kb, var[:], var[:], AF.Sqrt, [("ki", "var")], [("ki", "var")], bias=C["eps"][:, 0:1])
            RECIP(kb, var[:], var[:], [("ki", "var")], [("ki", "var")])
            sl = slice(ts * 1024, (ts + 1) * 1024)
            TT(kb, "dve", ki[:, sl], ki[:, sl], mean[:], ALU.subtract, [("ki", "raw"), ("ki", "mean")], [("ki", "raw")])
            TT(kb, "dve", ki[:, sl], ki[:, sl], var[:], ALU.mult, [("ki", "raw"), ("ki", "var")], [("ki", "raw")])
            TS(kb, "dve", kio[:, sl], ki[:, sl], C["ikw"][:, 0:1], C["ikb"][:, 0:1], ALU.mult, ALU.add,
               [("ki", "raw")], [("ki", "o")])
        kb.dma("sp", ch, S["kiT"], kio[:], reads=[("ki", "o")], writes=[("kiT",)])
        for g in range(2):
            for tt in range(16):
                t = g * 16 + tt
                TR(kb, pst[:, tt, :], sm[:, t * 128:(t + 1) * 128], C["ident"][0:32, 0:32], [("ki", "sm")],
                   [("ki", "pst")], signal=(tt == 15))
            CP(kb, "dve", tmo[:, g * 16:(g + 1) * 16, :], pst[:], [("ki", "pst")], [("ki", "tmo")])
        kb.dma("sp", ch, S["tmaj"].rearrange("(n p) c -> p n c", p=128), tmo[:], reads=[("ki", "tmo")],
               writes=[("tmaj",)])
        kb.end_phase()


N_BIS = 13
SCALE_A = 128 ** -0.5


def phase_dsa_attn(kb, C, pj, S, y_dram):
    tag = "at"
    with ExitStack() as st:
        kT = kb.sb("at_kT", [128, 8, T], BF16, st)
        Vt = kb.sb("at_V", [128, NT, 1024], BF16, st)
        kiT = kb.sb("at_kiT", [128, T], BF16, st)
        score1 = kb.sb("at_sc", [128, T], F32, st)
        score = [score1, score1]
        mask = kb.sb("at_mask", [128, T], BF16, st)
        junk = mask
        maskT1 = kb.sb("at_maskT", [128, NT, 128], BF16, st)
        maskT = [maskT1, maskT1]
        rbuf = [kb.sb(f"at_r{i}", [128, 2, 512], BF16, st) for i in range(2)]
        dsg = kb.sb("at_dsg", [128, 16, 128], BF16, st)
        qb = [kb.sb(f"at_q{i}", [128, 8, 128], BF16, st) for i in range(2)]
        qib1 = kb.sb("at_qi", [128, 8, 128], BF16, st)
        qib = [qib1, qib1]
        wt = [kb.sb(f"at_w{i}", [128, 16], F32, st) for i in range(2)]
        bs = [kb.sb(f"at_bs{i}", [128, 8], F32, st) for i in range(2)]
        wks1 = kb.sb("at_wk", [128, N_BIS], F32, st)
        wks = [wks1, wks1]
        fpow = kb.sb("at_fpow", [128, N_BIS], F32, st)
        za1 = kb.sb("at_za", [128, 8, 128], F32, st)
        za = [za1, za1]
        pt = [kb.sb(f"at_pt{i}", [128, 4, 128], BF16, st) for i in range(2)]
        ost1 = kb.sb("at_ost", [128, 8, 256], F32, st)
        ost = [ost1, ost1]
        yst1 = kb.sb("at_yst", [128, 8, 128], BF16, st)
        yst = [yst1, yst1]
        biasS = C["biasS"]
        ident_b = kb.sb("at_identb", [128, 128], BF16, st)
        ones_b = kb.sb("at_onesb", [128, 128], BF16, st)
        ips = [kb.ps(f"at_ips{i}", [128, 2, 512], F32, st) for i in range(2)]
        lg = [ips[i][:, 0, :].rearrange("p (a b) -> p a b", b=128) for i in range(2)]
        po1 = kb.ps("at_po", [128, 128], F32, st)
        prs1 = kb.ps("at_prs", [128, 128], F32, st)
        po, prs = [po1, po1], [prs1, prs1]
        sps1 = kb.ps("at_sps", [128, 512], F32, st)
        sps = [sps1, sps1]
        tps = ips[0][:, 0, :].bitcast(BF16)[:, 0:512].rearrange("p (a b) -> p a b", b=128)
        chl = kb.chan("at_ld")
        chq = [kb.chan(f"at_q{i}") for i in range(2)]
        chy = [kb.chan(f"at_y{i}") for i in range(2)]
        chz = kb.chan("at_z")
        kb.dma("sp", chl, kT[:], S["kT"].rearrange("h p t -> p h t"), writes=[(tag, "kT")])
        kb.dma("sp", chl, Vt[:], S["V"].rearrange("(n p) c -> p n c", p=128), writes=[(tag, "V")])
        kb.dma("sp", chl, kiT[:], S["kiT"], writes=[(tag, "kiT")])
        CP(kb, "dve", ident_b[:], C["ident"][:], [], [(tag, "cst")])
        CP(kb, "dve", ones_b[:], C["ones_f"][:], [], [(tag, "cst")])
        for k in range(1, N_BIS + 1):
            MS(kb, "dve", fpow[:, k - 1:k], 2.0 ** -k, [(tag, "cst")])
        qTv = S["qT"].rearrange("h p t -> p h t")
        qiTv = S["qiT"].rearrange("h p t -> p h t")
        zav = pj[1024:2048, :].rearrange("(h p) t -> p h t", p=128)
        yv = y_dram[0:1024, :].rearrange("(h p) t -> p h t", p=128)
        cnt = {"ips": 0, "r": 0, "lg": 0, "pt": 0, "po": 0, "sps": 0}

        def stage_a(i):
            s = i % 2
            c0, c1 = i * 128, (i + 1) * 128
            kb.dma("sp", chq[s], qb[s][:], qTv[:, :, c0:c1], writes=[(tag, "q", s)])
            kb.dma("sp", chq[s], qib[s][:], qiTv[:, :, c0:c1], writes=[(tag, "qi")])
            kb.dma("sp", chq[s], wt[s][:], S["tmaj"][c0:c1, 0:16], writes=[(tag, "w", s)])
            nW = (i + 4) // 4
            Wi = nW * 512
            sk = (tag, "score")
            TT(kb, "dve", dsg[:], ident_b[:].unsqueeze(1).to_broadcast([128, 16, 128]),
               wt[s][:, 0:16].unsqueeze(2).to_broadcast([128, 16, 128]), ALU.mult, [(tag, "w", s), (tag, "cst")],
               [(tag, "dsg")])
            for w in range(nW):
                sp_ = cnt["sps"] % 2
                cnt["sps"] += 1
                units = []

                def acc(u):
                    h0, r_ = u
                    for e_ in range(2):
                        MM(kb, sps[sp_][:], dsg[:, h0 + e_, :], rbuf[r_][:, e_, :], h0 + e_ == 0, h0 + e_ == 15,
                           [(tag, "dsg"), (tag, "r", r_, e_)], [(tag, "sps", 0)], signal=(e_ == 1))

                for pair in range(8):
                    p = cnt["ips"] % 2
                    cnt["ips"] += 1
                    r = cnt["r"] % 2
                    cnt["r"] += 1
                    for e_ in range(2):
                        base = e_ * 64
                        MM(kb, ips[p][:, e_, :], qib[s][base:base + 64, pair, :],
                           kiT[base:base + 64, w * 512:(w + 1) * 512], True, True, [(tag, "qi"), (tag, "kiT")],
                           [(tag, "ips", p)], signal=(e_ == 1))
                    ACT(kb, rbuf[r][:, 0, :], ips[p][:, 0, :], AF.Relu, [(tag, "ips", p)], [(tag, "r", r, 0)])
                    if pair % 2 == 0:
                        TS(kb, "dve", rbuf[r][:, 1, :], ips[p][:, 1, :], 0.0, None, ALU.max, None, [(tag, "ips", p)],
                           [(tag, "r", r, 1)])
                    else:
                        ACT(kb, rbuf[r][:, 1, :], ips[p][:, 1, :], AF.Relu, [(tag, "ips", p)], [(tag, "r", r, 1)])
                    if units:
                        acc(units.pop())
                    units.append((2 * pair, r))
                acc(units.pop())
                CP(kb, "dve", score[s][:, w * 512:(w + 1) * 512], sps[sp_][:], [(tag, "sps", 0)], [sk])
            b = bs[s]
            bk = (tag, "bs", s)
            TS(kb, "dve", junk[:, 0:Wi], score[s][:, 0:Wi], 1.0, None, ALU.mult, ALU.max, [sk], [(tag, "mask"), bk],
               accum_out=b[:, 0:1])
            TS(kb, "dve", junk[:, 0:Wi], score[s][:, 0:Wi], -1.0, None, ALU.mult, ALU.max, [sk], [(tag, "mask"), bk],
               accum_out=b[:, 6:7])
            TT(kb, "dve", b[:, 0:1], b[:, 0:1], b[:, 6:7], ALU.max, [bk], [bk])
            TT(kb, "dve", score[s][:, Wi - 512:Wi], score[s][:, Wi - 512:Wi],
               C["cbase"][:, 384 - (i % 4) * 128:896 - (i % 4) * 128], ALU.add, [sk], [sk])
            TS(kb, "dve", b[:, 1:2], b[:, 0:1], -1.001, -1e-20, ALU.mult, ALU.add, [bk], [bk])
            TS(kb, "dve", b[:, 2:3], b[:, 0:1], 2.002, 2e-20, ALU.mult, ALU.add, [bk], [bk])
            wk = wks[s]
            TS(kb, "dve", wk[:], fpow[:], b[:, 2:3], None, ALU.mult, None, [bk, (tag, "cst")], [bk])
            TT(kb, "dve", b[:, 3:4], b[:, 1:2], wk[:, 0:1], ALU.add, [bk], [bk])
            for k in range(1, N_BIS + 1):
                TS(kb, "dve", junk[:, 0:Wi], score[s][:, 0:Wi], b[:, 3:4], None, ALU.is_ge, ALU.add,
                   [sk, bk], [(tag, "mask"), bk], accum_out=b[:, 4:5])
                TS(kb, "dve", b[:, 5:6], b[:, 4:5], 255.5, -0.5, ALU.is_ge, ALU.add, [bk], [bk])
                STT(kb, b[:, 3:4], b[:, 5:6], wk[:, k - 1:k], b[:, 3:4], ALU.mult, ALU.add, [bk], [bk])
            STT(kb, b[:, 1:2], wk[:, N_BIS - 1:N_BIS], -0.5, b[:, 3:4], ALU.mult, ALU.add, [bk], [bk])

        def stage_t(i):
            s = i % 2
            n = i + 1
            TS(kb, "dve", mask[:, 0:n * 128], score[s][:, 0:n * 128], bs[s][:, 1:2], None, ALU.is_ge, None,
               [(tag, "score"), (tag, "bs", s)], [(tag, "mask")])
            for j0 in range(0, n, 4):
                nb = min(4, n - j0)
                for jj in range(nb):
                    j = j0 + jj
                    TR(kb, tps[:, jj, :], mask[:, j * 128:(j + 1) * 128], ident_b[:], [(tag, "mask"), (tag, "cst")],
                       [(tag, "ips", 0)], signal=(jj == nb - 1))
                ACT(kb, maskT[s][:, j0:j0 + nb, :], tps[:, 0:nb, :], AF.Copy, [(tag, "ips", 0)], [(tag, "maskT")],
                    scale=30000.0, bias=-30000.0)

        def stage_b(i):
            s = i % 2
            n = i + 1
            groups = [(h, j0, min(4, n - j0)) for h in range(8) for j0 in range(0, n, 4)]
            slots = {}

            def qk(gi):
                h, j0, nb = groups[gi]
                p = cnt["lg"] % 2
                cnt["lg"] += 1
                x = cnt["pt"] % 2
                cnt["pt"] += 1
                slots[gi] = x
                for jj in range(nb):
                    j = j0 + jj
                    near = (i - j) <= 1
                    MM(kb, lg[p][:, jj, :], kT[:, h, j * 128:(j + 1) * 128], qb[s][:, h, :], True, False,
                       [(tag, "kT"), (tag, "q", s)], [(tag, "ips", p)], signal=False)
                    if near:
                        MM(kb, lg[p][:, jj, :], ident_b[:], biasS[:, h, i - j, :], False, False,
                           [(tag, "cst")], [(tag, "ips", p)], signal=False)
                    MM(kb, lg[p][:, jj, :], ident_b[:], maskT[s][:, j, :], False, True,
                       [(tag, "cst"), (tag, "maskT")], [(tag, "ips", p)], signal=(jj == nb - 1))
                ACT(kb, pt[x][:, 0:nb, :], lg[p][:, 0:nb, :], AF.Exp, [(tag, "ips", p)], [(tag, "pt", x)],
                    scale=SCALE_A, bias=C["cb"][:, h:h + 1])

            def pv(gi):
                h, j0, nb = groups[gi]
                x = slots.pop(gi)
                for jj in range(nb):
                    j = j0 + jj
                    MM(kb, po[0][:], Vt[:, j, h * 128:(h + 1) * 128], pt[x][:, jj, :], j == 0, j == n - 1,
                       [(tag, "V"), (tag, "pt", x)], [(tag, "po", 0)])
                    MM(kb, prs[0][:], ones_b[:], pt[x][:, jj, :], j == 0, j == n - 1,
                       [(tag, "cst"), (tag, "pt", x)], [(tag, "prs", 0)], signal=(jj == nb - 1))
                if j0 + nb == n:
                    CP(kb, "act", ost[s][:, h, 0:128], po[0][:], [(tag, "po", 0)], [(tag, "ost", h)])
                    CP(kb, "act", ost[s][:, h, 128:256], prs[0][:], [(tag, "prs", 0)], [(tag, "ost", h)])

            qk(0)
            for gi in range(len(groups)):
                if gi + 1 < len(groups):
                    qk(gi + 1)
                pv(gi)

        def stage_f(i):
            s = i % 2
            keys = [(tag, "ost", h) for h in range(8)]
            kb.dma("sp", chz, za[s][:], zav[:, :, i * 128:(i + 1) * 128], writes=[(tag, "za")])
            ACT(kb, za[s][:], za[s][:], AF.Silu, [(tag, "za")], [(tag, "za")])
            RECIP(kb, ost[s][:, :, 128:256], ost[s][:, :, 128:256], keys, keys)
            TT(kb, "dve", ost[s][:, :, 0:128], ost[s][:, :, 0:128], ost[s][:, :, 128:256], ALU.mult, keys, keys)
            TT(kb, "dve", yst[s][:], ost[s][:, :, 0:128], za[s][:], ALU.mult, keys + [(tag, "za")], [(tag, "yst")])
            kb.dma("pool", chy[s], yv[:, :, i * 128:(i + 1) * 128], yst[s][:], reads=[(tag, "yst")],
                   writes=[(tag, "y", i)])

        stage_a(0)
        stage_t(0)
        for i in range(NT):
            if i + 1 < NT:
                stage_a(i + 1)
            stage_b(i)
            if i + 1 < NT:
                stage_t(i + 1)
            stage_f(i)
        kb.end_phase()


def phase_gdn_prep(kb, C, pj, S):
    tag = "gp"
    with ExitStack() as st:
        W = T + 3
        HB = T // 2
        ident_b = kb.sb("gp_identb", [128, 128], BF16, st)
        dg = kb.sb("gp_dg", [128, 24, 4, 128], BF16, st)
        rawb = [kb.sb(f"gp_rawb{i}", [128, W], BF16, st) for i in range(2)]
        acc = [kb.sb(f"gp_acc{i}", [128, T], F32, st) for i in range(2)]
        sq = [kb.sb(f"gp_sq{i}", [128, T], F32, st) for i in range(2)]
        rn = [kb.sb(f"gp_rn{i}", [128, T], F32, st) for i in range(2)]
        pcv = kb.ps("gp_pcv", [128, HB], F32, st)
        ssp = kb.ps("gp_ssp", [128, HB], F32, st)
        chl = [kb.chan(f"gp_l{i}") for i in range(2)]
        chs = [kb.chan(f"gp_s{i}") for i in range(2)]
        CP(kb, "dve", ident_b[:], C["ident"][:], [], [(tag, "cst")])
        for cc in range(24):
            TT(kb, "dve", dg[:, cc, :, :], ident_b[:].unsqueeze(1).to_broadcast([128, 4, 128]),
               C["cvw"][:, cc, :].unsqueeze(2).to_broadcast([128, 4, 128]), ALU.mult, [(tag, "cst")], [(tag, "dg")])

        def head(cc):
            s = cc % 2
            r0 = 2048 + cc * 128
            MS(kb, "dve", rawb[s][:, 0:3], 0.0, [(tag, "raw", s)])
            kb.dma("pool", chl[s], rawb[s][:, 3:W], pj[r0:r0 + 128, :], writes=[(tag, "raw", s)])
            for hb in range(2):
                for b in range(4):
                    c0 = hb * HB + b * 512
                    for j in range(4):
                        MM(kb, pcv[:, b * 512:(b + 1) * 512], dg[:, cc, j, :], rawb[s][:, c0 + j:c0 + j + 512],
                           j == 0, j == 3, [(tag, "dg"), (tag, "raw", s)], [(tag, "pcv")], signal=(j == 3 and b == 3))
                ACT(kb, acc[s][:, hb * HB:(hb + 1) * HB], pcv[:], AF.Silu, [(tag, "pcv")], [(tag, "acc", s, hb)])
                if cc < 16:
                    ACT(kb, sq[s][:, hb * HB:(hb + 1) * HB], acc[s][:, hb * HB:(hb + 1) * HB], AF.Square,
                        [(tag, "acc", s, hb)], [(tag, "sq", s, hb)])
                    for b in range(4):
                        c0 = hb * HB + b * 512
                        MM(kb, ssp[:, b * 512:(b + 1) * 512], C["ones_f"][:], sq[s][:, c0:c0 + 512], True, True,
                           [(tag, "sq", s, hb)], [(tag, "ssp")], signal=(b == 3))
                    ACT(kb, rn[s][:, hb * HB:(hb + 1) * HB], ssp[:], AF.Sqrt, [(tag, "ssp")],
                        [(tag, "rn", s, hb)], bias=C["eps"][:, 0:1])

        def tail(cc):
            s = cc % 2
            akeys = [(tag, "acc", s, 0), (tag, "acc", s, 1)]
            if cc < 16:
                keys = [(tag, "rn", s, 0), (tag, "rn", s, 1)]
                RECIP(kb, rn[s][:], rn[s][:], keys, keys)
                STT(kb, acc[s][:], acc[s][:], (128 ** -0.5) if cc < 8 else 1.0, rn[s][:], ALU.mult, ALU.mult,
                    akeys + keys, akeys)
            kb.dma("sp", chs[s], S["gqkv"][cc * 128:(cc + 1) * 128, :], acc[s][:], reads=akeys,
                   writes=[("gqkv", cc)])

        head(0)
        for cc in range(24):
            if cc + 1 < 24:
                head(cc + 1)
            tail(cc)
        kb.end_phase()


import os
GDN_TILES = int(os.environ.get("GDN_TILES", "32"))
GDN_STOP = int(os.environ.get("GDN_STOP", "99"))
GDN_SUB = float(os.environ.get("GDN_SUB", "99"))
GDN_EVAC = os.environ.get("GDN_EVAC", "act")


def phase_gdn(kb, C, pj, S, y_dram):
    with ExitStack() as stc:
        C = dict(C)
        chc = kb.chan("gd_const")
        for name, shape, dt in GDN_CONST_SPECS:
            d = kb.nc.dram_tensor(name, list(shape), dt, kind="ExternalInput").ap()
            t = kb.sb("c_" + name, shape, dt, stc)
            kb.dma("sp", chc, t[:], d, writes=[("const", name)])
            C[name] = t
        kb.end_phase()
        phase_gdn_prep(kb, C, pj, S)
        _phase_gdn_main(kb, C, pj, S, y_dram)


def _phase_gdn_main(kb, C, pj, S, y_dram):
    tag = "gd"
    H = 8
    with ExitStack() as st:
        def fb(name, shape=(128, H, 128), dt=F32):
            return kb.sb("gd_" + name, list(shape), dt, st)

        tm = fb("tm", (128, NT, 32))
        beta = fb("beta", (128, NT, 8))
        g = fb("g", (128, NT, 8))
        t1 = fb("t1", (128, NT, 8))
        t2 = fb("t2", (128, NT, 8))
        nA = fb("nA", (128, 8))
        qT, kT, vT = fb("qT"), fb("kT"), fb("vT")
        gd, egrow, gcr = fb("gdiag"), fb("egrow"), fb("gcr")
        P1, E1, E2 = fb("P1"), fb("E1"), fb("E2")
        HS = (128, H, 128)
        X = [fb("X0", HS, BF16), fb("X1", HS, BF16)]
        Y = [fb("Y0", HS, BF16), fb("Y1", HS, BF16)]
        P, attnT = fb("P", HS, BF16), fb("attnT", HS, BF16)
        vb, kbg, kd, kd1 = fb("vb", HS, BF16), fb("kbg", HS, BF16), fb("kd", HS, BF16), fb("kd1", HS, BF16)
        smk = fb("smk", (128, 16))
        u, wT, qgT, vnew = fb("u"), fb("wT", HS, BF16), fb("qgT", HS, BF16), fb("vnew", HS, BF16)
        Sst, oacc, zb = fb("S"), fb("oacc"), fb("zb")
        Sb = fb("Sb", HS, BF16)
        ident_b = fb("identb", (128, 128), BF16)
        osq, orn = fb("osq"), fb("orn")
        yo = fb("yo", (128, H, 128), BF16)
        sm = fb("sm", (128, 64))
        pA = kb.ps("gd_pA", [128, H, 128], F32, st)
        pB = kb.ps("gd_pB", [128, H, 128], F32, st)
        pC = kb.ps("gd_pC", [128, H, 128], F32, st)
        pO = kb.ps("gd_pO", [128, H, 64], F32, st)
        psm = kb.ps("gd_psm", [128, 32], F32, st)
        ch = kb.chan("gd_l")
        chq = kb.chan("gd_q")
        chz = kb.chan("gd_z")
        chy = kb.chan("gd_y")
        K_ = lambda n: (tag, n)

        def bc_h(ap2d):
            return ap2d.unsqueeze(1).to_broadcast([128, H, 128])

        def bc_f(ap2d):
            return ap2d.unsqueeze(2).to_broadcast([128, H, 128])

        kb.dma("sp", ch, tm[:], S["tmaj"].rearrange("(n p) c -> p n c", p=128), writes=[K_("tm")])
        ACT(kb, beta[:], tm[:, :, 16:24], AF.Sigmoid, [K_("tm")], [K_("beta")])
        dtb = C["dtb_bc"][:].unsqueeze(1).to_broadcast([128, NT, 8])
        TT(kb, "dve", g[:], tm[:, :, 24:32], dtb, ALU.add, [K_("tm")], [K_("g")])
        TS(kb, "dve", t1[:], g[:], -1.0, None, ALU.mult, None, [K_("g")], [K_("t1")])
        TT(kb, "dve", t1[:], t1[:], g[:], ALU.max, [K_("t1"), K_("g")], [K_("t1")])
        ACT(kb, t1[:], t1[:], AF.Exp, [K_("t1")], [K_("t1")], scale=-1.0)
        TS(kb, "dve", t1[:], t1[:], 1.0, None, ALU.add, None, [K_("t1")], [K_("t1")])
        ACT(kb, t1[:], t1[:], AF.Ln, [K_("t1")], [K_("t1")])
        TS(kb, "dve", t2[:], g[:], 0.0, None, ALU.max, None, [K_("g")], [K_("t2")])
        TT(kb, "dve", t2[:], t2[:], t1[:], ALU.add, [K_("t1"), K_("t2")], [K_("t2")])
        ACT(kb, nA[:], C["alog_bc"][:], AF.Exp, [], [K_("nA")])
        TS(kb, "dve", nA[:], nA[:], -1.0, None, ALU.mult, None, [K_("nA")], [K_("nA")])
        TT(kb, "dve", g[:], t2[:], nA[:].unsqueeze(1).to_broadcast([128, NT, 8]), ALU.mult, [K_("t2"), K_("nA")],
           [K_("g")])
        MS(kb, "dve", Sst[:], 0.0, [K_("S")])
        MS(kb, "dve", Sb[:], 0.0, [K_("Sb")])
        MS(kb, "dve", vnew[:], 0.0, [K_("vnew")])
        CP(kb, "dve", ident_b[:], C["ident"][:], [], [K_("identb")])
        pAb = pA[:, 0:4, :].bitcast(BF16).rearrange("p a (c b) -> p (a c) b", b=128)
        gq = S["gqkv"]
        qv = gq[0:1024, :].rearrange("(h p) t -> p h t", p=128)
        kv = gq[1024:2048, :].rearrange("(h p) t -> p h t", p=128)
        vv = gq[2048:3072, :].rearrange("(h p) t -> p h t", p=128)
        zv = pj[5120:6144, :].rearrange("(h p) t -> p h t", p=128)
        yv = y_dram[1024:2048, :].rearrange("(h p) t -> p h t", p=128)

        HH = 4

        def tile_shared(n):
            c0, c1 = n * 128, (n + 1) * 128
            kb.dma("sp", chq, qT[:], qv[:, :, c0:c1], writes=[K_("qT")])
            kb.dma("sp", chq, kT[:], kv[:, :, c0:c1], writes=[K_("kT")])
            kb.dma("sp", chq, vT[:], vv[:, :, c0:c1], writes=[K_("vT")])
            kb.dma("sp", chz, zb[:], zv[:, :, c0:c1], writes=[K_("zb")])
            gn = g[:, n, :]
            bn = beta[:, n, :]
            MM(kb, psm[:, 0:8], C["U2"][:], gn, True, True, [K_("g")], [K_("psm")], signal=False)
            MM(kb, psm[:, 8:16], C["Bsame"][:], gn, True, True, [K_("g")], [K_("psm")], signal=False)
            MM(kb, psm[:, 16:24], C["Bsel0"][:], gn, True, True, [K_("g")], [K_("psm")], signal=False)
            MM(kb, psm[:, 24:32], C["Bsel1"][:], gn, True, True, [K_("g")], [K_("psm")])
            CP(kb, "dve", sm[:, 0:32], psm[:], [K_("psm")], [K_("sm")])
            ACT(kb, sm[:, 32:40], sm[:, 0:8], AF.Exp, [K_("sm")], [K_("sm")])
            TT(kb, "dve", sm[:, 40:48], sm[:, 8:16], sm[:, 0:8], ALU.subtract, [K_("sm")], [K_("sm")])
            ACT(kb, sm[:, 40:48], sm[:, 40:48], AF.Exp, [K_("sm")], [K_("sm")])
            ACT(kb, sm[:, 16:32], sm[:, 16:32], AF.Exp, [K_("sm")], [K_("sm")])
            TT(kb, "dve", sm[:, 48:56], sm[:, 32:40], bn, ALU.mult, [K_("sm"), K_("beta")], [K_("sm")])
            TS(kb, "dve", sm[:, 56:64], bn, -1.0, None, ALU.mult, None, [K_("beta")], [K_("sm")])
            TS(kb, "dve", smk[:, 0:8], sm[:, 40:48], C["Bsel0"][:, 0:1], None, ALU.mult, None, [K_("sm")], [K_("smk")])
            TS(kb, "dve", smk[:, 8:16], sm[:, 40:48], C["Bsel1"][:, 0:1], None, ALU.mult, None, [K_("sm")], [K_("smk")])
            ACT(kb, zb[:], zb[:], AF.Silu, [K_("zb")], [K_("zb")])

        def tile_half(n, hh):
            c0, c1 = n * 128, (n + 1) * 128
            h0, h1 = hh * HH, (hh + 1) * HH
            hs = slice(h0, h1)
            heads = range(h0, h1)
            last = h1 - 1
            k_ = lambda nm: (tag, nm, hh)
            gn = g[:, n, hs]
            bn = beta[:, n, hs]

            def bh(ap2d):
                return ap2d.unsqueeze(1).to_broadcast([128, HH, 128])

            def bf(ap2d):
                return ap2d.unsqueeze(2).to_broadcast([128, HH, 128])

            pAb = pA[:, h0:h0 + 2, :].bitcast(BF16).rearrange("p a (c b) -> p (a c) b", b=128)
            gc = sm[:, h0:h1]
            TT(kb, "dve", gd[:, hs, :], bh(C["U2"][:]), bf(gn), ALU.mult, [K_("g")], [k_("gdiag")])
            MM(kb, pA[:, hs, :], C["ones_f"][:], gd[:, hs, :], True, True, [k_("gdiag")], [k_("pA")])
            yield
            CP(kb, "dve", gcr[:, hs, :], pA[:, hs, :], [k_("pA")], [k_("gcr")])
            ACT(kb, egrow[:, hs, :], gcr[:, hs, :], AF.Exp, [k_("gcr")], [k_("egrow")])
            TT(kb, "dve", P1[:, hs, :], gcr[:, hs, :], bf(gc), ALU.subtract, [k_("gcr"), K_("sm")], [k_("P1")])
            yield
            ACT(kb, E1[:, hs, :], P1[:, hs, :], AF.Relu, [k_("P1")], [k_("E1")])
            ACT(kb, E1[:, hs, :], E1[:, hs, :], AF.Exp, [k_("E1")], [k_("E1")], scale=-1.0)
            ACT(kb, E2[:, hs, :], P1[:, hs, :], AF.Relu, [k_("P1")], [k_("E2")], scale=-1.0)
            ACT(kb, E2[:, hs, :], E2[:, hs, :], AF.Exp, [k_("E2")], [k_("E2")], scale=-1.0)
            yield
            TT(kb, "dve", E1[:, hs, :], E1[:, hs, :], bh(C["MLs"][:]), ALU.mult, [k_("E1")], [k_("E1")])
            TT(kb, "dve", E1[:, hs, :], E1[:, hs, :], bf(sm[:, 56 + h0:56 + h1]), ALU.mult, [k_("E1"), K_("sm")],
               [k_("E1")])
            TT(kb, "dve", E2[:, hs, :], E2[:, hs, :], bh(C["MU"][:]), ALU.mult, [k_("E2")], [k_("E2")])
            yield
            for h in heads:
                MM(kb, pB[:, h, :], kT[:, h, :], kT[:, h, :], True, True, [K_("kT")], [k_("pB")], signal=(h == last))
            for h in heads:
                MM(kb, pC[:, h, :], kT[:, h, :], qT[:, h, :], True, True, [K_("kT"), K_("qT")], [k_("pC")],
                   signal=(h == last))
            yield
            TT(kb, "dve", X[0][:, hs, :], pB[:, hs, :], E1[:, hs, :], ALU.mult, [k_("pB"), k_("E1")], [k_("X0")])
            TT(kb, "dve", attnT[:, hs, :], pC[:, hs, :], E2[:, hs, :], ALU.mult, [k_("pC"), k_("E2")], [k_("attnT")])
            yield
            for h in heads:
                TR(kb, pAb[:, h - h0, :], X[0][:, h, :], ident_b[:], [k_("X0"), K_("identb")], [k_("pA")],
                   signal=(h == last))
            yield
            CP(kb, "dve", Y[0][:, hs, :], pAb, [k_("pA")], [k_("Y0")])
            TT(kb, "dve", P[:, hs, :], Y[0][:, hs, :], bh(C["ident"][:]), ALU.add, [k_("Y0")], [k_("P")])
            yield
            cur = 0
            for lvl in range(5):
                nxt = 1 - cur
                xk, yk = k_(f"X{cur}"), k_(f"Y{cur}")
                xn, yn = k_(f"X{nxt}"), k_(f"Y{nxt}")
                for h in heads:
                    MM(kb, pB[:, h, :], Y[cur][:, h, :], X[cur][:, h, :], True, True, [xk, yk], [k_("pB")],
                       signal=(h == last))
                if lvl < 4:
                    for h in heads:
                        MM(kb, pC[:, h, :], X[cur][:, h, :], Y[cur][:, h, :], True, True, [xk, yk], [k_("pC")],
                           signal=(h == last))
                yield
                CP(kb, GDN_EVAC, X[nxt][:, hs, :], pB[:, hs, :], [k_("pB")], [xn])
                if lvl < 4:
                    CP(kb, GDN_EVAC, Y[nxt][:, hs, :], pC[:, hs, :], [k_("pC")], [yn])
                yield
                for h in heads:
                    MM(kb, pA[:, h, :], X[nxt][:, h, :], P[:, h, :], True, True, [xn, k_("P")], [k_("pA")],
                       signal=(h == last))
                yield
                TT(kb, "dve", P[:, hs, :], P[:, hs, :], pA[:, hs, :], ALU.add, [k_("pA"), k_("P")], [k_("P")])
                yield
                cur = nxt
            for h in heads:
                TR(kb, pB[:, h, :], kT[:, h, :], C["ident"][:], [K_("kT")], [k_("pB")], signal=(h == last))
            for h in heads:
                TR(kb, pC[:, h, :], vT[:, h, :], C["ident"][:], [K_("vT")], [k_("pC")], signal=(h == last))
            yield
            TT(kb, "dve", kbg[:, hs, :], pB[:, hs, :], bf(sm[:, 48 + h0:48 + h1]), ALU.mult, [k_("pB"), K_("sm")],
               [k_("kbg")])
            TT(kb, "dve", kd[:, hs, :], pB[:, hs, :], bf(smk[:, h0:h1]), ALU.mult, [k_("pB"), K_("smk")], [k_("kd")])
            TT(kb, "dve", kd1[:, hs, :], pB[:, hs, :], bf(smk[:, 8 + h0:8 + h1]), ALU.mult, [k_("pB"), K_("smk")],
               [k_("kd")])
            TT(kb, "dve", vb[:, hs, :], pC[:, hs, :], bf(bn), ALU.mult, [k_("pC"), K_("beta")], [k_("vb")])
            yield
            for h in heads:
                MM(kb, pA[:, h, :], P[:, h, :], vb[:, h, :], True, True, [k_("P"), k_("vb")], [k_("pA")],
                   signal=(h == last))
            for h in heads:
                MM(kb, pB[:, h, :], kbg[:, h, :], P[:, h, :], True, True, [k_("P"), k_("kbg")], [k_("pB")],
                   signal=(h == last))
            yield
            CP(kb, "dve", u[:, hs, :], pA[:, hs, :], [k_("pA")], [k_("u")])
            CP(kb, GDN_EVAC, wT[:, hs, :], pB[:, hs, :], [k_("pB")], [k_("wT")])
            TT(kb, "dve", qgT[:, hs, :], qT[:, hs, :], egrow[:, hs, :], ALU.mult, [K_("qT"), k_("egrow")], [k_("qgT")])
            yield
            for c in range(2):
                r0, r1 = c * 64, (c + 1) * 64
                for h in heads:
                    MM(kb, pA[r0:r1, h, :], wT[:, h, r0:r1], Sb[:, h, :], True, True, [k_("wT"), k_("Sb")], [k_("pA")],
                       signal=(h == last))
                yield
                TT(kb, "dve", vnew[r0:r1, hs, :], u[r0:r1, hs, :], pA[r0:r1, hs, :], ALU.subtract,
                   [k_("u"), k_("pA")], [k_("vnew")])
                yield
                for h in heads:
                    MM(kb, pO[:, h, :], Sb[:, h, :], qgT[:, h, r0:r1], True, False, [k_("Sb"), k_("qgT")], [k_("pO")])
                    MM(kb, pO[:, h, :], vnew[r0:r1, h, :], attnT[r0:r1, h, r0:r1], False, True,
                       [k_("vnew"), k_("attnT")], [k_("pO")], signal=(h == last))
                for h in heads:
                    MM(kb, pB[:, h, :], (kd, kd1)[c][:, h, :], vnew[:, h, :], True, True, [k_("kd"), k_("vnew")],
                       [k_("pB")], signal=(h == last))
                yield
                CP(kb, GDN_EVAC, oacc[:, hs, r0:r1], pO[:, hs, :], [k_("pO")], [k_("oacc")])
                TT(kb, "dve", Sst[:, hs, :], Sst[:, hs, :], bf(sm[:, 16 + 8 * c + h0:16 + 8 * c + h1]), ALU.mult,
                   [k_("S"), K_("sm")], [k_("S")])
                TT(kb, "dve", Sst[:, hs, :], Sst[:, hs, :], pB[:, hs, :], ALU.add, [k_("S"), k_("pB")], [k_("S")])
                CP(kb, "act", Sb[:, hs, :], Sst[:, hs, :], [k_("S")], [k_("Sb")])
                yield
            ACT(kb, osq[:, hs, :], oacc[:, hs, :], AF.Square, [k_("oacc")], [k_("osq")])
            MM(kb, pC[:, hs, :], C["ones_f"][:], osq[:, hs, :], True, True, [k_("osq")], [k_("pC")])
            yield
            CP(kb, "dve", orn[:, hs, :], pC[:, hs, :], [k_("pC")], [k_("orn")])
            ACT(kb, orn[:, hs, :], orn[:, hs, :], AF.Sqrt, [k_("orn")], [k_("orn")], bias=C["eps"][:, 0:1],
                scale=1.0 / 128)
            yield
            RECIP(kb, orn[:, hs, :], orn[:, hs, :], [k_("orn")], [k_("orn")])
            STT(kb, oacc[:, hs, :], oacc[:, hs, :], C["onorm"][:, 0:1], orn[:, hs, :], ALU.mult, ALU.mult,
                [k_("oacc"), k_("orn")], [k_("oacc")])
            TT(kb, "dve", yo[:, hs, :], oacc[:, hs, :], zb[:, hs, :], ALU.mult, [k_("oacc"), K_("zb")], [k_("yo")])
            kb.dma("pool", chy, yv[:, hs, c0:c1], yo[:, hs, :], reads=[k_("yo")], writes=[("y0b", n, hh)])

        for n in range(GDN_TILES):
            tile_shared(n)
            gens = [tile_half(n, 0), tile_half(n, 1)]
            while gens:
                for gnr in list(gens):
                    try:
                        next(gnr)
                    except StopIteration:
                        gens.remove(gnr)
        kb.end_phase()


def load_consts(kb, nc, names_shapes):
    C = {}
    ch = kb.chan("const")
    for name, shape, dt in names_shapes:
        d = nc.dram_tensor(name, list(shape), dt, kind="ExternalInput").ap()
        if name == "cbase":
            t = kb.sb("c_" + name, shape, BF16)
            kb.dma("pool", ch, t[:], d, writes=[("const", name)])
        else:
            t = kb.sb("c_" + name, shape, dt)
            kb.dma("sp", ch, t[:], d, writes=[("const", name)])
        C[name] = t
    kb.end_phase()
    with ExitStack() as st:
        d = nc.dram_tensor("biasT", [128, 8, 2, 128], F32, kind="ExternalInput").ap()
        C["biasS"] = kb.sb("c_biasS", [128, 8, 2, 128], BF16)
        bt = kb.sb("biasT_tmp", [128, 8, 2, 128], F32, st)
        kb.dma("sp", ch, bt[:], d, writes=[("const", "biasT")])
        for h in range(8):
            TS(kb, "dve", C["biasS"][:, h, :, :], bt[:, h, :, :], C["cb"][:, h:h + 1], 128 ** 0.5,
               ALU.subtract, ALU.mult, [("const", "biasT")], [("const", "biasS")])
        kb.end_phase()
    return C


CONST_SPECS = [
    ("ident", (128, 128), F32),
    ("ones_f", (128, 128), F32),
    ("eps", (128, 1), F32),
    ("normwT", (128, 2, 16), F32),
    ("cbase", (128, 896), F32),
    ("cb", (128, 8), F32),
    ("qnormT", (128, 4), F32),
    ("kvnormT", (128, 2), F32),
    ("qgain", (128, 1), F32),
    ("kgain", (128, 1), F32),
    ("ikw", (128, 1), F32),
    ("ikb", (128, 1), F32),
]

CD_CONST_SPECS = [
    ("dww", (128, 8, 31), F32),
    ("dwb", (128, 8), F32),
    ("lnw", (128, 8), F32),
    ("lnb", (128, 8), F32),
    ("dcw", (128, 8, 3), F32),
]

GDN_CONST_SPECS = [
    ("cvw", (128, 24, 4), F32),
    ("alog_bc", (128, 8), F32),
    ("dtb_bc", (128, 8), F32),
    ("onorm", (128, 1), F32),
    ("U2", (128, 128), F32),
    ("Bsame", (128, 128), F32),
    ("Bsel0", (128, 128), F32),
    ("Bsel1", (128, 128), F32),
    ("MLs", (128, 128), F32),
    ("MU", (128, 128), F32),
]


def build_program(layers=(0, 1), l0_parts=("a", "b"), debug_out=False):
    nc = bass.Bass("TRN2", target_bir_lowering=False)

    def din(name, shape, dt=F32):
        return nc.dram_tensor(name, list(shape), dt, kind="ExternalInput").ap()

    def dscr(name, shape, dt=F32):
        return nc.dram_tensor(name, list(shape), dt, kind="Internal").ap()

    x = din("x", [T, D])
    out = nc.dram_tensor("out", [T, D], F32, kind="ExternalOutput").ap()
    kb = KB(nc)
    C = load_consts(kb, nc, CONST_SPECS)
    src = x
    if 0 in layers:
        ab_w_in = din("ab_w_in", [48, 128, 16 * 128])
        ab_w_out = din("ab_w_out", [16, 128, D])
        W = {"w_uqiq": din("w_uqiq", [16, 128, 4 * 128]), "w_uk": din("w_uk", [8, 128, 2 * 128]),
             "w_uv": din("w_uv", [2, 128, 1024])}
        pj0 = dscr("pj0", [6144, T])
        S = {"qT": dscr("s_qT", [8, 128, T], BF16), "qiT": dscr("s_qiT", [8, 128, T], BF16),
             "kT": dscr("s_kT", [8, 128, T], BF16), "V": dscr("s_V", [T, 1024], BF16),
             "kiT": dscr("s_kiT", [128, T], BF16), "tmaj": dscr("s_tmaj", [T, 32]),
             "gqkv": dscr("s_gqkv", [3072, T])}
        if debug_out:
            y0 = nc.dram_tensor("y0", [2048, T], BF16, kind="ExternalOutput").ap()
        else:
            y0 = dscr("y0", [2048, T], BF16)
        x1 = dscr("x1", [T, D]) if 1 in layers else out
        phase_inproj(kb, C, "l0", src, C["normwT"][:, 0, :], ab_w_in, 48, pj0)
        phase_ki_tmaj(kb, C, pj0, S)
        if "a" in l0_parts:
            phase_dsa_prep(kb, C, pj0, W, S)
            phase_dsa_attn(kb, C, pj0, S, y0)
        if "b" in l0_parts:
            phase_gdn(kb, C, pj0, S, y0)
        if not debug_out:
            phase_outproj(kb, C, "l0o", y0, ab_w_out, src, x1)
        src = x1
    if 1 in layers:
        cd_w_in = din("cd_w_in", [56, 128, 16 * 128])
        cd_w_out = din("cd_w_out", [16, 128, D])
        pj1 = dscr("pj1", [7168, T])
        y1 = dscr("y1", [2048, T], BF16)
        phase_inproj(kb, C, "l1", src, C["normwT"][:, 1, :], cd_w_in, 56, pj1)
        phase_cd_mix(kb, C, pj1, C, y1)
        phase_outproj(kb, C, "l1o", y1, cd_w_out, src, out)
    kb.finish()
    kb.emit()
    kb.close()
    return nc, kb


def tile_w_in(w, nch):
    K, N = w.shape
    assert N == nch * 128
    return np.ascontiguousarray(w.reshape(K // 128, 128, nch, 128).transpose(2, 1, 0, 3)).reshape(nch, 128, -1)


def t5_bucket_np(dist):
    import math
    max_exact = 16
    dd = np.maximum(dist, 1).astype(np.float32)
    large = max_exact + (np.log(dd / max_exact) / math.log(128 / max_exact) * (32 - max_exact)).astype(np.int32)
    large = np.minimum(large, 31)
    return np.where(dist < max_exact, dist, large)


def colT(v, k):
    return np.ascontiguousarray(np.asarray(v, np.float32).reshape(k, 128).T)


def host_consts(inp):
    f = np.float32
    c = {}
    c["ident"] = np.eye(128, dtype=f)
    c["ones_f"] = np.ones((128, 128), f)
    c["eps"] = np.full((128, 1), EPS, f)
    c["normwT"] = np.ascontiguousarray(inp["norm_w"].reshape(2, 16, 128).transpose(2, 0, 1)).astype(f)
    c["dww"] = np.ascontiguousarray(inp["c_dw_w"][0].reshape(31, 8, 128).transpose(2, 1, 0)).astype(f)
    c["dwb"] = colT(inp["c_dw_b"][0], 8)
    c["lnw"] = colT(inp["c_ln_w"][0], 8)
    c["lnb"] = colT(inp["c_ln_b"][0], 8)
    c["dcw"] = np.ascontiguousarray(inp["d_conv_w"][0].reshape(3, 8, 128).transpose(2, 1, 0)).astype(f)
    r = np.arange(128)[:, None]
    cc = np.arange(896)[None, :]
    c["cbase"] = np.where(cc <= r + 384, 0.0, -1e30).astype(f)
    kl = np.arange(128)[:, None, None]
    dd = np.arange(2)[None, :, None]
    ql = np.arange(128)[None, None, :]
    dist = np.maximum(dd * 128 + ql - kl, 0)
    bt = np.asarray(inp["rel_bias"], f)[t5_bucket_np(dist)]
    c["biasT"] = np.ascontiguousarray(bt.transpose(0, 3, 1, 2))
    c["cb"] = np.ascontiguousarray(np.broadcast_to(np.asarray(inp["rel_bias"], f)[31][None, :], (128, 8)))
    c["qnormT"] = colT(inp["a_q_norm"][0], 4)
    c["kvnormT"] = colT(inp["a_kv_norm"][0], 2)
    c["qgain"] = np.asarray(inp["a_q_gain"][0], f).reshape(128, 1).copy()
    c["kgain"] = np.asarray(inp["a_k_gain"][0], f).reshape(128, 1).copy()
    c["ikw"] = np.tile(np.asarray(inp["a_ik_norm_w"][0], f), 2).reshape(128, 1).copy()
    c["ikb"] = np.tile(np.asarray(inp["a_ik_norm_b"][0], f), 2).reshape(128, 1).copy()
    c["cvw"] = np.ascontiguousarray(np.asarray(inp["b_conv_w"][0], f).reshape(4, 24, 128).transpose(2, 1, 0))
    c["alog_bc"] = np.ascontiguousarray(np.broadcast_to(np.asarray(inp["b_a_log"][0], f)[None, :], (128, 8)))
    c["dtb_bc"] = np.ascontiguousarray(np.broadcast_to(np.asarray(inp["b_dt_bias"][0], f)[None, :], (128, 8)))
    c["onorm"] = np.asarray(inp["b_o_norm"][0], f).reshape(128, 1).copy()
    a = np.arange(128)
    same = (a[:, None] // 64) == (a[None, :] // 64)
    c["U2"] = (same & (a[:, None] <= a[None, :])).astype(f)
    c["Bsame"] = same.astype(f)
    c["Bsel0"] = np.ascontiguousarray(np.broadcast_to((a[:, None] < 64), (128, 128))).astype(f)
    c["Bsel1"] = np.ascontiguousarray(np.broadcast_to((a[:, None] >= 64), (128, 128))).astype(f)
    c["MLs"] = (same & (a[:, None] > a[None, :])).astype(f)
    c["MU"] = (same & (a[:, None] <= a[None, :])).astype(f)
    return c


def host_shared(inp, layers=(0, 1)):
    f = np.float32
    sh = host_consts(inp)
    if 0 in layers:
        w = np.asarray(inp["ab_w_in"][0], f)
        wp = np.zeros((D, 6144), f)
        wp[:, 0:832] = w[:, 0:832]
        wp[:, 896:912] = w[:, 832:848]
        wp[:, 912:928] = w[:, 4944:4960]
        wp[:, 1024:2048] = w[:, 848:1872]
        wp[:, 2048:5120] = w[:, 1872:4944]
        wp[:, 5120:6144] = w[:, 4960:5984]
        sh["ab_w_in"] = tile_w_in(wp, 48)
        sh["ab_w_out"] = np.ascontiguousarray(np.asarray(inp["ab_w_out"][0], f).reshape(16, 128, D))
        sh["w_uqiq"] = tile_w_in(np.concatenate([inp["a_w_uq"][0], inp["a_w_iq"][0]], axis=1).astype(f), 16)
        sh["w_uk"] = tile_w_in(np.asarray(inp["a_w_uk"][0], f), 8)
        sh["w_uv"] = np.ascontiguousarray(np.asarray(inp["a_w_uv"][0], f).reshape(2, 128, 1024))
    if 1 in layers:
        sh["cd_w_in"] = tile_w_in(np.asarray(inp["cd_w_in"][0], f), 56)
        sh["cd_w_out"] = np.ascontiguousarray(np.asarray(inp["cd_w_out"][0], f).reshape(16, 128, D))
    return sh


def kernel(**inputs):
    inp = {k: np.asarray(v) for k, v in inputs.items()}
    nc, kb = build_program()
    sh = host_shared(inp)
    x = np.ascontiguousarray(inp["x"], dtype=np.float32)
    in_maps = [dict(sh, x=x[b]) for b in range(8)]
    res = run_bass_kernel_spmd(nc, in_maps, core_ids=list(range(8)))
    return np.stack([np.asarray(r["out"], np.float32) for r in res.results], axis=0)
```

```python
from contextlib import ExitStack
import numpy as np
import concourse.bass as bass
import concourse.mybir as mybir
from concourse.bass_utils import run_bass_kernel_spmd

F32 = mybir.dt.float32
BF16 = mybir.dt.bfloat16
ALU = mybir.AluOpType
AF = mybir.ActivationFunctionType
AX = mybir.AxisListType

T = 4096
D = 2048
NT = T // 128
EPS = 1e-6
ENGS = ("pe", "act", "dve", "pool", "sp")


class Chan:
    def __init__(self, sem, name):
        self.sem = sem
        self.name = name
        self.n = 0


class KB:
    def __init__(self, nc):
        self.nc = nc
        self.es = ExitStack()
        self.q = {e: [] for e in ENGS}
        self.sems = {}
        self.cnt = {}
        self.seen = {e: {} for e in ENGS}
        self.lastw = {}
        self.readers = {}
        self.chans = []
        self.chan_by_sem = {}
        self.nins = 0
        self.pending = {e: False for e in ENGS}
        for e in ENGS:
            self.sems[e] = self.es.enter_context(nc.semaphore("s_" + e))
            self.cnt[e] = 0

    def sb(self, name, shape, dt, stack=None):
        return (stack or self.es).enter_context(self.nc.sbuf_tensor(name, list(shape), dt))

    def ps(self, name, shape, dt=F32, stack=None):
        return (stack or self.es).enter_context(self.nc.psum_tensor(name, list(shape), dt))

    def chan(self, name):
        c = Chan(self.es.enter_context(self.nc.semaphore("c_" + name)), name)
        self.chans.append(c)
        self.chan_by_sem[id(c.sem)] = c
        return c

    def _need0(self, eng, sem, val):
        ch = self.chan_by_sem.get(id(sem))
        if ch is not None:
            val = max(val, 16 * ch.n)
        cur = self.seen[eng].get(id(sem), 0)
        if val > cur:
            self.seen[eng][id(sem)] = val
            self.q[eng].append(("wait", sem, val))

    def _deps(self, eng, reads, writes, my_sem):
        for r in reads:
            ev = self.lastw.get(r)
            if ev is not None:
                self._need(eng, ev[0], ev[1])
        for w in writes:
            ev = self.lastw.get(w)
            if ev is not None:
                self._need(eng, ev[0], ev[1])
            rd = self.readers.get(w)
            if rd:
                for sem, val in rd.values():
                    if sem is my_sem:
                        continue
                    self._need(eng, sem, val)

    def _need(self, eng, sem, val):
        if eng == "pe" and sem is self.sems["pe"]:
            return
        self._need0(eng, sem, val)

    def _commit(self, ev, reads, writes):
        for w in writes:
            self.lastw[w] = ev
            self.readers[w] = {}
        for r in reads:
            d = self.readers.setdefault(r, {})
            d[id(ev[0])] = ev

    def op(self, eng, fn, reads=(), writes=(), signal=True):
        sem = self.sems[eng]
        self._deps(eng, reads, writes, sem)
        if signal:
            self.cnt[eng] += 1
            self.pending[eng] = False
            ev = (sem, self.cnt[eng])
            self.q[eng].append(("ins", fn, sem, 1))
        else:
            self.pending[eng] = True
            ev = (sem, self.cnt[eng] + 1)
            self.q[eng].append(("ins0", fn))
        self._commit(ev, reads, writes)
        self.nins += 1
        return ev

    def dma(self, eng, ch, out, in_, reads=(), writes=(), **kw):
        self._deps(eng, reads, writes, None)
        ch.n += 1
        ev = (ch.sem, 16 * ch.n)
        self.q[eng].append(("ins", lambda e, o=out, i=in_, k=kw: e.dma_start(out=o, in_=i, **k), ch.sem, 16))
        self._commit(ev, reads, writes)
        self.nins += 1
        return ev

    def _flush_pending(self):
        for e in ENGS:
            assert not self.pending[e], "non-signaling op left pending at a barrier on " + e

    def _all_events(self):
        self._flush_pending()
        evs = [(self.sems[e], self.cnt[e]) for e in ENGS if self.cnt[e] > 0]
        evs += [(c.sem, 16 * c.n) for c in self.chans if c.n > 0]
        return evs

    def barrier(self):
        evs = self._all_events()
        for e in ENGS:
            for sem, val in evs:
                self._need(e, sem, val)

    def finish(self, final_eng="sp"):
        for sem, val in self._all_events():
            self._need(final_eng, sem, val)

    def emit(self):
        nc = self.nc
        q = self.q
        self.q = {e: [] for e in ENGS}

        def replay(eng_obj, items):
            for it in items:
                if it[0] == "wait":
                    eng_obj.wait_ge(it[1], it[2])
                elif it[0] == "ins0":
                    it[1](eng_obj)
                else:
                    it[1](eng_obj).then_inc(it[2], it[3])

        with nc.Block() as block:
            @block.tensor
            def _(e):
                replay(e, q["pe"])

            @block.scalar
            def _(e):
                replay(e, q["act"])

            @block.vector
            def _(e):
                replay(e, q["dve"])

            @block.gpsimd
            def _(e):
                replay(e, q["pool"])

            @block.sync
            def _(e):
                replay(e, q["sp"])

    def end_phase(self):
        self.barrier()
        self.emit()

    def close(self):
        self.es.close()


def MM(kb, out, lhsT, rhs, start, stop, reads, writes, signal=None):
    if signal is None:
        signal = stop
    return kb.op("pe", lambda e: e.matmul(out, lhsT, rhs, start=start, stop=stop), reads, writes, signal=signal)


def TR(kb, out, in_, ident, reads, writes, signal=True):
    return kb.op("pe", lambda e: e.transpose(out, in_, ident), reads, writes, signal=signal)


def ACT(kb, out, in_, func, reads, writes, bias=None, scale=None, accum_out=None):
    kw = {}
    if bias is not None:
        kw["bias"] = bias
    if scale is not None:
        kw["scale"] = scale
    if accum_out is not None:
        kw["accum_out"] = accum_out
    return kb.op("act", lambda e: e.activation(out=out, in_=in_, func=func, **kw), reads, writes)


def TS(kb, eng, out, in0, s1, s2, op0, op1, reads, writes, accum_out=None):
    kw = {}
    if op1 is not None:
        kw["op1"] = op1
    if accum_out is not None:
        kw["accum_out"] = accum_out
    return kb.op(eng, lambda e: e.tensor_scalar(out=out, in0=in0, scalar1=s1, scalar2=s2, op0=op0, **kw),
                 reads, writes)


def TT(kb, eng, out, in0, in1, op, reads, writes):
    return kb.op(eng, lambda e: e.tensor_tensor(out=out, in0=in0, in1=in1, op=op), reads, writes)


def STT(kb, out, in0, scalar, in1, op0, op1, reads, writes):
    return kb.op("dve", lambda e: e.scalar_tensor_tensor(out=out, in0=in0, scalar=scalar, in1=in1,
                                                         op0=op0, op1=op1), reads, writes)


def CP(kb, eng, out, in_, reads, writes):
    if eng == "act":
        return kb.op("act", lambda e: e.copy(out=out, in_=in_), reads, writes)
    return kb.op(eng, lambda e: e.tensor_copy(out=out, in_=in_), reads, writes)


def MS(kb, eng, ap, val, writes):
    return kb.op(eng, lambda e: e.memset(ap, val), (), writes)


def RECIP(kb, out, in_, reads, writes):
    return kb.op("dve", lambda e: e.reciprocal(out=out, in_=in_), reads, writes)


def phase_norm_T(kb, C, x_dram, normwT, hT, tag):
    with ExitStack() as st:
        xt = [kb.sb(f"{tag}_xt{i}", [128, D], F32, st) for i in range(2)]
        xn = [kb.sb(f"{tag}_xn{i}", [128, D], F32, st) for i in range(2)]
        sq = kb.sb(f"{tag}_sq", [128, D], BF16, st)
        sm = [kb.sb(f"{tag}_sm{i}", [128, 4], F32, st) for i in range(2)]
        pst = [kb.ps(f"{tag}_pt{i}", [128, 8, 128], F32, st) for i in range(2)]
        ch = [kb.chan(f"{tag}_x{i}") for i in range(2)]
        for tt in range(NT):
            s = tt % 2
            kb.dma("sp", ch[s], xt[s][:], x_dram[tt * 128:(tt + 1) * 128, :], writes=[(tag, "xt", s)])
            ACT(kb, sq[:], xt[s][:], AF.Square, [(tag, "xt", s)], [(tag, "sq"), (tag, "ss", s)],
                accum_out=sm[s][:, 0:1])
            ACT(kb, sm[s][:, 1:2], sm[s][:, 0:1], AF.Sqrt, [(tag, "ss", s)], [(tag, "sd", s)],
                bias=C["eps"][:, 0:1], scale=1.0 / D)
            RECIP(kb, sm[s][:, 2:3], sm[s][:, 1:2], [(tag, "sd", s)], [(tag, "rs", s)])
            TS(kb, "dve", xn[s][:], xt[s][:], sm[s][:, 2:3], None, ALU.mult, None,
               [(tag, "xt", s), (tag, "rs", s)], [(tag, "xn", s)])
            for half in range(2):
                p = half
                for kk in range(8):
                    k = half * 8 + kk
                    TR(kb, pst[p][:, kk, :], xn[s][:, k * 128:(k + 1) * 128], C["ident"][:],
                       [(tag, "xn", s)], [(tag, "pt", p)], signal=(kk == 7))
                nw = normwT[:, half * 8:(half + 1) * 8].unsqueeze(2).to_broadcast([128, 8, 128])
                TT(kb, "dve", hT[:, half * 8:(half + 1) * 8, tt * 128:(tt + 1) * 128], pst[p][:], nw,
                   ALU.mult, [(tag, "pt", p)], [("hT", tt)])
        kb.end_phase()


def gemm_fm(kb, tag, actT, act_keys, KC, w_dram, NCH, sink):
    with ExitStack() as st:
        wb = [kb.sb(f"{tag}_wb{i}", [128, KC * 128], BF16, st) for i in range(2)]
        wch = [kb.chan(f"{tag}_w{i}") for i in range(2)]
        pss = [kb.ps(f"{tag}_ps{i}", [128, 1024], F32, st) for i in range(2)]
        it = 0
        for c in range(NCH):
            s = c % 2
            kb.dma("pool", wch[s], wb[s][:], w_dram[c], writes=[(tag, "wb", s)])
            for ts in range(4):
                p = it % 2
                it += 1
                for b in range(2):
                    t0 = ts * 1024 + b * 512
                    for k in range(KC):
                        MM(kb, pss[p][:, b * 512:(b + 1) * 512], wb[s][:, k * 128:(k + 1) * 128],
                           actT[:, k, t0:t0 + 512], k == 0, k == KC - 1,
                           [(tag, "wb", s)] + act_keys, [(tag, "ps", p)])
                sink(c, ts, pss[p], (tag, "ps", p), st)
        kb.end_phase()


class StoreSink:
    def __init__(self, kb, tag, dst, st):
        self.kb = kb
        self.tag = tag
        self.dst = dst
        self.stg = [kb.sb(f"{tag}_stg{i}", [128, 1024], F32, st) for i in range(3)]
        self.ch = [kb.chan(f"{tag}_st{i}") for i in range(3)]
        self.i = 0

    def __call__(self, c, ts, ps, pkey, st):
        kb = self.kb
        s = self.i % 3
        eng = "act" if self.i % 2 == 0 else "dve"
        self.i += 1
        CP(kb, eng, self.stg[s][:], ps[:], [pkey], [(self.tag, "stg", s)])
        kb.dma("sp", self.ch[s], self.dst[c * 128:(c + 1) * 128, ts * 1024:(ts + 1) * 1024], self.stg[s][:],
               reads=[(self.tag, "stg", s)], writes=[(self.tag, "dst", c)])


def phase_inproj(kb, C, tag, x_dram, normwT, w_dram, NCH, pj):
    with ExitStack() as st:
        hT = kb.sb(f"{tag}_hT", [128, 16, T], BF16, st)
        phase_norm_T(kb, C, x_dram, normwT, hT, tag + "n")
        with ExitStack() as st2:
            sink = StoreSink(kb, tag + "s", pj, st2)
            gemm_fm(kb, tag + "g", hT, [], 16, w_dram, NCH, sink)


def phase_outproj(kb, C, tag, yT_dram, wo_dram, xres_dram, out_dram):
    with ExitStack() as st:
        wo = kb.sb(f"{tag}_wo", [128, 16, D], BF16, st)
        wch = kb.chan(f"{tag}_w")
        for k in range(16):
            kb.dma("pool", wch, wo[:, k, :], wo_dram[k], writes=[(tag, "wo")])
        yt = [kb.sb(f"{tag}_yt{i}", [128, 16, 128], BF16, st) for i in range(2)]
        xr = [kb.sb(f"{tag}_xr{i}", [128, D], F32, st) for i in range(2)]
        ot = [kb.sb(f"{tag}_ot{i}", [128, D], F32, st) for i in range(2)]
        ps = [kb.ps(f"{tag}_ps{i}", [128, 512], F32, st) for i in range(4)]
        chy = [kb.chan(f"{tag}_y{i}") for i in range(2)]
        chx = [kb.chan(f"{tag}_x{i}") for i in range(2)]
        cho = [kb.chan(f"{tag}_o{i}") for i in range(2)]
        yv = yT_dram.rearrange("(k p) t -> p k t", p=128)
        for tt in range(NT):
            s = tt % 2
            kb.dma("sp", chy[s], yt[s][:], yv[:, :, tt * 128:(tt + 1) * 128], writes=[(tag, "yt", s)])
            kb.dma("sp", chx[s], xr[s][:], xres_dram[tt * 128:(tt + 1) * 128, :], writes=[(tag, "xr", s)])
            for nb in range(4):
                for k in range(16):
                    MM(kb, ps[nb][:], yt[s][:, k, :], wo[:, k, nb * 512:(nb + 1) * 512], k == 0, k == 15,
                       [(tag, "yt", s), (tag, "wo")], [(tag, "ps", nb)])
                TT(kb, "dve", ot[s][:, nb * 512:(nb + 1) * 512], ps[nb][:], xr[s][:, nb * 512:(nb + 1) * 512],
                   ALU.add, [(tag, "ps", nb), (tag, "xr", s)], [(tag, "ot", s)])
            kb.dma("pool", cho[s], out_dram[tt * 128:(tt + 1) * 128, :], ot[s][:],
                   reads=[(tag, "ot", s)], writes=[(tag, "out", tt)])
        kb.end_phase()


def phase_cd_mix(kb, C, pj, cw_unused, y_dram):
    with ExitStack() as stc:
        cw = {}
        chc = kb.chan("cd_const")
        for name, shape, dt in CD_CONST_SPECS:
            d = kb.nc.dram_tensor(name, list(shape), dt, kind="ExternalInput").ap()
            t = kb.sb("c_" + name, shape, dt, stc)
            kb.dma("sp", chc, t[:], d, writes=[("const", name)])
            cw[name] = t
        kb.end_phase()
        _phase_cd_mix(kb, C, pj, cw, y_dram)


def _phase_cd_mix(kb, C, pj, cw, y_dram):
    HB = 2048
    tag = "cd"
    with ExitStack() as st:
        QB = 1024
        ident_b = kb.sb("cd_identb", [128, 128], BF16, st)
        dg = kb.sb("cd_dg", [128, 8, 31, 128], BF16, st)
        uc = kb.sb("cd_uc", [128, 8, QB], F32, st)
        ab = [kb.sb(f"cd_a{i}", [128, 30 + QB], F32, st) for i in range(2)]
        gb = [kb.sb(f"cd_g{i}", [128, 30 + QB], F32, st) for i in range(2)]
        ub = [kb.sb(f"cd_u{i}", [128, 30 + QB], BF16, st) for i in range(2)]
        sq = [kb.sb(f"cd_sq{i}", [128, QB], F32, st) for i in range(2)]
        mean = kb.sb("cd_mean", [128, QB], F32, st)
        rstd = kb.sb("cd_rstd", [128, QB], F32, st)
        m2 = kb.sb("cd_m2", [128, QB], F32, st)
        zb = [kb.sb(f"cd_z{i}", [128, QB], F32, st) for i in range(2)]
        yb = [kb.sb(f"cd_y{i}", [128, QB], BF16, st) for i in range(2)]
        pc = [kb.ps(f"cd_pc{i}", [128, QB], F32, st) for i in range(2)]
        ps_sum = kb.ps("cd_pss", [128, QB], F32, st)
        ps_ssq = kb.ps("cd_psq", [128, QB], F32, st)
        cha = [kb.chan(f"cd_a{i}") for i in range(2)]
        chg = [kb.chan(f"cd_g{i}") for i in range(2)]
        chz = [kb.chan(f"cd_z{i}") for i in range(2)]
        chy = [kb.chan(f"cd_y{i}") for i in range(2)]
        CP(kb, "dve", ident_b[:], C["ident"][:], [], [(tag, "cst")])
        for cc in range(8):
            TT(kb, "dve", dg[:, cc, :, :], ident_b[:].unsqueeze(1).to_broadcast([128, 31, 128]),
               cw["dww"][:, cc, :].unsqueeze(2).to_broadcast([128, 31, 128]), ALU.mult, [(tag, "cst")], [(tag, "dg")])
        it = 0
        for tq in range(T // QB):
            t0 = tq * QB
            for cc in range(8):
                s = it % 2
                it += 1
                r0 = cc * 128
                if tq == 0:
                    MS(kb, "pool", ab[s][:, 0:30], 0.0, [(tag, "a", s)])
                    MS(kb, "pool", gb[s][:, 0:30], 0.0, [(tag, "g", s)])
                    kb.dma("sp", cha[s], ab[s][:, 30:30 + QB], pj[r0:r0 + 128, 0:QB], writes=[(tag, "a", s)])
                    kb.dma("sp", chg[s], gb[s][:, 30:30 + QB], pj[1024 + r0:1024 + r0 + 128, 0:QB],
                           writes=[(tag, "g", s)])
                else:
                    kb.dma("sp", cha[s], ab[s][:], pj[r0:r0 + 128, t0 - 30:t0 + QB], writes=[(tag, "a", s)])
                    kb.dma("sp", chg[s], gb[s][:], pj[1024 + r0:1024 + r0 + 128, t0 - 30:t0 + QB],
                           writes=[(tag, "g", s)])
                ACT(kb, gb[s][:], gb[s][:], AF.Sigmoid, [(tag, "g", s)], [(tag, "g", s)])
                TT(kb, "dve", ub[s][:], ab[s][:], gb[s][:], ALU.mult, [(tag, "a", s), (tag, "g", s)], [(tag, "u", s)])
                for b in range(QB // 512):
                    for j in range(31):
                        MM(kb, pc[s][:, b * 512:(b + 1) * 512], dg[:, cc, j, :], ub[s][:, j + b * 512:j + b * 512 + 512],
                           j == 0, j == 30, [(tag, "dg"), (tag, "u", s)], [(tag, "pc", s)])
                ACT(kb, uc[:, cc, :], pc[s][:], AF.Identity, [(tag, "pc", s)], [(tag, "uc", cc)],
                    bias=cw["dwb"][:, cc:cc + 1])
                ACT(kb, sq[s][:], uc[:, cc, :], AF.Square, [(tag, "uc", cc)], [(tag, "sq", s)])
                for tb in range(QB // 512):
                    last = tb == QB // 512 - 1
                    MM(kb, ps_sum[:, tb * 512:(tb + 1) * 512], C["ones_f"][:], uc[:, cc, tb * 512:(tb + 1) * 512],
                       cc == 0, cc == 7, [(tag, "uc", cc)], [(tag, "pss")], signal=last)
                    MM(kb, ps_ssq[:, tb * 512:(tb + 1) * 512], C["ones_f"][:], sq[s][:, tb * 512:(tb + 1) * 512],
                       cc == 0, cc == 7, [(tag, "sq", s)], [(tag, "psq")], signal=last)
            ACT(kb, mean[:], ps_sum[:], AF.Copy, [(tag, "pss")], [(tag, "mean")], scale=1.0 / 1024)
            TT(kb, "dve", m2[:], mean[:], mean[:], ALU.mult, [(tag, "mean")], [(tag, "m2")])
            STT(kb, m2[:], ps_ssq[:], 1.0 / 1024, m2[:], ALU.mult, ALU.subtract, [(tag, "psq"), (tag, "m2")],
                [(tag, "m2")])
            ACT(kb, m2[:], m2[:], AF.Sqrt, [(tag, "m2")], [(tag, "m2")], bias=C["eps"][:, 0:1])
            RECIP(kb, rstd[:], m2[:], [(tag, "m2")], [(tag, "rstd")])
            for cc in range(8):
                s = cc % 2
                r0 = cc * 128
                kb.dma("sp", chz[s], zb[s][:], pj[2048 + r0:2048 + r0 + 128, t0:t0 + QB], writes=[(tag, "z", s)])
                TT(kb, "dve", uc[:, cc, :], uc[:, cc, :], mean[:], ALU.subtract, [(tag, "uc", cc), (tag, "mean")],
                   [(tag, "uc", cc)])
                TT(kb, "dve", uc[:, cc, :], uc[:, cc, :], rstd[:], ALU.mult, [(tag, "uc", cc), (tag, "rstd")],
                   [(tag, "uc", cc)])
                ACT(kb, uc[:, cc, :], uc[:, cc, :], AF.Silu, [(tag, "uc", cc)], [(tag, "uc", cc)],
                    scale=cw["lnw"][:, cc:cc + 1], bias=cw["lnb"][:, cc:cc + 1])
                ACT(kb, zb[s][:], zb[s][:], AF.Silu, [(tag, "z", s)], [(tag, "z", s)])
                TT(kb, "dve", yb[s][:], uc[:, cc, :], zb[s][:], ALU.mult, [(tag, "uc", cc), (tag, "z", s)],
                   [(tag, "y", s)])
                kb.dma("pool", chy[s], y_dram[r0:r0 + 128, t0:t0 + QB], yb[s][:], reads=[(tag, "y", s)],
                       writes=[(tag, "yd", cc, tq)])
        kb.end_phase()
    tag = "sc"
    with ExitStack() as st:
        W = T + 2
        bg = [kb.sb(f"sc_b{i}", [128, T], F32, st) for i in range(2)]
        cg = [kb.sb(f"sc_c{i}", [128, W], F32, st) for i in range(2)]
        ud = [kb.sb(f"sc_u{i}", [128, W], F32, st) for i in range(2)]
        zd = [kb.sb(f"sc_z{i}", [128, T], F32, st) for i in range(2)]
        acc = [kb.sb(f"sc_acc{i}", [128, T], F32, st) for i in range(2)]
        yb = [kb.sb(f"sc_y{i}", [128, T], BF16, st) for i in range(2)]
        chs = {n: [kb.chan(f"sc_{n}{i}") for i in range(2)] for n in ("b", "c", "u", "z", "y")}
        for cc in range(8):
            s = cc % 2
            r0 = cc * 128
            MS(kb, "pool", cg[s][:, 0:2], 0.0, [(tag, "c", s)])
            MS(kb, "pool", ud[s][:, 0:2], 0.0, [(tag, "u", s)])
            kb.dma("sp", chs["b"][s], bg[s][:], pj[3072 + r0:3072 + r0 + 128, :], writes=[(tag, "b", s)])
            kb.dma("sp", chs["c"][s], cg[s][:, 2:W], pj[4096 + r0:4096 + r0 + 128, :], writes=[(tag, "c", s)])
            kb.dma("sp", chs["u"][s], ud[s][:, 2:W], pj[5120 + r0:5120 + r0 + 128, :], writes=[(tag, "u", s)])
            kb.dma("sp", chs["z"][s], zd[s][:], pj[6144 + r0:6144 + r0 + 128, :], writes=[(tag, "z", s)])
            TT(kb, "dve", cg[s][:], cg[s][:], ud[s][:], ALU.mult, [(tag, "c", s), (tag, "u", s)], [(tag, "c", s)])
            TS(kb, "dve", acc[s][:], cg[s][:, 2:W], cw["dcw"][:, cc, 2:3], None, ALU.mult, None,
               [(tag, "c", s)], [(tag, "acc", s)])
            for j in range(2):
                STT(kb, acc[s][:], cg[s][:, j:j + T], cw["dcw"][:, cc, j:j + 1], acc[s][:], ALU.mult, ALU.add,
                    [(tag, "c", s), (tag, "acc", s)], [(tag, "acc", s)])
            ACT(kb, zd[s][:], zd[s][:], AF.Silu, [(tag, "z", s)], [(tag, "z", s)])
            TT(kb, "pool", bg[s][:], bg[s][:], zd[s][:], ALU.mult, [(tag, "b", s), (tag, "z", s)], [(tag, "b", s)])
            TT(kb, "dve", yb[s][:], acc[s][:], bg[s][:], ALU.mult, [(tag, "acc", s), (tag, "b", s)], [(tag, "y", s)])
            kb.dma("pool", chs["y"][s], y_dram[1024 + r0:1024 + r0 + 128, :], yb[s][:], reads=[(tag, "y", s)],
                   writes=[(tag, "yd", cc)])
        kb.end_phase()


def colnorm_phase(kb, C, tag, src, KC, gT, outT):
    with ExitStack() as st:
        raw = [kb.sb(f"{tag}_raw{i}", [128, KC, 1024], F32, st) for i in range(2)]
        sq = [kb.sb(f"{tag}_sq{i}", [128, 1024], F32, st) for i in range(2)]
        rs = kb.sb(f"{tag}_rs", [128, 1024], F32, st)
        ssp = kb.ps(f"{tag}_ssp", [128, 1024], F32, st)
        ch = [kb.chan(f"{tag}_l{i}") for i in range(2)]
        sv = src.rearrange("(k p) t -> p k t", p=128)
        for ts in range(4):
            s = ts % 2
            kb.dma("sp", ch[s], raw[s][:], sv[:, :, ts * 1024:(ts + 1) * 1024], writes=[(tag, "raw", s)])
            for k in range(KC):
                q = k % 2
                ACT(kb, sq[q][:], raw[s][:, k, :], AF.Square, [(tag, "raw", s)], [(tag, "sq", q)])
                for b in range(2):
                    MM(kb, ssp[:, b * 512:(b + 1) * 512], C["ones_f"][:], sq[q][:, b * 512:(b + 1) * 512],
                       k == 0, k == KC - 1, [(tag, "sq", q)], [(tag, "ssp")], signal=True)
            ACT(kb, rs[:], ssp[:], AF.Sqrt, [(tag, "ssp")], [(tag, "rs")], bias=C["eps"][:, 0:1],
                scale=1.0 / (KC * 128))
            RECIP(kb, rs[:], rs[:], [(tag, "rs")], [(tag, "rs")])
            for k in range(KC):
                STT(kb, outT[:, k, ts * 1024:(ts + 1) * 1024], raw[s][:, k, :], gT[:, k:k + 1], rs[:],
                    ALU.mult, ALU.mult, [(tag, "raw", s), (tag, "rs")], [(tag, "out", k, ts)])
        kb.end_phase()


class HeadNormSink:
    def __init__(self, kb, C, tag, dst, gain, n_norm, raw_dst, st):
        self.kb, self.C, self.tag, self.dst, self.gain = kb, C, tag, dst, gain
        self.n_norm, self.raw_dst = n_norm, raw_dst
        self.sq = [kb.sb(f"{tag}_sq{i}", [128, 1024], F32, st) for i in range(2)]
        self.rs = [kb.sb(f"{tag}_rs{i}", [128, 1024], F32, st) for i in range(2)]
        self.ob = [kb.sb(f"{tag}_ob{i}", [128, 1024], BF16, st) for i in range(2)]
        self.ssp = [kb.ps(f"{tag}_ssp{i}", [128, 1024], F32, st) for i in range(2)]
        self.ch = [kb.chan(f"{tag}_o{i}") for i in range(2)]
        self.i = 0

    def __call__(self, c, ts, ps, pkey, st):
        kb, C, tag = self.kb, self.C, self.tag
        s = self.i % 2
        self.i += 1
        if c < self.n_norm:
            ACT(kb, self.sq[s][:], ps[:], AF.Square, [pkey], [(tag, "sq", s)])
            for b in range(2):
                MM(kb, self.ssp[s][:, b * 512:(b + 1) * 512], C["ones_f"][:], self.sq[s][:, b * 512:(b + 1) * 512],
                   True, True, [(tag, "sq", s)], [(tag, "ssp", s)], signal=(b == 1))
            ACT(kb, self.rs[s][:], self.ssp[s][:], AF.Sqrt, [(tag, "ssp", s)], [(tag, "rs", s)], bias=C["eps"][:, 0:1],
                scale=1.0 / 128)
            RECIP(kb, self.rs[s][:], self.rs[s][:], [(tag, "rs", s)], [(tag, "rs", s)])
            STT(kb, self.ob[s][:], ps[:], self.gain, self.rs[s][:], ALU.mult, ALU.mult,
                [pkey, (tag, "rs", s)], [(tag, "ob", s)])
            d = self.dst[c]
        else:
            CP(kb, "act", self.ob[s][:], ps[:], [pkey], [(tag, "ob", s)])
            d = self.raw_dst[c - self.n_norm]
        kb.dma("sp", self.ch[s], d[:, ts * 1024:(ts + 1) * 1024], self.ob[s][:], reads=[(tag, "ob", s)],
               writes=[(tag, "dst", c)])


def phase_dsa_prep(kb, C, pj, W, S):
    with ExitStack() as st:
        cqn = kb.sb("cqn", [128, 4, T], BF16, st)
        colnorm_phase(kb, C, "cq", pj[0:512, :], 4, C["qnormT"], cqn)
        with ExitStack() as st2:
            sink = HeadNormSink(kb, C, "qs", S["qT"], C["qgain"][:, 0:1], 8, S["qiT"], st2)
            gemm_fm(kb, "qg", cqn, [], 4, W["w_uqiq"], 16, sink)
    with ExitStack() as st:
        ckvn = kb.sb("ckvn", [128, 2, T], BF16, st)
        colnorm_phase(kb, C, "ckv", pj[512:768, :], 2, C["kvnormT"], ckvn)
        with ExitStack() as st2:
            sink = HeadNormSink(kb, C, "ks", S["kT"], C["kgain"][:, 0:1], 8, None, st2)
            gemm_fm(kb, "kg", ckvn, [], 2, W["w_uk"], 8, sink)
        with ExitStack() as st2:
            wv = kb.sb("wv", [128, 2, 1024], BF16, st2)
            chw = kb.chan("wv")
            for k in range(2):
                kb.dma("pool", chw, wv[:, k, :], W["w_uv"][k], writes=[("wv",)])
            vps = [kb.ps(f"v_ps{i}", [128, 1024], F32, st2) for i in range(2)]
            vb = [kb.sb(f"v_b{i}", [128, 1024], BF16, st2) for i in range(2)]
            chv = [kb.chan(f"v_o{i}") for i in range(2)]
            for tt in range(NT):
                s = tt % 2
                for b in range(2):
                    for k in range(2):
                        MM(kb, vps[s][:, b * 512:(b + 1) * 512], ckvn[:, k, tt * 128:(tt + 1) * 128],
                           wv[:, k, b * 512:(b + 1) * 512], k == 0, k == 1, [("wv",)], [("v", "ps", s)])
                CP(kb, "act" if tt % 2 else "dve", vb[s][:], vps[s][:], [("v", "ps", s)], [("v", "b", s)])
                kb.dma("sp", chv[s], S["V"][tt * 128:(tt + 1) * 128, :], vb[s][:], reads=[("v", "b", s)],
                       writes=[("V", tt)])
            kb.end_phase()


def phase_ki_tmaj(kb, C, pj, S):
    with ExitStack() as st:
        ki = kb.sb("ki_raw", [128, T], F32, st)
        sq = kb.sb("ki_sq", [128, T], F32, st)
        mean = kb.sb("ki_mean", [128, 1024], F32, st)
        var = kb.sb("ki_var", [128, 1024], F32, st)
        kio = kb.sb("ki_o", [128, T], BF16, st)
        sm = kb.sb("ki_sm", [32, T], F32, st)
        tmo = kb.sb("ki_tmo", [128, NT, 32], F32, st)
        ps1 = kb.ps("ki_ps1", [128, 1024], F32, st)
        ps2 = kb.ps("ki_ps2", [128, 1024], F32, st)
        pst = kb.ps("ki_pst", [128, 16, 32], F32, st)
        ch = kb.chan("ki")
        kb.dma("sp", ch, ki[0:64, :], pj[768:832, :], writes=[("ki", "raw")])
        kb.dma("sp", ch, ki[64:128, :], pj[768:832, :], writes=[("ki", "raw")])
        kb.dma("sp", ch, sm[:], pj[896:928, :], writes=[("ki", "sm")])
        ACT(kb, sq[:], ki[:], AF.Square, [("ki", "raw")], [("ki", "sq")])
        for ts in range(4):
            for b in range(2):
                c0 = ts * 1024 + b * 512
                MM(kb, ps1[:, b * 512:(b + 1) * 512], C["ones_f"][0:64, :], ki[0:64, c0:c0 + 512], True, True,
                   [("ki", "raw")], [("ki", "ps1")])
                MM(kb, ps2[:, b * 512:(b + 1) * 512], C["ones_f"][0:64, :], sq[0:64, c0:c0 + 512], True, True,
                   [("ki", "sq")], [("ki", "ps2")])
            ACT(kb, mean[:], ps1[:], AF.Copy, [("ki", "ps1")], [("ki", "mean")], scale=1.0 / 64)
            TT(kb, "dve", var[:], mean[:], mean[:], ALU.mult, [("ki", "mean")], [("ki", "var")])
            STT(kb, var[:], ps2[:], 1.0 / 64, var[:], ALU.mult, ALU.subtract, [("ki", "ps2"), ("ki", "var")],
                [("ki", "var")])
            ACT(kb, var[:], var[:], AF.Sqrt, [("ki", "var")], [("ki", "var")], bias=C["eps"][:, 0:1])
            RECIP(kb, var[:], var[:], [("ki", "var")], [("ki", "var")])
            sl = slice(ts * 1024, (ts + 1) * 1024)
            TT(kb, "dve", ki[:, sl], ki[:, sl], mean[:], ALU.subtract, [("ki", "raw"), ("ki", "mean")], [("ki", "raw")])
            TT(kb, "dve", ki[:, sl], ki[:, sl], var[:], ALU.mult, [("ki", "raw"), ("ki", "var")], [("ki", "raw")])
            TS(kb, "dve", kio[:, sl], ki[:, sl], C["ikw"][:, 0:1], C["ikb"][:, 0:1], ALU.mult, ALU.add,
               [("ki", "raw")], [("ki", "o")])
        kb.dma("sp", ch, S["kiT"], kio[:], reads=[("ki", "o")], writes=[("kiT",)])
        for g in range(2):
            for tt in range(16):
                t = g * 16 + tt
                TR(kb, pst[:, tt, :], sm[:, t * 128:(t + 1) * 128], C["ident"][0:32, 0:32], [("ki", "sm")],
                   [("ki", "pst")], signal=(tt == 15))
            CP(kb, "dve", tmo[:, g * 16:(g + 1) * 16, :], pst[:], [("ki", "pst")], [("ki", "tmo")])
        kb.dma("sp", ch, S["tmaj"].rearrange("(n p) c -> p n c", p=128), tmo[:], reads=[("ki", "tmo")],
               writes=[("tmaj",)])
        kb.end_phase()


N_BIS = 13
SCALE_A = 128 ** -0.5


def phase_dsa_attn(kb, C, pj, S, y_dram):
    tag = "at"
    with ExitStack() as st:
        kT = kb.sb("at_kT", [128, 8, T], BF16, st)
        Vt = kb.sb("at_V", [128, NT, 1024], BF16, st)
        kiT = kb.sb("at_kiT", [128, T], BF16, st)
        score1 = kb.sb("at_sc", [128, T], F32, st)
        score = [score1, score1]
        mask = kb.sb("at_mask", [128, T], BF16, st)
        junk = mask
        maskT1 = kb.sb("at_maskT", [128, NT, 128], BF16, st)
        maskT = [maskT1, maskT1]
        rbuf = [kb.sb(f"at_r{i}", [128, 2, 512], BF16, st) for i in range(2)]
        dsg = kb.sb("at_dsg", [128, 16, 128], BF16, st)
        qb = [kb.sb(f"at_q{i}", [128, 8, 128], BF16, st) for i in range(2)]
        qib1 = kb.sb("at_qi", [128, 8, 128], BF16, st)
        qib = [qib1, qib1]
        wt = [kb.sb(f"at_w{i}", [128, 16], F32, st) for i in range(2)]
        bs = [kb.sb(f"at_bs{i}", [128, 8], F32, st) for i in range(2)]
        wks1 = kb.sb("at_wk", [128, N_BIS], F32, st)
        wks = [wks1, wks1]
        fpow = kb.sb("at_fpow", [128, N_BIS], F32, st)
        za1 = kb.sb("at_za", [128, 8, 128], F32, st)
        za = [za1, za1]
        pt = [kb.sb(f"at_pt{i}", [128, 4, 128], BF16, st) for i in range(2)]
        ost1 = kb.sb("at_ost", [128, 8, 256], F32, st)
        ost = [ost1, ost1]
        yst1 = kb.sb("at_yst", [128, 8, 128], BF16, st)
        yst = [yst1, yst1]
        biasS = C["biasS"]
        ident_b = kb.sb("at_identb", [128, 128], BF16, st)
        ones_b = kb.sb("at_onesb", [128, 128], BF16, st)
        ips = [kb.ps(f"at_ips{i}", [128, 2, 512], F32, st) for i in range(2)]
        lg = [ips[i][:, 0, :].rearrange("p (a b) -> p a b", b=128) for i in range(2)]
        po1 = kb.ps("at_po", [128, 128], F32, st)
        prs1 = kb.ps("at_prs", [128, 128], F32, st)
        po, prs = [po1, po1], [prs1, prs1]
        sps1 = kb.ps("at_sps", [128, 512], F32, st)
        sps = [sps1, sps1]
        tps = ips[0][:, 0, :].bitcast(BF16)[:, 0:512].rearrange("p (a b) -> p a b", b=128)
        chl = kb.chan("at_ld")
        chq = [kb.chan(f"at_q{i}") for i in range(2)]
        chy = [kb.chan(f"at_y{i}") for i in range(2)]
        chz = kb.chan("at_z")
        kb.dma("sp", chl, kT[:], S["kT"].rearrange("h p t -> p h t"), writes=[(tag, "kT")])
        kb.dma("sp", chl, Vt[:], S["V"].rearrange("(n p) c -> p n c", p=128), writes=[(tag, "V")])
        kb.dma("sp", chl, kiT[:], S["kiT"], writes=[(tag, "kiT")])
        CP(kb, "dve", ident_b[:], C["ident"][:], [], [(tag, "cst")])
        CP(kb, "dve", ones_b[:], C["ones_f"][:], [], [(tag, "cst")])
        for k in range(1, N_BIS + 1):
            MS(kb, "dve", fpow[:, k - 1:k], 2.0 ** -k, [(tag, "cst")])
        qTv = S["qT"].rearrange("h p t -> p h t")
        qiTv = S["qiT"].rearrange("h p t -> p h t")
        zav = pj[1024:2048, :].rearrange("(h p) t -> p h t", p=128)
        yv = y_dram[0:1024, :].rearrange("(h p) t -> p h t", p=128)
        cnt = {"ips": 0, "r": 0, "lg": 0, "pt": 0, "po": 0, "sps": 0}

        def stage_a(i):
            s = i % 2
            c0, c1 = i * 128, (i + 1) * 128
            kb.dma("sp", chq[s], qb[s][:], qTv[:, :, c0:c1], writes=[(tag, "q", s)])
            kb.dma("sp", chq[s], qib[s][:], qiTv[:, :, c0:c1], writes=[(tag, "qi")])
            kb.dma("sp", chq[s], wt[s][:], S["tmaj"][c0:c1, 0:16], writes=[(tag, "w", s)])
            nW = (i + 4) // 4
            Wi = nW * 512
            sk = (tag, "score")
            TT(kb, "dve", dsg[:], ident_b[:].unsqueeze(1).to_broadcast([128, 16, 128]),
               wt[s][:, 0:16].unsqueeze(2).to_broadcast([128, 16, 128]), ALU.mult, [(tag, "w", s), (tag, "cst")],
               [(tag, "dsg")])
            for w in range(nW):
                sp_ = cnt["sps"] % 2
                cnt["sps"] += 1
                units = []

                def acc(u):
                    h0, r_ = u
                    for e_ in range(2):
                        MM(kb, sps[sp_][:], dsg[:, h0 + e_, :], rbuf[r_][:, e_, :], h0 + e_ == 0, h0 + e_ == 15,
                           [(tag, "dsg"), (tag, "r", r_, e_)], [(tag, "sps", 0)], signal=(e_ == 1))

                for pair in range(8):
                    p = cnt["ips"] % 2
                    cnt["ips"] += 1
                    r = cnt["r"] % 2
                    cnt["r"] += 1
                    for e_ in range(2):
                        base = e_ * 64
                        MM(kb, ips[p][:, e_, :], qib[s][base:base + 64, pair, :],
                           kiT[base:base + 64, w * 512:(w + 1) * 512], True, True, [(tag, "qi"), (tag, "kiT")],
                           [(tag, "ips", p)], signal=(e_ == 1))
                    ACT(kb, rbuf[r][:, 0, :], ips[p][:, 0, :], AF.Relu, [(tag, "ips", p)], [(tag, "r", r, 0)])
                    if pair % 2 == 0:
                        TS(kb, "dve", rbuf[r][:, 1, :], ips[p][:, 1, :], 0.0, None, ALU.max, None, [(tag, "ips", p)],
                           [(tag, "r", r, 1)])
                    else:
                        ACT(kb, rbuf[r][:, 1, :], ips[p][:, 1, :], AF.Relu, [(tag, "ips", p)], [(tag, "r", r, 1)])
                    if units:
                        acc(units.pop())
                    units.append((2 * pair, r))
                acc(units.pop())
                CP(kb, "dve", score[s][:, w * 512:(w + 1) * 512], sps[sp_][:], [(tag, "sps", 0)], [sk])
            b = bs[s]
            bk = (tag, "bs", s)
            TS(kb, "dve", junk[:, 0:Wi], score[s][:, 0:Wi], 1.0, None, ALU.mult, ALU.max, [sk], [(tag, "mask"), bk],
               accum_out=b[:, 0:1])
            TS(kb, "dve", junk[:, 0:Wi], score[s][:, 0:Wi], -1.0, None, ALU.mult, ALU.max, [sk], [(tag, "mask"), bk],
               accum_out=b[:, 6:7])
            TT(kb, "dve", b[:, 0:1], b[:, 0:1], b[:, 6:7], ALU.max, [bk], [bk])
            TT(kb, "dve", score[s][:, Wi - 512:Wi], score[s][:, Wi - 512:Wi],
               C["cbase"][:, 384 - (i % 4) * 128:896 - (i % 4) * 128], ALU.add, [sk], [sk])
            TS(kb, "dve", b[:, 1:2], b[:, 0:1], -1.001, -1e-20, ALU.mult, ALU.add, [bk], [bk])
            TS(kb, "dve", b[:, 2:3], b[:, 0:1], 2.002, 2e-20, ALU.mult, ALU.add, [bk], [bk])
            wk = wks[s]
            TS(kb, "dve", wk[:], fpow[:], b[:, 2:3], None, ALU.mult, None, [bk, (tag, "cst")], [bk])
            TT(kb, "dve", b[:, 3:4], b[:, 1:2], wk[:, 0:1], ALU.add, [bk], [bk])
            for k in range(1, N_BIS + 1):
                TS(kb, "dve", junk[:, 0:Wi], score[s][:, 0:Wi], b[:, 3:4], None, ALU.is_ge, ALU.add,
                   [sk, bk], [(tag, "mask"), bk], accum_out=b[:, 4:5])
                TS(kb, "dve", b[:, 5:6], b[:, 4:5], 255.5, -0.5, ALU.is_ge, ALU.add, [bk], [bk])
                STT(kb, b[:, 3:4], b[:, 5:6], wk[:, k - 1:k], b[:, 3:4], ALU.mult, ALU.add, [bk], [bk])
            STT(kb, b[:, 1:2], wk[:, N_BIS - 1:N_BIS], -0.5, b[:, 3:4], ALU.mult, ALU.add, [bk], [bk])

        def stage_t(i):
            s = i % 2
            n = i + 1
            TS(kb, "dve", mask[:, 0:n * 128], score[s][:, 0:n * 128], bs[s][:, 1:2], None, ALU.is_ge, None,
               [(tag, "score"), (tag, "bs", s)], [(tag, "mask")])
            for j0 in range(0, n, 4):
                nb = min(4, n - j0)
                for jj in range(nb):
                    j = j0 + jj
                    TR(kb, tps[:, jj, :], mask[:, j * 128:(j + 1) * 128], ident_b[:], [(tag, "mask"), (tag, "cst")],
                       [(tag, "ips", 0)], signal=(jj == nb - 1))
                ACT(kb, maskT[s][:, j0:j0 + nb, :], tps[:, 0:nb, :], AF.Copy, [(tag, "ips", 0)], [(tag, "maskT")],
                    scale=30000.0, bias=-30000.0)

        def stage_b(i):
            s = i % 2
            n = i + 1
            groups = [(h, j0, min(4, n - j0)) for h in range(8) for j0 in range(0, n, 4)]
            slots = {}

            def qk(gi):
                h, j0, nb = groups[gi]
                p = cnt["lg"] % 2
                cnt["lg"] += 1
                x = cnt["pt"] % 2
                cnt["pt"] += 1
                slots[gi] = x
                for jj in range(nb):
                    j = j0 + jj
                    near = (i - j) <= 1
                    MM(kb, lg[p][:, jj, :], kT[:, h, j * 128:(j + 1) * 128], qb[s][:, h, :], True, False,
                       [(tag, "kT"), (tag, "q", s)], [(tag, "ips", p)], signal=False)
                    if near:
                        MM(kb, lg[p][:, jj, :], ident_b[:], biasS[:, h, i - j, :], False, False,
                           [(tag, "cst")], [(tag, "ips", p)], signal=False)
                    MM(kb, lg[p][:, jj, :], ident_b[:], maskT[s][:, j, :], False, True,
                       [(tag, "cst"), (tag, "maskT")], [(tag, "ips", p)], signal=(jj == nb - 1))
                ACT(kb, pt[x][:, 0:nb, :], lg[p][:, 0:nb, :], AF.Exp, [(tag, "ips", p)], [(tag, "pt", x)],
                    scale=SCALE_A, bias=C["cb"][:, h:h + 1])

            def pv(gi):
                h, j0, nb = groups[gi]
                x = slots.pop(gi)
                for jj in range(nb):
                    j = j0 + jj
                    MM(kb, po[0][:], Vt[:, j, h * 128:(h + 1) * 128], pt[x][:, jj, :], j == 0, j == n - 1,
                       [(tag, "V"), (tag, "pt", x)], [(tag, "po", 0)])
                    MM(kb, prs[0][:], ones_b[:], pt[x][:, jj, :], j == 0, j == n - 1,
                       [(tag, "cst"), (tag, "pt", x)], [(tag, "prs", 0)], signal=(jj == nb - 1))
                if j0 + nb == n:
                    CP(kb, "act", ost[s][:, h, 0:128], po[0][:], [(tag, "po", 0)], [(tag, "ost", h)])
                    CP(kb, "act", ost[s][:, h, 128:256], prs[0][:], [(tag, "prs", 0)], [(tag, "ost", h)])

            qk(0)
            for gi in range(len(groups)):
                if gi + 1 < len(groups):
                    qk(gi + 1)
                pv(gi)

        def stage_f(i):
            s = i % 2
            keys = [(tag, "ost", h) for h in range(8)]
            kb.dma("sp", chz, za[s][:], zav[:, :, i * 128:(i + 1) * 128], writes=[(tag, "za")])
            ACT(kb, za[s][:], za[s][:], AF.Silu, [(tag, "za")], [(tag, "za")])
            RECIP(kb, ost[s][:, :, 128:256], ost[s][:, :, 128:256], keys, keys)
            TT(kb, "dve", ost[s][:, :, 0:128], ost[s][:, :, 0:128], ost[s][:, :, 128:256], ALU.mult, keys, keys)
            TT(kb, "dve", yst[s][:], ost[s][:, :, 0:128], za[s][:], ALU.mult, keys + [(tag, "za")], [(tag, "yst")])
            kb.dma("pool", chy[s], yv[:, :, i * 128:(i + 1) * 128], yst[s][:], reads=[(tag, "yst")],
                   writes=[(tag, "y", i)])

        stage_a(0)
        stage_t(0)
        for i in range(NT):
            if i + 1 < NT:
                stage_a(i + 1)
            stage_b(i)
            if i + 1 < NT:
                stage_t(i + 1)
            stage_f(i)
        kb.end_phase()


def phase_gdn_prep(kb, C, pj, S):
    tag = "gp"
    with ExitStack() as st:
        W = T + 3
        HB = T // 2
        ident_b = kb.sb("gp_identb", [128, 128], BF16, st)
        dg = kb.sb("gp_dg", [128, 24, 4, 128], BF16, st)
        rawb = [kb.sb(f"gp_rawb{i}", [128, W], BF16, st) for i in range(2)]
        acc = [kb.sb(f"gp_acc{i}", [128, T], F32, st) for i in range(2)]
        sq = [kb.sb(f"gp_sq{i}", [128, T], F32, st) for i in range(2)]
        rn = [kb.sb(f"gp_rn{i}", [128, T], F32, st) for i in range(2)]
        pcv = kb.ps("gp_pcv", [128, HB], F32, st)
        ssp = kb.ps("gp_ssp", [128, HB], F32, st)
        chl = [kb.chan(f"gp_l{i}") for i in range(2)]
        chs = [kb.chan(f"gp_s{i}") for i in range(2)]
        CP(kb, "dve", ident_b[:], C["ident"][:], [], [(tag, "cst")])
        for cc in range(24):
            TT(kb, "dve", dg[:, cc, :, :], ident_b[:].unsqueeze(1).to_broadcast([128, 4, 128]),
               C["cvw"][:, cc, :].unsqueeze(2).to_broadcast([128, 4, 128]), ALU.mult, [(tag, "cst")], [(tag, "dg")])

        def head(cc):
            s = cc % 2
            r0 = 2048 + cc * 128
            MS(kb, "dve", rawb[s][:, 0:3], 0.0, [(tag, "raw", s)])
            kb.dma("pool", chl[s], rawb[s][:, 3:W], pj[r0:r0 + 128, :], writes=[(tag, "raw", s)])
            for hb in range(2):
                for b in range(4):
                    c0 = hb * HB + b * 512
                    for j in range(4):
                        MM(kb, pcv[:, b * 512:(b + 1) * 512], dg[:, cc, j, :], rawb[s][:, c0 + j:c0 + j + 512],
                           j == 0, j == 3, [(tag, "dg"), (tag, "raw", s)], [(tag, "pcv")], signal=(j == 3 and b == 3))
                ACT(kb, acc[s][:, hb * HB:(hb + 1) * HB], pcv[:], AF.Silu, [(tag, "pcv")], [(tag, "acc", s, hb)])
                if cc < 16:
                    ACT(kb, sq[s][:, hb * HB:(hb + 1) * HB], acc[s][:, hb * HB:(hb + 1) * HB], AF.Square,
                        [(tag, "acc", s, hb)], [(tag, "sq", s, hb)])
                    for b in range(4):
                        c0 = hb * HB + b * 512
                        MM(kb, ssp[:, b * 512:(b + 1) * 512], C["ones_f"][:], sq[s][:, c0:c0 + 512], True, True,
                           [(tag, "sq", s, hb)], [(tag, "ssp")], signal=(b == 3))
                    ACT(kb, rn[s][:, hb * HB:(hb + 1) * HB], ssp[:], AF.Sqrt, [(tag, "ssp")],
                        [(tag, "rn", s, hb)], bias=C["eps"][:, 0:1])

        def tail(cc):
            s = cc % 2
            akeys = [(tag, "acc", s, 0), (tag, "acc", s, 1)]
            if cc < 16:
                keys = [(tag, "rn", s, 0), (tag, "rn", s, 1)]
                RECIP(kb, rn[s][:], rn[s][:], keys, keys)
                STT(kb, acc[s][:], acc[s][:], (128 ** -0.5) if cc < 8 else 1.0, rn[s][:], ALU.mult, ALU.mult,
                    akeys + keys, akeys)
            kb.dma("sp", chs[s], S["gqkv"][cc * 128:(cc + 1) * 128, :], acc[s][:], reads=akeys,
                   writes=[("gqkv", cc)])

        head(0)
        for cc in range(24):
            if cc + 1 < 24:
                head(cc + 1)
            tail(cc)
        kb.end_phase()


import os
GDN_TILES = int(os.environ.get("GDN_TILES", "32"))
GDN_STOP = int(os.environ.get("GDN_STOP", "99"))
GDN_SUB = float(os.environ.get("GDN_SUB", "99"))
GDN_EVAC = os.environ.get("GDN_EVAC", "act")


def phase_gdn(kb, C, pj, S, y_dram):
    with ExitStack() as stc:
        C = dict(C)
        chc = kb.chan("gd_const")
        for name, shape, dt in GDN_CONST_SPECS:
            d = kb.nc.dram_tensor(name, list(shape), dt, kind="ExternalInput").ap()
            t = kb.sb("c_" + name, shape, dt, stc)
            kb.dma("sp", chc, t[:], d, writes=[("const", name)])
            C[name] = t
        kb.end_phase()
        phase_gdn_prep(kb, C, pj, S)
        _phase_gdn_main(kb, C, pj, S, y_dram)


def _phase_gdn_main(kb, C, pj, S, y_dram):
    tag = "gd"
    H = 8
    with ExitStack() as st:
        def fb(name, shape=(128, H, 128), dt=F32):
            return kb.sb("gd_" + name, list(shape), dt, st)

        tm = fb("tm", (128, NT, 32))
        beta = fb("beta", (128, NT, 8))
        g = fb("g", (128, NT, 8))
        t1 = fb("t1", (128, NT, 8))
        t2 = fb("t2", (128, NT, 8))
        nA = fb("nA", (128, 8))
        qT, kT, vT = fb("qT"), fb("kT"), fb("vT")
        gd, egrow, gcr = fb("gdiag"), fb("egrow"), fb("gcr")
        P1, E1, E2 = fb("P1"), fb("E1"), fb("E2")
        HS = (128, H, 128)
        X = [fb("X0", HS, BF16), fb("X1", HS, BF16)]
        Y = [fb("Y0", HS, BF16), fb("Y1", HS, BF16)]
        P, attnT = fb("P", HS, BF16), fb("attnT", HS, BF16)
        vb, kbg, kd, kd1 = fb("vb", HS, BF16), fb("kbg", HS, BF16), fb("kd", HS, BF16), fb("kd1", HS, BF16)
        smk = fb("smk", (128, 16))
        u, wT, qgT, vnew = fb("u"), fb("wT", HS, BF16), fb("qgT", HS, BF16), fb("vnew", HS, BF16)
        Sst, oacc, zb = fb("S"), fb("oacc"), fb("zb")
        Sb = fb("Sb", HS, BF16)
        ident_b = fb("identb", (128, 128), BF16)
        osq, orn = fb("osq"), fb("orn")
        yo = fb("yo", (128, H, 128), BF16)
        sm = fb("sm", (128, 64))
        pA = kb.ps("gd_pA", [128, H, 128], F32, st)
        pB = kb.ps("gd_pB", [128, H, 128], F32, st)
        pC = kb.ps("gd_pC", [128, H, 128], F32, st)
        pO = kb.ps("gd_pO", [128, H, 64], F32, st)
        psm = kb.ps("gd_psm", [128, 32], F32, st)
        ch = kb.chan("gd_l")
        chq = kb.chan("gd_q")
        chz = kb.chan("gd_z")
        chy = kb.chan("gd_y")
        K_ = lambda n: (tag, n)

        def bc_h(ap2d):
            return ap2d.unsqueeze(1).to_broadcast([128, H, 128])

        def bc_f(ap2d):
            return ap2d.unsqueeze(2).to_broadcast([128, H, 128])

        kb.dma("sp", ch, tm[:], S["tmaj"].rearrange("(n p) c -> p n c", p=128), writes=[K_("tm")])
        ACT(kb, beta[:], tm[:, :, 16:24], AF.Sigmoid, [K_("tm")], [K_("beta")])
        dtb = C["dtb_bc"][:].unsqueeze(1).to_broadcast([128, NT, 8])
        TT(kb, "dve", g[:], tm[:, :, 24:32], dtb, ALU.add, [K_("tm")], [K_("g")])
        TS(kb, "dve", t1[:], g[:], -1.0, None, ALU.mult, None, [K_("g")], [K_("t1")])
        TT(kb, "dve", t1[:], t1[:], g[:], ALU.max, [K_("t1"), K_("g")], [K_("t1")])
        ACT(kb, t1[:], t1[:], AF.Exp, [K_("t1")], [K_("t1")], scale=-1.0)
        TS(kb, "dve", t1[:], t1[:], 1.0, None, ALU.add, None, [K_("t1")], [K_("t1")])
        ACT(kb, t1[:], t1[:], AF.Ln, [K_("t1")], [K_("t1")])
        TS(kb, "dve", t2[:], g[:], 0.0, None, ALU.max, None, [K_("g")], [K_("t2")])
        TT(kb, "dve", t2[:], t2[:], t1[:], ALU.add, [K_("t1"), K_("t2")], [K_("t2")])
        ACT(kb, nA[:], C["alog_bc"][:], AF.Exp, [], [K_("nA")])
        TS(kb, "dve", nA[:], nA[:], -1.0, None, ALU.mult, None, [K_("nA")], [K_("nA")])
        TT(kb, "dve", g[:], t2[:], nA[:].unsqueeze(1).to_broadcast([128, NT, 8]), ALU.mult, [K_("t2"), K_("nA")],
           [K_("g")])
        MS(kb, "dve", Sst[:], 0.0, [K_("S")])
        MS(kb, "dve", Sb[:], 0.0, [K_("Sb")])
        MS(kb, "dve", vnew[:], 0.0, [K_("vnew")])
        CP(kb, "dve", ident_b[:], C["ident"][:], [], [K_("identb")])
        pAb = pA[:, 0:4, :].bitcast(BF16).rearrange("p a (c b) -> p (a c) b", b=128)
        gq = S["gqkv"]
        qv = gq[0:1024, :].rearrange("(h p) t -> p h t", p=128)
        kv = gq[1024:2048, :].rearrange("(h p) t -> p h t", p=128)
        vv = gq[2048:3072, :].rearrange("(h p) t -> p h t", p=128)
        zv = pj[5120:6144, :].rearrange("(h p) t -> p h t", p=128)
        yv = y_dram[1024:2048, :].rearrange("(h p) t -> p h t", p=128)

        HH = 4

        def tile_shared(n):
            c0, c1 = n * 128, (n + 1) * 128
            kb.dma("sp", chq, qT[:], qv[:, :, c0:c1], writes=[K_("qT")])
            kb.dma("sp", chq, kT[:], kv[:, :, c0:c1], writes=[K_("kT")])
            kb.dma("sp", chq, vT[:], vv[:, :, c0:c1], writes=[K_("vT")])
            kb.dma("sp", chz, zb[:], zv[:, :, c0:c1], writes=[K_("zb")])
            gn = g[:, n, :]
            bn = beta[:, n, :]
            MM(kb, psm[:, 0:8], C["U2"][:], gn, True, True, [K_("g")], [K_("psm")], signal=False)
            MM(kb, psm[:, 8:16], C["Bsame"][:], gn, True, True, [K_("g")], [K_("psm")], signal=False)
            MM(kb, psm[:, 16:24], C["Bsel0"][:], gn, True, True, [K_("g")], [K_("psm")], signal=False)
            MM(kb, psm[:, 24:32], C["Bsel1"][:], gn, True, True, [K_("g")], [K_("psm")])
            CP(kb, "dve", sm[:, 0:32], psm[:], [K_("psm")], [K_("sm")])
            ACT(kb, sm[:, 32:40], sm[:, 0:8], AF.Exp, [K_("sm")], [K_("sm")])
            TT(kb, "dve", sm[:, 40:48], sm[:, 8:16], sm[:, 0:8], ALU.subtract, [K_("sm")], [K_("sm")])
            ACT(kb, sm[:, 40:48], sm[:, 40:48], AF.Exp, [K_("sm")], [K_("sm")])
            ACT(kb, sm[:, 16:32], sm[:, 16:32], AF.Exp, [K_("sm")], [K_("sm")])
            TT(kb, "dve", sm[:, 48:56], sm[:, 32:40], bn, ALU.mult, [K_("sm"), K_("beta")], [K_("sm")])
            TS(kb, "dve", sm[:, 56:64], bn, -1.0, None, ALU.mult, None, [K_("beta")], [K_("sm")])
            TS(kb, "dve", smk[:, 0:8], sm[:, 40:48], C["Bsel0"][:, 0:1], None, ALU.mult, None, [K_("sm")], [K_("smk")])
            TS(kb, "dve", smk[:, 8:16], sm[:, 40:48], C["Bsel1"][:, 0:1], None, ALU.mult, None, [K_("sm")], [K_("smk")])
            ACT(kb, zb[:], zb[:], AF.Silu, [K_("zb")], [K_("zb")])

        def tile_half(n, hh):
            c0, c1 = n * 128, (n + 1) * 128
            h0, h1 = hh * HH, (hh + 1) * HH
            hs = slice(h0, h1)
            heads = range(h0, h1)
            last = h1 - 1
            k_ = lambda nm: (tag, nm, hh)
            gn = g[:, n, hs]
            bn = beta[:, n, hs]

            def bh(ap2d):
                return ap2d.unsqueeze(1).to_broadcast([128, HH, 128])

            def bf(ap2d):
                return ap2d.unsqueeze(2).to_broadcast([128, HH, 128])

            pAb = pA[:, h0:h0 + 2, :].bitcast(BF16).rearrange("p a (c b) -> p (a c) b", b=128)
            gc = sm[:, h0:h1]
            TT(kb, "dve", gd[:, hs, :], bh(C["U2"][:]), bf(gn), ALU.mult, [K_("g")], [k_("gdiag")])
            MM(kb, pA[:, hs, :], C["ones_f"][:], gd[:, hs, :], True, True, [k_("gdiag")], [k_("pA")])
            yield
            CP(kb, "dve", gcr[:, hs, :], pA[:, hs, :], [k_("pA")], [k_("gcr")])
            ACT(kb, egrow[:, hs, :], gcr[:, hs, :], AF.Exp, [k_("gcr")], [k_("egrow")])
            TT(kb, "dve", P1[:, hs, :], gcr[:, hs, :], bf(gc), ALU.subtract, [k_("gcr"), K_("sm")], [k_("P1")])
            yield
            ACT(kb, E1[:, hs, :], P1[:, hs, :], AF.Relu, [k_("P1")], [k_("E1")])
            ACT(kb, E1[:, hs, :], E1[:, hs, :], AF.Exp, [k_("E1")], [k_("E1")], scale=-1.0)
            ACT(kb, E2[:, hs, :], P1[:, hs, :], AF.Relu, [k_("P1")], [k_("E2")], scale=-1.0)
            ACT(kb, E2[:, hs, :], E2[:, hs, :], AF.Exp, [k_("E2")], [k_("E2")], scale=-1.0)
            yield
            TT(kb, "dve", E1[:, hs, :], E1[:, hs, :], bh(C["MLs"][:]), ALU.mult, [k_("E1")], [k_("E1")])
            TT(kb, "dve", E1[:, hs, :], E1[:, hs, :], bf(sm[:, 56 + h0:56 + h1]), ALU.mult, [k_("E1"), K_("sm")],
               [k_("E1")])
            TT(kb, "dve", E2[:, hs, :], E2[:, hs, :], bh(C["MU"][:]), ALU.mult, [k_("E2")], [k_("E2")])
            yield
            for h in heads:
                MM(kb, pB[:, h, :], kT[:, h, :], kT[:, h, :], True, True, [K_("kT")], [k_("pB")], signal=(h == last))
            for h in heads:
                MM(kb, pC[:, h, :], kT[:, h, :], qT[:, h, :], True, True, [K_("kT"), K_("qT")], [k_("pC")],
                   signal=(h == last))
            yield
            TT(kb, "dve", X[0][:, hs, :], pB[:, hs, :], E1[:, hs, :], ALU.mult, [k_("pB"), k_("E1")], [k_("X0")])
            TT(kb, "dve", attnT[:, hs, :], pC[:, hs, :], E2[:, hs, :], ALU.mult, [k_("pC"), k_("E2")], [k_("attnT")])
            yield
            for h in heads:
                TR(kb, pAb[:, h - h0, :], X[0][:, h, :], ident_b[:], [k_("X0"), K_("identb")], [k_("pA")],
                   signal=(h == last))
            yield
            CP(kb, "dve", Y[0][:, hs, :], pAb, [k_("pA")], [k_("Y0")])
            TT(kb, "dve", P[:, hs, :], Y[0][:, hs, :], bh(C["ident"][:]), ALU.add, [k_("Y0")], [k_("P")])
            yield
            cur = 0
            for lvl in range(5):
                nxt = 1 - cur
                xk, yk = k_(f"X{cur}"), k_(f"Y{cur}")
                xn, yn = k_(f"X{nxt}"), k_(f"Y{nxt}")
                for h in heads:
                    MM(kb, pB[:, h, :], Y[cur][:, h, :], X[cur][:, h, :], True, True, [xk, yk], [k_("pB")],
                       signal=(h == last))
                if lvl < 4:
                    for h in heads:
                        MM(kb, pC[:, h, :], X[cur][:, h, :], Y[cur][:, h, :], True, True, [xk, yk], [k_("pC")],
                           signal=(h == last))
                yield
                CP(kb, GDN_EVAC, X[nxt][:, hs, :], pB[:, hs, :], [k_("pB")], [xn])
                if lvl < 4:
                    CP(kb, GDN_EVAC, Y[nxt][:, hs, :], pC[:, hs, :], [k_("pC")], [yn])
                yield
                for h in heads:
                    MM(kb, pA[:, h, :], X[nxt][:, h, :], P[:, h, :], True, True, [xn, k_("P")], [k_("pA")],
                       signal=(h == last))
                yield
                TT(kb, "dve", P[:, hs, :], P[:, hs, :], pA[:, hs, :], ALU.add, [k_("pA"), k_("P")], [k_("P")])
                yield
                cur = nxt
            for h in heads:
                TR(kb, pB[:, h, :], kT[:, h, :], C["ident"][:], [K_("kT")], [k_("pB")], signal=(h == last))
            for h in heads:
                TR(kb, pC[:, h, :], vT[:, h, :], C["ident"][:], [K_("vT")], [k_("pC")], signal=(h == last))
            yield
            TT(kb, "dve", kbg[:, hs, :], pB[:, hs, :], bf(sm[:, 48 + h0:48 + h1]), ALU.mult, [k_("pB"), K_("sm")],
               [k_("kbg")])
            TT(kb, "dve", kd[:, hs, :], pB[:, hs, :], bf(smk[:, h0:h1]), ALU.mult, [k_("pB"), K_("smk")], [k_("kd")])
            TT(kb, "dve", kd1[:, hs, :], pB[:, hs, :], bf(smk[:, 8 + h0:8 + h1]), ALU.mult, [k_("pB"), K_("smk")],
               [k_("kd")])
            TT(kb, "dve", vb[:, hs, :], pC[:, hs, :], bf(bn), ALU.mult, [k_("pC"), K_("beta")], [k_("vb")])
            yield
            for h in heads:
                MM(kb, pA[:, h, :], P[:, h, :], vb[:, h, :], True, True, [k_("P"), k_("vb")], [k_("pA")],
                   signal=(h == last))
            for h in heads:
                MM(kb, pB[:, h, :], kbg[:, h, :], P[:, h, :], True, True, [k_("P"), k_("kbg")], [k_("pB")],
                   signal=(h == last))
            yield
            CP(kb, "dve", u[:, hs, :], pA[:, hs, :], [k_("pA")], [k_("u")])
            CP(kb, GDN_EVAC, wT[:, hs, :], pB[:, hs, :], [k_("pB")], [k_("wT")])
            TT(kb, "dve", qgT[:, hs, :], qT[:, hs, :], egrow[:, hs, :], ALU.mult, [K_("qT"), k_("egrow")], [k_("qgT")])
            yield
            for c in range(2):
                r0, r1 = c * 64, (c + 1) * 64
                for h in heads:
                    MM(kb, pA[r0:r1, h, :], wT[:, h, r0:r1], Sb[:, h, :], True, True, [k_("wT"), k_("Sb")], [k_("pA")],
                       signal=(h == last))
                yield
                TT(kb, "dve", vnew[r0:r1, hs, :], u[r0:r1, hs, :], pA[r0:r1, hs, :], ALU.subtract,
                   [k_("u"), k_("pA")], [k_("vnew")])
                yield
                for h in heads:
                    MM(kb, pO[:, h, :], Sb[:, h, :], qgT[:, h, r0:r1], True, False, [k_("Sb"), k_("qgT")], [k_("pO")])
                    MM(kb, pO[:, h, :], vnew[r0:r1, h, :], attnT[r0:r1, h, r0:r1], False, True,
                       [k_("vnew"), k_("attnT")], [k_("pO")], signal=(h == last))
                for h in heads:
                    MM(kb, pB[:, h, :], (kd, kd1)[c][:, h, :], vnew[:, h, :], True, True, [k_("kd"), k_("vnew")],
                       [k_("pB")], signal=(h == last))
                yield
                CP(kb, GDN_EVAC, oacc[:, hs, r0:r1], pO[:, hs, :], [k_("pO")], [k_("oacc")])
                TT(kb, "dve", Sst[:, hs, :], Sst[:, hs, :], bf(sm[:, 16 + 8 * c + h0:16 + 8 * c + h1]), ALU.mult,
                   [k_("S"), K_("sm")], [k_("S")])
                TT(kb, "dve", Sst[:, hs, :], Sst[:, hs, :], pB[:, hs, :], ALU.add, [k_("S"), k_("pB")], [k_("S")])
                CP(kb, "act", Sb[:, hs, :], Sst[:, hs, :], [k_("S")], [k_("Sb")])
                yield
            ACT(kb, osq[:, hs, :], oacc[:, hs, :], AF.Square, [k_("oacc")], [k_("osq")])
            MM(kb, pC[:, hs, :], C["ones_f"][:], osq[:, hs, :], True, True, [k_("osq")], [k_("pC")])
            yield
            CP(kb, "dve", orn[:, hs, :], pC[:, hs, :], [k_("pC")], [k_("orn")])
            ACT(kb, orn[:, hs, :], orn[:, hs, :], AF.Sqrt, [k_("orn")], [k_("orn")], bias=C["eps"][:, 0:1],
                scale=1.0 / 128)
            yield
            RECIP(kb, orn[:, hs, :], orn[:, hs, :], [k_("orn")], [k_("orn")])
            STT(kb, oacc[:, hs, :], oacc[:, hs, :], C["onorm"][:, 0:1], orn[:, hs, :], ALU.mult, ALU.mult,
                [k_("oacc"), k_("orn")], [k_("oacc")])
            TT(kb, "dve", yo[:, hs, :], oacc[:, hs, :], zb[:, hs, :], ALU.mult, [k_("oacc"), K_("zb")], [k_("yo")])
            kb.dma("pool", chy, yv[:, hs, c0:c1], yo[:, hs, :], reads=[k_("yo")], writes=[("y0b", n, hh)])

        for n in range(GDN_TILES):
            tile_shared(n)
            gens = [tile_half(n, 0), tile_half(n, 1)]
            while gens:
                for gnr in list(gens):
                    try:
                        next(gnr)
                    except StopIteration:
                        gens.remove(gnr)
        kb.end_phase()


def load_consts(kb, nc, names_shapes):
    C = {}
    ch = kb.chan("const")
    for name, shape, dt in names_shapes:
        d = nc.dram_tensor(name, list(shape), dt, kind="ExternalInput").ap()
        if name == "cbase":
            t = kb.sb("c_" + name, shape, BF16)
            kb.dma("pool", ch, t[:], d, writes=[("const", name)])
        else:
            t = kb.sb("c_" + name, shape, dt)
            kb.dma("sp", ch, t[:], d, writes=[("const", name)])
        C[name] = t
    kb.end_phase()
    with ExitStack() as st:
        d = nc.dram_tensor("biasT", [128, 8, 2, 128], F32, kind="ExternalInput").ap()
        C["biasS"] = kb.sb("c_biasS", [128, 8, 2, 128], BF16)
        bt = kb.sb("biasT_tmp", [128, 8, 2, 128], F32, st)
        kb.dma("sp", ch, bt[:], d, writes=[("const", "biasT")])
        for h in range(8):
            TS(kb, "dve", C["biasS"][:, h, :, :], bt[:, h, :, :], C["cb"][:, h:h + 1], 128 ** 0.5,
               ALU.subtract, ALU.mult, [("const", "biasT")], [("const", "biasS")])
        kb.end_phase()
    return C


CONST_SPECS = [
    ("ident", (128, 128), F32),
    ("ones_f", (128, 128), F32),
    ("eps", (128, 1), F32),
    ("normwT", (128, 2, 16), F32),
    ("cbase", (128, 896), F32),
    ("cb", (128, 8), F32),
    ("qnormT", (128, 4), F32),
    ("kvnormT", (128, 2), F32),
    ("qgain", (128, 1), F32),
    ("kgain", (128, 1), F32),
    ("ikw", (128, 1), F32),
    ("ikb", (128, 1), F32),
]

CD_CONST_SPECS = [
    ("dww", (128, 8, 31), F32),
    ("dwb", (128, 8), F32),
    ("lnw", (128, 8), F32),
    ("lnb", (128, 8), F32),
    ("dcw", (128, 8, 3), F32),
]

GDN_CONST_SPECS = [
    ("cvw", (128, 24, 4), F32),
    ("alog_bc", (128, 8), F32),
    ("dtb_bc", (128, 8), F32),
    ("onorm", (128, 1), F32),
    ("U2", (128, 128), F32),
    ("Bsame", (128, 128), F32),
    ("Bsel0", (128, 128), F32),
    ("Bsel1", (128, 128), F32),
    ("MLs", (128, 128), F32),
    ("MU", (128, 128), F32),
]


def build_program(layers=(0, 1), l0_parts=("a", "b"), debug_out=False):
    nc = bass.Bass("TRN2", target_bir_lowering=False)

    def din(name, shape, dt=F32):
        return nc.dram_tensor(name, list(shape), dt, kind="ExternalInput").ap()

    def dscr(name, shape, dt=F32):
        return nc.dram_tensor(name, list(shape), dt, kind="Internal").ap()

    x = din("x", [T, D])
    out = nc.dram_tensor("out", [T, D], F32, kind="ExternalOutput").ap()
    kb = KB(nc)
    C = load_consts(kb, nc, CONST_SPECS)
    src = x
    if 0 in layers:
        ab_w_in = din("ab_w_in", [48, 128, 16 * 128])
        ab_w_out = din("ab_w_out", [16, 128, D])
        W = {"w_uqiq": din("w_uqiq", [16, 128, 4 * 128]), "w_uk": din("w_uk", [8, 128, 2 * 128]),
             "w_uv": din("w_uv", [2, 128, 1024])}
        pj0 = dscr("pj0", [6144, T])
        S = {"qT": dscr("s_qT", [8, 128, T], BF16), "qiT": dscr("s_qiT", [8, 128, T], BF16),
             "kT": dscr("s_kT", [8, 128, T], BF16), "V": dscr("s_V", [T, 1024], BF16),
             "kiT": dscr("s_kiT", [128, T], BF16), "tmaj": dscr("s_tmaj", [T, 32]),
             "gqkv": dscr("s_gqkv", [3072, T])}
        if debug_out:
            y0 = nc.dram_tensor("y0", [2048, T], BF16, kind="ExternalOutput").ap()
        else:
            y0 = dscr("y0", [2048, T], BF16)
        x1 = dscr("x1", [T, D]) if 1 in layers else out
        phase_inproj(kb, C, "l0", src, C["normwT"][:, 0, :], ab_w_in, 48, pj0)
        phase_ki_tmaj(kb, C, pj0, S)
        if "a" in l0_parts:
            phase_dsa_prep(kb, C, pj0, W, S)
            phase_dsa_attn(kb, C, pj0, S, y0)
        if "b" in l0_parts:
            phase_gdn(kb, C, pj0, S, y0)
        if not debug_out:
            phase_outproj(kb, C, "l0o", y0, ab_w_out, src, x1)
        src = x1
    if 1 in layers:
        cd_w_in = din("cd_w_in", [56, 128, 16 * 128])
        cd_w_out = din("cd_w_out", [16, 128, D])
        pj1 = dscr("pj1", [7168, T])
        y1 = dscr("y1", [2048, T], BF16)
        phase_inproj(kb, C, "l1", src, C["normwT"][:, 1, :], cd_w_in, 56, pj1)
        phase_cd_mix(kb, C, pj1, C, y1)
        phase_outproj(kb, C, "l1o", y1, cd_w_out, src, out)
    kb.finish()
    kb.emit()
    kb.close()
    return nc, kb


def tile_w_in(w, nch):
    K, N = w.shape
    assert N == nch * 128
    return np.ascontiguousarray(w.reshape(K // 128, 128, nch, 128).transpose(2, 1, 0, 3)).reshape(nch, 128, -1)


def t5_bucket_np(dist):
    import math
    max_exact = 16
    dd = np.maximum(dist, 1).astype(np.float32)
    large = max_exact + (np.log(dd / max_exact) / math.log(128 / max_exact) * (32 - max_exact)).astype(np.int32)
    large = np.minimum(large, 31)
    return np.where(dist < max_exact, dist, large)


def colT(v, k):
    return np.ascontiguousarray(np.asarray(v, np.float32).reshape(k, 128).T)


def host_consts(inp):
    f = np.float32
    c = {}
    c["ident"] = np.eye(128, dtype=f)
    c["ones_f"] = np.ones((128, 128), f)
    c["eps"] = np.full((128, 1), EPS, f)
    c["normwT"] = np.ascontiguousarray(inp["norm_w"].reshape(2, 16, 128).transpose(2, 0, 1)).astype(f)
    c["dww"] = np.ascontiguousarray(inp["c_dw_w"][0].reshape(31, 8, 128).transpose(2, 1, 0)).astype(f)
    c["dwb"] = colT(inp["c_dw_b"][0], 8)
    c["lnw"] = colT(inp["c_ln_w"][0], 8)
    c["lnb"] = colT(inp["c_ln_b"][0], 8)
    c["dcw"] = np.ascontiguousarray(inp["d_conv_w"][0].reshape(3, 8, 128).transpose(2, 1, 0)).astype(f)
    r = np.arange(128)[:, None]
    cc = np.arange(896)[None, :]
    c["cbase"] = np.where(cc <= r + 384, 0.0, -1e30).astype(f)
    kl = np.arange(128)[:, None, None]
    dd = np.arange(2)[None, :, None]
    ql = np.arange(128)[None, None, :]
    dist = np.maximum(dd * 128 + ql - kl, 0)
    bt = np.asarray(inp["rel_bias"], f)[t5_bucket_np(dist)]
    c["biasT"] = np.ascontiguousarray(bt.transpose(0, 3, 1, 2))
    c["cb"] = np.ascontiguousarray(np.broadcast_to(np.asarray(inp["rel_bias"], f)[31][None, :], (128, 8)))
    c["qnormT"] = colT(inp["a_q_norm"][0], 4)
    c["kvnormT"] = colT(inp["a_kv_norm"][0], 2)
    c["qgain"] = np.asarray(inp["a_q_gain"][0], f).reshape(128, 1).copy()
    c["kgain"] = np.asarray(inp["a_k_gain"][0], f).reshape(128, 1).copy()
    c["ikw"] = np.tile(np.asarray(inp["a_ik_norm_w"][0], f), 2).reshape(128, 1).copy()
    c["ikb"] = np.tile(np.asarray(inp["a_ik_norm_b"][0], f), 2).reshape(128, 1).copy()
    c["cvw"] = np.ascontiguousarray(np.asarray(inp["b_conv_w"][0], f).reshape(4, 24, 128).transpose(2, 1, 0))
    c["alog_bc"] = np.ascontiguousarray(np.broadcast_to(np.asarray(inp["b_a_log"][0], f)[None, :], (128, 8)))
    c["dtb_bc"] = np.ascontiguousarray(np.broadcast_to(np.asarray(inp["b_dt_bias"][0], f)[None, :], (128, 8)))
    c["onorm"] = np.asarray(inp["b_o_norm"][0], f).reshape(128, 1).copy()
    a = np.arange(128)
    same = (a[:, None] // 64) == (a[None, :] // 64)
    c["U2"] = (same & (a[:, None] <= a[None, :])).astype(f)
    c["Bsame"] = same.astype(f)
    c["Bsel0"] = np.ascontiguousarray(np.broadcast_to((a[:, None] < 64), (128, 128))).astype(f)
    c["Bsel1"] = np.ascontiguousarray(np.broadcast_to((a[:, None] >= 64), (128, 128))).astype(f)
    c["MLs"] = (same & (a[:, None] > a[None, :])).astype(f)
    c["MU"] = (same & (a[:, None] <= a[None, :])).astype(f)
    return c


def host_shared(inp, layers=(0, 1)):
    f = np.float32
    sh = host_consts(inp)
    if 0 in layers:
        w = np.asarray(inp["ab_w_in"][0], f)
        wp = np.zeros((D, 6144), f)
        wp[:, 0:832] = w[:, 0:832]
        wp[:, 896:912] = w[:, 832:848]
        wp[:, 912:928] = w[:, 4944:4960]
        wp[:, 1024:2048] = w[:, 848:1872]
        wp[:, 2048:5120] = w[:, 1872:4944]
        wp[:, 5120:6144] = w[:, 4960:5984]
        sh["ab_w_in"] = tile_w_in(wp, 48)
        sh["ab_w_out"] = np.ascontiguousarray(np.asarray(inp["ab_w_out"][0], f).reshape(16, 128, D))
        sh["w_uqiq"] = tile_w_in(np.concatenate([inp["a_w_uq"][0], inp["a_w_iq"][0]], axis=1).astype(f), 16)
        sh["w_uk"] = tile_w_in(np.asarray(inp["a_w_uk"][0], f), 8)
        sh["w_uv"] = np.ascontiguousarray(np.asarray(inp["a_w_uv"][0], f).reshape(2, 128, 1024))
    if 1 in layers:
        sh["cd_w_in"] = tile_w_in(np.asarray(inp["cd_w_in"][0], f), 56)
        sh["cd_w_out"] = np.ascontiguousarray(np.asarray(inp["cd_w_out"][0], f).reshape(16, 128, D))
    return sh


def kernel(**inputs):
    inp = {k: np.asarray(v) for k, v in inputs.items()}
    nc, kb = build_program()
    sh = host_shared(inp)
    x = np.ascontiguousarray(inp["x"], dtype=np.float32)
    in_maps = [dict(sh, x=x[b]) for b in range(8)]
    res = run_bass_kernel_spmd(nc, in_maps, core_ids=list(range(8)))
    return np.stack([np.asarray(r["out"], np.float32) for r in res.results], axis=0)
```
